# Optimizing a Trainium2 kernel written in Bass

```python
import jax
import jax.numpy as jnp
from jax import lax
import numpy as np

D_MODEL = 1024
BATCH = 8
SEQ = 4096
DEPTH = 1

N_META = 16
HEAD_DIM = 64
ATTN_Q_HEADS = 8
ATTN_KV_HEADS = 2
ATTN_GROUP = ATTN_Q_HEADS // ATTN_KV_HEADS
WINDOW = 128
BLOCK = 128
ROPE_THETA = 500000.0
ROPE_DIM = HEAD_DIM // 4
RWKV_HEADS = 8
RWKV_HEAD = 64
RWKV_DIM = RWKV_HEADS * RWKV_HEAD
DECAY_LORA = 64
AAA_LORA = 64
GATE_LORA = 160
RWKV_LN_EPS = 64e-5
D_FF = -(-8 * D_MODEL // (3 * 256)) * 256
Q_W = ATTN_Q_HEADS * HEAD_DIM
KV_W = ATTN_KV_HEADS * HEAD_DIM
ATTN_PROJ = Q_W + 2 * KV_W
RWKV_PROJ = 3 * RWKV_DIM + DECAY_LORA + AAA_LORA + GATE_LORA
D_IN = ATTN_PROJ + RWKV_PROJ + 2 * D_MODEL
RMS_EPS = 1e-6
NEG_INF = -1e30

kernel_name = 'hybrid_swa_sink_rwkv7_gated_block'


def rms_norm(x, g):
    xf = x.astype(jnp.float32)
    y = xf * lax.rsqrt(jnp.mean(xf * xf, axis=-1, keepdims=True) + RMS_EPS)
    return (y * g.astype(jnp.float32)).astype(x.dtype)


def partial_rope(t, pos):
    half = ROPE_DIM // 2
    inv_freq = jnp.power(jnp.float32(ROPE_THETA), -jnp.arange(half, dtype=jnp.float32) * (2.0 / ROPE_DIM))
    ang = pos.astype(jnp.float32)[:, None] * inv_freq[None, :]
    cos = jnp.cos(ang)[None, :, None, :]
    sin = jnp.sin(ang)[None, :, None, :]
    tf = t.astype(jnp.float32)
    t1 = tf[..., :half]
    t2 = tf[..., half:ROPE_DIM]
    out = jnp.concatenate([t1 * cos - t2 * sin, t2 * cos + t1 * sin, tf[..., ROPE_DIM:]], axis=-1)
    return out.astype(t.dtype)


def sliding_window_sink_attention(q, k, v, sinks):
    B, L = q.shape[0], q.shape[1]
    pad = BLOCK - N_META
    n_blk = (L + pad) // BLOCK

    def blockify(t):
        t = jnp.pad(t, ((0, 0), (pad, 0), (0, 0), (0, 0)))
        return t.reshape(B, n_blk, BLOCK, t.shape[2], t.shape[3])

    def prev_block(t):
        return jnp.pad(t, ((0, 0), (1, 0), (0, 0), (0, 0), (0, 0)))[:, :-1]

    qb = blockify(q).reshape(B, n_blk, BLOCK, ATTN_KV_HEADS, ATTN_GROUP, HEAD_DIM)
    kb = blockify(k)
    vb = blockify(v)
    k_band = jnp.concatenate([prev_block(kb), kb], axis=2)
    v_band = jnp.concatenate([prev_block(vb), vb], axis=2)
    k_meta = k[:, :N_META]
    v_meta = v[:, :N_META]

    scale = HEAD_DIM ** -0.5
    s_band = jnp.einsum('bnqhgd,bnkhd->bhgnqk', qb, k_band).astype(jnp.float32) * scale
    s_meta = jnp.einsum('bnqhgd,bmhd->bhgnqm', qb, k_meta).astype(jnp.float32) * scale

    q_pos = jnp.arange(n_blk * BLOCK).reshape(n_blk, BLOCK) - pad
    k_pos = (jnp.arange(n_blk)[:, None] - 1) * BLOCK + jnp.arange(2 * BLOCK)[None, :] - pad
    qp = q_pos[:, :, None]
    kp = k_pos[:, None, :]
    band_ok = (kp >= N_META) & (kp <= qp) & (qp - kp < WINDOW)
    meta_ok = jnp.arange(N_META)[None, None, :] <= qp
    s_band = jnp.where(band_ok[None, None, None], s_band, NEG_INF)
    s_meta = jnp.where(meta_ok[None, None, None], s_meta, NEG_INF)
    sink = sinks.astype(jnp.float32).reshape(ATTN_KV_HEADS, ATTN_GROUP)[None, :, :, None, None, None]
    sink = jnp.broadcast_to(sink, s_band.shape[:-1] + (1,))

    probs = jax.nn.softmax(jnp.concatenate([s_meta, s_band, sink], axis=-1), axis=-1)
    p_meta = probs[..., :N_META].astype(v.dtype)
    p_band = probs[..., N_META:N_META + 2 * BLOCK].astype(v.dtype)
    out = (jnp.einsum('bhgnqm,bmhd->bnqhgd', p_meta, v_meta)
           + jnp.einsum('bhgnqk,bnkhd->bnqhgd', p_band, v_band))
    return out.reshape(B, n_blk * BLOCK, Q_W)[:, pad:]


def token_shift(t):
    return jnp.pad(t, ((0, 0), (1, 0), (0, 0)))[:, :-1]


def wkv7_scan(r, decay, k, v, aa, bb):
    B, L, H, N = r.shape

    def step(S, inp):
        r_t, w_t, k_t, v_t, a_t, b_t = inp
        sa = jnp.einsum('bhvk,bhk->bhv', S, a_t)
        S = S * w_t[:, :, None, :] + sa[..., None] * b_t[:, :, None, :] + v_t[..., None] * k_t[:, :, None, :]
        y = jnp.einsum('bhvk,bhk->bhv', S, r_t)
        return S, y

    xs = (jnp.moveaxis(r, 1, 0), jnp.moveaxis(decay, 1, 0), jnp.moveaxis(k, 1, 0),
          jnp.moveaxis(v, 1, 0), jnp.moveaxis(aa, 1, 0), jnp.moveaxis(bb, 1, 0))
    S0 = jnp.zeros((B, H, N, N), jnp.float32)
    _, ys = lax.scan(step, S0, xs)
    return jnp.moveaxis(ys, 0, 1)


def rwkv7_time_mix(p, mix, w0, w2, a0, a2, g2, k_k, k_a, r_k, ln_w, ln_b):
    B, L = p.shape[0], p.shape[1]
    f32 = jnp.float32
    pf = p.astype(f32)
    pf = pf + (token_shift(pf) - pf) * mix.astype(f32)
    o1, o2, o3 = RWKV_DIM, 2 * RWKV_DIM, 3 * RWKV_DIM
    o4 = o3 + DECAY_LORA
    o5 = o4 + AAA_LORA
    r = pf[..., :o1]
    k = pf[..., o1:o2]
    v = pf[..., o2:o3]
    dw = pf[..., o3:o4]
    da = pf[..., o4:o5]
    dg = pf[..., o5:]
    w = -jax.nn.softplus(-(w0.astype(f32) + jnp.tanh(dw) @ w2.astype(f32))) - 0.5
    a = jax.nn.sigmoid(a0.astype(f32) + da @ a2.astype(f32))
    g = jax.nn.sigmoid(dg) @ g2.astype(f32)
    hs = (B, L, RWKV_HEADS, RWKV_HEAD)
    kk = (k * k_k.astype(f32)).reshape(hs)
    kk = kk / jnp.maximum(jnp.sqrt(jnp.sum(kk * kk, axis=-1, keepdims=True)), 1e-12)
    k = k * (1.0 + (a - 1.0) * k_a.astype(f32))
    r = r.reshape(hs)
    k = k.reshape(hs)
    v = v.reshape(hs)
    a = a.reshape(hs)
    decay = jnp.exp(-jnp.exp(w)).reshape(hs)
    y = wkv7_scan(r, decay, k, v, -kk, kk * a)
    mean = jnp.mean(y, axis=-1, keepdims=True)
    var = jnp.mean(jnp.square(y - mean), axis=-1, keepdims=True)
    y = ((y - mean) * lax.rsqrt(var + RWKV_LN_EPS) * ln_w.astype(f32).reshape(RWKV_HEADS, RWKV_HEAD)
         + ln_b.astype(f32).reshape(RWKV_HEADS, RWKV_HEAD))
    y = y + jnp.sum(r * k * r_k.astype(f32), axis=-1, keepdims=True) * v
    return (y.reshape(B, L, RWKV_DIM) * g).astype(p.dtype)


def setup_inputs(seed: int = 0) -> dict:
    key = jax.random.key(seed)
    ks = jax.random.split(key, 25)
    f32 = jnp.float32

    def nrm(k, shape, scale):
        return jax.random.normal(k, shape, f32) * scale

    def unif(k, shape, lo, hi):
        return jax.random.uniform(k, shape, f32, lo, hi)

    Dp = DEPTH
    return {
        'x': nrm(ks[0], (BATCH, SEQ, D_MODEL), 1.0),
        'meta_tokens': nrm(ks[1], (N_META, D_MODEL), 1.0),
        'norm_mix_g': 1.0 + nrm(ks[2], (Dp, D_MODEL), 0.02),
        'w_in': nrm(ks[3], (Dp, D_MODEL, D_IN), D_MODEL ** -0.5),
        'b_in': nrm(ks[4], (Dp, D_IN), 0.02),
        'attn_sinks': nrm(ks[5], (Dp, ATTN_Q_HEADS), 1.0),
        'rwkv_mix': unif(ks[6], (Dp, RWKV_PROJ), 0.0, 1.0),
        'rwkv_w0': unif(ks[7], (Dp, RWKV_DIM), -6.0, -1.0),
        'rwkv_w2': nrm(ks[8], (Dp, DECAY_LORA, RWKV_DIM), 0.1 * DECAY_LORA ** -0.5),
        'rwkv_a0': nrm(ks[9], (Dp, RWKV_DIM), 0.1),
        'rwkv_a2': nrm(ks[10], (Dp, AAA_LORA, RWKV_DIM), 0.5 * AAA_LORA ** -0.5),
        'rwkv_g2': nrm(ks[11], (Dp, GATE_LORA, RWKV_DIM), GATE_LORA ** -0.5),
        'rwkv_k_k': 0.85 + nrm(ks[12], (Dp, RWKV_DIM), 0.02),
        'rwkv_k_a': 1.0 + nrm(ks[13], (Dp, RWKV_DIM), 0.02),
        'rwkv_r_k': -0.04 + nrm(ks[14], (Dp, RWKV_HEADS, RWKV_HEAD), 0.02),
        'rwkv_ln_w': 1.0 + nrm(ks[15], (Dp, RWKV_DIM), 0.02),
        'rwkv_ln_b': nrm(ks[16], (Dp, RWKV_DIM), 0.02),
        'w_br_attn': nrm(ks[17], (Dp, Q_W, D_MODEL), Q_W ** -0.5),
        'w_br_rwkv': nrm(ks[18], (Dp, RWKV_DIM, D_MODEL), RWKV_DIM ** -0.5),
        'w_o': nrm(ks[19], (Dp, D_MODEL, D_MODEL), D_MODEL ** -0.5),
        'norm_ffn_g': 1.0 + nrm(ks[20], (Dp, D_MODEL), 0.02),
        'w_ffn_gate': nrm(ks[21], (Dp, D_MODEL, D_FF), D_MODEL ** -0.5),
        'w_ffn_up': nrm(ks[22], (Dp, D_MODEL, D_FF), D_MODEL ** -0.5),
        'w_ffn_down': nrm(ks[23], (Dp, D_FF, D_MODEL), D_FF ** -0.5),
        'norm_final_g': 1.0 + nrm(ks[24], (D_MODEL,), 0.02),
    }


def reference(x, meta_tokens, norm_mix_g, w_in, b_in, attn_sinks, rwkv_mix, rwkv_w0, rwkv_w2,
              rwkv_a0, rwkv_a2, rwkv_g2, rwkv_k_k, rwkv_k_a, rwkv_r_k, rwkv_ln_w, rwkv_ln_b,
              w_br_attn, w_br_rwkv, w_o, norm_ffn_g, w_ffn_gate, w_ffn_up, w_ffn_down,
              norm_final_g):
    B = x.shape[0]
    meta = jnp.broadcast_to(meta_tokens.astype(x.dtype)[None], (B, N_META, D_MODEL))
    h = jnp.concatenate([meta, x], axis=1)
    L = h.shape[1]
    pos = jnp.arange(L, dtype=jnp.int32)
    for layer in range(DEPTH):
        u = rms_norm(h, norm_mix_g[layer])
        proj = u @ w_in[layer] + b_in[layer]
        p_attn = proj[..., :ATTN_PROJ]
        p_rwkv = proj[..., ATTN_PROJ:ATTN_PROJ + RWKV_PROJ]
        gates = jax.nn.sigmoid(proj[..., ATTN_PROJ + RWKV_PROJ:].astype(jnp.float32)).astype(h.dtype)
        q = p_attn[..., :Q_W].reshape(B, L, ATTN_Q_HEADS, HEAD_DIM)
        k = p_attn[..., Q_W:Q_W + KV_W].reshape(B, L, ATTN_KV_HEADS, HEAD_DIM)
        v = p_attn[..., Q_W + KV_W:].reshape(B, L, ATTN_KV_HEADS, HEAD_DIM)
        q = partial_rope(q, pos)
        k = partial_rope(k, pos)
        y_attn = sliding_window_sink_attention(q, k, v, attn_sinks[layer])
        y_rwkv = rwkv7_time_mix(p_rwkv, rwkv_mix[layer], rwkv_w0[layer], rwkv_w2[layer],
                                rwkv_a0[layer], rwkv_a2[layer], rwkv_g2[layer], rwkv_k_k[layer],
                                rwkv_k_a[layer], rwkv_r_k[layer], rwkv_ln_w[layer],
                                rwkv_ln_b[layer])
        merged = (gates[..., :D_MODEL] * (y_attn @ w_br_attn[layer])
                  + gates[..., D_MODEL:] * (y_rwkv @ w_br_rwkv[layer]))
        h = h + merged @ w_o[layer]
        f = rms_norm(h, norm_ffn_g[layer])
        h = h + (jax.nn.silu(f @ w_ffn_gate[layer]) * (f @ w_ffn_up[layer])) @ w_ffn_down[layer]
    return rms_norm(h, norm_final_g)[:, N_META:]
```

```python
import numpy as np
import ml_dtypes
from contextlib import ExitStack
import concourse.bass as bass
import concourse.mybir as mybir
from concourse.bass_utils import run_bass_kernel_spmd

F32 = mybir.dt.float32
BF16 = mybir.dt.bfloat16
AF = mybir.ActivationFunctionType
ALU = mybir.AluOpType
AX = mybir.AxisListType

NTILES = 33
D = 1024
DFF = 2816
NFC = 22
RMS_EPS = 1e-6
LN_EPS = 64e-5
CFAC = -float(np.exp(-0.5))
NEG = -1e30
MD = F32

C_ID = 0
C_MB = 128
C_ML = 256
C_I64 = 320
C_OBD = 384
C_TRI = 512
C_TRI0 = 640
C_AM = 768
C_ROPE = 768 + 816
C_END = C_ROPE + 33 * 16
P_GMIX, P_GFFN, P_BFM, P_BG, P_MIX, P_A0, P_KK, P_KA, P_RK, P_END = 0, 8, 16, 32, 48, 64, 68, 72, 76, 80
RA_BQ, RA_W0, RA_SK, RA_END = 0, 768, 1280, 1288


class Tile:
    def __init__(self, t, name, excl=False):
        self.t = t
        self.name = name
        self.w = None
        self.r = {}
        self.excl = excl

    def __getitem__(self, i):
        return self.t[i]


class Chan:
    def __init__(self, sem, key):
        self.sem = sem
        self.key = key
        self.count = 0


class Sched:
    def __init__(self, nc, es):
        self.nc = nc
        self.es = es
        self.E = {}
        for name, eng in (("pe", nc.tensor), ("act", nc.scalar), ("dve", nc.vector),
                          ("pool", nc.gpsimd), ("sp", nc.sync)):
            sem = es.enter_context(nc.semaphore("sem_" + name))
            self.E[name] = dict(eng=eng, sem=sem, count=0, seen={}, name=name)
        self.chans = []

    def chan(self, name):
        c = Chan(self.es.enter_context(self.nc.semaphore("ch_" + name)), "ch_" + name)
        self.chans.append(c)
        return c

    def _waits(self, E, R, W):
        deps = {}

        def add(d):
            key, val, sem = d
            if key not in deps or deps[key][0] < val:
                deps[key] = (val, sem)
        for t in R:
            if t.w is not None:
                add(t.w)
            if t.excl:
                for key, (val, sem) in t.r.items():
                    if key != E["name"]:
                        add((key, val, sem))
        for t in W:
            if t.w is not None:
                add(t.w)
            for key, (val, sem) in t.r.items():
                add((key, val, sem))
        for key, (val, sem) in deps.items():
            if key == "pe" and E["name"] == "pe":
                continue
            if E["seen"].get(key, 0) < val:
                E["eng"].wait_ge(sem, val)
                E["seen"][key] = val

    def op(self, ename, fns, R=(), W=()):
        E = self.E[ename]
        self._waits(E, R, W)
        if not isinstance(fns, (list, tuple)):
            fns = [fns]
        inst = None
        for f in fns:
            inst = f(E["eng"])
        E["count"] += 1
        inst.then_inc(E["sem"], 1)
        for t in W:
            t.w = (ename, E["count"], E["sem"])
            t.r = {}
        for t in R:
            if t not in W:
                t.r[ename] = (E["count"], E["sem"])

    def dma(self, qname, chan, fn, R=(), W=()):
        E = self.E[qname]
        self._waits(E, R, W)
        inst = fn(E["eng"])
        chan.count += 16
        inst.then_inc(chan.sem, 16)
        for t in W:
            t.w = (chan.key, chan.count, chan.sem)
            t.r = {}
        for t in R:
            t.r[chan.key] = (chan.count, chan.sem)

    def finalize(self, chan, tiles):
        for t in tiles:
            t.w = (chan.key, chan.count, chan.sem)

    def barrier(self):
        for name, E in self.E.items():
            for oname, O in self.E.items():
                if oname == name or O["count"] == 0:
                    continue
                if E["seen"].get(oname, 0) < O["count"]:
                    E["eng"].wait_ge(O["sem"], O["count"])
                    E["seen"][oname] = O["count"]
            for c in self.chans:
                if c.count and E["seen"].get(c.key, 0) < c.count:
                    E["eng"].wait_ge(c.sem, c.count)
                    E["seen"][c.key] = c.count


def build_program(nt=NTILES, phases=("A1", "A2", "B"), dbg=None, dbg_n=-1, md=None, scr_ext=False, stop=None):
    global MD
    if md is not None:
        MD = md
    nc = bass.Bass("TRN2", target_bir_lowering=False)

    def din(name, shape, dt=F32):
        return nc.dram_tensor(name, list(shape), dt, kind="ExternalInput").ap()

    xe = din("xe", [NTILES * 128, D])
    w_qkv = din("w_qkv", [D, 768])
    w_fm = din("w_fm", [D, 2048])
    w_gate = din("w_gate", [D, 2048])
    w_ba = din("w_ba", [512, D])
    w_br = din("w_br", [512, D])
    w_o = din("w_o", [D, D])
    w_fg = din("w_fg", [D, DFF])
    w_fu = din("w_fu", [D, DFF])
    w_fd = din("w_fd", [DFF, D])
    w2d = din("w2", [64, 512])
    a2d = din("a2", [64, 512])
    g2d = din("g2p", [256, 512])
    pfmd = din("pfm", [128, P_END])
    rowsAd = din("rowsA", [1, RA_END])
    gfind = din("gfin", [1, D])
    lnstd = din("lnst", [128, 2 * 4 * 64])
    cstd = din("cst", [128, C_END])
    outd = nc.dram_tensor("out", [(NTILES - 1) * 128, D], F32, kind="ExternalOutput").ap()
    skind = "ExternalOutput" if scr_ext else "Internal"
    yscr = nc.dram_tensor("yscr", [NTILES - 1, 128, 8 * 128], BF16, kind=skind).ap()
    mscr = nc.dram_tensor("mscr", [NTILES - 1, 128, 8 * 128], BF16, kind=skind).ap()
    dbg_out = {}
    if dbg:
        for name, shape in dbg.items():
            dbg_out[name] = nc.dram_tensor("dbg_" + name, list(shape), F32, kind="ExternalOutput").ap()

    with ExitStack() as es0:
        S = Sched(nc, es0)
        ch_w = S.chan("w")
        ch_x = [S.chan("x0"), S.chan("x1")]
        ch_y = [S.chan("y0"), S.chan("y1")]
        ch_st = [S.chan("s0"), S.chan("s1")]
        ch_dbg = S.chan("dbg")
        yscr_t = [Tile(None, "yscr%d" % i) for i in range(NTILES - 1)]
        mscr_t = [Tile(None, "mscr%d" % i) for i in range(NTILES - 1)]

        def V(fn, R=(), W=()):
            S.op("dve", fn, R, W)

        def A(fn, R=(), W=()):
            S.op("act", fn, R, W)

        def G(fn, R=(), W=()):
            S.op("pool", fn, R, W)

        def PE(fns, R=(), W=()):
            S.op("pe", fns, R, W)

        def mk_alloc(es, pfx):
            def sb(name, shape, dt=F32):
                return Tile(es.enter_context(nc.sbuf_tensor(pfx + name, list(shape), dt)), pfx + name)
            return sb

        def dump(name, tile_ap, tiles):
            if name in dbg_out:
                S.dma("sp", ch_dbg, lambda e: e.dma_start(out=dbg_out[name], in_=tile_ap), R=tiles, W=[])

        def norm_T(x, xs, junk, st4, cst, pfm, gcol, TR2, uT):
            A(lambda e: e.activation(out=junk[:], in_=x[:], func=AF.Square, accum_out=st4[:, 0:1]), R=[x], W=[junk, st4])
            A(lambda e: e.activation(out=st4[:, 1:2], in_=st4[:, 0:1], func=AF.Sqrt, scale=1.0 / D, bias=cst[:, C_END:C_END + 1]),
              R=[st4, cst], W=[st4])
            V(lambda e: e.reciprocal(out=st4[:, 2:3], in_=st4[:, 1:2]), R=[st4], W=[st4])
            A(lambda e: e.activation(out=xs[:], in_=x[:], func=AF.Identity, scale=st4[:, 2:3], bias=cst[:, C_END + 2:C_END + 3]),
              R=[x, st4, cst], W=[xs])
            for h in range(2):
                PE([lambda e, c=c: e.transpose(TR2[h][:, (c % 4) * 128:(c % 4 + 1) * 128], xs[:, c * 128:(c + 1) * 128], cst[:, C_ID:C_ID + 128])
                    for c in range(4 * h, 4 * h + 4)], R=[xs, cst], W=[TR2[h]])
                V(lambda e, h=h: e.tensor_tensor(out=uT[:, 4 * h:4 * h + 4, :], in0=TR2[h][:].rearrange("p (c k) -> p c k", k=128),
                                                 in1=pfm[:, gcol + 4 * h:gcol + 4 * h + 4].unsqueeze(2).broadcast_to([128, 4, 128]), op=ALU.mult),
                  R=[TR2[h], pfm], W=[uT])

        def load_consts(sb, with_rowsA):
            cst = sb("cst", [128, C_END + 4])
            pfm = sb("pfm", [128, P_END])
            G(lambda e: e.memset(cst[:, C_END:C_END + 1], RMS_EPS), W=[cst])
            G(lambda e: e.memset(cst[:, C_END + 1:C_END + 2], LN_EPS), W=[cst])
            G(lambda e: e.memset(cst[:, C_END + 2:C_END + 4], 0.0), W=[cst])
            S.dma("sp", ch_w, lambda e: e.dma_start(out=cst[:, 0:C_END], in_=cstd), W=[cst])
            S.dma("sp", ch_w, lambda e: e.dma_start(out=pfm[:], in_=pfmd), W=[pfm])
            return cst, pfm

        def wload(tile_, out_ap, in_ap):
            S.dma("pool", ch_w, lambda e: e.dma_start(out=out_ap, in_=in_ap), W=[tile_])

        if "A1" in phases:
            with ExitStack() as es:
                sb = mk_alloc(es, "a1_")
                cst, pfm = load_consts(sb, True)
                rowsA = sb("rowsA", [128, RA_END])
                S.dma("sp", ch_w, lambda e: e.dma_start(out=rowsA[:], in_=rowsAd.broadcast_to([128, RA_END])), W=[rowsA])
                lnst = sb("lnst", [128, 2, 4, 64])
                S.dma("sp", ch_w, lambda e: e.dma_start(out=lnst[:].rearrange("p a c v -> p (a c v)"), in_=lnstd), W=[lnst])
                wqkv = sb("wqkv", [128, 8, 768], BF16)
                wfm = sb("wfm", [128, 8, 2048], BF16)
                w2 = sb("w2", [64, 512])
                a2 = sb("a2", [64, 512])
                g2 = sb("g2", [128, 2, 512])
                S.dma("sp", ch_w, lambda e: e.dma_start(out=w2[:], in_=w2d), W=[w2])
                S.dma("sp", ch_w, lambda e: e.dma_start(out=a2[:], in_=a2d), W=[a2])
                S.dma("sp", ch_w, lambda e: e.dma_start(out=g2[:], in_=g2d.rearrange("(c p) n -> p c n", p=128)), W=[g2])
                wload(wqkv, wqkv[:], w_qkv.rearrange("(c p) n -> p c n", p=128))
                for hh in range(2):
                    wload(wfm, wfm[:, 4 * hh:4 * hh + 4, :], w_fm.rearrange("(c p) n -> p c n", p=128)[:, 4 * hh:4 * hh + 4, :])
                identb = sb("identb", [128, 128], BF16)
                ones2 = sb("ones2", [128, 2])
                G(lambda e: e.memset(ones2[:], 1.0), W=[ones2])
                S.finalize(ch_w, [cst, pfm, rowsA, lnst, wqkv, wfm, w2, a2, g2])
                V(lambda e: e.tensor_copy(identb[:], cst[:, C_ID:C_ID + 128]), R=[cst], W=[identb])
                if stop == "loads":
                    nt = 0

                PS = []
                for i in range(8):
                    PS.append(Tile(es.enter_context(nc.psum_tensor("ps%d" % i, [128, 512], F32)), "ps%d" % i, excl=True))

                xb = [sb("xb0", [128, D]), sb("xb1", [128, D])]
                xs = sb("xs", [128, D])
                junk = sb("junk", [128, D], BF16)
                st4 = sb("st4", [128, 4])
                uT = sb("uT", [128, 8, 128], BF16)
                stg = sb("stg", [128, 16, 129])
                pf = sb("pf", [128, 16, 128])
                qkv = sb("qkv", [128, 768])
                rtmp = sb("rtmp", [128, 4, 10, 8])
                qT = sb("qT", [128, 4, 128], BF16)
                Kbuf = sb("Kbuf", [128, 272], BF16)
                Vbuf = sb("Vbuf", [128, 3, 128], BF16)
                Pb = [sb("Pb0", [128, 272], BF16), sb("Pb1", [128, 272], BF16)]
                PT = [sb("PT0", [128, 3, 128], BF16), sb("PT1", [128, 3, 128], BF16)]
                sm = sb("sm", [128, 5, 8])
                yat = sb("yat", [128, 8, 64])
                yT = [sb("yT0", [128, 8, 128], BF16), sb("yT1", [128, 8, 128], BF16)]
                th = sb("th", [64, 128])
                sgd = sb("sgd", [128, 2, 128])
                B1 = sb("B1", [128, 4, 128]); B2 = sb("B2", [128, 4, 128]); B3 = sb("B3", [128, 4, 128])
                B4 = sb("B4", [128, 4, 128]); B5 = sb("B5", [128, 4, 128])
                arT = sb("arT", [128, 4, 2, 2, 64], MD)
                BT = sb("BT", [128, 4, 128], MD); KT = sb("KT", [128, 4, 128], MD)
                BH = sb("BH", [128, 4, 128], MD); KH = sb("KH", [128, 4, 128], MD)
                cumC = sb("cumC", [128, 4, 2]); WC = sb("WC", [128, 4, 2])
                rk = sb("rk", [128, 2, 4])
                Ast = [sb("Ast0", [128, 4, 64], MD), sb("Ast1", [128, 4, 64], MD)]
                Nst = [sb("Nst0", [128, 4, 64], MD), sb("Nst1", [128, 4, 64], MD)]
                NB = sb("NB", [128, 4, 2, 64], MD); NK = sb("NK", [128, 4, 2, 64], MD)
                Pc = [sb("Pc0", [128, 4, 64], MD), sb("Pc1", [128, 4, 64], MD)]
                Vst = sb("Vst", [128, 4, 64], MD); Vst32 = sb("Vst32", [128, 4, 64])
                BKst = sb("BKst", [128, 2, 4, 64], MD)
                R1 = sb("R1", [128, 4, 64], MD); Ust = sb("Ust", [128, 4, 64], MD)
                Yst = sb("Yst", [128, 4, 64]); yc = sb("yc", [128, 4, 64]); ysq = sb("ysq", [128, 4, 64])
                ST32 = sb("ST32", [128, 4, 64]); STm = sb("STm", [128, 4, 64], MD) if MD != F32 else ST32
                gst = sb("gst", [128, 6, 4])
                G(lambda e: e.memset(ST32[:], 0.0), W=[ST32])
                if MD != F32:
                    G(lambda e: e.memset(STm[:], 0.0), W=[STm])
                G(lambda e: e.memset(stg[:], 0.0), W=[stg])
                G(lambda e: e.memset(Vbuf[:], 0.0), W=[Vbuf])
                G(lambda e: e.memset(Kbuf[:], 0.0), W=[Kbuf])

                def v3(ap2d, k=64):
                    return ap2d.rearrange("p (c k) -> p c k", k=k)

                def cq(t):
                    return t.rearrange("p c (q t) -> p c q t", q=2)

                for n in range(nt):
                    xt = xb[n % 2]
                    S.dma("sp", ch_x[n % 2], lambda e, n=n, xt=xt: e.dma_start(out=xt[:], in_=xe[n * 128:(n + 1) * 128, :]), W=[xt])
                    norm_T(xt, xs, junk, st4, cst, pfm, P_GMIX, [PS[0], PS[1]], uT)
                    if stop == "norm":
                        break
                    PE([lambda e, kc=kc: e.matmul(PS[2][:, 0:512], uT[:, kc, :], wqkv[:, kc, 0:512], start=(kc == 0), stop=(kc == 7)) for kc in range(8)],
                       R=[uT, wqkv], W=[PS[2]])
                    PE([lambda e, kc=kc: e.matmul(PS[3][:, 0:256], uT[:, kc, :], wqkv[:, kc, 512:768], start=(kc == 0), stop=(kc == 7)) for kc in range(8)],
                       R=[uT, wqkv], W=[PS[3]])
                    V(lambda e: e.tensor_tensor(out=qkv[:, 0:512], in0=PS[2][:, 0:512], in1=rowsA[:, RA_BQ:RA_BQ + 512], op=ALU.add), R=[PS[2], rowsA], W=[qkv])
                    V(lambda e: e.tensor_tensor(out=qkv[:, 512:768], in0=PS[3][:, 0:256], in1=rowsA[:, RA_BQ + 512:RA_BQ + 768], op=ALU.add), R=[PS[3], rowsA], W=[qkv])
                    for g in range(4):
                        bank = PS[2 + (g % 2)]
                        fns = []
                        for i in range(4):
                            col = (4 * g + i) * 128
                            for kc in range(8):
                                fns.append(lambda e, i=i, col=col, kc=kc, bank=bank: e.matmul(bank[:, i * 128:(i + 1) * 128], wfm[:, kc, col:col + 128], uT[:, kc, :],
                                                                                             start=(kc == 0), stop=(kc == 7)))
                        PE(fns, R=[uT, wfm], W=[bank])
                        V(lambda e, g=g, bank=bank: e.tensor_tensor(out=stg[:, 4 * g:4 * g + 4, 1:129], in0=v3(bank[:], 128),
                                                                    in1=pfm[:, P_BFM + 4 * g:P_BFM + 4 * g + 4].unsqueeze(2).broadcast_to([128, 4, 128]), op=ALU.add),
                          R=[bank, pfm], W=[stg])
                    if n == 0:
                        G(lambda e: e.memset(stg[:, :, 1:113], 0.0), W=[stg])
                    G(lambda e: e.tensor_tensor(out=pf[:], in0=stg[:, :, 0:128], in1=stg[:, :, 1:129], op=ALU.subtract), R=[stg], W=[pf])
                    G(lambda e: e.tensor_tensor(out=pf[:], in0=pf[:], in1=pfm[:, P_MIX:P_MIX + 16].unsqueeze(2).broadcast_to([128, 16, 128]), op=ALU.mult),
                      R=[pfm], W=[pf])
                    V(lambda e: e.tensor_tensor(out=pf[:], in0=pf[:], in1=stg[:, :, 1:129], op=ALU.add), R=[stg], W=[pf])
                    V(lambda e: e.tensor_copy(stg[:, :, 0:1], stg[:, :, 128:129]), R=[], W=[stg])
                    if n == dbg_n:
                        dump("pf", pf[:].rearrange("p c t -> p (c t)"), [pf])
                        dump("qkv0", qkv[:], [qkv])

                    if stop == "inproj":
                        break
                    slot = n % 2
                    q10 = qkv[:, 0:640].rearrange("p (h d) -> p h d", d=64)
                    cosb = cst[:, C_ROPE + 16 * n:C_ROPE + 16 * n + 8].unsqueeze(1).broadcast_to([128, 10, 8])
                    sinb = cst[:, C_ROPE + 16 * n + 8:C_ROPE + 16 * n + 16].unsqueeze(1).broadcast_to([128, 10, 8])
                    G(lambda e: e.tensor_tensor(out=rtmp[:, 0], in0=q10[:, :, 0:8], in1=cosb, op=ALU.mult), R=[qkv, cst], W=[rtmp])
                    G(lambda e: e.tensor_tensor(out=rtmp[:, 1], in0=q10[:, :, 8:16], in1=sinb, op=ALU.mult), R=[qkv, cst], W=[rtmp])
                    G(lambda e: e.tensor_tensor(out=rtmp[:, 2], in0=q10[:, :, 8:16], in1=cosb, op=ALU.mult), R=[qkv, cst], W=[rtmp])
                    G(lambda e: e.tensor_tensor(out=rtmp[:, 3], in0=q10[:, :, 0:8], in1=sinb, op=ALU.mult), R=[qkv, cst], W=[rtmp])
                    G(lambda e: e.tensor_tensor(out=q10[:, :, 0:8], in0=rtmp[:, 0], in1=rtmp[:, 1], op=ALU.subtract), R=[rtmp], W=[qkv])
                    G(lambda e: e.tensor_tensor(out=q10[:, :, 8:16], in0=rtmp[:, 2], in1=rtmp[:, 3], op=ALU.add), R=[rtmp], W=[qkv])
                    PE([lambda e, c=c: e.transpose(PS[4][:, c * 128:(c + 1) * 128], qkv[:, c * 128:(c + 1) * 128], cst[:, C_ID:C_ID + 128]) for c in range(4)],
                       R=[qkv, cst], W=[PS[4]])
                    PE(lambda e: e.transpose(PS[5][:, 0:128], qkv[:, 512:640], cst[:, C_ID:C_ID + 128]), R=[qkv, cst], W=[PS[5]])
                    A(lambda e: e.activation(out=qT[:], in_=v3(PS[4][:], 128), func=AF.Copy, scale=0.125), R=[PS[4]], W=[qT])
                    A(lambda e: e.activation(out=Kbuf[:, slot * 128:(slot + 1) * 128], in_=PS[5][:, 0:128], func=AF.Copy), R=[PS[5]], W=[Kbuf])
                    V(lambda e: e.tensor_copy(Vbuf[:, slot, :], qkv[:, 640:768]), R=[qkv], W=[Vbuf])
                    if n == 0:
                        A(lambda e: e.activation(out=Kbuf[:, 256:272], in_=PS[5][:, 112:128], func=AF.Copy), R=[PS[5]], W=[Kbuf])
                        PE(lambda e: e.matmul(PS[5][0:16, 128:256], cst[:, C_ID + 112:C_ID + 128], qkv[:, 640:768], start=True, stop=True), R=[qkv, cst], W=[PS[5]])
                        V(lambda e: e.tensor_copy(Vbuf[0:16, 2, :], PS[5][0:16, 128:256]), R=[PS[5]], W=[Vbuf])
                    yTn = yT[n % 2]
                    if n >= 1:
                        mvar = 2 if n == 1 else (0 if n % 2 == 0 else 1)
                        mask = cst[:, C_AM + 272 * mvar:C_AM + 272 * (mvar + 1)]
                        for s in range(8):
                            c, j = s // 2, s % 2
                            sl = slice(64 * j, 64 * j + 64)
                            SC = PS[4 + (s % 2)]
                            Pk = Pb[s % 2]
                            PTk = PT[s % 2]
                            PE(lambda e, c=c, sl=sl, SC=SC: e.matmul(SC[:, 0:272], qT[sl, c, :], Kbuf[sl, 0:272], start=True, stop=True), R=[qT, Kbuf], W=[SC])
                            V(lambda e, SC=SC: e.tensor_tensor(out=SC[:, 0:272], in0=SC[:, 0:272], in1=mask, op=ALU.add), R=[cst], W=[SC])
                            V(lambda e, SC=SC, s=s: e.tensor_reduce(out=sm[:, 0, s:s + 1], in_=SC[:, 0:272], axis=AX.X, op=ALU.max), R=[SC], W=[sm])
                            V(lambda e, s=s: e.tensor_scalar(out=sm[:, 1, s:s + 1], in0=sm[:, 0, s:s + 1], scalar1=rowsA[:, RA_SK + s:RA_SK + s + 1], scalar2=-1.0,
                                                             op0=ALU.max, op1=ALU.mult), R=[rowsA], W=[sm])
                            A(lambda e, SC=SC, Pk=Pk, s=s: e.activation(out=Pk[:], in_=SC[:, 0:272], func=AF.Exp, bias=sm[:, 1, s:s + 1], scale=1.0,
                                                                        accum_out=sm[:, 2, s:s + 1]), R=[SC], W=[Pk, sm])
                            A(lambda e, s=s: e.activation(out=sm[:, 3, s:s + 1], in_=rowsA[:, RA_SK + s:RA_SK + s + 1], func=AF.Exp, bias=sm[:, 1, s:s + 1], scale=1.0),
                              R=[rowsA], W=[sm])
                            PE([lambda e, Pk=Pk, b=b, nk=nk: e.matmul(PS[6][0:nk, b * 128:(b + 1) * 128], Pk[:, b * 128:b * 128 + nk], identb[:], start=True, stop=True)
                                for b, nk in ((0, 128), (1, 128), (2, 16))], R=[Pk, identb], W=[PS[6]])
                            A(lambda e, PTk=PTk: e.activation(out=PTk[:, 0:2, :], in_=v3(PS[6][:, 0:256], 128), func=AF.Copy), R=[PS[6]], W=[PTk])
                            A(lambda e, PTk=PTk: e.activation(out=PTk[0:16, 2, :], in_=PS[6][0:16, 256:384], func=AF.Copy), R=[PS[6]], W=[PTk])
                            PE([lambda e, PTk=PTk, s=s, sl=sl: e.matmul(PS[7][:, s * 64:(s + 1) * 64], PTk[:, 0, :], Vbuf[:, 0, sl], start=True, stop=False),
                                lambda e, PTk=PTk, s=s, sl=sl: e.matmul(PS[7][:, s * 64:(s + 1) * 64], PTk[:, 1, :], Vbuf[:, 1, sl], start=False, stop=False),
                                lambda e, PTk=PTk, s=s, sl=sl: e.matmul(PS[7][:, s * 64:(s + 1) * 64], PTk[0:16, 2, :], Vbuf[0:16, 2, sl], start=False, stop=True)],
                               R=[PTk, Vbuf], W=[PS[7]])
                        V(lambda e: e.tensor_tensor(out=sm[:, 2, :], in0=sm[:, 2, :], in1=sm[:, 3, :], op=ALU.add), R=[], W=[sm])
                        V(lambda e: e.reciprocal(out=sm[:, 4, :], in_=sm[:, 2, :]), R=[], W=[sm])
                        V(lambda e: e.tensor_tensor(out=yat[:], in0=v3(PS[7][:], 64), in1=sm[:, 4, :].unsqueeze(2).broadcast_to([128, 8, 64]), op=ALU.mult),
                          R=[PS[7]], W=[yat, sm])
                        PE([lambda e, c=c: e.transpose(PS[6][:, c * 128:(c + 1) * 128], yat[:, 2 * c:2 * c + 2, :].rearrange("p a d -> p (a d)"), cst[:, C_ID:C_ID + 128])
                            for c in range(4)], R=[yat, cst], W=[PS[6]])
                        A(lambda e: e.activation(out=yTn[:, 0:4, :], in_=v3(PS[6][:], 128), func=AF.Copy), R=[PS[6]], W=[yTn])

                    if stop == "attn":
                        break
                    tri = cst[:, (C_TRI0 if n == 0 else C_TRI):(C_TRI0 if n == 0 else C_TRI) + 128]
                    A(lambda e: e.activation(out=th[:], in_=pf[0:64, 12, :], func=AF.Tanh), R=[pf], W=[th])
                    PE(lambda e: e.matmul(PS[2][:, 0:512], th[:], w2[:], start=True, stop=True), R=[th, w2], W=[PS[2]])
                    B4f = B4[:].rearrange("p c t -> p (c t)")
                    V(lambda e: e.tensor_tensor(out=B4f, in0=PS[2][:, 0:512], in1=rowsA[:, RA_W0:RA_W0 + 512], op=ALU.add), R=[PS[2], rowsA], W=[B4])
                    A(lambda e: e.activation(out=B4f, in_=B4f, func=AF.Sigmoid), R=[], W=[B4])
                    for q in range(2):
                        PE([lambda e, c=c, q=q: e.matmul(PS[q][:, c * 128:(c + 1) * 128], B4f[64 * q:64 * q + 64, c * 128:(c + 1) * 128],
                                                         tri[64 * q:64 * q + 64, :], start=True, stop=True) for c in range(4)], R=[B4, cst], W=[PS[q]])
                    if stop == "r_cums":
                        break

                    def cums(a):
                        return [PS[q][:].rearrange("p (c a t) -> p c a t", c=4, a=2)[:, :, a, :] for q in range(2)]
                    PE([lambda e, c=c: e.matmul(PS[3][:, c * 128:(c + 1) * 128], a2[:, c * 128:(c + 1) * 128], pf[0:64, 13, :], start=True, stop=True) for c in range(4)],
                       R=[pf, a2], W=[PS[3]])
                    for c in range(4):
                        A(lambda e, c=c: e.activation(out=B3[:, c, :], in_=PS[3][:, c * 128:(c + 1) * 128], func=AF.Sigmoid, bias=pfm[:, P_A0 + c:P_A0 + c + 1], scale=1.0),
                          R=[PS[3], pfm], W=[B3])
                    if stop == "r_a":
                        break
                    A(lambda e: e.activation(out=sgd[:, 0, :], in_=pf[:, 14, :], func=AF.Sigmoid), R=[pf], W=[sgd])
                    A(lambda e: e.activation(out=sgd[0:32, 1, :], in_=pf[0:32, 15, :], func=AF.Sigmoid), R=[pf], W=[sgd])
                    fns = []
                    for c in range(4):
                        fns.append(lambda e, c=c: e.matmul(PS[2][:, c * 128:(c + 1) * 128], g2[:, 0, c * 128:(c + 1) * 128], sgd[:, 0, :], start=True, stop=False))
                        fns.append(lambda e, c=c: e.matmul(PS[2][:, c * 128:(c + 1) * 128], g2[0:32, 1, c * 128:(c + 1) * 128], sgd[0:32, 1, :], start=False, stop=True))
                    PE(fns, R=[sgd, g2], W=[PS[2]])
                    A(lambda e: e.activation(out=B5[:].rearrange("p c t -> p (c t)"), in_=PS[2][:], func=AF.Copy), R=[PS[2]], W=[B5])
                    if stop == "r_g":
                        break
                    kview = pf[:, 4:8, :]
                    rview = pf[:, 0:4, :]
                    vview = pf[:, 8:12, :]

                    def bc(col):
                        return pfm[:, col:col + 4].unsqueeze(2).broadcast_to([128, 4, 128])
                    V(lambda e: e.tensor_tensor(out=B1[:], in0=kview, in1=bc(P_KK), op=ALU.mult), R=[pf, pfm], W=[B1])
                    G(lambda e: e.tensor_tensor(out=B2[:], in0=B1[:], in1=B1[:], op=ALU.mult), R=[B1], W=[B2])
                    PE([lambda e, c=c: e.matmul(PS[3][:, c * 128:(c + 1) * 128], cst[:, C_OBD:C_OBD + 128], B2[:, c, :], start=True, stop=True) for c in range(4)],
                       R=[B2, cst], W=[PS[3]])
                    A(lambda e: e.activation(out=B2[:].rearrange("p c t -> p (c t)"), in_=PS[3][:], func=AF.Sqrt), R=[PS[3]], W=[B2])
                    V(lambda e: e.tensor_scalar(out=B2[:], in0=B2[:], scalar1=1e-12, scalar2=None, op0=ALU.max), R=[], W=[B2])
                    V(lambda e: e.reciprocal(out=B2[:], in_=B2[:]), R=[], W=[B2])
                    V(lambda e: e.tensor_tensor(out=B1[:], in0=B1[:], in1=B2[:], op=ALU.mult), R=[B2], W=[B1])
                    if stop == "r_kk":
                        break
                    V(lambda e: e.scalar_tensor_tensor(out=B2[:], in0=B3[:], scalar=-1.0, in1=bc(P_KA), op0=ALU.add, op1=ALU.mult), R=[B3, pfm], W=[B2])
                    V(lambda e: e.scalar_tensor_tensor(out=kview, in0=B2[:], scalar=1.0, in1=kview, op0=ALU.add, op1=ALU.mult), R=[B2], W=[pf])
                    G(lambda e: e.tensor_tensor(out=B3[:], in0=B1[:], in1=B3[:], op=ALU.mult), R=[B1], W=[B3])
                    if stop == "r_kmod":
                        break
                    G(lambda e: e.tensor_tensor(out=B2[:], in0=rview, in1=kview, op=ALU.mult), R=[pf], W=[B2])
                    G(lambda e: e.tensor_tensor(out=B2[:], in0=B2[:], in1=bc(P_RK), op=ALU.mult), R=[pfm], W=[B2])
                    fns = []
                    for q in range(2):
                        for c in range(4):
                            for j in range(2):
                                sl = slice(64 * j, 64 * j + 64)
                                fns.append(lambda e, q=q, c=c, sl=sl: e.matmul(PS[5][sl, (q * 4 + c) * 2:(q * 4 + c) * 2 + 2], B2[sl, c, 64 * q:64 * q + 64], ones2[sl, :],
                                                                              start=True, stop=True))
                    PE(fns, R=[B2, ones2], W=[PS[5]])
                    V(lambda e: e.tensor_copy(rk[:].rearrange("p q c -> p (q c)"), PS[5][:, 0:16].rearrange("p (x two) -> p x two", two=2)[:, :, 0]), R=[PS[5]], W=[rk])
                    if stop == "r_rk":
                        break
                    cex = cums(1)
                    cin = cums(0)
                    B4q = cq(B4[:])
                    for hh in range(2):
                        A(lambda e, hh=hh: e.activation(out=B4[:, :, 64 * hh:64 * hh + 64], in_=cex[hh], func=AF.Exp), R=[PS[hh]], W=[B4])
                    V(lambda e: e.scalar_tensor_tensor(out=arT[:, :, :, 0, :], in0=cq(B1[:]), scalar=-1.0, in1=B4q, op0=ALU.mult, op1=ALU.mult), R=[B1, B4], W=[arT])
                    for hh in range(2):
                        A(lambda e, hh=hh: e.activation(out=B4[:, :, 64 * hh:64 * hh + 64], in_=cin[hh], func=AF.Exp), R=[PS[hh]], W=[B4])
                    V(lambda e: e.tensor_tensor(out=arT[:, :, :, 1, :], in0=cq(rview), in1=B4q, op=ALU.mult), R=[pf, B4], W=[arT])
                    for hh in range(2):
                        A(lambda e, hh=hh: e.activation(out=B4[:, :, 64 * hh:64 * hh + 64], in_=cin[hh], func=AF.Exp, scale=-1.0), R=[PS[hh]], W=[B4])
                    V(lambda e: e.tensor_tensor(out=BT[:], in0=B3[:], in1=B4[:], op=ALU.mult), R=[B3, B4], W=[BT])
                    V(lambda e: e.tensor_tensor(out=KT[:], in0=kview, in1=B4[:], op=ALU.mult), R=[pf, B4], W=[KT])
                    if stop == "r_exp":
                        break
                    for hh in range(2):
                        V(lambda e, hh=hh: e.tensor_copy(cumC[:, :, hh], cin[hh][:, :, 63]), R=[PS[hh]], W=[cumC])
                    for c in range(4):
                        for q in range(2):
                            A(lambda e, c=c, q=q: e.activation(out=B4[:, c, 64 * q:64 * q + 64], in_=cin[q][:, c, :], func=AF.Exp, scale=-1.0,
                                                               bias=cumC[:, c, q:q + 1]), R=[PS[q], cumC], W=[B4])
                    V(lambda e: e.tensor_tensor(out=BH[:], in0=B3[:], in1=B4[:], op=ALU.mult), R=[B3, B4], W=[BH])
                    V(lambda e: e.tensor_tensor(out=KH[:], in0=kview, in1=B4[:], op=ALU.mult), R=[pf, B4], W=[KH])
                    A(lambda e: e.activation(out=WC[:], in_=cumC[:], func=AF.Exp), R=[cumC], W=[WC])
                    if n == dbg_n:
                        dump("B5", B5[:].rearrange("p c t -> p (c t)"), [B5])
                        dump("B3", B3[:].rearrange("p c t -> p (c t)"), [B3])
                        dump("yat", yat[:].rearrange("p s d -> p (s d)"), [yat])

                    if stop == "rwkvprep":
                        break
                    mb4 = cst[:, C_MB:C_MB + 128].rearrange("p (a t) -> p a t", a=2).unsqueeze(1).broadcast_to([128, 4, 2, 64])
                    ml4 = cst[:, C_ML:C_ML + 64].unsqueeze(1).broadcast_to([128, 4, 64])
                    i64 = cst[:, C_I64:C_I64 + 64].unsqueeze(1).broadcast_to([128, 4, 64])
                    idm = cst if MD == F32 else identb

                    def hl(fn):
                        out = []
                        for c in range(4):
                            for j in range(2):
                                out += fn(c, slice(64 * j, 64 * j + 64), j)
                        return out

                    for q in range(2):
                        tq = slice(64 * q, 64 * q + 64)
                        PE(hl(lambda c, sl, j: [lambda e: e.matmul(PS[4][sl, c * 64:(c + 1) * 64], arT[sl, c, q, 0, :], BT[sl, c, tq], start=True, stop=True)]),
                           R=[arT, BT], W=[PS[4]])
                        PE(hl(lambda c, sl, j: [lambda e: e.matmul(PS[6][sl, c * 128:(c + 1) * 128], BT[sl, c, tq], arT[sl, c, q, :, :].rearrange("p a t -> p (a t)"),
                                                                   start=True, stop=True)]), R=[arT, BT], W=[PS[6]])
                        PE(hl(lambda c, sl, j: [lambda e: e.matmul(PS[7][sl, c * 128:(c + 1) * 128], KT[sl, c, tq], arT[sl, c, q, :, :].rearrange("p a t -> p (a t)"),
                                                                   start=True, stop=True)]), R=[arT, KT], W=[PS[7]])
                        V(lambda e: e.tensor_tensor(out=Ast[0][:], in0=v3(PS[4][:, 0:256]), in1=ml4, op=ALU.mult), R=[PS[4], cst], W=[Ast[0]])
                        V(lambda e: e.tensor_tensor(out=NB[:], in0=PS[6][:].rearrange("p (c a t) -> p c a t", c=4, a=2), in1=mb4, op=ALU.mult), R=[PS[6], cst], W=[NB])
                        V(lambda e: e.tensor_tensor(out=NK[:], in0=PS[7][:].rearrange("p (c a t) -> p c a t", c=4, a=2), in1=mb4, op=ALU.mult), R=[PS[7], cst], W=[NK])
                        G(lambda e: e.tensor_copy(Nst[0][:], NB[:, :, 0, :]), R=[NB], W=[Nst[0]])
                        G(lambda e: e.tensor_tensor(out=Pc[0][:], in0=NB[:, :, 0, :], in1=i64, op=ALU.add), R=[NB, cst], W=[Pc[0]])
                        if stop == "c_A":
                            break
                        idsl = lambda sl, j: (cst[sl, C_ID + 64 * j:C_ID + 64 * j + 64])
                        PE(hl(lambda c, sl, j: [lambda e: e.matmul(PS[2][sl, c * 64:(c + 1) * 64], pf[sl, 8 + c, tq], idsl(sl, j), start=True, stop=True)]),
                           R=[pf, cst], W=[PS[2]])
                        idm_sl = lambda sl, j: (idm[sl, (C_ID if MD == F32 else 0) + 64 * j:(C_ID if MD == F32 else 0) + 64 * j + 64])
                        PE(hl(lambda c, sl, j: [lambda e: e.matmul(PS[3][sl, c * 64:(c + 1) * 64], BH[sl, c, tq], idm_sl(sl, j), start=True, stop=True),
                                                lambda e: e.matmul(PS[3][sl, 256 + c * 64:256 + (c + 1) * 64], KH[sl, c, tq], idm_sl(sl, j), start=True, stop=True)]),
                           R=[BH, KH, idm], W=[PS[3]])
                        A(lambda e: e.activation(out=Vst32[:], in_=v3(PS[2][:, 0:256]), func=AF.Copy), R=[PS[2]], W=[Vst32])
                        if MD != F32:
                            A(lambda e: e.activation(out=Vst[:], in_=v3(PS[2][:, 0:256]), func=AF.Copy), R=[PS[2]], W=[Vst])
                            Vm = Vst
                        else:
                            Vm = Vst32
                        A(lambda e: e.activation(out=BKst[:], in_=PS[3][:].rearrange("p (a c t) -> p a c t", a=2, c=4), func=AF.Copy), R=[PS[3]], W=[BKst])
                        if stop == "c_T":
                            break
                        cur = 0
                        for lvl in range(1, 6):
                            nxt = 1 - cur
                            fns = hl(lambda c, sl, j: [lambda e: e.matmul(PS[5][sl, c * 64:(c + 1) * 64], Nst[cur][sl, c, :], Ast[cur][sl, c, :], start=True, stop=True)])
                            if lvl < 5:
                                fns += hl(lambda c, sl, j: [lambda e: e.matmul(PS[5][sl, 256 + c * 64:256 + (c + 1) * 64], Ast[cur][sl, c, :], Nst[cur][sl, c, :],
                                                                               start=True, stop=True)])
                            PE(fns, R=[Nst[cur], Ast[cur]], W=[PS[5]])
                            A(lambda e, nxt=nxt: e.activation(out=Ast[nxt][:], in_=v3(PS[5][:, 0:256]), func=AF.Copy), R=[PS[5]], W=[Ast[nxt]])
                            if lvl < 5:
                                V(lambda e, nxt=nxt: e.tensor_copy(Nst[nxt][:], v3(PS[5][:, 256:512])), R=[PS[5]], W=[Nst[nxt]])
                            pc, pn = Pc[(lvl - 1) % 2], Pc[lvl % 2]
                            PE(hl(lambda c, sl, j: [lambda e: e.matmul(PS[4][sl, 256 + c * 64:256 + (c + 1) * 64], Ast[nxt][sl, c, :], pc[sl, c, :], start=True, stop=True)]),
                               R=[Ast[nxt], pc], W=[PS[4]])
                            V(lambda e, pc=pc, pn=pn: e.tensor_tensor(out=pn[:], in0=v3(PS[4][:, 256:512]), in1=pc[:], op=ALU.add), R=[PS[4], pc], W=[pn])
                            cur = nxt
                        if stop == "c_chain":
                            break
                        TT = Pc[5 % 2]
                        PE(hl(lambda c, sl, j: [lambda e: e.matmul(PS[2][sl, 256 + c * 64:256 + (c + 1) * 64], arT[sl, c, q, 0, :], STm[sl, c, :], start=True, stop=False),
                                                lambda e: e.matmul(PS[2][sl, 256 + c * 64:256 + (c + 1) * 64], NK[sl, c, 0, :], Vm[sl, c, :], start=False, stop=True)]),
                           R=[arT, STm, NK, Vm], W=[PS[2]])
                        A(lambda e: e.activation(out=R1[:], in_=v3(PS[2][:, 256:512]), func=AF.Copy), R=[PS[2]], W=[R1])
                        PE(hl(lambda c, sl, j: [lambda e: e.matmul(PS[0][sl, c * 64:(c + 1) * 64], TT[sl, c, :], R1[sl, c, :], start=True, stop=True)]),
                           R=[TT, R1], W=[PS[0]])
                        A(lambda e: e.activation(out=Ust[:], in_=v3(PS[0][:, 0:256]), func=AF.Copy), R=[PS[0]], W=[Ust])
                        if stop == "c_U":
                            break
                        if n >= 1:
                            PE(hl(lambda c, sl, j: [lambda e: e.matmul(PS[0][sl, 256 + c * 64:256 + (c + 1) * 64], arT[sl, c, q, 1, :], STm[sl, c, :], start=True, stop=False),
                                                    lambda e: e.matmul(PS[0][sl, 256 + c * 64:256 + (c + 1) * 64], NB[sl, c, 1, :], Ust[sl, c, :], start=False, stop=False),
                                                    lambda e: e.matmul(PS[0][sl, 256 + c * 64:256 + (c + 1) * 64], NK[sl, c, 1, :], Vm[sl, c, :], start=False, stop=True)]),
                               R=[arT, STm, NB, Ust, NK, Vm], W=[PS[0]])
                            V(lambda e: e.tensor_copy(Yst[:], v3(PS[0][:, 256:512])), R=[PS[0]], W=[Yst])
                        PE(hl(lambda c, sl, j: [lambda e: e.matmul(PS[1][sl, c * 64:(c + 1) * 64], BKst[sl, 0, c, :], Ust[sl, c, :], start=True, stop=False),
                                                lambda e: e.matmul(PS[1][sl, c * 64:(c + 1) * 64], BKst[sl, 1, c, :], Vm[sl, c, :], start=False, stop=True)]),
                           R=[BKst, Ust, Vm], W=[PS[1]])
                        G(lambda e: e.tensor_tensor(out=ST32[:], in0=ST32[:], in1=WC[:, :, q:q + 1].broadcast_to([128, 4, 64]), op=ALU.mult), R=[WC], W=[ST32])
                        V(lambda e: e.tensor_tensor(out=ST32[:], in0=ST32[:], in1=v3(PS[1][:, 0:256]), op=ALU.add), R=[PS[1]], W=[ST32])
                        if MD != F32:
                            G(lambda e: e.tensor_copy(STm[:], ST32[:]), R=[ST32], W=[STm])
                        if stop == "c_S":
                            break
                        if n == dbg_n and q == 1:
                            dump("Yst", Yst[:].rearrange("p c v -> p (c v)"), [Yst])
                            dump("ST", ST32[:].rearrange("p c v -> p (c v)"), [ST32])
                        if n >= 1:
                            V(lambda e: e.tensor_reduce(out=gst[:, 0, :], in_=Yst[:], axis=AX.X, op=ALU.add), R=[Yst], W=[gst])
                            V(lambda e: e.tensor_scalar(out=gst[:, 1, :], in0=gst[:, 0, :], scalar1=-1.0 / 64, scalar2=None, op0=ALU.mult), R=[], W=[gst])
                            V(lambda e: e.tensor_tensor(out=yc[:], in0=Yst[:], in1=gst[:, 1, :].unsqueeze(2).broadcast_to([128, 4, 64]), op=ALU.add), R=[Yst], W=[yc, gst])
                            G(lambda e: e.tensor_tensor(out=ysq[:], in0=yc[:], in1=yc[:], op=ALU.mult), R=[yc], W=[ysq])
                            V(lambda e: e.tensor_reduce(out=gst[:, 2, :], in_=ysq[:], axis=AX.X, op=ALU.add), R=[ysq], W=[gst])
                            A(lambda e: e.activation(out=gst[:, 3, :], in_=gst[:, 2, :], func=AF.Sqrt, scale=1.0 / 64, bias=cst[:, C_END + 1:C_END + 2]), R=[cst], W=[gst])
                            V(lambda e: e.reciprocal(out=gst[:, 4, :], in_=gst[:, 3, :]), R=[], W=[gst])
                            V(lambda e: e.tensor_tensor(out=yc[:], in0=yc[:], in1=gst[:, 4, :].unsqueeze(2).broadcast_to([128, 4, 64]), op=ALU.mult), R=[], W=[yc, gst])
                            G(lambda e: e.tensor_tensor(out=yc[:], in0=yc[:], in1=lnst[:, 0], op=ALU.mult), R=[lnst], W=[yc])
                            G(lambda e: e.tensor_tensor(out=yc[:], in0=yc[:], in1=lnst[:, 1], op=ALU.add), R=[lnst], W=[yc])
                            V(lambda e: e.tensor_tensor(out=ysq[:], in0=Vst32[:], in1=rk[:, q, :].unsqueeze(2).broadcast_to([128, 4, 64]), op=ALU.mult), R=[Vst32, rk], W=[ysq])
                            V(lambda e: e.tensor_tensor(out=yc[:], in0=yc[:], in1=ysq[:], op=ALU.add), R=[ysq], W=[yc])
                            PE(hl(lambda c, sl, j: [lambda e: e.matmul(PS[1][sl, 256 + c * 64:256 + (c + 1) * 64], yc[sl, c, :], idsl(sl, j), start=True, stop=True)]),
                               R=[yc, cst], W=[PS[1]])
                            V(lambda e: e.tensor_tensor(out=yTn[:, 4:8, tq], in0=v3(PS[1][:, 256:512]), in1=B5[:, :, tq], op=ALU.mult), R=[PS[1], B5], W=[yTn])
                    if n >= 1:
                        S.dma("sp", ch_y[n % 2], lambda e, n=n, yTn=yTn: e.dma_start(out=yscr[n - 1], in_=yTn[:].rearrange("p c t -> p (c t)")), R=[yTn], W=[yscr_t[n - 1]])
                S.barrier()

        if "A2" in phases:
            with ExitStack() as es:
                sb = mk_alloc(es, "a2_")
                cst, pfm = load_consts(sb, False)
                wgate = sb("wgate", [128, 8, 2048], BF16)
                wba = sb("wba", [128, 4, D], BF16)
                wbr = sb("wbr", [128, 4, D], BF16)
                for hh in range(2):
                    wload(wgate, wgate[:, 4 * hh:4 * hh + 4, :], w_gate.rearrange("(c p) n -> p c n", p=128)[:, 4 * hh:4 * hh + 4, :])
                wload(wba, wba[:], w_ba.rearrange("(c p) n -> p c n", p=128))
                wload(wbr, wbr[:], w_br.rearrange("(c p) n -> p c n", p=128))
                S.finalize(ch_w, [cst, pfm, wgate, wba, wbr])
                PS = [Tile(es.enter_context(nc.psum_tensor("psb%d" % i, [128, 512], F32)), "psb%d" % i, excl=True) for i in range(8)]
                xb = [sb("xb0", [128, D]), sb("xb1", [128, D])]
                yT = [sb("yT0", [128, 8, 128], BF16), sb("yT1", [128, 8, 128], BF16)]
                xs = sb("xs", [128, D])
                junk = sb("junk", [128, D], BF16)
                st4 = sb("st4", [128, 4])
                uT = sb("uT", [128, 8, 128], BF16)
                sg = sb("sg", [128, 16, 128])
                t1 = sb("t1", [128, 8, 128])
                t2 = sb("t2", [128, 8, 128])
                mT = [sb("mT0", [128, 8, 128], BF16), sb("mT1", [128, 8, 128], BF16)]
                for n in range(1, nt):
                    xt = xb[n % 2]
                    yTn = yT[n % 2]
                    mTn = mT[n % 2]
                    S.dma("sp", ch_x[n % 2], lambda e, n=n, xt=xt: e.dma_start(out=xt[:], in_=xe[n * 128:(n + 1) * 128, :]), W=[xt])
                    S.dma("sp", ch_y[n % 2], lambda e, n=n, yTn=yTn: e.dma_start(out=yTn[:].rearrange("p c t -> p (c t)"), in_=yscr[n - 1]), R=[yscr_t[n - 1]], W=[yTn])
                    norm_T(xt, xs, junk, st4, cst, pfm, P_GMIX, [PS[0], PS[1]], uT)
                    for g in range(4):
                        bank = PS[2 + (g % 2)]
                        fns = []
                        for i in range(4):
                            col = (4 * g + i) * 128
                            for kc in range(8):
                                fns.append(lambda e, i=i, col=col, kc=kc, bank=bank: e.matmul(bank[:, i * 128:(i + 1) * 128], wgate[:, kc, col:col + 128], uT[:, kc, :],
                                                                                             start=(kc == 0), stop=(kc == 7)))
                        PE(fns, R=[uT, wgate], W=[bank])
                        for i in range(4):
                            A(lambda e, g=g, i=i, bank=bank: e.activation(out=sg[:, 4 * g + i, :], in_=bank[:, i * 128:(i + 1) * 128], func=AF.Sigmoid,
                                                                          bias=pfm[:, P_BG + 4 * g + i:P_BG + 4 * g + i + 1], scale=1.0), R=[bank, pfm], W=[sg])
                    for br, (wb, off) in enumerate(((wba, 0), (wbr, 4))):
                        for hh in range(2):
                            bank = PS[4 + 2 * br + hh]
                            fns = []
                            for i in range(4):
                                fc = 4 * hh + i
                                for kc in range(4):
                                    fns.append(lambda e, i=i, fc=fc, kc=kc, bank=bank, wb=wb, off=off: e.matmul(bank[:, i * 128:(i + 1) * 128], wb[:, kc, fc * 128:(fc + 1) * 128],
                                                                                                                yTn[:, off + kc, :], start=(kc == 0), stop=(kc == 3)))
                            PE(fns, R=[yTn, wb], W=[bank])
                    for hh in range(2):
                        V(lambda e, hh=hh: e.tensor_tensor(out=t1[:, 4 * hh:4 * hh + 4, :], in0=PS[4 + hh][:].rearrange("p (c t) -> p c t", c=4), in1=sg[:, 4 * hh:4 * hh + 4, :], op=ALU.mult),
                          R=[PS[4 + hh], sg], W=[t1])
                        V(lambda e, hh=hh: e.tensor_tensor(out=t2[:, 4 * hh:4 * hh + 4, :], in0=PS[6 + hh][:].rearrange("p (c t) -> p c t", c=4), in1=sg[:, 8 + 4 * hh:8 + 4 * hh + 4, :], op=ALU.mult),
                          R=[PS[6 + hh], sg], W=[t2])
                    G(lambda e, mTn=mTn: e.tensor_tensor(out=mTn[:], in0=t1[:], in1=t2[:], op=ALU.add), R=[t1, t2], W=[mTn])
                    S.dma("sp", ch_st[n % 2], lambda e, n=n, mTn=mTn: e.dma_start(out=mscr[n - 1], in_=mTn[:].rearrange("p c t -> p (c t)")), R=[mTn], W=[mscr_t[n - 1]])
                S.barrier()

        if "B" in phases:
            with ExitStack() as es:
                sb = mk_alloc(es, "b_")
                cst, pfm = load_consts(sb, False)
                gfin = sb("gfin", [128, D])
                S.dma("sp", ch_w, lambda e: e.dma_start(out=gfin[:], in_=gfind.broadcast_to([128, D])), W=[gfin])
                wo = sb("wo", [128, 8, D], BF16)
                wg = sb("wg", [128, 8, DFF], BF16)
                wu = sb("wu", [128, 8, DFF], BF16)
                wd = sb("wd", [128, NFC, D], BF16)
                for hh in range(2):
                    wload(wo, wo[:, 4 * hh:4 * hh + 4, :], w_o.rearrange("(c p) n -> p c n", p=128)[:, 4 * hh:4 * hh + 4, :])
                for wt, wsrc in ((wg, w_fg), (wu, w_fu)):
                    for hh in range(2):
                        for ch in range(2):
                            wload(wt, wt[:, 4 * hh:4 * hh + 4, ch * 1408:(ch + 1) * 1408],
                                  wsrc.rearrange("(c p) n -> p c n", p=128)[:, 4 * hh:4 * hh + 4, ch * 1408:(ch + 1) * 1408])
                for hh in range(2):
                    wload(wd, wd[:, 11 * hh:11 * hh + 11, :], w_fd.rearrange("(c p) n -> p c n", p=128)[:, 11 * hh:11 * hh + 11, :])
                S.finalize(ch_w, [cst, pfm, gfin, wo, wg, wu, wd])
                PS = [Tile(es.enter_context(nc.psum_tensor("psc%d" % i, [128, 512], F32)), "psc%d" % i, excl=True) for i in range(8)]
                xb = [sb("xb0", [128, D]), sb("xb1", [128, D])]
                mT = [sb("mT0", [128, 8, 128], BF16), sb("mT1", [128, 8, 128], BF16)]
                h1 = sb("h1", [128, D])
                xs = sb("xs", [128, D])
                junk = sb("junk", [128, D], BF16)
                st4 = sb("st4", [128, 4])
                fT = sb("fT", [128, 8, 128], BF16)
                sl_ = sb("silu", [128, 4, 128])
                aT = sb("aT", [128, NFC, 128], BF16)
                ob = [sb("ob0", [128, D]), sb("ob1", [128, D])]
                for n in range(1, nt):
                    xt = xb[n % 2]
                    mTn = mT[n % 2]
                    o = ob[n % 2]
                    S.dma("sp", ch_x[n % 2], lambda e, n=n, xt=xt: e.dma_start(out=xt[:], in_=xe[n * 128:(n + 1) * 128, :]), W=[xt])
                    S.dma("sp", ch_y[n % 2], lambda e, n=n, mTn=mTn: e.dma_start(out=mTn[:].rearrange("p c t -> p (c t)"), in_=mscr[n - 1]), R=[mscr_t[n - 1]], W=[mTn])
                    for hh in range(2):
                        PE([lambda e, kc=kc, hh=hh: e.matmul(PS[hh][:], mTn[:, kc, :], wo[:, kc, hh * 512:(hh + 1) * 512], start=(kc == 0), stop=(kc == 7)) for kc in range(8)],
                           R=[mTn, wo], W=[PS[hh]])
                        V(lambda e, hh=hh: e.tensor_tensor(out=h1[:, hh * 512:(hh + 1) * 512], in0=PS[hh][:], in1=xt[:, hh * 512:(hh + 1) * 512], op=ALU.add),
                          R=[PS[hh], xt], W=[h1])
                    norm_T(h1, xs, junk, st4, cst, pfm, P_GFFN, [PS[2], PS[3]], fT)
                    ngrp = (NFC + 3) // 4
                    for g in range(ngrp):
                        nchunk = min(4, NFC - 4 * g)
                        bg = PS[4 + 2 * (g % 2)]
                        bu = PS[5 + 2 * (g % 2)]
                        for bank, wt in ((bg, wg), (bu, wu)):
                            fns = []
                            for i in range(nchunk):
                                fc = 4 * g + i
                                for kc in range(8):
                                    fns.append(lambda e, i=i, fc=fc, kc=kc, bank=bank, wt=wt: e.matmul(bank[:, i * 128:(i + 1) * 128], wt[:, kc, fc * 128:(fc + 1) * 128], fT[:, kc, :],
                                                                                                       start=(kc == 0), stop=(kc == 7)))
                            PE(fns, R=[fT, wt], W=[bank])
                        A(lambda e, bg=bg, nchunk=nchunk: e.activation(out=sl_[:, 0:nchunk, :], in_=bg[:, 0:nchunk * 128].rearrange("p (c t) -> p c t", c=nchunk), func=AF.Silu),
                          R=[bg], W=[sl_])
                        V(lambda e, bu=bu, g=g, nchunk=nchunk: e.tensor_tensor(out=aT[:, 4 * g:4 * g + nchunk, :], in0=bu[:, 0:nchunk * 128].rearrange("p (c t) -> p c t", c=nchunk),
                                                                                in1=sl_[:, 0:nchunk, :], op=ALU.mult), R=[bu, sl_], W=[aT])
                    for hh in range(2):
                        PE([lambda e, fc=fc, hh=hh: e.matmul(PS[hh][:], aT[:, fc, :], wd[:, fc, hh * 512:(hh + 1) * 512], start=(fc == 0), stop=(fc == NFC - 1)) for fc in range(NFC)],
                           R=[aT, wd], W=[PS[hh]])
                        V(lambda e, hh=hh: e.tensor_tensor(out=h1[:, hh * 512:(hh + 1) * 512], in0=PS[hh][:], in1=h1[:, hh * 512:(hh + 1) * 512], op=ALU.add),
                          R=[PS[hh]], W=[h1])
                    A(lambda e: e.activation(out=junk[:], in_=h1[:], func=AF.Square, accum_out=st4[:, 0:1]), R=[h1], W=[junk, st4])
                    A(lambda e: e.activation(out=st4[:, 1:2], in_=st4[:, 0:1], func=AF.Sqrt, scale=1.0 / D, bias=cst[:, C_END:C_END + 1]), R=[cst], W=[st4])
                    V(lambda e: e.reciprocal(out=st4[:, 2:3], in_=st4[:, 1:2]), R=[], W=[st4])
                    A(lambda e: e.activation(out=xs[:], in_=h1[:], func=AF.Identity, scale=st4[:, 2:3], bias=cst[:, C_END + 2:C_END + 3]), R=[h1, cst], W=[xs, st4])
                    G(lambda e, o=o: e.tensor_tensor(out=o[:], in0=xs[:], in1=gfin[:], op=ALU.mult), R=[xs, gfin], W=[o])
                    S.dma("sp", ch_st[n % 2], lambda e, n=n, o=o: e.dma_start(out=outd[(n - 1) * 128:n * 128, :], in_=o[:]), R=[o], W=[])
                S.barrier()
        else:
            S.barrier()
    return nc


QPERM = [0, 4, 1, 5, 2, 6, 3, 7]


def make_consts():
    c = np.zeros((128, C_END), np.float32)
    c[:, C_ID:C_ID + 128] = np.eye(128, dtype=np.float32)
    s = np.arange(64)
    for j in range(2):
        rows = slice(64 * j, 64 * j + 64)
        c[rows, C_MB:C_MB + 64] = (s[None, :] > s[:, None])
        c[rows, C_MB + 64:C_MB + 128] = (s[None, :] >= s[:, None])
        c[rows, C_ML:C_ML + 64] = (s[None, :] < s[:, None])
        c[rows, C_I64:C_I64 + 64] = np.eye(64)
        c[rows, C_OBD + 64 * j:C_OBD + 64 * j + 64] = 1.0
        c[rows, C_TRI:C_TRI + 64] = CFAC * (s[:, None] <= s[None, :])
        c[rows, C_TRI + 64:C_TRI + 128] = CFAC * (s[:, None] < s[None, :])
    c[64:128, C_TRI0:C_TRI0 + 128] = c[64:128, C_TRI:C_TRI + 128]
    c[64:64 + 48, C_TRI0:C_TRI0 + 128] = 0.0
    i = np.arange(128)
    own = np.where(i[None, :] <= i[:, None], 0.0, NEG)
    prev = np.where(i[None, :] > i[:, None], 0.0, NEG)
    full = np.full((128, 128), NEG)
    for var, (a, b) in enumerate(((own, prev), (prev, own), (full, own))):
        base = C_AM + 272 * var
        c[:, base:base + 128] = a
        c[:, base + 128:base + 256] = b
        c[:, base + 256:base + 272] = 0.0
    half = 8
    inv_freq = np.power(np.float32(500000.0), -np.arange(half, dtype=np.float32) * np.float32(2.0 / 16)).astype(np.float32)
    for n in range(NTILES):
        pos = (n * 128 + np.arange(128) - 112).astype(np.float32)
        ang = (pos[:, None] * inv_freq[None, :]).astype(np.float32)
        c[:, C_ROPE + 16 * n:C_ROPE + 16 * n + 8] = np.cos(ang)
        c[:, C_ROPE + 16 * n + 8:C_ROPE + 16 * n + 16] = np.sin(ang)
    return c


def prep_shared(inp):
    f = np.float32
    w_in = np.asarray(inp["w_in"][0], f)
    b_in = np.asarray(inp["b_in"][0], f)
    qcols = np.concatenate([np.arange(h * 64, (h + 1) * 64) for h in QPERM])
    w_qkv = np.ascontiguousarray(np.concatenate([w_in[:, qcols], w_in[:, 512:768]], axis=1))
    b_qkv = np.concatenate([b_in[qcols], b_in[512:768]])
    R0 = 768
    w_fm = np.zeros((D, 2048), f)
    b_fm = np.zeros((2048,), f)
    mix = np.asarray(inp["rwkv_mix"][0], f)
    mix_fm = np.zeros((2048,), f)

    def put(dst0, src0, n):
        w_fm[:, dst0:dst0 + n] = w_in[:, R0 + src0:R0 + src0 + n]
        b_fm[dst0:dst0 + n] = b_in[R0 + src0:R0 + src0 + n]
        mix_fm[dst0:dst0 + n] = mix[src0:src0 + n]
    put(0, 0, 1536)
    put(1536, 1536, 64)
    put(1664, 1600, 64)
    put(1792, 1664, 128)
    put(1920, 1792, 32)
    G0 = 768 + 1824
    w_gate = np.ascontiguousarray(w_in[:, G0:G0 + 2048])
    b_gate = b_in[G0:G0 + 2048]
    rows_perm = qcols
    sh = {
        "w_qkv": w_qkv, "w_fm": w_fm, "w_gate": w_gate,
        "w_ba": np.ascontiguousarray(np.asarray(inp["w_br_attn"][0], f)[rows_perm, :]),
        "w_br": np.ascontiguousarray(np.asarray(inp["w_br_rwkv"][0], f)),
        "w_o": np.ascontiguousarray(np.asarray(inp["w_o"][0], f)),
        "w_fg": np.ascontiguousarray(np.asarray(inp["w_ffn_gate"][0], f)),
        "w_fu": np.ascontiguousarray(np.asarray(inp["w_ffn_up"][0], f)),
        "w_fd": np.ascontiguousarray(np.asarray(inp["w_ffn_down"][0], f)),
        "w2": np.ascontiguousarray(np.asarray(inp["rwkv_w2"][0], f)),
        "a2": np.ascontiguousarray(np.asarray(inp["rwkv_a2"][0], f)),
    }
    g2p = np.zeros((256, 512), f)
    g2p[0:160] = np.asarray(inp["rwkv_g2"][0], f)
    sh["g2p"] = g2p
    pfm = np.zeros((128, P_END), f)

    def fm(vec, ncol):
        return np.asarray(vec, f).reshape(ncol, 128).T
    pfm[:, P_GMIX:P_GMIX + 8] = fm(inp["norm_mix_g"][0], 8)
    pfm[:, P_GFFN:P_GFFN + 8] = fm(inp["norm_ffn_g"][0], 8)
    pfm[:, P_BFM:P_BFM + 16] = fm(b_fm, 16)
    pfm[:, P_BG:P_BG + 16] = fm(b_gate, 16)
    pfm[:, P_MIX:P_MIX + 16] = fm(mix_fm, 16)
    pfm[:, P_A0:P_A0 + 4] = fm(inp["rwkv_a0"][0], 4)
    pfm[:, P_KK:P_KK + 4] = fm(inp["rwkv_k_k"][0], 4)
    pfm[:, P_KA:P_KA + 4] = fm(inp["rwkv_k_a"][0], 4)
    pfm[:, P_RK:P_RK + 4] = fm(np.asarray(inp["rwkv_r_k"][0], f).reshape(-1), 4)
    sh["pfm"] = pfm
    rowsA = np.zeros((1, RA_END), f)
    rowsA[0, RA_BQ:RA_BQ + 768] = b_qkv
    rowsA[0, RA_W0:RA_W0 + 512] = np.asarray(inp["rwkv_w0"][0], f)
    rowsA[0, RA_SK:RA_SK + 8] = np.asarray(inp["attn_sinks"][0], f)[QPERM]
    sh["rowsA"] = rowsA
    sh["gfin"] = np.asarray(inp["norm_final_g"], f).reshape(1, D).copy()
    lnst = np.zeros((128, 2, 4, 64), f)
    for a, key in enumerate(("rwkv_ln_w", "rwkv_ln_b")):
        v = np.asarray(inp[key][0], f).reshape(4, 2, 64)
        for j in range(2):
            lnst[64 * j:64 * j + 64, a, :, :] = v[None, :, j, :]
    sh["lnst"] = lnst.reshape(128, -1)
    sh["cst"] = make_consts()
    return sh


def prep_xe(inp, b):
    xe = np.zeros((NTILES * 128, D), np.float32)
    xe[112:128] = np.asarray(inp["meta_tokens"], np.float32)
    xe[128:] = np.asarray(inp["x"][b], np.float32)
    return xe


_NC_CACHE = {}


def kernel(**inputs):
    n = 8
    sh = prep_shared(inputs)
    in_maps = []
    for b in range(n):
        m = dict(sh)
        m["xe"] = prep_xe(inputs, b)
        in_maps.append(m)
    if "nc" not in _NC_CACHE:
        _NC_CACHE["nc"] = build_program()
    res = run_bass_kernel_spmd(_NC_CACHE["nc"], in_maps, core_ids=list(range(n)))
    out = np.stack([np.asarray(r["out"], np.float32).reshape(4096, D) for r in res.results], axis=0)
    return out
```

```python
import numpy as np
import ml_dtypes
from contextlib import ExitStack
import concourse.bass as bass
import concourse.mybir as mybir
from concourse.bass_utils import run_bass_kernel_spmd

F32 = mybir.dt.float32
BF16 = mybir.dt.bfloat16
AF = mybir.ActivationFunctionType
ALU = mybir.AluOpType
AX = mybir.AxisListType

NTILES = 33
D = 1024
DFF = 2816
NFC = 22
RMS_EPS = 1e-6
LN_EPS = 64e-5
CFAC = -float(np.exp(-0.5))
NEG = -1e30
MD = BF16

C_ID = 0
C_MB = 128
C_ML = 256
C_I64 = 320
C_OBD = 384
C_TRI = 512
C_TRI0 = 640
C_AM = 768
C_ROPE = 768 + 816
C_END = C_ROPE + 33 * 16
P_GMIX, P_GFFN, P_BFM, P_BG, P_MIX, P_A0, P_KK, P_KA, P_RK, P_END = 0, 8, 16, 32, 48, 64, 68, 72, 76, 80
RA_BQ, RA_W0, RA_SK, RA_END = 0, 768, 1280, 1288


class Tile:
    def __init__(self, t, name, excl=False):
        self.t = t
        self.name = name
        self.w = None
        self.r = {}
        self.excl = excl

    def __getitem__(self, i):
        return self.t[i]


class Chan:
    def __init__(self, sem, key):
        self.sem = sem
        self.key = key
        self.count = 0


class Sched:
    def __init__(self, nc, es):
        self.nc = nc
        self.es = es
        self.E = {}
        for name, eng in (("pe", nc.tensor), ("act", nc.scalar), ("dve", nc.vector),
                          ("pool", nc.gpsimd), ("sp", nc.sync)):
            sem = es.enter_context(nc.semaphore("sem_" + name))
            self.E[name] = dict(eng=eng, sem=sem, count=0, seen={}, name=name)
        self.chans = []

    def chan(self, name):
        c = Chan(self.es.enter_context(self.nc.semaphore("ch_" + name)), "ch_" + name)
        self.chans.append(c)
        return c

    def _waits(self, E, R, W):
        deps = {}

        def add(d):
            key, val, sem = d
            if key not in deps or deps[key][0] < val:
                deps[key] = (val, sem)
        for t in R:
            if t.w is not None:
                add(t.w)
            if t.excl:
                for key, (val, sem) in t.r.items():
                    if key != E["name"]:
                        add((key, val, sem))
        for t in W:
            if t.w is not None:
                add(t.w)
            for key, (val, sem) in t.r.items():
                add((key, val, sem))
        for key, (val, sem) in deps.items():
            if key == "pe" and E["name"] == "pe":
                continue
            if E["seen"].get(key, 0) < val:
                E["eng"].wait_ge(sem, val)
                E["seen"][key] = val

    def op(self, ename, fns, R=(), W=()):
        E = self.E[ename]
        self._waits(E, R, W)
        if not isinstance(fns, (list, tuple)):
            fns = [fns]
        inst = None
        for f in fns:
            inst = f(E["eng"])
        E["count"] += 1
        inst.then_inc(E["sem"], 1)
        for t in W:
            t.w = (ename, E["count"], E["sem"])
            t.r = {}
        for t in R:
            if t not in W:
                t.r[ename] = (E["count"], E["sem"])

    def dma(self, qname, chan, fn, R=(), W=()):
        E = self.E[qname]
        self._waits(E, R, W)
        inst = fn(E["eng"])
        chan.count += 16
        inst.then_inc(chan.sem, 16)
        for t in W:
            t.w = (chan.key, chan.count, chan.sem)
            t.r = {}
        for t in R:
            t.r[chan.key] = (chan.count, chan.sem)

    def finalize(self, chan, tiles):
        for t in tiles:
            t.w = (chan.key, chan.count, chan.sem)

    def barrier(self):
        for name, E in self.E.items():
            for oname, O in self.E.items():
                if oname == name or O["count"] == 0:
                    continue
                if E["seen"].get(oname, 0) < O["count"]:
                    E["eng"].wait_ge(O["sem"], O["count"])
                    E["seen"][oname] = O["count"]
            for c in self.chans:
                if c.count and E["seen"].get(c.key, 0) < c.count:
                    E["eng"].wait_ge(c.sem, c.count)
                    E["seen"][c.key] = c.count


def build_program(nt=NTILES, phases=("A1", "A2", "B"), dbg=None, dbg_n=-1, md=None, scr_ext=False, stop=None):
    global MD
    if md is not None:
        MD = md
    nc = bass.Bass("TRN2", target_bir_lowering=False)

    def din(name, shape, dt=F32):
        return nc.dram_tensor(name, list(shape), dt, kind="ExternalInput").ap()

    xe = din("xe", [NTILES * 128, D])
    w_qkv = din("w_qkv", [D, 768])
    w_fm = din("w_fm", [D, 2048])
    w_gate = din("w_gate", [D, 2048])
    w_ba = din("w_ba", [512, D])
    w_br = din("w_br", [512, D])
    w_o = din("w_o", [D, D])
    w_fg = din("w_fg", [D, DFF])
    w_fu = din("w_fu", [D, DFF])
    w_fd = din("w_fd", [DFF, D])
    w2d = din("w2", [64, 512])
    a2d = din("a2", [64, 512])
    g2d = din("g2p", [256, 512])
    pfmd = din("pfm", [128, P_END])
    rowsAd = din("rowsA", [1, RA_END])
    gfind = din("gfin", [1, D])
    lnstd = din("lnst", [128, 2 * 4 * 64])
    cstd = din("cst", [128, C_END])
    outd = nc.dram_tensor("out", [(NTILES - 1) * 128, D], F32, kind="ExternalOutput").ap()
    skind = "ExternalOutput" if scr_ext else "Internal"
    yscr = nc.dram_tensor("yscr", [NTILES - 1, 128, 8 * 128], BF16, kind=skind).ap()
    mscr = nc.dram_tensor("mscr", [NTILES - 1, 128, 8 * 128], BF16, kind=skind).ap()
    dbg_out = {}
    if dbg:
        for name, shape in dbg.items():
            dbg_out[name] = nc.dram_tensor("dbg_" + name, list(shape), F32, kind="ExternalOutput").ap()

    with ExitStack() as es0:
        S = Sched(nc, es0)
        ch_w = S.chan("w")
        ch_x = [S.chan("x0"), S.chan("x1")]
        ch_y = [S.chan("y0"), S.chan("y1")]
        ch_st = [S.chan("s0"), S.chan("s1")]
        ch_dbg = S.chan("dbg")
        yscr_t = [Tile(None, "yscr%d" % i) for i in range(NTILES - 1)]
        mscr_t = [Tile(None, "mscr%d" % i) for i in range(NTILES - 1)]

        def V(fn, R=(), W=()):
            S.op("dve", fn, R, W)

        def A(fn, R=(), W=()):
            S.op("act", fn, R, W)

        def G(fn, R=(), W=()):
            S.op("pool", fn, R, W)

        def PE(fns, R=(), W=()):
            S.op("pe", fns, R, W)

        def mk_alloc(es, pfx):
            def sb(name, shape, dt=F32):
                return Tile(es.enter_context(nc.sbuf_tensor(pfx + name, list(shape), dt)), pfx + name)
            return sb

        def dump(name, tile_ap, tiles):
            if name in dbg_out:
                S.dma("sp", ch_dbg, lambda e: e.dma_start(out=dbg_out[name], in_=tile_ap), R=tiles, W=[])

        def run(gen):
            for _ in gen:
                pass

        def par(gens, weights=None):
            gens = list(gens)
            w = list(weights) if weights else [1] * len(gens)
            alive = [True] * len(gens)
            while any(alive):
                for i, g in enumerate(gens):
                    if not alive[i]:
                        continue
                    for _ in range(w[i]):
                        try:
                            next(g)
                        except StopIteration:
                            alive[i] = False
                            break
                yield

        def norm_T(x, xs, st4, cst, ce, pfm, gcol, TR2, uT):
            yield A(lambda e: e.activation(out=xs[:], in_=x[:], func=AF.Square, accum_out=st4[:, 0:1]), R=[x], W=[xs, st4])
            yield A(lambda e: e.activation(out=st4[:, 1:2], in_=st4[:, 0:1], func=AF.Sqrt, scale=1.0 / D, bias=cst[:, ce:ce + 1]),
                    R=[st4, cst], W=[st4])
            yield V(lambda e: e.reciprocal(out=st4[:, 2:3], in_=st4[:, 1:2]), R=[st4], W=[st4])
            yield A(lambda e: e.activation(out=xs[:], in_=x[:], func=AF.Identity, scale=st4[:, 2:3], bias=cst[:, ce + 2:ce + 3]),
                    R=[x, st4, cst], W=[xs])
            for h in range(2):
                yield PE([lambda e, c=c: e.transpose(TR2[h][:, (c % 4) * 128:(c % 4 + 1) * 128], xs[:, c * 128:(c + 1) * 128], cst[:, C_ID:C_ID + 128])
                          for c in range(4 * h, 4 * h + 4)], R=[xs, cst], W=[TR2[h]])
                yield V(lambda e, h=h: e.tensor_tensor(out=uT[:, 4 * h:4 * h + 4, :], in0=TR2[h][:].rearrange("p (c k) -> p c k", k=128),
                                                       in1=pfm[:, gcol + 4 * h:gcol + 4 * h + 4].unsqueeze(2).broadcast_to([128, 4, 128]), op=ALU.mult),
                        R=[TR2[h], pfm], W=[uT])

        def load_consts(sb, ncols):
            cst = sb("cst", [128, ncols + 4])
            pfm = sb("pfm", [128, P_END])
            G(lambda e: e.memset(cst[:, ncols:ncols + 1], RMS_EPS), W=[cst])
            G(lambda e: e.memset(cst[:, ncols + 1:ncols + 2], LN_EPS), W=[cst])
            G(lambda e: e.memset(cst[:, ncols + 2:ncols + 4], 0.0), W=[cst])
            S.dma("sp", ch_w, lambda e: e.dma_start(out=cst[:, 0:ncols], in_=cstd[:, 0:ncols]), W=[cst])
            S.dma("sp", ch_w, lambda e: e.dma_start(out=pfm[:], in_=pfmd), W=[pfm])
            return cst, pfm

        def wload(tile_, out_ap, in_ap):
            S.dma("pool", ch_w, lambda e: e.dma_start(out=out_ap, in_=in_ap), W=[tile_])

        if "A1" in phases:
            with ExitStack() as es:
                sb = mk_alloc(es, "a1_")
                CE = C_END
                cst, pfm = load_consts(sb, C_END)
                rowsA = sb("rowsA", [128, RA_END])
                S.dma("sp", ch_w, lambda e: e.dma_start(out=rowsA[:], in_=rowsAd.broadcast_to([128, RA_END])), W=[rowsA])
                lnst = sb("lnst", [128, 2, 4, 64])
                S.dma("sp", ch_w, lambda e: e.dma_start(out=lnst[:].rearrange("p a c v -> p (a c v)"), in_=lnstd), W=[lnst])
                wqkv = sb("wqkv", [128, 8, 768], BF16)
                wfm = sb("wfm", [128, 8, 2048], BF16)
                w2 = sb("w2", [64, 512])
                a2 = sb("a2", [64, 512])
                g2 = sb("g2", [128, 2, 512])
                S.dma("sp", ch_w, lambda e: e.dma_start(out=w2[:], in_=w2d), W=[w2])
                S.dma("sp", ch_w, lambda e: e.dma_start(out=a2[:], in_=a2d), W=[a2])
                S.dma("sp", ch_w, lambda e: e.dma_start(out=g2[:], in_=g2d.rearrange("(c p) n -> p c n", p=128)), W=[g2])
                wload(wqkv, wqkv[:], w_qkv.rearrange("(c p) n -> p c n", p=128))
                for hh in range(2):
                    wload(wfm, wfm[:, 4 * hh:4 * hh + 4, :], w_fm.rearrange("(c p) n -> p c n", p=128)[:, 4 * hh:4 * hh + 4, :])
                identb = sb("identb", [128, 128], BF16)
                ones2 = sb("ones2", [128, 2])
                G(lambda e: e.memset(ones2[:], 1.0), W=[ones2])
                S.finalize(ch_w, [cst, pfm, rowsA, lnst, wqkv, wfm, w2, a2, g2])
                V(lambda e: e.tensor_copy(identb[:], cst[:, C_ID:C_ID + 128]), R=[cst], W=[identb])

                PS = [Tile(es.enter_context(nc.psum_tensor("ps%d" % i, [128, 512], F32)), "ps%d" % i, excl=True) for i in range(8)]
                H0, H1, A0, A1_, A2_, R0, R1_, R2 = PS

                xb = [sb("xb0", [128, D]), sb("xb1", [128, D])]
                xs = sb("xs", [128, D])
                st4 = sb("st4", [128, 4])
                uT = sb("uT", [128, 8, 128], BF16)
                stg = sb("stg", [128, 16, 129])
                pfs = [sb("pf0", [128, 16, 128]), sb("pf1", [128, 16, 128])]
                qkvs = [sb("qkv0", [128, 768]), sb("qkv1", [128, 768])]
                rtmp = sb("rtmp", [128, 4, 10, 8])
                qT = sb("qT", [128, 4, 128], BF16)
                Kbuf = sb("Kbuf", [128, 272], BF16)
                Vbuf = sb("Vbuf", [128, 3, 128], BF16)
                Pb = [sb("Pb0", [128, 272], BF16), sb("Pb1", [128, 272], BF16)]
                PT = [sb("PT0", [128, 3, 128], BF16), sb("PT1", [128, 3, 128], BF16)]
                sm = sb("sm", [128, 5, 8])
                yat = sb("yat", [128, 8, 64])
                yTa = [sb("yTa0", [128, 4, 128], BF16), sb("yTa1", [128, 4, 128], BF16)]
                yTr = [sb("yTr0", [128, 4, 128], BF16), sb("yTr1", [128, 4, 128], BF16)]
                th = sb("th", [64, 128])
                sgd = sb("sgd", [128, 2, 128])
                B1 = sb("B1", [128, 4, 128]); B2 = sb("B2", [128, 4, 128]); B3 = sb("B3", [128, 4, 128])
                B4 = sb("B4", [128, 4, 128]); B5 = sb("B5", [128, 4, 128])
                arT = sb("arT", [128, 4, 2, 2, 64], MD)
                BT = sb("BT", [128, 4, 128], MD); KT = sb("KT", [128, 4, 128], MD)
                BH = sb("BH", [128, 4, 128], MD); KH = sb("KH", [128, 4, 128], MD)
                cumC = sb("cumC", [128, 4, 2]); WC = sb("WC", [128, 4, 2])
                rk = sb("rk", [128, 2, 4])
                Ast = [sb("Ast0", [128, 2, 4, 64], MD), sb("Ast1", [128, 2, 4, 64], MD)]
                Nst = [sb("Nst0", [128, 2, 4, 64], MD), sb("Nst1", [128, 2, 4, 64], MD)]
                NB = sb("NB", [128, 2, 4, 2, 64], MD); NK = sb("NK", [128, 2, 4, 2, 64], MD)
                Pc = [sb("Pc0", [128, 2, 4, 64], MD), sb("Pc1", [128, 2, 4, 64], MD)]
                Vst32 = sb("Vst32", [128, 2, 4, 64])
                Vst = sb("Vst", [128, 2, 4, 64], MD) if MD != F32 else Vst32
                BKst = sb("BKst", [128, 2, 2, 4, 64], MD)
                R1 = sb("R1", [128, 4, 64], MD); Ust = sb("Ust", [128, 4, 64], MD)
                Yst = sb("Yst", [128, 2, 4, 64]); yc = sb("yc", [128, 2, 4, 64]); ysq = sb("ysq", [128, 2, 4, 64])
                ST32 = sb("ST32", [128, 4, 64])
                STm = sb("STm", [128, 4, 64], MD) if MD != F32 else ST32
                gst = sb("gst", [128, 6, 8])
                G(lambda e: e.memset(ST32[:], 0.0), W=[ST32])
                if MD != F32:
                    G(lambda e: e.memset(STm[:], 0.0), W=[STm])
                G(lambda e: e.memset(stg[:], 0.0), W=[stg])
                G(lambda e: e.memset(Vbuf[:], 0.0), W=[Vbuf])
                G(lambda e: e.memset(Kbuf[:], 0.0), W=[Kbuf])

                def v3(ap2d, k=64):
                    return ap2d.rearrange("p (c k) -> p c k", k=k)

                def v4(ap2d):
                    return ap2d.rearrange("p (q c k) -> p q c k", q=2, c=4)

                def cq(t):
                    return t.rearrange("p c (q t) -> p c q t", q=2)

                def qc(t):
                    return t.rearrange("p c (q t) -> p q c t", q=2)

                ID0 = C_ID if MD == F32 else 0
                idm = cst if MD == F32 else identb

                def idsl(sl, j):
                    return cst[sl, C_ID + 64 * j:C_ID + 64 * j + 64]

                def idm_sl(sl, j):
                    return idm[sl, ID0 + 64 * j:ID0 + 64 * j + 64]

                def hl(fn, qs=(0,)):
                    out = []
                    for q in qs:
                        for c in range(4):
                            for j in range(2):
                                out += fn(q, c, slice(64 * j, 64 * j + 64), j)
                    return out

                def head(n):
                    xt = xb[n % 2]
                    pf = pfs[n % 2]
                    qkv = qkvs[n % 2]
                    yield S.dma("sp", ch_x[n % 2], lambda e: e.dma_start(out=xt[:], in_=xe[n * 128:(n + 1) * 128, :]), W=[xt])
                    yield from norm_T(xt, xs, st4, cst, CE, pfm, P_GMIX, [H0, H1], uT)
                    yield PE([lambda e, kc=kc: e.matmul(H0[:, 0:512], uT[:, kc, :], wqkv[:, kc, 0:512], start=(kc == 0), stop=(kc == 7)) for kc in range(8)],
                             R=[uT, wqkv], W=[H0])
                    yield PE([lambda e, kc=kc: e.matmul(H1[:, 0:256], uT[:, kc, :], wqkv[:, kc, 512:768], start=(kc == 0), stop=(kc == 7)) for kc in range(8)],
                             R=[uT, wqkv], W=[H1])
                    yield V(lambda e: e.tensor_tensor(out=qkv[:, 0:512], in0=H0[:, 0:512], in1=rowsA[:, RA_BQ:RA_BQ + 512], op=ALU.add), R=[H0, rowsA], W=[qkv])
                    yield V(lambda e: e.tensor_tensor(out=qkv[:, 512:768], in0=H1[:, 0:256], in1=rowsA[:, RA_BQ + 512:RA_BQ + 768], op=ALU.add), R=[H1, rowsA], W=[qkv])
                    for g in range(4):
                        bank = (H0, H1)[g % 2]
                        fns = []
                        for i in range(4):
                            col = (4 * g + i) * 128
                            for kc in range(8):
                                fns.append(lambda e, i=i, col=col, kc=kc, bank=bank: e.matmul(bank[:, i * 128:(i + 1) * 128], wfm[:, kc, col:col + 128], uT[:, kc, :],
                                                                                             start=(kc == 0), stop=(kc == 7)))
                        yield PE(fns, R=[uT, wfm], W=[bank])
                        yield V(lambda e, g=g, bank=bank: e.tensor_tensor(out=stg[:, 4 * g:4 * g + 4, 1:129], in0=v3(bank[:], 128),
                                                                          in1=pfm[:, P_BFM + 4 * g:P_BFM + 4 * g + 4].unsqueeze(2).broadcast_to([128, 4, 128]), op=ALU.add),
                                R=[bank, pfm], W=[stg])
                    if n == 0:
                        yield G(lambda e: e.memset(stg[:, :, 1:113], 0.0), W=[stg])
                    yield G(lambda e: e.tensor_tensor(out=pf[:], in0=stg[:, :, 0:128], in1=stg[:, :, 1:129], op=ALU.subtract), R=[stg], W=[pf])
                    yield G(lambda e: e.tensor_tensor(out=pf[:], in0=pf[:], in1=pfm[:, P_MIX:P_MIX + 16].unsqueeze(2).broadcast_to([128, 16, 128]), op=ALU.mult),
                            R=[pfm], W=[pf])
                    yield V(lambda e: e.tensor_tensor(out=pf[:], in0=pf[:], in1=stg[:, :, 1:129], op=ALU.add), R=[stg], W=[pf])
                    yield V(lambda e: e.tensor_copy(stg[:, :, 0:1], stg[:, :, 128:129]), R=[], W=[stg])

                def attention(n):
                    qkv = qkvs[n % 2]
                    slot = n % 2
                    q10 = qkv[:, 0:640].rearrange("p (h d) -> p h d", d=64)
                    cosb = cst[:, C_ROPE + 16 * n:C_ROPE + 16 * n + 8].unsqueeze(1).broadcast_to([128, 10, 8])
                    sinb = cst[:, C_ROPE + 16 * n + 8:C_ROPE + 16 * n + 16].unsqueeze(1).broadcast_to([128, 10, 8])
                    yield G(lambda e: e.tensor_tensor(out=rtmp[:, 0], in0=q10[:, :, 0:8], in1=cosb, op=ALU.mult), R=[qkv, cst], W=[rtmp])
                    yield G(lambda e: e.tensor_tensor(out=rtmp[:, 1], in0=q10[:, :, 8:16], in1=sinb, op=ALU.mult), R=[qkv, cst], W=[rtmp])
                    yield G(lambda e: e.tensor_tensor(out=rtmp[:, 2], in0=q10[:, :, 8:16], in1=cosb, op=ALU.mult), R=[qkv, cst], W=[rtmp])
                    yield G(lambda e: e.tensor_tensor(out=rtmp[:, 3], in0=q10[:, :, 0:8], in1=sinb, op=ALU.mult), R=[qkv, cst], W=[rtmp])
                    yield G(lambda e: e.tensor_tensor(out=q10[:, :, 0:8], in0=rtmp[:, 0], in1=rtmp[:, 1], op=ALU.subtract), R=[rtmp], W=[qkv])
                    yield G(lambda e: e.tensor_tensor(out=q10[:, :, 8:16], in0=rtmp[:, 2], in1=rtmp[:, 3], op=ALU.add), R=[rtmp], W=[qkv])
                    yield PE([lambda e, c=c: e.transpose(A0[:, c * 128:(c + 1) * 128], qkv[:, c * 128:(c + 1) * 128], cst[:, C_ID:C_ID + 128]) for c in range(4)],
                             R=[qkv, cst], W=[A0])
                    yield PE(lambda e: e.transpose(A1_[:, 0:128], qkv[:, 512:640], cst[:, C_ID:C_ID + 128]), R=[qkv, cst], W=[A1_])
                    yield A(lambda e: e.activation(out=qT[:], in_=v3(A0[:], 128), func=AF.Copy, scale=0.125), R=[A0], W=[qT])
                    yield A(lambda e: e.activation(out=Kbuf[:, slot * 128:(slot + 1) * 128], in_=A1_[:, 0:128], func=AF.Copy), R=[A1_], W=[Kbuf])
                    yield V(lambda e: e.tensor_copy(Vbuf[:, slot, :], qkv[:, 640:768]), R=[qkv], W=[Vbuf])
                    if n == 0:
                        yield A(lambda e: e.activation(out=Kbuf[:, 256:272], in_=A1_[:, 112:128], func=AF.Copy), R=[A1_], W=[Kbuf])
                        yield PE(lambda e: e.matmul(A1_[0:16, 128:256], cst[:, C_ID + 112:C_ID + 128], qkv[:, 640:768], start=True, stop=True), R=[qkv, cst], W=[A1_])
                        yield V(lambda e: e.tensor_copy(Vbuf[0:16, 2, :], A1_[0:16, 128:256]), R=[A1_], W=[Vbuf])
                        return
                    yTn = yTa[n % 2]
                    mvar = 2 if n == 1 else (0 if n % 2 == 0 else 1)
                    mask = cst[:, C_AM + 272 * mvar:C_AM + 272 * (mvar + 1)]
                    for s in range(8):
                        c, j = s // 2, s % 2
                        sl = slice(64 * j, 64 * j + 64)
                        SC = A0
                        Pk = Pb[s % 2]
                        PTk = PT[s % 2]
                        yield PE(lambda e: e.matmul(SC[:, 0:272], qT[sl, c, :], Kbuf[sl, 0:272], start=True, stop=True), R=[qT, Kbuf], W=[SC])
                        yield V(lambda e: e.tensor_tensor(out=SC[:, 0:272], in0=SC[:, 0:272], in1=mask, op=ALU.add), R=[cst], W=[SC])
                        yield V(lambda e: e.tensor_reduce(out=sm[:, 0, s:s + 1], in_=SC[:, 0:272], axis=AX.X, op=ALU.max), R=[SC], W=[sm])
                        yield V(lambda e: e.tensor_scalar(out=sm[:, 1, s:s + 1], in0=sm[:, 0, s:s + 1], scalar1=rowsA[:, RA_SK + s:RA_SK + s + 1], scalar2=-1.0,
                                                          op0=ALU.max, op1=ALU.mult), R=[rowsA], W=[sm])
                        yield A(lambda e: e.activation(out=Pk[:], in_=SC[:, 0:272], func=AF.Exp, bias=sm[:, 1, s:s + 1], scale=1.0,
                                                       accum_out=sm[:, 2, s:s + 1]), R=[SC], W=[Pk, sm])
                        yield A(lambda e: e.activation(out=sm[:, 3, s:s + 1], in_=rowsA[:, RA_SK + s:RA_SK + s + 1], func=AF.Exp, bias=sm[:, 1, s:s + 1], scale=1.0),
                                R=[rowsA], W=[sm])
                        yield PE([lambda e, b=b, nk=nk: e.matmul(A1_[0:nk, b * 128:(b + 1) * 128], Pk[:, b * 128:b * 128 + nk], identb[:], start=True, stop=True)
                                  for b, nk in ((0, 128), (1, 128), (2, 16))], R=[Pk, identb], W=[A1_])
                        yield A(lambda e: e.activation(out=PTk[:, 0:2, :], in_=v3(A1_[:, 0:256], 128), func=AF.Copy), R=[A1_], W=[PTk])
                        yield A(lambda e: e.activation(out=PTk[0:16, 2, :], in_=A1_[0:16, 256:384], func=AF.Copy), R=[A1_], W=[PTk])
                        yield PE([lambda e: e.matmul(A2_[:, s * 64:(s + 1) * 64], PTk[:, 0, :], Vbuf[:, 0, sl], start=True, stop=False),
                                  lambda e: e.matmul(A2_[:, s * 64:(s + 1) * 64], PTk[:, 1, :], Vbuf[:, 1, sl], start=False, stop=False),
                                  lambda e: e.matmul(A2_[:, s * 64:(s + 1) * 64], PTk[0:16, 2, :], Vbuf[0:16, 2, sl], start=False, stop=True)],
                                 R=[PTk, Vbuf], W=[A2_])
                    yield V(lambda e: e.tensor_tensor(out=sm[:, 2, :], in0=sm[:, 2, :], in1=sm[:, 3, :], op=ALU.add), R=[], W=[sm])
                    yield V(lambda e: e.reciprocal(out=sm[:, 4, :], in_=sm[:, 2, :]), R=[], W=[sm])
                    yield V(lambda e: e.tensor_tensor(out=yat[:], in0=v3(A2_[:], 64), in1=sm[:, 4, :].unsqueeze(2).broadcast_to([128, 8, 64]), op=ALU.mult),
                            R=[A2_], W=[yat, sm])
                    yield PE([lambda e, c=c: e.transpose(A1_[:, c * 128:(c + 1) * 128], yat[:, 2 * c:2 * c + 2, :].rearrange("p a d -> p (a d)"), cst[:, C_ID:C_ID + 128])
                              for c in range(4)], R=[yat, cst], W=[A1_])
                    yield A(lambda e: e.activation(out=yTn[:], in_=v3(A1_[:], 128), func=AF.Copy), R=[A1_], W=[yTn])
                    if n == dbg_n:
                        dump("yat", yat[:].rearrange("p s d -> p (s d)"), [yat])
                    yield S.dma("sp", ch_y[n % 2], lambda e: e.dma_start(out=yscr[n - 1][:, 0:512], in_=yTn[:].rearrange("p c t -> p (c t)")), R=[yTn], W=[yscr_t[n - 1]])

                def rwkv(n):
                    pf = pfs[n % 2]
                    tri0 = C_TRI0 if n == 0 else C_TRI
                    tri = cst[:, tri0:tri0 + 128]
                    yield A(lambda e: e.activation(out=th[:], in_=pf[0:64, 12, :], func=AF.Tanh), R=[pf], W=[th])
                    yield PE(lambda e: e.matmul(R0[:, 0:512], th[:], w2[:], start=True, stop=True), R=[th, w2], W=[R0])
                    B4f = B4[:].rearrange("p c t -> p (c t)")
                    yield V(lambda e: e.tensor_tensor(out=B4f, in0=R0[:, 0:512], in1=rowsA[:, RA_W0:RA_W0 + 512], op=ALU.add), R=[R0, rowsA], W=[B4])
                    yield A(lambda e: e.activation(out=B4f, in_=B4f, func=AF.Sigmoid), R=[], W=[B4])
                    CB = (R1_, R2)
                    for q in range(2):
                        yield PE([lambda e, c=c, q=q: e.matmul(CB[q][:, c * 128:(c + 1) * 128], B4f[64 * q:64 * q + 64, c * 128:(c + 1) * 128],
                                                               tri[64 * q:64 * q + 64, :], start=True, stop=True) for c in range(4)], R=[B4, cst], W=[CB[q]])

                    def cums(a):
                        return [CB[q][:].rearrange("p (c a t) -> p c a t", c=4, a=2)[:, :, a, :] for q in range(2)]
                    yield PE([lambda e, c=c: e.matmul(R0[:, c * 128:(c + 1) * 128], a2[:, c * 128:(c + 1) * 128], pf[0:64, 13, :], start=True, stop=True) for c in range(4)],
                             R=[pf, a2], W=[R0])
                    for c in range(4):
                        yield A(lambda e, c=c: e.activation(out=B3[:, c, :], in_=R0[:, c * 128:(c + 1) * 128], func=AF.Sigmoid, bias=pfm[:, P_A0 + c:P_A0 + c + 1], scale=1.0),
                                R=[R0, pfm], W=[B3])
                    yield A(lambda e: e.activation(out=sgd[:, 0, :], in_=pf[:, 14, :], func=AF.Sigmoid), R=[pf], W=[sgd])
                    yield A(lambda e: e.activation(out=sgd[0:32, 1, :], in_=pf[0:32, 15, :], func=AF.Sigmoid), R=[pf], W=[sgd])
                    fns = []
                    for c in range(4):
                        fns.append(lambda e, c=c: e.matmul(R0[:, c * 128:(c + 1) * 128], g2[:, 0, c * 128:(c + 1) * 128], sgd[:, 0, :], start=True, stop=False))
                        fns.append(lambda e, c=c: e.matmul(R0[:, c * 128:(c + 1) * 128], g2[0:32, 1, c * 128:(c + 1) * 128], sgd[0:32, 1, :], start=False, stop=True))
                    yield PE(fns, R=[sgd, g2], W=[R0])
                    yield A(lambda e: e.activation(out=B5[:].rearrange("p c t -> p (c t)"), in_=R0[:], func=AF.Copy), R=[R0], W=[B5])
                    kview = pf[:, 4:8, :]
                    rview = pf[:, 0:4, :]

                    def bc(col):
                        return pfm[:, col:col + 4].unsqueeze(2).broadcast_to([128, 4, 128])
                    yield V(lambda e: e.tensor_tensor(out=B1[:], in0=kview, in1=bc(P_KK), op=ALU.mult), R=[pf, pfm], W=[B1])
                    yield G(lambda e: e.tensor_tensor(out=B2[:], in0=B1[:], in1=B1[:], op=ALU.mult), R=[B1], W=[B2])
                    yield PE([lambda e, c=c: e.matmul(R0[:, c * 128:(c + 1) * 128], cst[:, C_OBD:C_OBD + 128], B2[:, c, :], start=True, stop=True) for c in range(4)],
                             R=[B2, cst], W=[R0])
                    yield A(lambda e: e.activation(out=B2[:].rearrange("p c t -> p (c t)"), in_=R0[:], func=AF.Sqrt), R=[R0], W=[B2])
                    yield V(lambda e: e.tensor_scalar(out=B2[:], in0=B2[:], scalar1=1e-12, scalar2=None, op0=ALU.max), R=[], W=[B2])
                    yield V(lambda e: e.reciprocal(out=B2[:], in_=B2[:]), R=[], W=[B2])
                    yield V(lambda e: e.tensor_tensor(out=B1[:], in0=B1[:], in1=B2[:], op=ALU.mult), R=[B2], W=[B1])
                    yield V(lambda e: e.scalar_tensor_tensor(out=B2[:], in0=B3[:], scalar=-1.0, in1=bc(P_KA), op0=ALU.add, op1=ALU.mult), R=[B3, pfm], W=[B2])
                    yield V(lambda e: e.scalar_tensor_tensor(out=kview, in0=B2[:], scalar=1.0, in1=kview, op0=ALU.add, op1=ALU.mult), R=[B2], W=[pf])
                    yield G(lambda e: e.tensor_tensor(out=B3[:], in0=B1[:], in1=B3[:], op=ALU.mult), R=[B1], W=[B3])
                    yield G(lambda e: e.tensor_tensor(out=B2[:], in0=rview, in1=kview, op=ALU.mult), R=[pf], W=[B2])
                    yield G(lambda e: e.tensor_tensor(out=B2[:], in0=B2[:], in1=bc(P_RK), op=ALU.mult), R=[pfm], W=[B2])
                    fns = []
                    for q in range(2):
                        for c in range(4):
                            for j in range(2):
                                sl = slice(64 * j, 64 * j + 64)
                                fns.append(lambda e, q=q, c=c, sl=sl: e.matmul(R0[sl, (q * 4 + c) * 2:(q * 4 + c) * 2 + 2], B2[sl, c, 64 * q:64 * q + 64], ones2[sl, :],
                                                                              start=True, stop=True))
                    yield PE(fns, R=[B2, ones2], W=[R0])
                    yield V(lambda e: e.tensor_copy(rk[:].rearrange("p q c -> p (q c)"), R0[:, 0:16].rearrange("p (x two) -> p x two", two=2)[:, :, 0]), R=[R0], W=[rk])
                    cex = cums(1)
                    cin = cums(0)
                    B4q = cq(B4[:])
                    for hh in range(2):
                        yield A(lambda e, hh=hh: e.activation(out=B4[:, :, 64 * hh:64 * hh + 64], in_=cex[hh], func=AF.Exp), R=[CB[hh]], W=[B4])
                    yield V(lambda e: e.scalar_tensor_tensor(out=arT[:, :, :, 0, :], in0=cq(B1[:]), scalar=-1.0, in1=B4q, op0=ALU.mult, op1=ALU.mult), R=[B1, B4], W=[arT])
                    for hh in range(2):
                        yield A(lambda e, hh=hh: e.activation(out=B4[:, :, 64 * hh:64 * hh + 64], in_=cin[hh], func=AF.Exp), R=[CB[hh]], W=[B4])
                    yield V(lambda e: e.tensor_tensor(out=arT[:, :, :, 1, :], in0=cq(rview), in1=B4q, op=ALU.mult), R=[pf, B4], W=[arT])
                    for hh in range(2):
                        yield A(lambda e, hh=hh: e.activation(out=B4[:, :, 64 * hh:64 * hh + 64], in_=cin[hh], func=AF.Exp, scale=-1.0), R=[CB[hh]], W=[B4])
                    yield V(lambda e: e.tensor_tensor(out=BT[:], in0=B3[:], in1=B4[:], op=ALU.mult), R=[B3, B4], W=[BT])
                    yield V(lambda e: e.tensor_tensor(out=KT[:], in0=kview, in1=B4[:], op=ALU.mult), R=[pf, B4], W=[KT])
                    for hh in range(2):
                        yield V(lambda e, hh=hh: e.tensor_copy(cumC[:, :, hh], cin[hh][:, :, 63]), R=[CB[hh]], W=[cumC])
                    for c in range(4):
                        for q in range(2):
                            yield A(lambda e, c=c, q=q: e.activation(out=B4[:, c, 64 * q:64 * q + 64], in_=cin[q][:, c, :], func=AF.Exp, scale=-1.0,
                                                                     bias=cumC[:, c, q:q + 1]), R=[CB[q], cumC], W=[B4])
                    yield V(lambda e: e.tensor_tensor(out=BH[:], in0=B3[:], in1=B4[:], op=ALU.mult), R=[B3, B4], W=[BH])
                    yield V(lambda e: e.tensor_tensor(out=KH[:], in0=kview, in1=B4[:], op=ALU.mult), R=[pf, B4], W=[KH])
                    yield A(lambda e: e.activation(out=WC[:], in_=cumC[:], func=AF.Exp), R=[cumC], W=[WC])

                    mb = cst[:, C_MB:C_MB + 128].rearrange("p (a t) -> p a t", a=2).unsqueeze(1).broadcast_to([128, 4, 2, 64])
                    ml8 = cst[:, C_ML:C_ML + 64].unsqueeze(1).broadcast_to([128, 8, 64])
                    i8 = cst[:, C_I64:C_I64 + 64].unsqueeze(1).broadcast_to([128, 8, 64])
                    Q2 = (0, 1)

                    def tq(q):
                        return slice(64 * q, 64 * q + 64)
                    yield PE(hl(lambda q, c, sl, j: [lambda e: e.matmul(R0[sl, q * 256 + c * 64:q * 256 + (c + 1) * 64], arT[sl, c, q, 0, :], BT[sl, c, tq(q)], start=True, stop=True)], Q2),
                             R=[arT, BT], W=[R0])
                    for q in Q2:
                        yield PE(hl(lambda q, c, sl, j: [lambda e: e.matmul(CB[q][sl, c * 128:(c + 1) * 128], BT[sl, c, tq(q)], arT[sl, c, q, :, :].rearrange("p a t -> p (a t)"),
                                                                            start=True, stop=True)], (q,)), R=[arT, BT], W=[CB[q]])
                    yield V(lambda e: e.tensor_tensor(out=Ast[0][:].rearrange("p q c t -> p (q c) t"), in0=v3(R0[:]), in1=ml8, op=ALU.mult), R=[R0, cst], W=[Ast[0]])
                    for q in Q2:
                        yield V(lambda e, q=q: e.tensor_tensor(out=NB[:, q], in0=CB[q][:].rearrange("p (c a t) -> p c a t", c=4, a=2), in1=mb, op=ALU.mult), R=[CB[q], cst], W=[NB])
                    for q in Q2:
                        yield PE(hl(lambda q, c, sl, j: [lambda e: e.matmul(CB[q][sl, c * 128:(c + 1) * 128], KT[sl, c, tq(q)], arT[sl, c, q, :, :].rearrange("p a t -> p (a t)"),
                                                                            start=True, stop=True)], (q,)), R=[arT, KT], W=[CB[q]])
                    for q in Q2:
                        yield V(lambda e, q=q: e.tensor_tensor(out=NK[:, q], in0=CB[q][:].rearrange("p (c a t) -> p c a t", c=4, a=2), in1=mb, op=ALU.mult), R=[CB[q], cst], W=[NK])
                    yield G(lambda e: e.tensor_copy(Nst[0][:], NB[:, :, :, 0, :]), R=[NB], W=[Nst[0]])
                    yield G(lambda e: e.tensor_tensor(out=Pc[0][:].rearrange("p q c t -> p (q c) t"), in0=Nst[0][:].rearrange("p q c t -> p (q c) t"), in1=i8, op=ALU.add),
                            R=[Nst[0], cst], W=[Pc[0]])
                    yield PE(hl(lambda q, c, sl, j: [lambda e: e.matmul(R0[sl, q * 256 + c * 64:q * 256 + (c + 1) * 64], pf[sl, 8 + c, tq(q)], idsl(sl, j), start=True, stop=True)], Q2),
                             R=[pf, cst], W=[R0])
                    yield A(lambda e: e.activation(out=Vst32[:], in_=v4(R0[:]), func=AF.Copy), R=[R0], W=[Vst32])
                    if MD != F32:
                        yield V(lambda e: e.tensor_copy(Vst[:], v4(R0[:])), R=[R0], W=[Vst])
                    for q in Q2:
                        yield PE(hl(lambda q, c, sl, j: [lambda e: e.matmul(CB[q][sl, c * 64:(c + 1) * 64], BH[sl, c, tq(q)], idm_sl(sl, j), start=True, stop=True),
                                                         lambda e: e.matmul(CB[q][sl, 256 + c * 64:256 + (c + 1) * 64], KH[sl, c, tq(q)], idm_sl(sl, j), start=True, stop=True)], (q,)),
                                 R=[BH, KH, idm], W=[CB[q]])
                    for q in Q2:
                        yield A(lambda e, q=q: e.activation(out=BKst[:, q], in_=CB[q][:].rearrange("p (a c t) -> p a c t", a=2, c=4), func=AF.Copy), R=[CB[q]], W=[BKst])
                    cur = 0
                    for lvl in range(1, 6):
                        nxt = 1 - cur
                        yield PE(hl(lambda q, c, sl, j: [lambda e: e.matmul(R0[sl, q * 256 + c * 64:q * 256 + (c + 1) * 64], Nst[cur][sl, q, c, :], Ast[cur][sl, q, c, :], start=True, stop=True)], Q2),
                                 R=[Nst[cur], Ast[cur]], W=[R0])
                        if lvl < 5:
                            yield PE(hl(lambda q, c, sl, j: [lambda e: e.matmul(R1_[sl, q * 256 + c * 64:q * 256 + (c + 1) * 64], Ast[cur][sl, q, c, :], Nst[cur][sl, q, c, :], start=True, stop=True)], Q2),
                                     R=[Nst[cur], Ast[cur]], W=[R1_])
                        yield A(lambda e, nxt=nxt: e.activation(out=Ast[nxt][:], in_=v4(R0[:]), func=AF.Copy), R=[R0], W=[Ast[nxt]])
                        if lvl < 5:
                            yield V(lambda e, nxt=nxt: e.tensor_copy(Nst[nxt][:], v4(R1_[:])), R=[R1_], W=[Nst[nxt]])
                        pc, pn = Pc[(lvl - 1) % 2], Pc[lvl % 2]
                        yield PE(hl(lambda q, c, sl, j: [lambda e: e.matmul(R2[sl, q * 256 + c * 64:q * 256 + (c + 1) * 64], Ast[nxt][sl, q, c, :], pc[sl, q, c, :], start=True, stop=True)], Q2),
                                 R=[Ast[nxt], pc], W=[R2])
                        yield V(lambda e, pc=pc, pn=pn: e.tensor_tensor(out=pn[:], in0=v4(R2[:]), in1=pc[:], op=ALU.add), R=[R2, pc], W=[pn])
                        cur = nxt
                    TT = Pc[5 % 2]
                    for q in Q2:
                        yield PE(hl(lambda q, c, sl, j: [lambda e: e.matmul(R0[sl, c * 64:(c + 1) * 64], arT[sl, c, q, 0, :], STm[sl, c, :], start=True, stop=False),
                                                         lambda e: e.matmul(R0[sl, c * 64:(c + 1) * 64], NK[sl, q, c, 0, :], Vst[sl, q, c, :], start=False, stop=True)], (q,)),
                                 R=[arT, STm, NK, Vst], W=[R0])
                        yield A(lambda e: e.activation(out=R1[:], in_=v3(R0[:, 0:256]), func=AF.Copy), R=[R0], W=[R1])
                        yield PE(hl(lambda q, c, sl, j: [lambda e: e.matmul(R0[sl, 256 + c * 64:256 + (c + 1) * 64], TT[sl, q, c, :], R1[sl, c, :], start=True, stop=True)], (q,)),
                                 R=[TT, R1], W=[R0])
                        yield A(lambda e: e.activation(out=Ust[:], in_=v3(R0[:, 256:512]), func=AF.Copy), R=[R0], W=[Ust])
                        if n >= 1:
                            yield PE(hl(lambda q, c, sl, j: [lambda e: e.matmul(R1_[sl, c * 64:(c + 1) * 64], arT[sl, c, q, 1, :], STm[sl, c, :], start=True, stop=False),
                                                             lambda e: e.matmul(R1_[sl, c * 64:(c + 1) * 64], NB[sl, q, c, 1, :], Ust[sl, c, :], start=False, stop=False),
                                                             lambda e: e.matmul(R1_[sl, c * 64:(c + 1) * 64], NK[sl, q, c, 1, :], Vst[sl, q, c, :], start=False, stop=True)], (q,)),
                                     R=[arT, STm, NB, Ust, NK, Vst], W=[R1_])
                            yield V(lambda e, q=q: e.tensor_copy(Yst[:, q], v3(R1_[:, 0:256])), R=[R1_], W=[Yst])
                        yield PE(hl(lambda q, c, sl, j: [lambda e: e.matmul(R1_[sl, 256 + c * 64:256 + (c + 1) * 64], BKst[sl, q, 0, c, :], Ust[sl, c, :], start=True, stop=False),
                                                         lambda e: e.matmul(R1_[sl, 256 + c * 64:256 + (c + 1) * 64], BKst[sl, q, 1, c, :], Vst[sl, q, c, :], start=False, stop=True)], (q,)),
                                 R=[BKst, Ust, Vst], W=[R1_])
                        yield G(lambda e, q=q: e.tensor_tensor(out=ST32[:], in0=ST32[:], in1=WC[:, :, q:q + 1].broadcast_to([128, 4, 64]), op=ALU.mult), R=[WC], W=[ST32])
                        yield V(lambda e: e.tensor_tensor(out=ST32[:], in0=ST32[:], in1=v3(R1_[:, 256:512]), op=ALU.add), R=[R1_], W=[ST32])
                        if MD != F32:
                            yield G(lambda e: e.tensor_copy(STm[:], ST32[:]), R=[ST32], W=[STm])
                    if n == 0:
                        return
                    Y8 = Yst[:].rearrange("p q c v -> p (q c) v")
                    yc8 = yc[:].rearrange("p q c v -> p (q c) v")
                    ysq8 = ysq[:].rearrange("p q c v -> p (q c) v")
                    V32_8 = Vst32[:].rearrange("p q c v -> p (q c) v")

                    def b8(ap):
                        return ap.unsqueeze(2).broadcast_to([128, 8, 64])
                    yield V(lambda e: e.tensor_reduce(out=gst[:, 0, :], in_=Y8, axis=AX.X, op=ALU.add), R=[Yst], W=[gst])
                    yield V(lambda e: e.tensor_scalar(out=gst[:, 1, :], in0=gst[:, 0, :], scalar1=-1.0 / 64, scalar2=None, op0=ALU.mult), R=[], W=[gst])
                    yield V(lambda e: e.tensor_tensor(out=yc8, in0=Y8, in1=b8(gst[:, 1, :]), op=ALU.add), R=[Yst], W=[yc, gst])
                    yield G(lambda e: e.tensor_tensor(out=ysq8, in0=yc8, in1=yc8, op=ALU.mult), R=[yc], W=[ysq])
                    yield V(lambda e: e.tensor_reduce(out=gst[:, 2, :], in_=ysq8, axis=AX.X, op=ALU.add), R=[ysq], W=[gst])
                    yield A(lambda e: e.activation(out=gst[:, 3, :], in_=gst[:, 2, :], func=AF.Sqrt, scale=1.0 / 64, bias=cst[:, CE + 1:CE + 2]), R=[cst], W=[gst])
                    yield V(lambda e: e.reciprocal(out=gst[:, 4, :], in_=gst[:, 3, :]), R=[], W=[gst])
                    yield V(lambda e: e.tensor_tensor(out=yc8, in0=yc8, in1=b8(gst[:, 4, :]), op=ALU.mult), R=[], W=[yc, gst])
                    yield G(lambda e: e.tensor_tensor(out=yc[:], in0=yc[:], in1=lnst[:, 0].unsqueeze(1).broadcast_to([128, 2, 4, 64]), op=ALU.mult), R=[lnst], W=[yc])
                    yield G(lambda e: e.tensor_tensor(out=yc[:], in0=yc[:], in1=lnst[:, 1].unsqueeze(1).broadcast_to([128, 2, 4, 64]), op=ALU.add), R=[lnst], W=[yc])
                    yield V(lambda e: e.tensor_tensor(out=ysq8, in0=V32_8, in1=b8(rk[:].rearrange("p q c -> p (q c)")), op=ALU.mult), R=[Vst32, rk], W=[ysq])
                    yield V(lambda e: e.tensor_tensor(out=yc8, in0=yc8, in1=ysq8, op=ALU.add), R=[ysq], W=[yc])
                    yield PE(hl(lambda q, c, sl, j: [lambda e: e.matmul(R2[sl, q * 256 + c * 64:q * 256 + (c + 1) * 64], yc[sl, q, c, :], idsl(sl, j), start=True, stop=True)], Q2),
                             R=[yc, cst], W=[R2])
                    yTn = yTr[n % 2]
                    yield V(lambda e: e.tensor_tensor(out=qc(yTn[:]), in0=v4(R2[:]), in1=qc(B5[:]), op=ALU.mult), R=[R2, B5], W=[yTn])
                    yield S.dma("sp", ch_st[n % 2], lambda e: e.dma_start(out=yscr[n - 1][:, 512:1024], in_=yTn[:].rearrange("p c t -> p (c t)")), R=[yTn], W=[yscr_t[n - 1]])

                run(head(0))
                for n in range(nt):
                    streams = [rwkv(n), attention(n)]
                    wts = [2, 1]
                    if n + 1 < nt:
                        streams.append(head(n + 1))
                        wts.append(1)
                    run(par(streams, wts))
                S.barrier()

        if "A2" in phases:
            with ExitStack() as es:
                sb = mk_alloc(es, "a2_")
                CE = 128
                cst, pfm = load_consts(sb, 128)
                wgate = sb("wgate", [128, 8, 2048], BF16)
                wba = sb("wba", [128, 4, D], BF16)
                wbr = sb("wbr", [128, 4, D], BF16)
                for hh in range(2):
                    wload(wgate, wgate[:, 4 * hh:4 * hh + 4, :], w_gate.rearrange("(c p) n -> p c n", p=128)[:, 4 * hh:4 * hh + 4, :])
                wload(wba, wba[:], w_ba.rearrange("(c p) n -> p c n", p=128))
                wload(wbr, wbr[:], w_br.rearrange("(c p) n -> p c n", p=128))
                S.finalize(ch_w, [cst, pfm, wgate, wba, wbr])
                PS = [Tile(es.enter_context(nc.psum_tensor("psb%d" % i, [128, 512], F32)), "psb%d" % i, excl=True) for i in range(8)]
                xb = [sb("xb0", [128, D]), sb("xb1", [128, D])]
                yT = [sb("yT0", [128, 8, 128], BF16), sb("yT1", [128, 8, 128], BF16)]
                xs = sb("xs", [128, D])
                st4 = sb("st4", [128, 4])
                uTs = [sb("uT0", [128, 8, 128], BF16), sb("uT1", [128, 8, 128], BF16)]
                sg = sb("sg", [128, 16, 128])
                t1 = sb("t1", [128, 8, 128])
                t2 = sb("t2", [128, 8, 128])
                mT = [sb("mT0", [128, 8, 128], BF16), sb("mT1", [128, 8, 128], BF16)]

                def front2(n):
                    xt = xb[n % 2]
                    yTn = yT[n % 2]
                    yield S.dma("sp", ch_x[n % 2], lambda e: e.dma_start(out=xt[:], in_=xe[n * 128:(n + 1) * 128, :]), W=[xt])
                    yield S.dma("sp", ch_y[n % 2], lambda e: e.dma_start(out=yTn[:].rearrange("p c t -> p (c t)"), in_=yscr[n - 1]), R=[yscr_t[n - 1]], W=[yTn])
                    yield from norm_T(xt, xs, st4, cst, CE, pfm, P_GMIX, [PS[0], PS[1]], uTs[n % 2])

                def back2(n):
                    yTn = yT[n % 2]
                    mTn = mT[n % 2]
                    uT = uTs[n % 2]
                    for g in range(4):
                        bank = PS[2 + (g % 2)]
                        fns = []
                        for i in range(4):
                            col = (4 * g + i) * 128
                            for kc in range(8):
                                fns.append(lambda e, i=i, col=col, kc=kc, bank=bank: e.matmul(bank[:, i * 128:(i + 1) * 128], wgate[:, kc, col:col + 128], uT[:, kc, :],
                                                                                             start=(kc == 0), stop=(kc == 7)))
                        yield PE(fns, R=[uT, wgate], W=[bank])
                        for i in range(4):
                            yield A(lambda e, g=g, i=i, bank=bank: e.activation(out=sg[:, 4 * g + i, :], in_=bank[:, i * 128:(i + 1) * 128], func=AF.Sigmoid,
                                                                                bias=pfm[:, P_BG + 4 * g + i:P_BG + 4 * g + i + 1], scale=1.0), R=[bank, pfm], W=[sg])
                    for br, (wb, off) in enumerate(((wba, 0), (wbr, 4))):
                        for hh in range(2):
                            bank = PS[4 + 2 * br + hh]
                            fns = []
                            for i in range(4):
                                fc = 4 * hh + i
                                for kc in range(4):
                                    fns.append(lambda e, i=i, fc=fc, kc=kc, bank=bank, wb=wb, off=off: e.matmul(bank[:, i * 128:(i + 1) * 128], wb[:, kc, fc * 128:(fc + 1) * 128],
                                                                                                                yTn[:, off + kc, :], start=(kc == 0), stop=(kc == 3)))
                            yield PE(fns, R=[yTn, wb], W=[bank])
                    for hh in range(2):
                        yield V(lambda e, hh=hh: e.tensor_tensor(out=t1[:, 4 * hh:4 * hh + 4, :], in0=PS[4 + hh][:].rearrange("p (c t) -> p c t", c=4), in1=sg[:, 4 * hh:4 * hh + 4, :], op=ALU.mult),
                                R=[PS[4 + hh], sg], W=[t1])
                        yield V(lambda e, hh=hh: e.tensor_tensor(out=t2[:, 4 * hh:4 * hh + 4, :], in0=PS[6 + hh][:].rearrange("p (c t) -> p c t", c=4), in1=sg[:, 8 + 4 * hh:8 + 4 * hh + 4, :], op=ALU.mult),
                                R=[PS[6 + hh], sg], W=[t2])
                    yield G(lambda e: e.tensor_tensor(out=mTn[:], in0=t1[:], in1=t2[:], op=ALU.add), R=[t1, t2], W=[mTn])
                    yield S.dma("sp", ch_st[n % 2], lambda e: e.dma_start(out=mscr[n - 1], in_=mTn[:].rearrange("p c t -> p (c t)")), R=[mTn], W=[mscr_t[n - 1]])

                if nt > 1:
                    run(front2(1))
                for n in range(1, nt):
                    streams = [back2(n)]
                    wts = [2]
                    if n + 1 < nt:
                        streams.append(front2(n + 1))
                        wts.append(1)
                    run(par(streams, wts))
                S.barrier()

        if "B" in phases:
            with ExitStack() as es:
                sb = mk_alloc(es, "b_")
                CE = 128
                cst, pfm = load_consts(sb, 128)
                gfin = sb("gfin", [128, D])
                S.dma("sp", ch_w, lambda e: e.dma_start(out=gfin[:], in_=gfind.broadcast_to([128, D])), W=[gfin])
                wo = sb("wo", [128, 8, D], BF16)
                wg = sb("wg", [128, 8, DFF], BF16)
                wu = sb("wu", [128, 8, DFF], BF16)
                wd = sb("wd", [128, NFC, D], BF16)
                for hh in range(2):
                    wload(wo, wo[:, 4 * hh:4 * hh + 4, :], w_o.rearrange("(c p) n -> p c n", p=128)[:, 4 * hh:4 * hh + 4, :])
                for wt, wsrc in ((wg, w_fg), (wu, w_fu)):
                    for hh in range(2):
                        for ch in range(2):
                            wload(wt, wt[:, 4 * hh:4 * hh + 4, ch * 1408:(ch + 1) * 1408],
                                  wsrc.rearrange("(c p) n -> p c n", p=128)[:, 4 * hh:4 * hh + 4, ch * 1408:(ch + 1) * 1408])
                for hh in range(2):
                    wload(wd, wd[:, 11 * hh:11 * hh + 11, :], w_fd.rearrange("(c p) n -> p c n", p=128)[:, 11 * hh:11 * hh + 11, :])
                S.finalize(ch_w, [cst, pfm, gfin, wo, wg, wu, wd])
                PS = [Tile(es.enter_context(nc.psum_tensor("psc%d" % i, [128, 512], F32)), "psc%d" % i, excl=True) for i in range(8)]
                xb = [sb("xb0", [128, D]), sb("xb1", [128, D])]
                mT = [sb("mT0", [128, 8, 128], BF16), sb("mT1", [128, 8, 128], BF16)]
                h1s = [sb("h1a", [128, D]), sb("h1b", [128, D])]
                xsF = sb("xsF", [128, D])
                xsB = [sb("xsB0", [128, D]), sb("xsB1", [128, D])]
                st4 = sb("st4", [128, 4])
                st4b = sb("st4b", [128, 4])
                fTs = [sb("fT0", [128, 8, 128], BF16), sb("fT1", [128, 8, 128], BF16)]
                sl_ = sb("silu", [128, 4, 128])
                aT = sb("aT", [128, NFC, 128], BF16)

                def front3(n):
                    xt = xb[n % 2]
                    mTn = mT[n % 2]
                    h1 = h1s[n % 2]
                    yield S.dma("sp", ch_x[n % 2], lambda e: e.dma_start(out=xt[:], in_=xe[n * 128:(n + 1) * 128, :]), W=[xt])
                    yield S.dma("sp", ch_y[n % 2], lambda e: e.dma_start(out=mTn[:].rearrange("p c t -> p (c t)"), in_=mscr[n - 1]), R=[mscr_t[n - 1]], W=[mTn])
                    for hh in range(2):
                        yield PE([lambda e, kc=kc, hh=hh: e.matmul(PS[hh][:], mTn[:, kc, :], wo[:, kc, hh * 512:(hh + 1) * 512], start=(kc == 0), stop=(kc == 7)) for kc in range(8)],
                                 R=[mTn, wo], W=[PS[hh]])
                        yield V(lambda e, hh=hh: e.tensor_tensor(out=h1[:, hh * 512:(hh + 1) * 512], in0=PS[hh][:], in1=xt[:, hh * 512:(hh + 1) * 512], op=ALU.add),
                                R=[PS[hh], xt], W=[h1])
                    yield from norm_T(h1, xsF, st4, cst, CE, pfm, P_GFFN, [PS[2], PS[3]], fTs[n % 2])

                def back3(n):
                    h1 = h1s[n % 2]
                    fT = fTs[n % 2]
                    o = xsB[n % 2]
                    ngrp = (NFC + 3) // 4
                    for g in range(ngrp):
                        nchunk = min(4, NFC - 4 * g)
                        bg = PS[4 + 2 * (g % 2)]
                        bu = PS[5 + 2 * (g % 2)]
                        for bank, wt in ((bg, wg), (bu, wu)):
                            fns = []
                            for i in range(nchunk):
                                fc = 4 * g + i
                                for kc in range(8):
                                    fns.append(lambda e, i=i, fc=fc, kc=kc, bank=bank, wt=wt: e.matmul(bank[:, i * 128:(i + 1) * 128], wt[:, kc, fc * 128:(fc + 1) * 128], fT[:, kc, :],
                                                                                                       start=(kc == 0), stop=(kc == 7)))
                            yield PE(fns, R=[fT, wt], W=[bank])
                        yield A(lambda e: e.activation(out=sl_[:, 0:nchunk, :], in_=bg[:, 0:nchunk * 128].rearrange("p (c t) -> p c t", c=nchunk), func=AF.Silu),
                                R=[bg], W=[sl_])
                        yield V(lambda e: e.tensor_tensor(out=aT[:, 4 * g:4 * g + nchunk, :], in0=bu[:, 0:nchunk * 128].rearrange("p (c t) -> p c t", c=nchunk),
                                                          in1=sl_[:, 0:nchunk, :], op=ALU.mult), R=[bu, sl_], W=[aT])
                    for hh in range(2):
                        yield PE([lambda e, fc=fc, hh=hh: e.matmul(PS[4 + hh][:], aT[:, fc, :], wd[:, fc, hh * 512:(hh + 1) * 512], start=(fc == 0), stop=(fc == NFC - 1)) for fc in range(NFC)],
                                 R=[aT, wd], W=[PS[4 + hh]])
                        yield V(lambda e, hh=hh: e.tensor_tensor(out=h1[:, hh * 512:(hh + 1) * 512], in0=PS[4 + hh][:], in1=h1[:, hh * 512:(hh + 1) * 512], op=ALU.add),
                                R=[PS[4 + hh]], W=[h1])
                    yield A(lambda e: e.activation(out=o[:], in_=h1[:], func=AF.Square, accum_out=st4b[:, 0:1]), R=[h1], W=[o, st4b])
                    yield A(lambda e: e.activation(out=st4b[:, 1:2], in_=st4b[:, 0:1], func=AF.Sqrt, scale=1.0 / D, bias=cst[:, CE:CE + 1]), R=[cst], W=[st4b])
                    yield V(lambda e: e.reciprocal(out=st4b[:, 2:3], in_=st4b[:, 1:2]), R=[], W=[st4b])
                    yield A(lambda e: e.activation(out=o[:], in_=h1[:], func=AF.Identity, scale=st4b[:, 2:3], bias=cst[:, CE + 2:CE + 3]), R=[h1, cst], W=[o, st4b])
                    yield G(lambda e: e.tensor_tensor(out=o[:], in0=o[:], in1=gfin[:], op=ALU.mult), R=[gfin], W=[o])
                    yield S.dma("sp", ch_st[n % 2], lambda e: e.dma_start(out=outd[(n - 1) * 128:n * 128, :], in_=o[:]), R=[o], W=[])

                if nt > 1:
                    run(front3(1))
                for n in range(1, nt):
                    streams = [back3(n)]
                    wts = [2]
                    if n + 1 < nt:
                        streams.append(front3(n + 1))
                        wts.append(1)
                    run(par(streams, wts))
                S.barrier()
        else:
            S.barrier()
    return nc


QPERM = [0, 4, 1, 5, 2, 6, 3, 7]


def make_consts():
    c = np.zeros((128, C_END), np.float32)
    c[:, C_ID:C_ID + 128] = np.eye(128, dtype=np.float32)
    s = np.arange(64)
    for j in range(2):
        rows = slice(64 * j, 64 * j + 64)
        c[rows, C_MB:C_MB + 64] = (s[None, :] > s[:, None])
        c[rows, C_MB + 64:C_MB + 128] = (s[None, :] >= s[:, None])
        c[rows, C_ML:C_ML + 64] = (s[None, :] < s[:, None])
        c[rows, C_I64:C_I64 + 64] = np.eye(64)
        c[rows, C_OBD + 64 * j:C_OBD + 64 * j + 64] = 1.0
        c[rows, C_TRI:C_TRI + 64] = CFAC * (s[:, None] <= s[None, :])
        c[rows, C_TRI + 64:C_TRI + 128] = CFAC * (s[:, None] < s[None, :])
    c[64:128, C_TRI0:C_TRI0 + 128] = c[64:128, C_TRI:C_TRI + 128]
    c[64:64 + 48, C_TRI0:C_TRI0 + 128] = 0.0
    i = np.arange(128)
    own = np.where(i[None, :] <= i[:, None], 0.0, NEG)
    prev = np.where(i[None, :] > i[:, None], 0.0, NEG)
    full = np.full((128, 128), NEG)
    for var, (a, b) in enumerate(((own, prev), (prev, own), (full, own))):
        base = C_AM + 272 * var
        c[:, base:base + 128] = a
        c[:, base + 128:base + 256] = b
        c[:, base + 256:base + 272] = 0.0
    half = 8
    inv_freq = np.power(np.float32(500000.0), -np.arange(half, dtype=np.float32) * np.float32(2.0 / 16)).astype(np.float32)
    for n in range(NTILES):
        pos = (n * 128 + np.arange(128) - 112).astype(np.float32)
        ang = (pos[:, None] * inv_freq[None, :]).astype(np.float32)
        c[:, C_ROPE + 16 * n:C_ROPE + 16 * n + 8] = np.cos(ang)
        c[:, C_ROPE + 16 * n + 8:C_ROPE + 16 * n + 16] = np.sin(ang)
    return c


def prep_shared(inp):
    f = np.float32
    w_in = np.asarray(inp["w_in"][0], f)
    b_in = np.asarray(inp["b_in"][0], f)
    qcols = np.concatenate([np.arange(h * 64, (h + 1) * 64) for h in QPERM])
    w_qkv = np.ascontiguousarray(np.concatenate([w_in[:, qcols], w_in[:, 512:768]], axis=1))
    b_qkv = np.concatenate([b_in[qcols], b_in[512:768]])
    R0 = 768
    w_fm = np.zeros((D, 2048), f)
    b_fm = np.zeros((2048,), f)
    mix = np.asarray(inp["rwkv_mix"][0], f)
    mix_fm = np.zeros((2048,), f)

    def put(dst0, src0, n):
        w_fm[:, dst0:dst0 + n] = w_in[:, R0 + src0:R0 + src0 + n]
        b_fm[dst0:dst0 + n] = b_in[R0 + src0:R0 + src0 + n]
        mix_fm[dst0:dst0 + n] = mix[src0:src0 + n]
    put(0, 0, 1536)
    put(1536, 1536, 64)
    put(1664, 1600, 64)
    put(1792, 1664, 128)
    put(1920, 1792, 32)
    G0 = 768 + 1824
    w_gate = np.ascontiguousarray(w_in[:, G0:G0 + 2048])
    b_gate = b_in[G0:G0 + 2048]
    rows_perm = qcols
    sh = {
        "w_qkv": w_qkv, "w_fm": w_fm, "w_gate": w_gate,
        "w_ba": np.ascontiguousarray(np.asarray(inp["w_br_attn"][0], f)[rows_perm, :]),
        "w_br": np.ascontiguousarray(np.asarray(inp["w_br_rwkv"][0], f)),
        "w_o": np.ascontiguousarray(np.asarray(inp["w_o"][0], f)),
        "w_fg": np.ascontiguousarray(np.asarray(inp["w_ffn_gate"][0], f)),
        "w_fu": np.ascontiguousarray(np.asarray(inp["w_ffn_up"][0], f)),
        "w_fd": np.ascontiguousarray(np.asarray(inp["w_ffn_down"][0], f)),
        "w2": np.ascontiguousarray(np.asarray(inp["rwkv_w2"][0], f)),
        "a2": np.ascontiguousarray(np.asarray(inp["rwkv_a2"][0], f)),
    }
    g2p = np.zeros((256, 512), f)
    g2p[0:160] = np.asarray(inp["rwkv_g2"][0], f)
    sh["g2p"] = g2p
    pfm = np.zeros((128, P_END), f)

    def fm(vec, ncol):
        return np.asarray(vec, f).reshape(ncol, 128).T
    pfm[:, P_GMIX:P_GMIX + 8] = fm(inp["norm_mix_g"][0], 8)
    pfm[:, P_GFFN:P_GFFN + 8] = fm(inp["norm_ffn_g"][0], 8)
    pfm[:, P_BFM:P_BFM + 16] = fm(b_fm, 16)
    pfm[:, P_BG:P_BG + 16] = fm(b_gate, 16)
    pfm[:, P_MIX:P_MIX + 16] = fm(mix_fm, 16)
    pfm[:, P_A0:P_A0 + 4] = fm(inp["rwkv_a0"][0], 4)
    pfm[:, P_KK:P_KK + 4] = fm(inp["rwkv_k_k"][0], 4)
    pfm[:, P_KA:P_KA + 4] = fm(inp["rwkv_k_a"][0], 4)
    pfm[:, P_RK:P_RK + 4] = fm(np.asarray(inp["rwkv_r_k"][0], f).reshape(-1), 4)
    sh["pfm"] = pfm
    rowsA = np.zeros((1, RA_END), f)
    rowsA[0, RA_BQ:RA_BQ + 768] = b_qkv
    rowsA[0, RA_W0:RA_W0 + 512] = np.asarray(inp["rwkv_w0"][0], f)
    rowsA[0, RA_SK:RA_SK + 8] = np.asarray(inp["attn_sinks"][0], f)[QPERM]
    sh["rowsA"] = rowsA
    sh["gfin"] = np.asarray(inp["norm_final_g"], f).reshape(1, D).copy()
    lnst = np.zeros((128, 2, 4, 64), f)
    for a, key in enumerate(("rwkv_ln_w", "rwkv_ln_b")):
        v = np.asarray(inp[key][0], f).reshape(4, 2, 64)
        for j in range(2):
            lnst[64 * j:64 * j + 64, a, :, :] = v[None, :, j, :]
    sh["lnst"] = lnst.reshape(128, -1)
    sh["cst"] = make_consts()
    return sh


def prep_xe(inp, b):
    xe = np.zeros((NTILES * 128, D), np.float32)
    xe[112:128] = np.asarray(inp["meta_tokens"], np.float32)
    xe[128:] = np.asarray(inp["x"][b], np.float32)
    return xe


_NC_CACHE = {}


def kernel(**inputs):
    n = 8
    sh = prep_shared(inputs)
    in_maps = []
    for b in range(n):
        m = dict(sh)
        m["xe"] = prep_xe(inputs, b)
        in_maps.append(m)
    if "nc" not in _NC_CACHE:
        _NC_CACHE["nc"] = build_program()
    res = run_bass_kernel_spmd(_NC_CACHE["nc"], in_maps, core_ids=list(range(n)))
    out = np.stack([np.asarray(r["out"], np.float32).reshape(4096, D) for r in res.results], axis=0)
    return out
```

```python
import numpy as np
import ml_dtypes
from contextlib import ExitStack
import concourse.bass as bass
import concourse.mybir as mybir
from concourse.bass_utils import run_bass_kernel_spmd

F32 = mybir.dt.float32
BF16 = mybir.dt.bfloat16
AF = mybir.ActivationFunctionType
ALU = mybir.AluOpType
AX = mybir.AxisListType

NTILES = 33
D = 1024
DFF = 2816
NFC = 22
RMS_EPS = 1e-6
LN_EPS = 64e-5
CFAC = -float(np.exp(-0.5))
NEG = -1e30
MD = BF16

C_ID = 0
C_MB = 128
C_ML = 256
C_I64 = 320
C_OBD = 384
C_TRI = 512
C_TRI0 = 640
C_AM = 768
C_ROPE = 768 + 816
C_END = C_ROPE + 33 * 16
P_GMIX, P_GFFN, P_BFM, P_BG, P_MIX, P_A0, P_KK, P_KA, P_RK, P_END = 0, 8, 16, 32, 48, 64, 68, 72, 76, 80
RA_BQ, RA_W0, RA_SK, RA_END = 0, 768, 1280, 1288


class Tile:
    def __init__(self, t, name, excl=False):
        self.t = t
        self.name = name
        self.w = None
        self.r = {}
        self.excl = excl

    def __getitem__(self, i):
        return self.t[i]


class Chan:
    def __init__(self, sem, key):
        self.sem = sem
        self.key = key
        self.count = 0


class Sched:
    def __init__(self, nc, es):
        self.nc = nc
        self.es = es
        self.E = {}
        for name, eng in (("pe", nc.tensor), ("act", nc.scalar), ("dve", nc.vector),
                          ("pool", nc.gpsimd), ("sp", nc.sync)):
            sem = es.enter_context(nc.semaphore("sem_" + name))
            self.E[name] = dict(eng=eng, sem=sem, count=0, seen={}, name=name)
        self.chans = []

    def chan(self, name):
        c = Chan(self.es.enter_context(self.nc.semaphore("ch_" + name)), "ch_" + name)
        self.chans.append(c)
        return c

    def _waits(self, E, R, W):
        deps = {}

        def add(d):
            key, val, sem = d
            if key not in deps or deps[key][0] < val:
                deps[key] = (val, sem)
        for t in R:
            if t.w is not None:
                add(t.w)
            if t.excl:
                for key, (val, sem) in t.r.items():
                    if key != E["name"]:
                        add((key, val, sem))
        for t in W:
            if t.w is not None:
                add(t.w)
            for key, (val, sem) in t.r.items():
                add((key, val, sem))
        for key, (val, sem) in deps.items():
            if key == "pe" and E["name"] == "pe":
                continue
            if E["seen"].get(key, 0) < val:
                E["eng"].wait_ge(sem, val)
                E["seen"][key] = val

    def op(self, ename, fns, R=(), W=()):
        E = self.E[ename]
        self._waits(E, R, W)
        if not isinstance(fns, (list, tuple)):
            fns = [fns]
        inst = None
        for f in fns:
            inst = f(E["eng"])
        E["count"] += 1
        inst.then_inc(E["sem"], 1)
        for t in W:
            t.w = (ename, E["count"], E["sem"])
            t.r = {}
        for t in R:
            if t not in W:
                t.r[ename] = (E["count"], E["sem"])

    def dma(self, qname, chan, fn, R=(), W=()):
        E = self.E[qname]
        self._waits(E, R, W)
        inst = fn(E["eng"])
        chan.count += 16
        inst.then_inc(chan.sem, 16)
        for t in W:
            t.w = (chan.key, chan.count, chan.sem)
            t.r = {}
        for t in R:
            t.r[chan.key] = (chan.count, chan.sem)

    def finalize(self, chan, tiles):
        for t in tiles:
            t.w = (chan.key, chan.count, chan.sem)

    def barrier(self):
        for name, E in self.E.items():
            for oname, O in self.E.items():
                if oname == name or O["count"] == 0:
                    continue
                if E["seen"].get(oname, 0) < O["count"]:
                    E["eng"].wait_ge(O["sem"], O["count"])
                    E["seen"][oname] = O["count"]
            for c in self.chans:
                if c.count and E["seen"].get(c.key, 0) < c.count:
                    E["eng"].wait_ge(c.sem, c.count)
                    E["seen"][c.key] = c.count


def build_program(nt=NTILES, phases=("A1", "A2", "B"), dbg=None, dbg_n=-1, md=None, scr_ext=False, stop=None):
    global MD
    if md is not None:
        MD = md
    nc = bass.Bass("TRN2", target_bir_lowering=False)

    def din(name, shape, dt=F32):
        return nc.dram_tensor(name, list(shape), dt, kind="ExternalInput").ap()

    xe = din("xe", [NTILES * 128, D])
    w_qkv = din("w_qkv", [D, 768])
    w_fm = din("w_fm", [D, 2048])
    w_gate = din("w_gate", [D, 2048])
    w_ba = din("w_ba", [512, D])
    w_br = din("w_br", [512, D])
    w_o = din("w_o", [D, D])
    w_fg = din("w_fg", [D, DFF])
    w_fu = din("w_fu", [D, DFF])
    w_fd = din("w_fd", [DFF, D])
    w2d = din("w2", [64, 512])
    a2d = din("a2", [64, 512])
    g2d = din("g2p", [256, 512])
    pfmd = din("pfm", [128, P_END])
    rowsAd = din("rowsA", [1, RA_END])
    gfind = din("gfin", [1, D])
    lnstd = din("lnst", [128, 2 * 4 * 64])
    cstd = din("cst", [128, C_END])
    outd = nc.dram_tensor("out", [(NTILES - 1) * 128, D], F32, kind="ExternalOutput").ap()
    skind = "ExternalOutput" if scr_ext else "Internal"
    yscr = nc.dram_tensor("yscr", [NTILES - 1, 128, 8 * 128], BF16, kind=skind).ap()
    mscr = nc.dram_tensor("mscr", [NTILES - 1, 128, 8 * 128], BF16, kind=skind).ap()
    dbg_out = {}
    if dbg:
        for name, shape in dbg.items():
            dbg_out[name] = nc.dram_tensor("dbg_" + name, list(shape), F32, kind="ExternalOutput").ap()

    with ExitStack() as es0:
        S = Sched(nc, es0)
        ch_w = S.chan("w")
        ch_x = [S.chan("x0"), S.chan("x1")]
        ch_y = [S.chan("y0"), S.chan("y1")]
        ch_st = [S.chan("s0"), S.chan("s1")]
        ch_dbg = S.chan("dbg")
        yscr_t = [Tile(None, "yscr%d" % i) for i in range(NTILES - 1)]
        mscr_t = [Tile(None, "mscr%d" % i) for i in range(NTILES - 1)]

        def V(fn, R=(), W=()):
            S.op("dve", fn, R, W)

        def A(fn, R=(), W=()):
            S.op("act", fn, R, W)

        def G(fn, R=(), W=()):
            S.op("pool", fn, R, W)

        def PE(fns, R=(), W=()):
            S.op("pe", fns, R, W)

        def mk_alloc(es, pfx):
            def sb(name, shape, dt=F32):
                return Tile(es.enter_context(nc.sbuf_tensor(pfx + name, list(shape), dt)), pfx + name)
            return sb

        def dump(name, tile_ap, tiles):
            if name in dbg_out:
                S.dma("sp", ch_dbg, lambda e: e.dma_start(out=dbg_out[name], in_=tile_ap), R=tiles, W=[])

        def run(gen):
            for _ in gen:
                pass

        def par(gens, weights=None):
            gens = list(gens)
            w = list(weights) if weights else [1] * len(gens)
            alive = [True] * len(gens)
            while any(alive):
                for i, g in enumerate(gens):
                    if not alive[i]:
                        continue
                    for _ in range(w[i]):
                        try:
                            next(g)
                        except StopIteration:
                            alive[i] = False
                            break
                yield

        def norm_T(x, xs, st4, cst, ce, pfm, gcol, TR2, uT):
            yield A(lambda e: e.activation(out=xs[:], in_=x[:], func=AF.Square, accum_out=st4[:, 0:1]), R=[x], W=[xs, st4])
            yield V(lambda e: e.tensor_scalar(out=st4[:, 1:2], in0=st4[:, 0:1], scalar1=1.0 / D, scalar2=RMS_EPS, op0=ALU.mult, op1=ALU.add), R=[st4], W=[st4])
            yield G(lambda e: e.tensor_tensor(out=st4[:, 2:3], in0=st4[:, 1:2], in1=cst[:, ce + 4:ce + 5], op=ALU.pow), R=[st4, cst], W=[st4])
            yield A(lambda e: e.activation(out=xs[:], in_=x[:], func=AF.Identity, scale=st4[:, 2:3], bias=cst[:, ce + 2:ce + 3]),
                    R=[x, st4, cst], W=[xs])
            for h in range(2):
                yield PE([lambda e, c=c: e.transpose(TR2[h][:, (c % 4) * 128:(c % 4 + 1) * 128], xs[:, c * 128:(c + 1) * 128], cst[:, C_ID:C_ID + 128])
                          for c in range(4 * h, 4 * h + 4)], R=[xs, cst], W=[TR2[h]])
                yield V(lambda e, h=h: e.tensor_tensor(out=uT[:, 4 * h:4 * h + 4, :], in0=TR2[h][:].rearrange("p (c k) -> p c k", k=128),
                                                       in1=pfm[:, gcol + 4 * h:gcol + 4 * h + 4].unsqueeze(2).broadcast_to([128, 4, 128]), op=ALU.mult),
                        R=[TR2[h], pfm], W=[uT])

        def load_consts(sb, ncols):
            cst = sb("cst", [128, ncols + 12])
            pfm = sb("pfm", [128, P_END])
            G(lambda e: e.memset(cst[:, ncols:ncols + 1], RMS_EPS), W=[cst])
            G(lambda e: e.memset(cst[:, ncols + 1:ncols + 2], LN_EPS), W=[cst])
            G(lambda e: e.memset(cst[:, ncols + 2:ncols + 4], 0.0), W=[cst])
            G(lambda e: e.memset(cst[:, ncols + 4:ncols + 12], -0.5), W=[cst])
            S.dma("sp", ch_w, lambda e: e.dma_start(out=cst[:, 0:ncols], in_=cstd[:, 0:ncols]), W=[cst])
            S.dma("sp", ch_w, lambda e: e.dma_start(out=pfm[:], in_=pfmd), W=[pfm])
            return cst, pfm

        def wload(tile_, out_ap, in_ap):
            S.dma("pool", ch_w, lambda e: e.dma_start(out=out_ap, in_=in_ap), W=[tile_])

        if "A1" in phases:
            with ExitStack() as es:
                sb = mk_alloc(es, "a1_")
                CE = C_END
                cst, pfm = load_consts(sb, C_END)
                rowsA = sb("rowsA", [128, RA_END])
                S.dma("sp", ch_w, lambda e: e.dma_start(out=rowsA[:], in_=rowsAd.broadcast_to([128, RA_END])), W=[rowsA])
                lnst = sb("lnst", [128, 2, 4, 64])
                S.dma("sp", ch_w, lambda e: e.dma_start(out=lnst[:].rearrange("p a c v -> p (a c v)"), in_=lnstd), W=[lnst])
                wqkv = sb("wqkv", [128, 8, 768], BF16)
                wfm = sb("wfm", [128, 8, 2048], BF16)
                w2 = sb("w2", [64, 512])
                a2 = sb("a2", [64, 512])
                g2 = sb("g2", [128, 2, 512])
                S.dma("sp", ch_w, lambda e: e.dma_start(out=w2[:], in_=w2d), W=[w2])
                S.dma("sp", ch_w, lambda e: e.dma_start(out=a2[:], in_=a2d), W=[a2])
                S.dma("sp", ch_w, lambda e: e.dma_start(out=g2[:], in_=g2d.rearrange("(c p) n -> p c n", p=128)), W=[g2])
                wload(wqkv, wqkv[:], w_qkv.rearrange("(c p) n -> p c n", p=128))
                for hh in range(2):
                    wload(wfm, wfm[:, 4 * hh:4 * hh + 4, :], w_fm.rearrange("(c p) n -> p c n", p=128)[:, 4 * hh:4 * hh + 4, :])
                identb = sb("identb", [128, 128], BF16)
                ones2 = sb("ones2", [128, 2])
                G(lambda e: e.memset(ones2[:], 1.0), W=[ones2])
                S.finalize(ch_w, [cst, pfm, rowsA, lnst, wqkv, wfm, w2, a2, g2])
                V(lambda e: e.tensor_copy(identb[:], cst[:, C_ID:C_ID + 128]), R=[cst], W=[identb])

                PS = [Tile(es.enter_context(nc.psum_tensor("ps%d" % i, [128, 512], F32)), "ps%d" % i, excl=True) for i in range(8)]
                H0, Q0, A0, A1_, A2_, R0, R1_, R2 = PS
                H1 = H0

                xb = [sb("xb0", [128, D]), sb("xb1", [128, D])]
                xs = sb("xs", [128, D])
                st4 = sb("st4", [128, 4])
                uT = sb("uT", [128, 8, 128], BF16)
                stg = sb("stg", [128, 16, 129])
                pfs = [sb("pf0", [128, 16, 128]), sb("pf1", [128, 16, 128])]
                qkvs = [sb("qkv0", [128, 768]), sb("qkv1", [128, 768])]
                rtmp = sb("rtmp", [128, 4, 10, 8])
                qT = sb("qT", [128, 4, 128], BF16)
                Kbuf = sb("Kbuf", [128, 272], BF16)
                Vbuf = sb("Vbuf", [128, 3, 128], BF16)
                Pb = [sb("Pb0", [128, 272], BF16), sb("Pb1", [128, 272], BF16)]
                PT = [sb("PT0", [128, 3, 128], BF16), sb("PT1", [128, 3, 128], BF16)]
                sm = sb("sm", [128, 5, 8])
                yat = sb("yat", [128, 8, 64])
                yTa = [sb("yTa0", [128, 4, 128], BF16), sb("yTa1", [128, 4, 128], BF16)]
                yTr = [sb("yTr0", [128, 4, 128], BF16), sb("yTr1", [128, 4, 128], BF16)]
                th = sb("th", [64, 128])
                sgd = sb("sgd", [128, 2, 128])
                B1 = sb("B1", [128, 4, 128]); B2 = sb("B2", [128, 4, 128]); B3 = sb("B3", [128, 4, 128])
                B4 = sb("B4", [128, 4, 128])
                arTs = [sb("arT%d" % i, [128, 4, 2, 2, 64], MD) for i in range(2)]
                BT = sb("BT", [128, 4, 128], MD); KT = sb("KT", [128, 4, 128], MD)
                BH = sb("BH", [128, 4, 128], MD); KH = sb("KH", [128, 4, 128], MD)
                cumC = sb("cumC", [128, 4, 2])
                WCs = [sb("WC%d" % i, [128, 4, 2]) for i in range(2)]
                rks = [sb("rk%d" % i, [128, 2, 4]) for i in range(2)]
                B5s = [sb("B5_%d" % i, [128, 4, 128]) for i in range(2)]
                TTs = [sb("TT%d" % i, [128, 2, 4, 64], MD) for i in range(2)]
                Ast = [sb("Ast0", [128, 2, 4, 64], MD), sb("Ast1", [128, 2, 4, 64], MD)]
                Nst = [sb("Nst0", [128, 2, 4, 64], MD), sb("Nst1", [128, 2, 4, 64], MD)]
                NBs = [sb("NB%d" % i, [128, 2, 4, 2, 64], MD) for i in range(2)]
                NKs = [sb("NK%d" % i, [128, 2, 4, 2, 64], MD) for i in range(2)]
                Pc = [sb("Pc0", [128, 2, 4, 64], MD), sb("Pc1", [128, 2, 4, 64], MD)]
                Vst32s = [sb("Vst32_%d" % i, [128, 2, 4, 64]) for i in range(2)]
                Vsts = [sb("Vst_%d" % i, [128, 2, 4, 64], MD) for i in range(2)] if MD != F32 else Vst32s
                BKsts = [sb("BKst%d" % i, [128, 2, 2, 4, 64], MD) for i in range(2)]
                R1 = sb("R1", [128, 4, 64], MD); Ust = sb("Ust", [128, 4, 64], MD)
                Yst = sb("Yst", [128, 2, 4, 64]); yc = sb("yc", [128, 2, 4, 64]); ysq = sb("ysq", [128, 2, 4, 64])
                ST32 = sb("ST32", [128, 4, 64])
                STm = sb("STm", [128, 4, 64], MD) if MD != F32 else ST32
                gst = sb("gst", [128, 6, 8])
                hb = sb("hb", [128, 4])
                V(lambda e: e.tensor_scalar(out=hb[:], in0=pfm[:, P_A0:P_A0 + 4], scalar1=0.5, scalar2=None, op0=ALU.mult), R=[pfm], W=[hb])
                G(lambda e: e.memset(ST32[:], 0.0), W=[ST32])
                if MD != F32:
                    G(lambda e: e.memset(STm[:], 0.0), W=[STm])
                G(lambda e: e.memset(stg[:], 0.0), W=[stg])
                G(lambda e: e.memset(Vbuf[:], 0.0), W=[Vbuf])
                G(lambda e: e.memset(Kbuf[:], 0.0), W=[Kbuf])

                def v3(ap2d, k=64):
                    return ap2d.rearrange("p (c k) -> p c k", k=k)

                def v4(ap2d):
                    return ap2d.rearrange("p (q c k) -> p q c k", q=2, c=4)

                def cq(t):
                    return t.rearrange("p c (q t) -> p c q t", q=2)

                def qc(t):
                    return t.rearrange("p c (q t) -> p q c t", q=2)

                ID0 = C_ID if MD == F32 else 0
                idm = cst if MD == F32 else identb

                def idsl(sl, j):
                    return cst[sl, C_ID + 64 * j:C_ID + 64 * j + 64]

                def idm_sl(sl, j):
                    return idm[sl, ID0 + 64 * j:ID0 + 64 * j + 64]

                def hl(fn, qs=(0,)):
                    out = []
                    for q in qs:
                        for c in range(4):
                            for j in range(2):
                                out += fn(q, c, slice(64 * j, 64 * j + 64), j)
                    return out

                def head(n):
                    xt = xb[n % 2]
                    pf = pfs[n % 2]
                    qkv = qkvs[n % 2]
                    yield S.dma("sp", ch_x[n % 2], lambda e: e.dma_start(out=xt[:], in_=xe[n * 128:(n + 1) * 128, :]), W=[xt])
                    yield from norm_T(xt, xs, st4, cst, CE, pfm, P_GMIX, [H0, H1], uT)
                    yield PE([lambda e, kc=kc: e.matmul(H0[:, 0:512], uT[:, kc, :], wqkv[:, kc, 0:512], start=(kc == 0), stop=(kc == 7)) for kc in range(8)],
                             R=[uT, wqkv], W=[H0])
                    yield V(lambda e: e.tensor_tensor(out=qkv[:, 0:512], in0=H0[:, 0:512], in1=rowsA[:, RA_BQ:RA_BQ + 512], op=ALU.add), R=[H0, rowsA], W=[qkv])
                    yield PE([lambda e, kc=kc: e.matmul(H1[:, 0:256], uT[:, kc, :], wqkv[:, kc, 512:768], start=(kc == 0), stop=(kc == 7)) for kc in range(8)],
                             R=[uT, wqkv], W=[H1])
                    yield V(lambda e: e.tensor_tensor(out=qkv[:, 512:768], in0=H1[:, 0:256], in1=rowsA[:, RA_BQ + 512:RA_BQ + 768], op=ALU.add), R=[H1, rowsA], W=[qkv])
                    for g in range(4):
                        bank = (H0, H1)[g % 2]
                        fns = []
                        for i in range(4):
                            col = (4 * g + i) * 128
                            for kc in range(8):
                                fns.append(lambda e, i=i, col=col, kc=kc, bank=bank: e.matmul(bank[:, i * 128:(i + 1) * 128], wfm[:, kc, col:col + 128], uT[:, kc, :],
                                                                                             start=(kc == 0), stop=(kc == 7)))
                        yield PE(fns, R=[uT, wfm], W=[bank])
                        yield V(lambda e, g=g, bank=bank: e.tensor_tensor(out=stg[:, 4 * g:4 * g + 4, 1:129], in0=v3(bank[:], 128),
                                                                          in1=pfm[:, P_BFM + 4 * g:P_BFM + 4 * g + 4].unsqueeze(2).broadcast_to([128, 4, 128]), op=ALU.add),
                                R=[bank, pfm], W=[stg])
                    if n == 0:
                        yield G(lambda e: e.memset(stg[:, :, 1:113], 0.0), W=[stg])
                    yield G(lambda e: e.tensor_tensor(out=pf[:], in0=stg[:, :, 0:128], in1=stg[:, :, 1:129], op=ALU.subtract), R=[stg], W=[pf])
                    yield G(lambda e: e.tensor_tensor(out=pf[:], in0=pf[:], in1=pfm[:, P_MIX:P_MIX + 16].unsqueeze(2).broadcast_to([128, 16, 128]), op=ALU.mult),
                            R=[pfm], W=[pf])
                    yield G(lambda e: e.tensor_tensor(out=pf[:], in0=pf[:], in1=stg[:, :, 1:129], op=ALU.add), R=[stg], W=[pf])
                    yield G(lambda e: e.tensor_copy(stg[:, :, 0:1], stg[:, :, 128:129]), R=[], W=[stg])

                def attention(n):
                    qkv = qkvs[n % 2]
                    slot = n % 2
                    q10 = qkv[:, 0:640].rearrange("p (h d) -> p h d", d=64)
                    cosb = cst[:, C_ROPE + 16 * n:C_ROPE + 16 * n + 8].unsqueeze(1).broadcast_to([128, 10, 8])
                    sinb = cst[:, C_ROPE + 16 * n + 8:C_ROPE + 16 * n + 16].unsqueeze(1).broadcast_to([128, 10, 8])
                    yield G(lambda e: e.tensor_tensor(out=rtmp[:, 0], in0=q10[:, :, 0:8], in1=cosb, op=ALU.mult), R=[qkv, cst], W=[rtmp])
                    yield G(lambda e: e.tensor_tensor(out=rtmp[:, 1], in0=q10[:, :, 8:16], in1=sinb, op=ALU.mult), R=[qkv, cst], W=[rtmp])
                    yield G(lambda e: e.tensor_tensor(out=rtmp[:, 2], in0=q10[:, :, 8:16], in1=cosb, op=ALU.mult), R=[qkv, cst], W=[rtmp])
                    yield G(lambda e: e.tensor_tensor(out=rtmp[:, 3], in0=q10[:, :, 0:8], in1=sinb, op=ALU.mult), R=[qkv, cst], W=[rtmp])
                    yield G(lambda e: e.tensor_tensor(out=q10[:, :, 0:8], in0=rtmp[:, 0], in1=rtmp[:, 1], op=ALU.subtract), R=[rtmp], W=[qkv])
                    yield G(lambda e: e.tensor_tensor(out=q10[:, :, 8:16], in0=rtmp[:, 2], in1=rtmp[:, 3], op=ALU.add), R=[rtmp], W=[qkv])
                    yield PE([lambda e, c=c: e.transpose(A0[:, c * 128:(c + 1) * 128], qkv[:, c * 128:(c + 1) * 128], cst[:, C_ID:C_ID + 128]) for c in range(4)],
                             R=[qkv, cst], W=[A0])
                    yield PE(lambda e: e.transpose(A1_[:, 0:128], qkv[:, 512:640], cst[:, C_ID:C_ID + 128]), R=[qkv, cst], W=[A1_])
                    yield A(lambda e: e.activation(out=qT[:], in_=v3(A0[:], 128), func=AF.Copy, scale=0.125), R=[A0], W=[qT])
                    yield A(lambda e: e.activation(out=Kbuf[:, slot * 128:(slot + 1) * 128], in_=A1_[:, 0:128], func=AF.Copy), R=[A1_], W=[Kbuf])
                    yield V(lambda e: e.tensor_copy(Vbuf[:, slot, :], qkv[:, 640:768]), R=[qkv], W=[Vbuf])
                    if n == 0:
                        yield A(lambda e: e.activation(out=Kbuf[:, 256:272], in_=A1_[:, 112:128], func=AF.Copy), R=[A1_], W=[Kbuf])
                        yield PE(lambda e: e.matmul(A1_[0:16, 128:256], cst[:, C_ID + 112:C_ID + 128], qkv[:, 640:768], start=True, stop=True), R=[qkv, cst], W=[A1_])
                        yield V(lambda e: e.tensor_copy(Vbuf[0:16, 2, :], A1_[0:16, 128:256]), R=[A1_], W=[Vbuf])
                        return
                    yTn = yTa[n % 2]
                    mvar = 2 if n == 1 else (0 if n % 2 == 0 else 1)
                    mask = cst[:, C_AM + 272 * mvar:C_AM + 272 * (mvar + 1)]
                    for s in range(8):
                        c, j = s // 2, s % 2
                        sl = slice(64 * j, 64 * j + 64)
                        SC = A0
                        Pk = Pb[s % 2]
                        PTk = PT[s % 2]
                        yield PE(lambda e: e.matmul(SC[:, 0:272], qT[sl, c, :], Kbuf[sl, 0:272], start=True, stop=True), R=[qT, Kbuf], W=[SC])
                        yield V(lambda e: e.tensor_tensor(out=SC[:, 0:272], in0=SC[:, 0:272], in1=mask, op=ALU.add), R=[cst], W=[SC])
                        yield V(lambda e: e.tensor_reduce(out=sm[:, 0, s:s + 1], in_=SC[:, 0:272], axis=AX.X, op=ALU.max), R=[SC], W=[sm])
                        yield V(lambda e: e.tensor_scalar(out=sm[:, 1, s:s + 1], in0=sm[:, 0, s:s + 1], scalar1=rowsA[:, RA_SK + s:RA_SK + s + 1], scalar2=-1.0,
                                                          op0=ALU.max, op1=ALU.mult), R=[rowsA], W=[sm])
                        yield A(lambda e: e.activation(out=Pk[:], in_=SC[:, 0:272], func=AF.Exp, bias=sm[:, 1, s:s + 1], scale=1.0,
                                                       accum_out=sm[:, 2, s:s + 1]), R=[SC], W=[Pk, sm])
                        yield A(lambda e: e.activation(out=sm[:, 3, s:s + 1], in_=rowsA[:, RA_SK + s:RA_SK + s + 1], func=AF.Exp, bias=sm[:, 1, s:s + 1], scale=1.0),
                                R=[rowsA], W=[sm])
                        yield PE([lambda e, b=b, nk=nk: e.matmul(A1_[0:nk, b * 128:(b + 1) * 128], Pk[:, b * 128:b * 128 + nk], identb[:], start=True, stop=True)
                                  for b, nk in ((0, 128), (1, 128), (2, 16))], R=[Pk, identb], W=[A1_])
                        yield A(lambda e: e.activation(out=PTk[:, 0:2, :], in_=v3(A1_[:, 0:256], 128), func=AF.Copy), R=[A1_], W=[PTk])
                        yield A(lambda e: e.activation(out=PTk[0:16, 2, :], in_=A1_[0:16, 256:384], func=AF.Copy), R=[A1_], W=[PTk])
                        yield PE([lambda e: e.matmul(A2_[:, s * 64:(s + 1) * 64], PTk[:, 0, :], Vbuf[:, 0, sl], start=True, stop=False),
                                  lambda e: e.matmul(A2_[:, s * 64:(s + 1) * 64], PTk[:, 1, :], Vbuf[:, 1, sl], start=False, stop=False),
                                  lambda e: e.matmul(A2_[:, s * 64:(s + 1) * 64], PTk[0:16, 2, :], Vbuf[0:16, 2, sl], start=False, stop=True)],
                                 R=[PTk, Vbuf], W=[A2_])
                    yield V(lambda e: e.tensor_tensor(out=sm[:, 2, :], in0=sm[:, 2, :], in1=sm[:, 3, :], op=ALU.add), R=[], W=[sm])
                    yield V(lambda e: e.reciprocal(out=sm[:, 4, :], in_=sm[:, 2, :]), R=[], W=[sm])
                    yield V(lambda e: e.tensor_tensor(out=yat[:], in0=v3(A2_[:], 64), in1=sm[:, 4, :].unsqueeze(2).broadcast_to([128, 8, 64]), op=ALU.mult),
                            R=[A2_], W=[yat, sm])
                    yield PE([lambda e, c=c: e.transpose(A1_[:, c * 128:(c + 1) * 128], yat[:, 2 * c:2 * c + 2, :].rearrange("p a d -> p (a d)"), cst[:, C_ID:C_ID + 128])
                              for c in range(4)], R=[yat, cst], W=[A1_])
                    yield A(lambda e: e.activation(out=yTn[:], in_=v3(A1_[:], 128), func=AF.Copy), R=[A1_], W=[yTn])
                    if n == dbg_n:
                        dump("yat", yat[:].rearrange("p s d -> p (s d)"), [yat])
                    yield S.dma("sp", ch_y[n % 2], lambda e: e.dma_start(out=yscr[n - 1][:, 0:512], in_=yTn[:].rearrange("p c t -> p (c t)")), R=[yTn], W=[yscr_t[n - 1]])

                def pre(n):
                    pf = pfs[n % 2]
                    pp_ = n % 2
                    arT, NB, NK, Vst, Vst32, BKst, WC, rk, B5 = arTs[pp_], NBs[pp_], NKs[pp_], Vsts[pp_], Vst32s[pp_], BKsts[pp_], WCs[pp_], rks[pp_], B5s[pp_]
                    tri0 = C_TRI0 if n == 0 else C_TRI
                    tri = cst[:, tri0:tri0 + 128]
                    yield A(lambda e: e.activation(out=th[:], in_=pf[0:64, 12, :], func=AF.Tanh), R=[pf], W=[th])
                    yield PE(lambda e: e.matmul(R0[:, 0:512], th[:], w2[:], start=True, stop=True), R=[th, w2], W=[R0])
                    B4f = B4[:].rearrange("p c t -> p (c t)")
                    yield V(lambda e: e.tensor_tensor(out=B4f, in0=R0[:, 0:512], in1=rowsA[:, RA_W0:RA_W0 + 512], op=ALU.add), R=[R0, rowsA], W=[B4])
                    yield A(lambda e: e.activation(out=B4f, in_=B4f, func=AF.Tanh, scale=0.5), R=[], W=[B4])
                    yield V(lambda e: e.tensor_scalar(out=B4f, in0=B4f, scalar1=0.5, scalar2=0.5, op0=ALU.mult, op1=ALU.add), R=[], W=[B4])
                    CB = (R1_, R2)
                    for q in range(2):
                        yield PE([lambda e, c=c, q=q: e.matmul(CB[q][:, c * 128:(c + 1) * 128], B4f[64 * q:64 * q + 64, c * 128:(c + 1) * 128],
                                                               tri[64 * q:64 * q + 64, :], start=True, stop=True) for c in range(4)], R=[B4, cst], W=[CB[q]])

                    def cums(a):
                        return [CB[q][:].rearrange("p (c a t) -> p c a t", c=4, a=2)[:, :, a, :] for q in range(2)]
                    yield PE([lambda e, c=c: e.matmul(R0[:, c * 128:(c + 1) * 128], a2[:, c * 128:(c + 1) * 128], pf[0:64, 13, :], start=True, stop=True) for c in range(4)],
                             R=[pf, a2], W=[R0])
                    for c in range(4):
                        yield A(lambda e, c=c: e.activation(out=B3[:, c, :], in_=R0[:, c * 128:(c + 1) * 128], func=AF.Tanh, bias=hb[:, c:c + 1], scale=0.5),
                                R=[R0, hb], W=[B3])
                    yield V(lambda e: e.tensor_scalar(out=B3[:], in0=B3[:], scalar1=0.5, scalar2=0.5, op0=ALU.mult, op1=ALU.add), R=[], W=[B3])
                    yield A(lambda e: e.activation(out=sgd[:], in_=pf[:, 14:16, :], func=AF.Tanh, scale=0.5), R=[pf], W=[sgd])
                    yield V(lambda e: e.tensor_scalar(out=sgd[:], in0=sgd[:], scalar1=0.5, scalar2=0.5, op0=ALU.mult, op1=ALU.add), R=[], W=[sgd])
                    fns = []
                    for c in range(4):
                        fns.append(lambda e, c=c: e.matmul(R0[:, c * 128:(c + 1) * 128], g2[:, 0, c * 128:(c + 1) * 128], sgd[:, 0, :], start=True, stop=False))
                        fns.append(lambda e, c=c: e.matmul(R0[:, c * 128:(c + 1) * 128], g2[0:32, 1, c * 128:(c + 1) * 128], sgd[0:32, 1, :], start=False, stop=True))
                    yield PE(fns, R=[sgd, g2], W=[R0])
                    yield A(lambda e: e.activation(out=B5[:].rearrange("p c t -> p (c t)"), in_=R0[:], func=AF.Copy), R=[R0], W=[B5])
                    kview = pf[:, 4:8, :]
                    rview = pf[:, 0:4, :]

                    def bc(col):
                        return pfm[:, col:col + 4].unsqueeze(2).broadcast_to([128, 4, 128])
                    yield V(lambda e: e.tensor_tensor(out=B1[:], in0=kview, in1=bc(P_KK), op=ALU.mult), R=[pf, pfm], W=[B1])
                    yield V(lambda e: e.tensor_tensor(out=B2[:], in0=B1[:], in1=B1[:], op=ALU.mult), R=[B1], W=[B2])
                    yield PE([lambda e, c=c: e.matmul(R0[:, c * 128:(c + 1) * 128], cst[:, C_OBD:C_OBD + 128], B2[:, c, :], start=True, stop=True) for c in range(4)],
                             R=[B2, cst], W=[R0])
                    yield A(lambda e: e.activation(out=B2[:].rearrange("p c t -> p (c t)"), in_=R0[:], func=AF.Sqrt), R=[R0], W=[B2])
                    yield V(lambda e: e.tensor_scalar(out=B2[:], in0=B2[:], scalar1=1e-12, scalar2=None, op0=ALU.max), R=[], W=[B2])
                    yield V(lambda e: e.reciprocal(out=B2[:], in_=B2[:]), R=[], W=[B2])
                    yield V(lambda e: e.tensor_tensor(out=B1[:], in0=B1[:], in1=B2[:], op=ALU.mult), R=[B2], W=[B1])
                    yield V(lambda e: e.scalar_tensor_tensor(out=B2[:], in0=B3[:], scalar=-1.0, in1=bc(P_KA), op0=ALU.add, op1=ALU.mult), R=[B3, pfm], W=[B2])
                    yield V(lambda e: e.scalar_tensor_tensor(out=kview, in0=B2[:], scalar=1.0, in1=kview, op0=ALU.add, op1=ALU.mult), R=[B2], W=[pf])
                    yield V(lambda e: e.tensor_tensor(out=B3[:], in0=B1[:], in1=B3[:], op=ALU.mult), R=[B1], W=[B3])
                    cex = cums(1)
                    cin = cums(0)
                    B4q = cq(B4[:])
                    for hh in range(2):
                        yield A(lambda e, hh=hh: e.activation(out=B4[:, :, 64 * hh:64 * hh + 64], in_=cex[hh], func=AF.Exp), R=[CB[hh]], W=[B4])
                    yield V(lambda e: e.scalar_tensor_tensor(out=arT[:, :, :, 0, :], in0=cq(B1[:]), scalar=-1.0, in1=B4q, op0=ALU.mult, op1=ALU.mult), R=[B1, B4], W=[arT])
                    for hh in range(2):
                        yield A(lambda e, hh=hh: e.activation(out=B4[:, :, 64 * hh:64 * hh + 64], in_=cin[hh], func=AF.Exp), R=[CB[hh]], W=[B4])
                    yield V(lambda e: e.tensor_tensor(out=arT[:, :, :, 1, :], in0=cq(rview), in1=B4q, op=ALU.mult), R=[pf, B4], W=[arT])
                    for hh in range(2):
                        yield A(lambda e, hh=hh: e.activation(out=B4[:, :, 64 * hh:64 * hh + 64], in_=cin[hh], func=AF.Exp, scale=-1.0), R=[CB[hh]], W=[B4])
                    yield V(lambda e: e.tensor_tensor(out=BT[:], in0=B3[:], in1=B4[:], op=ALU.mult), R=[B3, B4], W=[BT])
                    yield V(lambda e: e.tensor_tensor(out=KT[:], in0=kview, in1=B4[:], op=ALU.mult), R=[pf, B4], W=[KT])
                    for hh in range(2):
                        yield V(lambda e, hh=hh: e.tensor_copy(cumC[:, :, hh], cin[hh][:, :, 63]), R=[CB[hh]], W=[cumC])
                    for c in range(4):
                        for q in range(2):
                            yield A(lambda e, c=c, q=q: e.activation(out=B4[:, c, 64 * q:64 * q + 64], in_=cin[q][:, c, :], func=AF.Exp, scale=-1.0,
                                                                     bias=cumC[:, c, q:q + 1]), R=[CB[q], cumC], W=[B4])
                    yield V(lambda e: e.tensor_tensor(out=BH[:], in0=B3[:], in1=B4[:], op=ALU.mult), R=[B3, B4], W=[BH])
                    yield V(lambda e: e.tensor_tensor(out=KH[:], in0=kview, in1=B4[:], op=ALU.mult), R=[pf, B4], W=[KH])
                    yield A(lambda e: e.activation(out=WC[:], in_=cumC[:], func=AF.Exp), R=[cumC], W=[WC])

                    mb = cst[:, C_MB:C_MB + 128].rearrange("p (a t) -> p a t", a=2).unsqueeze(1).broadcast_to([128, 4, 2, 64])
                    ml8 = cst[:, C_ML:C_ML + 64].unsqueeze(1).broadcast_to([128, 8, 64])
                    i8 = cst[:, C_I64:C_I64 + 64].unsqueeze(1).broadcast_to([128, 8, 64])
                    Q2 = (0, 1)

                    def tq(q):
                        return slice(64 * q, 64 * q + 64)
                    yield PE(hl(lambda q, c, sl, j: [lambda e: e.matmul(R0[sl, q * 256 + c * 64:q * 256 + (c + 1) * 64], arT[sl, c, q, 0, :], BT[sl, c, tq(q)], start=True, stop=True)], Q2),
                             R=[arT, BT], W=[R0])
                    for q in Q2:
                        yield PE(hl(lambda q, c, sl, j: [lambda e: e.matmul(CB[q][sl, c * 128:(c + 1) * 128], BT[sl, c, tq(q)], arT[sl, c, q, :, :].rearrange("p a t -> p (a t)"),
                                                                            start=True, stop=True)], (q,)), R=[arT, BT], W=[CB[q]])
                    yield V(lambda e: e.tensor_tensor(out=Ast[0][:].rearrange("p q c t -> p (q c) t"), in0=v3(R0[:]), in1=ml8, op=ALU.mult), R=[R0, cst], W=[Ast[0]])
                    for q in Q2:
                        yield V(lambda e, q=q: e.tensor_tensor(out=NB[:, q], in0=CB[q][:].rearrange("p (c a t) -> p c a t", c=4, a=2), in1=mb, op=ALU.mult), R=[CB[q], cst], W=[NB])
                    for q in Q2:
                        yield PE(hl(lambda q, c, sl, j: [lambda e: e.matmul(CB[q][sl, c * 128:(c + 1) * 128], KT[sl, c, tq(q)], arT[sl, c, q, :, :].rearrange("p a t -> p (a t)"),
                                                                            start=True, stop=True)], (q,)), R=[arT, KT], W=[CB[q]])
                    for q in Q2:
                        yield V(lambda e, q=q: e.tensor_tensor(out=NK[:, q], in0=CB[q][:].rearrange("p (c a t) -> p c a t", c=4, a=2), in1=mb, op=ALU.mult), R=[CB[q], cst], W=[NK])
                    yield G(lambda e: e.tensor_copy(Nst[0][:], NB[:, :, :, 0, :]), R=[NB], W=[Nst[0]])
                    yield G(lambda e: e.tensor_tensor(out=Pc[0][:].rearrange("p q c t -> p (q c) t"), in0=Nst[0][:].rearrange("p q c t -> p (q c) t"), in1=i8, op=ALU.add),
                            R=[Nst[0], cst], W=[Pc[0]])
                    yield PE(hl(lambda q, c, sl, j: [lambda e: e.matmul(R0[sl, q * 256 + c * 64:q * 256 + (c + 1) * 64], pf[sl, 8 + c, tq(q)], idsl(sl, j), start=True, stop=True)], Q2),
                             R=[pf, cst], W=[R0])
                    yield A(lambda e: e.activation(out=Vst32[:], in_=v4(R0[:]), func=AF.Copy), R=[R0], W=[Vst32])
                    if MD != F32:
                        yield V(lambda e: e.tensor_copy(Vst[:], v4(R0[:])), R=[R0], W=[Vst])
                    for q in Q2:
                        yield PE(hl(lambda q, c, sl, j: [lambda e: e.matmul(CB[q][sl, c * 64:(c + 1) * 64], BH[sl, c, tq(q)], idm_sl(sl, j), start=True, stop=True),
                                                         lambda e: e.matmul(CB[q][sl, 256 + c * 64:256 + (c + 1) * 64], KH[sl, c, tq(q)], idm_sl(sl, j), start=True, stop=True)], (q,)),
                                 R=[BH, KH, idm], W=[CB[q]])
                    for q in Q2:
                        yield A(lambda e, q=q: e.activation(out=BKst[:, q], in_=CB[q][:].rearrange("p (a c t) -> p a c t", a=2, c=4), func=AF.Copy), R=[CB[q]], W=[BKst])
                    cur = 0
                    for lvl in range(1, 6):
                        nxt = 1 - cur
                        yield PE(hl(lambda q, c, sl, j: [lambda e: e.matmul(R0[sl, q * 256 + c * 64:q * 256 + (c + 1) * 64], Nst[cur][sl, q, c, :], Ast[cur][sl, q, c, :], start=True, stop=True)], Q2),
                                 R=[Nst[cur], Ast[cur]], W=[R0])
                        if lvl < 5:
                            yield PE(hl(lambda q, c, sl, j: [lambda e: e.matmul(R1_[sl, q * 256 + c * 64:q * 256 + (c + 1) * 64], Ast[cur][sl, q, c, :], Nst[cur][sl, q, c, :], start=True, stop=True)], Q2),
                                     R=[Nst[cur], Ast[cur]], W=[R1_])
                        yield A(lambda e, nxt=nxt: e.activation(out=Ast[nxt][:], in_=v4(R0[:]), func=AF.Copy), R=[R0], W=[Ast[nxt]])
                        if lvl < 5:
                            yield V(lambda e, nxt=nxt: e.tensor_copy(Nst[nxt][:], v4(R1_[:])), R=[R1_], W=[Nst[nxt]])
                        pc, pn = Pc[(lvl - 1) % 2], (Pc[lvl % 2] if lvl < 5 else TTs[pp_])
                        yield PE(hl(lambda q, c, sl, j: [lambda e: e.matmul(R2[sl, q * 256 + c * 64:q * 256 + (c + 1) * 64], Ast[nxt][sl, q, c, :], pc[sl, q, c, :], start=True, stop=True)], Q2),
                                 R=[Ast[nxt], pc], W=[R2])
                        yield V(lambda e, pc=pc, pn=pn: e.tensor_tensor(out=pn[:], in0=v4(R2[:]), in1=pc[:], op=ALU.add), R=[R2, pc], W=[pn])
                        cur = nxt
                    yield G(lambda e: e.tensor_tensor(out=B2[:], in0=rview, in1=kview, op=ALU.mult), R=[pf], W=[B2])
                    yield G(lambda e: e.tensor_tensor(out=B2[:], in0=B2[:], in1=bc(P_RK), op=ALU.mult), R=[pfm], W=[B2])
                    fns = []
                    for q in range(2):
                        for c in range(4):
                            for j in range(2):
                                sl = slice(64 * j, 64 * j + 64)
                                fns.append(lambda e, q=q, c=c, sl=sl: e.matmul(R0[sl, (q * 4 + c) * 2:(q * 4 + c) * 2 + 2], B2[sl, c, 64 * q:64 * q + 64], ones2[sl, :],
                                                                              start=True, stop=True))
                    yield PE(fns, R=[B2, ones2], W=[R0])
                    yield A(lambda e: e.activation(out=rk[:].rearrange("p q c -> p (q c)"), in_=R0[:, 0:16].rearrange("p (x two) -> p x two", two=2)[:, :, 0], func=AF.Copy), R=[R0], W=[rk])
                    return

                def seq(n):
                    pp_ = n % 2
                    arT, NB, NK, Vst, Vst32, BKst, WC, rk, B5 = arTs[pp_], NBs[pp_], NKs[pp_], Vsts[pp_], Vst32s[pp_], BKsts[pp_], WCs[pp_], rks[pp_], B5s[pp_]
                    TT = TTs[pp_]
                    Q2 = (0, 1)
                    for q in Q2:
                        yield PE(hl(lambda q, c, sl, j: [lambda e: e.matmul(Q0[sl, c * 64:(c + 1) * 64], arT[sl, c, q, 0, :], STm[sl, c, :], start=True, stop=False),
                                                         lambda e: e.matmul(Q0[sl, c * 64:(c + 1) * 64], NK[sl, q, c, 0, :], Vst[sl, q, c, :], start=False, stop=True)], (q,)),
                                 R=[arT, STm, NK, Vst], W=[Q0])
                        yield A(lambda e: e.activation(out=R1[:], in_=v3(Q0[:, 0:256]), func=AF.Copy), R=[Q0], W=[R1])
                        yield PE(hl(lambda q, c, sl, j: [lambda e: e.matmul(Q0[sl, 256 + c * 64:256 + (c + 1) * 64], TT[sl, q, c, :], R1[sl, c, :], start=True, stop=True)], (q,)),
                                 R=[TT, R1], W=[Q0])
                        yield A(lambda e: e.activation(out=Ust[:], in_=v3(Q0[:, 256:512]), func=AF.Copy), R=[Q0], W=[Ust])
                        if n >= 1:
                            yield PE(hl(lambda q, c, sl, j: [lambda e: e.matmul(Q0[sl, c * 64:(c + 1) * 64], arT[sl, c, q, 1, :], STm[sl, c, :], start=True, stop=False),
                                                             lambda e: e.matmul(Q0[sl, c * 64:(c + 1) * 64], NB[sl, q, c, 1, :], Ust[sl, c, :], start=False, stop=False),
                                                             lambda e: e.matmul(Q0[sl, c * 64:(c + 1) * 64], NK[sl, q, c, 1, :], Vst[sl, q, c, :], start=False, stop=True)], (q,)),
                                     R=[arT, STm, NB, Ust, NK, Vst], W=[Q0])
                            yield V(lambda e, q=q: e.tensor_copy(Yst[:, q], v3(Q0[:, 0:256])), R=[Q0], W=[Yst])
                        yield PE(hl(lambda q, c, sl, j: [lambda e: e.matmul(Q0[sl, 256 + c * 64:256 + (c + 1) * 64], BKst[sl, q, 0, c, :], Ust[sl, c, :], start=True, stop=False),
                                                         lambda e: e.matmul(Q0[sl, 256 + c * 64:256 + (c + 1) * 64], BKst[sl, q, 1, c, :], Vst[sl, q, c, :], start=False, stop=True)], (q,)),
                                 R=[BKst, Ust, Vst], W=[Q0])
                        yield G(lambda e, q=q: e.tensor_tensor(out=ST32[:], in0=ST32[:], in1=WC[:, :, q:q + 1].broadcast_to([128, 4, 64]), op=ALU.mult), R=[WC], W=[ST32])
                        yield V(lambda e: e.tensor_tensor(out=ST32[:], in0=ST32[:], in1=v3(Q0[:, 256:512]), op=ALU.add), R=[Q0], W=[ST32])
                        if MD != F32:
                            yield G(lambda e: e.tensor_copy(STm[:], ST32[:]), R=[ST32], W=[STm])
                    if n == 0:
                        return
                    Y8 = Yst[:].rearrange("p q c v -> p (q c) v")
                    yc8 = yc[:].rearrange("p q c v -> p (q c) v")
                    ysq8 = ysq[:].rearrange("p q c v -> p (q c) v")
                    V32_8 = Vst32[:].rearrange("p q c v -> p (q c) v")

                    def b8(ap):
                        return ap.unsqueeze(2).broadcast_to([128, 8, 64])
                    yield V(lambda e: e.tensor_reduce(out=gst[:, 0, :], in_=Y8, axis=AX.X, op=ALU.add), R=[Yst], W=[gst])
                    yield V(lambda e: e.tensor_scalar(out=gst[:, 1, :], in0=gst[:, 0, :], scalar1=-1.0 / 64, scalar2=None, op0=ALU.mult), R=[], W=[gst])
                    yield V(lambda e: e.tensor_tensor(out=yc8, in0=Y8, in1=b8(gst[:, 1, :]), op=ALU.add), R=[Yst], W=[yc, gst])
                    yield G(lambda e: e.tensor_tensor(out=ysq8, in0=yc8, in1=yc8, op=ALU.mult), R=[yc], W=[ysq])
                    yield V(lambda e: e.tensor_reduce(out=gst[:, 2, :], in_=ysq8, axis=AX.X, op=ALU.add), R=[ysq], W=[gst])
                    yield V(lambda e: e.tensor_scalar(out=gst[:, 3, :], in0=gst[:, 2, :], scalar1=1.0 / 64, scalar2=LN_EPS, op0=ALU.mult, op1=ALU.add), R=[], W=[gst])
                    yield G(lambda e: e.tensor_tensor(out=gst[:, 4, :], in0=gst[:, 3, :], in1=cst[:, CE + 4:CE + 12], op=ALU.pow), R=[cst], W=[gst])
                    yield V(lambda e: e.tensor_tensor(out=yc8, in0=yc8, in1=b8(gst[:, 4, :]), op=ALU.mult), R=[], W=[yc, gst])
                    yield G(lambda e: e.tensor_tensor(out=yc[:], in0=yc[:], in1=lnst[:, 0].unsqueeze(1).broadcast_to([128, 2, 4, 64]), op=ALU.mult), R=[lnst], W=[yc])
                    yield G(lambda e: e.tensor_tensor(out=yc[:], in0=yc[:], in1=lnst[:, 1].unsqueeze(1).broadcast_to([128, 2, 4, 64]), op=ALU.add), R=[lnst], W=[yc])
                    yield V(lambda e: e.tensor_tensor(out=ysq8, in0=V32_8, in1=b8(rk[:].rearrange("p q c -> p (q c)")), op=ALU.mult), R=[Vst32, rk], W=[ysq])
                    yield V(lambda e: e.tensor_tensor(out=yc8, in0=yc8, in1=ysq8, op=ALU.add), R=[ysq], W=[yc])
                    yield PE(hl(lambda q, c, sl, j: [lambda e: e.matmul(Q0[sl, q * 256 + c * 64:q * 256 + (c + 1) * 64], yc[sl, q, c, :], idsl(sl, j), start=True, stop=True)], Q2),
                             R=[yc, cst], W=[Q0])
                    yTn = yTr[n % 2]
                    yield V(lambda e: e.tensor_tensor(out=qc(yTn[:]), in0=v4(Q0[:]), in1=qc(B5[:]), op=ALU.mult), R=[Q0, B5], W=[yTn])
                    yield S.dma("sp", ch_st[n % 2], lambda e: e.dma_start(out=yscr[n - 1][:, 512:1024], in_=yTn[:].rearrange("p c t -> p (c t)")), R=[yTn], W=[yscr_t[n - 1]])

                import os as _os
                W_PRE, W_ATT, W_SEQ, W_HEAD = [int(v) for v in _os.environ.get("KW", "3,3,1,1").split(",")]
                run(head(0))
                if nt > 1:
                    run(par([pre(0), attention(0), head(1)], [W_PRE, W_ATT, W_HEAD]))
                else:
                    run(par([pre(0), attention(0)], [W_PRE, W_ATT]))
                _skip = _os.environ.get("KSKIP", "")
                for i in range(nt):
                    streams = [seq(i)] if "seq" not in _skip else []
                    wts = [W_SEQ] if "seq" not in _skip else []
                    if i + 1 < nt:
                        if "pre" not in _skip:
                            streams += [pre(i + 1)]
                            wts += [W_PRE]
                        if "att" not in _skip:
                            streams += [attention(i + 1)]
                            wts += [W_ATT]
                    if i + 2 < nt:
                        streams.append(head(i + 2))
                        wts.append(W_HEAD)
                    run(par(streams, wts))
                S.barrier()

        if "A2" in phases:
            with ExitStack() as es:
                sb = mk_alloc(es, "a2_")
                CE = 128
                cst, pfm = load_consts(sb, 128)
                wgate = sb("wgate", [128, 8, 2048], BF16)
                wba = sb("wba", [128, 4, D], BF16)
                wbr = sb("wbr", [128, 4, D], BF16)
                for hh in range(2):
                    wload(wgate, wgate[:, 4 * hh:4 * hh + 4, :], w_gate.rearrange("(c p) n -> p c n", p=128)[:, 4 * hh:4 * hh + 4, :])
                wload(wba, wba[:], w_ba.rearrange("(c p) n -> p c n", p=128))
                wload(wbr, wbr[:], w_br.rearrange("(c p) n -> p c n", p=128))
                S.finalize(ch_w, [cst, pfm, wgate, wba, wbr])
                PS = [Tile(es.enter_context(nc.psum_tensor("psb%d" % i, [128, 512], F32)), "psb%d" % i, excl=True) for i in range(8)]
                xb = [sb("xb0", [128, D]), sb("xb1", [128, D])]
                yT = [sb("yT0", [128, 8, 128], BF16), sb("yT1", [128, 8, 128], BF16)]
                xs = sb("xs", [128, D])
                st4 = sb("st4", [128, 4])
                uTs = [sb("uT0", [128, 8, 128], BF16), sb("uT1", [128, 8, 128], BF16)]
                sg = sb("sg", [128, 16, 128])
                hbg = sb("hbg", [128, 16])
                V(lambda e: e.tensor_scalar(out=hbg[:], in0=pfm[:, P_BG:P_BG + 16], scalar1=0.5, scalar2=None, op0=ALU.mult), R=[pfm], W=[hbg])
                t1 = sb("t1", [128, 8, 128])
                t2 = sb("t2", [128, 8, 128])
                mT = [sb("mT0", [128, 8, 128], BF16), sb("mT1", [128, 8, 128], BF16)]

                def front2(n):
                    xt = xb[n % 2]
                    yTn = yT[n % 2]
                    yield S.dma("sp", ch_x[n % 2], lambda e: e.dma_start(out=xt[:], in_=xe[n * 128:(n + 1) * 128, :]), W=[xt])
                    yield S.dma("sp", ch_y[n % 2], lambda e: e.dma_start(out=yTn[:].rearrange("p c t -> p (c t)"), in_=yscr[n - 1]), R=[yscr_t[n - 1]], W=[yTn])
                    yield from norm_T(xt, xs, st4, cst, CE, pfm, P_GMIX, [PS[0], PS[1]], uTs[n % 2])

                def back2(n):
                    yTn = yT[n % 2]
                    mTn = mT[n % 2]
                    uT = uTs[n % 2]
                    for g in range(4):
                        bank = PS[2 + (g % 2)]
                        fns = []
                        for i in range(4):
                            col = (4 * g + i) * 128
                            for kc in range(8):
                                fns.append(lambda e, i=i, col=col, kc=kc, bank=bank: e.matmul(bank[:, i * 128:(i + 1) * 128], wgate[:, kc, col:col + 128], uT[:, kc, :],
                                                                                             start=(kc == 0), stop=(kc == 7)))
                        yield PE(fns, R=[uT, wgate], W=[bank])
                        for i in range(4):
                            yield A(lambda e, g=g, i=i, bank=bank: e.activation(out=sg[:, 4 * g + i, :], in_=bank[:, i * 128:(i + 1) * 128], func=AF.Tanh,
                                                                                bias=hbg[:, 4 * g + i:4 * g + i + 1], scale=0.5), R=[bank, hbg], W=[sg])
                    for br, (wb, off) in enumerate(((wba, 0), (wbr, 4))):
                        for hh in range(2):
                            bank = PS[4 + 2 * br + hh]
                            fns = []
                            for i in range(4):
                                fc = 4 * hh + i
                                for kc in range(4):
                                    fns.append(lambda e, i=i, fc=fc, kc=kc, bank=bank, wb=wb, off=off: e.matmul(bank[:, i * 128:(i + 1) * 128], wb[:, kc, fc * 128:(fc + 1) * 128],
                                                                                                                yTn[:, off + kc, :], start=(kc == 0), stop=(kc == 3)))
                            yield PE(fns, R=[yTn, wb], W=[bank])
                    for hh in range(2):
                        yield V(lambda e, hh=hh: e.scalar_tensor_tensor(out=t1[:, 4 * hh:4 * hh + 4, :], in0=sg[:, 4 * hh:4 * hh + 4, :], scalar=1.0,
                                                                        in1=PS[4 + hh][:].rearrange("p (c t) -> p c t", c=4), op0=ALU.add, op1=ALU.mult),
                                R=[PS[4 + hh], sg], W=[t1])
                        yield V(lambda e, hh=hh: e.scalar_tensor_tensor(out=t2[:, 4 * hh:4 * hh + 4, :], in0=sg[:, 8 + 4 * hh:8 + 4 * hh + 4, :], scalar=1.0,
                                                                        in1=PS[6 + hh][:].rearrange("p (c t) -> p c t", c=4), op0=ALU.add, op1=ALU.mult),
                                R=[PS[6 + hh], sg], W=[t2])
                    yield G(lambda e: e.tensor_tensor(out=t1[:], in0=t1[:], in1=t2[:], op=ALU.add), R=[t2], W=[t1])
                    yield A(lambda e: e.activation(out=mTn[:], in_=t1[:], func=AF.Copy, scale=0.5), R=[t1], W=[mTn])
                    yield S.dma("sp", ch_st[n % 2], lambda e: e.dma_start(out=mscr[n - 1], in_=mTn[:].rearrange("p c t -> p (c t)")), R=[mTn], W=[mscr_t[n - 1]])

                if nt > 1:
                    run(front2(1))
                for n in range(1, nt):
                    streams = [back2(n)]
                    wts = [2]
                    if n + 1 < nt:
                        streams.append(front2(n + 1))
                        wts.append(1)
                    run(par(streams, wts))
                S.barrier()

        if "B" in phases:
            with ExitStack() as es:
                sb = mk_alloc(es, "b_")
                CE = 128
                cst, pfm = load_consts(sb, 128)
                gfin = sb("gfin", [128, D])
                S.dma("sp", ch_w, lambda e: e.dma_start(out=gfin[:], in_=gfind.broadcast_to([128, D])), W=[gfin])
                wo = sb("wo", [128, 8, D], BF16)
                wg = sb("wg", [128, 8, DFF], BF16)
                wu = sb("wu", [128, 8, DFF], BF16)
                wd = sb("wd", [128, NFC, D], BF16)
                for hh in range(2):
                    wload(wo, wo[:, 4 * hh:4 * hh + 4, :], w_o.rearrange("(c p) n -> p c n", p=128)[:, 4 * hh:4 * hh + 4, :])
                for wt, wsrc in ((wg, w_fg), (wu, w_fu)):
                    for hh in range(2):
                        for ch in range(2):
                            wload(wt, wt[:, 4 * hh:4 * hh + 4, ch * 1408:(ch + 1) * 1408],
                                  wsrc.rearrange("(c p) n -> p c n", p=128)[:, 4 * hh:4 * hh + 4, ch * 1408:(ch + 1) * 1408])
                for hh in range(2):
                    wload(wd, wd[:, 11 * hh:11 * hh + 11, :], w_fd.rearrange("(c p) n -> p c n", p=128)[:, 11 * hh:11 * hh + 11, :])
                S.finalize(ch_w, [cst, pfm, gfin, wo, wg, wu, wd])
                PS = [Tile(es.enter_context(nc.psum_tensor("psc%d" % i, [128, 512], F32)), "psc%d" % i, excl=True) for i in range(8)]
                xb = [sb("xb0", [128, D]), sb("xb1", [128, D])]
                mT = [sb("mT0", [128, 8, 128], BF16), sb("mT1", [128, 8, 128], BF16)]
                h1s = [sb("h1a", [128, D]), sb("h1b", [128, D])]
                xsF = sb("xsF", [128, D])
                xsB = [sb("xsB0", [128, D]), sb("xsB1", [128, D])]
                st4 = sb("st4", [128, 4])
                st4b = sb("st4b", [128, 4])
                fTs = [sb("fT0", [128, 8, 128], BF16), sb("fT1", [128, 8, 128], BF16)]
                sl_ = sb("silu", [128, 4, 128])
                aT = sb("aT", [128, NFC, 128], BF16)

                def front3(n):
                    xt = xb[n % 2]
                    mTn = mT[n % 2]
                    h1 = h1s[n % 2]
                    yield S.dma("sp", ch_x[n % 2], lambda e: e.dma_start(out=xt[:], in_=xe[n * 128:(n + 1) * 128, :]), W=[xt])
                    yield S.dma("sp", ch_y[n % 2], lambda e: e.dma_start(out=mTn[:].rearrange("p c t -> p (c t)"), in_=mscr[n - 1]), R=[mscr_t[n - 1]], W=[mTn])
                    for hh in range(2):
                        yield PE([lambda e, kc=kc, hh=hh: e.matmul(PS[hh][:], mTn[:, kc, :], wo[:, kc, hh * 512:(hh + 1) * 512], start=(kc == 0), stop=(kc == 7)) for kc in range(8)],
                                 R=[mTn, wo], W=[PS[hh]])
                        yield V(lambda e, hh=hh: e.tensor_tensor(out=h1[:, hh * 512:(hh + 1) * 512], in0=PS[hh][:], in1=xt[:, hh * 512:(hh + 1) * 512], op=ALU.add),
                                R=[PS[hh], xt], W=[h1])
                    yield from norm_T(h1, xsF, st4, cst, CE, pfm, P_GFFN, [PS[2], PS[3]], fTs[n % 2])

                def back3(n):
                    h1 = h1s[n % 2]
                    fT = fTs[n % 2]
                    o = xsB[n % 2]
                    ngrp = (NFC + 3) // 4
                    for g in range(ngrp):
                        nchunk = min(4, NFC - 4 * g)
                        bg = PS[4 + 2 * (g % 2)]
                        bu = PS[5 + 2 * (g % 2)]
                        for bank, wt in ((bg, wg), (bu, wu)):
                            fns = []
                            for i in range(nchunk):
                                fc = 4 * g + i
                                for kc in range(8):
                                    fns.append(lambda e, i=i, fc=fc, kc=kc, bank=bank, wt=wt: e.matmul(bank[:, i * 128:(i + 1) * 128], wt[:, kc, fc * 128:(fc + 1) * 128], fT[:, kc, :],
                                                                                                       start=(kc == 0), stop=(kc == 7)))
                            yield PE(fns, R=[fT, wt], W=[bank])
                        yield A(lambda e: e.activation(out=sl_[:, 0:nchunk, :], in_=bg[:, 0:nchunk * 128].rearrange("p (c t) -> p c t", c=nchunk), func=AF.Tanh, scale=0.5),
                                R=[bg], W=[sl_])
                        yield V(lambda e: e.scalar_tensor_tensor(out=sl_[:, 0:nchunk, :], in0=sl_[:, 0:nchunk, :], scalar=1.0,
                                                                 in1=bg[:, 0:nchunk * 128].rearrange("p (c t) -> p c t", c=nchunk), op0=ALU.add, op1=ALU.mult), R=[bg], W=[sl_])
                        yield V(lambda e: e.scalar_tensor_tensor(out=aT[:, 4 * g:4 * g + nchunk, :], in0=sl_[:, 0:nchunk, :], scalar=0.5,
                                                                 in1=bu[:, 0:nchunk * 128].rearrange("p (c t) -> p c t", c=nchunk), op0=ALU.mult, op1=ALU.mult), R=[bu, sl_], W=[aT])
                    for hh in range(2):
                        yield PE([lambda e, fc=fc, hh=hh: e.matmul(PS[4 + hh][:], aT[:, fc, :], wd[:, fc, hh * 512:(hh + 1) * 512], start=(fc == 0), stop=(fc == NFC - 1)) for fc in range(NFC)],
                                 R=[aT, wd], W=[PS[4 + hh]])
                        yield V(lambda e, hh=hh: e.tensor_tensor(out=h1[:, hh * 512:(hh + 1) * 512], in0=PS[4 + hh][:], in1=h1[:, hh * 512:(hh + 1) * 512], op=ALU.add),
                                R=[PS[4 + hh]], W=[h1])
                    yield A(lambda e: e.activation(out=o[:], in_=h1[:], func=AF.Square, accum_out=st4b[:, 0:1]), R=[h1], W=[o, st4b])
                    yield V(lambda e: e.tensor_scalar(out=st4b[:, 1:2], in0=st4b[:, 0:1], scalar1=1.0 / D, scalar2=RMS_EPS, op0=ALU.mult, op1=ALU.add), R=[], W=[st4b])
                    yield G(lambda e: e.tensor_tensor(out=st4b[:, 2:3], in0=st4b[:, 1:2], in1=cst[:, CE + 4:CE + 5], op=ALU.pow), R=[cst], W=[st4b])
                    yield A(lambda e: e.activation(out=o[:], in_=h1[:], func=AF.Identity, scale=st4b[:, 2:3], bias=cst[:, CE + 2:CE + 3]), R=[h1, cst], W=[o, st4b])
                    yield G(lambda e: e.tensor_tensor(out=o[:], in0=o[:], in1=gfin[:], op=ALU.mult), R=[gfin], W=[o])
                    yield S.dma("sp", ch_st[n % 2], lambda e: e.dma_start(out=outd[(n - 1) * 128:n * 128, :], in_=o[:]), R=[o], W=[])

                if nt > 1:
                    run(front3(1))
                for n in range(1, nt):
                    streams = [back3(n)]
                    wts = [2]
                    if n + 1 < nt:
                        streams.append(front3(n + 1))
                        wts.append(1)
                    run(par(streams, wts))
                S.barrier()
        else:
            S.barrier()
    return nc


QPERM = [0, 4, 1, 5, 2, 6, 3, 7]


def make_consts():
    c = np.zeros((128, C_END), np.float32)
    c[:, C_ID:C_ID + 128] = np.eye(128, dtype=np.float32)
    s = np.arange(64)
    for j in range(2):
        rows = slice(64 * j, 64 * j + 64)
        c[rows, C_MB:C_MB + 64] = (s[None, :] > s[:, None])
        c[rows, C_MB + 64:C_MB + 128] = (s[None, :] >= s[:, None])
        c[rows, C_ML:C_ML + 64] = (s[None, :] < s[:, None])
        c[rows, C_I64:C_I64 + 64] = np.eye(64)
        c[rows, C_OBD + 64 * j:C_OBD + 64 * j + 64] = 1.0
        c[rows, C_TRI:C_TRI + 64] = CFAC * (s[:, None] <= s[None, :])
        c[rows, C_TRI + 64:C_TRI + 128] = CFAC * (s[:, None] < s[None, :])
    c[64:128, C_TRI0:C_TRI0 + 128] = c[64:128, C_TRI:C_TRI + 128]
    c[64:64 + 48, C_TRI0:C_TRI0 + 128] = 0.0
    i = np.arange(128)
    own = np.where(i[None, :] <= i[:, None], 0.0, NEG)
    prev = np.where(i[None, :] > i[:, None], 0.0, NEG)
    full = np.full((128, 128), NEG)
    for var, (a, b) in enumerate(((own, prev), (prev, own), (full, own))):
        base = C_AM + 272 * var
        c[:, base:base + 128] = a
        c[:, base + 128:base + 256] = b
        c[:, base + 256:base + 272] = 0.0
    half = 8
    inv_freq = np.power(np.float32(500000.0), -np.arange(half, dtype=np.float32) * np.float32(2.0 / 16)).astype(np.float32)
    for n in range(NTILES):
        pos = (n * 128 + np.arange(128) - 112).astype(np.float32)
        ang = (pos[:, None] * inv_freq[None, :]).astype(np.float32)
        c[:, C_ROPE + 16 * n:C_ROPE + 16 * n + 8] = np.cos(ang)
        c[:, C_ROPE + 16 * n + 8:C_ROPE + 16 * n + 16] = np.sin(ang)
    return c


def prep_shared(inp):
    f = np.float32
    w_in = np.asarray(inp["w_in"][0], f)
    b_in = np.asarray(inp["b_in"][0], f)
    qcols = np.concatenate([np.arange(h * 64, (h + 1) * 64) for h in QPERM])
    w_qkv = np.ascontiguousarray(np.concatenate([w_in[:, qcols], w_in[:, 512:768]], axis=1))
    b_qkv = np.concatenate([b_in[qcols], b_in[512:768]])
    R0 = 768
    w_fm = np.zeros((D, 2048), f)
    b_fm = np.zeros((2048,), f)
    mix = np.asarray(inp["rwkv_mix"][0], f)
    mix_fm = np.zeros((2048,), f)

    def put(dst0, src0, n):
        w_fm[:, dst0:dst0 + n] = w_in[:, R0 + src0:R0 + src0 + n]
        b_fm[dst0:dst0 + n] = b_in[R0 + src0:R0 + src0 + n]
        mix_fm[dst0:dst0 + n] = mix[src0:src0 + n]
    put(0, 0, 1536)
    put(1536, 1536, 64)
    put(1664, 1600, 64)
    put(1792, 1664, 128)
    put(1920, 1792, 32)
    G0 = 768 + 1824
    w_gate = np.ascontiguousarray(w_in[:, G0:G0 + 2048])
    b_gate = b_in[G0:G0 + 2048]
    rows_perm = qcols
    sh = {
        "w_qkv": w_qkv, "w_fm": w_fm, "w_gate": w_gate,
        "w_ba": np.ascontiguousarray(np.asarray(inp["w_br_attn"][0], f)[rows_perm, :]),
        "w_br": np.ascontiguousarray(np.asarray(inp["w_br_rwkv"][0], f)),
        "w_o": np.ascontiguousarray(np.asarray(inp["w_o"][0], f)),
        "w_fg": np.ascontiguousarray(np.asarray(inp["w_ffn_gate"][0], f)),
        "w_fu": np.ascontiguousarray(np.asarray(inp["w_ffn_up"][0], f)),
        "w_fd": np.ascontiguousarray(np.asarray(inp["w_ffn_down"][0], f)),
        "w2": np.ascontiguousarray(np.asarray(inp["rwkv_w2"][0], f)),
        "a2": np.ascontiguousarray(np.asarray(inp["rwkv_a2"][0], f)),
    }
    g2p = np.zeros((256, 512), f)
    g2p[0:160] = np.asarray(inp["rwkv_g2"][0], f)
    sh["g2p"] = g2p
    pfm = np.zeros((128, P_END), f)

    def fm(vec, ncol):
        return np.asarray(vec, f).reshape(ncol, 128).T
    pfm[:, P_GMIX:P_GMIX + 8] = fm(inp["norm_mix_g"][0], 8)
    pfm[:, P_GFFN:P_GFFN + 8] = fm(inp["norm_ffn_g"][0], 8)
    pfm[:, P_BFM:P_BFM + 16] = fm(b_fm, 16)
    pfm[:, P_BG:P_BG + 16] = fm(b_gate, 16)
    pfm[:, P_MIX:P_MIX + 16] = fm(mix_fm, 16)
    pfm[:, P_A0:P_A0 + 4] = fm(inp["rwkv_a0"][0], 4)
    pfm[:, P_KK:P_KK + 4] = fm(inp["rwkv_k_k"][0], 4)
    pfm[:, P_KA:P_KA + 4] = fm(inp["rwkv_k_a"][0], 4)
    pfm[:, P_RK:P_RK + 4] = fm(np.asarray(inp["rwkv_r_k"][0], f).reshape(-1), 4)
    sh["pfm"] = pfm
    rowsA = np.zeros((1, RA_END), f)
    rowsA[0, RA_BQ:RA_BQ + 768] = b_qkv
    rowsA[0, RA_W0:RA_W0 + 512] = np.asarray(inp["rwkv_w0"][0], f)
    rowsA[0, RA_SK:RA_SK + 8] = np.asarray(inp["attn_sinks"][0], f)[QPERM]
    sh["rowsA"] = rowsA
    sh["gfin"] = np.asarray(inp["norm_final_g"], f).reshape(1, D).copy()
    lnst = np.zeros((128, 2, 4, 64), f)
    for a, key in enumerate(("rwkv_ln_w", "rwkv_ln_b")):
        v = np.asarray(inp[key][0], f).reshape(4, 2, 64)
        for j in range(2):
            lnst[64 * j:64 * j + 64, a, :, :] = v[None, :, j, :]
    sh["lnst"] = lnst.reshape(128, -1)
    sh["cst"] = make_consts()
    return sh


def prep_xe(inp, b):
    xe = np.zeros((NTILES * 128, D), np.float32)
    xe[112:128] = np.asarray(inp["meta_tokens"], np.float32)
    xe[128:] = np.asarray(inp["x"][b], np.float32)
    return xe


_NC_CACHE = {}


def kernel(**inputs):
    n = 8
    sh = prep_shared(inputs)
    in_maps = []
    for b in range(n):
        m = dict(sh)
        m["xe"] = prep_xe(inputs, b)
        in_maps.append(m)
    if "nc" not in _NC_CACHE:
        _NC_CACHE["nc"] = build_program()
    res = run_bass_kernel_spmd(_NC_CACHE["nc"], in_maps, core_ids=list(range(n)))
    out = np.stack([np.asarray(r["out"], np.float32).reshape(4096, D) for r in res.results], axis=0)
    return out
```

```python
import numpy as np
import ml_dtypes
from contextlib import ExitStack
import concourse.bass as bass
import concourse.mybir as mybir
from concourse.bass_utils import run_bass_kernel_spmd

F32 = mybir.dt.float32
BF16 = mybir.dt.bfloat16
AF = mybir.ActivationFunctionType
ALU = mybir.AluOpType
AX = mybir.AxisListType

NTILES = 33
D = 1024
DFF = 2816
NFC = 22
RMS_EPS = 1e-6
LN_EPS = 64e-5
CFAC = -float(np.exp(-0.5))
NEG = -1e30
MD = BF16

C_ID = 0
C_MB = 128
C_ML = 256
C_I64 = 320
C_OBD = 384
C_TRI = 512
C_TRI0 = 640
C_AM = 768
C_ROPE = 768 + 816
C_END = C_ROPE + 33 * 16
P_GMIX, P_GFFN, P_BFM, P_BG, P_MIX, P_A0, P_KK, P_KA, P_RK, P_END = 0, 8, 16, 32, 48, 64, 68, 72, 76, 80
RA_BQ, RA_W0, RA_SK, RA_END = 0, 768, 1280, 1288


class Tile:
    def __init__(self, t, name, excl=False):
        self.t = t
        self.name = name
        self.w = None
        self.r = {}
        self.excl = excl

    def __getitem__(self, i):
        return self.t[i]


class Chan:
    def __init__(self, sem, key):
        self.sem = sem
        self.key = key
        self.count = 0


class Sched:
    def __init__(self, nc, es):
        self.nc = nc
        self.es = es
        self.E = {}
        for name, eng in (("pe", nc.tensor), ("act", nc.scalar), ("dve", nc.vector),
                          ("pool", nc.gpsimd), ("sp", nc.sync)):
            sem = es.enter_context(nc.semaphore("sem_" + name))
            self.E[name] = dict(eng=eng, sem=sem, count=0, seen={}, name=name)
        self.chans = []

    def chan(self, name):
        c = Chan(self.es.enter_context(self.nc.semaphore("ch_" + name)), "ch_" + name)
        self.chans.append(c)
        return c

    def _waits(self, E, R, W):
        deps = {}

        def add(d):
            key, val, sem = d
            if key not in deps or deps[key][0] < val:
                deps[key] = (val, sem)
        for t in R:
            if t.w is not None:
                add(t.w)
            if t.excl:
                for key, (val, sem) in t.r.items():
                    if key != E["name"]:
                        add((key, val, sem))
        for t in W:
            if t.w is not None:
                add(t.w)
            for key, (val, sem) in t.r.items():
                add((key, val, sem))
        for key, (val, sem) in deps.items():
            if key == "pe" and E["name"] == "pe":
                continue
            if E["seen"].get(key, 0) < val:
                E["eng"].wait_ge(sem, val)
                E["seen"][key] = val

    def op(self, ename, fns, R=(), W=()):
        E = self.E[ename]
        self._waits(E, R, W)
        if not isinstance(fns, (list, tuple)):
            fns = [fns]
        inst = None
        for f in fns:
            inst = f(E["eng"])
        E["count"] += 1
        inst.then_inc(E["sem"], 1)
        for t in W:
            t.w = (ename, E["count"], E["sem"])
            t.r = {}
        for t in R:
            if t not in W:
                t.r[ename] = (E["count"], E["sem"])

    def dma(self, qname, chan, fn, R=(), W=()):
        E = self.E[qname]
        self._waits(E, R, W)
        inst = fn(E["eng"])
        chan.count += 16
        inst.then_inc(chan.sem, 16)
        for t in W:
            t.w = (chan.key, chan.count, chan.sem)
            t.r = {}
        for t in R:
            t.r[chan.key] = (chan.count, chan.sem)

    def finalize(self, chan, tiles):
        for t in tiles:
            t.w = (chan.key, chan.count, chan.sem)

    def barrier(self):
        for name, E in self.E.items():
            for oname, O in self.E.items():
                if oname == name or O["count"] == 0:
                    continue
                if E["seen"].get(oname, 0) < O["count"]:
                    E["eng"].wait_ge(O["sem"], O["count"])
                    E["seen"][oname] = O["count"]
            for c in self.chans:
                if c.count and E["seen"].get(c.key, 0) < c.count:
                    E["eng"].wait_ge(c.sem, c.count)
                    E["seen"][c.key] = c.count


def build_program(nt=NTILES, phases=("A1", "A2", "B"), dbg=None, dbg_n=-1, md=None, scr_ext=False, stop=None):
    global MD
    if md is not None:
        MD = md
    nc = bass.Bass("TRN2", target_bir_lowering=False)

    def din(name, shape, dt=F32):
        return nc.dram_tensor(name, list(shape), dt, kind="ExternalInput").ap()

    xe = din("xe", [NTILES * 128, D])
    w_qkv = din("w_qkv", [D, 768])
    w_fm = din("w_fm", [D, 2048])
    w_gate = din("w_gate", [D, 2048])
    w_ba = din("w_ba", [512, D])
    w_br = din("w_br", [512, D])
    w_o = din("w_o", [D, D])
    w_fg = din("w_fg", [D, DFF])
    w_fu = din("w_fu", [D, DFF])
    w_fd = din("w_fd", [DFF, D])
    w2d = din("w2", [64, 512])
    a2d = din("a2", [64, 512])
    g2d = din("g2p", [256, 512])
    pfmd = din("pfm", [128, P_END])
    rowsAd = din("rowsA", [1, RA_END])
    gfind = din("gfin", [1, D])
    lnstd = din("lnst", [128, 2 * 4 * 64])
    cstd = din("cst", [128, C_END])
    outd = nc.dram_tensor("out", [(NTILES - 1) * 128, D], F32, kind="ExternalOutput").ap()
    skind = "ExternalOutput" if scr_ext else "Internal"
    yscr = nc.dram_tensor("yscr", [NTILES - 1, 128, 8 * 128], BF16, kind=skind).ap()
    mscr = nc.dram_tensor("mscr", [NTILES - 1, 128, 8 * 128], BF16, kind=skind).ap()
    dbg_out = {}
    if dbg:
        for name, shape in dbg.items():
            dbg_out[name] = nc.dram_tensor("dbg_" + name, list(shape), F32, kind="ExternalOutput").ap()

    with ExitStack() as es0:
        S = Sched(nc, es0)
        ch_w = S.chan("w")
        ch_x = [S.chan("x0"), S.chan("x1")]
        ch_y = [S.chan("y0"), S.chan("y1")]
        ch_st = [S.chan("s0"), S.chan("s1")]
        ch_dbg = S.chan("dbg")
        yscr_t = [Tile(None, "yscr%d" % i) for i in range(NTILES - 1)]
        mscr_t = [Tile(None, "mscr%d" % i) for i in range(NTILES - 1)]

        def V(fn, R=(), W=()):
            S.op("dve", fn, R, W)

        def A(fn, R=(), W=()):
            S.op("act", fn, R, W)

        def G(fn, R=(), W=()):
            S.op("pool", fn, R, W)

        def PE(fns, R=(), W=()):
            S.op("pe", fns, R, W)

        def mk_alloc(es, pfx):
            def sb(name, shape, dt=F32):
                return Tile(es.enter_context(nc.sbuf_tensor(pfx + name, list(shape), dt)), pfx + name)
            return sb

        def dump(name, tile_ap, tiles):
            if name in dbg_out:
                S.dma("sp", ch_dbg, lambda e: e.dma_start(out=dbg_out[name], in_=tile_ap), R=tiles, W=[])

        def run(gen):
            for _ in gen:
                pass

        def par(gens, weights=None):
            gens = list(gens)
            w = list(weights) if weights else [1] * len(gens)
            alive = [True] * len(gens)
            while any(alive):
                for i, g in enumerate(gens):
                    if not alive[i]:
                        continue
                    for _ in range(w[i]):
                        try:
                            next(g)
                        except StopIteration:
                            alive[i] = False
                            break
                yield

        def norm_T(x, xs, st4, cst, ce, pfm, gcol, TR2, uT):
            yield A(lambda e: e.activation(out=xs[:], in_=x[:], func=AF.Square, accum_out=st4[:, 0:1]), R=[x], W=[xs, st4])
            yield V(lambda e: e.tensor_scalar(out=st4[:, 1:2], in0=st4[:, 0:1], scalar1=1.0 / D, scalar2=RMS_EPS, op0=ALU.mult, op1=ALU.add), R=[st4], W=[st4])
            yield G(lambda e: e.tensor_tensor(out=st4[:, 2:3], in0=st4[:, 1:2], in1=cst[:, ce + 4:ce + 5], op=ALU.pow), R=[st4, cst], W=[st4])
            yield A(lambda e: e.activation(out=xs[:], in_=x[:], func=AF.Identity, scale=st4[:, 2:3], bias=cst[:, ce + 2:ce + 3]),
                    R=[x, st4, cst], W=[xs])
            for h in range(2):
                yield PE([lambda e, c=c: e.transpose(TR2[h][:, (c % 4) * 128:(c % 4 + 1) * 128], xs[:, c * 128:(c + 1) * 128], cst[:, C_ID:C_ID + 128])
                          for c in range(4 * h, 4 * h + 4)], R=[xs, cst], W=[TR2[h]])
                yield V(lambda e, h=h: e.tensor_tensor(out=uT[:, 4 * h:4 * h + 4, :], in0=TR2[h][:].rearrange("p (c k) -> p c k", k=128),
                                                       in1=pfm[:, gcol + 4 * h:gcol + 4 * h + 4].unsqueeze(2).broadcast_to([128, 4, 128]), op=ALU.mult),
                        R=[TR2[h], pfm], W=[uT])

        def load_consts(sb, ncols):
            cst = sb("cst", [128, ncols + 12])
            pfm = sb("pfm", [128, P_END])
            G(lambda e: e.memset(cst[:, ncols:ncols + 1], RMS_EPS), W=[cst])
            G(lambda e: e.memset(cst[:, ncols + 1:ncols + 2], LN_EPS), W=[cst])
            G(lambda e: e.memset(cst[:, ncols + 2:ncols + 4], 0.0), W=[cst])
            G(lambda e: e.memset(cst[:, ncols + 4:ncols + 12], -0.5), W=[cst])
            S.dma("sp", ch_w, lambda e: e.dma_start(out=cst[:, 0:ncols], in_=cstd[:, 0:ncols]), W=[cst])
            S.dma("sp", ch_w, lambda e: e.dma_start(out=pfm[:], in_=pfmd), W=[pfm])
            return cst, pfm

        def wload(tile_, out_ap, in_ap):
            S.dma("pool", ch_w, lambda e: e.dma_start(out=out_ap, in_=in_ap), W=[tile_])

        pre_w = {}
        es_a2w = ExitStack()
        es_bw = ExitStack()

        def alloc_a2w():
            sbw = mk_alloc(es_a2w, "a2w_")
            wba = sbw("wba", [128, 4, D], BF16)
            wbr = sbw("wbr", [128, 4, D], BF16)
            pre_w["a2"] = (wba, wbr)

        def load_a2w():
            wba, wbr = pre_w["a2"]
            wload(wba, wba[:], w_ba.rearrange("(c p) n -> p c n", p=128))
            wload(wbr, wbr[:], w_br.rearrange("(c p) n -> p c n", p=128))

        def alloc_bw():
            sbw = mk_alloc(es_bw, "bw_")
            wg = sbw("wg", [128, 8, DFF], BF16)
            wu = sbw("wu", [128, 8, DFF], BF16)
            pre_w["b"] = (wg, wu)

        def load_bw():
            wg, wu = pre_w["b"]
            for wt, wsrc in ((wg, w_fg), (wu, w_fu)):
                for hh in range(2):
                    for ch in range(2):
                        wload(wt, wt[:, 4 * hh:4 * hh + 4, ch * 1408:(ch + 1) * 1408],
                              wsrc.rearrange("(c p) n -> p c n", p=128)[:, 4 * hh:4 * hh + 4, ch * 1408:(ch + 1) * 1408])

        if "A2" in phases:
            alloc_a2w()
        if "A1" in phases:
            with ExitStack() as es:
                sb = mk_alloc(es, "a1_")
                CE = C_END
                cst, pfm = load_consts(sb, C_END)
                rowsA = sb("rowsA", [128, RA_END])
                S.dma("sp", ch_w, lambda e: e.dma_start(out=rowsA[:], in_=rowsAd.broadcast_to([128, RA_END])), W=[rowsA])
                lnst = sb("lnst", [128, 2, 4, 64])
                S.dma("sp", ch_w, lambda e: e.dma_start(out=lnst[:].rearrange("p a c v -> p (a c v)"), in_=lnstd), W=[lnst])
                wqkv = sb("wqkv", [128, 8, 768], BF16)
                wfm = sb("wfm", [128, 8, 2048], BF16)
                w2 = sb("w2", [64, 512])
                a2 = sb("a2", [64, 512])
                g2 = sb("g2", [128, 2, 512])
                S.dma("sp", ch_w, lambda e: e.dma_start(out=w2[:], in_=w2d), W=[w2])
                S.dma("sp", ch_w, lambda e: e.dma_start(out=a2[:], in_=a2d), W=[a2])
                S.dma("sp", ch_w, lambda e: e.dma_start(out=g2[:], in_=g2d.rearrange("(c p) n -> p c n", p=128)), W=[g2])
                wload(wqkv, wqkv[:], w_qkv.rearrange("(c p) n -> p c n", p=128))
                for hh in range(2):
                    wload(wfm, wfm[:, 4 * hh:4 * hh + 4, :], w_fm.rearrange("(c p) n -> p c n", p=128)[:, 4 * hh:4 * hh + 4, :])
                identb = sb("identb", [128, 128], BF16)
                ones2 = sb("ones2", [128, 2])
                G(lambda e: e.memset(ones2[:], 1.0), W=[ones2])
                S.finalize(ch_w, [cst, pfm, rowsA, lnst, wqkv, wfm, w2, a2, g2])
                if "A2" in phases:
                    load_a2w()
                    pre_w["a2_loaded"] = True
                V(lambda e: e.tensor_copy(identb[:], cst[:, C_ID:C_ID + 128]), R=[cst], W=[identb])

                PS = [Tile(es.enter_context(nc.psum_tensor("ps%d" % i, [128, 512], F32)), "ps%d" % i, excl=True) for i in range(8)]
                H0, Q0, A0, A1_, A2_, R0, R1_, R2 = PS
                H1 = H0

                xb = [sb("xb0", [128, D]), sb("xb1", [128, D])]
                xs = sb("xs", [128, D])
                st4 = sb("st4", [128, 4])
                uT = sb("uT", [128, 8, 128], BF16)
                stg = sb("stg", [128, 16, 129])
                pfs = [sb("pf0", [128, 16, 128]), sb("pf1", [128, 16, 128])]
                qkvs = [sb("qkv0", [128, 768]), sb("qkv1", [128, 768])]
                rtmp = sb("rtmp", [128, 4, 10, 8])
                qT = sb("qT", [128, 4, 128], BF16)
                Kbuf = sb("Kbuf", [128, 272], BF16)
                Vbuf = sb("Vbuf", [128, 3, 128], BF16)
                Pb = [sb("Pb0", [128, 272], BF16), sb("Pb1", [128, 272], BF16)]
                PT = [sb("PT0", [128, 3, 128], BF16), sb("PT1", [128, 3, 128], BF16)]
                sm = sb("sm", [128, 5, 8])
                yat = sb("yat", [128, 8, 64])
                yTa = [sb("yTa0", [128, 4, 128], BF16), sb("yTa1", [128, 4, 128], BF16)]
                yTr = [sb("yTr0", [128, 4, 128], BF16), sb("yTr1", [128, 4, 128], BF16)]
                th = sb("th", [64, 128])
                sgd = sb("sgd", [128, 2, 128])
                B1 = sb("B1", [128, 4, 128]); B2 = sb("B2", [128, 4, 128]); B3 = sb("B3", [128, 4, 128])
                B4 = sb("B4", [128, 4, 128])
                arTs = [sb("arT%d" % i, [128, 4, 2, 2, 64], MD) for i in range(2)]
                BT = sb("BT", [128, 4, 128], MD); KT = sb("KT", [128, 4, 128], MD)
                BH = sb("BH", [128, 4, 128], MD); KH = sb("KH", [128, 4, 128], MD)
                cumC = sb("cumC", [128, 4, 2])
                WCs = [sb("WC%d" % i, [128, 4, 2]) for i in range(2)]
                rks = [sb("rk%d" % i, [128, 2, 4]) for i in range(2)]
                B5s = [sb("B5_%d" % i, [128, 4, 128]) for i in range(2)]
                TTs = [sb("TT%d" % i, [128, 2, 4, 64], MD) for i in range(2)]
                Ast = [sb("Ast0", [128, 2, 4, 64], MD), sb("Ast1", [128, 2, 4, 64], MD)]
                Nst = [sb("Nst0", [128, 2, 4, 64], MD), sb("Nst1", [128, 2, 4, 64], MD)]
                NBs = [sb("NB%d" % i, [128, 2, 4, 2, 64], MD) for i in range(2)]
                NKs = [sb("NK%d" % i, [128, 2, 4, 2, 64], MD) for i in range(2)]
                Pc = [sb("Pc0", [128, 2, 4, 64], MD), sb("Pc1", [128, 2, 4, 64], MD)]
                Vst32s = [sb("Vst32_%d" % i, [128, 2, 4, 64]) for i in range(2)]
                Vsts = [sb("Vst_%d" % i, [128, 2, 4, 64], MD) for i in range(2)] if MD != F32 else Vst32s
                BKsts = [sb("BKst%d" % i, [128, 2, 2, 4, 64], MD) for i in range(2)]
                R1 = sb("R1", [128, 4, 64], MD); Ust = sb("Ust", [128, 4, 64], MD)
                Yst = sb("Yst", [128, 2, 4, 64]); yc = sb("yc", [128, 2, 4, 64]); ysq = sb("ysq", [128, 2, 4, 64])
                ST32 = sb("ST32", [128, 4, 64])
                STm = sb("STm", [128, 4, 64], MD) if MD != F32 else ST32
                gst = sb("gst", [128, 6, 8])
                hb = sb("hb", [128, 4])
                V(lambda e: e.tensor_scalar(out=hb[:], in0=pfm[:, P_A0:P_A0 + 4], scalar1=0.5, scalar2=None, op0=ALU.mult), R=[pfm], W=[hb])
                G(lambda e: e.memset(ST32[:], 0.0), W=[ST32])
                if MD != F32:
                    G(lambda e: e.memset(STm[:], 0.0), W=[STm])
                G(lambda e: e.memset(stg[:], 0.0), W=[stg])
                G(lambda e: e.memset(Vbuf[:], 0.0), W=[Vbuf])
                G(lambda e: e.memset(Kbuf[:], 0.0), W=[Kbuf])

                def v3(ap2d, k=64):
                    return ap2d.rearrange("p (c k) -> p c k", k=k)

                def v4(ap2d):
                    return ap2d.rearrange("p (q c k) -> p q c k", q=2, c=4)

                def cq(t):
                    return t.rearrange("p c (q t) -> p c q t", q=2)

                def qc(t):
                    return t.rearrange("p c (q t) -> p q c t", q=2)

                ID0 = C_ID if MD == F32 else 0
                idm = cst if MD == F32 else identb

                def idsl(sl, j):
                    return cst[sl, C_ID + 64 * j:C_ID + 64 * j + 64]

                def idm_sl(sl, j):
                    return idm[sl, ID0 + 64 * j:ID0 + 64 * j + 64]

                def hl(fn, qs=(0,)):
                    out = []
                    for q in qs:
                        for c in range(4):
                            for j in range(2):
                                out += fn(q, c, slice(64 * j, 64 * j + 64), j)
                    return out

                def head(n):
                    xt = xb[n % 2]
                    pf = pfs[n % 2]
                    qkv = qkvs[n % 2]
                    yield S.dma("sp", ch_x[n % 2], lambda e: e.dma_start(out=xt[:], in_=xe[n * 128:(n + 1) * 128, :]), W=[xt])
                    yield from norm_T(xt, xs, st4, cst, CE, pfm, P_GMIX, [H0, H1], uT)
                    yield PE([lambda e, kc=kc: e.matmul(H0[:, 0:512], uT[:, kc, :], wqkv[:, kc, 0:512], start=(kc == 0), stop=(kc == 7)) for kc in range(8)],
                             R=[uT, wqkv], W=[H0])
                    yield V(lambda e: e.tensor_tensor(out=qkv[:, 0:512], in0=H0[:, 0:512], in1=rowsA[:, RA_BQ:RA_BQ + 512], op=ALU.add), R=[H0, rowsA], W=[qkv])
                    yield PE([lambda e, kc=kc: e.matmul(H1[:, 0:256], uT[:, kc, :], wqkv[:, kc, 512:768], start=(kc == 0), stop=(kc == 7)) for kc in range(8)],
                             R=[uT, wqkv], W=[H1])
                    yield V(lambda e: e.tensor_tensor(out=qkv[:, 512:768], in0=H1[:, 0:256], in1=rowsA[:, RA_BQ + 512:RA_BQ + 768], op=ALU.add), R=[H1, rowsA], W=[qkv])
                    for g in range(4):
                        bank = (H0, H1)[g % 2]
                        fns = []
                        for i in range(4):
                            col = (4 * g + i) * 128
                            for kc in range(8):
                                fns.append(lambda e, i=i, col=col, kc=kc, bank=bank: e.matmul(bank[:, i * 128:(i + 1) * 128], wfm[:, kc, col:col + 128], uT[:, kc, :],
                                                                                             start=(kc == 0), stop=(kc == 7)))
                        yield PE(fns, R=[uT, wfm], W=[bank])
                        yield V(lambda e, g=g, bank=bank: e.tensor_tensor(out=stg[:, 4 * g:4 * g + 4, 1:129], in0=v3(bank[:], 128),
                                                                          in1=pfm[:, P_BFM + 4 * g:P_BFM + 4 * g + 4].unsqueeze(2).broadcast_to([128, 4, 128]), op=ALU.add),
                                R=[bank, pfm], W=[stg])
                    if n == 0:
                        yield G(lambda e: e.memset(stg[:, :, 1:113], 0.0), W=[stg])
                    yield G(lambda e: e.tensor_tensor(out=pf[:], in0=stg[:, :, 0:128], in1=stg[:, :, 1:129], op=ALU.subtract), R=[stg], W=[pf])
                    yield G(lambda e: e.tensor_tensor(out=pf[:], in0=pf[:], in1=pfm[:, P_MIX:P_MIX + 16].unsqueeze(2).broadcast_to([128, 16, 128]), op=ALU.mult),
                            R=[pfm], W=[pf])
                    yield G(lambda e: e.tensor_tensor(out=pf[:], in0=pf[:], in1=stg[:, :, 1:129], op=ALU.add), R=[stg], W=[pf])
                    yield G(lambda e: e.tensor_copy(stg[:, :, 0:1], stg[:, :, 128:129]), R=[], W=[stg])

                def attention(n):
                    qkv = qkvs[n % 2]
                    slot = n % 2
                    q10 = qkv[:, 0:640].rearrange("p (h d) -> p h d", d=64)
                    cosb = cst[:, C_ROPE + 16 * n:C_ROPE + 16 * n + 8].unsqueeze(1).broadcast_to([128, 10, 8])
                    sinb = cst[:, C_ROPE + 16 * n + 8:C_ROPE + 16 * n + 16].unsqueeze(1).broadcast_to([128, 10, 8])
                    yield G(lambda e: e.tensor_tensor(out=rtmp[:, 0], in0=q10[:, :, 0:8], in1=cosb, op=ALU.mult), R=[qkv, cst], W=[rtmp])
                    yield G(lambda e: e.tensor_tensor(out=rtmp[:, 1], in0=q10[:, :, 8:16], in1=sinb, op=ALU.mult), R=[qkv, cst], W=[rtmp])
                    yield G(lambda e: e.tensor_tensor(out=rtmp[:, 2], in0=q10[:, :, 8:16], in1=cosb, op=ALU.mult), R=[qkv, cst], W=[rtmp])
                    yield G(lambda e: e.tensor_tensor(out=rtmp[:, 3], in0=q10[:, :, 0:8], in1=sinb, op=ALU.mult), R=[qkv, cst], W=[rtmp])
                    yield G(lambda e: e.tensor_tensor(out=q10[:, :, 0:8], in0=rtmp[:, 0], in1=rtmp[:, 1], op=ALU.subtract), R=[rtmp], W=[qkv])
                    yield G(lambda e: e.tensor_tensor(out=q10[:, :, 8:16], in0=rtmp[:, 2], in1=rtmp[:, 3], op=ALU.add), R=[rtmp], W=[qkv])
                    yield PE([lambda e, c=c: e.transpose(A0[:, c * 128:(c + 1) * 128], qkv[:, c * 128:(c + 1) * 128], cst[:, C_ID:C_ID + 128]) for c in range(4)],
                             R=[qkv, cst], W=[A0])
                    yield PE(lambda e: e.transpose(A1_[:, 0:128], qkv[:, 512:640], cst[:, C_ID:C_ID + 128]), R=[qkv, cst], W=[A1_])
                    yield A(lambda e: e.activation(out=qT[:], in_=v3(A0[:], 128), func=AF.Copy, scale=0.125), R=[A0], W=[qT])
                    yield A(lambda e: e.activation(out=Kbuf[:, slot * 128:(slot + 1) * 128], in_=A1_[:, 0:128], func=AF.Copy), R=[A1_], W=[Kbuf])
                    yield V(lambda e: e.tensor_copy(Vbuf[:, slot, :], qkv[:, 640:768]), R=[qkv], W=[Vbuf])
                    if n == 0:
                        yield A(lambda e: e.activation(out=Kbuf[:, 256:272], in_=A1_[:, 112:128], func=AF.Copy), R=[A1_], W=[Kbuf])
                        yield PE(lambda e: e.matmul(A1_[0:16, 128:256], cst[:, C_ID + 112:C_ID + 128], qkv[:, 640:768], start=True, stop=True), R=[qkv, cst], W=[A1_])
                        yield V(lambda e: e.tensor_copy(Vbuf[0:16, 2, :], A1_[0:16, 128:256]), R=[A1_], W=[Vbuf])
                        return
                    yTn = yTa[n % 2]
                    mvar = 2 if n == 1 else (0 if n % 2 == 0 else 1)
                    mask = cst[:, C_AM + 272 * mvar:C_AM + 272 * (mvar + 1)]
                    for s in range(8):
                        c, j = s // 2, s % 2
                        sl = slice(64 * j, 64 * j + 64)
                        SC = A0
                        Pk = Pb[s % 2]
                        PTk = PT[s % 2]
                        yield PE(lambda e: e.matmul(SC[:, 0:272], qT[sl, c, :], Kbuf[sl, 0:272], start=True, stop=True), R=[qT, Kbuf], W=[SC])
                        yield V(lambda e: e.tensor_tensor(out=SC[:, 0:272], in0=SC[:, 0:272], in1=mask, op=ALU.add), R=[cst], W=[SC])
                        yield V(lambda e: e.tensor_reduce(out=sm[:, 0, s:s + 1], in_=SC[:, 0:272], axis=AX.X, op=ALU.max), R=[SC], W=[sm])
                        yield V(lambda e: e.tensor_scalar(out=sm[:, 1, s:s + 1], in0=sm[:, 0, s:s + 1], scalar1=rowsA[:, RA_SK + s:RA_SK + s + 1], scalar2=-1.0,
                                                          op0=ALU.max, op1=ALU.mult), R=[rowsA], W=[sm])
                        yield A(lambda e: e.activation(out=Pk[:], in_=SC[:, 0:272], func=AF.Exp, bias=sm[:, 1, s:s + 1], scale=1.0,
                                                       accum_out=sm[:, 2, s:s + 1]), R=[SC], W=[Pk, sm])
                        yield A(lambda e: e.activation(out=sm[:, 3, s:s + 1], in_=rowsA[:, RA_SK + s:RA_SK + s + 1], func=AF.Exp, bias=sm[:, 1, s:s + 1], scale=1.0),
                                R=[rowsA], W=[sm])
                        yield PE([lambda e, b=b, nk=nk: e.matmul(A1_[0:nk, b * 128:(b + 1) * 128], Pk[:, b * 128:b * 128 + nk], identb[:], start=True, stop=True)
                                  for b, nk in ((0, 128), (1, 128), (2, 16))], R=[Pk, identb], W=[A1_])
                        yield A(lambda e: e.activation(out=PTk[:, 0:2, :], in_=v3(A1_[:, 0:256], 128), func=AF.Copy), R=[A1_], W=[PTk])
                        yield A(lambda e: e.activation(out=PTk[0:16, 2, :], in_=A1_[0:16, 256:384], func=AF.Copy), R=[A1_], W=[PTk])
                        yield PE([lambda e: e.matmul(A2_[:, s * 64:(s + 1) * 64], PTk[:, 0, :], Vbuf[:, 0, sl], start=True, stop=False),
                                  lambda e: e.matmul(A2_[:, s * 64:(s + 1) * 64], PTk[:, 1, :], Vbuf[:, 1, sl], start=False, stop=False),
                                  lambda e: e.matmul(A2_[:, s * 64:(s + 1) * 64], PTk[0:16, 2, :], Vbuf[0:16, 2, sl], start=False, stop=True)],
                                 R=[PTk, Vbuf], W=[A2_])
                    yield V(lambda e: e.tensor_tensor(out=sm[:, 2, :], in0=sm[:, 2, :], in1=sm[:, 3, :], op=ALU.add), R=[], W=[sm])
                    yield V(lambda e: e.reciprocal(out=sm[:, 4, :], in_=sm[:, 2, :]), R=[], W=[sm])
                    yield V(lambda e: e.tensor_tensor(out=yat[:], in0=v3(A2_[:], 64), in1=sm[:, 4, :].unsqueeze(2).broadcast_to([128, 8, 64]), op=ALU.mult),
                            R=[A2_], W=[yat, sm])
                    yield PE([lambda e, c=c: e.transpose(A1_[:, c * 128:(c + 1) * 128], yat[:, 2 * c:2 * c + 2, :].rearrange("p a d -> p (a d)"), cst[:, C_ID:C_ID + 128])
                              for c in range(4)], R=[yat, cst], W=[A1_])
                    yield A(lambda e: e.activation(out=yTn[:], in_=v3(A1_[:], 128), func=AF.Copy), R=[A1_], W=[yTn])
                    if n == dbg_n:
                        dump("yat", yat[:].rearrange("p s d -> p (s d)"), [yat])
                    yield S.dma("sp", ch_y[n % 2], lambda e: e.dma_start(out=yscr[n - 1][:, 0:512], in_=yTn[:].rearrange("p c t -> p (c t)")), R=[yTn], W=[yscr_t[n - 1]])

                def pre(n):
                    pf = pfs[n % 2]
                    pp_ = n % 2
                    arT, NB, NK, Vst, Vst32, BKst, WC, rk, B5 = arTs[pp_], NBs[pp_], NKs[pp_], Vsts[pp_], Vst32s[pp_], BKsts[pp_], WCs[pp_], rks[pp_], B5s[pp_]
                    tri0 = C_TRI0 if n == 0 else C_TRI
                    tri = cst[:, tri0:tri0 + 128]
                    yield A(lambda e: e.activation(out=th[:], in_=pf[0:64, 12, :], func=AF.Tanh), R=[pf], W=[th])
                    yield PE(lambda e: e.matmul(R0[:, 0:512], th[:], w2[:], start=True, stop=True), R=[th, w2], W=[R0])
                    B4f = B4[:].rearrange("p c t -> p (c t)")
                    yield V(lambda e: e.tensor_tensor(out=B4f, in0=R0[:, 0:512], in1=rowsA[:, RA_W0:RA_W0 + 512], op=ALU.add), R=[R0, rowsA], W=[B4])
                    yield A(lambda e: e.activation(out=B4f, in_=B4f, func=AF.Tanh, scale=0.5), R=[], W=[B4])
                    yield V(lambda e: e.tensor_scalar(out=B4f, in0=B4f, scalar1=0.5, scalar2=0.5, op0=ALU.mult, op1=ALU.add), R=[], W=[B4])
                    CB = (R1_, R2)
                    for q in range(2):
                        yield PE([lambda e, c=c, q=q: e.matmul(CB[q][:, c * 128:(c + 1) * 128], B4f[64 * q:64 * q + 64, c * 128:(c + 1) * 128],
                                                               tri[64 * q:64 * q + 64, :], start=True, stop=True) for c in range(4)], R=[B4, cst], W=[CB[q]])

                    def cums(a):
                        return [CB[q][:].rearrange("p (c a t) -> p c a t", c=4, a=2)[:, :, a, :] for q in range(2)]
                    yield PE([lambda e, c=c: e.matmul(R0[:, c * 128:(c + 1) * 128], a2[:, c * 128:(c + 1) * 128], pf[0:64, 13, :], start=True, stop=True) for c in range(4)],
                             R=[pf, a2], W=[R0])
                    for c in range(4):
                        yield A(lambda e, c=c: e.activation(out=B3[:, c, :], in_=R0[:, c * 128:(c + 1) * 128], func=AF.Tanh, bias=hb[:, c:c + 1], scale=0.5),
                                R=[R0, hb], W=[B3])
                    yield V(lambda e: e.tensor_scalar(out=B3[:], in0=B3[:], scalar1=0.5, scalar2=0.5, op0=ALU.mult, op1=ALU.add), R=[], W=[B3])
                    yield A(lambda e: e.activation(out=sgd[:], in_=pf[:, 14:16, :], func=AF.Tanh, scale=0.5), R=[pf], W=[sgd])
                    yield V(lambda e: e.tensor_scalar(out=sgd[:], in0=sgd[:], scalar1=0.5, scalar2=0.5, op0=ALU.mult, op1=ALU.add), R=[], W=[sgd])
                    fns = []
                    for c in range(4):
                        fns.append(lambda e, c=c: e.matmul(R0[:, c * 128:(c + 1) * 128], g2[:, 0, c * 128:(c + 1) * 128], sgd[:, 0, :], start=True, stop=False))
                        fns.append(lambda e, c=c: e.matmul(R0[:, c * 128:(c + 1) * 128], g2[0:32, 1, c * 128:(c + 1) * 128], sgd[0:32, 1, :], start=False, stop=True))
                    yield PE(fns, R=[sgd, g2], W=[R0])
                    yield A(lambda e: e.activation(out=B5[:].rearrange("p c t -> p (c t)"), in_=R0[:], func=AF.Copy), R=[R0], W=[B5])
                    kview = pf[:, 4:8, :]
                    rview = pf[:, 0:4, :]

                    def bc(col):
                        return pfm[:, col:col + 4].unsqueeze(2).broadcast_to([128, 4, 128])
                    yield V(lambda e: e.tensor_tensor(out=B1[:], in0=kview, in1=bc(P_KK), op=ALU.mult), R=[pf, pfm], W=[B1])
                    yield V(lambda e: e.tensor_tensor(out=B2[:], in0=B1[:], in1=B1[:], op=ALU.mult), R=[B1], W=[B2])
                    yield PE([lambda e, c=c: e.matmul(R0[:, c * 128:(c + 1) * 128], cst[:, C_OBD:C_OBD + 128], B2[:, c, :], start=True, stop=True) for c in range(4)],
                             R=[B2, cst], W=[R0])
                    yield A(lambda e: e.activation(out=B2[:].rearrange("p c t -> p (c t)"), in_=R0[:], func=AF.Sqrt), R=[R0], W=[B2])
                    yield V(lambda e: e.tensor_scalar(out=B2[:], in0=B2[:], scalar1=1e-12, scalar2=None, op0=ALU.max), R=[], W=[B2])
                    yield V(lambda e: e.reciprocal(out=B2[:], in_=B2[:]), R=[], W=[B2])
                    yield V(lambda e: e.tensor_tensor(out=B1[:], in0=B1[:], in1=B2[:], op=ALU.mult), R=[B2], W=[B1])
                    yield V(lambda e: e.scalar_tensor_tensor(out=B2[:], in0=B3[:], scalar=-1.0, in1=bc(P_KA), op0=ALU.add, op1=ALU.mult), R=[B3, pfm], W=[B2])
                    yield V(lambda e: e.scalar_tensor_tensor(out=kview, in0=B2[:], scalar=1.0, in1=kview, op0=ALU.add, op1=ALU.mult), R=[B2], W=[pf])
                    yield V(lambda e: e.tensor_tensor(out=B3[:], in0=B1[:], in1=B3[:], op=ALU.mult), R=[B1], W=[B3])
                    cex = cums(1)
                    cin = cums(0)
                    B4q = cq(B4[:])
                    for hh in range(2):
                        yield A(lambda e, hh=hh: e.activation(out=B4[:, :, 64 * hh:64 * hh + 64], in_=cex[hh], func=AF.Exp), R=[CB[hh]], W=[B4])
                    yield V(lambda e: e.scalar_tensor_tensor(out=arT[:, :, :, 0, :], in0=cq(B1[:]), scalar=-1.0, in1=B4q, op0=ALU.mult, op1=ALU.mult), R=[B1, B4], W=[arT])
                    for hh in range(2):
                        yield A(lambda e, hh=hh: e.activation(out=B4[:, :, 64 * hh:64 * hh + 64], in_=cin[hh], func=AF.Exp), R=[CB[hh]], W=[B4])
                    yield V(lambda e: e.tensor_tensor(out=arT[:, :, :, 1, :], in0=cq(rview), in1=B4q, op=ALU.mult), R=[pf, B4], W=[arT])
                    for hh in range(2):
                        yield A(lambda e, hh=hh: e.activation(out=B4[:, :, 64 * hh:64 * hh + 64], in_=cin[hh], func=AF.Exp, scale=-1.0), R=[CB[hh]], W=[B4])
                    yield V(lambda e: e.tensor_tensor(out=BT[:], in0=B3[:], in1=B4[:], op=ALU.mult), R=[B3, B4], W=[BT])
                    yield V(lambda e: e.tensor_tensor(out=KT[:], in0=kview, in1=B4[:], op=ALU.mult), R=[pf, B4], W=[KT])
                    for hh in range(2):
                        yield V(lambda e, hh=hh: e.tensor_copy(cumC[:, :, hh], cin[hh][:, :, 63]), R=[CB[hh]], W=[cumC])
                    for c in range(4):
                        for q in range(2):
                            yield A(lambda e, c=c, q=q: e.activation(out=B4[:, c, 64 * q:64 * q + 64], in_=cin[q][:, c, :], func=AF.Exp, scale=-1.0,
                                                                     bias=cumC[:, c, q:q + 1]), R=[CB[q], cumC], W=[B4])
                    yield V(lambda e: e.tensor_tensor(out=BH[:], in0=B3[:], in1=B4[:], op=ALU.mult), R=[B3, B4], W=[BH])
                    yield V(lambda e: e.tensor_tensor(out=KH[:], in0=kview, in1=B4[:], op=ALU.mult), R=[pf, B4], W=[KH])
                    yield A(lambda e: e.activation(out=WC[:], in_=cumC[:], func=AF.Exp), R=[cumC], W=[WC])

                    mb = cst[:, C_MB:C_MB + 128].rearrange("p (a t) -> p a t", a=2).unsqueeze(1).broadcast_to([128, 4, 2, 64])
                    ml8 = cst[:, C_ML:C_ML + 64].unsqueeze(1).broadcast_to([128, 8, 64])
                    i8 = cst[:, C_I64:C_I64 + 64].unsqueeze(1).broadcast_to([128, 8, 64])
                    Q2 = (0, 1)

                    def tq(q):
                        return slice(64 * q, 64 * q + 64)
                    yield PE(hl(lambda q, c, sl, j: [lambda e: e.matmul(R0[sl, q * 256 + c * 64:q * 256 + (c + 1) * 64], arT[sl, c, q, 0, :], BT[sl, c, tq(q)], start=True, stop=True)], Q2),
                             R=[arT, BT], W=[R0])
                    for q in Q2:
                        yield PE(hl(lambda q, c, sl, j: [lambda e: e.matmul(CB[q][sl, c * 128:(c + 1) * 128], BT[sl, c, tq(q)], arT[sl, c, q, :, :].rearrange("p a t -> p (a t)"),
                                                                            start=True, stop=True)], (q,)), R=[arT, BT], W=[CB[q]])
                    yield V(lambda e: e.tensor_tensor(out=Ast[0][:].rearrange("p q c t -> p (q c) t"), in0=v3(R0[:]), in1=ml8, op=ALU.mult), R=[R0, cst], W=[Ast[0]])
                    for q in Q2:
                        yield V(lambda e, q=q: e.tensor_tensor(out=NB[:, q], in0=CB[q][:].rearrange("p (c a t) -> p c a t", c=4, a=2), in1=mb, op=ALU.mult), R=[CB[q], cst], W=[NB])
                    for q in Q2:
                        yield PE(hl(lambda q, c, sl, j: [lambda e: e.matmul(CB[q][sl, c * 128:(c + 1) * 128], KT[sl, c, tq(q)], arT[sl, c, q, :, :].rearrange("p a t -> p (a t)"),
                                                                            start=True, stop=True)], (q,)), R=[arT, KT], W=[CB[q]])
                    for q in Q2:
                        yield V(lambda e, q=q: e.tensor_tensor(out=NK[:, q], in0=CB[q][:].rearrange("p (c a t) -> p c a t", c=4, a=2), in1=mb, op=ALU.mult), R=[CB[q], cst], W=[NK])
                    yield G(lambda e: e.tensor_copy(Nst[0][:], NB[:, :, :, 0, :]), R=[NB], W=[Nst[0]])
                    yield G(lambda e: e.tensor_tensor(out=Pc[0][:].rearrange("p q c t -> p (q c) t"), in0=Nst[0][:].rearrange("p q c t -> p (q c) t"), in1=i8, op=ALU.add),
                            R=[Nst[0], cst], W=[Pc[0]])
                    yield PE(hl(lambda q, c, sl, j: [lambda e: e.matmul(R0[sl, q * 256 + c * 64:q * 256 + (c + 1) * 64], pf[sl, 8 + c, tq(q)], idsl(sl, j), start=True, stop=True)], Q2),
                             R=[pf, cst], W=[R0])
                    yield A(lambda e: e.activation(out=Vst32[:], in_=v4(R0[:]), func=AF.Copy), R=[R0], W=[Vst32])
                    if MD != F32:
                        yield V(lambda e: e.tensor_copy(Vst[:], v4(R0[:])), R=[R0], W=[Vst])
                    for q in Q2:
                        yield PE(hl(lambda q, c, sl, j: [lambda e: e.matmul(CB[q][sl, c * 64:(c + 1) * 64], BH[sl, c, tq(q)], idm_sl(sl, j), start=True, stop=True),
                                                         lambda e: e.matmul(CB[q][sl, 256 + c * 64:256 + (c + 1) * 64], KH[sl, c, tq(q)], idm_sl(sl, j), start=True, stop=True)], (q,)),
                                 R=[BH, KH, idm], W=[CB[q]])
                    for q in Q2:
                        yield A(lambda e, q=q: e.activation(out=BKst[:, q], in_=CB[q][:].rearrange("p (a c t) -> p a c t", a=2, c=4), func=AF.Copy), R=[CB[q]], W=[BKst])
                    cur = 0
                    for lvl in range(1, 6):
                        nxt = 1 - cur
                        yield PE(hl(lambda q, c, sl, j: [lambda e: e.matmul(R0[sl, q * 256 + c * 64:q * 256 + (c + 1) * 64], Nst[cur][sl, q, c, :], Ast[cur][sl, q, c, :], start=True, stop=True)], Q2),
                                 R=[Nst[cur], Ast[cur]], W=[R0])
                        if lvl < 5:
                            yield PE(hl(lambda q, c, sl, j: [lambda e: e.matmul(R1_[sl, q * 256 + c * 64:q * 256 + (c + 1) * 64], Ast[cur][sl, q, c, :], Nst[cur][sl, q, c, :], start=True, stop=True)], Q2),
                                     R=[Nst[cur], Ast[cur]], W=[R1_])
                        yield A(lambda e, nxt=nxt: e.activation(out=Ast[nxt][:], in_=v4(R0[:]), func=AF.Copy), R=[R0], W=[Ast[nxt]])
                        if lvl < 5:
                            yield V(lambda e, nxt=nxt: e.tensor_copy(Nst[nxt][:], v4(R1_[:])), R=[R1_], W=[Nst[nxt]])
                        pc, pn = Pc[(lvl - 1) % 2], (Pc[lvl % 2] if lvl < 5 else TTs[pp_])
                        yield PE(hl(lambda q, c, sl, j: [lambda e: e.matmul(R2[sl, q * 256 + c * 64:q * 256 + (c + 1) * 64], Ast[nxt][sl, q, c, :], pc[sl, q, c, :], start=True, stop=True)], Q2),
                                 R=[Ast[nxt], pc], W=[R2])
                        yield V(lambda e, pc=pc, pn=pn: e.tensor_tensor(out=pn[:], in0=v4(R2[:]), in1=pc[:], op=ALU.add), R=[R2, pc], W=[pn])
                        cur = nxt
                    yield G(lambda e: e.tensor_tensor(out=B2[:], in0=rview, in1=kview, op=ALU.mult), R=[pf], W=[B2])
                    yield G(lambda e: e.tensor_tensor(out=B2[:], in0=B2[:], in1=bc(P_RK), op=ALU.mult), R=[pfm], W=[B2])
                    fns = []
                    for q in range(2):
                        for c in range(4):
                            for j in range(2):
                                sl = slice(64 * j, 64 * j + 64)
                                fns.append(lambda e, q=q, c=c, sl=sl: e.matmul(R0[sl, (q * 4 + c) * 2:(q * 4 + c) * 2 + 2], B2[sl, c, 64 * q:64 * q + 64], ones2[sl, :],
                                                                              start=True, stop=True))
                    yield PE(fns, R=[B2, ones2], W=[R0])
                    yield A(lambda e: e.activation(out=rk[:].rearrange("p q c -> p (q c)"), in_=R0[:, 0:16].rearrange("p (x two) -> p x two", two=2)[:, :, 0], func=AF.Copy), R=[R0], W=[rk])
                    return

                def seq(n):
                    pp_ = n % 2
                    arT, NB, NK, Vst, Vst32, BKst, WC, rk, B5 = arTs[pp_], NBs[pp_], NKs[pp_], Vsts[pp_], Vst32s[pp_], BKsts[pp_], WCs[pp_], rks[pp_], B5s[pp_]
                    TT = TTs[pp_]
                    Q2 = (0, 1)
                    for q in Q2:
                        yield PE(hl(lambda q, c, sl, j: [lambda e: e.matmul(Q0[sl, c * 64:(c + 1) * 64], arT[sl, c, q, 0, :], STm[sl, c, :], start=True, stop=False),
                                                         lambda e: e.matmul(Q0[sl, c * 64:(c + 1) * 64], NK[sl, q, c, 0, :], Vst[sl, q, c, :], start=False, stop=True)], (q,)),
                                 R=[arT, STm, NK, Vst], W=[Q0])
                        yield A(lambda e: e.activation(out=R1[:], in_=v3(Q0[:, 0:256]), func=AF.Copy), R=[Q0], W=[R1])
                        yield PE(hl(lambda q, c, sl, j: [lambda e: e.matmul(Q0[sl, 256 + c * 64:256 + (c + 1) * 64], TT[sl, q, c, :], R1[sl, c, :], start=True, stop=True)], (q,)),
                                 R=[TT, R1], W=[Q0])
                        yield A(lambda e: e.activation(out=Ust[:], in_=v3(Q0[:, 256:512]), func=AF.Copy), R=[Q0], W=[Ust])
                        if n >= 1:
                            yield PE(hl(lambda q, c, sl, j: [lambda e: e.matmul(Q0[sl, c * 64:(c + 1) * 64], arT[sl, c, q, 1, :], STm[sl, c, :], start=True, stop=False),
                                                             lambda e: e.matmul(Q0[sl, c * 64:(c + 1) * 64], NB[sl, q, c, 1, :], Ust[sl, c, :], start=False, stop=False),
                                                             lambda e: e.matmul(Q0[sl, c * 64:(c + 1) * 64], NK[sl, q, c, 1, :], Vst[sl, q, c, :], start=False, stop=True)], (q,)),
                                     R=[arT, STm, NB, Ust, NK, Vst], W=[Q0])
                            yield V(lambda e, q=q: e.tensor_copy(Yst[:, q], v3(Q0[:, 0:256])), R=[Q0], W=[Yst])
                        yield PE(hl(lambda q, c, sl, j: [lambda e: e.matmul(Q0[sl, 256 + c * 64:256 + (c + 1) * 64], BKst[sl, q, 0, c, :], Ust[sl, c, :], start=True, stop=False),
                                                         lambda e: e.matmul(Q0[sl, 256 + c * 64:256 + (c + 1) * 64], BKst[sl, q, 1, c, :], Vst[sl, q, c, :], start=False, stop=True)], (q,)),
                                 R=[BKst, Ust, Vst], W=[Q0])
                        yield G(lambda e, q=q: e.tensor_tensor(out=ST32[:], in0=ST32[:], in1=WC[:, :, q:q + 1].broadcast_to([128, 4, 64]), op=ALU.mult), R=[WC], W=[ST32])
                        yield V(lambda e: e.tensor_tensor(out=ST32[:], in0=ST32[:], in1=v3(Q0[:, 256:512]), op=ALU.add), R=[Q0], W=[ST32])
                        if MD != F32:
                            yield G(lambda e: e.tensor_copy(STm[:], ST32[:]), R=[ST32], W=[STm])
                    if n == 0:
                        return
                    Y8 = Yst[:].rearrange("p q c v -> p (q c) v")
                    yc8 = yc[:].rearrange("p q c v -> p (q c) v")
                    ysq8 = ysq[:].rearrange("p q c v -> p (q c) v")
                    V32_8 = Vst32[:].rearrange("p q c v -> p (q c) v")

                    def b8(ap):
                        return ap.unsqueeze(2).broadcast_to([128, 8, 64])
                    yield V(lambda e: e.tensor_reduce(out=gst[:, 0, :], in_=Y8, axis=AX.X, op=ALU.add), R=[Yst], W=[gst])
                    yield V(lambda e: e.tensor_scalar(out=gst[:, 1, :], in0=gst[:, 0, :], scalar1=-1.0 / 64, scalar2=None, op0=ALU.mult), R=[], W=[gst])
                    yield V(lambda e: e.tensor_tensor(out=yc8, in0=Y8, in1=b8(gst[:, 1, :]), op=ALU.add), R=[Yst], W=[yc, gst])
                    yield G(lambda e: e.tensor_tensor(out=ysq8, in0=yc8, in1=yc8, op=ALU.mult), R=[yc], W=[ysq])
                    yield V(lambda e: e.tensor_reduce(out=gst[:, 2, :], in_=ysq8, axis=AX.X, op=ALU.add), R=[ysq], W=[gst])
                    yield V(lambda e: e.tensor_scalar(out=gst[:, 3, :], in0=gst[:, 2, :], scalar1=1.0 / 64, scalar2=LN_EPS, op0=ALU.mult, op1=ALU.add), R=[], W=[gst])
                    yield G(lambda e: e.tensor_tensor(out=gst[:, 4, :], in0=gst[:, 3, :], in1=cst[:, CE + 4:CE + 12], op=ALU.pow), R=[cst], W=[gst])
                    yield V(lambda e: e.tensor_tensor(out=yc8, in0=yc8, in1=b8(gst[:, 4, :]), op=ALU.mult), R=[], W=[yc, gst])
                    yield G(lambda e: e.tensor_tensor(out=yc[:], in0=yc[:], in1=lnst[:, 0].unsqueeze(1).broadcast_to([128, 2, 4, 64]), op=ALU.mult), R=[lnst], W=[yc])
                    yield G(lambda e: e.tensor_tensor(out=yc[:], in0=yc[:], in1=lnst[:, 1].unsqueeze(1).broadcast_to([128, 2, 4, 64]), op=ALU.add), R=[lnst], W=[yc])
                    yield V(lambda e: e.tensor_tensor(out=ysq8, in0=V32_8, in1=b8(rk[:].rearrange("p q c -> p (q c)")), op=ALU.mult), R=[Vst32, rk], W=[ysq])
                    yield V(lambda e: e.tensor_tensor(out=yc8, in0=yc8, in1=ysq8, op=ALU.add), R=[ysq], W=[yc])
                    yield PE(hl(lambda q, c, sl, j: [lambda e: e.matmul(Q0[sl, q * 256 + c * 64:q * 256 + (c + 1) * 64], yc[sl, q, c, :], idsl(sl, j), start=True, stop=True)], Q2),
                             R=[yc, cst], W=[Q0])
                    yTn = yTr[n % 2]
                    yield V(lambda e: e.tensor_tensor(out=qc(yTn[:]), in0=v4(Q0[:]), in1=qc(B5[:]), op=ALU.mult), R=[Q0, B5], W=[yTn])
                    yield S.dma("sp", ch_st[n % 2], lambda e: e.dma_start(out=yscr[n - 1][:, 512:1024], in_=yTn[:].rearrange("p c t -> p (c t)")), R=[yTn], W=[yscr_t[n - 1]])

                import os as _os
                W_PRE, W_ATT, W_SEQ, W_HEAD = [int(v) for v in _os.environ.get("KW", "3,3,1,1").split(",")]
                run(head(0))
                if nt > 1:
                    run(par([pre(0), attention(0), head(1)], [W_PRE, W_ATT, W_HEAD]))
                else:
                    run(par([pre(0), attention(0)], [W_PRE, W_ATT]))
                _skip = _os.environ.get("KSKIP", "")
                for i in range(nt):
                    streams = [seq(i)] if "seq" not in _skip else []
                    wts = [W_SEQ] if "seq" not in _skip else []
                    if i + 1 < nt:
                        if "pre" not in _skip:
                            streams += [pre(i + 1)]
                            wts += [W_PRE]
                        if "att" not in _skip:
                            streams += [attention(i + 1)]
                            wts += [W_ATT]
                    if i + 2 < nt:
                        streams.append(head(i + 2))
                        wts.append(W_HEAD)
                    run(par(streams, wts))
                S.barrier()

        if "B" in phases:
            alloc_bw()
        if "A2" in phases:
            with ExitStack() as es:
                sb = mk_alloc(es, "a2_")
                CE = 128
                cst, pfm = load_consts(sb, 128)
                wba, wbr = pre_w["a2"]
                if not pre_w.get("a2_loaded"):
                    load_a2w()
                wgate = sb("wgate", [128, 8, 2048], BF16)
                for hh in range(2):
                    wload(wgate, wgate[:, 4 * hh:4 * hh + 4, :], w_gate.rearrange("(c p) n -> p c n", p=128)[:, 4 * hh:4 * hh + 4, :])
                S.finalize(ch_w, [cst, pfm, wgate, wba, wbr])
                if "B" in phases:
                    load_bw()
                    pre_w["b_loaded"] = True
                PS = [Tile(es.enter_context(nc.psum_tensor("psb%d" % i, [128, 512], F32)), "psb%d" % i, excl=True) for i in range(8)]
                xb = [sb("xb0", [128, D]), sb("xb1", [128, D])]
                yT = [sb("yT0", [128, 8, 128], BF16), sb("yT1", [128, 8, 128], BF16)]
                xs = sb("xs", [128, D])
                st4 = sb("st4", [128, 4])
                uTs = [sb("uT0", [128, 8, 128], BF16), sb("uT1", [128, 8, 128], BF16)]
                sg = sb("sg", [128, 16, 128])
                hbg = sb("hbg", [128, 16])
                V(lambda e: e.tensor_scalar(out=hbg[:], in0=pfm[:, P_BG:P_BG + 16], scalar1=0.5, scalar2=None, op0=ALU.mult), R=[pfm], W=[hbg])
                t1 = sb("t1", [128, 8, 128])
                t2 = sb("t2", [128, 8, 128])
                mT = [sb("mT0", [128, 8, 128], BF16), sb("mT1", [128, 8, 128], BF16)]

                def front2(n):
                    xt = xb[n % 2]
                    yTn = yT[n % 2]
                    yield S.dma("sp", ch_x[n % 2], lambda e: e.dma_start(out=xt[:], in_=xe[n * 128:(n + 1) * 128, :]), W=[xt])
                    yield S.dma("sp", ch_y[n % 2], lambda e: e.dma_start(out=yTn[:].rearrange("p c t -> p (c t)"), in_=yscr[n - 1]), R=[yscr_t[n - 1]], W=[yTn])
                    yield from norm_T(xt, xs, st4, cst, CE, pfm, P_GMIX, [PS[0], PS[1]], uTs[n % 2])

                def back2(n):
                    yTn = yT[n % 2]
                    mTn = mT[n % 2]
                    uT = uTs[n % 2]
                    for br, (wb, off) in enumerate(((wba, 0), (wbr, 4))):
                        for hh in range(2):
                            bank = PS[4 + 2 * br + hh]
                            fns = []
                            for i in range(4):
                                fc = 4 * hh + i
                                for kc in range(4):
                                    fns.append(lambda e, i=i, fc=fc, kc=kc, bank=bank, wb=wb, off=off: e.matmul(bank[:, i * 128:(i + 1) * 128], wb[:, kc, fc * 128:(fc + 1) * 128],
                                                                                                                yTn[:, off + kc, :], start=(kc == 0), stop=(kc == 3)))
                            yield PE(fns, R=[yTn, wb], W=[bank])
                    for g in range(4):
                        bank = PS[2 + (g % 2)]
                        fns = []
                        for i in range(4):
                            col = (4 * g + i) * 128
                            for kc in range(8):
                                fns.append(lambda e, i=i, col=col, kc=kc, bank=bank: e.matmul(bank[:, i * 128:(i + 1) * 128], wgate[:, kc, col:col + 128], uT[:, kc, :],
                                                                                             start=(kc == 0), stop=(kc == 7)))
                        yield PE(fns, R=[uT, wgate], W=[bank])
                        for i in range(4):
                            yield A(lambda e, g=g, i=i, bank=bank: e.activation(out=sg[:, 4 * g + i, :], in_=bank[:, i * 128:(i + 1) * 128], func=AF.Tanh,
                                                                                bias=hbg[:, 4 * g + i:4 * g + i + 1], scale=0.5), R=[bank, hbg], W=[sg])
                    for hh in range(2):
                        yield V(lambda e, hh=hh: e.scalar_tensor_tensor(out=t1[:, 4 * hh:4 * hh + 4, :], in0=sg[:, 4 * hh:4 * hh + 4, :], scalar=1.0,
                                                                        in1=PS[4 + hh][:].rearrange("p (c t) -> p c t", c=4), op0=ALU.add, op1=ALU.mult),
                                R=[PS[4 + hh], sg], W=[t1])
                        yield V(lambda e, hh=hh: e.scalar_tensor_tensor(out=t2[:, 4 * hh:4 * hh + 4, :], in0=sg[:, 8 + 4 * hh:8 + 4 * hh + 4, :], scalar=1.0,
                                                                        in1=PS[6 + hh][:].rearrange("p (c t) -> p c t", c=4), op0=ALU.add, op1=ALU.mult),
                                R=[PS[6 + hh], sg], W=[t2])
                    yield G(lambda e: e.tensor_tensor(out=t1[:], in0=t1[:], in1=t2[:], op=ALU.add), R=[t2], W=[t1])
                    yield A(lambda e: e.activation(out=mTn[:], in_=t1[:], func=AF.Copy, scale=0.5), R=[t1], W=[mTn])
                    yield S.dma("sp", ch_st[n % 2], lambda e: e.dma_start(out=mscr[n - 1], in_=mTn[:].rearrange("p c t -> p (c t)")), R=[mTn], W=[mscr_t[n - 1]])

                if nt > 1:
                    run(front2(1))
                for n in range(1, nt):
                    streams = [back2(n)]
                    wts = [2]
                    if n + 1 < nt:
                        streams.append(front2(n + 1))
                        wts.append(1)
                    run(par(streams, wts))
                S.barrier()

        if "B" in phases:
            with ExitStack() as es:
                sb = mk_alloc(es, "b_")
                CE = 128
                cst, pfm = load_consts(sb, 128)
                gfin = sb("gfin", [128, D])
                S.dma("sp", ch_w, lambda e: e.dma_start(out=gfin[:], in_=gfind.broadcast_to([128, D])), W=[gfin])
                wo = sb("wo", [128, 8, D], BF16)
                wg, wu = pre_w["b"]
                wd = sb("wd", [128, NFC, D], BF16)
                for hh in range(2):
                    wload(wo, wo[:, 4 * hh:4 * hh + 4, :], w_o.rearrange("(c p) n -> p c n", p=128)[:, 4 * hh:4 * hh + 4, :])
                if not pre_w.get("b_loaded"):
                    load_bw()
                for hh in range(2):
                    wload(wd, wd[:, 11 * hh:11 * hh + 11, :], w_fd.rearrange("(c p) n -> p c n", p=128)[:, 11 * hh:11 * hh + 11, :])
                S.finalize(ch_w, [cst, pfm, gfin, wo, wg, wu, wd])
                PS = [Tile(es.enter_context(nc.psum_tensor("psc%d" % i, [128, 512], F32)), "psc%d" % i, excl=True) for i in range(8)]
                xb = [sb("xb0", [128, D])] * 2
                mT = [sb("mT0", [128, 8, 128], BF16)] * 2
                h1s = [sb("h1a", [128, D]), sb("h1b", [128, D])]
                xsF = sb("xsF", [128, D])
                xsB = [sb("xsB0", [128, D]), sb("xsB1", [128, D])]
                st4 = sb("st4", [128, 4])
                st4b = sb("st4b", [128, 4])
                fTs = [sb("fT0", [128, 8, 128], BF16), sb("fT1", [128, 8, 128], BF16)]
                sl_ = sb("silu", [128, 4, 128])
                aT = sb("aT", [128, NFC, 128], BF16)

                def front3(n):
                    xt = xb[n % 2]
                    mTn = mT[n % 2]
                    h1 = h1s[n % 2]
                    yield S.dma("sp", ch_x[n % 2], lambda e: e.dma_start(out=xt[:], in_=xe[n * 128:(n + 1) * 128, :]), W=[xt])
                    yield S.dma("sp", ch_y[n % 2], lambda e: e.dma_start(out=mTn[:].rearrange("p c t -> p (c t)"), in_=mscr[n - 1]), R=[mscr_t[n - 1]], W=[mTn])
                    for hh in range(2):
                        yield PE([lambda e, kc=kc, hh=hh: e.matmul(PS[hh][:], mTn[:, kc, :], wo[:, kc, hh * 512:(hh + 1) * 512], start=(kc == 0), stop=(kc == 7)) for kc in range(8)],
                                 R=[mTn, wo], W=[PS[hh]])
                        yield V(lambda e, hh=hh: e.tensor_tensor(out=h1[:, hh * 512:(hh + 1) * 512], in0=PS[hh][:], in1=xt[:, hh * 512:(hh + 1) * 512], op=ALU.add),
                                R=[PS[hh], xt], W=[h1])
                    yield from norm_T(h1, xsF, st4, cst, CE, pfm, P_GFFN, [PS[2], PS[3]], fTs[n % 2])

                def back3(n):
                    h1 = h1s[n % 2]
                    fT = fTs[n % 2]
                    o = xsB[n % 2]
                    ngrp = (NFC + 3) // 4
                    for g in range(ngrp):
                        nchunk = min(4, NFC - 4 * g)
                        bg = PS[4 + 2 * (g % 2)]
                        bu = PS[5 + 2 * (g % 2)]
                        for bank, wt in ((bg, wg), (bu, wu)):
                            fns = []
                            for i in range(nchunk):
                                fc = 4 * g + i
                                for kc in range(8):
                                    fns.append(lambda e, i=i, fc=fc, kc=kc, bank=bank, wt=wt: e.matmul(bank[:, i * 128:(i + 1) * 128], wt[:, kc, fc * 128:(fc + 1) * 128], fT[:, kc, :],
                                                                                                       start=(kc == 0), stop=(kc == 7)))
                            yield PE(fns, R=[fT, wt], W=[bank])
                        yield A(lambda e: e.activation(out=sl_[:, 0:nchunk, :], in_=bg[:, 0:nchunk * 128].rearrange("p (c t) -> p c t", c=nchunk), func=AF.Tanh, scale=0.5),
                                R=[bg], W=[sl_])
                        yield V(lambda e: e.scalar_tensor_tensor(out=sl_[:, 0:nchunk, :], in0=sl_[:, 0:nchunk, :], scalar=1.0,
                                                                 in1=bg[:, 0:nchunk * 128].rearrange("p (c t) -> p c t", c=nchunk), op0=ALU.add, op1=ALU.mult), R=[bg], W=[sl_])
                        yield V(lambda e: e.scalar_tensor_tensor(out=aT[:, 4 * g:4 * g + nchunk, :], in0=sl_[:, 0:nchunk, :], scalar=0.5,
                                                                 in1=bu[:, 0:nchunk * 128].rearrange("p (c t) -> p c t", c=nchunk), op0=ALU.mult, op1=ALU.mult), R=[bu, sl_], W=[aT])
                    for hh in range(2):
                        yield PE([lambda e, fc=fc, hh=hh: e.matmul(PS[4 + hh][:], aT[:, fc, :], wd[:, fc, hh * 512:(hh + 1) * 512], start=(fc == 0), stop=(fc == NFC - 1)) for fc in range(NFC)],
                                 R=[aT, wd], W=[PS[4 + hh]])
                        yield V(lambda e, hh=hh: e.tensor_tensor(out=h1[:, hh * 512:(hh + 1) * 512], in0=PS[4 + hh][:], in1=h1[:, hh * 512:(hh + 1) * 512], op=ALU.add),
                                R=[PS[4 + hh]], W=[h1])
                    yield A(lambda e: e.activation(out=o[:], in_=h1[:], func=AF.Square, accum_out=st4b[:, 0:1]), R=[h1], W=[o, st4b])
                    yield V(lambda e: e.tensor_scalar(out=st4b[:, 1:2], in0=st4b[:, 0:1], scalar1=1.0 / D, scalar2=RMS_EPS, op0=ALU.mult, op1=ALU.add), R=[], W=[st4b])
                    yield G(lambda e: e.tensor_tensor(out=st4b[:, 2:3], in0=st4b[:, 1:2], in1=cst[:, CE + 4:CE + 5], op=ALU.pow), R=[cst], W=[st4b])
                    yield A(lambda e: e.activation(out=o[:], in_=h1[:], func=AF.Identity, scale=st4b[:, 2:3], bias=cst[:, CE + 2:CE + 3]), R=[h1, cst], W=[o, st4b])
                    yield G(lambda e: e.tensor_tensor(out=o[:], in0=o[:], in1=gfin[:], op=ALU.mult), R=[gfin], W=[o])
                    yield S.dma("sp", ch_st[n % 2], lambda e: e.dma_start(out=outd[(n - 1) * 128:n * 128, :], in_=o[:]), R=[o], W=[])

                if nt > 1:
                    run(front3(1))
                for n in range(1, nt):
                    streams = [back3(n)]
                    wts = [2]
                    if n + 1 < nt:
                        streams.append(front3(n + 1))
                        wts.append(1)
                    run(par(streams, wts))
                S.barrier()
        else:
            S.barrier()
        es_bw.close()
        es_a2w.close()
    return nc


QPERM = [0, 4, 1, 5, 2, 6, 3, 7]


def make_consts():
    c = np.zeros((128, C_END), np.float32)
    c[:, C_ID:C_ID + 128] = np.eye(128, dtype=np.float32)
    s = np.arange(64)
    for j in range(2):
        rows = slice(64 * j, 64 * j + 64)
        c[rows, C_MB:C_MB + 64] = (s[None, :] > s[:, None])
        c[rows, C_MB + 64:C_MB + 128] = (s[None, :] >= s[:, None])
        c[rows, C_ML:C_ML + 64] = (s[None, :] < s[:, None])
        c[rows, C_I64:C_I64 + 64] = np.eye(64)
        c[rows, C_OBD + 64 * j:C_OBD + 64 * j + 64] = 1.0
        c[rows, C_TRI:C_TRI + 64] = CFAC * (s[:, None] <= s[None, :])
        c[rows, C_TRI + 64:C_TRI + 128] = CFAC * (s[:, None] < s[None, :])
    c[64:128, C_TRI0:C_TRI0 + 128] = c[64:128, C_TRI:C_TRI + 128]
    c[64:64 + 48, C_TRI0:C_TRI0 + 128] = 0.0
    i = np.arange(128)
    own = np.where(i[None, :] <= i[:, None], 0.0, NEG)
    prev = np.where(i[None, :] > i[:, None], 0.0, NEG)
    full = np.full((128, 128), NEG)
    for var, (a, b) in enumerate(((own, prev), (prev, own), (full, own))):
        base = C_AM + 272 * var
        c[:, base:base + 128] = a
        c[:, base + 128:base + 256] = b
        c[:, base + 256:base + 272] = 0.0
    half = 8
    inv_freq = np.power(np.float32(500000.0), -np.arange(half, dtype=np.float32) * np.float32(2.0 / 16)).astype(np.float32)
    for n in range(NTILES):
        pos = (n * 128 + np.arange(128) - 112).astype(np.float32)
        ang = (pos[:, None] * inv_freq[None, :]).astype(np.float32)
        c[:, C_ROPE + 16 * n:C_ROPE + 16 * n + 8] = np.cos(ang)
        c[:, C_ROPE + 16 * n + 8:C_ROPE + 16 * n + 16] = np.sin(ang)
    return c


def prep_shared(inp):
    f = np.float32
    w_in = np.asarray(inp["w_in"][0], f)
    b_in = np.asarray(inp["b_in"][0], f)
    qcols = np.concatenate([np.arange(h * 64, (h + 1) * 64) for h in QPERM])
    w_qkv = np.ascontiguousarray(np.concatenate([w_in[:, qcols], w_in[:, 512:768]], axis=1))
    b_qkv = np.concatenate([b_in[qcols], b_in[512:768]])
    R0 = 768
    w_fm = np.zeros((D, 2048), f)
    b_fm = np.zeros((2048,), f)
    mix = np.asarray(inp["rwkv_mix"][0], f)
    mix_fm = np.zeros((2048,), f)

    def put(dst0, src0, n):
        w_fm[:, dst0:dst0 + n] = w_in[:, R0 + src0:R0 + src0 + n]
        b_fm[dst0:dst0 + n] = b_in[R0 + src0:R0 + src0 + n]
        mix_fm[dst0:dst0 + n] = mix[src0:src0 + n]
    put(0, 0, 1536)
    put(1536, 1536, 64)
    put(1664, 1600, 64)
    put(1792, 1664, 128)
    put(1920, 1792, 32)
    G0 = 768 + 1824
    w_gate = np.ascontiguousarray(w_in[:, G0:G0 + 2048])
    b_gate = b_in[G0:G0 + 2048]
    rows_perm = qcols
    sh = {
        "w_qkv": w_qkv, "w_fm": w_fm, "w_gate": w_gate,
        "w_ba": np.ascontiguousarray(np.asarray(inp["w_br_attn"][0], f)[rows_perm, :]),
        "w_br": np.ascontiguousarray(np.asarray(inp["w_br_rwkv"][0], f)),
        "w_o": np.ascontiguousarray(np.asarray(inp["w_o"][0], f)),
        "w_fg": np.ascontiguousarray(np.asarray(inp["w_ffn_gate"][0], f)),
        "w_fu": np.ascontiguousarray(np.asarray(inp["w_ffn_up"][0], f)),
        "w_fd": np.ascontiguousarray(np.asarray(inp["w_ffn_down"][0], f)),
        "w2": np.ascontiguousarray(np.asarray(inp["rwkv_w2"][0], f)),
        "a2": np.ascontiguousarray(np.asarray(inp["rwkv_a2"][0], f)),
    }
    g2p = np.zeros((256, 512), f)
    g2p[0:160] = np.asarray(inp["rwkv_g2"][0], f)
    sh["g2p"] = g2p
    pfm = np.zeros((128, P_END), f)

    def fm(vec, ncol):
        return np.asarray(vec, f).reshape(ncol, 128).T
    pfm[:, P_GMIX:P_GMIX + 8] = fm(inp["norm_mix_g"][0], 8)
    pfm[:, P_GFFN:P_GFFN + 8] = fm(inp["norm_ffn_g"][0], 8)
    pfm[:, P_BFM:P_BFM + 16] = fm(b_fm, 16)
    pfm[:, P_BG:P_BG + 16] = fm(b_gate, 16)
    pfm[:, P_MIX:P_MIX + 16] = fm(mix_fm, 16)
    pfm[:, P_A0:P_A0 + 4] = fm(inp["rwkv_a0"][0], 4)
    pfm[:, P_KK:P_KK + 4] = fm(inp["rwkv_k_k"][0], 4)
    pfm[:, P_KA:P_KA + 4] = fm(inp["rwkv_k_a"][0], 4)
    pfm[:, P_RK:P_RK + 4] = fm(np.asarray(inp["rwkv_r_k"][0], f).reshape(-1), 4)
    sh["pfm"] = pfm
    rowsA = np.zeros((1, RA_END), f)
    rowsA[0, RA_BQ:RA_BQ + 768] = b_qkv
    rowsA[0, RA_W0:RA_W0 + 512] = np.asarray(inp["rwkv_w0"][0], f)
    rowsA[0, RA_SK:RA_SK + 8] = np.asarray(inp["attn_sinks"][0], f)[QPERM]
    sh["rowsA"] = rowsA
    sh["gfin"] = np.asarray(inp["norm_final_g"], f).reshape(1, D).copy()
    lnst = np.zeros((128, 2, 4, 64), f)
    for a, key in enumerate(("rwkv_ln_w", "rwkv_ln_b")):
        v = np.asarray(inp[key][0], f).reshape(4, 2, 64)
        for j in range(2):
            lnst[64 * j:64 * j + 64, a, :, :] = v[None, :, j, :]
    sh["lnst"] = lnst.reshape(128, -1)
    sh["cst"] = make_consts()
    return sh


def prep_xe(inp, b):
    xe = np.zeros((NTILES * 128, D), np.float32)
    xe[112:128] = np.asarray(inp["meta_tokens"], np.float32)
    xe[128:] = np.asarray(inp["x"][b], np.float32)
    return xe


_NC_CACHE = {}


def kernel(**inputs):
    n = 8
    sh = prep_shared(inputs)
    in_maps = []
    for b in range(n):
        m = dict(sh)
        m["xe"] = prep_xe(inputs, b)
        in_maps.append(m)
    if "nc" not in _NC_CACHE:
        _NC_CACHE["nc"] = build_program()
    res = run_bass_kernel_spmd(_NC_CACHE["nc"], in_maps, core_ids=list(range(n)))
    out = np.stack([np.asarray(r["out"], np.float32).reshape(4096, D) for r in res.results], axis=0)
    return out
```

```python
import numpy as np
import ml_dtypes
from contextlib import ExitStack
import concourse.bass as bass
import concourse.mybir as mybir
from concourse.bass_utils import run_bass_kernel_spmd

F32 = mybir.dt.float32
BF16 = mybir.dt.bfloat16
AF = mybir.ActivationFunctionType
ALU = mybir.AluOpType
AX = mybir.AxisListType

NTILES = 33
D = 1024
DFF = 2816
NFC = 22
RMS_EPS = 1e-6
LN_EPS = 64e-5
CFAC = -float(np.exp(-0.5))
NEG = -1e30
MD = BF16

C_ID = 0
C_MB = 128
C_ML = 256
C_I64 = 320
C_OBD = 384
C_TRI = 512
C_TRI0 = 640
C_AM = 768
C_ROPE = 768 + 816
C_END = C_ROPE + 33 * 16
P_GMIX, P_GFFN, P_BFM, P_BG, P_MIX, P_A0, P_KK, P_KA, P_RK, P_END = 0, 8, 16, 32, 48, 64, 68, 72, 76, 80
RA_BQ, RA_W0, RA_SK, RA_END = 0, 768, 1280, 1288


class Tile:
    def __init__(self, t, name, excl=False):
        self.t = t
        self.name = name
        self.w = None
        self.r = {}
        self.excl = excl
        self.tw = 0.0
        self.tr = 0.0
        self.weng = None

    def __getitem__(self, i):
        return self.t[i]


class Chan:
    def __init__(self, sem, key):
        self.sem = sem
        self.key = key
        self.count = 0


class Sched:
    def __init__(self, nc, es):
        self.nc = nc
        self.es = es
        self.E = {}
        for name, eng in (("pe", nc.tensor), ("act", nc.scalar), ("dve", nc.vector),
                          ("pool", nc.gpsimd), ("sp", nc.sync)):
            sem = es.enter_context(nc.semaphore("sem_" + name))
            self.E[name] = dict(eng=eng, sem=sem, count=0, seen={}, name=name)
        self.chans = []

    def chan(self, name):
        c = Chan(self.es.enter_context(self.nc.semaphore("ch_" + name)), "ch_" + name)
        self.chans.append(c)
        return c

    def _waits(self, E, R, W):
        deps = {}

        def add(d):
            key, val, sem = d
            if key not in deps or deps[key][0] < val:
                deps[key] = (val, sem)
        for t in R:
            if t.w is not None:
                add(t.w)
            if t.excl:
                for key, (val, sem) in t.r.items():
                    if key != E["name"]:
                        add((key, val, sem))
        for t in W:
            if t.w is not None:
                add(t.w)
            for key, (val, sem) in t.r.items():
                add((key, val, sem))
        for key, (val, sem) in deps.items():
            if key == "pe" and E["name"] == "pe":
                continue
            if E["seen"].get(key, 0) < val:
                E["eng"].wait_ge(sem, val)
                E["seen"][key] = val

    def op(self, ename, fns, R=(), W=()):
        E = self.E[ename]
        self._waits(E, R, W)
        if not isinstance(fns, (list, tuple)):
            fns = [fns]
        inst = None
        for f in fns:
            inst = f(E["eng"])
        E["count"] += 1
        inst.then_inc(E["sem"], 1)
        for t in W:
            t.w = (ename, E["count"], E["sem"])
            t.r = {}
        for t in R:
            if t not in W:
                t.r[ename] = (E["count"], E["sem"])

    def dma(self, qname, chan, fn, R=(), W=()):
        E = self.E[qname]
        self._waits(E, R, W)
        inst = fn(E["eng"])
        chan.count += 16
        inst.then_inc(chan.sem, 16)
        for t in W:
            t.w = (chan.key, chan.count, chan.sem)
            t.r = {}
        for t in R:
            t.r[chan.key] = (chan.count, chan.sem)

    def finalize(self, chan, tiles):
        for t in tiles:
            t.w = (chan.key, chan.count, chan.sem)

    def barrier(self):
        for name, E in self.E.items():
            for oname, O in self.E.items():
                if oname == name or O["count"] == 0:
                    continue
                if E["seen"].get(oname, 0) < O["count"]:
                    E["eng"].wait_ge(O["sem"], O["count"])
                    E["seen"][oname] = O["count"]
            for c in self.chans:
                if c.count and E["seen"].get(c.key, 0) < c.count:
                    E["eng"].wait_ge(c.sem, c.count)
                    E["seen"][c.key] = c.count


def build_program(nt=NTILES, phases=("A1", "A2", "B"), dbg=None, dbg_n=-1, md=None, scr_ext=False, stop=None):
    global MD
    if md is not None:
        MD = md
    nc = bass.Bass("TRN2", target_bir_lowering=False)

    def din(name, shape, dt=F32):
        return nc.dram_tensor(name, list(shape), dt, kind="ExternalInput").ap()

    xe = din("xe", [NTILES * 128, D])
    w_qkv = din("w_qkv", [D, 768])
    w_fm = din("w_fm", [D, 2048])
    w_gate = din("w_gate", [D, 2048])
    w_ba = din("w_ba", [512, D])
    w_br = din("w_br", [512, D])
    w_o = din("w_o", [D, D])
    w_fg = din("w_fg", [D, DFF])
    w_fu = din("w_fu", [D, DFF])
    w_fd = din("w_fd", [DFF, D])
    w2d = din("w2", [64, 512])
    a2d = din("a2", [64, 512])
    g2d = din("g2p", [256, 512])
    pfmd = din("pfm", [128, P_END])
    rowsAd = din("rowsA", [1, RA_END])
    gfind = din("gfin", [1, D])
    lnstd = din("lnst", [128, 2 * 4 * 64])
    cstd = din("cst", [128, C_END])
    outd = nc.dram_tensor("out", [(NTILES - 1) * 128, D], F32, kind="ExternalOutput").ap()
    skind = "ExternalOutput" if scr_ext else "Internal"
    yscr = nc.dram_tensor("yscr", [NTILES - 1, 128, 8 * 128], BF16, kind=skind).ap()
    mscr = nc.dram_tensor("mscr", [NTILES - 1, 128, 8 * 128], BF16, kind=skind).ap()
    dbg_out = {}
    if dbg:
        for name, shape in dbg.items():
            dbg_out[name] = nc.dram_tensor("dbg_" + name, list(shape), F32, kind="ExternalOutput").ap()

    with ExitStack() as es0:
        S = Sched(nc, es0)
        ch_w = S.chan("w")
        ch_x = [S.chan("x0"), S.chan("x1")]
        ch_y = [S.chan("y0"), S.chan("y1")]
        ch_st = [S.chan("s0"), S.chan("s1")]
        ch_dbg = S.chan("dbg")
        yscr_t = [Tile(None, "yscr%d" % i) for i in range(NTILES - 1)]
        mscr_t = [Tile(None, "mscr%d" % i) for i in range(NTILES - 1)]

        ST = {"defer": False, "small": False}
        eng_free = {"pe": 0.0, "act": 0.0, "dve": 0.0, "pool": 0.0, "sp": 0.0}
        HOP = 0.3

        class Op:
            __slots__ = ("ename", "fns", "R", "W", "dur", "chan")

            def __init__(self, ename, fns, R, W, dur, chan=None):
                self.ename = ename; self.fns = fns; self.R = R; self.W = W; self.dur = dur; self.chan = chan

        def est_start(op):
            t = eng_free[op.ename]
            for x in op.R:
                tw = getattr(x, "tw", 0.0)
                if x.weng != op.ename:
                    tw += HOP
                t = max(t, tw)
                if x.excl:
                    t = max(t, getattr(x, "tr", 0.0) + HOP)
            for x in op.W:
                t = max(t, getattr(x, "tw", 0.0) + (HOP if x.weng != op.ename else 0.0), getattr(x, "tr", 0.0) + HOP)
            return t

        def emit(op):
            t0 = est_start(op)
            t1 = t0 + op.dur
            if op.chan is None:
                S.op(op.ename, op.fns, op.R, op.W)
                eng_free[op.ename] = t1
            else:
                S.dma(op.ename, op.chan, op.fns, op.R, op.W)
                eng_free[op.ename] = t0 + 0.1
                t1 = t0 + 2.5
            for x in op.W:
                x.tw = t1; x.weng = op.ename; x.tr = 0.0
            for x in op.R:
                x.tr = max(getattr(x, "tr", 0.0), t1)

        def mkop(ename, fns, R, W, dur, chan=None):
            op = Op(ename, fns, list(R), list(W), dur, chan)
            if ST["defer"]:
                return op
            emit(op)
            return None

        def V(fn, R=(), W=(), d=0.5):
            return mkop("dve", fn, R, W, d)

        def A(fn, R=(), W=(), d=0.5):
            return mkop("act", fn, R, W, d)

        def G(fn, R=(), W=(), d=1.2):
            return mkop("pool", fn, R, W, d)

        def PE(fns, R=(), W=(), d=None):
            n_ = len(fns) if isinstance(fns, (list, tuple)) else 1
            if d is None:
                d = n_ * (0.03 if ST["small"] else 0.1) + 0.1
            ST["small"] = False
            return mkop("pe", fns, R, W, d)

        def DMA(qname, chan, fn, R=(), W=()):
            return mkop(qname, fn, R, W, 2.5, chan)

        def dump(name, tile_ap, tiles):
            if name in dbg_out:
                S.dma("sp", ch_dbg, lambda e: e.dma_start(out=dbg_out[name], in_=tile_ap), R=tiles, W=[])

        def run(gens):
            if not isinstance(gens, (list, tuple)):
                gens = [gens]
            ST["defer"] = True
            heads = []
            for g in gens:
                heads.append(next(g, None))
            try:
                while True:
                    best = None
                    bt = None
                    for i, h in enumerate(heads):
                        if h is None:
                            continue
                        t = est_start(h)
                        if bt is None or t < bt:
                            bt = t; best = i
                    if best is None:
                        break
                    ST["defer"] = False
                    emit(heads[best])
                    ST["defer"] = True
                    h = next(gens[best], None)
                    while h is None:
                        try:
                            h = next(gens[best])
                        except StopIteration:
                            h = None
                            break
                    heads[best] = h
            finally:
                ST["defer"] = False

        def par(gens, weights=None):
            return list(gens)

        def mk_alloc(es, pfx):
            def sb(name, shape, dt=F32):
                return Tile(es.enter_context(nc.sbuf_tensor(pfx + name, list(shape), dt)), pfx + name)
            return sb

        def norm_T(x, xs, st4, cst, ce, pfm, gcol, TR2, uT):
            yield A(lambda e: e.activation(out=xs[:], in_=x[:], func=AF.Square, accum_out=st4[:, 0:1]), R=[x], W=[xs, st4])
            yield V(lambda e: e.tensor_scalar(out=st4[:, 1:2], in0=st4[:, 0:1], scalar1=1.0 / D, scalar2=RMS_EPS, op0=ALU.mult, op1=ALU.add), R=[st4], W=[st4])
            yield G(lambda e: e.tensor_tensor(out=st4[:, 2:3], in0=st4[:, 1:2], in1=cst[:, ce + 4:ce + 5], op=ALU.pow), R=[st4, cst], W=[st4])
            yield A(lambda e: e.activation(out=xs[:], in_=x[:], func=AF.Identity, scale=st4[:, 2:3], bias=cst[:, ce + 2:ce + 3]),
                    R=[x, st4, cst], W=[xs])
            for h in range(2):
                yield PE([lambda e, c=c: e.transpose(TR2[h][:, (c % 4) * 128:(c % 4 + 1) * 128], xs[:, c * 128:(c + 1) * 128], cst[:, C_ID:C_ID + 128])
                          for c in range(4 * h, 4 * h + 4)], R=[xs, cst], W=[TR2[h]])
                yield V(lambda e, h=h: e.tensor_tensor(out=uT[:, 4 * h:4 * h + 4, :], in0=TR2[h][:].rearrange("p (c k) -> p c k", k=128),
                                                       in1=pfm[:, gcol + 4 * h:gcol + 4 * h + 4].unsqueeze(2).broadcast_to([128, 4, 128]), op=ALU.mult),
                        R=[TR2[h], pfm], W=[uT])

        def load_consts(sb, ncols):
            cst = sb("cst", [128, ncols + 12])
            pfm = sb("pfm", [128, P_END])
            G(lambda e: e.memset(cst[:, ncols:ncols + 1], RMS_EPS), W=[cst])
            G(lambda e: e.memset(cst[:, ncols + 1:ncols + 2], LN_EPS), W=[cst])
            G(lambda e: e.memset(cst[:, ncols + 2:ncols + 4], 0.0), W=[cst])
            G(lambda e: e.memset(cst[:, ncols + 4:ncols + 12], -0.5), W=[cst])
            S.dma("sp", ch_w, lambda e: e.dma_start(out=cst[:, 0:ncols], in_=cstd[:, 0:ncols]), W=[cst])
            S.dma("sp", ch_w, lambda e: e.dma_start(out=pfm[:], in_=pfmd), W=[pfm])
            return cst, pfm

        def wload(tile_, out_ap, in_ap):
            S.dma("pool", ch_w, lambda e: e.dma_start(out=out_ap, in_=in_ap), W=[tile_])

        if "A1" in phases:
            with ExitStack() as es:
                sb = mk_alloc(es, "a1_")
                CE = C_END
                cst, pfm = load_consts(sb, C_END)
                rowsA = sb("rowsA", [128, RA_END])
                S.dma("sp", ch_w, lambda e: e.dma_start(out=rowsA[:], in_=rowsAd.broadcast_to([128, RA_END])), W=[rowsA])
                lnst = sb("lnst", [128, 2, 4, 64])
                S.dma("sp", ch_w, lambda e: e.dma_start(out=lnst[:].rearrange("p a c v -> p (a c v)"), in_=lnstd), W=[lnst])
                wqkv = sb("wqkv", [128, 8, 768], BF16)
                wfm = sb("wfm", [128, 8, 2048], BF16)
                w2 = sb("w2", [64, 512])
                a2 = sb("a2", [64, 512])
                g2 = sb("g2", [128, 2, 512])
                S.dma("sp", ch_w, lambda e: e.dma_start(out=w2[:], in_=w2d), W=[w2])
                S.dma("sp", ch_w, lambda e: e.dma_start(out=a2[:], in_=a2d), W=[a2])
                S.dma("sp", ch_w, lambda e: e.dma_start(out=g2[:], in_=g2d.rearrange("(c p) n -> p c n", p=128)), W=[g2])
                wload(wqkv, wqkv[:], w_qkv.rearrange("(c p) n -> p c n", p=128))
                for hh in range(2):
                    wload(wfm, wfm[:, 4 * hh:4 * hh + 4, :], w_fm.rearrange("(c p) n -> p c n", p=128)[:, 4 * hh:4 * hh + 4, :])
                identb = sb("identb", [128, 128], BF16)
                ones2 = sb("ones2", [128, 2])
                G(lambda e: e.memset(ones2[:], 1.0), W=[ones2])
                S.finalize(ch_w, [cst, pfm, rowsA, lnst, wqkv, wfm, w2, a2, g2])
                V(lambda e: e.tensor_copy(identb[:], cst[:, C_ID:C_ID + 128]), R=[cst], W=[identb])

                PS = [Tile(es.enter_context(nc.psum_tensor("ps%d" % i, [128, 512], F32)), "ps%d" % i, excl=True) for i in range(8)]
                H0, Q0, A0, A1_, A2_, R0, R1_, R2 = PS
                H1 = H0

                xb = [sb("xb0", [128, D]), sb("xb1", [128, D])]
                xs = sb("xs", [128, D])
                st4 = sb("st4", [128, 4])
                uT = sb("uT", [128, 8, 128], BF16)
                stg = sb("stg", [128, 16, 129])
                pfs = [sb("pf0", [128, 16, 128]), sb("pf1", [128, 16, 128])]
                qkvs = [sb("qkv0", [128, 768]), sb("qkv1", [128, 768])]
                rtmp = sb("rtmp", [128, 4, 10, 8])
                qT = sb("qT", [128, 4, 128], BF16)
                Kbuf = sb("Kbuf", [128, 272], BF16)
                Vbuf = sb("Vbuf", [128, 3, 128], BF16)
                Pb = [sb("Pb0", [128, 272], BF16), sb("Pb1", [128, 272], BF16)]
                PT = [sb("PT0", [128, 3, 128], BF16), sb("PT1", [128, 3, 128], BF16)]
                sm = sb("sm", [128, 5, 8])
                yat = sb("yat", [128, 8, 64])
                yTa = [sb("yTa0", [128, 4, 128], BF16), sb("yTa1", [128, 4, 128], BF16)]
                yTr = [sb("yTr0", [128, 4, 128], BF16), sb("yTr1", [128, 4, 128], BF16)]
                th = sb("th", [64, 128])
                sgd = sb("sgd", [128, 2, 128])
                B1 = sb("B1", [128, 4, 128]); B2 = sb("B2", [128, 4, 128]); B3 = sb("B3", [128, 4, 128])
                B4 = sb("B4", [128, 4, 128])
                arTs = [sb("arT%d" % i, [128, 4, 2, 2, 64], MD) for i in range(2)]
                BT = sb("BT", [128, 4, 128], MD); KT = sb("KT", [128, 4, 128], MD)
                BH = sb("BH", [128, 4, 128], MD); KH = sb("KH", [128, 4, 128], MD)
                cumC = sb("cumC", [128, 4, 2])
                WCs = [sb("WC%d" % i, [128, 4, 2]) for i in range(2)]
                rks = [sb("rk%d" % i, [128, 2, 4]) for i in range(2)]
                B5s = [sb("B5_%d" % i, [128, 4, 128]) for i in range(2)]
                TTs = [sb("TT%d" % i, [128, 2, 4, 64], MD) for i in range(2)]
                Ast = [sb("Ast0", [128, 2, 4, 64], MD), sb("Ast1", [128, 2, 4, 64], MD)]
                Nst = [sb("Nst0", [128, 2, 4, 64], MD), sb("Nst1", [128, 2, 4, 64], MD)]
                NBs = [sb("NB%d" % i, [128, 2, 4, 2, 64], MD) for i in range(2)]
                NKs = [sb("NK%d" % i, [128, 2, 4, 2, 64], MD) for i in range(2)]
                Pc = [sb("Pc0", [128, 2, 4, 64], MD), sb("Pc1", [128, 2, 4, 64], MD)]
                Vst32s = [sb("Vst32_%d" % i, [128, 2, 4, 64]) for i in range(2)]
                Vsts = [sb("Vst_%d" % i, [128, 2, 4, 64], MD) for i in range(2)] if MD != F32 else Vst32s
                BKsts = [sb("BKst%d" % i, [128, 2, 2, 4, 64], MD) for i in range(2)]
                R1 = sb("R1", [128, 4, 64], MD); Ust = sb("Ust", [128, 4, 64], MD)
                Yst = sb("Yst", [128, 2, 4, 64]); yc = sb("yc", [128, 2, 4, 64]); ysq = sb("ysq", [128, 2, 4, 64])
                ST32 = sb("ST32", [128, 4, 64])
                STm = sb("STm", [128, 4, 64], MD) if MD != F32 else ST32
                gst = sb("gst", [128, 6, 8])
                hb = sb("hb", [128, 4])
                V(lambda e: e.tensor_scalar(out=hb[:], in0=pfm[:, P_A0:P_A0 + 4], scalar1=0.5, scalar2=None, op0=ALU.mult), R=[pfm], W=[hb])
                G(lambda e: e.memset(ST32[:], 0.0), W=[ST32])
                if MD != F32:
                    G(lambda e: e.memset(STm[:], 0.0), W=[STm])
                G(lambda e: e.memset(stg[:], 0.0), W=[stg])
                G(lambda e: e.memset(Vbuf[:], 0.0), W=[Vbuf])
                G(lambda e: e.memset(Kbuf[:], 0.0), W=[Kbuf])

                def v3(ap2d, k=64):
                    return ap2d.rearrange("p (c k) -> p c k", k=k)

                def v4(ap2d):
                    return ap2d.rearrange("p (q c k) -> p q c k", q=2, c=4)

                def cq(t):
                    return t.rearrange("p c (q t) -> p c q t", q=2)

                def qc(t):
                    return t.rearrange("p c (q t) -> p q c t", q=2)

                ID0 = C_ID if MD == F32 else 0
                idm = cst if MD == F32 else identb

                def idsl(sl, j):
                    return cst[sl, C_ID + 64 * j:C_ID + 64 * j + 64]

                def idm_sl(sl, j):
                    return idm[sl, ID0 + 64 * j:ID0 + 64 * j + 64]

                def hl(fn, qs=(0,)):
                    ST["small"] = True
                    out = []
                    for q in qs:
                        for c in range(4):
                            for j in range(2):
                                out += fn(q, c, slice(64 * j, 64 * j + 64), j)
                    return out

                def head(n):
                    xt = xb[n % 2]
                    pf = pfs[n % 2]
                    qkv = qkvs[n % 2]
                    yield DMA("sp", ch_x[n % 2], lambda e: e.dma_start(out=xt[:], in_=xe[n * 128:(n + 1) * 128, :]), W=[xt])
                    yield from norm_T(xt, xs, st4, cst, CE, pfm, P_GMIX, [H0, H1], uT)
                    yield PE([lambda e, kc=kc: e.matmul(H0[:, 0:512], uT[:, kc, :], wqkv[:, kc, 0:512], start=(kc == 0), stop=(kc == 7)) for kc in range(8)],
                             R=[uT, wqkv], W=[H0])
                    yield V(lambda e: e.tensor_tensor(out=qkv[:, 0:512], in0=H0[:, 0:512], in1=rowsA[:, RA_BQ:RA_BQ + 512], op=ALU.add), R=[H0, rowsA], W=[qkv])
                    yield PE([lambda e, kc=kc: e.matmul(H1[:, 0:256], uT[:, kc, :], wqkv[:, kc, 512:768], start=(kc == 0), stop=(kc == 7)) for kc in range(8)],
                             R=[uT, wqkv], W=[H1])
                    yield V(lambda e: e.tensor_tensor(out=qkv[:, 512:768], in0=H1[:, 0:256], in1=rowsA[:, RA_BQ + 512:RA_BQ + 768], op=ALU.add), R=[H1, rowsA], W=[qkv])
                    for g in range(4):
                        bank = (H0, H1)[g % 2]
                        fns = []
                        for i in range(4):
                            col = (4 * g + i) * 128
                            for kc in range(8):
                                fns.append(lambda e, i=i, col=col, kc=kc, bank=bank: e.matmul(bank[:, i * 128:(i + 1) * 128], wfm[:, kc, col:col + 128], uT[:, kc, :],
                                                                                             start=(kc == 0), stop=(kc == 7)))
                        yield PE(fns, R=[uT, wfm], W=[bank])
                        yield V(lambda e, g=g, bank=bank: e.tensor_tensor(out=stg[:, 4 * g:4 * g + 4, 1:129], in0=v3(bank[:], 128),
                                                                          in1=pfm[:, P_BFM + 4 * g:P_BFM + 4 * g + 4].unsqueeze(2).broadcast_to([128, 4, 128]), op=ALU.add),
                                R=[bank, pfm], W=[stg])
                    if n == 0:
                        yield G(lambda e: e.memset(stg[:, :, 1:113], 0.0), W=[stg])
                    yield G(lambda e: e.tensor_tensor(out=pf[:], in0=stg[:, :, 0:128], in1=stg[:, :, 1:129], op=ALU.subtract), R=[stg], W=[pf], d=3.6)
                    yield G(lambda e: e.tensor_tensor(out=pf[:], in0=pf[:], in1=pfm[:, P_MIX:P_MIX + 16].unsqueeze(2).broadcast_to([128, 16, 128]), op=ALU.mult),
                            R=[pfm], W=[pf], d=3.6)
                    yield G(lambda e: e.tensor_tensor(out=pf[:], in0=pf[:], in1=stg[:, :, 1:129], op=ALU.add), R=[stg], W=[pf], d=3.6)
                    yield G(lambda e: e.tensor_copy(stg[:, :, 0:1], stg[:, :, 128:129]), R=[], W=[stg])

                def attention(n):
                    qkv = qkvs[n % 2]
                    slot = n % 2
                    q10 = qkv[:, 0:640].rearrange("p (h d) -> p h d", d=64)
                    cosb = cst[:, C_ROPE + 16 * n:C_ROPE + 16 * n + 8].unsqueeze(1).broadcast_to([128, 10, 8])
                    sinb = cst[:, C_ROPE + 16 * n + 8:C_ROPE + 16 * n + 16].unsqueeze(1).broadcast_to([128, 10, 8])
                    yield G(lambda e: e.tensor_tensor(out=rtmp[:, 0], in0=q10[:, :, 0:8], in1=cosb, op=ALU.mult), R=[qkv, cst], W=[rtmp])
                    yield G(lambda e: e.tensor_tensor(out=rtmp[:, 1], in0=q10[:, :, 8:16], in1=sinb, op=ALU.mult), R=[qkv, cst], W=[rtmp])
                    yield G(lambda e: e.tensor_tensor(out=rtmp[:, 2], in0=q10[:, :, 8:16], in1=cosb, op=ALU.mult), R=[qkv, cst], W=[rtmp])
                    yield G(lambda e: e.tensor_tensor(out=rtmp[:, 3], in0=q10[:, :, 0:8], in1=sinb, op=ALU.mult), R=[qkv, cst], W=[rtmp])
                    yield G(lambda e: e.tensor_tensor(out=q10[:, :, 0:8], in0=rtmp[:, 0], in1=rtmp[:, 1], op=ALU.subtract), R=[rtmp], W=[qkv])
                    yield G(lambda e: e.tensor_tensor(out=q10[:, :, 8:16], in0=rtmp[:, 2], in1=rtmp[:, 3], op=ALU.add), R=[rtmp], W=[qkv])
                    yield PE([lambda e, c=c: e.transpose(A0[:, c * 128:(c + 1) * 128], qkv[:, c * 128:(c + 1) * 128], cst[:, C_ID:C_ID + 128]) for c in range(4)],
                             R=[qkv, cst], W=[A0])
                    yield PE(lambda e: e.transpose(A1_[:, 0:128], qkv[:, 512:640], cst[:, C_ID:C_ID + 128]), R=[qkv, cst], W=[A1_])
                    yield A(lambda e: e.activation(out=qT[:], in_=v3(A0[:], 128), func=AF.Copy, scale=0.125), R=[A0], W=[qT])
                    yield A(lambda e: e.activation(out=Kbuf[:, slot * 128:(slot + 1) * 128], in_=A1_[:, 0:128], func=AF.Copy), R=[A1_], W=[Kbuf])
                    yield V(lambda e: e.tensor_copy(Vbuf[:, slot, :], qkv[:, 640:768]), R=[qkv], W=[Vbuf])
                    if n == 0:
                        yield A(lambda e: e.activation(out=Kbuf[:, 256:272], in_=A1_[:, 112:128], func=AF.Copy), R=[A1_], W=[Kbuf])
                        yield PE(lambda e: e.matmul(A1_[0:16, 128:256], cst[:, C_ID + 112:C_ID + 128], qkv[:, 640:768], start=True, stop=True), R=[qkv, cst], W=[A1_])
                        yield V(lambda e: e.tensor_copy(Vbuf[0:16, 2, :], A1_[0:16, 128:256]), R=[A1_], W=[Vbuf])
                        return
                    yTn = yTa[n % 2]
                    mvar = 2 if n == 1 else (0 if n % 2 == 0 else 1)
                    mask = cst[:, C_AM + 272 * mvar:C_AM + 272 * (mvar + 1)]
                    for s in range(8):
                        c, j = s // 2, s % 2
                        sl = slice(64 * j, 64 * j + 64)
                        SC = A0
                        Pk = Pb[s % 2]
                        PTk = PT[s % 2]
                        yield PE(lambda e: e.matmul(SC[:, 0:272], qT[sl, c, :], Kbuf[sl, 0:272], start=True, stop=True), R=[qT, Kbuf], W=[SC])
                        yield V(lambda e: e.tensor_tensor(out=SC[:, 0:272], in0=SC[:, 0:272], in1=mask, op=ALU.add), R=[cst], W=[SC])
                        yield V(lambda e: e.tensor_reduce(out=sm[:, 0, s:s + 1], in_=SC[:, 0:272], axis=AX.X, op=ALU.max), R=[SC], W=[sm])
                        yield V(lambda e: e.tensor_scalar(out=sm[:, 1, s:s + 1], in0=sm[:, 0, s:s + 1], scalar1=rowsA[:, RA_SK + s:RA_SK + s + 1], scalar2=-1.0,
                                                          op0=ALU.max, op1=ALU.mult), R=[rowsA], W=[sm])
                        yield A(lambda e: e.activation(out=Pk[:], in_=SC[:, 0:272], func=AF.Exp, bias=sm[:, 1, s:s + 1], scale=1.0,
                                                       accum_out=sm[:, 2, s:s + 1]), R=[SC], W=[Pk, sm])
                        yield A(lambda e: e.activation(out=sm[:, 3, s:s + 1], in_=rowsA[:, RA_SK + s:RA_SK + s + 1], func=AF.Exp, bias=sm[:, 1, s:s + 1], scale=1.0),
                                R=[rowsA], W=[sm])
                        yield PE([lambda e, b=b, nk=nk: e.matmul(A1_[0:nk, b * 128:(b + 1) * 128], Pk[:, b * 128:b * 128 + nk], identb[:], start=True, stop=True)
                                  for b, nk in ((0, 128), (1, 128), (2, 16))], R=[Pk, identb], W=[A1_])
                        yield A(lambda e: e.activation(out=PTk[:, 0:2, :], in_=v3(A1_[:, 0:256], 128), func=AF.Copy), R=[A1_], W=[PTk])
                        yield A(lambda e: e.activation(out=PTk[0:16, 2, :], in_=A1_[0:16, 256:384], func=AF.Copy), R=[A1_], W=[PTk])
                        yield PE([lambda e: e.matmul(A2_[:, s * 64:(s + 1) * 64], PTk[:, 0, :], Vbuf[:, 0, sl], start=True, stop=False),
                                  lambda e: e.matmul(A2_[:, s * 64:(s + 1) * 64], PTk[:, 1, :], Vbuf[:, 1, sl], start=False, stop=False),
                                  lambda e: e.matmul(A2_[:, s * 64:(s + 1) * 64], PTk[0:16, 2, :], Vbuf[0:16, 2, sl], start=False, stop=True)],
                                 R=[PTk, Vbuf], W=[A2_])
                    yield V(lambda e: e.tensor_tensor(out=sm[:, 2, :], in0=sm[:, 2, :], in1=sm[:, 3, :], op=ALU.add), R=[], W=[sm])
                    yield V(lambda e: e.reciprocal(out=sm[:, 4, :], in_=sm[:, 2, :]), R=[], W=[sm])
                    yield V(lambda e: e.tensor_tensor(out=yat[:], in0=v3(A2_[:], 64), in1=sm[:, 4, :].unsqueeze(2).broadcast_to([128, 8, 64]), op=ALU.mult),
                            R=[A2_], W=[yat, sm])
                    yield PE([lambda e, c=c: e.transpose(A1_[:, c * 128:(c + 1) * 128], yat[:, 2 * c:2 * c + 2, :].rearrange("p a d -> p (a d)"), cst[:, C_ID:C_ID + 128])
                              for c in range(4)], R=[yat, cst], W=[A1_])
                    yield A(lambda e: e.activation(out=yTn[:], in_=v3(A1_[:], 128), func=AF.Copy), R=[A1_], W=[yTn])
                    if n == dbg_n:
                        dump("yat", yat[:].rearrange("p s d -> p (s d)"), [yat])
                    yield DMA("sp", ch_y[n % 2], lambda e: e.dma_start(out=yscr[n - 1][:, 0:512], in_=yTn[:].rearrange("p c t -> p (c t)")), R=[yTn], W=[yscr_t[n - 1]])

                def pre(n):
                    pf = pfs[n % 2]
                    pp_ = n % 2
                    arT, NB, NK, Vst, Vst32, BKst, WC, rk, B5 = arTs[pp_], NBs[pp_], NKs[pp_], Vsts[pp_], Vst32s[pp_], BKsts[pp_], WCs[pp_], rks[pp_], B5s[pp_]
                    tri0 = C_TRI0 if n == 0 else C_TRI
                    tri = cst[:, tri0:tri0 + 128]
                    yield A(lambda e: e.activation(out=th[:], in_=pf[0:64, 12, :], func=AF.Tanh), R=[pf], W=[th])
                    yield PE(lambda e: e.matmul(R0[:, 0:512], th[:], w2[:], start=True, stop=True), R=[th, w2], W=[R0])
                    B4f = B4[:].rearrange("p c t -> p (c t)")
                    yield V(lambda e: e.tensor_tensor(out=B4f, in0=R0[:, 0:512], in1=rowsA[:, RA_W0:RA_W0 + 512], op=ALU.add), R=[R0, rowsA], W=[B4])
                    yield A(lambda e: e.activation(out=B4f, in_=B4f, func=AF.Tanh, scale=0.5), R=[], W=[B4])
                    yield V(lambda e: e.tensor_scalar(out=B4f, in0=B4f, scalar1=0.5, scalar2=0.5, op0=ALU.mult, op1=ALU.add), R=[], W=[B4])
                    CB = (R1_, R2)
                    for q in range(2):
                        yield PE([lambda e, c=c, q=q: e.matmul(CB[q][:, c * 128:(c + 1) * 128], B4f[64 * q:64 * q + 64, c * 128:(c + 1) * 128],
                                                               tri[64 * q:64 * q + 64, :], start=True, stop=True) for c in range(4)], R=[B4, cst], W=[CB[q]])

                    def cums(a):
                        return [CB[q][:].rearrange("p (c a t) -> p c a t", c=4, a=2)[:, :, a, :] for q in range(2)]
                    yield PE([lambda e, c=c: e.matmul(R0[:, c * 128:(c + 1) * 128], a2[:, c * 128:(c + 1) * 128], pf[0:64, 13, :], start=True, stop=True) for c in range(4)],
                             R=[pf, a2], W=[R0])
                    for c in range(4):
                        yield A(lambda e, c=c: e.activation(out=B3[:, c, :], in_=R0[:, c * 128:(c + 1) * 128], func=AF.Tanh, bias=hb[:, c:c + 1], scale=0.5),
                                R=[R0, hb], W=[B3])
                    yield V(lambda e: e.tensor_scalar(out=B3[:], in0=B3[:], scalar1=0.5, scalar2=0.5, op0=ALU.mult, op1=ALU.add), R=[], W=[B3])
                    yield A(lambda e: e.activation(out=sgd[:], in_=pf[:, 14:16, :], func=AF.Tanh, scale=0.5), R=[pf], W=[sgd])
                    yield V(lambda e: e.tensor_scalar(out=sgd[:], in0=sgd[:], scalar1=0.5, scalar2=0.5, op0=ALU.mult, op1=ALU.add), R=[], W=[sgd])
                    fns = []
                    for c in range(4):
                        fns.append(lambda e, c=c: e.matmul(R0[:, c * 128:(c + 1) * 128], g2[:, 0, c * 128:(c + 1) * 128], sgd[:, 0, :], start=True, stop=False))
                        fns.append(lambda e, c=c: e.matmul(R0[:, c * 128:(c + 1) * 128], g2[0:32, 1, c * 128:(c + 1) * 128], sgd[0:32, 1, :], start=False, stop=True))
                    yield PE(fns, R=[sgd, g2], W=[R0])
                    yield A(lambda e: e.activation(out=B5[:].rearrange("p c t -> p (c t)"), in_=R0[:], func=AF.Copy), R=[R0], W=[B5])
                    kview = pf[:, 4:8, :]
                    rview = pf[:, 0:4, :]

                    def bc(col):
                        return pfm[:, col:col + 4].unsqueeze(2).broadcast_to([128, 4, 128])
                    yield V(lambda e: e.tensor_tensor(out=B1[:], in0=kview, in1=bc(P_KK), op=ALU.mult), R=[pf, pfm], W=[B1])
                    yield V(lambda e: e.tensor_tensor(out=B2[:], in0=B1[:], in1=B1[:], op=ALU.mult), R=[B1], W=[B2])
                    yield PE([lambda e, c=c: e.matmul(R0[:, c * 128:(c + 1) * 128], cst[:, C_OBD:C_OBD + 128], B2[:, c, :], start=True, stop=True) for c in range(4)],
                             R=[B2, cst], W=[R0])
                    yield A(lambda e: e.activation(out=B2[:].rearrange("p c t -> p (c t)"), in_=R0[:], func=AF.Sqrt), R=[R0], W=[B2])
                    yield V(lambda e: e.tensor_scalar(out=B2[:], in0=B2[:], scalar1=1e-12, scalar2=None, op0=ALU.max), R=[], W=[B2])
                    yield V(lambda e: e.reciprocal(out=B2[:], in_=B2[:]), R=[], W=[B2])
                    yield V(lambda e: e.tensor_tensor(out=B1[:], in0=B1[:], in1=B2[:], op=ALU.mult), R=[B2], W=[B1])
                    yield V(lambda e: e.scalar_tensor_tensor(out=B2[:], in0=B3[:], scalar=-1.0, in1=bc(P_KA), op0=ALU.add, op1=ALU.mult), R=[B3, pfm], W=[B2])
                    yield V(lambda e: e.scalar_tensor_tensor(out=kview, in0=B2[:], scalar=1.0, in1=kview, op0=ALU.add, op1=ALU.mult), R=[B2], W=[pf])
                    yield V(lambda e: e.tensor_tensor(out=B3[:], in0=B1[:], in1=B3[:], op=ALU.mult), R=[B1], W=[B3])
                    cex = cums(1)
                    cin = cums(0)
                    B4q = cq(B4[:])
                    for hh in range(2):
                        yield A(lambda e, hh=hh: e.activation(out=B4[:, :, 64 * hh:64 * hh + 64], in_=cex[hh], func=AF.Exp), R=[CB[hh]], W=[B4])
                    yield V(lambda e: e.scalar_tensor_tensor(out=arT[:, :, :, 0, :], in0=cq(B1[:]), scalar=-1.0, in1=B4q, op0=ALU.mult, op1=ALU.mult), R=[B1, B4], W=[arT])
                    for hh in range(2):
                        yield A(lambda e, hh=hh: e.activation(out=B4[:, :, 64 * hh:64 * hh + 64], in_=cin[hh], func=AF.Exp), R=[CB[hh]], W=[B4])
                    yield V(lambda e: e.tensor_tensor(out=arT[:, :, :, 1, :], in0=cq(rview), in1=B4q, op=ALU.mult), R=[pf, B4], W=[arT])
                    for hh in range(2):
                        yield A(lambda e, hh=hh: e.activation(out=B4[:, :, 64 * hh:64 * hh + 64], in_=cin[hh], func=AF.Exp, scale=-1.0), R=[CB[hh]], W=[B4])
                    yield V(lambda e: e.tensor_tensor(out=BT[:], in0=B3[:], in1=B4[:], op=ALU.mult), R=[B3, B4], W=[BT])
                    yield V(lambda e: e.tensor_tensor(out=KT[:], in0=kview, in1=B4[:], op=ALU.mult), R=[pf, B4], W=[KT])
                    for hh in range(2):
                        yield V(lambda e, hh=hh: e.tensor_copy(cumC[:, :, hh], cin[hh][:, :, 63]), R=[CB[hh]], W=[cumC])
                    for c in range(4):
                        for q in range(2):
                            yield A(lambda e, c=c, q=q: e.activation(out=B4[:, c, 64 * q:64 * q + 64], in_=cin[q][:, c, :], func=AF.Exp, scale=-1.0,
                                                                     bias=cumC[:, c, q:q + 1]), R=[CB[q], cumC], W=[B4])
                    yield V(lambda e: e.tensor_tensor(out=BH[:], in0=B3[:], in1=B4[:], op=ALU.mult), R=[B3, B4], W=[BH])
                    yield V(lambda e: e.tensor_tensor(out=KH[:], in0=kview, in1=B4[:], op=ALU.mult), R=[pf, B4], W=[KH])
                    yield A(lambda e: e.activation(out=WC[:], in_=cumC[:], func=AF.Exp), R=[cumC], W=[WC])

                    mb = cst[:, C_MB:C_MB + 128].rearrange("p (a t) -> p a t", a=2).unsqueeze(1).broadcast_to([128, 4, 2, 64])
                    ml8 = cst[:, C_ML:C_ML + 64].unsqueeze(1).broadcast_to([128, 8, 64])
                    i8 = cst[:, C_I64:C_I64 + 64].unsqueeze(1).broadcast_to([128, 8, 64])
                    Q2 = (0, 1)

                    def tq(q):
                        return slice(64 * q, 64 * q + 64)
                    yield PE(hl(lambda q, c, sl, j: [lambda e: e.matmul(R0[sl, q * 256 + c * 64:q * 256 + (c + 1) * 64], arT[sl, c, q, 0, :], BT[sl, c, tq(q)], start=True, stop=True)], Q2),
                             R=[arT, BT], W=[R0])
                    for q in Q2:
                        yield PE(hl(lambda q, c, sl, j: [lambda e: e.matmul(CB[q][sl, c * 128:(c + 1) * 128], BT[sl, c, tq(q)], arT[sl, c, q, :, :].rearrange("p a t -> p (a t)"),
                                                                            start=True, stop=True)], (q,)), R=[arT, BT], W=[CB[q]])
                    yield V(lambda e: e.tensor_tensor(out=Ast[0][:].rearrange("p q c t -> p (q c) t"), in0=v3(R0[:]), in1=ml8, op=ALU.mult), R=[R0, cst], W=[Ast[0]])
                    for q in Q2:
                        yield V(lambda e, q=q: e.tensor_tensor(out=NB[:, q], in0=CB[q][:].rearrange("p (c a t) -> p c a t", c=4, a=2), in1=mb, op=ALU.mult), R=[CB[q], cst], W=[NB])
                    for q in Q2:
                        yield PE(hl(lambda q, c, sl, j: [lambda e: e.matmul(CB[q][sl, c * 128:(c + 1) * 128], KT[sl, c, tq(q)], arT[sl, c, q, :, :].rearrange("p a t -> p (a t)"),
                                                                            start=True, stop=True)], (q,)), R=[arT, KT], W=[CB[q]])
                    for q in Q2:
                        yield V(lambda e, q=q: e.tensor_tensor(out=NK[:, q], in0=CB[q][:].rearrange("p (c a t) -> p c a t", c=4, a=2), in1=mb, op=ALU.mult), R=[CB[q], cst], W=[NK])
                    yield G(lambda e: e.tensor_copy(Nst[0][:], NB[:, :, :, 0, :]), R=[NB], W=[Nst[0]])
                    yield G(lambda e: e.tensor_tensor(out=Pc[0][:].rearrange("p q c t -> p (q c) t"), in0=Nst[0][:].rearrange("p q c t -> p (q c) t"), in1=i8, op=ALU.add),
                            R=[Nst[0], cst], W=[Pc[0]])
                    yield PE(hl(lambda q, c, sl, j: [lambda e: e.matmul(R0[sl, q * 256 + c * 64:q * 256 + (c + 1) * 64], pf[sl, 8 + c, tq(q)], idsl(sl, j), start=True, stop=True)], Q2),
                             R=[pf, cst], W=[R0])
                    yield A(lambda e: e.activation(out=Vst32[:], in_=v4(R0[:]), func=AF.Copy), R=[R0], W=[Vst32])
                    if MD != F32:
                        yield V(lambda e: e.tensor_copy(Vst[:], v4(R0[:])), R=[R0], W=[Vst])
                    for q in Q2:
                        yield PE(hl(lambda q, c, sl, j: [lambda e: e.matmul(CB[q][sl, c * 64:(c + 1) * 64], BH[sl, c, tq(q)], idm_sl(sl, j), start=True, stop=True),
                                                         lambda e: e.matmul(CB[q][sl, 256 + c * 64:256 + (c + 1) * 64], KH[sl, c, tq(q)], idm_sl(sl, j), start=True, stop=True)], (q,)),
                                 R=[BH, KH, idm], W=[CB[q]])
                    for q in Q2:
                        yield A(lambda e, q=q: e.activation(out=BKst[:, q], in_=CB[q][:].rearrange("p (a c t) -> p a c t", a=2, c=4), func=AF.Copy), R=[CB[q]], W=[BKst])
                    cur = 0
                    for lvl in range(1, 6):
                        nxt = 1 - cur
                        yield PE(hl(lambda q, c, sl, j: [lambda e: e.matmul(R0[sl, q * 256 + c * 64:q * 256 + (c + 1) * 64], Nst[cur][sl, q, c, :], Ast[cur][sl, q, c, :], start=True, stop=True)], Q2),
                                 R=[Nst[cur], Ast[cur]], W=[R0])
                        if lvl < 5:
                            yield PE(hl(lambda q, c, sl, j: [lambda e: e.matmul(R1_[sl, q * 256 + c * 64:q * 256 + (c + 1) * 64], Ast[cur][sl, q, c, :], Nst[cur][sl, q, c, :], start=True, stop=True)], Q2),
                                     R=[Nst[cur], Ast[cur]], W=[R1_])
                        yield A(lambda e, nxt=nxt: e.activation(out=Ast[nxt][:], in_=v4(R0[:]), func=AF.Copy), R=[R0], W=[Ast[nxt]])
                        if lvl < 5:
                            yield V(lambda e, nxt=nxt: e.tensor_copy(Nst[nxt][:], v4(R1_[:])), R=[R1_], W=[Nst[nxt]])
                        pc, pn = Pc[(lvl - 1) % 2], (Pc[lvl % 2] if lvl < 5 else TTs[pp_])
                        yield PE(hl(lambda q, c, sl, j: [lambda e: e.matmul(R2[sl, q * 256 + c * 64:q * 256 + (c + 1) * 64], Ast[nxt][sl, q, c, :], pc[sl, q, c, :], start=True, stop=True)], Q2),
                                 R=[Ast[nxt], pc], W=[R2])
                        yield V(lambda e, pc=pc, pn=pn: e.tensor_tensor(out=pn[:], in0=v4(R2[:]), in1=pc[:], op=ALU.add), R=[R2, pc], W=[pn])
                        cur = nxt
                    yield G(lambda e: e.tensor_tensor(out=B2[:], in0=rview, in1=kview, op=ALU.mult), R=[pf], W=[B2])
                    yield G(lambda e: e.tensor_tensor(out=B2[:], in0=B2[:], in1=bc(P_RK), op=ALU.mult), R=[pfm], W=[B2])
                    fns = []
                    for q in range(2):
                        for c in range(4):
                            for j in range(2):
                                sl = slice(64 * j, 64 * j + 64)
                                fns.append(lambda e, q=q, c=c, sl=sl: e.matmul(R0[sl, (q * 4 + c) * 2:(q * 4 + c) * 2 + 2], B2[sl, c, 64 * q:64 * q + 64], ones2[sl, :],
                                                                              start=True, stop=True))
                    yield PE(fns, R=[B2, ones2], W=[R0])
                    yield A(lambda e: e.activation(out=rk[:].rearrange("p q c -> p (q c)"), in_=R0[:, 0:16].rearrange("p (x two) -> p x two", two=2)[:, :, 0], func=AF.Copy), R=[R0], W=[rk])
                    return

                def seq(n):
                    pp_ = n % 2
                    arT, NB, NK, Vst, Vst32, BKst, WC, rk, B5 = arTs[pp_], NBs[pp_], NKs[pp_], Vsts[pp_], Vst32s[pp_], BKsts[pp_], WCs[pp_], rks[pp_], B5s[pp_]
                    TT = TTs[pp_]
                    Q2 = (0, 1)
                    for q in Q2:
                        yield PE(hl(lambda q, c, sl, j: [lambda e: e.matmul(Q0[sl, c * 64:(c + 1) * 64], arT[sl, c, q, 0, :], STm[sl, c, :], start=True, stop=False),
                                                         lambda e: e.matmul(Q0[sl, c * 64:(c + 1) * 64], NK[sl, q, c, 0, :], Vst[sl, q, c, :], start=False, stop=True)], (q,)),
                                 R=[arT, STm, NK, Vst], W=[Q0])
                        yield A(lambda e: e.activation(out=R1[:], in_=v3(Q0[:, 0:256]), func=AF.Copy), R=[Q0], W=[R1])
                        yield PE(hl(lambda q, c, sl, j: [lambda e: e.matmul(Q0[sl, 256 + c * 64:256 + (c + 1) * 64], TT[sl, q, c, :], R1[sl, c, :], start=True, stop=True)], (q,)),
                                 R=[TT, R1], W=[Q0])
                        yield A(lambda e: e.activation(out=Ust[:], in_=v3(Q0[:, 256:512]), func=AF.Copy), R=[Q0], W=[Ust])
                        if n >= 1:
                            yield PE(hl(lambda q, c, sl, j: [lambda e: e.matmul(Q0[sl, c * 64:(c + 1) * 64], arT[sl, c, q, 1, :], STm[sl, c, :], start=True, stop=False),
                                                             lambda e: e.matmul(Q0[sl, c * 64:(c + 1) * 64], NB[sl, q, c, 1, :], Ust[sl, c, :], start=False, stop=False),
                                                             lambda e: e.matmul(Q0[sl, c * 64:(c + 1) * 64], NK[sl, q, c, 1, :], Vst[sl, q, c, :], start=False, stop=True)], (q,)),
                                     R=[arT, STm, NB, Ust, NK, Vst], W=[Q0])
                            yield V(lambda e, q=q: e.tensor_copy(Yst[:, q], v3(Q0[:, 0:256])), R=[Q0], W=[Yst])
                        yield PE(hl(lambda q, c, sl, j: [lambda e: e.matmul(Q0[sl, 256 + c * 64:256 + (c + 1) * 64], BKst[sl, q, 0, c, :], Ust[sl, c, :], start=True, stop=False),
                                                         lambda e: e.matmul(Q0[sl, 256 + c * 64:256 + (c + 1) * 64], BKst[sl, q, 1, c, :], Vst[sl, q, c, :], start=False, stop=True)], (q,)),
                                 R=[BKst, Ust, Vst], W=[Q0])
                        yield G(lambda e, q=q: e.tensor_tensor(out=ST32[:], in0=ST32[:], in1=WC[:, :, q:q + 1].broadcast_to([128, 4, 64]), op=ALU.mult), R=[WC], W=[ST32])
                        yield V(lambda e: e.tensor_tensor(out=ST32[:], in0=ST32[:], in1=v3(Q0[:, 256:512]), op=ALU.add), R=[Q0], W=[ST32])
                        if MD != F32:
                            yield G(lambda e: e.tensor_copy(STm[:], ST32[:]), R=[ST32], W=[STm])
                    if n == 0:
                        return
                    Y8 = Yst[:].rearrange("p q c v -> p (q c) v")
                    yc8 = yc[:].rearrange("p q c v -> p (q c) v")
                    ysq8 = ysq[:].rearrange("p q c v -> p (q c) v")
                    V32_8 = Vst32[:].rearrange("p q c v -> p (q c) v")

                    def b8(ap):
                        return ap.unsqueeze(2).broadcast_to([128, 8, 64])
                    yield V(lambda e: e.tensor_reduce(out=gst[:, 0, :], in_=Y8, axis=AX.X, op=ALU.add), R=[Yst], W=[gst])
                    yield V(lambda e: e.tensor_scalar(out=gst[:, 1, :], in0=gst[:, 0, :], scalar1=-1.0 / 64, scalar2=None, op0=ALU.mult), R=[], W=[gst])
                    yield V(lambda e: e.tensor_tensor(out=yc8, in0=Y8, in1=b8(gst[:, 1, :]), op=ALU.add), R=[Yst], W=[yc, gst])
                    yield G(lambda e: e.tensor_tensor(out=ysq8, in0=yc8, in1=yc8, op=ALU.mult), R=[yc], W=[ysq])
                    yield V(lambda e: e.tensor_reduce(out=gst[:, 2, :], in_=ysq8, axis=AX.X, op=ALU.add), R=[ysq], W=[gst])
                    yield V(lambda e: e.tensor_scalar(out=gst[:, 3, :], in0=gst[:, 2, :], scalar1=1.0 / 64, scalar2=LN_EPS, op0=ALU.mult, op1=ALU.add), R=[], W=[gst])
                    yield G(lambda e: e.tensor_tensor(out=gst[:, 4, :], in0=gst[:, 3, :], in1=cst[:, CE + 4:CE + 12], op=ALU.pow), R=[cst], W=[gst])
                    yield V(lambda e: e.tensor_tensor(out=yc8, in0=yc8, in1=b8(gst[:, 4, :]), op=ALU.mult), R=[], W=[yc, gst])
                    yield G(lambda e: e.tensor_tensor(out=yc[:], in0=yc[:], in1=lnst[:, 0].unsqueeze(1).broadcast_to([128, 2, 4, 64]), op=ALU.mult), R=[lnst], W=[yc])
                    yield G(lambda e: e.tensor_tensor(out=yc[:], in0=yc[:], in1=lnst[:, 1].unsqueeze(1).broadcast_to([128, 2, 4, 64]), op=ALU.add), R=[lnst], W=[yc])
                    yield V(lambda e: e.tensor_tensor(out=ysq8, in0=V32_8, in1=b8(rk[:].rearrange("p q c -> p (q c)")), op=ALU.mult), R=[Vst32, rk], W=[ysq])
                    yield V(lambda e: e.tensor_tensor(out=yc8, in0=yc8, in1=ysq8, op=ALU.add), R=[ysq], W=[yc])
                    yield PE(hl(lambda q, c, sl, j: [lambda e: e.matmul(Q0[sl, q * 256 + c * 64:q * 256 + (c + 1) * 64], yc[sl, q, c, :], idsl(sl, j), start=True, stop=True)], Q2),
                             R=[yc, cst], W=[Q0])
                    yTn = yTr[n % 2]
                    yield V(lambda e: e.tensor_tensor(out=qc(yTn[:]), in0=v4(Q0[:]), in1=qc(B5[:]), op=ALU.mult), R=[Q0, B5], W=[yTn])
                    yield DMA("sp", ch_st[n % 2], lambda e: e.dma_start(out=yscr[n - 1][:, 512:1024], in_=yTn[:].rearrange("p c t -> p (c t)")), R=[yTn], W=[yscr_t[n - 1]])

                import os as _os
                W_PRE, W_ATT, W_SEQ, W_HEAD = [int(v) for v in _os.environ.get("KW", "3,3,1,1").split(",")]
                run(head(0))
                if nt > 1:
                    run(par([pre(0), attention(0), head(1)], [W_PRE, W_ATT, W_HEAD]))
                else:
                    run(par([pre(0), attention(0)], [W_PRE, W_ATT]))
                _skip = _os.environ.get("KSKIP", "")
                for i in range(nt):
                    streams = [seq(i)] if "seq" not in _skip else []
                    wts = [W_SEQ] if "seq" not in _skip else []
                    if i + 1 < nt:
                        if "pre" not in _skip:
                            streams += [pre(i + 1)]
                            wts += [W_PRE]
                        if "att" not in _skip:
                            streams += [attention(i + 1)]
                            wts += [W_ATT]
                    if i + 2 < nt:
                        streams.append(head(i + 2))
                        wts.append(W_HEAD)
                    run(par(streams, wts))
                S.barrier()

        if "A2" in phases:
            with ExitStack() as es:
                sb = mk_alloc(es, "a2_")
                CE = 128
                cst, pfm = load_consts(sb, 128)
                wgate = sb("wgate", [128, 8, 2048], BF16)
                wba = sb("wba", [128, 4, D], BF16)
                wbr = sb("wbr", [128, 4, D], BF16)
                for hh in range(2):
                    wload(wgate, wgate[:, 4 * hh:4 * hh + 4, :], w_gate.rearrange("(c p) n -> p c n", p=128)[:, 4 * hh:4 * hh + 4, :])
                wload(wba, wba[:], w_ba.rearrange("(c p) n -> p c n", p=128))
                wload(wbr, wbr[:], w_br.rearrange("(c p) n -> p c n", p=128))
                S.finalize(ch_w, [cst, pfm, wgate, wba, wbr])
                PS = [Tile(es.enter_context(nc.psum_tensor("psb%d" % i, [128, 512], F32)), "psb%d" % i, excl=True) for i in range(8)]
                xb = [sb("xb0", [128, D]), sb("xb1", [128, D])]
                yT = [sb("yT0", [128, 8, 128], BF16), sb("yT1", [128, 8, 128], BF16)]
                xs = sb("xs", [128, D])
                st4 = sb("st4", [128, 4])
                uTs = [sb("uT0", [128, 8, 128], BF16), sb("uT1", [128, 8, 128], BF16)]
                sg = sb("sg", [128, 16, 128])
                hbg = sb("hbg", [128, 16])
                V(lambda e: e.tensor_scalar(out=hbg[:], in0=pfm[:, P_BG:P_BG + 16], scalar1=0.5, scalar2=None, op0=ALU.mult), R=[pfm], W=[hbg])
                t1 = sb("t1", [128, 8, 128])
                t2 = sb("t2", [128, 8, 128])
                mT = [sb("mT0", [128, 8, 128], BF16), sb("mT1", [128, 8, 128], BF16)]

                def front2(n):
                    xt = xb[n % 2]
                    yTn = yT[n % 2]
                    yield DMA("sp", ch_x[n % 2], lambda e: e.dma_start(out=xt[:], in_=xe[n * 128:(n + 1) * 128, :]), W=[xt])
                    yield DMA("sp", ch_y[n % 2], lambda e: e.dma_start(out=yTn[:].rearrange("p c t -> p (c t)"), in_=yscr[n - 1]), R=[yscr_t[n - 1]], W=[yTn])
                    yield from norm_T(xt, xs, st4, cst, CE, pfm, P_GMIX, [PS[0], PS[1]], uTs[n % 2])

                def back2(n):
                    yTn = yT[n % 2]
                    mTn = mT[n % 2]
                    uT = uTs[n % 2]
                    for g in range(4):
                        bank = PS[2 + (g % 2)]
                        fns = []
                        for i in range(4):
                            col = (4 * g + i) * 128
                            for kc in range(8):
                                fns.append(lambda e, i=i, col=col, kc=kc, bank=bank: e.matmul(bank[:, i * 128:(i + 1) * 128], wgate[:, kc, col:col + 128], uT[:, kc, :],
                                                                                             start=(kc == 0), stop=(kc == 7)))
                        yield PE(fns, R=[uT, wgate], W=[bank])
                        for i in range(4):
                            yield A(lambda e, g=g, i=i, bank=bank: e.activation(out=sg[:, 4 * g + i, :], in_=bank[:, i * 128:(i + 1) * 128], func=AF.Tanh,
                                                                                bias=hbg[:, 4 * g + i:4 * g + i + 1], scale=0.5), R=[bank, hbg], W=[sg])
                    for br, (wb, off) in enumerate(((wba, 0), (wbr, 4))):
                        for hh in range(2):
                            bank = PS[4 + 2 * br + hh]
                            fns = []
                            for i in range(4):
                                fc = 4 * hh + i
                                for kc in range(4):
                                    fns.append(lambda e, i=i, fc=fc, kc=kc, bank=bank, wb=wb, off=off: e.matmul(bank[:, i * 128:(i + 1) * 128], wb[:, kc, fc * 128:(fc + 1) * 128],
                                                                                                                yTn[:, off + kc, :], start=(kc == 0), stop=(kc == 3)))
                            yield PE(fns, R=[yTn, wb], W=[bank])
                    for hh in range(2):
                        yield V(lambda e, hh=hh: e.scalar_tensor_tensor(out=t1[:, 4 * hh:4 * hh + 4, :], in0=sg[:, 4 * hh:4 * hh + 4, :], scalar=1.0,
                                                                        in1=PS[4 + hh][:].rearrange("p (c t) -> p c t", c=4), op0=ALU.add, op1=ALU.mult),
                                R=[PS[4 + hh], sg], W=[t1])
                        yield V(lambda e, hh=hh: e.scalar_tensor_tensor(out=t2[:, 4 * hh:4 * hh + 4, :], in0=sg[:, 8 + 4 * hh:8 + 4 * hh + 4, :], scalar=1.0,
                                                                        in1=PS[6 + hh][:].rearrange("p (c t) -> p c t", c=4), op0=ALU.add, op1=ALU.mult),
                                R=[PS[6 + hh], sg], W=[t2])
                    yield G(lambda e: e.tensor_tensor(out=t1[:], in0=t1[:], in1=t2[:], op=ALU.add), R=[t2], W=[t1])
                    yield A(lambda e: e.activation(out=mTn[:], in_=t1[:], func=AF.Copy, scale=0.5), R=[t1], W=[mTn])
                    yield DMA("sp", ch_st[n % 2], lambda e: e.dma_start(out=mscr[n - 1], in_=mTn[:].rearrange("p c t -> p (c t)")), R=[mTn], W=[mscr_t[n - 1]])

                if nt > 1:
                    run(front2(1))
                for n in range(1, nt):
                    streams = [back2(n)]
                    wts = [2]
                    if n + 1 < nt:
                        streams.append(front2(n + 1))
                        wts.append(1)
                    run(par(streams, wts))
                S.barrier()

        if "B" in phases:
            with ExitStack() as es:
                sb = mk_alloc(es, "b_")
                CE = 128
                cst, pfm = load_consts(sb, 128)
                gfin = sb("gfin", [128, D])
                S.dma("sp", ch_w, lambda e: e.dma_start(out=gfin[:], in_=gfind.broadcast_to([128, D])), W=[gfin])
                wo = sb("wo", [128, 8, D], BF16)
                wg = sb("wg", [128, 8, DFF], BF16)
                wu = sb("wu", [128, 8, DFF], BF16)
                wd = sb("wd", [128, NFC, D], BF16)
                for hh in range(2):
                    wload(wo, wo[:, 4 * hh:4 * hh + 4, :], w_o.rearrange("(c p) n -> p c n", p=128)[:, 4 * hh:4 * hh + 4, :])
                for wt, wsrc in ((wg, w_fg), (wu, w_fu)):
                    for hh in range(2):
                        for ch in range(2):
                            wload(wt, wt[:, 4 * hh:4 * hh + 4, ch * 1408:(ch + 1) * 1408],
                                  wsrc.rearrange("(c p) n -> p c n", p=128)[:, 4 * hh:4 * hh + 4, ch * 1408:(ch + 1) * 1408])
                for hh in range(2):
                    wload(wd, wd[:, 11 * hh:11 * hh + 11, :], w_fd.rearrange("(c p) n -> p c n", p=128)[:, 11 * hh:11 * hh + 11, :])
                S.finalize(ch_w, [cst, pfm, gfin, wo, wg, wu, wd])
                PS = [Tile(es.enter_context(nc.psum_tensor("psc%d" % i, [128, 512], F32)), "psc%d" % i, excl=True) for i in range(8)]
                xb = [sb("xb0", [128, D]), sb("xb1", [128, D])]
                mT = [sb("mT0", [128, 8, 128], BF16), sb("mT1", [128, 8, 128], BF16)]
                h1s = [sb("h1a", [128, D]), sb("h1b", [128, D])]
                xsF = sb("xsF", [128, D])
                xsB = [sb("xsB0", [128, D]), sb("xsB1", [128, D])]
                st4 = sb("st4", [128, 4])
                st4b = sb("st4b", [128, 4])
                fTs = [sb("fT0", [128, 8, 128], BF16), sb("fT1", [128, 8, 128], BF16)]
                sl_ = sb("silu", [128, 4, 128])
                aT = sb("aT", [128, NFC, 128], BF16)

                def front3(n):
                    xt = xb[n % 2]
                    mTn = mT[n % 2]
                    h1 = h1s[n % 2]
                    yield DMA("sp", ch_x[n % 2], lambda e: e.dma_start(out=xt[:], in_=xe[n * 128:(n + 1) * 128, :]), W=[xt])
                    yield DMA("sp", ch_y[n % 2], lambda e: e.dma_start(out=mTn[:].rearrange("p c t -> p (c t)"), in_=mscr[n - 1]), R=[mscr_t[n - 1]], W=[mTn])
                    for hh in range(2):
                        yield PE([lambda e, kc=kc, hh=hh: e.matmul(PS[hh][:], mTn[:, kc, :], wo[:, kc, hh * 512:(hh + 1) * 512], start=(kc == 0), stop=(kc == 7)) for kc in range(8)],
                                 R=[mTn, wo], W=[PS[hh]])
                        yield V(lambda e, hh=hh: e.tensor_tensor(out=h1[:, hh * 512:(hh + 1) * 512], in0=PS[hh][:], in1=xt[:, hh * 512:(hh + 1) * 512], op=ALU.add),
                                R=[PS[hh], xt], W=[h1])
                    yield from norm_T(h1, xsF, st4, cst, CE, pfm, P_GFFN, [PS[2], PS[3]], fTs[n % 2])

                def back3(n):
                    h1 = h1s[n % 2]
                    fT = fTs[n % 2]
                    o = xsB[n % 2]
                    ngrp = (NFC + 3) // 4
                    for g in range(ngrp):
                        nchunk = min(4, NFC - 4 * g)
                        bg = PS[4 + 2 * (g % 2)]
                        bu = PS[5 + 2 * (g % 2)]
                        for bank, wt in ((bg, wg), (bu, wu)):
                            fns = []
                            for i in range(nchunk):
                                fc = 4 * g + i
                                for kc in range(8):
                                    fns.append(lambda e, i=i, fc=fc, kc=kc, bank=bank, wt=wt: e.matmul(bank[:, i * 128:(i + 1) * 128], wt[:, kc, fc * 128:(fc + 1) * 128], fT[:, kc, :],
                                                                                                       start=(kc == 0), stop=(kc == 7)))
                            yield PE(fns, R=[fT, wt], W=[bank])
                        yield A(lambda e: e.activation(out=sl_[:, 0:nchunk, :], in_=bg[:, 0:nchunk * 128].rearrange("p (c t) -> p c t", c=nchunk), func=AF.Tanh, scale=0.5),
                                R=[bg], W=[sl_])
                        yield V(lambda e: e.scalar_tensor_tensor(out=sl_[:, 0:nchunk, :], in0=sl_[:, 0:nchunk, :], scalar=1.0,
                                                                 in1=bg[:, 0:nchunk * 128].rearrange("p (c t) -> p c t", c=nchunk), op0=ALU.add, op1=ALU.mult), R=[bg], W=[sl_])
                        yield V(lambda e: e.scalar_tensor_tensor(out=aT[:, 4 * g:4 * g + nchunk, :], in0=sl_[:, 0:nchunk, :], scalar=0.5,
                                                                 in1=bu[:, 0:nchunk * 128].rearrange("p (c t) -> p c t", c=nchunk), op0=ALU.mult, op1=ALU.mult), R=[bu, sl_], W=[aT])
                    for hh in range(2):
                        yield PE([lambda e, fc=fc, hh=hh: e.matmul(PS[4 + hh][:], aT[:, fc, :], wd[:, fc, hh * 512:(hh + 1) * 512], start=(fc == 0), stop=(fc == NFC - 1)) for fc in range(NFC)],
                                 R=[aT, wd], W=[PS[4 + hh]])
                        yield V(lambda e, hh=hh: e.tensor_tensor(out=h1[:, hh * 512:(hh + 1) * 512], in0=PS[4 + hh][:], in1=h1[:, hh * 512:(hh + 1) * 512], op=ALU.add),
                                R=[PS[4 + hh]], W=[h1])
                    yield A(lambda e: e.activation(out=o[:], in_=h1[:], func=AF.Square, accum_out=st4b[:, 0:1]), R=[h1], W=[o, st4b])
                    yield V(lambda e: e.tensor_scalar(out=st4b[:, 1:2], in0=st4b[:, 0:1], scalar1=1.0 / D, scalar2=RMS_EPS, op0=ALU.mult, op1=ALU.add), R=[], W=[st4b])
                    yield G(lambda e: e.tensor_tensor(out=st4b[:, 2:3], in0=st4b[:, 1:2], in1=cst[:, CE + 4:CE + 5], op=ALU.pow), R=[cst], W=[st4b])
                    yield A(lambda e: e.activation(out=o[:], in_=h1[:], func=AF.Identity, scale=st4b[:, 2:3], bias=cst[:, CE + 2:CE + 3]), R=[h1, cst], W=[o, st4b])
                    yield G(lambda e: e.tensor_tensor(out=o[:], in0=o[:], in1=gfin[:], op=ALU.mult), R=[gfin], W=[o])
                    yield DMA("sp", ch_st[n % 2], lambda e: e.dma_start(out=outd[(n - 1) * 128:n * 128, :], in_=o[:]), R=[o], W=[])

                if nt > 1:
                    run(front3(1))
                for n in range(1, nt):
                    streams = [back3(n)]
                    wts = [2]
                    if n + 1 < nt:
                        streams.append(front3(n + 1))
                        wts.append(1)
                    run(par(streams, wts))
                S.barrier()
        else:
            S.barrier()
    return nc


QPERM = [0, 4, 1, 5, 2, 6, 3, 7]


def make_consts():
    c = np.zeros((128, C_END), np.float32)
    c[:, C_ID:C_ID + 128] = np.eye(128, dtype=np.float32)
    s = np.arange(64)
    for j in range(2):
        rows = slice(64 * j, 64 * j + 64)
        c[rows, C_MB:C_MB + 64] = (s[None, :] > s[:, None])
        c[rows, C_MB + 64:C_MB + 128] = (s[None, :] >= s[:, None])
        c[rows, C_ML:C_ML + 64] = (s[None, :] < s[:, None])
        c[rows, C_I64:C_I64 + 64] = np.eye(64)
        c[rows, C_OBD + 64 * j:C_OBD + 64 * j + 64] = 1.0
        c[rows, C_TRI:C_TRI + 64] = CFAC * (s[:, None] <= s[None, :])
        c[rows, C_TRI + 64:C_TRI + 128] = CFAC * (s[:, None] < s[None, :])
    c[64:128, C_TRI0:C_TRI0 + 128] = c[64:128, C_TRI:C_TRI + 128]
    c[64:64 + 48, C_TRI0:C_TRI0 + 128] = 0.0
    i = np.arange(128)
    own = np.where(i[None, :] <= i[:, None], 0.0, NEG)
    prev = np.where(i[None, :] > i[:, None], 0.0, NEG)
    full = np.full((128, 128), NEG)
    for var, (a, b) in enumerate(((own, prev), (prev, own), (full, own))):
        base = C_AM + 272 * var
        c[:, base:base + 128] = a
        c[:, base + 128:base + 256] = b
        c[:, base + 256:base + 272] = 0.0
    half = 8
    inv_freq = np.power(np.float32(500000.0), -np.arange(half, dtype=np.float32) * np.float32(2.0 / 16)).astype(np.float32)
    for n in range(NTILES):
        pos = (n * 128 + np.arange(128) - 112).astype(np.float32)
        ang = (pos[:, None] * inv_freq[None, :]).astype(np.float32)
        c[:, C_ROPE + 16 * n:C_ROPE + 16 * n + 8] = np.cos(ang)
        c[:, C_ROPE + 16 * n + 8:C_ROPE + 16 * n + 16] = np.sin(ang)
    return c


def prep_shared(inp):
    f = np.float32
    w_in = np.asarray(inp["w_in"][0], f)
    b_in = np.asarray(inp["b_in"][0], f)
    qcols = np.concatenate([np.arange(h * 64, (h + 1) * 64) for h in QPERM])
    w_qkv = np.ascontiguousarray(np.concatenate([w_in[:, qcols], w_in[:, 512:768]], axis=1))
    b_qkv = np.concatenate([b_in[qcols], b_in[512:768]])
    R0 = 768
    w_fm = np.zeros((D, 2048), f)
    b_fm = np.zeros((2048,), f)
    mix = np.asarray(inp["rwkv_mix"][0], f)
    mix_fm = np.zeros((2048,), f)

    def put(dst0, src0, n):
        w_fm[:, dst0:dst0 + n] = w_in[:, R0 + src0:R0 + src0 + n]
        b_fm[dst0:dst0 + n] = b_in[R0 + src0:R0 + src0 + n]
        mix_fm[dst0:dst0 + n] = mix[src0:src0 + n]
    put(0, 0, 1536)
    put(1536, 1536, 64)
    put(1664, 1600, 64)
    put(1792, 1664, 128)
    put(1920, 1792, 32)
    G0 = 768 + 1824
    w_gate = np.ascontiguousarray(w_in[:, G0:G0 + 2048])
    b_gate = b_in[G0:G0 + 2048]
    rows_perm = qcols
    sh = {
        "w_qkv": w_qkv, "w_fm": w_fm, "w_gate": w_gate,
        "w_ba": np.ascontiguousarray(np.asarray(inp["w_br_attn"][0], f)[rows_perm, :]),
        "w_br": np.ascontiguousarray(np.asarray(inp["w_br_rwkv"][0], f)),
        "w_o": np.ascontiguousarray(np.asarray(inp["w_o"][0], f)),
        "w_fg": np.ascontiguousarray(np.asarray(inp["w_ffn_gate"][0], f)),
        "w_fu": np.ascontiguousarray(np.asarray(inp["w_ffn_up"][0], f)),
        "w_fd": np.ascontiguousarray(np.asarray(inp["w_ffn_down"][0], f)),
        "w2": np.ascontiguousarray(np.asarray(inp["rwkv_w2"][0], f)),
        "a2": np.ascontiguousarray(np.asarray(inp["rwkv_a2"][0], f)),
    }
    g2p = np.zeros((256, 512), f)
    g2p[0:160] = np.asarray(inp["rwkv_g2"][0], f)
    sh["g2p"] = g2p
    pfm = np.zeros((128, P_END), f)

    def fm(vec, ncol):
        return np.asarray(vec, f).reshape(ncol, 128).T
    pfm[:, P_GMIX:P_GMIX + 8] = fm(inp["norm_mix_g"][0], 8)
    pfm[:, P_GFFN:P_GFFN + 8] = fm(inp["norm_ffn_g"][0], 8)
    pfm[:, P_BFM:P_BFM + 16] = fm(b_fm, 16)
    pfm[:, P_BG:P_BG + 16] = fm(b_gate, 16)
    pfm[:, P_MIX:P_MIX + 16] = fm(mix_fm, 16)
    pfm[:, P_A0:P_A0 + 4] = fm(inp["rwkv_a0"][0], 4)
    pfm[:, P_KK:P_KK + 4] = fm(inp["rwkv_k_k"][0], 4)
    pfm[:, P_KA:P_KA + 4] = fm(inp["rwkv_k_a"][0], 4)
    pfm[:, P_RK:P_RK + 4] = fm(np.asarray(inp["rwkv_r_k"][0], f).reshape(-1), 4)
    sh["pfm"] = pfm
    rowsA = np.zeros((1, RA_END), f)
    rowsA[0, RA_BQ:RA_BQ + 768] = b_qkv
    rowsA[0, RA_W0:RA_W0 + 512] = np.asarray(inp["rwkv_w0"][0], f)
    rowsA[0, RA_SK:RA_SK + 8] = np.asarray(inp["attn_sinks"][0], f)[QPERM]
    sh["rowsA"] = rowsA
    sh["gfin"] = np.asarray(inp["norm_final_g"], f).reshape(1, D).copy()
    lnst = np.zeros((128, 2, 4, 64), f)
    for a, key in enumerate(("rwkv_ln_w", "rwkv_ln_b")):
        v = np.asarray(inp[key][0], f).reshape(4, 2, 64)
        for j in range(2):
            lnst[64 * j:64 * j + 64, a, :, :] = v[None, :, j, :]
    sh["lnst"] = lnst.reshape(128, -1)
    sh["cst"] = make_consts()
    return sh


def prep_xe(inp, b):
    xe = np.zeros((NTILES * 128, D), np.float32)
    xe[112:128] = np.asarray(inp["meta_tokens"], np.float32)
    xe[128:] = np.asarray(inp["x"][b], np.float32)
    return xe


_NC_CACHE = {}


def kernel(**inputs):
    n = 8
    sh = prep_shared(inputs)
    in_maps = []
    for b in range(n):
        m = dict(sh)
        m["xe"] = prep_xe(inputs, b)
        in_maps.append(m)
    if "nc" not in _NC_CACHE:
        _NC_CACHE["nc"] = build_program()
    res = run_bass_kernel_spmd(_NC_CACHE["nc"], in_maps, core_ids=list(range(n)))
    out = np.stack([np.asarray(r["out"], np.float32).reshape(4096, D) for r in res.results], axis=0)
    return out
```

```python
import numpy as np
import ml_dtypes
from contextlib import ExitStack
import concourse.bass as bass
import concourse.mybir as mybir
from concourse.bass_utils import run_bass_kernel_spmd

F32 = mybir.dt.float32
BF16 = mybir.dt.bfloat16
AF = mybir.ActivationFunctionType
ALU = mybir.AluOpType
AX = mybir.AxisListType

NTILES = 33
D = 1024
DFF = 2816
NFC = 22
RMS_EPS = 1e-6
LN_EPS = 64e-5
CFAC = -float(np.exp(-0.5))
NEG = -1e30
MD = BF16

C_ID = 0
C_MB = 128
C_ML = 256
C_I64 = 320
C_OBD = 384
C_TRI = 512
C_TRI0 = 640
C_AM = 768
C_ROPE = 768 + 816
C_END = C_ROPE + 33 * 16
P_GMIX, P_GFFN, P_BFM, P_BG, P_MIX, P_A0, P_KK, P_KA, P_RK, P_END = 0, 8, 16, 32, 48, 64, 68, 72, 76, 80
RA_BQ, RA_W0, RA_SK, RA_END = 0, 768, 1280, 1288


class Tile:
    def __init__(self, t, name, excl=False):
        self.t = t
        self.name = name
        self.w = None
        self.r = {}
        self.excl = excl
        self.tw = 0.0
        self.tr = 0.0
        self.weng = None

    def __getitem__(self, i):
        return self.t[i]


class Chan:
    def __init__(self, sem, key):
        self.sem = sem
        self.key = key
        self.count = 0


class Sched:
    def __init__(self, nc, es):
        self.nc = nc
        self.es = es
        self.E = {}
        for name, eng in (("pe", nc.tensor), ("act", nc.scalar), ("dve", nc.vector),
                          ("pool", nc.gpsimd), ("sp", nc.sync)):
            sem = es.enter_context(nc.semaphore("sem_" + name))
            self.E[name] = dict(eng=eng, sem=sem, count=0, seen={}, name=name)
        self.chans = []

    def chan(self, name):
        c = Chan(self.es.enter_context(self.nc.semaphore("ch_" + name)), "ch_" + name)
        self.chans.append(c)
        return c

    def _waits(self, E, R, W):
        deps = {}

        def add(d):
            key, val, sem = d
            if key not in deps or deps[key][0] < val:
                deps[key] = (val, sem)
        for t in R:
            if t.w is not None:
                add(t.w)
            if t.excl:
                for key, (val, sem) in t.r.items():
                    if key != E["name"]:
                        add((key, val, sem))
        for t in W:
            if t.w is not None:
                add(t.w)
            for key, (val, sem) in t.r.items():
                add((key, val, sem))
        for key, (val, sem) in deps.items():
            if key == "pe" and E["name"] == "pe":
                continue
            if E["seen"].get(key, 0) < val:
                E["eng"].wait_ge(sem, val)
                E["seen"][key] = val

    def op(self, ename, fns, R=(), W=()):
        E = self.E[ename]
        self._waits(E, R, W)
        if not isinstance(fns, (list, tuple)):
            fns = [fns]
        inst = None
        for f in fns:
            inst = f(E["eng"])
        E["count"] += 1
        inst.then_inc(E["sem"], 1)
        for t in W:
            t.w = (ename, E["count"], E["sem"])
            t.r = {}
        for t in R:
            if t not in W:
                t.r[ename] = (E["count"], E["sem"])

    def dma(self, qname, chan, fn, R=(), W=()):
        E = self.E[qname]
        self._waits(E, R, W)
        inst = fn(E["eng"])
        chan.count += 16
        inst.then_inc(chan.sem, 16)
        for t in W:
            t.w = (chan.key, chan.count, chan.sem)
            t.r = {}
        for t in R:
            t.r[chan.key] = (chan.count, chan.sem)

    def finalize(self, chan, tiles):
        for t in tiles:
            t.w = (chan.key, chan.count, chan.sem)

    def barrier(self):
        for name, E in self.E.items():
            for oname, O in self.E.items():
                if oname == name or O["count"] == 0:
                    continue
                if E["seen"].get(oname, 0) < O["count"]:
                    E["eng"].wait_ge(O["sem"], O["count"])
                    E["seen"][oname] = O["count"]
            for c in self.chans:
                if c.count and E["seen"].get(c.key, 0) < c.count:
                    E["eng"].wait_ge(c.sem, c.count)
                    E["seen"][c.key] = c.count


def build_program(nt=NTILES, phases=("A1", "A2", "B"), dbg=None, dbg_n=-1, md=None, scr_ext=False, stop=None):
    global MD
    if md is not None:
        MD = md
    nc = bass.Bass("TRN2", target_bir_lowering=False)

    def din(name, shape, dt=F32):
        return nc.dram_tensor(name, list(shape), dt, kind="ExternalInput").ap()

    xe = din("xe", [NTILES * 128, D])
    w_qkv = din("w_qkv", [D, 768])
    w_fm = din("w_fm", [D, 2048])
    w_gate = din("w_gate", [D, 2048])
    w_ba = din("w_ba", [512, D])
    w_br = din("w_br", [512, D])
    w_o = din("w_o", [D, D])
    w_fg = din("w_fg", [D, DFF])
    w_fu = din("w_fu", [D, DFF])
    w_fd = din("w_fd", [DFF, D])
    w2d = din("w2", [64, 512])
    a2d = din("a2", [64, 512])
    g2d = din("g2p", [256, 512])
    pfmd = din("pfm", [128, P_END])
    rowsAd = din("rowsA", [1, RA_END])
    gfind = din("gfin", [1, D])
    lnstd = din("lnst", [128, 2 * 4 * 64])
    cstd = din("cst", [128, C_END])
    outd = nc.dram_tensor("out", [(NTILES - 1) * 128, D], F32, kind="ExternalOutput").ap()
    skind = "ExternalOutput" if scr_ext else "Internal"
    yscr = nc.dram_tensor("yscr", [NTILES - 1, 128, 8 * 128], BF16, kind=skind).ap()
    mscr = nc.dram_tensor("mscr", [NTILES - 1, 128, 8 * 128], BF16, kind=skind).ap()
    dbg_out = {}
    if dbg:
        for name, shape in dbg.items():
            dbg_out[name] = nc.dram_tensor("dbg_" + name, list(shape), F32, kind="ExternalOutput").ap()

    with ExitStack() as es0:
        S = Sched(nc, es0)
        ch_w = S.chan("w")
        ch_x = [S.chan("x0"), S.chan("x1")]
        ch_y = [S.chan("y0"), S.chan("y1")]
        ch_st = [S.chan("s0"), S.chan("s1")]
        ch_dbg = S.chan("dbg")
        yscr_t = [Tile(None, "yscr%d" % i) for i in range(NTILES - 1)]
        mscr_t = [Tile(None, "mscr%d" % i) for i in range(NTILES - 1)]

        ST = {"defer": False, "small": False}
        eng_free = {"pe": 0.0, "act": 0.0, "dve": 0.0, "pool": 0.0, "sp": 0.0}
        HOP = 0.3

        class Op:
            __slots__ = ("ename", "fns", "R", "W", "dur", "chan")

            def __init__(self, ename, fns, R, W, dur, chan=None):
                self.ename = ename; self.fns = fns; self.R = R; self.W = W; self.dur = dur; self.chan = chan

        def est_start(op):
            t = eng_free[op.ename]
            for x in op.R:
                tw = getattr(x, "tw", 0.0)
                if x.weng != op.ename:
                    tw += HOP
                t = max(t, tw)
                if x.excl:
                    t = max(t, getattr(x, "tr", 0.0) + HOP)
            for x in op.W:
                t = max(t, getattr(x, "tw", 0.0) + (HOP if x.weng != op.ename else 0.0), getattr(x, "tr", 0.0) + HOP)
            return t

        def emit(op):
            t0 = est_start(op)
            t1 = t0 + op.dur
            if op.chan is None:
                S.op(op.ename, op.fns, op.R, op.W)
                eng_free[op.ename] = t1
            else:
                S.dma(op.ename, op.chan, op.fns, op.R, op.W)
                eng_free[op.ename] = t0 + 0.1
                t1 = t0 + 2.5
            for x in op.W:
                x.tw = t1; x.weng = op.ename; x.tr = 0.0
            for x in op.R:
                x.tr = max(getattr(x, "tr", 0.0), t1)

        def mkop(ename, fns, R, W, dur, chan=None):
            op = Op(ename, fns, list(R), list(W), dur, chan)
            if ST["defer"]:
                return op
            emit(op)
            return None

        def V(fn, R=(), W=(), d=0.5):
            return mkop("dve", fn, R, W, d)

        def A(fn, R=(), W=(), d=0.5):
            return mkop("act", fn, R, W, d)

        def G(fn, R=(), W=(), d=1.2):
            return mkop("pool", fn, R, W, d)

        def PE(fns, R=(), W=(), d=None):
            n_ = len(fns) if isinstance(fns, (list, tuple)) else 1
            if d is None:
                d = n_ * (0.03 if ST["small"] else 0.1) + 0.1
            ST["small"] = False
            return mkop("pe", fns, R, W, d)

        def DMA(qname, chan, fn, R=(), W=()):
            return mkop(qname, fn, R, W, 2.5, chan)

        def dump(name, tile_ap, tiles):
            if name in dbg_out:
                S.dma("sp", ch_dbg, lambda e: e.dma_start(out=dbg_out[name], in_=tile_ap), R=tiles, W=[])

        def run(gens):
            if not isinstance(gens, (list, tuple)):
                gens = [gens]
            ST["defer"] = True
            heads = []
            for g in gens:
                heads.append(next(g, None))
            try:
                while True:
                    best = None
                    bt = None
                    for i, h in enumerate(heads):
                        if h is None:
                            continue
                        t = est_start(h)
                        if bt is None or t < bt:
                            bt = t; best = i
                    if best is None:
                        break
                    ST["defer"] = False
                    emit(heads[best])
                    ST["defer"] = True
                    h = next(gens[best], None)
                    while h is None:
                        try:
                            h = next(gens[best])
                        except StopIteration:
                            h = None
                            break
                    heads[best] = h
            finally:
                ST["defer"] = False

        def par(gens, weights=None):
            return list(gens)

        def mk_alloc(es, pfx):
            def sb(name, shape, dt=F32):
                return Tile(es.enter_context(nc.sbuf_tensor(pfx + name, list(shape), dt)), pfx + name)
            return sb

        def norm_T(x, xs, st4, cst, ce, pfm, gcol, TR2, uT):
            yield A(lambda e: e.activation(out=xs[:], in_=x[:], func=AF.Square, accum_out=st4[:, 0:1]), R=[x], W=[xs, st4])
            yield V(lambda e: e.tensor_scalar(out=st4[:, 1:2], in0=st4[:, 0:1], scalar1=1.0 / D, scalar2=RMS_EPS, op0=ALU.mult, op1=ALU.add), R=[st4], W=[st4])
            yield G(lambda e: e.tensor_tensor(out=st4[:, 2:3], in0=st4[:, 1:2], in1=cst[:, ce + 4:ce + 5], op=ALU.pow), R=[st4, cst], W=[st4])
            yield A(lambda e: e.activation(out=xs[:], in_=x[:], func=AF.Identity, scale=st4[:, 2:3], bias=cst[:, ce + 2:ce + 3]),
                    R=[x, st4, cst], W=[xs])
            for h in range(2):
                yield PE([lambda e, c=c: e.transpose(TR2[h][:, (c % 4) * 128:(c % 4 + 1) * 128], xs[:, c * 128:(c + 1) * 128], cst[:, C_ID:C_ID + 128])
                          for c in range(4 * h, 4 * h + 4)], R=[xs, cst], W=[TR2[h]])
                yield V(lambda e, h=h: e.tensor_tensor(out=uT[:, 4 * h:4 * h + 4, :], in0=TR2[h][:].rearrange("p (c k) -> p c k", k=128),
                                                       in1=pfm[:, gcol + 4 * h:gcol + 4 * h + 4].unsqueeze(2).broadcast_to([128, 4, 128]), op=ALU.mult),
                        R=[TR2[h], pfm], W=[uT])

        def load_consts(sb, ncols):
            cst = sb("cst", [128, ncols + 12])
            pfm = sb("pfm", [128, P_END])
            G(lambda e: e.memset(cst[:, ncols:ncols + 1], RMS_EPS), W=[cst])
            G(lambda e: e.memset(cst[:, ncols + 1:ncols + 2], LN_EPS), W=[cst])
            G(lambda e: e.memset(cst[:, ncols + 2:ncols + 4], 0.0), W=[cst])
            G(lambda e: e.memset(cst[:, ncols + 4:ncols + 12], -0.5), W=[cst])
            S.dma("sp", ch_w, lambda e: e.dma_start(out=cst[:, 0:ncols], in_=cstd[:, 0:ncols]), W=[cst])
            S.dma("sp", ch_w, lambda e: e.dma_start(out=pfm[:], in_=pfmd), W=[pfm])
            return cst, pfm

        wl_n = [0]

        def wload_blk(tile_, out_ap, in_ap):
            wl_n[0] += 1
            ch = S.chan("wb%d" % wl_n[0])
            t = Tile(tile_.t, "%s_blk%d" % (tile_.name, wl_n[0]))
            S.dma("pool", ch, lambda e: e.dma_start(out=out_ap, in_=in_ap), W=[t])
            return t

        def wload(tile_, out_ap, in_ap):
            S.dma("pool", ch_w, lambda e: e.dma_start(out=out_ap, in_=in_ap), W=[tile_])

        if "A1" in phases:
            with ExitStack() as es:
                sb = mk_alloc(es, "a1_")
                CE = C_END
                cst, pfm = load_consts(sb, C_END)
                rowsA = sb("rowsA", [128, RA_END])
                S.dma("sp", ch_w, lambda e: e.dma_start(out=rowsA[:], in_=rowsAd.broadcast_to([128, RA_END])), W=[rowsA])
                lnst = sb("lnst", [128, 2, 4, 64])
                S.dma("sp", ch_w, lambda e: e.dma_start(out=lnst[:].rearrange("p a c v -> p (a c v)"), in_=lnstd), W=[lnst])
                wqkv = sb("wqkv", [128, 8, 768], BF16)
                wfm = sb("wfm", [128, 8, 2048], BF16)
                w2 = sb("w2", [64, 512])
                a2 = sb("a2", [64, 512])
                g2 = sb("g2", [128, 2, 512])
                S.dma("sp", ch_w, lambda e: e.dma_start(out=w2[:], in_=w2d), W=[w2])
                S.dma("sp", ch_w, lambda e: e.dma_start(out=a2[:], in_=a2d), W=[a2])
                S.dma("sp", ch_w, lambda e: e.dma_start(out=g2[:], in_=g2d.rearrange("(c p) n -> p c n", p=128)), W=[g2])
                wqkv_b = wload_blk(wqkv, wqkv[:], w_qkv.rearrange("(c p) n -> p c n", p=128))
                wfm_b = [wload_blk(wfm, wfm[:, :, 512 * g:512 * (g + 1)], w_fm.rearrange("(c p) n -> p c n", p=128)[:, :, 512 * g:512 * (g + 1)]) for g in range(4)]
                identb = sb("identb", [128, 128], BF16)
                ones2 = sb("ones2", [128, 2])
                G(lambda e: e.memset(ones2[:], 1.0), W=[ones2])
                S.finalize(ch_w, [cst, pfm, rowsA, lnst, w2, a2, g2])
                V(lambda e: e.tensor_copy(identb[:], cst[:, C_ID:C_ID + 128]), R=[cst], W=[identb])

                PS = [Tile(es.enter_context(nc.psum_tensor("ps%d" % i, [128, 512], F32)), "ps%d" % i, excl=True) for i in range(8)]
                H0, Q0, A0, A1_, A2_, R0, R1_, R2 = PS
                H1 = H0

                xb = [sb("xb0", [128, D]), sb("xb1", [128, D])]
                xs = sb("xs", [128, D])
                st4 = sb("st4", [128, 4])
                uT = sb("uT", [128, 8, 128], BF16)
                stg = sb("stg", [128, 16, 129])
                pfs = [sb("pf0", [128, 16, 128]), sb("pf1", [128, 16, 128])]
                qkvs = [sb("qkv0", [128, 768]), sb("qkv1", [128, 768])]
                rtmp = sb("rtmp", [128, 4, 10, 8])
                qT = sb("qT", [128, 4, 128], BF16)
                Kbuf = sb("Kbuf", [128, 272], BF16)
                Vbuf = sb("Vbuf", [128, 3, 128], BF16)
                Pb = [sb("Pb0", [128, 272], BF16), sb("Pb1", [128, 272], BF16)]
                PT = [sb("PT0", [128, 3, 128], BF16), sb("PT1", [128, 3, 128], BF16)]
                sm = sb("sm", [128, 5, 8])
                yat = sb("yat", [128, 8, 64])
                yTa = [sb("yTa0", [128, 4, 128], BF16), sb("yTa1", [128, 4, 128], BF16)]
                yTr = [sb("yTr0", [128, 4, 128], BF16), sb("yTr1", [128, 4, 128], BF16)]
                th = sb("th", [64, 128])
                sgd = sb("sgd", [128, 2, 128])
                B1 = sb("B1", [128, 4, 128]); B2 = sb("B2", [128, 4, 128]); B3 = sb("B3", [128, 4, 128])
                B4 = sb("B4", [128, 4, 128])
                arTs = [sb("arT%d" % i, [128, 4, 2, 2, 64], MD) for i in range(2)]
                BT = sb("BT", [128, 4, 128], MD); KT = sb("KT", [128, 4, 128], MD)
                BH = sb("BH", [128, 4, 128], MD); KH = sb("KH", [128, 4, 128], MD)
                cumC = sb("cumC", [128, 4, 2])
                WCs = [sb("WC%d" % i, [128, 4, 2]) for i in range(2)]
                rks = [sb("rk%d" % i, [128, 2, 4]) for i in range(2)]
                B5s = [sb("B5_%d" % i, [128, 4, 128]) for i in range(2)]
                TTs = [sb("TT%d" % i, [128, 2, 4, 64], MD) for i in range(2)]
                Ast = [sb("Ast0", [128, 2, 4, 64], MD), sb("Ast1", [128, 2, 4, 64], MD)]
                Nst = [sb("Nst0", [128, 2, 4, 64], MD), sb("Nst1", [128, 2, 4, 64], MD)]
                NBs = [sb("NB%d" % i, [128, 2, 4, 2, 64], MD) for i in range(2)]
                NKs = [sb("NK%d" % i, [128, 2, 4, 2, 64], MD) for i in range(2)]
                Pc = [sb("Pc0", [128, 2, 4, 64], MD), sb("Pc1", [128, 2, 4, 64], MD)]
                Vst32s = [sb("Vst32_%d" % i, [128, 2, 4, 64]) for i in range(2)]
                Vsts = [sb("Vst_%d" % i, [128, 2, 4, 64], MD) for i in range(2)] if MD != F32 else Vst32s
                BKsts = [sb("BKst%d" % i, [128, 2, 2, 4, 64], MD) for i in range(2)]
                R1 = sb("R1", [128, 4, 64], MD); Ust = sb("Ust", [128, 4, 64], MD)
                Yst = sb("Yst", [128, 2, 4, 64]); yc = sb("yc", [128, 2, 4, 64]); ysq = sb("ysq", [128, 2, 4, 64])
                ST32 = sb("ST32", [128, 4, 64])
                STm = sb("STm", [128, 4, 64], MD) if MD != F32 else ST32
                gst = sb("gst", [128, 6, 8])
                hb = sb("hb", [128, 4])
                V(lambda e: e.tensor_scalar(out=hb[:], in0=pfm[:, P_A0:P_A0 + 4], scalar1=0.5, scalar2=None, op0=ALU.mult), R=[pfm], W=[hb])
                G(lambda e: e.memset(ST32[:], 0.0), W=[ST32])
                if MD != F32:
                    G(lambda e: e.memset(STm[:], 0.0), W=[STm])
                G(lambda e: e.memset(stg[:], 0.0), W=[stg])
                G(lambda e: e.memset(Vbuf[:], 0.0), W=[Vbuf])
                G(lambda e: e.memset(Kbuf[:], 0.0), W=[Kbuf])

                def v3(ap2d, k=64):
                    return ap2d.rearrange("p (c k) -> p c k", k=k)

                def v4(ap2d):
                    return ap2d.rearrange("p (q c k) -> p q c k", q=2, c=4)

                def cq(t):
                    return t.rearrange("p c (q t) -> p c q t", q=2)

                def qc(t):
                    return t.rearrange("p c (q t) -> p q c t", q=2)

                ID0 = C_ID if MD == F32 else 0
                idm = cst if MD == F32 else identb

                def idsl(sl, j):
                    return cst[sl, C_ID + 64 * j:C_ID + 64 * j + 64]

                def idm_sl(sl, j):
                    return idm[sl, ID0 + 64 * j:ID0 + 64 * j + 64]

                def hl(fn, qs=(0,)):
                    ST["small"] = True
                    out = []
                    for q in qs:
                        for c in range(4):
                            for j in range(2):
                                out += fn(q, c, slice(64 * j, 64 * j + 64), j)
                    return out

                def head(n):
                    xt = xb[n % 2]
                    pf = pfs[n % 2]
                    qkv = qkvs[n % 2]
                    yield DMA("sp", ch_x[n % 2], lambda e: e.dma_start(out=xt[:], in_=xe[n * 128:(n + 1) * 128, :]), W=[xt])
                    yield from norm_T(xt, xs, st4, cst, CE, pfm, P_GMIX, [H0, H1], uT)
                    yield PE([lambda e, kc=kc: e.matmul(H0[:, 0:512], uT[:, kc, :], wqkv[:, kc, 0:512], start=(kc == 0), stop=(kc == 7)) for kc in range(8)],
                             R=[uT, wqkv_b], W=[H0])
                    yield V(lambda e: e.tensor_tensor(out=qkv[:, 0:512], in0=H0[:, 0:512], in1=rowsA[:, RA_BQ:RA_BQ + 512], op=ALU.add), R=[H0, rowsA], W=[qkv])
                    yield PE([lambda e, kc=kc: e.matmul(H1[:, 0:256], uT[:, kc, :], wqkv[:, kc, 512:768], start=(kc == 0), stop=(kc == 7)) for kc in range(8)],
                             R=[uT, wqkv_b], W=[H1])
                    yield V(lambda e: e.tensor_tensor(out=qkv[:, 512:768], in0=H1[:, 0:256], in1=rowsA[:, RA_BQ + 512:RA_BQ + 768], op=ALU.add), R=[H1, rowsA], W=[qkv])
                    for g in range(4):
                        bank = (H0, H1)[g % 2]
                        fns = []
                        for i in range(4):
                            col = (4 * g + i) * 128
                            for kc in range(8):
                                fns.append(lambda e, i=i, col=col, kc=kc, bank=bank: e.matmul(bank[:, i * 128:(i + 1) * 128], wfm[:, kc, col:col + 128], uT[:, kc, :],
                                                                                             start=(kc == 0), stop=(kc == 7)))
                        yield PE(fns, R=[uT, wfm_b[g]], W=[bank])
                        yield V(lambda e, g=g, bank=bank: e.tensor_tensor(out=stg[:, 4 * g:4 * g + 4, 1:129], in0=v3(bank[:], 128),
                                                                          in1=pfm[:, P_BFM + 4 * g:P_BFM + 4 * g + 4].unsqueeze(2).broadcast_to([128, 4, 128]), op=ALU.add),
                                R=[bank, pfm], W=[stg])
                    if n == 0:
                        yield G(lambda e: e.memset(stg[:, :, 1:113], 0.0), W=[stg])
                    yield G(lambda e: e.tensor_tensor(out=pf[:], in0=stg[:, :, 0:128], in1=stg[:, :, 1:129], op=ALU.subtract), R=[stg], W=[pf], d=3.6)
                    yield G(lambda e: e.tensor_tensor(out=pf[:], in0=pf[:], in1=pfm[:, P_MIX:P_MIX + 16].unsqueeze(2).broadcast_to([128, 16, 128]), op=ALU.mult),
                            R=[pfm], W=[pf], d=3.6)
                    yield G(lambda e: e.tensor_tensor(out=pf[:], in0=pf[:], in1=stg[:, :, 1:129], op=ALU.add), R=[stg], W=[pf], d=3.6)
                    yield G(lambda e: e.tensor_copy(stg[:, :, 0:1], stg[:, :, 128:129]), R=[], W=[stg])

                def attention(n):
                    qkv = qkvs[n % 2]
                    slot = n % 2
                    q10 = qkv[:, 0:640].rearrange("p (h d) -> p h d", d=64)
                    cosb = cst[:, C_ROPE + 16 * n:C_ROPE + 16 * n + 8].unsqueeze(1).broadcast_to([128, 10, 8])
                    sinb = cst[:, C_ROPE + 16 * n + 8:C_ROPE + 16 * n + 16].unsqueeze(1).broadcast_to([128, 10, 8])
                    yield G(lambda e: e.tensor_tensor(out=rtmp[:, 0], in0=q10[:, :, 0:8], in1=cosb, op=ALU.mult), R=[qkv, cst], W=[rtmp])
                    yield G(lambda e: e.tensor_tensor(out=rtmp[:, 1], in0=q10[:, :, 8:16], in1=sinb, op=ALU.mult), R=[qkv, cst], W=[rtmp])
                    yield G(lambda e: e.tensor_tensor(out=rtmp[:, 2], in0=q10[:, :, 8:16], in1=cosb, op=ALU.mult), R=[qkv, cst], W=[rtmp])
                    yield G(lambda e: e.tensor_tensor(out=rtmp[:, 3], in0=q10[:, :, 0:8], in1=sinb, op=ALU.mult), R=[qkv, cst], W=[rtmp])
                    yield G(lambda e: e.tensor_tensor(out=q10[:, :, 0:8], in0=rtmp[:, 0], in1=rtmp[:, 1], op=ALU.subtract), R=[rtmp], W=[qkv])
                    yield G(lambda e: e.tensor_tensor(out=q10[:, :, 8:16], in0=rtmp[:, 2], in1=rtmp[:, 3], op=ALU.add), R=[rtmp], W=[qkv])
                    yield PE([lambda e, c=c: e.transpose(A0[:, c * 128:(c + 1) * 128], qkv[:, c * 128:(c + 1) * 128], cst[:, C_ID:C_ID + 128]) for c in range(4)],
                             R=[qkv, cst], W=[A0])
                    yield PE(lambda e: e.transpose(A1_[:, 0:128], qkv[:, 512:640], cst[:, C_ID:C_ID + 128]), R=[qkv, cst], W=[A1_])
                    yield A(lambda e: e.activation(out=qT[:], in_=v3(A0[:], 128), func=AF.Copy, scale=0.125), R=[A0], W=[qT])
                    yield A(lambda e: e.activation(out=Kbuf[:, slot * 128:(slot + 1) * 128], in_=A1_[:, 0:128], func=AF.Copy), R=[A1_], W=[Kbuf])
                    yield V(lambda e: e.tensor_copy(Vbuf[:, slot, :], qkv[:, 640:768]), R=[qkv], W=[Vbuf])
                    if n == 0:
                        yield A(lambda e: e.activation(out=Kbuf[:, 256:272], in_=A1_[:, 112:128], func=AF.Copy), R=[A1_], W=[Kbuf])
                        yield PE(lambda e: e.matmul(A1_[0:16, 128:256], cst[:, C_ID + 112:C_ID + 128], qkv[:, 640:768], start=True, stop=True), R=[qkv, cst], W=[A1_])
                        yield V(lambda e: e.tensor_copy(Vbuf[0:16, 2, :], A1_[0:16, 128:256]), R=[A1_], W=[Vbuf])
                        return
                    yTn = yTa[n % 2]
                    mvar = 2 if n == 1 else (0 if n % 2 == 0 else 1)
                    mask = cst[:, C_AM + 272 * mvar:C_AM + 272 * (mvar + 1)]
                    for s in range(8):
                        c, j = s // 2, s % 2
                        sl = slice(64 * j, 64 * j + 64)
                        SC = A0
                        Pk = Pb[s % 2]
                        PTk = PT[s % 2]
                        yield PE(lambda e: e.matmul(SC[:, 0:272], qT[sl, c, :], Kbuf[sl, 0:272], start=True, stop=True), R=[qT, Kbuf], W=[SC])
                        yield V(lambda e: e.tensor_tensor(out=SC[:, 0:272], in0=SC[:, 0:272], in1=mask, op=ALU.add), R=[cst], W=[SC])
                        yield V(lambda e: e.tensor_reduce(out=sm[:, 0, s:s + 1], in_=SC[:, 0:272], axis=AX.X, op=ALU.max), R=[SC], W=[sm])
                        yield V(lambda e: e.tensor_scalar(out=sm[:, 1, s:s + 1], in0=sm[:, 0, s:s + 1], scalar1=rowsA[:, RA_SK + s:RA_SK + s + 1], scalar2=-1.0,
                                                          op0=ALU.max, op1=ALU.mult), R=[rowsA], W=[sm])
                        yield A(lambda e: e.activation(out=Pk[:], in_=SC[:, 0:272], func=AF.Exp, bias=sm[:, 1, s:s + 1], scale=1.0,
                                                       accum_out=sm[:, 2, s:s + 1]), R=[SC], W=[Pk, sm])
                        yield A(lambda e: e.activation(out=sm[:, 3, s:s + 1], in_=rowsA[:, RA_SK + s:RA_SK + s + 1], func=AF.Exp, bias=sm[:, 1, s:s + 1], scale=1.0),
                                R=[rowsA], W=[sm])
                        yield PE([lambda e, b=b, nk=nk: e.matmul(A1_[0:nk, b * 128:(b + 1) * 128], Pk[:, b * 128:b * 128 + nk], identb[:], start=True, stop=True)
                                  for b, nk in ((0, 128), (1, 128), (2, 16))], R=[Pk, identb], W=[A1_])
                        yield A(lambda e: e.activation(out=PTk[:, 0:2, :], in_=v3(A1_[:, 0:256], 128), func=AF.Copy), R=[A1_], W=[PTk])
                        yield A(lambda e: e.activation(out=PTk[0:16, 2, :], in_=A1_[0:16, 256:384], func=AF.Copy), R=[A1_], W=[PTk])
                        yield PE([lambda e: e.matmul(A2_[:, s * 64:(s + 1) * 64], PTk[:, 0, :], Vbuf[:, 0, sl], start=True, stop=False),
                                  lambda e: e.matmul(A2_[:, s * 64:(s + 1) * 64], PTk[:, 1, :], Vbuf[:, 1, sl], start=False, stop=False),
                                  lambda e: e.matmul(A2_[:, s * 64:(s + 1) * 64], PTk[0:16, 2, :], Vbuf[0:16, 2, sl], start=False, stop=True)],
                                 R=[PTk, Vbuf], W=[A2_])
                    yield V(lambda e: e.tensor_tensor(out=sm[:, 2, :], in0=sm[:, 2, :], in1=sm[:, 3, :], op=ALU.add), R=[], W=[sm])
                    yield V(lambda e: e.reciprocal(out=sm[:, 4, :], in_=sm[:, 2, :]), R=[], W=[sm])
                    yield V(lambda e: e.tensor_tensor(out=yat[:], in0=v3(A2_[:], 64), in1=sm[:, 4, :].unsqueeze(2).broadcast_to([128, 8, 64]), op=ALU.mult),
                            R=[A2_], W=[yat, sm])
                    yield PE([lambda e, c=c: e.transpose(A1_[:, c * 128:(c + 1) * 128], yat[:, 2 * c:2 * c + 2, :].rearrange("p a d -> p (a d)"), cst[:, C_ID:C_ID + 128])
                              for c in range(4)], R=[yat, cst], W=[A1_])
                    yield A(lambda e: e.activation(out=yTn[:], in_=v3(A1_[:], 128), func=AF.Copy), R=[A1_], W=[yTn])
                    if n == dbg_n:
                        dump("yat", yat[:].rearrange("p s d -> p (s d)"), [yat])
                    yield DMA("sp", ch_y[n % 2], lambda e: e.dma_start(out=yscr[n - 1][:, 0:512], in_=yTn[:].rearrange("p c t -> p (c t)")), R=[yTn], W=[yscr_t[n - 1]])

                def pre(n):
                    pf = pfs[n % 2]
                    pp_ = n % 2
                    arT, NB, NK, Vst, Vst32, BKst, WC, rk, B5 = arTs[pp_], NBs[pp_], NKs[pp_], Vsts[pp_], Vst32s[pp_], BKsts[pp_], WCs[pp_], rks[pp_], B5s[pp_]
                    tri0 = C_TRI0 if n == 0 else C_TRI
                    tri = cst[:, tri0:tri0 + 128]
                    yield A(lambda e: e.activation(out=th[:], in_=pf[0:64, 12, :], func=AF.Tanh), R=[pf], W=[th])
                    yield PE(lambda e: e.matmul(R0[:, 0:512], th[:], w2[:], start=True, stop=True), R=[th, w2], W=[R0])
                    B4f = B4[:].rearrange("p c t -> p (c t)")
                    yield V(lambda e: e.tensor_tensor(out=B4f, in0=R0[:, 0:512], in1=rowsA[:, RA_W0:RA_W0 + 512], op=ALU.add), R=[R0, rowsA], W=[B4])
                    yield A(lambda e: e.activation(out=B4f, in_=B4f, func=AF.Tanh, scale=0.5), R=[], W=[B4])
                    yield V(lambda e: e.tensor_scalar(out=B4f, in0=B4f, scalar1=0.5, scalar2=0.5, op0=ALU.mult, op1=ALU.add), R=[], W=[B4])
                    CB = (R1_, R2)
                    for q in range(2):
                        yield PE([lambda e, c=c, q=q: e.matmul(CB[q][:, c * 128:(c + 1) * 128], B4f[64 * q:64 * q + 64, c * 128:(c + 1) * 128],
                                                               tri[64 * q:64 * q + 64, :], start=True, stop=True) for c in range(4)], R=[B4, cst], W=[CB[q]])

                    def cums(a):
                        return [CB[q][:].rearrange("p (c a t) -> p c a t", c=4, a=2)[:, :, a, :] for q in range(2)]
                    yield PE([lambda e, c=c: e.matmul(R0[:, c * 128:(c + 1) * 128], a2[:, c * 128:(c + 1) * 128], pf[0:64, 13, :], start=True, stop=True) for c in range(4)],
                             R=[pf, a2], W=[R0])
                    for c in range(4):
                        yield A(lambda e, c=c: e.activation(out=B3[:, c, :], in_=R0[:, c * 128:(c + 1) * 128], func=AF.Tanh, bias=hb[:, c:c + 1], scale=0.5),
                                R=[R0, hb], W=[B3])
                    yield V(lambda e: e.tensor_scalar(out=B3[:], in0=B3[:], scalar1=0.5, scalar2=0.5, op0=ALU.mult, op1=ALU.add), R=[], W=[B3])
                    yield A(lambda e: e.activation(out=sgd[:], in_=pf[:, 14:16, :], func=AF.Tanh, scale=0.5), R=[pf], W=[sgd])
                    yield V(lambda e: e.tensor_scalar(out=sgd[:], in0=sgd[:], scalar1=0.5, scalar2=0.5, op0=ALU.mult, op1=ALU.add), R=[], W=[sgd])
                    fns = []
                    for c in range(4):
                        fns.append(lambda e, c=c: e.matmul(R0[:, c * 128:(c + 1) * 128], g2[:, 0, c * 128:(c + 1) * 128], sgd[:, 0, :], start=True, stop=False))
                        fns.append(lambda e, c=c: e.matmul(R0[:, c * 128:(c + 1) * 128], g2[0:32, 1, c * 128:(c + 1) * 128], sgd[0:32, 1, :], start=False, stop=True))
                    yield PE(fns, R=[sgd, g2], W=[R0])
                    yield A(lambda e: e.activation(out=B5[:].rearrange("p c t -> p (c t)"), in_=R0[:], func=AF.Copy), R=[R0], W=[B5])
                    kview = pf[:, 4:8, :]
                    rview = pf[:, 0:4, :]

                    def bc(col):
                        return pfm[:, col:col + 4].unsqueeze(2).broadcast_to([128, 4, 128])
                    yield V(lambda e: e.tensor_tensor(out=B1[:], in0=kview, in1=bc(P_KK), op=ALU.mult), R=[pf, pfm], W=[B1])
                    yield V(lambda e: e.tensor_tensor(out=B2[:], in0=B1[:], in1=B1[:], op=ALU.mult), R=[B1], W=[B2])
                    yield PE([lambda e, c=c: e.matmul(R0[:, c * 128:(c + 1) * 128], cst[:, C_OBD:C_OBD + 128], B2[:, c, :], start=True, stop=True) for c in range(4)],
                             R=[B2, cst], W=[R0])
                    yield A(lambda e: e.activation(out=B2[:].rearrange("p c t -> p (c t)"), in_=R0[:], func=AF.Sqrt), R=[R0], W=[B2])
                    yield V(lambda e: e.tensor_scalar(out=B2[:], in0=B2[:], scalar1=1e-12, scalar2=None, op0=ALU.max), R=[], W=[B2])
                    yield V(lambda e: e.reciprocal(out=B2[:], in_=B2[:]), R=[], W=[B2])
                    yield V(lambda e: e.tensor_tensor(out=B1[:], in0=B1[:], in1=B2[:], op=ALU.mult), R=[B2], W=[B1])
                    yield V(lambda e: e.scalar_tensor_tensor(out=B2[:], in0=B3[:], scalar=-1.0, in1=bc(P_KA), op0=ALU.add, op1=ALU.mult), R=[B3, pfm], W=[B2])
                    yield V(lambda e: e.scalar_tensor_tensor(out=kview, in0=B2[:], scalar=1.0, in1=kview, op0=ALU.add, op1=ALU.mult), R=[B2], W=[pf])
                    yield V(lambda e: e.tensor_tensor(out=B3[:], in0=B1[:], in1=B3[:], op=ALU.mult), R=[B1], W=[B3])
                    cex = cums(1)
                    cin = cums(0)
                    B4q = cq(B4[:])
                    for hh in range(2):
                        yield A(lambda e, hh=hh: e.activation(out=B4[:, :, 64 * hh:64 * hh + 64], in_=cex[hh], func=AF.Exp), R=[CB[hh]], W=[B4])
                    yield V(lambda e: e.scalar_tensor_tensor(out=arT[:, :, :, 0, :], in0=cq(B1[:]), scalar=-1.0, in1=B4q, op0=ALU.mult, op1=ALU.mult), R=[B1, B4], W=[arT])
                    for hh in range(2):
                        yield A(lambda e, hh=hh: e.activation(out=B4[:, :, 64 * hh:64 * hh + 64], in_=cin[hh], func=AF.Exp), R=[CB[hh]], W=[B4])
                    yield V(lambda e: e.tensor_tensor(out=arT[:, :, :, 1, :], in0=cq(rview), in1=B4q, op=ALU.mult), R=[pf, B4], W=[arT])
                    for hh in range(2):
                        yield A(lambda e, hh=hh: e.activation(out=B4[:, :, 64 * hh:64 * hh + 64], in_=cin[hh], func=AF.Exp, scale=-1.0), R=[CB[hh]], W=[B4])
                    yield V(lambda e: e.tensor_tensor(out=BT[:], in0=B3[:], in1=B4[:], op=ALU.mult), R=[B3, B4], W=[BT])
                    yield V(lambda e: e.tensor_tensor(out=KT[:], in0=kview, in1=B4[:], op=ALU.mult), R=[pf, B4], W=[KT])
                    for hh in range(2):
                        yield V(lambda e, hh=hh: e.tensor_copy(cumC[:, :, hh], cin[hh][:, :, 63]), R=[CB[hh]], W=[cumC])
                    for c in range(4):
                        for q in range(2):
                            yield A(lambda e, c=c, q=q: e.activation(out=B4[:, c, 64 * q:64 * q + 64], in_=cin[q][:, c, :], func=AF.Exp, scale=-1.0,
                                                                     bias=cumC[:, c, q:q + 1]), R=[CB[q], cumC], W=[B4])
                    yield V(lambda e: e.tensor_tensor(out=BH[:], in0=B3[:], in1=B4[:], op=ALU.mult), R=[B3, B4], W=[BH])
                    yield V(lambda e: e.tensor_tensor(out=KH[:], in0=kview, in1=B4[:], op=ALU.mult), R=[pf, B4], W=[KH])
                    yield A(lambda e: e.activation(out=WC[:], in_=cumC[:], func=AF.Exp), R=[cumC], W=[WC])

                    mb = cst[:, C_MB:C_MB + 128].rearrange("p (a t) -> p a t", a=2).unsqueeze(1).broadcast_to([128, 4, 2, 64])
                    ml8 = cst[:, C_ML:C_ML + 64].unsqueeze(1).broadcast_to([128, 8, 64])
                    i8 = cst[:, C_I64:C_I64 + 64].unsqueeze(1).broadcast_to([128, 8, 64])
                    Q2 = (0, 1)

                    def tq(q):
                        return slice(64 * q, 64 * q + 64)
                    yield PE(hl(lambda q, c, sl, j: [lambda e: e.matmul(R0[sl, q * 256 + c * 64:q * 256 + (c + 1) * 64], arT[sl, c, q, 0, :], BT[sl, c, tq(q)], start=True, stop=True)], Q2),
                             R=[arT, BT], W=[R0])
                    for q in Q2:
                        yield PE(hl(lambda q, c, sl, j: [lambda e: e.matmul(CB[q][sl, c * 128:(c + 1) * 128], BT[sl, c, tq(q)], arT[sl, c, q, :, :].rearrange("p a t -> p (a t)"),
                                                                            start=True, stop=True)], (q,)), R=[arT, BT], W=[CB[q]])
                    yield V(lambda e: e.tensor_tensor(out=Ast[0][:].rearrange("p q c t -> p (q c) t"), in0=v3(R0[:]), in1=ml8, op=ALU.mult), R=[R0, cst], W=[Ast[0]])
                    for q in Q2:
                        yield V(lambda e, q=q: e.tensor_tensor(out=NB[:, q], in0=CB[q][:].rearrange("p (c a t) -> p c a t", c=4, a=2), in1=mb, op=ALU.mult), R=[CB[q], cst], W=[NB])
                    for q in Q2:
                        yield PE(hl(lambda q, c, sl, j: [lambda e: e.matmul(CB[q][sl, c * 128:(c + 1) * 128], KT[sl, c, tq(q)], arT[sl, c, q, :, :].rearrange("p a t -> p (a t)"),
                                                                            start=True, stop=True)], (q,)), R=[arT, KT], W=[CB[q]])
                    for q in Q2:
                        yield V(lambda e, q=q: e.tensor_tensor(out=NK[:, q], in0=CB[q][:].rearrange("p (c a t) -> p c a t", c=4, a=2), in1=mb, op=ALU.mult), R=[CB[q], cst], W=[NK])
                    yield G(lambda e: e.tensor_copy(Nst[0][:], NB[:, :, :, 0, :]), R=[NB], W=[Nst[0]])
                    yield G(lambda e: e.tensor_tensor(out=Pc[0][:].rearrange("p q c t -> p (q c) t"), in0=Nst[0][:].rearrange("p q c t -> p (q c) t"), in1=i8, op=ALU.add),
                            R=[Nst[0], cst], W=[Pc[0]])
                    yield PE(hl(lambda q, c, sl, j: [lambda e: e.matmul(R0[sl, q * 256 + c * 64:q * 256 + (c + 1) * 64], pf[sl, 8 + c, tq(q)], idsl(sl, j), start=True, stop=True)], Q2),
                             R=[pf, cst], W=[R0])
                    yield A(lambda e: e.activation(out=Vst32[:], in_=v4(R0[:]), func=AF.Copy), R=[R0], W=[Vst32])
                    if MD != F32:
                        yield V(lambda e: e.tensor_copy(Vst[:], v4(R0[:])), R=[R0], W=[Vst])
                    for q in Q2:
                        yield PE(hl(lambda q, c, sl, j: [lambda e: e.matmul(CB[q][sl, c * 64:(c + 1) * 64], BH[sl, c, tq(q)], idm_sl(sl, j), start=True, stop=True),
                                                         lambda e: e.matmul(CB[q][sl, 256 + c * 64:256 + (c + 1) * 64], KH[sl, c, tq(q)], idm_sl(sl, j), start=True, stop=True)], (q,)),
                                 R=[BH, KH, idm], W=[CB[q]])
                    for q in Q2:
                        yield A(lambda e, q=q: e.activation(out=BKst[:, q], in_=CB[q][:].rearrange("p (a c t) -> p a c t", a=2, c=4), func=AF.Copy), R=[CB[q]], W=[BKst])
                    cur = 0
                    for lvl in range(1, 6):
                        nxt = 1 - cur
                        yield PE(hl(lambda q, c, sl, j: [lambda e: e.matmul(R0[sl, q * 256 + c * 64:q * 256 + (c + 1) * 64], Nst[cur][sl, q, c, :], Ast[cur][sl, q, c, :], start=True, stop=True)], Q2),
                                 R=[Nst[cur], Ast[cur]], W=[R0])
                        if lvl < 5:
                            yield PE(hl(lambda q, c, sl, j: [lambda e: e.matmul(R1_[sl, q * 256 + c * 64:q * 256 + (c + 1) * 64], Ast[cur][sl, q, c, :], Nst[cur][sl, q, c, :], start=True, stop=True)], Q2),
                                     R=[Nst[cur], Ast[cur]], W=[R1_])
                        yield A(lambda e, nxt=nxt: e.activation(out=Ast[nxt][:], in_=v4(R0[:]), func=AF.Copy), R=[R0], W=[Ast[nxt]])
                        if lvl < 5:
                            yield V(lambda e, nxt=nxt: e.tensor_copy(Nst[nxt][:], v4(R1_[:])), R=[R1_], W=[Nst[nxt]])
                        pc, pn = Pc[(lvl - 1) % 2], (Pc[lvl % 2] if lvl < 5 else TTs[pp_])
                        yield PE(hl(lambda q, c, sl, j: [lambda e: e.matmul(R2[sl, q * 256 + c * 64:q * 256 + (c + 1) * 64], Ast[nxt][sl, q, c, :], pc[sl, q, c, :], start=True, stop=True)], Q2),
                                 R=[Ast[nxt], pc], W=[R2])
                        yield V(lambda e, pc=pc, pn=pn: e.tensor_tensor(out=pn[:], in0=v4(R2[:]), in1=pc[:], op=ALU.add), R=[R2, pc], W=[pn])
                        cur = nxt
                    yield G(lambda e: e.tensor_tensor(out=B2[:], in0=rview, in1=kview, op=ALU.mult), R=[pf], W=[B2])
                    yield G(lambda e: e.tensor_tensor(out=B2[:], in0=B2[:], in1=bc(P_RK), op=ALU.mult), R=[pfm], W=[B2])
                    fns = []
                    for q in range(2):
                        for c in range(4):
                            for j in range(2):
                                sl = slice(64 * j, 64 * j + 64)
                                fns.append(lambda e, q=q, c=c, sl=sl: e.matmul(R0[sl, (q * 4 + c) * 2:(q * 4 + c) * 2 + 2], B2[sl, c, 64 * q:64 * q + 64], ones2[sl, :],
                                                                              start=True, stop=True))
                    yield PE(fns, R=[B2, ones2], W=[R0])
                    yield A(lambda e: e.activation(out=rk[:].rearrange("p q c -> p (q c)"), in_=R0[:, 0:16].rearrange("p (x two) -> p x two", two=2)[:, :, 0], func=AF.Copy), R=[R0], W=[rk])
                    return

                def seq(n):
                    pp_ = n % 2
                    arT, NB, NK, Vst, Vst32, BKst, WC, rk, B5 = arTs[pp_], NBs[pp_], NKs[pp_], Vsts[pp_], Vst32s[pp_], BKsts[pp_], WCs[pp_], rks[pp_], B5s[pp_]
                    TT = TTs[pp_]
                    Q2 = (0, 1)
                    for q in Q2:
                        yield PE(hl(lambda q, c, sl, j: [lambda e: e.matmul(Q0[sl, c * 64:(c + 1) * 64], arT[sl, c, q, 0, :], STm[sl, c, :], start=True, stop=False),
                                                         lambda e: e.matmul(Q0[sl, c * 64:(c + 1) * 64], NK[sl, q, c, 0, :], Vst[sl, q, c, :], start=False, stop=True)], (q,)),
                                 R=[arT, STm, NK, Vst], W=[Q0])
                        yield A(lambda e: e.activation(out=R1[:], in_=v3(Q0[:, 0:256]), func=AF.Copy), R=[Q0], W=[R1])
                        yield PE(hl(lambda q, c, sl, j: [lambda e: e.matmul(Q0[sl, 256 + c * 64:256 + (c + 1) * 64], TT[sl, q, c, :], R1[sl, c, :], start=True, stop=True)], (q,)),
                                 R=[TT, R1], W=[Q0])
                        yield A(lambda e: e.activation(out=Ust[:], in_=v3(Q0[:, 256:512]), func=AF.Copy), R=[Q0], W=[Ust])
                        if n >= 1:
                            yield PE(hl(lambda q, c, sl, j: [lambda e: e.matmul(Q0[sl, c * 64:(c + 1) * 64], arT[sl, c, q, 1, :], STm[sl, c, :], start=True, stop=False),
                                                             lambda e: e.matmul(Q0[sl, c * 64:(c + 1) * 64], NB[sl, q, c, 1, :], Ust[sl, c, :], start=False, stop=False),
                                                             lambda e: e.matmul(Q0[sl, c * 64:(c + 1) * 64], NK[sl, q, c, 1, :], Vst[sl, q, c, :], start=False, stop=True)], (q,)),
                                     R=[arT, STm, NB, Ust, NK, Vst], W=[Q0])
                            yield V(lambda e, q=q: e.tensor_copy(Yst[:, q], v3(Q0[:, 0:256])), R=[Q0], W=[Yst])
                        yield PE(hl(lambda q, c, sl, j: [lambda e: e.matmul(Q0[sl, 256 + c * 64:256 + (c + 1) * 64], BKst[sl, q, 0, c, :], Ust[sl, c, :], start=True, stop=False),
                                                         lambda e: e.matmul(Q0[sl, 256 + c * 64:256 + (c + 1) * 64], BKst[sl, q, 1, c, :], Vst[sl, q, c, :], start=False, stop=True)], (q,)),
                                 R=[BKst, Ust, Vst], W=[Q0])
                        yield G(lambda e, q=q: e.tensor_tensor(out=ST32[:], in0=ST32[:], in1=WC[:, :, q:q + 1].broadcast_to([128, 4, 64]), op=ALU.mult), R=[WC], W=[ST32])
                        yield V(lambda e: e.tensor_tensor(out=ST32[:], in0=ST32[:], in1=v3(Q0[:, 256:512]), op=ALU.add), R=[Q0], W=[ST32])
                        if MD != F32:
                            yield G(lambda e: e.tensor_copy(STm[:], ST32[:]), R=[ST32], W=[STm])
                    if n == 0:
                        return
                    Y8 = Yst[:].rearrange("p q c v -> p (q c) v")
                    yc8 = yc[:].rearrange("p q c v -> p (q c) v")
                    ysq8 = ysq[:].rearrange("p q c v -> p (q c) v")
                    V32_8 = Vst32[:].rearrange("p q c v -> p (q c) v")

                    def b8(ap):
                        return ap.unsqueeze(2).broadcast_to([128, 8, 64])
                    yield V(lambda e: e.tensor_reduce(out=gst[:, 0, :], in_=Y8, axis=AX.X, op=ALU.add), R=[Yst], W=[gst])
                    yield V(lambda e: e.tensor_scalar(out=gst[:, 1, :], in0=gst[:, 0, :], scalar1=-1.0 / 64, scalar2=None, op0=ALU.mult), R=[], W=[gst])
                    yield V(lambda e: e.tensor_tensor(out=yc8, in0=Y8, in1=b8(gst[:, 1, :]), op=ALU.add), R=[Yst], W=[yc, gst])
                    yield G(lambda e: e.tensor_tensor(out=ysq8, in0=yc8, in1=yc8, op=ALU.mult), R=[yc], W=[ysq])
                    yield V(lambda e: e.tensor_reduce(out=gst[:, 2, :], in_=ysq8, axis=AX.X, op=ALU.add), R=[ysq], W=[gst])
                    yield V(lambda e: e.tensor_scalar(out=gst[:, 3, :], in0=gst[:, 2, :], scalar1=1.0 / 64, scalar2=LN_EPS, op0=ALU.mult, op1=ALU.add), R=[], W=[gst])
                    yield G(lambda e: e.tensor_tensor(out=gst[:, 4, :], in0=gst[:, 3, :], in1=cst[:, CE + 4:CE + 12], op=ALU.pow), R=[cst], W=[gst])
                    yield V(lambda e: e.tensor_tensor(out=yc8, in0=yc8, in1=b8(gst[:, 4, :]), op=ALU.mult), R=[], W=[yc, gst])
                    yield G(lambda e: e.tensor_tensor(out=yc[:], in0=yc[:], in1=lnst[:, 0].unsqueeze(1).broadcast_to([128, 2, 4, 64]), op=ALU.mult), R=[lnst], W=[yc])
                    yield G(lambda e: e.tensor_tensor(out=yc[:], in0=yc[:], in1=lnst[:, 1].unsqueeze(1).broadcast_to([128, 2, 4, 64]), op=ALU.add), R=[lnst], W=[yc])
                    yield V(lambda e: e.tensor_tensor(out=ysq8, in0=V32_8, in1=b8(rk[:].rearrange("p q c -> p (q c)")), op=ALU.mult), R=[Vst32, rk], W=[ysq])
                    yield V(lambda e: e.tensor_tensor(out=yc8, in0=yc8, in1=ysq8, op=ALU.add), R=[ysq], W=[yc])
                    yield PE(hl(lambda q, c, sl, j: [lambda e: e.matmul(Q0[sl, q * 256 + c * 64:q * 256 + (c + 1) * 64], yc[sl, q, c, :], idsl(sl, j), start=True, stop=True)], Q2),
                             R=[yc, cst], W=[Q0])
                    yTn = yTr[n % 2]
                    yield V(lambda e: e.tensor_tensor(out=qc(yTn[:]), in0=v4(Q0[:]), in1=qc(B5[:]), op=ALU.mult), R=[Q0, B5], W=[yTn])
                    yield DMA("sp", ch_st[n % 2], lambda e: e.dma_start(out=yscr[n - 1][:, 512:1024], in_=yTn[:].rearrange("p c t -> p (c t)")), R=[yTn], W=[yscr_t[n - 1]])

                import os as _os
                W_PRE, W_ATT, W_SEQ, W_HEAD = [int(v) for v in _os.environ.get("KW", "3,3,1,1").split(",")]
                run(head(0))
                if nt > 1:
                    run(par([pre(0), attention(0), head(1)], [W_PRE, W_ATT, W_HEAD]))
                else:
                    run(par([pre(0), attention(0)], [W_PRE, W_ATT]))
                _skip = _os.environ.get("KSKIP", "")
                for i in range(nt):
                    streams = [seq(i)] if "seq" not in _skip else []
                    wts = [W_SEQ] if "seq" not in _skip else []
                    if i + 1 < nt:
                        if "pre" not in _skip:
                            streams += [pre(i + 1)]
                            wts += [W_PRE]
                        if "att" not in _skip:
                            streams += [attention(i + 1)]
                            wts += [W_ATT]
                    if i + 2 < nt:
                        streams.append(head(i + 2))
                        wts.append(W_HEAD)
                    run(par(streams, wts))
                S.barrier()

        if "A2" in phases:
            with ExitStack() as es:
                sb = mk_alloc(es, "a2_")
                CE = 128
                cst, pfm = load_consts(sb, 128)
                wgate = sb("wgate", [128, 8, 2048], BF16)
                wba = sb("wba", [128, 4, D], BF16)
                wbr = sb("wbr", [128, 4, D], BF16)
                wgate_b = [wload_blk(wgate, wgate[:, :, 512 * g:512 * (g + 1)], w_gate.rearrange("(c p) n -> p c n", p=128)[:, :, 512 * g:512 * (g + 1)]) for g in range(4)]
                wba_b = wload_blk(wba, wba[:], w_ba.rearrange("(c p) n -> p c n", p=128))
                wbr_b = wload_blk(wbr, wbr[:], w_br.rearrange("(c p) n -> p c n", p=128))
                S.finalize(ch_w, [cst, pfm])
                PS = [Tile(es.enter_context(nc.psum_tensor("psb%d" % i, [128, 512], F32)), "psb%d" % i, excl=True) for i in range(8)]
                xb = [sb("xb0", [128, D]), sb("xb1", [128, D])]
                yT = [sb("yT%d" % i, [128, 8, 128], BF16) for i in range(3)]
                xs = sb("xs", [128, D])
                st4 = sb("st4", [128, 4])
                uTs = [sb("uT0", [128, 8, 128], BF16), sb("uT1", [128, 8, 128], BF16)]
                sgs = [sb("sg0", [128, 16, 128]), sb("sg1", [128, 16, 128])]
                hbg = sb("hbg", [128, 16])
                V(lambda e: e.tensor_scalar(out=hbg[:], in0=pfm[:, P_BG:P_BG + 16], scalar1=0.5, scalar2=None, op0=ALU.mult), R=[pfm], W=[hbg])
                t1 = sb("t1", [128, 8, 128])
                t2 = sb("t2", [128, 8, 128])
                mT = [sb("mT0", [128, 8, 128], BF16), sb("mT1", [128, 8, 128], BF16)]

                def front2(n):
                    xt = xb[n % 2]
                    yTn = yT[n % 3]
                    yield DMA("sp", ch_x[n % 2], lambda e: e.dma_start(out=xt[:], in_=xe[n * 128:(n + 1) * 128, :]), W=[xt])
                    yield DMA("sp", ch_y[n % 2], lambda e: e.dma_start(out=yTn[:].rearrange("p c t -> p (c t)"), in_=yscr[n - 1]), R=[yscr_t[n - 1]], W=[yTn])
                    yield from norm_T(xt, xs, st4, cst, CE, pfm, P_GMIX, [PS[0], PS[1]], uTs[n % 2])

                def mid2(n):
                    uT = uTs[n % 2]
                    sg = sgs[n % 2]
                    for g in range(4):
                        bank = PS[2 + (g % 2)]
                        fns = []
                        for i in range(4):
                            col = (4 * g + i) * 128
                            for kc in range(8):
                                fns.append(lambda e, i=i, col=col, kc=kc, bank=bank: e.matmul(bank[:, i * 128:(i + 1) * 128], wgate[:, kc, col:col + 128], uT[:, kc, :],
                                                                                             start=(kc == 0), stop=(kc == 7)))
                        yield PE(fns, R=[uT, wgate_b[g]], W=[bank])
                        for i in range(4):
                            yield A(lambda e, g=g, i=i, bank=bank: e.activation(out=sg[:, 4 * g + i, :], in_=bank[:, i * 128:(i + 1) * 128], func=AF.Tanh,
                                                                                bias=hbg[:, 4 * g + i:4 * g + i + 1], scale=0.5), R=[bank, hbg], W=[sg])

                def tail2(n):
                    yTn = yT[n % 3]
                    mTn = mT[n % 2]
                    sg = sgs[n % 2]
                    for br, (wb, off, wb_b) in enumerate(((wba, 0, wba_b), (wbr, 4, wbr_b))):
                        for hh in range(2):
                            bank = PS[4 + 2 * br + hh]
                            fns = []
                            for i in range(4):
                                fc = 4 * hh + i
                                for kc in range(4):
                                    fns.append(lambda e, i=i, fc=fc, kc=kc, bank=bank, wb=wb, off=off: e.matmul(bank[:, i * 128:(i + 1) * 128], wb[:, kc, fc * 128:(fc + 1) * 128],
                                                                                                                yTn[:, off + kc, :], start=(kc == 0), stop=(kc == 3)))
                            yield PE(fns, R=[yTn, wb_b], W=[bank])
                    for hh in range(2):
                        yield V(lambda e, hh=hh: e.scalar_tensor_tensor(out=t1[:, 4 * hh:4 * hh + 4, :], in0=sg[:, 4 * hh:4 * hh + 4, :], scalar=1.0,
                                                                        in1=PS[4 + hh][:].rearrange("p (c t) -> p c t", c=4), op0=ALU.add, op1=ALU.mult),
                                R=[PS[4 + hh], sg], W=[t1])
                        yield V(lambda e, hh=hh: e.scalar_tensor_tensor(out=t2[:, 4 * hh:4 * hh + 4, :], in0=sg[:, 8 + 4 * hh:8 + 4 * hh + 4, :], scalar=1.0,
                                                                        in1=PS[6 + hh][:].rearrange("p (c t) -> p c t", c=4), op0=ALU.add, op1=ALU.mult),
                                R=[PS[6 + hh], sg], W=[t2])
                    yield G(lambda e: e.tensor_tensor(out=t1[:], in0=t1[:], in1=t2[:], op=ALU.add), R=[t2], W=[t1])
                    yield A(lambda e: e.activation(out=mTn[:], in_=t1[:], func=AF.Copy, scale=0.5), R=[t1], W=[mTn])
                    yield DMA("sp", ch_st[n % 2], lambda e: e.dma_start(out=mscr[n - 1], in_=mTn[:].rearrange("p c t -> p (c t)")), R=[mTn], W=[mscr_t[n - 1]])

                if nt > 1:
                    run(front2(1))
                if nt > 2:
                    run([mid2(1), front2(2)])
                elif nt > 1:
                    run(mid2(1))
                for n in range(1, nt):
                    streams = [tail2(n)]
                    if n + 1 < nt:
                        streams.append(mid2(n + 1))
                    if n + 2 < nt:
                        streams.append(front2(n + 2))
                    run(streams)
                S.barrier()

        if "B" in phases:
            with ExitStack() as es:
                sb = mk_alloc(es, "b_")
                CE = 128
                cst, pfm = load_consts(sb, 128)
                gfin = sb("gfin", [128, D])
                S.dma("sp", ch_w, lambda e: e.dma_start(out=gfin[:], in_=gfind.broadcast_to([128, D])), W=[gfin])
                wo = sb("wo", [128, 8, D], BF16)
                wg = sb("wg", [128, 8, DFF], BF16)
                wu = sb("wu", [128, 8, DFF], BF16)
                wd = sb("wd", [128, NFC, D], BF16)
                wo_b = wload_blk(wo, wo[:], w_o.rearrange("(c p) n -> p c n", p=128))
                ngrp_ = (NFC + 3) // 4
                wg_b = []
                wu_b = []
                for g in range(ngrp_):
                    c0, c1 = 512 * g, min(512 * (g + 1), DFF)
                    wg_b.append(wload_blk(wg, wg[:, :, c0:c1], w_fg.rearrange("(c p) n -> p c n", p=128)[:, :, c0:c1]))
                    wu_b.append(wload_blk(wu, wu[:, :, c0:c1], w_fu.rearrange("(c p) n -> p c n", p=128)[:, :, c0:c1]))
                wd_b = [wload_blk(wd, wd[:, 11 * hh:11 * hh + 11, :], w_fd.rearrange("(c p) n -> p c n", p=128)[:, 11 * hh:11 * hh + 11, :]) for hh in range(2)]
                S.finalize(ch_w, [cst, pfm, gfin])
                PS = [Tile(es.enter_context(nc.psum_tensor("psc%d" % i, [128, 512], F32)), "psc%d" % i, excl=True) for i in range(8)]
                xb = [sb("xb0", [128, D]), sb("xb1", [128, D])]
                mT = [sb("mT0", [128, 8, 128], BF16), sb("mT1", [128, 8, 128], BF16)]
                h1s = [sb("h1a", [128, D]), sb("h1b", [128, D]), sb("h1c", [128, D])]
                xsF = sb("xsF", [128, D])
                xsB = [sb("xsB0", [128, D]), sb("xsB1", [128, D])]
                st4 = sb("st4", [128, 4])
                st4b = sb("st4b", [128, 4])
                fTs = [sb("fT0", [128, 8, 128], BF16), sb("fT1", [128, 8, 128], BF16)]
                sl_ = sb("silu", [128, 4, 128])
                aTs = [sb("aT0", [128, NFC, 128], BF16), sb("aT1", [128, NFC, 128], BF16)]

                def front3(n):
                    xt = xb[n % 2]
                    mTn = mT[n % 2]
                    h1 = h1s[n % 3]
                    yield DMA("sp", ch_x[n % 2], lambda e: e.dma_start(out=xt[:], in_=xe[n * 128:(n + 1) * 128, :]), W=[xt])
                    yield DMA("sp", ch_y[n % 2], lambda e: e.dma_start(out=mTn[:].rearrange("p c t -> p (c t)"), in_=mscr[n - 1]), R=[mscr_t[n - 1]], W=[mTn])
                    for hh in range(2):
                        yield PE([lambda e, kc=kc, hh=hh: e.matmul(PS[0][:], mTn[:, kc, :], wo[:, kc, hh * 512:(hh + 1) * 512], start=(kc == 0), stop=(kc == 7)) for kc in range(8)],
                                 R=[mTn, wo_b], W=[PS[0]])
                        yield V(lambda e, hh=hh: e.tensor_tensor(out=h1[:, hh * 512:(hh + 1) * 512], in0=PS[0][:], in1=xt[:, hh * 512:(hh + 1) * 512], op=ALU.add),
                                R=[PS[0], xt], W=[h1])
                    yield from norm_T(h1, xsF, st4, cst, CE, pfm, P_GFFN, [PS[1], PS[1]], fTs[n % 2])

                def mid3(n):
                    fT = fTs[n % 2]
                    aT = aTs[n % 2]
                    ngrp = (NFC + 3) // 4
                    for g in range(ngrp):
                        nchunk = min(4, NFC - 4 * g)
                        bg = PS[2 + 2 * (g % 2)]
                        bu = PS[3 + 2 * (g % 2)]
                        for bank, wt, wt_b in ((bg, wg, wg_b[g]), (bu, wu, wu_b[g])):
                            fns = []
                            for i in range(nchunk):
                                fc = 4 * g + i
                                for kc in range(8):
                                    fns.append(lambda e, i=i, fc=fc, kc=kc, bank=bank, wt=wt: e.matmul(bank[:, i * 128:(i + 1) * 128], wt[:, kc, fc * 128:(fc + 1) * 128], fT[:, kc, :],
                                                                                                       start=(kc == 0), stop=(kc == 7)))
                            yield PE(fns, R=[fT, wt_b], W=[bank])
                        yield A(lambda e: e.activation(out=sl_[:, 0:nchunk, :], in_=bg[:, 0:nchunk * 128].rearrange("p (c t) -> p c t", c=nchunk), func=AF.Tanh, scale=0.5),
                                R=[bg], W=[sl_])
                        yield V(lambda e: e.scalar_tensor_tensor(out=sl_[:, 0:nchunk, :], in0=sl_[:, 0:nchunk, :], scalar=1.0,
                                                                 in1=bg[:, 0:nchunk * 128].rearrange("p (c t) -> p c t", c=nchunk), op0=ALU.add, op1=ALU.mult), R=[bg], W=[sl_])
                        yield V(lambda e: e.scalar_tensor_tensor(out=aT[:, 4 * g:4 * g + nchunk, :], in0=sl_[:, 0:nchunk, :], scalar=0.5,
                                                                 in1=bu[:, 0:nchunk * 128].rearrange("p (c t) -> p c t", c=nchunk), op0=ALU.mult, op1=ALU.mult), R=[bu, sl_], W=[aT])

                def tail3(n):
                    h1 = h1s[n % 3]
                    aT = aTs[n % 2]
                    o = xsB[n % 2]
                    for hh in range(2):
                        yield PE([lambda e, fc=fc, hh=hh: e.matmul(PS[6 + hh][:], aT[:, fc, :], wd[:, fc, hh * 512:(hh + 1) * 512], start=(fc == 0), stop=(fc == NFC - 1)) for fc in range(NFC)],
                                 R=[aT] + wd_b, W=[PS[6 + hh]])
                        yield V(lambda e, hh=hh: e.tensor_tensor(out=h1[:, hh * 512:(hh + 1) * 512], in0=PS[6 + hh][:], in1=h1[:, hh * 512:(hh + 1) * 512], op=ALU.add),
                                R=[PS[6 + hh]], W=[h1])
                    yield A(lambda e: e.activation(out=o[:], in_=h1[:], func=AF.Square, accum_out=st4b[:, 0:1]), R=[h1], W=[o, st4b])
                    yield V(lambda e: e.tensor_scalar(out=st4b[:, 1:2], in0=st4b[:, 0:1], scalar1=1.0 / D, scalar2=RMS_EPS, op0=ALU.mult, op1=ALU.add), R=[], W=[st4b])
                    yield G(lambda e: e.tensor_tensor(out=st4b[:, 2:3], in0=st4b[:, 1:2], in1=cst[:, CE + 4:CE + 5], op=ALU.pow), R=[cst], W=[st4b])
                    yield A(lambda e: e.activation(out=o[:], in_=h1[:], func=AF.Identity, scale=st4b[:, 2:3], bias=cst[:, CE + 2:CE + 3]), R=[h1, cst], W=[o, st4b])
                    yield G(lambda e: e.tensor_tensor(out=o[:], in0=o[:], in1=gfin[:], op=ALU.mult), R=[gfin], W=[o])
                    yield DMA("sp", ch_st[n % 2], lambda e: e.dma_start(out=outd[(n - 1) * 128:n * 128, :], in_=o[:]), R=[o], W=[])

                if nt > 1:
                    run(front3(1))
                if nt > 2:
                    run([mid3(1), front3(2)])
                elif nt > 1:
                    run(mid3(1))
                for n in range(1, nt):
                    streams = [tail3(n)]
                    if n + 1 < nt:
                        streams.append(mid3(n + 1))
                    if n + 2 < nt:
                        streams.append(front3(n + 2))
                    run(streams)
                S.barrier()
        else:
            S.barrier()
    return nc


QPERM = [0, 4, 1, 5, 2, 6, 3, 7]


def make_consts():
    c = np.zeros((128, C_END), np.float32)
    c[:, C_ID:C_ID + 128] = np.eye(128, dtype=np.float32)
    s = np.arange(64)
    for j in range(2):
        rows = slice(64 * j, 64 * j + 64)
        c[rows, C_MB:C_MB + 64] = (s[None, :] > s[:, None])
        c[rows, C_MB + 64:C_MB + 128] = (s[None, :] >= s[:, None])
        c[rows, C_ML:C_ML + 64] = (s[None, :] < s[:, None])
        c[rows, C_I64:C_I64 + 64] = np.eye(64)
        c[rows, C_OBD + 64 * j:C_OBD + 64 * j + 64] = 1.0
        c[rows, C_TRI:C_TRI + 64] = CFAC * (s[:, None] <= s[None, :])
        c[rows, C_TRI + 64:C_TRI + 128] = CFAC * (s[:, None] < s[None, :])
    c[64:128, C_TRI0:C_TRI0 + 128] = c[64:128, C_TRI:C_TRI + 128]
    c[64:64 + 48, C_TRI0:C_TRI0 + 128] = 0.0
    i = np.arange(128)
    own = np.where(i[None, :] <= i[:, None], 0.0, NEG)
    prev = np.where(i[None, :] > i[:, None], 0.0, NEG)
    full = np.full((128, 128), NEG)
    for var, (a, b) in enumerate(((own, prev), (prev, own), (full, own))):
        base = C_AM + 272 * var
        c[:, base:base + 128] = a
        c[:, base + 128:base + 256] = b
        c[:, base + 256:base + 272] = 0.0
    half = 8
    inv_freq = np.power(np.float32(500000.0), -np.arange(half, dtype=np.float32) * np.float32(2.0 / 16)).astype(np.float32)
    for n in range(NTILES):
        pos = (n * 128 + np.arange(128) - 112).astype(np.float32)
        ang = (pos[:, None] * inv_freq[None, :]).astype(np.float32)
        c[:, C_ROPE + 16 * n:C_ROPE + 16 * n + 8] = np.cos(ang)
        c[:, C_ROPE + 16 * n + 8:C_ROPE + 16 * n + 16] = np.sin(ang)
    return c


def prep_shared(inp):
    f = np.float32
    w_in = np.asarray(inp["w_in"][0], f)
    b_in = np.asarray(inp["b_in"][0], f)
    qcols = np.concatenate([np.arange(h * 64, (h + 1) * 64) for h in QPERM])
    w_qkv = np.ascontiguousarray(np.concatenate([w_in[:, qcols], w_in[:, 512:768]], axis=1))
    b_qkv = np.concatenate([b_in[qcols], b_in[512:768]])
    R0 = 768
    w_fm = np.zeros((D, 2048), f)
    b_fm = np.zeros((2048,), f)
    mix = np.asarray(inp["rwkv_mix"][0], f)
    mix_fm = np.zeros((2048,), f)

    def put(dst0, src0, n):
        w_fm[:, dst0:dst0 + n] = w_in[:, R0 + src0:R0 + src0 + n]
        b_fm[dst0:dst0 + n] = b_in[R0 + src0:R0 + src0 + n]
        mix_fm[dst0:dst0 + n] = mix[src0:src0 + n]
    put(0, 0, 1536)
    put(1536, 1536, 64)
    put(1664, 1600, 64)
    put(1792, 1664, 128)
    put(1920, 1792, 32)
    G0 = 768 + 1824
    w_gate = np.ascontiguousarray(w_in[:, G0:G0 + 2048])
    b_gate = b_in[G0:G0 + 2048]
    rows_perm = qcols
    sh = {
        "w_qkv": w_qkv, "w_fm": w_fm, "w_gate": w_gate,
        "w_ba": np.ascontiguousarray(np.asarray(inp["w_br_attn"][0], f)[rows_perm, :]),
        "w_br": np.ascontiguousarray(np.asarray(inp["w_br_rwkv"][0], f)),
        "w_o": np.ascontiguousarray(np.asarray(inp["w_o"][0], f)),
        "w_fg": np.ascontiguousarray(np.asarray(inp["w_ffn_gate"][0], f)),
        "w_fu": np.ascontiguousarray(np.asarray(inp["w_ffn_up"][0], f)),
        "w_fd": np.ascontiguousarray(np.asarray(inp["w_ffn_down"][0], f)),
        "w2": np.ascontiguousarray(np.asarray(inp["rwkv_w2"][0], f)),
        "a2": np.ascontiguousarray(np.asarray(inp["rwkv_a2"][0], f)),
    }
    g2p = np.zeros((256, 512), f)
    g2p[0:160] = np.asarray(inp["rwkv_g2"][0], f)
    sh["g2p"] = g2p
    pfm = np.zeros((128, P_END), f)

    def fm(vec, ncol):
        return np.asarray(vec, f).reshape(ncol, 128).T
    pfm[:, P_GMIX:P_GMIX + 8] = fm(inp["norm_mix_g"][0], 8)
    pfm[:, P_GFFN:P_GFFN + 8] = fm(inp["norm_ffn_g"][0], 8)
    pfm[:, P_BFM:P_BFM + 16] = fm(b_fm, 16)
    pfm[:, P_BG:P_BG + 16] = fm(b_gate, 16)
    pfm[:, P_MIX:P_MIX + 16] = fm(mix_fm, 16)
    pfm[:, P_A0:P_A0 + 4] = fm(inp["rwkv_a0"][0], 4)
    pfm[:, P_KK:P_KK + 4] = fm(inp["rwkv_k_k"][0], 4)
    pfm[:, P_KA:P_KA + 4] = fm(inp["rwkv_k_a"][0], 4)
    pfm[:, P_RK:P_RK + 4] = fm(np.asarray(inp["rwkv_r_k"][0], f).reshape(-1), 4)
    sh["pfm"] = pfm
    rowsA = np.zeros((1, RA_END), f)
    rowsA[0, RA_BQ:RA_BQ + 768] = b_qkv
    rowsA[0, RA_W0:RA_W0 + 512] = np.asarray(inp["rwkv_w0"][0], f)
    rowsA[0, RA_SK:RA_SK + 8] = np.asarray(inp["attn_sinks"][0], f)[QPERM]
    sh["rowsA"] = rowsA
    sh["gfin"] = np.asarray(inp["norm_final_g"], f).reshape(1, D).copy()
    lnst = np.zeros((128, 2, 4, 64), f)
    for a, key in enumerate(("rwkv_ln_w", "rwkv_ln_b")):
        v = np.asarray(inp[key][0], f).reshape(4, 2, 64)
        for j in range(2):
            lnst[64 * j:64 * j + 64, a, :, :] = v[None, :, j, :]
    sh["lnst"] = lnst.reshape(128, -1)
    sh["cst"] = make_consts()
    return sh


def prep_xe(inp, b):
    xe = np.zeros((NTILES * 128, D), np.float32)
    xe[112:128] = np.asarray(inp["meta_tokens"], np.float32)
    xe[128:] = np.asarray(inp["x"][b], np.float32)
    return xe


_NC_CACHE = {}


def kernel(**inputs):
    n = 8
    sh = prep_shared(inputs)
    in_maps = []
    for b in range(n):
        m = dict(sh)
        m["xe"] = prep_xe(inputs, b)
        in_maps.append(m)
    if "nc" not in _NC_CACHE:
        _NC_CACHE["nc"] = build_program()
    res = run_bass_kernel_spmd(_NC_CACHE["nc"], in_maps, core_ids=list(range(n)))
    out = np.stack([np.asarray(r["out"], np.float32).reshape(4096, D) for r in res.results], axis=0)
    return out
```

```python
import numpy as np
import ml_dtypes
from contextlib import ExitStack
import concourse.bass as bass
import concourse.mybir as mybir
from concourse.bass_utils import run_bass_kernel_spmd

F32 = mybir.dt.float32
BF16 = mybir.dt.bfloat16
AF = mybir.ActivationFunctionType
ALU = mybir.AluOpType
AX = mybir.AxisListType

NTILES = 33
D = 1024
DFF = 2816
NFC = 22
RMS_EPS = 1e-6
LN_EPS = 64e-5
CFAC = -float(np.exp(-0.5))
NEG = -1e30
MD = BF16

C_ID = 0
C_MB = 128
C_ML = 256
C_I64 = 320
C_OBD = 384
C_TRI = 512
C_TRI0 = 640
C_AM = 768
C_ROPE = 768 + 816
C_END = C_ROPE + 33 * 16
P_GMIX, P_GFFN, P_BFM, P_BG, P_MIX, P_A0, P_KK, P_KA, P_RK, P_END = 0, 8, 16, 32, 48, 64, 68, 72, 76, 80
RA_BQ, RA_W0, RA_SK, RA_END = 0, 768, 1280, 1288


class Tile:
    def __init__(self, t, name, excl=False):
        self.t = t
        self.name = name
        self.w = None
        self.r = {}
        self.excl = excl
        self.tw = 0.0
        self.tr = 0.0
        self.weng = None

    def __getitem__(self, i):
        return self.t[i]


class Chan:
    def __init__(self, sem, key):
        self.sem = sem
        self.key = key
        self.count = 0


class Sched:
    def __init__(self, nc, es):
        self.nc = nc
        self.es = es
        self.E = {}
        for name, eng in (("pe", nc.tensor), ("act", nc.scalar), ("dve", nc.vector),
                          ("pool", nc.gpsimd), ("sp", nc.sync)):
            sem = es.enter_context(nc.semaphore("sem_" + name))
            self.E[name] = dict(eng=eng, sem=sem, count=0, seen={}, name=name)
        self.chans = []

    def chan(self, name):
        c = Chan(self.es.enter_context(self.nc.semaphore("ch_" + name)), "ch_" + name)
        self.chans.append(c)
        return c

    def _waits(self, E, R, W):
        deps = {}

        def add(d):
            key, val, sem = d
            if key not in deps or deps[key][0] < val:
                deps[key] = (val, sem)
        for t in R:
            if t.w is not None:
                add(t.w)
            if t.excl:
                for key, (val, sem) in t.r.items():
                    if key != E["name"]:
                        add((key, val, sem))
        for t in W:
            if t.w is not None:
                add(t.w)
            for key, (val, sem) in t.r.items():
                add((key, val, sem))
        for key, (val, sem) in deps.items():
            if key == "pe" and E["name"] == "pe":
                continue
            if E["seen"].get(key, 0) < val:
                E["eng"].wait_ge(sem, val)
                E["seen"][key] = val

    def op(self, ename, fns, R=(), W=()):
        E = self.E[ename]
        self._waits(E, R, W)
        if not isinstance(fns, (list, tuple)):
            fns = [fns]
        inst = None
        for f in fns:
            inst = f(E["eng"])
        E["count"] += 1
        inst.then_inc(E["sem"], 1)
        for t in W:
            t.w = (ename, E["count"], E["sem"])
            t.r = {}
        for t in R:
            if t not in W:
                t.r[ename] = (E["count"], E["sem"])

    def dma(self, qname, chan, fn, R=(), W=()):
        E = self.E[qname]
        self._waits(E, R, W)
        inst = fn(E["eng"])
        chan.count += 16
        inst.then_inc(chan.sem, 16)
        for t in W:
            t.w = (chan.key, chan.count, chan.sem)
            t.r = {}
        for t in R:
            t.r[chan.key] = (chan.count, chan.sem)

    def finalize(self, chan, tiles):
        for t in tiles:
            t.w = (chan.key, chan.count, chan.sem)

    def barrier(self):
        for name, E in self.E.items():
            for oname, O in self.E.items():
                if oname == name or O["count"] == 0:
                    continue
                if E["seen"].get(oname, 0) < O["count"]:
                    E["eng"].wait_ge(O["sem"], O["count"])
                    E["seen"][oname] = O["count"]
            for c in self.chans:
                if c.count and E["seen"].get(c.key, 0) < c.count:
                    E["eng"].wait_ge(c.sem, c.count)
                    E["seen"][c.key] = c.count


def build_program(nt=NTILES, phases=("A1", "A2", "B"), dbg=None, dbg_n=-1, md=None, scr_ext=False, stop=None):
    global MD
    if md is not None:
        MD = md
    nc = bass.Bass("TRN2", target_bir_lowering=False)

    def din(name, shape, dt=F32):
        return nc.dram_tensor(name, list(shape), dt, kind="ExternalInput").ap()

    xe = din("xe", [NTILES * 128, D])
    w_qkv = din("w_qkv", [D, 768])
    w_fm = din("w_fm", [D, 2048])
    w_gate = din("w_gate", [D, 2048])
    w_ba = din("w_ba", [512, D])
    w_br = din("w_br", [512, D])
    w_o = din("w_o", [D, D])
    w_fg = din("w_fg", [D, DFF])
    w_fu = din("w_fu", [D, DFF])
    w_fd = din("w_fd", [DFF, D])
    w2d = din("w2", [64, 512])
    a2d = din("a2", [64, 512])
    g2d = din("g2p", [256, 512])
    pfmd = din("pfm", [128, P_END])
    rowsAd = din("rowsA", [1, RA_END])
    gfind = din("gfin", [1, D])
    lnstd = din("lnst", [128, 2 * 4 * 64])
    cstd = din("cst", [128, C_END])
    outd = nc.dram_tensor("out", [(NTILES - 1) * 128, D], F32, kind="ExternalOutput").ap()
    skind = "ExternalOutput" if scr_ext else "Internal"
    yscr = nc.dram_tensor("yscr", [NTILES - 1, 128, 8 * 128], BF16, kind=skind).ap()
    mscr = nc.dram_tensor("mscr", [NTILES - 1, 128, 8 * 128], BF16, kind=skind).ap()
    dbg_out = {}
    if dbg:
        for name, shape in dbg.items():
            dbg_out[name] = nc.dram_tensor("dbg_" + name, list(shape), F32, kind="ExternalOutput").ap()

    with ExitStack() as es0:
        S = Sched(nc, es0)
        ch_w = S.chan("w")
        ch_x = [S.chan("x0"), S.chan("x1")]
        ch_y = [S.chan("y0"), S.chan("y1")]
        ch_st = [S.chan("s0"), S.chan("s1")]
        ch_dbg = S.chan("dbg")
        yscr_t = [Tile(None, "yscr%d" % i) for i in range(NTILES - 1)]
        mscr_t = [Tile(None, "mscr%d" % i) for i in range(NTILES - 1)]

        ST = {"defer": False, "small": False}
        eng_free = {"pe": 0.0, "act": 0.0, "dve": 0.0, "pool": 0.0, "sp": 0.0}
        import os as _os0
        import random as _random
        _cfg = _os0.environ.get("KCFG", "0,0.3,0.1,0.03,0.5,0.0").split(",")
        _rng = _random.Random(int(_cfg[0]))
        HOP = float(_cfg[1]); KPE = float(_cfg[2]); KPS = float(_cfg[3]); KVD = float(_cfg[4]); JIT = float(_cfg[5])

        class Op:
            __slots__ = ("ename", "fns", "R", "W", "dur", "chan")

            def __init__(self, ename, fns, R, W, dur, chan=None):
                self.ename = ename; self.fns = fns; self.R = R; self.W = W; self.dur = dur; self.chan = chan

        def est_start(op):
            t = eng_free[op.ename]
            for x in op.R:
                tw = getattr(x, "tw", 0.0)
                if x.weng != op.ename:
                    tw += HOP
                t = max(t, tw)
                if x.excl:
                    t = max(t, getattr(x, "tr", 0.0) + HOP)
            for x in op.W:
                t = max(t, getattr(x, "tw", 0.0) + (HOP if x.weng != op.ename else 0.0), getattr(x, "tr", 0.0) + HOP)
            return t

        def emit(op):
            t0 = est_start(op)
            t1 = t0 + op.dur
            if op.chan is None:
                S.op(op.ename, op.fns, op.R, op.W)
                eng_free[op.ename] = t1
            else:
                S.dma(op.ename, op.chan, op.fns, op.R, op.W)
                eng_free[op.ename] = t0 + 0.1
                t1 = t0 + 2.5
            for x in op.W:
                x.tw = t1; x.weng = op.ename; x.tr = 0.0
            for x in op.R:
                x.tr = max(getattr(x, "tr", 0.0), t1)

        def mkop(ename, fns, R, W, dur, chan=None):
            if JIT > 0:
                dur = dur * (1.0 + JIT * (_rng.random() - 0.5))
            op = Op(ename, fns, list(R), list(W), dur, chan)
            if ST["defer"]:
                return op
            emit(op)
            return None

        def V(fn, R=(), W=(), d=None):
            d = KVD if d is None else d
            return mkop("dve", fn, R, W, d)

        def A(fn, R=(), W=(), d=None):
            d = KVD if d is None else d
            return mkop("act", fn, R, W, d)

        def G(fn, R=(), W=(), d=1.2):
            return mkop("pool", fn, R, W, d)

        def PE(fns, R=(), W=(), d=None):
            n_ = len(fns) if isinstance(fns, (list, tuple)) else 1
            if d is None:
                d = n_ * (KPS if ST["small"] else KPE) + 0.1
            ST["small"] = False
            return mkop("pe", fns, R, W, d)

        def DMA(qname, chan, fn, R=(), W=()):
            return mkop(qname, fn, R, W, 2.5, chan)

        def dump(name, tile_ap, tiles):
            if name in dbg_out:
                S.dma("sp", ch_dbg, lambda e: e.dma_start(out=dbg_out[name], in_=tile_ap), R=tiles, W=[])

        def run(gens, bonus=None):
            if not isinstance(gens, (list, tuple)):
                gens = [gens]
            if bonus is None:
                bonus = [0.0] * len(gens)
            ST["defer"] = True
            heads = []
            for g in gens:
                heads.append(next(g, None))
            try:
                while True:
                    best = None
                    bt = None
                    for i, h in enumerate(heads):
                        if h is None:
                            continue
                        t = est_start(h) - bonus[i]
                        if bt is None or t < bt:
                            bt = t; best = i
                    if best is None:
                        break
                    ST["defer"] = False
                    emit(heads[best])
                    ST["defer"] = True
                    h = next(gens[best], None)
                    while h is None:
                        try:
                            h = next(gens[best])
                        except StopIteration:
                            h = None
                            break
                    heads[best] = h
            finally:
                ST["defer"] = False

        def par(gens, weights=None):
            return list(gens)

        def mk_alloc(es, pfx):
            def sb(name, shape, dt=F32):
                return Tile(es.enter_context(nc.sbuf_tensor(pfx + name, list(shape), dt)), pfx + name)
            return sb

        def norm_T(x, xs, st4, cst, ce, pfm, gcol, TR2, uT):
            yield A(lambda e: e.activation(out=xs[:], in_=x[:], func=AF.Square, accum_out=st4[:, 0:1]), R=[x], W=[xs, st4])
            yield V(lambda e: e.tensor_scalar(out=st4[:, 1:2], in0=st4[:, 0:1], scalar1=1.0 / D, scalar2=RMS_EPS, op0=ALU.mult, op1=ALU.add), R=[st4], W=[st4])
            yield G(lambda e: e.tensor_tensor(out=st4[:, 2:3], in0=st4[:, 1:2], in1=cst[:, ce + 4:ce + 5], op=ALU.pow), R=[st4, cst], W=[st4])
            yield A(lambda e: e.activation(out=xs[:], in_=x[:], func=AF.Identity, scale=st4[:, 2:3], bias=cst[:, ce + 2:ce + 3]),
                    R=[x, st4, cst], W=[xs])
            for h in range(2):
                yield PE([lambda e, c=c: e.transpose(TR2[h][:, (c % 4) * 128:(c % 4 + 1) * 128], xs[:, c * 128:(c + 1) * 128], cst[:, C_ID:C_ID + 128])
                          for c in range(4 * h, 4 * h + 4)], R=[xs, cst], W=[TR2[h]])
                yield V(lambda e, h=h: e.tensor_tensor(out=uT[:, 4 * h:4 * h + 4, :], in0=TR2[h][:].rearrange("p (c k) -> p c k", k=128),
                                                       in1=pfm[:, gcol + 4 * h:gcol + 4 * h + 4].unsqueeze(2).broadcast_to([128, 4, 128]), op=ALU.mult),
                        R=[TR2[h], pfm], W=[uT])

        def load_consts(sb, ncols):
            cst = sb("cst", [128, ncols + 12])
            pfm = sb("pfm", [128, P_END])
            G(lambda e: e.memset(cst[:, ncols:ncols + 1], RMS_EPS), W=[cst])
            G(lambda e: e.memset(cst[:, ncols + 1:ncols + 2], LN_EPS), W=[cst])
            G(lambda e: e.memset(cst[:, ncols + 2:ncols + 4], 0.0), W=[cst])
            G(lambda e: e.memset(cst[:, ncols + 4:ncols + 12], -0.5), W=[cst])
            S.dma("sp", ch_w, lambda e: e.dma_start(out=cst[:, 0:ncols], in_=cstd[:, 0:ncols]), W=[cst])
            S.dma("sp", ch_w, lambda e: e.dma_start(out=pfm[:], in_=pfmd), W=[pfm])
            return cst, pfm

        wl_n = [0]

        def wload_blk(tile_, out_ap, in_ap):
            wl_n[0] += 1
            ch = S.chan("wb%d" % wl_n[0])
            t = Tile(tile_.t, "%s_blk%d" % (tile_.name, wl_n[0]))
            S.dma("pool", ch, lambda e: e.dma_start(out=out_ap, in_=in_ap), W=[t])
            return t

        def wload(tile_, out_ap, in_ap):
            S.dma("pool", ch_w, lambda e: e.dma_start(out=out_ap, in_=in_ap), W=[tile_])

        if "A1" in phases:
            with ExitStack() as es:
                sb = mk_alloc(es, "a1_")
                CE = C_END
                cst, pfm = load_consts(sb, C_END)
                rowsA = sb("rowsA", [128, RA_END])
                S.dma("sp", ch_w, lambda e: e.dma_start(out=rowsA[:], in_=rowsAd.broadcast_to([128, RA_END])), W=[rowsA])
                lnst = sb("lnst", [128, 2, 4, 64])
                S.dma("sp", ch_w, lambda e: e.dma_start(out=lnst[:].rearrange("p a c v -> p (a c v)"), in_=lnstd), W=[lnst])
                wqkv = sb("wqkv", [128, 8, 768], BF16)
                wfm = sb("wfm", [128, 8, 2048], BF16)
                w2 = sb("w2", [64, 512])
                a2 = sb("a2", [64, 512])
                g2 = sb("g2", [128, 2, 512])
                S.dma("sp", ch_w, lambda e: e.dma_start(out=w2[:], in_=w2d), W=[w2])
                S.dma("sp", ch_w, lambda e: e.dma_start(out=a2[:], in_=a2d), W=[a2])
                S.dma("sp", ch_w, lambda e: e.dma_start(out=g2[:], in_=g2d.rearrange("(c p) n -> p c n", p=128)), W=[g2])
                wqkv_b = wload_blk(wqkv, wqkv[:], w_qkv.rearrange("(c p) n -> p c n", p=128))
                wfm_b = [wload_blk(wfm, wfm[:, :, 512 * g:512 * (g + 1)], w_fm.rearrange("(c p) n -> p c n", p=128)[:, :, 512 * g:512 * (g + 1)]) for g in range(4)]
                identb = sb("identb", [128, 128], BF16)
                ones2 = sb("ones2", [128, 2])
                G(lambda e: e.memset(ones2[:], 1.0), W=[ones2])
                S.finalize(ch_w, [cst, pfm, rowsA, lnst, w2, a2, g2])
                V(lambda e: e.tensor_copy(identb[:], cst[:, C_ID:C_ID + 128]), R=[cst], W=[identb])

                PS = [Tile(es.enter_context(nc.psum_tensor("ps%d" % i, [128, 512], F32)), "ps%d" % i, excl=True) for i in range(8)]
                H0, Q0, A0, A1_, A2_, R0, R1_, R2 = PS
                H1 = H0

                xb = [sb("xb0", [128, D]), sb("xb1", [128, D])]
                xs = sb("xs", [128, D])
                st4 = sb("st4", [128, 4])
                uT = sb("uT", [128, 8, 128], BF16)
                stg = sb("stg", [128, 16, 129])
                pfs = [sb("pf0", [128, 16, 128]), sb("pf1", [128, 16, 128])]
                qkvs = [sb("qkv0", [128, 768]), sb("qkv1", [128, 768])]
                rtmp = sb("rtmp", [128, 4, 10, 8])
                qT = sb("qT", [128, 4, 128], BF16)
                Kbuf = sb("Kbuf", [128, 272], BF16)
                Vbuf = sb("Vbuf", [128, 3, 128], BF16)
                Pb = [sb("Pb0", [128, 272], BF16), sb("Pb1", [128, 272], BF16)]
                PT = [sb("PT0", [128, 3, 128], BF16), sb("PT1", [128, 3, 128], BF16)]
                sm = sb("sm", [128, 5, 8])
                yat = sb("yat", [128, 8, 64])
                yTa = [sb("yTa0", [128, 4, 128], BF16), sb("yTa1", [128, 4, 128], BF16)]
                yTr = [sb("yTr0", [128, 4, 128], BF16), sb("yTr1", [128, 4, 128], BF16)]
                th = sb("th", [64, 128])
                sgd = sb("sgd", [128, 2, 128])
                B1 = sb("B1", [128, 4, 128]); B2 = sb("B2", [128, 4, 128]); B3 = sb("B3", [128, 4, 128])
                B4 = sb("B4", [128, 4, 128])
                arTs = [sb("arT%d" % i, [128, 4, 2, 2, 64], MD) for i in range(2)]
                BT = sb("BT", [128, 4, 128], MD); KT = sb("KT", [128, 4, 128], MD)
                BH = sb("BH", [128, 4, 128], MD); KH = sb("KH", [128, 4, 128], MD)
                cumC = sb("cumC", [128, 4, 2])
                WCs = [sb("WC%d" % i, [128, 4, 2]) for i in range(2)]
                rks = [sb("rk%d" % i, [128, 2, 4]) for i in range(2)]
                B5s = [sb("B5_%d" % i, [128, 4, 128]) for i in range(2)]
                TTs = [sb("TT%d" % i, [128, 2, 4, 64], MD) for i in range(2)]
                Ast = [sb("Ast0", [128, 2, 4, 64], MD), sb("Ast1", [128, 2, 4, 64], MD)]
                Nst = [sb("Nst0", [128, 2, 4, 64], MD), sb("Nst1", [128, 2, 4, 64], MD)]
                NBs = [sb("NB%d" % i, [128, 2, 4, 2, 64], MD) for i in range(2)]
                NKs = [sb("NK%d" % i, [128, 2, 4, 2, 64], MD) for i in range(2)]
                Pc = [sb("Pc0", [128, 2, 4, 64], MD), sb("Pc1", [128, 2, 4, 64], MD)]
                Vst32s = [sb("Vst32_%d" % i, [128, 2, 4, 64]) for i in range(2)]
                Vsts = [sb("Vst_%d" % i, [128, 2, 4, 64], MD) for i in range(2)] if MD != F32 else Vst32s
                BKsts = [sb("BKst%d" % i, [128, 2, 2, 4, 64], MD) for i in range(2)]
                R1 = sb("R1", [128, 4, 64], MD); Ust = sb("Ust", [128, 4, 64], MD)
                Yst = sb("Yst", [128, 2, 4, 64]); yc = sb("yc", [128, 2, 4, 64]); ysq = sb("ysq", [128, 2, 4, 64])
                ST32 = sb("ST32", [128, 4, 64])
                STt = sb("STt", [128, 4, 64])
                STm = sb("STm", [128, 4, 64], MD) if MD != F32 else ST32
                gst = sb("gst", [128, 6, 8])
                hb = sb("hb", [128, 4])
                V(lambda e: e.tensor_scalar(out=hb[:], in0=pfm[:, P_A0:P_A0 + 4], scalar1=0.5, scalar2=None, op0=ALU.mult), R=[pfm], W=[hb])
                G(lambda e: e.memset(ST32[:], 0.0), W=[ST32])
                if MD != F32:
                    G(lambda e: e.memset(STm[:], 0.0), W=[STm])
                G(lambda e: e.memset(stg[:], 0.0), W=[stg])
                G(lambda e: e.memset(Vbuf[:], 0.0), W=[Vbuf])
                G(lambda e: e.memset(Kbuf[:], 0.0), W=[Kbuf])

                def v3(ap2d, k=64):
                    return ap2d.rearrange("p (c k) -> p c k", k=k)

                def v4(ap2d):
                    return ap2d.rearrange("p (q c k) -> p q c k", q=2, c=4)

                def cq(t):
                    return t.rearrange("p c (q t) -> p c q t", q=2)

                def qc(t):
                    return t.rearrange("p c (q t) -> p q c t", q=2)

                ID0 = C_ID if MD == F32 else 0
                idm = cst if MD == F32 else identb

                def idsl(sl, j):
                    return cst[sl, C_ID + 64 * j:C_ID + 64 * j + 64]

                def idm_sl(sl, j):
                    return idm[sl, ID0 + 64 * j:ID0 + 64 * j + 64]

                def hl(fn, qs=(0,)):
                    ST["small"] = True
                    out = []
                    for q in qs:
                        for c in range(4):
                            for j in range(2):
                                out += fn(q, c, slice(64 * j, 64 * j + 64), j)
                    return out

                def head(n):
                    xt = xb[n % 2]
                    pf = pfs[n % 2]
                    qkv = qkvs[n % 2]
                    yield DMA("sp", ch_x[n % 2], lambda e: e.dma_start(out=xt[:], in_=xe[n * 128:(n + 1) * 128, :]), W=[xt])
                    yield from norm_T(xt, xs, st4, cst, CE, pfm, P_GMIX, [H0, H1], uT)
                    yield PE([lambda e, kc=kc: e.matmul(H0[:, 0:512], uT[:, kc, :], wqkv[:, kc, 0:512], start=(kc == 0), stop=(kc == 7)) for kc in range(8)],
                             R=[uT, wqkv_b], W=[H0])
                    yield V(lambda e: e.tensor_tensor(out=qkv[:, 0:512], in0=H0[:, 0:512], in1=rowsA[:, RA_BQ:RA_BQ + 512], op=ALU.add), R=[H0, rowsA], W=[qkv])
                    yield PE([lambda e, kc=kc: e.matmul(H1[:, 0:256], uT[:, kc, :], wqkv[:, kc, 512:768], start=(kc == 0), stop=(kc == 7)) for kc in range(8)],
                             R=[uT, wqkv_b], W=[H1])
                    yield V(lambda e: e.tensor_tensor(out=qkv[:, 512:768], in0=H1[:, 0:256], in1=rowsA[:, RA_BQ + 512:RA_BQ + 768], op=ALU.add), R=[H1, rowsA], W=[qkv])
                    for g in range(4):
                        bank = (H0, H1)[g % 2]
                        fns = []
                        for i in range(4):
                            col = (4 * g + i) * 128
                            for kc in range(8):
                                fns.append(lambda e, i=i, col=col, kc=kc, bank=bank: e.matmul(bank[:, i * 128:(i + 1) * 128], wfm[:, kc, col:col + 128], uT[:, kc, :],
                                                                                             start=(kc == 0), stop=(kc == 7)))
                        yield PE(fns, R=[uT, wfm_b[g]], W=[bank])
                        yield V(lambda e, g=g, bank=bank: e.tensor_tensor(out=stg[:, 4 * g:4 * g + 4, 1:129], in0=v3(bank[:], 128),
                                                                          in1=pfm[:, P_BFM + 4 * g:P_BFM + 4 * g + 4].unsqueeze(2).broadcast_to([128, 4, 128]), op=ALU.add),
                                R=[bank, pfm], W=[stg])
                    if n == 0:
                        yield G(lambda e: e.memset(stg[:, :, 1:113], 0.0), W=[stg])
                    yield G(lambda e: e.tensor_tensor(out=pf[:], in0=stg[:, :, 0:128], in1=stg[:, :, 1:129], op=ALU.subtract), R=[stg], W=[pf], d=3.6)
                    yield G(lambda e: e.tensor_tensor(out=pf[:], in0=pf[:], in1=pfm[:, P_MIX:P_MIX + 16].unsqueeze(2).broadcast_to([128, 16, 128]), op=ALU.mult),
                            R=[pfm], W=[pf], d=3.6)
                    yield G(lambda e: e.tensor_tensor(out=pf[:], in0=pf[:], in1=stg[:, :, 1:129], op=ALU.add), R=[stg], W=[pf], d=3.6)
                    yield G(lambda e: e.tensor_copy(stg[:, :, 0:1], stg[:, :, 128:129]), R=[], W=[stg])

                def attention(n):
                    qkv = qkvs[n % 2]
                    slot = n % 2
                    q10 = qkv[:, 0:640].rearrange("p (h d) -> p h d", d=64)
                    cosb = cst[:, C_ROPE + 16 * n:C_ROPE + 16 * n + 8].unsqueeze(1).broadcast_to([128, 10, 8])
                    sinb = cst[:, C_ROPE + 16 * n + 8:C_ROPE + 16 * n + 16].unsqueeze(1).broadcast_to([128, 10, 8])
                    yield G(lambda e: e.tensor_tensor(out=rtmp[:, 0], in0=q10[:, :, 0:8], in1=cosb, op=ALU.mult), R=[qkv, cst], W=[rtmp])
                    yield G(lambda e: e.tensor_tensor(out=rtmp[:, 1], in0=q10[:, :, 8:16], in1=sinb, op=ALU.mult), R=[qkv, cst], W=[rtmp])
                    yield G(lambda e: e.tensor_tensor(out=rtmp[:, 2], in0=q10[:, :, 8:16], in1=cosb, op=ALU.mult), R=[qkv, cst], W=[rtmp])
                    yield G(lambda e: e.tensor_tensor(out=rtmp[:, 3], in0=q10[:, :, 0:8], in1=sinb, op=ALU.mult), R=[qkv, cst], W=[rtmp])
                    yield G(lambda e: e.tensor_tensor(out=q10[:, :, 0:8], in0=rtmp[:, 0], in1=rtmp[:, 1], op=ALU.subtract), R=[rtmp], W=[qkv])
                    yield G(lambda e: e.tensor_tensor(out=q10[:, :, 8:16], in0=rtmp[:, 2], in1=rtmp[:, 3], op=ALU.add), R=[rtmp], W=[qkv])
                    yield PE([lambda e, c=c: e.transpose(A0[:, c * 128:(c + 1) * 128], qkv[:, c * 128:(c + 1) * 128], cst[:, C_ID:C_ID + 128]) for c in range(4)],
                             R=[qkv, cst], W=[A0])
                    yield PE(lambda e: e.transpose(A1_[:, 0:128], qkv[:, 512:640], cst[:, C_ID:C_ID + 128]), R=[qkv, cst], W=[A1_])
                    yield A(lambda e: e.activation(out=qT[:], in_=v3(A0[:], 128), func=AF.Copy, scale=0.125), R=[A0], W=[qT])
                    yield A(lambda e: e.activation(out=Kbuf[:, slot * 128:(slot + 1) * 128], in_=A1_[:, 0:128], func=AF.Copy), R=[A1_], W=[Kbuf])
                    yield V(lambda e: e.tensor_copy(Vbuf[:, slot, :], qkv[:, 640:768]), R=[qkv], W=[Vbuf])
                    if n == 0:
                        yield A(lambda e: e.activation(out=Kbuf[:, 256:272], in_=A1_[:, 112:128], func=AF.Copy), R=[A1_], W=[Kbuf])
                        yield PE(lambda e: e.matmul(A1_[0:16, 128:256], cst[:, C_ID + 112:C_ID + 128], qkv[:, 640:768], start=True, stop=True), R=[qkv, cst], W=[A1_])
                        yield V(lambda e: e.tensor_copy(Vbuf[0:16, 2, :], A1_[0:16, 128:256]), R=[A1_], W=[Vbuf])
                        return
                    yTn = yTa[n % 2]
                    mvar = 2 if n == 1 else (0 if n % 2 == 0 else 1)
                    mask = cst[:, C_AM + 272 * mvar:C_AM + 272 * (mvar + 1)]
                    for s in range(8):
                        c, j = s // 2, s % 2
                        sl = slice(64 * j, 64 * j + 64)
                        SC = A0
                        Pk = Pb[s % 2]
                        PTk = PT[s % 2]
                        yield PE(lambda e: e.matmul(SC[:, 0:272], qT[sl, c, :], Kbuf[sl, 0:272], start=True, stop=True), R=[qT, Kbuf], W=[SC])
                        yield V(lambda e: e.tensor_tensor(out=SC[:, 0:272], in0=SC[:, 0:272], in1=mask, op=ALU.add), R=[cst], W=[SC])
                        yield V(lambda e: e.tensor_reduce(out=sm[:, 0, s:s + 1], in_=SC[:, 0:272], axis=AX.X, op=ALU.max), R=[SC], W=[sm])
                        yield V(lambda e: e.tensor_scalar(out=sm[:, 1, s:s + 1], in0=sm[:, 0, s:s + 1], scalar1=rowsA[:, RA_SK + s:RA_SK + s + 1], scalar2=-1.0,
                                                          op0=ALU.max, op1=ALU.mult), R=[rowsA], W=[sm])
                        yield A(lambda e: e.activation(out=Pk[:], in_=SC[:, 0:272], func=AF.Exp, bias=sm[:, 1, s:s + 1], scale=1.0,
                                                       accum_out=sm[:, 2, s:s + 1]), R=[SC], W=[Pk, sm])
                        yield A(lambda e: e.activation(out=sm[:, 3, s:s + 1], in_=rowsA[:, RA_SK + s:RA_SK + s + 1], func=AF.Exp, bias=sm[:, 1, s:s + 1], scale=1.0),
                                R=[rowsA], W=[sm])
                        yield PE([lambda e, b=b, nk=nk: e.matmul(A1_[0:nk, b * 128:(b + 1) * 128], Pk[:, b * 128:b * 128 + nk], identb[:], start=True, stop=True)
                                  for b, nk in ((0, 128), (1, 128), (2, 16))], R=[Pk, identb], W=[A1_])
                        yield A(lambda e: e.activation(out=PTk[:, 0:2, :], in_=v3(A1_[:, 0:256], 128), func=AF.Copy), R=[A1_], W=[PTk])
                        yield A(lambda e: e.activation(out=PTk[0:16, 2, :], in_=A1_[0:16, 256:384], func=AF.Copy), R=[A1_], W=[PTk])
                        yield PE([lambda e: e.matmul(A2_[:, s * 64:(s + 1) * 64], PTk[:, 0, :], Vbuf[:, 0, sl], start=True, stop=False),
                                  lambda e: e.matmul(A2_[:, s * 64:(s + 1) * 64], PTk[:, 1, :], Vbuf[:, 1, sl], start=False, stop=False),
                                  lambda e: e.matmul(A2_[:, s * 64:(s + 1) * 64], PTk[0:16, 2, :], Vbuf[0:16, 2, sl], start=False, stop=True)],
                                 R=[PTk, Vbuf], W=[A2_])
                    yield V(lambda e: e.tensor_tensor(out=sm[:, 2, :], in0=sm[:, 2, :], in1=sm[:, 3, :], op=ALU.add), R=[], W=[sm])
                    yield V(lambda e: e.reciprocal(out=sm[:, 4, :], in_=sm[:, 2, :]), R=[], W=[sm])
                    yield V(lambda e: e.tensor_tensor(out=yat[:], in0=v3(A2_[:], 64), in1=sm[:, 4, :].unsqueeze(2).broadcast_to([128, 8, 64]), op=ALU.mult),
                            R=[A2_], W=[yat, sm])
                    yield PE([lambda e, c=c: e.transpose(A1_[:, c * 128:(c + 1) * 128], yat[:, 2 * c:2 * c + 2, :].rearrange("p a d -> p (a d)"), cst[:, C_ID:C_ID + 128])
                              for c in range(4)], R=[yat, cst], W=[A1_])
                    yield A(lambda e: e.activation(out=yTn[:], in_=v3(A1_[:], 128), func=AF.Copy), R=[A1_], W=[yTn])
                    if n == dbg_n:
                        dump("yat", yat[:].rearrange("p s d -> p (s d)"), [yat])
                    yield DMA("sp", ch_y[n % 2], lambda e: e.dma_start(out=yscr[n - 1][:, 0:512], in_=yTn[:].rearrange("p c t -> p (c t)")), R=[yTn], W=[yscr_t[n - 1]])

                def pre(n):
                    pf = pfs[n % 2]
                    pp_ = n % 2
                    arT, NB, NK, Vst, Vst32, BKst, WC, rk, B5 = arTs[pp_], NBs[pp_], NKs[pp_], Vsts[pp_], Vst32s[pp_], BKsts[pp_], WCs[pp_], rks[pp_], B5s[pp_]
                    tri0 = C_TRI0 if n == 0 else C_TRI
                    tri = cst[:, tri0:tri0 + 128]
                    yield A(lambda e: e.activation(out=th[:], in_=pf[0:64, 12, :], func=AF.Tanh), R=[pf], W=[th])
                    yield PE(lambda e: e.matmul(R0[:, 0:512], th[:], w2[:], start=True, stop=True), R=[th, w2], W=[R0])
                    B4f = B4[:].rearrange("p c t -> p (c t)")
                    yield V(lambda e: e.tensor_tensor(out=B4f, in0=R0[:, 0:512], in1=rowsA[:, RA_W0:RA_W0 + 512], op=ALU.add), R=[R0, rowsA], W=[B4])
                    yield A(lambda e: e.activation(out=B4f, in_=B4f, func=AF.Tanh, scale=0.5), R=[], W=[B4])
                    yield V(lambda e: e.tensor_scalar(out=B4f, in0=B4f, scalar1=0.5, scalar2=0.5, op0=ALU.mult, op1=ALU.add), R=[], W=[B4])
                    CB = (R1_, R2)
                    for q in range(2):
                        yield PE([lambda e, c=c, q=q: e.matmul(CB[q][:, c * 128:(c + 1) * 128], B4f[64 * q:64 * q + 64, c * 128:(c + 1) * 128],
                                                               tri[64 * q:64 * q + 64, :], start=True, stop=True) for c in range(4)], R=[B4, cst], W=[CB[q]])

                    def cums(a):
                        return [CB[q][:].rearrange("p (c a t) -> p c a t", c=4, a=2)[:, :, a, :] for q in range(2)]
                    yield PE([lambda e, c=c: e.matmul(R0[:, c * 128:(c + 1) * 128], a2[:, c * 128:(c + 1) * 128], pf[0:64, 13, :], start=True, stop=True) for c in range(4)],
                             R=[pf, a2], W=[R0])
                    for c in range(4):
                        yield A(lambda e, c=c: e.activation(out=B3[:, c, :], in_=R0[:, c * 128:(c + 1) * 128], func=AF.Tanh, bias=hb[:, c:c + 1], scale=0.5),
                                R=[R0, hb], W=[B3])
                    yield V(lambda e: e.tensor_scalar(out=B3[:], in0=B3[:], scalar1=0.5, scalar2=0.5, op0=ALU.mult, op1=ALU.add), R=[], W=[B3])
                    yield A(lambda e: e.activation(out=sgd[:], in_=pf[:, 14:16, :], func=AF.Tanh, scale=0.5), R=[pf], W=[sgd])
                    yield V(lambda e: e.tensor_scalar(out=sgd[:], in0=sgd[:], scalar1=0.5, scalar2=0.5, op0=ALU.mult, op1=ALU.add), R=[], W=[sgd])
                    fns = []
                    for c in range(4):
                        fns.append(lambda e, c=c: e.matmul(R0[:, c * 128:(c + 1) * 128], g2[:, 0, c * 128:(c + 1) * 128], sgd[:, 0, :], start=True, stop=False))
                        fns.append(lambda e, c=c: e.matmul(R0[:, c * 128:(c + 1) * 128], g2[0:32, 1, c * 128:(c + 1) * 128], sgd[0:32, 1, :], start=False, stop=True))
                    yield PE(fns, R=[sgd, g2], W=[R0])
                    yield A(lambda e: e.activation(out=B5[:].rearrange("p c t -> p (c t)"), in_=R0[:], func=AF.Copy), R=[R0], W=[B5])
                    kview = pf[:, 4:8, :]
                    rview = pf[:, 0:4, :]

                    def bc(col):
                        return pfm[:, col:col + 4].unsqueeze(2).broadcast_to([128, 4, 128])
                    yield V(lambda e: e.tensor_tensor(out=B1[:], in0=kview, in1=bc(P_KK), op=ALU.mult), R=[pf, pfm], W=[B1])
                    yield V(lambda e: e.tensor_tensor(out=B2[:], in0=B1[:], in1=B1[:], op=ALU.mult), R=[B1], W=[B2])
                    yield PE([lambda e, c=c: e.matmul(R0[:, c * 128:(c + 1) * 128], cst[:, C_OBD:C_OBD + 128], B2[:, c, :], start=True, stop=True) for c in range(4)],
                             R=[B2, cst], W=[R0])
                    yield A(lambda e: e.activation(out=B2[:].rearrange("p c t -> p (c t)"), in_=R0[:], func=AF.Sqrt), R=[R0], W=[B2])
                    yield V(lambda e: e.tensor_scalar(out=B2[:], in0=B2[:], scalar1=1e-12, scalar2=None, op0=ALU.max), R=[], W=[B2])
                    yield V(lambda e: e.reciprocal(out=B2[:], in_=B2[:]), R=[], W=[B2])
                    yield V(lambda e: e.tensor_tensor(out=B1[:], in0=B1[:], in1=B2[:], op=ALU.mult), R=[B2], W=[B1])
                    yield V(lambda e: e.scalar_tensor_tensor(out=B2[:], in0=B3[:], scalar=-1.0, in1=bc(P_KA), op0=ALU.add, op1=ALU.mult), R=[B3, pfm], W=[B2])
                    yield V(lambda e: e.scalar_tensor_tensor(out=kview, in0=B2[:], scalar=1.0, in1=kview, op0=ALU.add, op1=ALU.mult), R=[B2], W=[pf])
                    yield V(lambda e: e.tensor_tensor(out=B3[:], in0=B1[:], in1=B3[:], op=ALU.mult), R=[B1], W=[B3])
                    cex = cums(1)
                    cin = cums(0)
                    B4q = cq(B4[:])
                    for hh in range(2):
                        yield A(lambda e, hh=hh: e.activation(out=B4[:, :, 64 * hh:64 * hh + 64], in_=cex[hh], func=AF.Exp), R=[CB[hh]], W=[B4])
                    yield V(lambda e: e.scalar_tensor_tensor(out=arT[:, :, :, 0, :], in0=cq(B1[:]), scalar=-1.0, in1=B4q, op0=ALU.mult, op1=ALU.mult), R=[B1, B4], W=[arT])
                    for hh in range(2):
                        yield A(lambda e, hh=hh: e.activation(out=B4[:, :, 64 * hh:64 * hh + 64], in_=cin[hh], func=AF.Exp), R=[CB[hh]], W=[B4])
                    yield V(lambda e: e.tensor_tensor(out=arT[:, :, :, 1, :], in0=cq(rview), in1=B4q, op=ALU.mult), R=[pf, B4], W=[arT])
                    for hh in range(2):
                        yield A(lambda e, hh=hh: e.activation(out=B4[:, :, 64 * hh:64 * hh + 64], in_=cin[hh], func=AF.Exp, scale=-1.0), R=[CB[hh]], W=[B4])
                    yield V(lambda e: e.tensor_tensor(out=BT[:], in0=B3[:], in1=B4[:], op=ALU.mult), R=[B3, B4], W=[BT])
                    yield V(lambda e: e.tensor_tensor(out=KT[:], in0=kview, in1=B4[:], op=ALU.mult), R=[pf, B4], W=[KT])
                    for hh in range(2):
                        yield V(lambda e, hh=hh: e.tensor_copy(cumC[:, :, hh], cin[hh][:, :, 63]), R=[CB[hh]], W=[cumC])
                    for c in range(4):
                        for q in range(2):
                            yield A(lambda e, c=c, q=q: e.activation(out=B4[:, c, 64 * q:64 * q + 64], in_=cin[q][:, c, :], func=AF.Exp, scale=-1.0,
                                                                     bias=cumC[:, c, q:q + 1]), R=[CB[q], cumC], W=[B4])
                    yield V(lambda e: e.tensor_tensor(out=BH[:], in0=B3[:], in1=B4[:], op=ALU.mult), R=[B3, B4], W=[BH])
                    yield V(lambda e: e.tensor_tensor(out=KH[:], in0=kview, in1=B4[:], op=ALU.mult), R=[pf, B4], W=[KH])
                    yield A(lambda e: e.activation(out=WC[:], in_=cumC[:], func=AF.Exp), R=[cumC], W=[WC])

                    mb = cst[:, C_MB:C_MB + 128].rearrange("p (a t) -> p a t", a=2).unsqueeze(1).broadcast_to([128, 4, 2, 64])
                    ml8 = cst[:, C_ML:C_ML + 64].unsqueeze(1).broadcast_to([128, 8, 64])
                    i8 = cst[:, C_I64:C_I64 + 64].unsqueeze(1).broadcast_to([128, 8, 64])
                    Q2 = (0, 1)

                    def tq(q):
                        return slice(64 * q, 64 * q + 64)
                    yield PE(hl(lambda q, c, sl, j: [lambda e: e.matmul(R0[sl, q * 256 + c * 64:q * 256 + (c + 1) * 64], arT[sl, c, q, 0, :], BT[sl, c, tq(q)], start=True, stop=True)], Q2),
                             R=[arT, BT], W=[R0])
                    for q in Q2:
                        yield PE(hl(lambda q, c, sl, j: [lambda e: e.matmul(CB[q][sl, c * 128:(c + 1) * 128], BT[sl, c, tq(q)], arT[sl, c, q, :, :].rearrange("p a t -> p (a t)"),
                                                                            start=True, stop=True)], (q,)), R=[arT, BT], W=[CB[q]])
                    yield V(lambda e: e.tensor_tensor(out=Ast[0][:].rearrange("p q c t -> p (q c) t"), in0=v3(R0[:]), in1=ml8, op=ALU.mult), R=[R0, cst], W=[Ast[0]])
                    for q in Q2:
                        yield V(lambda e, q=q: e.tensor_tensor(out=NB[:, q], in0=CB[q][:].rearrange("p (c a t) -> p c a t", c=4, a=2), in1=mb, op=ALU.mult), R=[CB[q], cst], W=[NB])
                    for q in Q2:
                        yield PE(hl(lambda q, c, sl, j: [lambda e: e.matmul(CB[q][sl, c * 128:(c + 1) * 128], KT[sl, c, tq(q)], arT[sl, c, q, :, :].rearrange("p a t -> p (a t)"),
                                                                            start=True, stop=True)], (q,)), R=[arT, KT], W=[CB[q]])
                    for q in Q2:
                        yield V(lambda e, q=q: e.tensor_tensor(out=NK[:, q], in0=CB[q][:].rearrange("p (c a t) -> p c a t", c=4, a=2), in1=mb, op=ALU.mult), R=[CB[q], cst], W=[NK])
                    yield G(lambda e: e.tensor_copy(Nst[0][:], NB[:, :, :, 0, :]), R=[NB], W=[Nst[0]])
                    yield G(lambda e: e.tensor_tensor(out=Pc[0][:].rearrange("p q c t -> p (q c) t"), in0=Nst[0][:].rearrange("p q c t -> p (q c) t"), in1=i8, op=ALU.add),
                            R=[Nst[0], cst], W=[Pc[0]])
                    yield PE(hl(lambda q, c, sl, j: [lambda e: e.matmul(R0[sl, q * 256 + c * 64:q * 256 + (c + 1) * 64], pf[sl, 8 + c, tq(q)], idsl(sl, j), start=True, stop=True)], Q2),
                             R=[pf, cst], W=[R0])
                    yield A(lambda e: e.activation(out=Vst32[:], in_=v4(R0[:]), func=AF.Copy), R=[R0], W=[Vst32])
                    if MD != F32:
                        yield V(lambda e: e.tensor_copy(Vst[:], v4(R0[:])), R=[R0], W=[Vst])
                    for q in Q2:
                        yield PE(hl(lambda q, c, sl, j: [lambda e: e.matmul(CB[q][sl, c * 64:(c + 1) * 64], BH[sl, c, tq(q)], idm_sl(sl, j), start=True, stop=True),
                                                         lambda e: e.matmul(CB[q][sl, 256 + c * 64:256 + (c + 1) * 64], KH[sl, c, tq(q)], idm_sl(sl, j), start=True, stop=True)], (q,)),
                                 R=[BH, KH, idm], W=[CB[q]])
                    for q in Q2:
                        yield A(lambda e, q=q: e.activation(out=BKst[:, q], in_=CB[q][:].rearrange("p (a c t) -> p a c t", a=2, c=4), func=AF.Copy), R=[CB[q]], W=[BKst])
                    cur = 0

                    def sq_fns(cur, lvl):
                        f = hl(lambda q, c, sl, j: [lambda e: e.matmul(R0[sl, q * 256 + c * 64:q * 256 + (c + 1) * 64], Nst[cur][sl, q, c, :], Ast[cur][sl, q, c, :], start=True, stop=True)], Q2)
                        if lvl < 5:
                            f += hl(lambda q, c, sl, j: [lambda e: e.matmul(R1_[sl, q * 256 + c * 64:q * 256 + (c + 1) * 64], Ast[cur][sl, q, c, :], Nst[cur][sl, q, c, :], start=True, stop=True)], Q2)
                        return f

                    def pp_fns(a_t, pc):
                        return hl(lambda q, c, sl, j: [lambda e: e.matmul(R2[sl, q * 256 + c * 64:q * 256 + (c + 1) * 64], a_t[sl, q, c, :], pc[sl, q, c, :], start=True, stop=True)], Q2)

                    yield PE(sq_fns(0, 1), R=[Nst[0], Ast[0]], W=[R0, R1_])
                    for lvl in range(1, 6):
                        nxt = 1 - cur
                        yield A(lambda e, nxt=nxt: e.activation(out=Ast[nxt][:], in_=v4(R0[:]), func=AF.Copy), R=[R0], W=[Ast[nxt]])
                        if lvl < 5:
                            yield V(lambda e, nxt=nxt: e.tensor_copy(Nst[nxt][:], v4(R1_[:])), R=[R1_], W=[Nst[nxt]])
                        pc, pn = Pc[(lvl - 1) % 2], (Pc[lvl % 2] if lvl < 5 else TTs[pp_])
                        fns = pp_fns(Ast[nxt], pc)
                        Wl = [R2]
                        Rl = [Ast[nxt], pc]
                        if lvl < 5:
                            fns += sq_fns(nxt, lvl + 1)
                            Wl += [R0] + ([R1_] if lvl + 1 < 5 else [])
                            Rl += [Nst[nxt]]
                        yield PE(fns, R=Rl, W=Wl)
                        yield V(lambda e, pc=pc, pn=pn: e.tensor_tensor(out=pn[:], in0=v4(R2[:]), in1=pc[:], op=ALU.add), R=[R2, pc], W=[pn])
                        cur = nxt
                    yield G(lambda e: e.tensor_tensor(out=B2[:], in0=rview, in1=kview, op=ALU.mult), R=[pf], W=[B2])
                    yield G(lambda e: e.tensor_tensor(out=B2[:], in0=B2[:], in1=bc(P_RK), op=ALU.mult), R=[pfm], W=[B2])
                    fns = []
                    for q in range(2):
                        for c in range(4):
                            for j in range(2):
                                sl = slice(64 * j, 64 * j + 64)
                                fns.append(lambda e, q=q, c=c, sl=sl: e.matmul(R0[sl, (q * 4 + c) * 2:(q * 4 + c) * 2 + 2], B2[sl, c, 64 * q:64 * q + 64], ones2[sl, :],
                                                                              start=True, stop=True))
                    yield PE(fns, R=[B2, ones2], W=[R0])
                    yield A(lambda e: e.activation(out=rk[:].rearrange("p q c -> p (q c)"), in_=R0[:, 0:16].rearrange("p (x two) -> p x two", two=2)[:, :, 0], func=AF.Copy), R=[R0], W=[rk])
                    return

                def seq(n):
                    pp_ = n % 2
                    arT, NB, NK, Vst, Vst32, BKst, WC, rk, B5 = arTs[pp_], NBs[pp_], NKs[pp_], Vsts[pp_], Vst32s[pp_], BKsts[pp_], WCs[pp_], rks[pp_], B5s[pp_]
                    TT = TTs[pp_]
                    Q2 = (0, 1)
                    for q in Q2:
                        yield PE(hl(lambda q, c, sl, j: [lambda e: e.matmul(Q0[sl, c * 64:(c + 1) * 64], arT[sl, c, q, 0, :], STm[sl, c, :], start=True, stop=False),
                                                         lambda e: e.matmul(Q0[sl, c * 64:(c + 1) * 64], NK[sl, q, c, 0, :], Vst[sl, q, c, :], start=False, stop=True)], (q,)),
                                 R=[arT, STm, NK, Vst], W=[Q0])
                        yield A(lambda e: e.activation(out=R1[:], in_=v3(Q0[:, 0:256]), func=AF.Copy), R=[Q0], W=[R1])
                        yield PE(hl(lambda q, c, sl, j: [lambda e: e.matmul(Q0[sl, 256 + c * 64:256 + (c + 1) * 64], TT[sl, q, c, :], R1[sl, c, :], start=True, stop=True)], (q,)),
                                 R=[TT, R1], W=[Q0])
                        yield A(lambda e: e.activation(out=Ust[:], in_=v3(Q0[:, 256:512]), func=AF.Copy), R=[Q0], W=[Ust])
                        if n >= 1:
                            yield PE(hl(lambda q, c, sl, j: [lambda e: e.matmul(Q0[sl, c * 64:(c + 1) * 64], arT[sl, c, q, 1, :], STm[sl, c, :], start=True, stop=False),
                                                             lambda e: e.matmul(Q0[sl, c * 64:(c + 1) * 64], NB[sl, q, c, 1, :], Ust[sl, c, :], start=False, stop=False),
                                                             lambda e: e.matmul(Q0[sl, c * 64:(c + 1) * 64], NK[sl, q, c, 1, :], Vst[sl, q, c, :], start=False, stop=True)], (q,)),
                                     R=[arT, STm, NB, Ust, NK, Vst], W=[Q0])
                            yield V(lambda e, q=q: e.tensor_copy(Yst[:, q], v3(Q0[:, 0:256])), R=[Q0], W=[Yst])
                        yield PE(hl(lambda q, c, sl, j: [lambda e: e.matmul(Q0[sl, 256 + c * 64:256 + (c + 1) * 64], BKst[sl, q, 0, c, :], Ust[sl, c, :], start=True, stop=False),
                                                         lambda e: e.matmul(Q0[sl, 256 + c * 64:256 + (c + 1) * 64], BKst[sl, q, 1, c, :], Vst[sl, q, c, :], start=False, stop=True)], (q,)),
                                 R=[BKst, Ust, Vst], W=[Q0])
                        yield G(lambda e, q=q: e.tensor_tensor(out=STt[:], in0=ST32[:], in1=WC[:, :, q:q + 1].broadcast_to([128, 4, 64]), op=ALU.mult), R=[WC, ST32], W=[STt])
                        if MD != F32:
                            yield V(lambda e: e.tensor_tensor(out=STm[:], in0=STt[:], in1=v3(Q0[:, 256:512]), op=ALU.add), R=[Q0, STt], W=[STm])
                        yield V(lambda e: e.tensor_tensor(out=ST32[:], in0=STt[:], in1=v3(Q0[:, 256:512]), op=ALU.add), R=[Q0, STt], W=[ST32])
                    if n == 0:
                        return
                    Y8 = Yst[:].rearrange("p q c v -> p (q c) v")
                    yc8 = yc[:].rearrange("p q c v -> p (q c) v")
                    ysq8 = ysq[:].rearrange("p q c v -> p (q c) v")
                    V32_8 = Vst32[:].rearrange("p q c v -> p (q c) v")

                    def b8(ap):
                        return ap.unsqueeze(2).broadcast_to([128, 8, 64])
                    yield V(lambda e: e.tensor_reduce(out=gst[:, 0, :], in_=Y8, axis=AX.X, op=ALU.add), R=[Yst], W=[gst])
                    yield V(lambda e: e.tensor_scalar(out=gst[:, 1, :], in0=gst[:, 0, :], scalar1=-1.0 / 64, scalar2=None, op0=ALU.mult), R=[], W=[gst])
                    yield V(lambda e: e.tensor_tensor(out=yc8, in0=Y8, in1=b8(gst[:, 1, :]), op=ALU.add), R=[Yst], W=[yc, gst])
                    yield G(lambda e: e.tensor_tensor(out=ysq8, in0=yc8, in1=yc8, op=ALU.mult), R=[yc], W=[ysq])
                    yield V(lambda e: e.tensor_reduce(out=gst[:, 2, :], in_=ysq8, axis=AX.X, op=ALU.add), R=[ysq], W=[gst])
                    yield V(lambda e: e.tensor_scalar(out=gst[:, 3, :], in0=gst[:, 2, :], scalar1=1.0 / 64, scalar2=LN_EPS, op0=ALU.mult, op1=ALU.add), R=[], W=[gst])
                    yield G(lambda e: e.tensor_tensor(out=gst[:, 4, :], in0=gst[:, 3, :], in1=cst[:, CE + 4:CE + 12], op=ALU.pow), R=[cst], W=[gst])
                    yield V(lambda e: e.tensor_tensor(out=yc8, in0=yc8, in1=b8(gst[:, 4, :]), op=ALU.mult), R=[], W=[yc, gst])
                    yield G(lambda e: e.tensor_tensor(out=yc[:], in0=yc[:], in1=lnst[:, 0].unsqueeze(1).broadcast_to([128, 2, 4, 64]), op=ALU.mult), R=[lnst], W=[yc])
                    yield G(lambda e: e.tensor_tensor(out=yc[:], in0=yc[:], in1=lnst[:, 1].unsqueeze(1).broadcast_to([128, 2, 4, 64]), op=ALU.add), R=[lnst], W=[yc])
                    yield V(lambda e: e.tensor_tensor(out=ysq8, in0=V32_8, in1=b8(rk[:].rearrange("p q c -> p (q c)")), op=ALU.mult), R=[Vst32, rk], W=[ysq])
                    yield V(lambda e: e.tensor_tensor(out=yc8, in0=yc8, in1=ysq8, op=ALU.add), R=[ysq], W=[yc])
                    yield PE(hl(lambda q, c, sl, j: [lambda e: e.matmul(Q0[sl, q * 256 + c * 64:q * 256 + (c + 1) * 64], yc[sl, q, c, :], idsl(sl, j), start=True, stop=True)], Q2),
                             R=[yc, cst], W=[Q0])
                    yTn = yTr[n % 2]
                    yield V(lambda e: e.tensor_tensor(out=qc(yTn[:]), in0=v4(Q0[:]), in1=qc(B5[:]), op=ALU.mult), R=[Q0, B5], W=[yTn])
                    yield DMA("sp", ch_st[n % 2], lambda e: e.dma_start(out=yscr[n - 1][:, 512:1024], in_=yTn[:].rearrange("p c t -> p (c t)")), R=[yTn], W=[yscr_t[n - 1]])

                import os as _os
                W_PRE, W_ATT, W_SEQ, W_HEAD = [int(v) for v in _os.environ.get("KW", "3,3,1,1").split(",")]
                run(head(0))
                if nt > 1:
                    run(par([pre(0), attention(0), head(1)], [W_PRE, W_ATT, W_HEAD]))
                else:
                    run(par([pre(0), attention(0)], [W_PRE, W_ATT]))
                _skip = _os.environ.get("KSKIP", "")
                B_SEQ, B_PRE, B_ATT, B_HEAD = [float(v) for v in _os.environ.get("KB", "0,0,0,0").split(",")]
                for i in range(nt):
                    streams = [seq(i)] if "seq" not in _skip else []
                    bon = [B_SEQ] if "seq" not in _skip else []
                    if i + 1 < nt:
                        if "pre" not in _skip:
                            streams += [pre(i + 1)]
                            bon += [B_PRE]
                        if "att" not in _skip:
                            streams += [attention(i + 1)]
                            bon += [B_ATT]
                    if i + 2 < nt:
                        streams.append(head(i + 2))
                        bon.append(B_HEAD)
                    run(streams, bon)
                S.barrier()

        es_bw = ExitStack()
        pre_w = {}
        if "B" in phases:
            sbw_ = mk_alloc(es_bw, "bw_")
            pre_w["wg"] = sbw_("wg", [128, 8, DFF], BF16)
            pre_w["wu"] = sbw_("wu", [128, 8, DFF], BF16)

        def load_bw():
            wg, wu = pre_w["wg"], pre_w["wu"]
            ngrp_ = (NFC + 3) // 4
            wg_b = []
            wu_b = []
            for g in range(ngrp_):
                c0, c1 = 512 * g, min(512 * (g + 1), DFF)
                wg_b.append(wload_blk(wg, wg[:, :, c0:c1], w_fg.rearrange("(c p) n -> p c n", p=128)[:, :, c0:c1]))
                wu_b.append(wload_blk(wu, wu[:, :, c0:c1], w_fu.rearrange("(c p) n -> p c n", p=128)[:, :, c0:c1]))
            pre_w["wg_b"] = wg_b
            pre_w["wu_b"] = wu_b

        if "A2" in phases:
            with ExitStack() as es:
                sb = mk_alloc(es, "a2_")
                CE = 128
                cst, pfm = load_consts(sb, 128)
                wgate = sb("wgate", [128, 8, 2048], BF16)
                wba = sb("wba", [128, 4, D], BF16)
                wbr = sb("wbr", [128, 4, D], BF16)
                wgate_b = [wload_blk(wgate, wgate[:, :, 512 * g:512 * (g + 1)], w_gate.rearrange("(c p) n -> p c n", p=128)[:, :, 512 * g:512 * (g + 1)]) for g in range(4)]
                wba_b = wload_blk(wba, wba[:], w_ba.rearrange("(c p) n -> p c n", p=128))
                wbr_b = wload_blk(wbr, wbr[:], w_br.rearrange("(c p) n -> p c n", p=128))
                S.finalize(ch_w, [cst, pfm])
                if "B" in phases:
                    load_bw()
                PS = [Tile(es.enter_context(nc.psum_tensor("psb%d" % i, [128, 512], F32)), "psb%d" % i, excl=True) for i in range(8)]
                xb = [sb("xb0", [128, D]), sb("xb1", [128, D])]
                yT = [sb("yT%d" % i, [128, 8, 128], BF16) for i in range(3)]
                xs = sb("xs", [128, D])
                st4 = sb("st4", [128, 4])
                uTs = [sb("uT0", [128, 8, 128], BF16), sb("uT1", [128, 8, 128], BF16)]
                sgs = [sb("sg0", [128, 16, 128]), sb("sg1", [128, 16, 128])]
                hbg = sb("hbg", [128, 16])
                V(lambda e: e.tensor_scalar(out=hbg[:], in0=pfm[:, P_BG:P_BG + 16], scalar1=0.5, scalar2=None, op0=ALU.mult), R=[pfm], W=[hbg])
                t1 = sb("t1", [128, 8, 128])
                t2 = sb("t2", [128, 8, 128])
                mT = [sb("mT0", [128, 8, 128], BF16), sb("mT1", [128, 8, 128], BF16)]

                def front2(n):
                    xt = xb[n % 2]
                    yTn = yT[n % 3]
                    yield DMA("sp", ch_x[n % 2], lambda e: e.dma_start(out=xt[:], in_=xe[n * 128:(n + 1) * 128, :]), W=[xt])
                    yield DMA("sp", ch_y[n % 2], lambda e: e.dma_start(out=yTn[:].rearrange("p c t -> p (c t)"), in_=yscr[n - 1]), R=[yscr_t[n - 1]], W=[yTn])
                    yield from norm_T(xt, xs, st4, cst, CE, pfm, P_GMIX, [PS[0], PS[1]], uTs[n % 2])

                def mid2(n):
                    uT = uTs[n % 2]
                    sg = sgs[n % 2]
                    for g in range(4):
                        bank = PS[2 + (g % 2)]
                        fns = []
                        for i in range(4):
                            col = (4 * g + i) * 128
                            for kc in range(8):
                                fns.append(lambda e, i=i, col=col, kc=kc, bank=bank: e.matmul(bank[:, i * 128:(i + 1) * 128], wgate[:, kc, col:col + 128], uT[:, kc, :],
                                                                                             start=(kc == 0), stop=(kc == 7)))
                        yield PE(fns, R=[uT, wgate_b[g]], W=[bank])
                        for i in range(4):
                            yield A(lambda e, g=g, i=i, bank=bank: e.activation(out=sg[:, 4 * g + i, :], in_=bank[:, i * 128:(i + 1) * 128], func=AF.Tanh,
                                                                                bias=hbg[:, 4 * g + i:4 * g + i + 1], scale=0.5), R=[bank, hbg], W=[sg])

                def tail2(n):
                    yTn = yT[n % 3]
                    mTn = mT[n % 2]
                    sg = sgs[n % 2]
                    for br, (wb, off, wb_b) in enumerate(((wba, 0, wba_b), (wbr, 4, wbr_b))):
                        for hh in range(2):
                            bank = PS[4 + 2 * br + hh]
                            fns = []
                            for i in range(4):
                                fc = 4 * hh + i
                                for kc in range(4):
                                    fns.append(lambda e, i=i, fc=fc, kc=kc, bank=bank, wb=wb, off=off: e.matmul(bank[:, i * 128:(i + 1) * 128], wb[:, kc, fc * 128:(fc + 1) * 128],
                                                                                                                yTn[:, off + kc, :], start=(kc == 0), stop=(kc == 3)))
                            yield PE(fns, R=[yTn, wb_b], W=[bank])
                    for hh in range(2):
                        yield V(lambda e, hh=hh: e.scalar_tensor_tensor(out=t1[:, 4 * hh:4 * hh + 4, :], in0=sg[:, 4 * hh:4 * hh + 4, :], scalar=1.0,
                                                                        in1=PS[4 + hh][:].rearrange("p (c t) -> p c t", c=4), op0=ALU.add, op1=ALU.mult),
                                R=[PS[4 + hh], sg], W=[t1])
                        yield V(lambda e, hh=hh: e.scalar_tensor_tensor(out=t2[:, 4 * hh:4 * hh + 4, :], in0=sg[:, 8 + 4 * hh:8 + 4 * hh + 4, :], scalar=1.0,
                                                                        in1=PS[6 + hh][:].rearrange("p (c t) -> p c t", c=4), op0=ALU.add, op1=ALU.mult),
                                R=[PS[6 + hh], sg], W=[t2])
                    yield G(lambda e: e.tensor_tensor(out=t1[:], in0=t1[:], in1=t2[:], op=ALU.add), R=[t2], W=[t1])
                    yield A(lambda e: e.activation(out=mTn[:], in_=t1[:], func=AF.Copy, scale=0.5), R=[t1], W=[mTn])
                    yield DMA("sp", ch_st[n % 2], lambda e: e.dma_start(out=mscr[n - 1], in_=mTn[:].rearrange("p c t -> p (c t)")), R=[mTn], W=[mscr_t[n - 1]])

                if nt > 1:
                    run(front2(1))
                if nt > 2:
                    run([mid2(1), front2(2)])
                elif nt > 1:
                    run(mid2(1))
                for n in range(1, nt):
                    streams = [tail2(n)]
                    if n + 1 < nt:
                        streams.append(mid2(n + 1))
                    if n + 2 < nt:
                        streams.append(front2(n + 2))
                    run(streams)
                S.barrier()

        if "B" in phases:
            with ExitStack() as es:
                sb = mk_alloc(es, "b_")
                CE = 128
                cst, pfm = load_consts(sb, 128)
                gfin = sb("gfin", [128, D])
                S.dma("sp", ch_w, lambda e: e.dma_start(out=gfin[:], in_=gfind.broadcast_to([128, D])), W=[gfin])
                wo = sb("wo", [128, 8, D], BF16)
                wg, wu = pre_w["wg"], pre_w["wu"]
                wd = sb("wd", [128, NFC, D], BF16)
                wo_b = wload_blk(wo, wo[:], w_o.rearrange("(c p) n -> p c n", p=128))
                if "wg_b" not in pre_w:
                    load_bw()
                wg_b, wu_b = pre_w["wg_b"], pre_w["wu_b"]
                wd_b = [wload_blk(wd, wd[:, 11 * hh:11 * hh + 11, :], w_fd.rearrange("(c p) n -> p c n", p=128)[:, 11 * hh:11 * hh + 11, :]) for hh in range(2)]
                S.finalize(ch_w, [cst, pfm, gfin])
                PS = [Tile(es.enter_context(nc.psum_tensor("psc%d" % i, [128, 512], F32)), "psc%d" % i, excl=True) for i in range(8)]
                xb = [sb("xb0", [128, D]), sb("xb1", [128, D])]
                mT = [sb("mT0", [128, 8, 128], BF16), sb("mT1", [128, 8, 128], BF16)]
                h1s = [sb("h1a", [128, D]), sb("h1b", [128, D]), sb("h1c", [128, D])]
                xsF = sb("xsF", [128, D])
                xsB = [sb("xsB0", [128, D]), sb("xsB1", [128, D])]
                st4 = sb("st4", [128, 4])
                st4b = sb("st4b", [128, 4])
                fTs = [sb("fT0", [128, 8, 128], BF16), sb("fT1", [128, 8, 128], BF16)]
                sl_ = sb("silu", [128, 4, 128])
                aTs = [sb("aT0", [128, NFC, 128], BF16), sb("aT1", [128, NFC, 128], BF16)]

                def front3(n):
                    xt = xb[n % 2]
                    mTn = mT[n % 2]
                    h1 = h1s[n % 3]
                    yield DMA("sp", ch_x[n % 2], lambda e: e.dma_start(out=xt[:], in_=xe[n * 128:(n + 1) * 128, :]), W=[xt])
                    yield DMA("sp", ch_y[n % 2], lambda e: e.dma_start(out=mTn[:].rearrange("p c t -> p (c t)"), in_=mscr[n - 1]), R=[mscr_t[n - 1]], W=[mTn])
                    for hh in range(2):
                        yield PE([lambda e, kc=kc, hh=hh: e.matmul(PS[0][:], mTn[:, kc, :], wo[:, kc, hh * 512:(hh + 1) * 512], start=(kc == 0), stop=(kc == 7)) for kc in range(8)],
                                 R=[mTn, wo_b], W=[PS[0]])
                        yield V(lambda e, hh=hh: e.tensor_tensor(out=h1[:, hh * 512:(hh + 1) * 512], in0=PS[0][:], in1=xt[:, hh * 512:(hh + 1) * 512], op=ALU.add),
                                R=[PS[0], xt], W=[h1])
                    yield from norm_T(h1, xsF, st4, cst, CE, pfm, P_GFFN, [PS[1], PS[1]], fTs[n % 2])

                def mid3(n):
                    fT = fTs[n % 2]
                    aT = aTs[n % 2]
                    ngrp = (NFC + 3) // 4
                    for g in range(ngrp):
                        nchunk = min(4, NFC - 4 * g)
                        bg = PS[2 + 2 * (g % 2)]
                        bu = PS[3 + 2 * (g % 2)]
                        for bank, wt, wt_b in ((bg, wg, wg_b[g]), (bu, wu, wu_b[g])):
                            fns = []
                            for i in range(nchunk):
                                fc = 4 * g + i
                                for kc in range(8):
                                    fns.append(lambda e, i=i, fc=fc, kc=kc, bank=bank, wt=wt: e.matmul(bank[:, i * 128:(i + 1) * 128], wt[:, kc, fc * 128:(fc + 1) * 128], fT[:, kc, :],
                                                                                                       start=(kc == 0), stop=(kc == 7)))
                            yield PE(fns, R=[fT, wt_b], W=[bank])
                        yield A(lambda e: e.activation(out=sl_[:, 0:nchunk, :], in_=bg[:, 0:nchunk * 128].rearrange("p (c t) -> p c t", c=nchunk), func=AF.Tanh, scale=0.5),
                                R=[bg], W=[sl_])
                        yield V(lambda e: e.scalar_tensor_tensor(out=sl_[:, 0:nchunk, :], in0=sl_[:, 0:nchunk, :], scalar=1.0,
                                                                 in1=bg[:, 0:nchunk * 128].rearrange("p (c t) -> p c t", c=nchunk), op0=ALU.add, op1=ALU.mult), R=[bg], W=[sl_])
                        yield V(lambda e: e.scalar_tensor_tensor(out=aT[:, 4 * g:4 * g + nchunk, :], in0=sl_[:, 0:nchunk, :], scalar=0.5,
                                                                 in1=bu[:, 0:nchunk * 128].rearrange("p (c t) -> p c t", c=nchunk), op0=ALU.mult, op1=ALU.mult), R=[bu, sl_], W=[aT])

                def tail3(n):
                    h1 = h1s[n % 3]
                    aT = aTs[n % 2]
                    o = xsB[n % 2]
                    for hh in range(2):
                        yield PE([lambda e, fc=fc, hh=hh: e.matmul(PS[6 + hh][:], aT[:, fc, :], wd[:, fc, hh * 512:(hh + 1) * 512], start=(fc == 0), stop=(fc == NFC - 1)) for fc in range(NFC)],
                                 R=[aT] + wd_b, W=[PS[6 + hh]])
                        yield V(lambda e, hh=hh: e.tensor_tensor(out=h1[:, hh * 512:(hh + 1) * 512], in0=PS[6 + hh][:], in1=h1[:, hh * 512:(hh + 1) * 512], op=ALU.add),
                                R=[PS[6 + hh]], W=[h1])
                    yield A(lambda e: e.activation(out=o[:], in_=h1[:], func=AF.Square, accum_out=st4b[:, 0:1]), R=[h1], W=[o, st4b])
                    yield V(lambda e: e.tensor_scalar(out=st4b[:, 1:2], in0=st4b[:, 0:1], scalar1=1.0 / D, scalar2=RMS_EPS, op0=ALU.mult, op1=ALU.add), R=[], W=[st4b])
                    yield G(lambda e: e.tensor_tensor(out=st4b[:, 2:3], in0=st4b[:, 1:2], in1=cst[:, CE + 4:CE + 5], op=ALU.pow), R=[cst], W=[st4b])
                    yield A(lambda e: e.activation(out=o[:], in_=h1[:], func=AF.Identity, scale=st4b[:, 2:3], bias=cst[:, CE + 2:CE + 3]), R=[h1, cst], W=[o, st4b])
                    yield G(lambda e: e.tensor_tensor(out=o[:], in0=o[:], in1=gfin[:], op=ALU.mult), R=[gfin], W=[o])
                    yield DMA("sp", ch_st[n % 2], lambda e: e.dma_start(out=outd[(n - 1) * 128:n * 128, :], in_=o[:]), R=[o], W=[])

                if nt > 1:
                    run(front3(1))
                if nt > 2:
                    run([mid3(1), front3(2)])
                elif nt > 1:
                    run(mid3(1))
                for n in range(1, nt):
                    streams = [tail3(n)]
                    if n + 1 < nt:
                        streams.append(mid3(n + 1))
                    if n + 2 < nt:
                        streams.append(front3(n + 2))
                    run(streams)
                S.barrier()
        else:
            S.barrier()
        es_bw.close()
    return nc


QPERM = [0, 4, 1, 5, 2, 6, 3, 7]


def make_consts():
    c = np.zeros((128, C_END), np.float32)
    c[:, C_ID:C_ID + 128] = np.eye(128, dtype=np.float32)
    s = np.arange(64)
    for j in range(2):
        rows = slice(64 * j, 64 * j + 64)
        c[rows, C_MB:C_MB + 64] = (s[None, :] > s[:, None])
        c[rows, C_MB + 64:C_MB + 128] = (s[None, :] >= s[:, None])
        c[rows, C_ML:C_ML + 64] = (s[None, :] < s[:, None])
        c[rows, C_I64:C_I64 + 64] = np.eye(64)
        c[rows, C_OBD + 64 * j:C_OBD + 64 * j + 64] = 1.0
        c[rows, C_TRI:C_TRI + 64] = CFAC * (s[:, None] <= s[None, :])
        c[rows, C_TRI + 64:C_TRI + 128] = CFAC * (s[:, None] < s[None, :])
    c[64:128, C_TRI0:C_TRI0 + 128] = c[64:128, C_TRI:C_TRI + 128]
    c[64:64 + 48, C_TRI0:C_TRI0 + 128] = 0.0
    i = np.arange(128)
    own = np.where(i[None, :] <= i[:, None], 0.0, NEG)
    prev = np.where(i[None, :] > i[:, None], 0.0, NEG)
    full = np.full((128, 128), NEG)
    for var, (a, b) in enumerate(((own, prev), (prev, own), (full, own))):
        base = C_AM + 272 * var
        c[:, base:base + 128] = a
        c[:, base + 128:base + 256] = b
        c[:, base + 256:base + 272] = 0.0
    half = 8
    inv_freq = np.power(np.float32(500000.0), -np.arange(half, dtype=np.float32) * np.float32(2.0 / 16)).astype(np.float32)
    for n in range(NTILES):
        pos = (n * 128 + np.arange(128) - 112).astype(np.float32)
        ang = (pos[:, None] * inv_freq[None, :]).astype(np.float32)
        c[:, C_ROPE + 16 * n:C_ROPE + 16 * n + 8] = np.cos(ang)
        c[:, C_ROPE + 16 * n + 8:C_ROPE + 16 * n + 16] = np.sin(ang)
    return c


def prep_shared(inp):
    f = np.float32
    w_in = np.asarray(inp["w_in"][0], f)
    b_in = np.asarray(inp["b_in"][0], f)
    qcols = np.concatenate([np.arange(h * 64, (h + 1) * 64) for h in QPERM])
    w_qkv = np.ascontiguousarray(np.concatenate([w_in[:, qcols], w_in[:, 512:768]], axis=1))
    b_qkv = np.concatenate([b_in[qcols], b_in[512:768]])
    R0 = 768
    w_fm = np.zeros((D, 2048), f)
    b_fm = np.zeros((2048,), f)
    mix = np.asarray(inp["rwkv_mix"][0], f)
    mix_fm = np.zeros((2048,), f)

    def put(dst0, src0, n):
        w_fm[:, dst0:dst0 + n] = w_in[:, R0 + src0:R0 + src0 + n]
        b_fm[dst0:dst0 + n] = b_in[R0 + src0:R0 + src0 + n]
        mix_fm[dst0:dst0 + n] = mix[src0:src0 + n]
    put(0, 0, 1536)
    put(1536, 1536, 64)
    put(1664, 1600, 64)
    put(1792, 1664, 128)
    put(1920, 1792, 32)
    G0 = 768 + 1824
    w_gate = np.ascontiguousarray(w_in[:, G0:G0 + 2048])
    b_gate = b_in[G0:G0 + 2048]
    rows_perm = qcols
    sh = {
        "w_qkv": w_qkv, "w_fm": w_fm, "w_gate": w_gate,
        "w_ba": np.ascontiguousarray(np.asarray(inp["w_br_attn"][0], f)[rows_perm, :]),
        "w_br": np.ascontiguousarray(np.asarray(inp["w_br_rwkv"][0], f)),
        "w_o": np.ascontiguousarray(np.asarray(inp["w_o"][0], f)),
        "w_fg": np.ascontiguousarray(np.asarray(inp["w_ffn_gate"][0], f)),
        "w_fu": np.ascontiguousarray(np.asarray(inp["w_ffn_up"][0], f)),
        "w_fd": np.ascontiguousarray(np.asarray(inp["w_ffn_down"][0], f)),
        "w2": np.ascontiguousarray(np.asarray(inp["rwkv_w2"][0], f)),
        "a2": np.ascontiguousarray(np.asarray(inp["rwkv_a2"][0], f)),
    }
    g2p = np.zeros((256, 512), f)
    g2p[0:160] = np.asarray(inp["rwkv_g2"][0], f)
    sh["g2p"] = g2p
    pfm = np.zeros((128, P_END), f)

    def fm(vec, ncol):
        return np.asarray(vec, f).reshape(ncol, 128).T
    pfm[:, P_GMIX:P_GMIX + 8] = fm(inp["norm_mix_g"][0], 8)
    pfm[:, P_GFFN:P_GFFN + 8] = fm(inp["norm_ffn_g"][0], 8)
    pfm[:, P_BFM:P_BFM + 16] = fm(b_fm, 16)
    pfm[:, P_BG:P_BG + 16] = fm(b_gate, 16)
    pfm[:, P_MIX:P_MIX + 16] = fm(mix_fm, 16)
    pfm[:, P_A0:P_A0 + 4] = fm(inp["rwkv_a0"][0], 4)
    pfm[:, P_KK:P_KK + 4] = fm(inp["rwkv_k_k"][0], 4)
    pfm[:, P_KA:P_KA + 4] = fm(inp["rwkv_k_a"][0], 4)
    pfm[:, P_RK:P_RK + 4] = fm(np.asarray(inp["rwkv_r_k"][0], f).reshape(-1), 4)
    sh["pfm"] = pfm
    rowsA = np.zeros((1, RA_END), f)
    rowsA[0, RA_BQ:RA_BQ + 768] = b_qkv
    rowsA[0, RA_W0:RA_W0 + 512] = np.asarray(inp["rwkv_w0"][0], f)
    rowsA[0, RA_SK:RA_SK + 8] = np.asarray(inp["attn_sinks"][0], f)[QPERM]
    sh["rowsA"] = rowsA
    sh["gfin"] = np.asarray(inp["norm_final_g"], f).reshape(1, D).copy()
    lnst = np.zeros((128, 2, 4, 64), f)
    for a, key in enumerate(("rwkv_ln_w", "rwkv_ln_b")):
        v = np.asarray(inp[key][0], f).reshape(4, 2, 64)
        for j in range(2):
            lnst[64 * j:64 * j + 64, a, :, :] = v[None, :, j, :]
    sh["lnst"] = lnst.reshape(128, -1)
    sh["cst"] = make_consts()
    return sh


def prep_xe(inp, b):
    xe = np.zeros((NTILES * 128, D), np.float32)
    xe[112:128] = np.asarray(inp["meta_tokens"], np.float32)
    xe[128:] = np.asarray(inp["x"][b], np.float32)
    return xe


_NC_CACHE = {}


def kernel(**inputs):
    n = 8
    sh = prep_shared(inputs)
    in_maps = []
    for b in range(n):
        m = dict(sh)
        m["xe"] = prep_xe(inputs, b)
        in_maps.append(m)
    if "nc" not in _NC_CACHE:
        _NC_CACHE["nc"] = build_program()
    res = run_bass_kernel_spmd(_NC_CACHE["nc"], in_maps, core_ids=list(range(n)))
    out = np.stack([np.asarray(r["out"], np.float32).reshape(4096, D) for r in res.results], axis=0)
    return out
```

```python
import numpy as np
import ml_dtypes
from contextlib import ExitStack
import concourse.bass as bass
import concourse.mybir as mybir
from concourse.bass_utils import run_bass_kernel_spmd

F32 = mybir.dt.float32
BF16 = mybir.dt.bfloat16
AF = mybir.ActivationFunctionType
ALU = mybir.AluOpType
AX = mybir.AxisListType

NTILES = 33
D = 1024
DFF = 2816
NFC = 22
RMS_EPS = 1e-6
LN_EPS = 64e-5
CFAC = -float(np.exp(-0.5))
NEG = -1e30
MD = BF16

C_ID = 0
C_MB = 128
C_ML = 256
C_I64 = 320
C_OBD = 384
C_TRI = 512
C_TRI0 = 640
C_AM = 768
C_ROPE = 768 + 816
C_END = C_ROPE + 33 * 16
P_GMIX, P_GFFN, P_BFM, P_BG, P_MIX, P_A0, P_KK, P_KA, P_RK, P_END = 0, 8, 16, 32, 48, 64, 68, 72, 76, 80
RA_BQ, RA_W0, RA_SK, RA_END = 0, 768, 1280, 1288


class Tile:
    def __init__(self, t, name, excl=False):
        self.t = t
        self.name = name
        self.w = None
        self.r = {}
        self.excl = excl
        self.tw = 0.0
        self.tr = 0.0
        self.weng = None

    def __getitem__(self, i):
        return self.t[i]


class Chan:
    def __init__(self, sem, key):
        self.sem = sem
        self.key = key
        self.count = 0


class Sched:
    def __init__(self, nc, es):
        self.nc = nc
        self.es = es
        self.E = {}
        for name, eng in (("pe", nc.tensor), ("act", nc.scalar), ("dve", nc.vector),
                          ("pool", nc.gpsimd), ("sp", nc.sync)):
            sem = es.enter_context(nc.semaphore("sem_" + name))
            self.E[name] = dict(eng=eng, sem=sem, count=0, seen={}, name=name)
        self.chans = []

    def chan(self, name):
        c = Chan(self.es.enter_context(self.nc.semaphore("ch_" + name)), "ch_" + name)
        self.chans.append(c)
        return c

    def _waits(self, E, R, W):
        deps = {}

        def add(d):
            key, val, sem = d
            if key not in deps or deps[key][0] < val:
                deps[key] = (val, sem)
        for t in R:
            if t.w is not None:
                add(t.w)
            if t.excl:
                for key, (val, sem) in t.r.items():
                    if key != E["name"]:
                        add((key, val, sem))
        for t in W:
            if t.w is not None:
                add(t.w)
            for key, (val, sem) in t.r.items():
                add((key, val, sem))
        for key, (val, sem) in deps.items():
            if key == "pe" and E["name"] == "pe":
                continue
            if E["seen"].get(key, 0) < val:
                E["eng"].wait_ge(sem, val)
                E["seen"][key] = val

    def op(self, ename, fns, R=(), W=()):
        E = self.E[ename]
        self._waits(E, R, W)
        if not isinstance(fns, (list, tuple)):
            fns = [fns]
        inst = None
        for f in fns:
            inst = f(E["eng"])
        E["count"] += 1
        inst.then_inc(E["sem"], 1)
        for t in W:
            t.w = (ename, E["count"], E["sem"])
            t.r = {}
        for t in R:
            if t not in W:
                t.r[ename] = (E["count"], E["sem"])

    def dma(self, qname, chan, fn, R=(), W=()):
        E = self.E[qname]
        self._waits(E, R, W)
        inst = fn(E["eng"])
        chan.count += 16
        inst.then_inc(chan.sem, 16)
        for t in W:
            t.w = (chan.key, chan.count, chan.sem)
            t.r = {}
        for t in R:
            t.r[chan.key] = (chan.count, chan.sem)

    def finalize(self, chan, tiles):
        for t in tiles:
            t.w = (chan.key, chan.count, chan.sem)

    def barrier(self):
        for name, E in self.E.items():
            for oname, O in self.E.items():
                if oname == name or O["count"] == 0:
                    continue
                if E["seen"].get(oname, 0) < O["count"]:
                    E["eng"].wait_ge(O["sem"], O["count"])
                    E["seen"][oname] = O["count"]
            for c in self.chans:
                if c.count and E["seen"].get(c.key, 0) < c.count:
                    E["eng"].wait_ge(c.sem, c.count)
                    E["seen"][c.key] = c.count


def build_program(nt=NTILES, phases=("A1", "A2", "B"), dbg=None, dbg_n=-1, md=None, scr_ext=False, stop=None):
    global MD
    if md is not None:
        MD = md
    nc = bass.Bass("TRN2", target_bir_lowering=False)

    def din(name, shape, dt=F32):
        return nc.dram_tensor(name, list(shape), dt, kind="ExternalInput").ap()

    xe = din("xe", [NTILES * 128, D])
    w_qkv = din("w_qkv", [D, 768])
    w_fm = din("w_fm", [D, 2048])
    w_gate = din("w_gate", [D, 2048])
    w_ba = din("w_ba", [512, D])
    w_br = din("w_br", [512, D])
    w_o = din("w_o", [D, D])
    w_fg = din("w_fg", [D, DFF])
    w_fu = din("w_fu", [D, DFF])
    w_fd = din("w_fd", [DFF, D])
    w2d = din("w2", [64, 512])
    a2d = din("a2", [64, 512])
    g2d = din("g2p", [256, 512])
    pfmd = din("pfm", [128, P_END])
    rowsAd = din("rowsA", [1, RA_END])
    gfind = din("gfin", [1, D])
    lnstd = din("lnst", [128, 2 * 4 * 64])
    cstd = din("cst", [128, C_END])
    outd = nc.dram_tensor("out", [(NTILES - 1) * 128, D], F32, kind="ExternalOutput").ap()
    skind = "ExternalOutput" if scr_ext else "Internal"
    yscr = nc.dram_tensor("yscr", [NTILES - 1, 128, 8 * 128], BF16, kind=skind).ap()
    mscr = nc.dram_tensor("mscr", [NTILES - 1, 128, 8 * 128], BF16, kind=skind).ap()
    dbg_out = {}
    if dbg:
        for name, shape in dbg.items():
            dbg_out[name] = nc.dram_tensor("dbg_" + name, list(shape), F32, kind="ExternalOutput").ap()

    with ExitStack() as es0:
        S = Sched(nc, es0)
        ch_w = S.chan("w")
        ch_x = [S.chan("x0"), S.chan("x1")]
        ch_y = [S.chan("y0"), S.chan("y1")]
        ch_st = [S.chan("s0"), S.chan("s1")]
        ch_dbg = S.chan("dbg")
        yscr_t = [Tile(None, "yscr%d" % i) for i in range(NTILES - 1)]
        mscr_t = [Tile(None, "mscr%d" % i) for i in range(NTILES - 1)]

        ST = {"defer": False, "small": False}
        eng_free = {"pe": 0.0, "act": 0.0, "dve": 0.0, "pool": 0.0, "sp": 0.0}
        import os as _os0
        import random as _random
        _cfg = _os0.environ.get("KCFG", "0,0.3,0.1,0.03,0.5,0.0").split(",")
        _rng = _random.Random(int(_cfg[0]))
        HOP = float(_cfg[1]); KPE = float(_cfg[2]); KPS = float(_cfg[3]); KVD = float(_cfg[4]); JIT = float(_cfg[5])

        class Op:
            __slots__ = ("ename", "fns", "R", "W", "dur", "chan")

            def __init__(self, ename, fns, R, W, dur, chan=None):
                self.ename = ename; self.fns = fns; self.R = R; self.W = W; self.dur = dur; self.chan = chan

        def est_start(op):
            t = eng_free[op.ename]
            for x in op.R:
                tw = getattr(x, "tw", 0.0)
                if x.weng != op.ename:
                    tw += HOP
                t = max(t, tw)
                if x.excl:
                    t = max(t, getattr(x, "tr", 0.0) + HOP)
            for x in op.W:
                t = max(t, getattr(x, "tw", 0.0) + (HOP if x.weng != op.ename else 0.0), getattr(x, "tr", 0.0) + HOP)
            return t

        def emit(op):
            t0 = est_start(op)
            t1 = t0 + op.dur
            if op.chan is None:
                S.op(op.ename, op.fns, op.R, op.W)
                eng_free[op.ename] = t1
            else:
                S.dma(op.ename, op.chan, op.fns, op.R, op.W)
                eng_free[op.ename] = t0 + 0.1
                t1 = t0 + 2.5
            for x in op.W:
                x.tw = t1; x.weng = op.ename; x.tr = 0.0
            for x in op.R:
                x.tr = max(getattr(x, "tr", 0.0), t1)

        def mkop(ename, fns, R, W, dur, chan=None):
            if JIT > 0:
                dur = dur * (1.0 + JIT * (_rng.random() - 0.5))
            op = Op(ename, fns, list(R), list(W), dur, chan)
            if ST["defer"]:
                return op
            emit(op)
            return None

        def V(fn, R=(), W=(), d=None):
            d = KVD if d is None else d
            return mkop("dve", fn, R, W, d)

        def A(fn, R=(), W=(), d=None):
            d = KVD if d is None else d
            return mkop("act", fn, R, W, d)

        def G(fn, R=(), W=(), d=1.2):
            return mkop("pool", fn, R, W, d)

        def PE(fns, R=(), W=(), d=None):
            n_ = len(fns) if isinstance(fns, (list, tuple)) else 1
            if d is None:
                d = n_ * (KPS if ST["small"] else KPE) + 0.1
            ST["small"] = False
            return mkop("pe", fns, R, W, d)

        def DMA(qname, chan, fn, R=(), W=()):
            return mkop(qname, fn, R, W, 2.5, chan)

        def dump(name, tile_ap, tiles):
            if name in dbg_out:
                S.dma("sp", ch_dbg, lambda e: e.dma_start(out=dbg_out[name], in_=tile_ap), R=tiles, W=[])

        def run(gens, bonus=None):
            if not isinstance(gens, (list, tuple)):
                gens = [gens]
            if bonus is None:
                bonus = [0.0] * len(gens)
            ST["defer"] = True
            heads = []
            for g in gens:
                heads.append(next(g, None))
            try:
                while True:
                    best = None
                    bt = None
                    for i, h in enumerate(heads):
                        if h is None:
                            continue
                        t = est_start(h) - bonus[i]
                        if bt is None or t < bt:
                            bt = t; best = i
                    if best is None:
                        break
                    ST["defer"] = False
                    emit(heads[best])
                    ST["defer"] = True
                    h = next(gens[best], None)
                    while h is None:
                        try:
                            h = next(gens[best])
                        except StopIteration:
                            h = None
                            break
                    heads[best] = h
            finally:
                ST["defer"] = False

        def par(gens, weights=None):
            return list(gens)

        def mk_alloc(es, pfx):
            def sb(name, shape, dt=F32):
                return Tile(es.enter_context(nc.sbuf_tensor(pfx + name, list(shape), dt)), pfx + name)
            return sb

        def norm_T(x, xs, st4, cst, ce, pfm, gcol, TR2, uT):
            yield A(lambda e: e.activation(out=xs[:], in_=x[:], func=AF.Square, accum_out=st4[:, 0:1]), R=[x], W=[xs, st4])
            yield V(lambda e: e.tensor_scalar(out=st4[:, 1:2], in0=st4[:, 0:1], scalar1=1.0 / D, scalar2=RMS_EPS, op0=ALU.mult, op1=ALU.add), R=[st4], W=[st4])
            yield G(lambda e: e.tensor_tensor(out=st4[:, 2:3], in0=st4[:, 1:2], in1=cst[:, ce + 4:ce + 5], op=ALU.pow), R=[st4, cst], W=[st4])
            yield A(lambda e: e.activation(out=xs[:], in_=x[:], func=AF.Identity, scale=st4[:, 2:3], bias=cst[:, ce + 2:ce + 3]),
                    R=[x, st4, cst], W=[xs])
            for h in range(2):
                yield PE([lambda e, c=c: e.transpose(TR2[h][:, (c % 4) * 128:(c % 4 + 1) * 128], xs[:, c * 128:(c + 1) * 128], cst[:, C_ID:C_ID + 128])
                          for c in range(4 * h, 4 * h + 4)], R=[xs, cst], W=[TR2[h]])
                yield V(lambda e, h=h: e.tensor_tensor(out=uT[:, 4 * h:4 * h + 4, :], in0=TR2[h][:].rearrange("p (c k) -> p c k", k=128),
                                                       in1=pfm[:, gcol + 4 * h:gcol + 4 * h + 4].unsqueeze(2).broadcast_to([128, 4, 128]), op=ALU.mult),
                        R=[TR2[h], pfm], W=[uT])

        def load_consts(sb, ncols):
            cst = sb("cst", [128, ncols + 12])
            pfm = sb("pfm", [128, P_END])
            G(lambda e: e.memset(cst[:, ncols:ncols + 1], RMS_EPS), W=[cst])
            G(lambda e: e.memset(cst[:, ncols + 1:ncols + 2], LN_EPS), W=[cst])
            G(lambda e: e.memset(cst[:, ncols + 2:ncols + 4], 0.0), W=[cst])
            G(lambda e: e.memset(cst[:, ncols + 4:ncols + 12], -0.5), W=[cst])
            S.dma("sp", ch_w, lambda e: e.dma_start(out=cst[:, 0:ncols], in_=cstd[:, 0:ncols]), W=[cst])
            S.dma("sp", ch_w, lambda e: e.dma_start(out=pfm[:], in_=pfmd), W=[pfm])
            return cst, pfm

        wl_n = [0]

        def wload_blk(tile_, out_ap, in_ap):
            wl_n[0] += 1
            ch = S.chan("wb%d" % wl_n[0])
            t = Tile(tile_.t, "%s_blk%d" % (tile_.name, wl_n[0]))
            S.dma("pool", ch, lambda e: e.dma_start(out=out_ap, in_=in_ap), W=[t])
            return t

        def wload(tile_, out_ap, in_ap):
            S.dma("pool", ch_w, lambda e: e.dma_start(out=out_ap, in_=in_ap), W=[tile_])

        if "A1" in phases:
            with ExitStack() as es:
                sb = mk_alloc(es, "a1_")
                CE = C_END
                cst, pfm = load_consts(sb, C_END)
                rowsA = sb("rowsA", [128, RA_END])
                S.dma("sp", ch_w, lambda e: e.dma_start(out=rowsA[:], in_=rowsAd.broadcast_to([128, RA_END])), W=[rowsA])
                lnst = sb("lnst", [128, 2, 4, 64])
                S.dma("sp", ch_w, lambda e: e.dma_start(out=lnst[:].rearrange("p a c v -> p (a c v)"), in_=lnstd), W=[lnst])
                wqkv = sb("wqkv", [128, 8, 768], BF16)
                wfm = sb("wfm", [128, 8, 2048], BF16)
                w2 = sb("w2", [64, 512])
                a2 = sb("a2", [64, 512])
                g2 = sb("g2", [128, 2, 512])
                S.dma("sp", ch_w, lambda e: e.dma_start(out=w2[:], in_=w2d), W=[w2])
                S.dma("sp", ch_w, lambda e: e.dma_start(out=a2[:], in_=a2d), W=[a2])
                S.dma("sp", ch_w, lambda e: e.dma_start(out=g2[:], in_=g2d.rearrange("(c p) n -> p c n", p=128)), W=[g2])
                wqkv_b = wload_blk(wqkv, wqkv[:], w_qkv.rearrange("(c p) n -> p c n", p=128))
                wfm_b = [wload_blk(wfm, wfm[:, :, 512 * g:512 * (g + 1)], w_fm.rearrange("(c p) n -> p c n", p=128)[:, :, 512 * g:512 * (g + 1)]) for g in range(4)]
                identb = sb("identb", [128, 128], BF16)
                ones2 = sb("ones2", [128, 2])
                G(lambda e: e.memset(ones2[:], 1.0), W=[ones2])
                S.finalize(ch_w, [cst, pfm, rowsA, lnst, w2, a2, g2])
                V(lambda e: e.tensor_copy(identb[:], cst[:, C_ID:C_ID + 128]), R=[cst], W=[identb])

                PS = [Tile(es.enter_context(nc.psum_tensor("ps%d" % i, [128, 512], F32)), "ps%d" % i, excl=True) for i in range(8)]
                H0, Q0, A0, A1_, A2_, R0, R1_, R2 = PS
                H1 = H0

                xb = [sb("xb0", [128, D]), sb("xb1", [128, D])]
                xs = sb("xs", [128, D])
                st4 = sb("st4", [128, 4])
                uT = sb("uT", [128, 8, 128], BF16)
                stg = sb("stg", [128, 16, 129])
                pfs = [sb("pf0", [128, 16, 128]), sb("pf1", [128, 16, 128])]
                qkvs = [sb("qkv0", [128, 768]), sb("qkv1", [128, 768])]
                rtmp = sb("rtmp", [128, 4, 10, 8])
                qT = sb("qT", [128, 4, 128], BF16)
                Kbuf = sb("Kbuf", [128, 272], BF16)
                Vbuf = sb("Vbuf", [128, 3, 128], BF16)
                Pb = [sb("Pb0", [128, 272], BF16), sb("Pb1", [128, 272], BF16)]
                PT = [sb("PT0", [128, 3, 128], BF16), sb("PT1", [128, 3, 128], BF16)]
                sm = sb("sm", [128, 5, 8])
                yat = sb("yat", [128, 8, 64])
                yTa = [sb("yTa0", [128, 4, 128], BF16), sb("yTa1", [128, 4, 128], BF16)]
                yTr = [sb("yTr0", [128, 4, 128], BF16), sb("yTr1", [128, 4, 128], BF16)]
                th = sb("th", [64, 128])
                sgd = sb("sgd", [128, 2, 128])
                B1 = sb("B1", [128, 4, 128]); B2 = sb("B2", [128, 4, 128]); B3 = sb("B3", [128, 4, 128])
                B4 = sb("B4", [128, 4, 128])
                arTs = [sb("arT%d" % i, [128, 4, 2, 2, 64], MD) for i in range(2)]
                BT = sb("BT", [128, 4, 128], MD); KT = sb("KT", [128, 4, 128], MD)
                BH = sb("BH", [128, 4, 128], MD); KH = sb("KH", [128, 4, 128], MD)
                cumC = sb("cumC", [128, 4, 2])
                WCs = [sb("WC%d" % i, [128, 4, 2]) for i in range(2)]
                rks = [sb("rk%d" % i, [128, 2, 4]) for i in range(2)]
                B5s = [sb("B5_%d" % i, [128, 4, 128]) for i in range(2)]
                TTs = [sb("TT%d" % i, [128, 2, 4, 64], MD) for i in range(2)]
                Ast = [sb("Ast0", [128, 2, 4, 64], MD), sb("Ast1", [128, 2, 4, 64], MD)]
                Nst = [sb("Nst0", [128, 2, 4, 64], MD), sb("Nst1", [128, 2, 4, 64], MD)]
                NBs = [sb("NB%d" % i, [128, 2, 4, 2, 64], MD) for i in range(2)]
                NKs = [sb("NK%d" % i, [128, 2, 4, 2, 64], MD) for i in range(2)]
                Pc = [sb("Pc0", [128, 2, 4, 64], MD), sb("Pc1", [128, 2, 4, 64], MD)]
                Vst32s = [sb("Vst32_%d" % i, [128, 2, 4, 64]) for i in range(2)]
                Vsts = [sb("Vst_%d" % i, [128, 2, 4, 64], MD) for i in range(2)] if MD != F32 else Vst32s
                BKsts = [sb("BKst%d" % i, [128, 2, 2, 4, 64], MD) for i in range(2)]
                R1 = sb("R1", [128, 4, 64], MD); Ust = sb("Ust", [128, 4, 64], MD)
                Yst = sb("Yst", [128, 2, 4, 64]); yc = sb("yc", [128, 2, 4, 64]); ysq = sb("ysq", [128, 2, 4, 64])
                ST32 = sb("ST32", [128, 4, 64])
                STt = sb("STt", [128, 4, 64])
                STm = sb("STm", [128, 4, 64], MD) if MD != F32 else ST32
                gst = sb("gst", [128, 6, 8])
                hb = sb("hb", [128, 4])
                V(lambda e: e.tensor_scalar(out=hb[:], in0=pfm[:, P_A0:P_A0 + 4], scalar1=0.5, scalar2=None, op0=ALU.mult), R=[pfm], W=[hb])
                G(lambda e: e.memset(ST32[:], 0.0), W=[ST32])
                if MD != F32:
                    G(lambda e: e.memset(STm[:], 0.0), W=[STm])
                G(lambda e: e.memset(stg[:], 0.0), W=[stg])
                G(lambda e: e.memset(Vbuf[:], 0.0), W=[Vbuf])
                G(lambda e: e.memset(Kbuf[:], 0.0), W=[Kbuf])

                def v3(ap2d, k=64):
                    return ap2d.rearrange("p (c k) -> p c k", k=k)

                def v4(ap2d):
                    return ap2d.rearrange("p (q c k) -> p q c k", q=2, c=4)

                def cq(t):
                    return t.rearrange("p c (q t) -> p c q t", q=2)

                def qc(t):
                    return t.rearrange("p c (q t) -> p q c t", q=2)

                ID0 = C_ID if MD == F32 else 0
                idm = cst if MD == F32 else identb

                def idsl(sl, j):
                    return cst[sl, C_ID + 64 * j:C_ID + 64 * j + 64]

                def idm_sl(sl, j):
                    return idm[sl, ID0 + 64 * j:ID0 + 64 * j + 64]

                def hl(fn, qs=(0,)):
                    ST["small"] = True
                    out = []
                    for q in qs:
                        for c in range(4):
                            for j in range(2):
                                out += fn(q, c, slice(64 * j, 64 * j + 64), j)
                    return out

                def head(n):
                    xt = xb[n % 2]
                    pf = pfs[n % 2]
                    qkv = qkvs[n % 2]
                    yield DMA("sp", ch_x[n % 2], lambda e: e.dma_start(out=xt[:], in_=xe[n * 128:(n + 1) * 128, :]), W=[xt])
                    yield from norm_T(xt, xs, st4, cst, CE, pfm, P_GMIX, [H0, H1], uT)
                    yield PE([lambda e, kc=kc: e.matmul(H0[:, 0:512], uT[:, kc, :], wqkv[:, kc, 0:512], start=(kc == 0), stop=(kc == 7)) for kc in range(8)],
                             R=[uT, wqkv_b], W=[H0])
                    yield V(lambda e: e.tensor_tensor(out=qkv[:, 0:512], in0=H0[:, 0:512], in1=rowsA[:, RA_BQ:RA_BQ + 512], op=ALU.add), R=[H0, rowsA], W=[qkv])
                    yield PE([lambda e, kc=kc: e.matmul(H1[:, 0:256], uT[:, kc, :], wqkv[:, kc, 512:768], start=(kc == 0), stop=(kc == 7)) for kc in range(8)],
                             R=[uT, wqkv_b], W=[H1])
                    yield V(lambda e: e.tensor_tensor(out=qkv[:, 512:768], in0=H1[:, 0:256], in1=rowsA[:, RA_BQ + 512:RA_BQ + 768], op=ALU.add), R=[H1, rowsA], W=[qkv])
                    for g in range(4):
                        bank = (H0, H1)[g % 2]
                        fns = []
                        for i in range(4):
                            col = (4 * g + i) * 128
                            for kc in range(8):
                                fns.append(lambda e, i=i, col=col, kc=kc, bank=bank: e.matmul(bank[:, i * 128:(i + 1) * 128], wfm[:, kc, col:col + 128], uT[:, kc, :],
                                                                                             start=(kc == 0), stop=(kc == 7)))
                        yield PE(fns, R=[uT, wfm_b[g]], W=[bank])
                        yield V(lambda e, g=g, bank=bank: e.tensor_tensor(out=stg[:, 4 * g:4 * g + 4, 1:129], in0=v3(bank[:], 128),
                                                                          in1=pfm[:, P_BFM + 4 * g:P_BFM + 4 * g + 4].unsqueeze(2).broadcast_to([128, 4, 128]), op=ALU.add),
                                R=[bank, pfm], W=[stg])
                    if n == 0:
                        yield G(lambda e: e.memset(stg[:, :, 1:113], 0.0), W=[stg])
                    yield G(lambda e: e.tensor_tensor(out=pf[:], in0=stg[:, :, 0:128], in1=stg[:, :, 1:129], op=ALU.subtract), R=[stg], W=[pf], d=3.6)
                    yield G(lambda e: e.tensor_tensor(out=pf[:], in0=pf[:], in1=pfm[:, P_MIX:P_MIX + 16].unsqueeze(2).broadcast_to([128, 16, 128]), op=ALU.mult),
                            R=[pfm], W=[pf], d=3.6)
                    yield G(lambda e: e.tensor_tensor(out=pf[:], in0=pf[:], in1=stg[:, :, 1:129], op=ALU.add), R=[stg], W=[pf], d=3.6)
                    yield G(lambda e: e.tensor_copy(stg[:, :, 0:1], stg[:, :, 128:129]), R=[], W=[stg])

                def attention(n):
                    qkv = qkvs[n % 2]
                    slot = n % 2
                    q10 = qkv[:, 0:640].rearrange("p (h d) -> p h d", d=64)
                    cosb = cst[:, C_ROPE + 16 * n:C_ROPE + 16 * n + 8].unsqueeze(1).broadcast_to([128, 10, 8])
                    sinb = cst[:, C_ROPE + 16 * n + 8:C_ROPE + 16 * n + 16].unsqueeze(1).broadcast_to([128, 10, 8])
                    yield G(lambda e: e.tensor_tensor(out=rtmp[:, 0], in0=q10[:, :, 0:8], in1=cosb, op=ALU.mult), R=[qkv, cst], W=[rtmp])
                    yield G(lambda e: e.tensor_tensor(out=rtmp[:, 1], in0=q10[:, :, 8:16], in1=sinb, op=ALU.mult), R=[qkv, cst], W=[rtmp])
                    yield G(lambda e: e.tensor_tensor(out=rtmp[:, 2], in0=q10[:, :, 8:16], in1=cosb, op=ALU.mult), R=[qkv, cst], W=[rtmp])
                    yield G(lambda e: e.tensor_tensor(out=rtmp[:, 3], in0=q10[:, :, 0:8], in1=sinb, op=ALU.mult), R=[qkv, cst], W=[rtmp])
                    yield G(lambda e: e.tensor_tensor(out=q10[:, :, 0:8], in0=rtmp[:, 0], in1=rtmp[:, 1], op=ALU.subtract), R=[rtmp], W=[qkv])
                    yield G(lambda e: e.tensor_tensor(out=q10[:, :, 8:16], in0=rtmp[:, 2], in1=rtmp[:, 3], op=ALU.add), R=[rtmp], W=[qkv])
                    yield PE([lambda e, c=c: e.transpose(A0[:, c * 128:(c + 1) * 128], qkv[:, c * 128:(c + 1) * 128], cst[:, C_ID:C_ID + 128]) for c in range(4)],
                             R=[qkv, cst], W=[A0])
                    yield PE(lambda e: e.transpose(A1_[:, 0:128], qkv[:, 512:640], cst[:, C_ID:C_ID + 128]), R=[qkv, cst], W=[A1_])
                    yield A(lambda e: e.activation(out=qT[:], in_=v3(A0[:], 128), func=AF.Copy, scale=0.125), R=[A0], W=[qT])
                    yield A(lambda e: e.activation(out=Kbuf[:, slot * 128:(slot + 1) * 128], in_=A1_[:, 0:128], func=AF.Copy), R=[A1_], W=[Kbuf])
                    yield V(lambda e: e.tensor_copy(Vbuf[:, slot, :], qkv[:, 640:768]), R=[qkv], W=[Vbuf])
                    if n == 0:
                        yield A(lambda e: e.activation(out=Kbuf[:, 256:272], in_=A1_[:, 112:128], func=AF.Copy), R=[A1_], W=[Kbuf])
                        yield PE(lambda e: e.matmul(A1_[0:16, 128:256], cst[:, C_ID + 112:C_ID + 128], qkv[:, 640:768], start=True, stop=True), R=[qkv, cst], W=[A1_])
                        yield V(lambda e: e.tensor_copy(Vbuf[0:16, 2, :], A1_[0:16, 128:256]), R=[A1_], W=[Vbuf])
                        return
                    yTn = yTa[n % 2]
                    mvar = 2 if n == 1 else (0 if n % 2 == 0 else 1)
                    mask = cst[:, C_AM + 272 * mvar:C_AM + 272 * (mvar + 1)]
                    for s in range(8):
                        c, j = s // 2, s % 2
                        sl = slice(64 * j, 64 * j + 64)
                        SC = A0
                        Pk = Pb[s % 2]
                        PTk = PT[s % 2]
                        yield PE(lambda e: e.matmul(SC[:, 0:272], qT[sl, c, :], Kbuf[sl, 0:272], start=True, stop=True), R=[qT, Kbuf], W=[SC])
                        yield V(lambda e: e.tensor_tensor(out=SC[:, 0:272], in0=SC[:, 0:272], in1=mask, op=ALU.add), R=[cst], W=[SC])
                        yield V(lambda e: e.tensor_reduce(out=sm[:, 0, s:s + 1], in_=SC[:, 0:272], axis=AX.X, op=ALU.max), R=[SC], W=[sm])
                        yield V(lambda e: e.tensor_scalar(out=sm[:, 1, s:s + 1], in0=sm[:, 0, s:s + 1], scalar1=rowsA[:, RA_SK + s:RA_SK + s + 1], scalar2=-1.0,
                                                          op0=ALU.max, op1=ALU.mult), R=[rowsA], W=[sm])
                        yield A(lambda e: e.activation(out=Pk[:], in_=SC[:, 0:272], func=AF.Exp, bias=sm[:, 1, s:s + 1], scale=1.0,
                                                       accum_out=sm[:, 2, s:s + 1]), R=[SC], W=[Pk, sm])
                        yield A(lambda e: e.activation(out=sm[:, 3, s:s + 1], in_=rowsA[:, RA_SK + s:RA_SK + s + 1], func=AF.Exp, bias=sm[:, 1, s:s + 1], scale=1.0),
                                R=[rowsA], W=[sm])
                        yield PE([lambda e, b=b, nk=nk: e.matmul(A1_[0:nk, b * 128:(b + 1) * 128], Pk[:, b * 128:b * 128 + nk], identb[:], start=True, stop=True)
                                  for b, nk in ((0, 128), (1, 128), (2, 16))], R=[Pk, identb], W=[A1_])
                        yield A(lambda e: e.activation(out=PTk[:, 0:2, :], in_=v3(A1_[:, 0:256], 128), func=AF.Copy), R=[A1_], W=[PTk])
                        yield A(lambda e: e.activation(out=PTk[0:16, 2, :], in_=A1_[0:16, 256:384], func=AF.Copy), R=[A1_], W=[PTk])
                        yield PE([lambda e: e.matmul(A2_[:, s * 64:(s + 1) * 64], PTk[:, 0, :], Vbuf[:, 0, sl], start=True, stop=False),
                                  lambda e: e.matmul(A2_[:, s * 64:(s + 1) * 64], PTk[:, 1, :], Vbuf[:, 1, sl], start=False, stop=False),
                                  lambda e: e.matmul(A2_[:, s * 64:(s + 1) * 64], PTk[0:16, 2, :], Vbuf[0:16, 2, sl], start=False, stop=True)],
                                 R=[PTk, Vbuf], W=[A2_])
                    yield V(lambda e: e.tensor_tensor(out=sm[:, 2, :], in0=sm[:, 2, :], in1=sm[:, 3, :], op=ALU.add), R=[], W=[sm])
                    yield V(lambda e: e.reciprocal(out=sm[:, 4, :], in_=sm[:, 2, :]), R=[], W=[sm])
                    yield V(lambda e: e.tensor_tensor(out=yat[:], in0=v3(A2_[:], 64), in1=sm[:, 4, :].unsqueeze(2).broadcast_to([128, 8, 64]), op=ALU.mult),
                            R=[A2_], W=[yat, sm])
                    yield PE([lambda e, c=c: e.transpose(A1_[:, c * 128:(c + 1) * 128], yat[:, 2 * c:2 * c + 2, :].rearrange("p a d -> p (a d)"), cst[:, C_ID:C_ID + 128])
                              for c in range(4)], R=[yat, cst], W=[A1_])
                    yield A(lambda e: e.activation(out=yTn[:], in_=v3(A1_[:], 128), func=AF.Copy), R=[A1_], W=[yTn])
                    if n == dbg_n:
                        dump("yat", yat[:].rearrange("p s d -> p (s d)"), [yat])
                    yield DMA("sp", ch_y[n % 2], lambda e: e.dma_start(out=yscr[n - 1][:, 0:512], in_=yTn[:].rearrange("p c t -> p (c t)")), R=[yTn], W=[yscr_t[n - 1]])

                def pre(n):
                    pf = pfs[n % 2]
                    pp_ = n % 2
                    arT, NB, NK, Vst, Vst32, BKst, WC, rk, B5 = arTs[pp_], NBs[pp_], NKs[pp_], Vsts[pp_], Vst32s[pp_], BKsts[pp_], WCs[pp_], rks[pp_], B5s[pp_]
                    tri0 = C_TRI0 if n == 0 else C_TRI
                    tri = cst[:, tri0:tri0 + 128]
                    yield A(lambda e: e.activation(out=th[:], in_=pf[0:64, 12, :], func=AF.Tanh), R=[pf], W=[th])
                    yield PE(lambda e: e.matmul(R0[:, 0:512], th[:], w2[:], start=True, stop=True), R=[th, w2], W=[R0])
                    B4f = B4[:].rearrange("p c t -> p (c t)")
                    yield V(lambda e: e.tensor_tensor(out=B4f, in0=R0[:, 0:512], in1=rowsA[:, RA_W0:RA_W0 + 512], op=ALU.add), R=[R0, rowsA], W=[B4])
                    yield A(lambda e: e.activation(out=B4f, in_=B4f, func=AF.Tanh, scale=0.5), R=[], W=[B4])
                    yield V(lambda e: e.tensor_scalar(out=B4f, in0=B4f, scalar1=0.5, scalar2=0.5, op0=ALU.mult, op1=ALU.add), R=[], W=[B4])
                    CB = (R1_, R2)
                    for q in range(2):
                        yield PE([lambda e, c=c, q=q: e.matmul(CB[q][:, c * 128:(c + 1) * 128], B4f[64 * q:64 * q + 64, c * 128:(c + 1) * 128],
                                                               tri[64 * q:64 * q + 64, :], start=True, stop=True) for c in range(4)], R=[B4, cst], W=[CB[q]])

                    def cums(a):
                        return [CB[q][:].rearrange("p (c a t) -> p c a t", c=4, a=2)[:, :, a, :] for q in range(2)]
                    yield PE([lambda e, c=c: e.matmul(R0[:, c * 128:(c + 1) * 128], a2[:, c * 128:(c + 1) * 128], pf[0:64, 13, :], start=True, stop=True) for c in range(4)],
                             R=[pf, a2], W=[R0])
                    for c in range(4):
                        yield A(lambda e, c=c: e.activation(out=B3[:, c, :], in_=R0[:, c * 128:(c + 1) * 128], func=AF.Tanh, bias=hb[:, c:c + 1], scale=0.5),
                                R=[R0, hb], W=[B3])
                    yield V(lambda e: e.tensor_scalar(out=B3[:], in0=B3[:], scalar1=0.5, scalar2=0.5, op0=ALU.mult, op1=ALU.add), R=[], W=[B3])
                    yield A(lambda e: e.activation(out=sgd[:], in_=pf[:, 14:16, :], func=AF.Tanh, scale=0.5), R=[pf], W=[sgd])
                    yield V(lambda e: e.tensor_scalar(out=sgd[:], in0=sgd[:], scalar1=0.5, scalar2=0.5, op0=ALU.mult, op1=ALU.add), R=[], W=[sgd])
                    fns = []
                    for c in range(4):
                        fns.append(lambda e, c=c: e.matmul(R0[:, c * 128:(c + 1) * 128], g2[:, 0, c * 128:(c + 1) * 128], sgd[:, 0, :], start=True, stop=False))
                        fns.append(lambda e, c=c: e.matmul(R0[:, c * 128:(c + 1) * 128], g2[0:32, 1, c * 128:(c + 1) * 128], sgd[0:32, 1, :], start=False, stop=True))
                    yield PE(fns, R=[sgd, g2], W=[R0])
                    yield A(lambda e: e.activation(out=B5[:].rearrange("p c t -> p (c t)"), in_=R0[:], func=AF.Copy), R=[R0], W=[B5])
                    kview = pf[:, 4:8, :]
                    rview = pf[:, 0:4, :]

                    def bc(col):
                        return pfm[:, col:col + 4].unsqueeze(2).broadcast_to([128, 4, 128])
                    yield V(lambda e: e.tensor_tensor(out=B1[:], in0=kview, in1=bc(P_KK), op=ALU.mult), R=[pf, pfm], W=[B1])
                    yield V(lambda e: e.tensor_tensor(out=B2[:], in0=B1[:], in1=B1[:], op=ALU.mult), R=[B1], W=[B2])
                    yield PE([lambda e, c=c: e.matmul(R0[:, c * 128:(c + 1) * 128], cst[:, C_OBD:C_OBD + 128], B2[:, c, :], start=True, stop=True) for c in range(4)],
                             R=[B2, cst], W=[R0])
                    yield A(lambda e: e.activation(out=B2[:].rearrange("p c t -> p (c t)"), in_=R0[:], func=AF.Sqrt), R=[R0], W=[B2])
                    yield V(lambda e: e.tensor_scalar(out=B2[:], in0=B2[:], scalar1=1e-12, scalar2=None, op0=ALU.max), R=[], W=[B2])
                    yield V(lambda e: e.reciprocal(out=B2[:], in_=B2[:]), R=[], W=[B2])
                    yield V(lambda e: e.tensor_tensor(out=B1[:], in0=B1[:], in1=B2[:], op=ALU.mult), R=[B2], W=[B1])
                    yield V(lambda e: e.scalar_tensor_tensor(out=B2[:], in0=B3[:], scalar=-1.0, in1=bc(P_KA), op0=ALU.add, op1=ALU.mult), R=[B3, pfm], W=[B2])
                    yield V(lambda e: e.scalar_tensor_tensor(out=kview, in0=B2[:], scalar=1.0, in1=kview, op0=ALU.add, op1=ALU.mult), R=[B2], W=[pf])
                    yield V(lambda e: e.tensor_tensor(out=B3[:], in0=B1[:], in1=B3[:], op=ALU.mult), R=[B1], W=[B3])
                    cex = cums(1)
                    cin = cums(0)
                    B4q = cq(B4[:])
                    for hh in range(2):
                        yield A(lambda e, hh=hh: e.activation(out=B4[:, :, 64 * hh:64 * hh + 64], in_=cex[hh], func=AF.Exp), R=[CB[hh]], W=[B4])
                    yield V(lambda e: e.scalar_tensor_tensor(out=arT[:, :, :, 0, :], in0=cq(B1[:]), scalar=-1.0, in1=B4q, op0=ALU.mult, op1=ALU.mult), R=[B1, B4], W=[arT])
                    for hh in range(2):
                        yield A(lambda e, hh=hh: e.activation(out=B4[:, :, 64 * hh:64 * hh + 64], in_=cin[hh], func=AF.Exp), R=[CB[hh]], W=[B4])
                    yield V(lambda e: e.tensor_tensor(out=arT[:, :, :, 1, :], in0=cq(rview), in1=B4q, op=ALU.mult), R=[pf, B4], W=[arT])
                    for hh in range(2):
                        yield A(lambda e, hh=hh: e.activation(out=B4[:, :, 64 * hh:64 * hh + 64], in_=cin[hh], func=AF.Exp, scale=-1.0), R=[CB[hh]], W=[B4])
                    yield V(lambda e: e.tensor_tensor(out=BT[:], in0=B3[:], in1=B4[:], op=ALU.mult), R=[B3, B4], W=[BT])
                    yield V(lambda e: e.tensor_tensor(out=KT[:], in0=kview, in1=B4[:], op=ALU.mult), R=[pf, B4], W=[KT])
                    for hh in range(2):
                        yield V(lambda e, hh=hh: e.tensor_copy(cumC[:, :, hh], cin[hh][:, :, 63]), R=[CB[hh]], W=[cumC])
                    for c in range(4):
                        for q in range(2):
                            yield A(lambda e, c=c, q=q: e.activation(out=B4[:, c, 64 * q:64 * q + 64], in_=cin[q][:, c, :], func=AF.Exp, scale=-1.0,
                                                                     bias=cumC[:, c, q:q + 1]), R=[CB[q], cumC], W=[B4])
                    yield V(lambda e: e.tensor_tensor(out=BH[:], in0=B3[:], in1=B4[:], op=ALU.mult), R=[B3, B4], W=[BH])
                    yield V(lambda e: e.tensor_tensor(out=KH[:], in0=kview, in1=B4[:], op=ALU.mult), R=[pf, B4], W=[KH])
                    yield A(lambda e: e.activation(out=WC[:], in_=cumC[:], func=AF.Exp), R=[cumC], W=[WC])

                    mb = cst[:, C_MB:C_MB + 128].rearrange("p (a t) -> p a t", a=2).unsqueeze(1).broadcast_to([128, 4, 2, 64])
                    ml8 = cst[:, C_ML:C_ML + 64].unsqueeze(1).broadcast_to([128, 8, 64])
                    i8 = cst[:, C_I64:C_I64 + 64].unsqueeze(1).broadcast_to([128, 8, 64])
                    Q2 = (0, 1)

                    def tq(q):
                        return slice(64 * q, 64 * q + 64)
                    yield PE(hl(lambda q, c, sl, j: [lambda e: e.matmul(R0[sl, q * 256 + c * 64:q * 256 + (c + 1) * 64], arT[sl, c, q, 0, :], BT[sl, c, tq(q)], start=True, stop=True)], Q2),
                             R=[arT, BT], W=[R0])
                    for q in Q2:
                        yield PE(hl(lambda q, c, sl, j: [lambda e: e.matmul(CB[q][sl, c * 128:(c + 1) * 128], BT[sl, c, tq(q)], arT[sl, c, q, :, :].rearrange("p a t -> p (a t)"),
                                                                            start=True, stop=True)], (q,)), R=[arT, BT], W=[CB[q]])
                    yield V(lambda e: e.tensor_tensor(out=Ast[0][:].rearrange("p q c t -> p (q c) t"), in0=v3(R0[:]), in1=ml8, op=ALU.mult), R=[R0, cst], W=[Ast[0]])
                    for q in Q2:
                        yield V(lambda e, q=q: e.tensor_tensor(out=NB[:, q], in0=CB[q][:].rearrange("p (c a t) -> p c a t", c=4, a=2), in1=mb, op=ALU.mult), R=[CB[q], cst], W=[NB])
                    for q in Q2:
                        yield PE(hl(lambda q, c, sl, j: [lambda e: e.matmul(CB[q][sl, c * 128:(c + 1) * 128], KT[sl, c, tq(q)], arT[sl, c, q, :, :].rearrange("p a t -> p (a t)"),
                                                                            start=True, stop=True)], (q,)), R=[arT, KT], W=[CB[q]])
                    for q in Q2:
                        yield V(lambda e, q=q: e.tensor_tensor(out=NK[:, q], in0=CB[q][:].rearrange("p (c a t) -> p c a t", c=4, a=2), in1=mb, op=ALU.mult), R=[CB[q], cst], W=[NK])
                    yield G(lambda e: e.tensor_copy(Nst[0][:], NB[:, :, :, 0, :]), R=[NB], W=[Nst[0]])
                    yield G(lambda e: e.tensor_tensor(out=Pc[0][:].rearrange("p q c t -> p (q c) t"), in0=Nst[0][:].rearrange("p q c t -> p (q c) t"), in1=i8, op=ALU.add),
                            R=[Nst[0], cst], W=[Pc[0]])
                    yield PE(hl(lambda q, c, sl, j: [lambda e: e.matmul(R0[sl, q * 256 + c * 64:q * 256 + (c + 1) * 64], pf[sl, 8 + c, tq(q)], idsl(sl, j), start=True, stop=True)], Q2),
                             R=[pf, cst], W=[R0])
                    yield A(lambda e: e.activation(out=Vst32[:], in_=v4(R0[:]), func=AF.Copy), R=[R0], W=[Vst32])
                    if MD != F32:
                        yield V(lambda e: e.tensor_copy(Vst[:], v4(R0[:])), R=[R0], W=[Vst])
                    for q in Q2:
                        yield PE(hl(lambda q, c, sl, j: [lambda e: e.matmul(CB[q][sl, c * 64:(c + 1) * 64], BH[sl, c, tq(q)], idm_sl(sl, j), start=True, stop=True),
                                                         lambda e: e.matmul(CB[q][sl, 256 + c * 64:256 + (c + 1) * 64], KH[sl, c, tq(q)], idm_sl(sl, j), start=True, stop=True)], (q,)),
                                 R=[BH, KH, idm], W=[CB[q]])
                    for q in Q2:
                        yield A(lambda e, q=q: e.activation(out=BKst[:, q], in_=CB[q][:].rearrange("p (a c t) -> p a c t", a=2, c=4), func=AF.Copy), R=[CB[q]], W=[BKst])
                    cur = 0

                    def sq_fns(cur, lvl):
                        f = hl(lambda q, c, sl, j: [lambda e: e.matmul(R0[sl, q * 256 + c * 64:q * 256 + (c + 1) * 64], Nst[cur][sl, q, c, :], Ast[cur][sl, q, c, :], start=True, stop=True)], Q2)
                        if lvl < 5:
                            f += hl(lambda q, c, sl, j: [lambda e: e.matmul(R1_[sl, q * 256 + c * 64:q * 256 + (c + 1) * 64], Ast[cur][sl, q, c, :], Nst[cur][sl, q, c, :], start=True, stop=True)], Q2)
                        return f

                    def pp_fns(a_t, pc):
                        return hl(lambda q, c, sl, j: [lambda e: e.matmul(R2[sl, q * 256 + c * 64:q * 256 + (c + 1) * 64], a_t[sl, q, c, :], pc[sl, q, c, :], start=True, stop=True)], Q2)

                    yield PE(sq_fns(0, 1), R=[Nst[0], Ast[0]], W=[R0, R1_])
                    for lvl in range(1, 6):
                        nxt = 1 - cur
                        yield A(lambda e, nxt=nxt: e.activation(out=Ast[nxt][:], in_=v4(R0[:]), func=AF.Copy), R=[R0], W=[Ast[nxt]])
                        if lvl < 5:
                            yield V(lambda e, nxt=nxt: e.tensor_copy(Nst[nxt][:], v4(R1_[:])), R=[R1_], W=[Nst[nxt]])
                        pc, pn = Pc[(lvl - 1) % 2], (Pc[lvl % 2] if lvl < 5 else TTs[pp_])
                        fns = pp_fns(Ast[nxt], pc)
                        Wl = [R2]
                        Rl = [Ast[nxt], pc]
                        if lvl < 5:
                            fns += sq_fns(nxt, lvl + 1)
                            Wl += [R0] + ([R1_] if lvl + 1 < 5 else [])
                            Rl += [Nst[nxt]]
                        yield PE(fns, R=Rl, W=Wl)
                        yield V(lambda e, pc=pc, pn=pn: e.tensor_tensor(out=pn[:], in0=v4(R2[:]), in1=pc[:], op=ALU.add), R=[R2, pc], W=[pn])
                        cur = nxt
                    yield G(lambda e: e.tensor_tensor(out=B2[:], in0=rview, in1=kview, op=ALU.mult), R=[pf], W=[B2])
                    yield G(lambda e: e.tensor_tensor(out=B2[:], in0=B2[:], in1=bc(P_RK), op=ALU.mult), R=[pfm], W=[B2])
                    fns = []
                    for q in range(2):
                        for c in range(4):
                            for j in range(2):
                                sl = slice(64 * j, 64 * j + 64)
                                fns.append(lambda e, q=q, c=c, sl=sl: e.matmul(R0[sl, (q * 4 + c) * 2:(q * 4 + c) * 2 + 2], B2[sl, c, 64 * q:64 * q + 64], ones2[sl, :],
                                                                              start=True, stop=True))
                    yield PE(fns, R=[B2, ones2], W=[R0])
                    yield A(lambda e: e.activation(out=rk[:].rearrange("p q c -> p (q c)"), in_=R0[:, 0:16].rearrange("p (x two) -> p x two", two=2)[:, :, 0], func=AF.Copy), R=[R0], W=[rk])
                    return

                def seq(n):
                    pp_ = n % 2
                    arT, NB, NK, Vst, Vst32, BKst, WC, rk, B5 = arTs[pp_], NBs[pp_], NKs[pp_], Vsts[pp_], Vst32s[pp_], BKsts[pp_], WCs[pp_], rks[pp_], B5s[pp_]
                    TT = TTs[pp_]
                    Q2 = (0, 1)
                    for q in Q2:
                        yield PE(hl(lambda q, c, sl, j: [lambda e: e.matmul(Q0[sl, c * 64:(c + 1) * 64], arT[sl, c, q, 0, :], STm[sl, c, :], start=True, stop=False),
                                                         lambda e: e.matmul(Q0[sl, c * 64:(c + 1) * 64], NK[sl, q, c, 0, :], Vst[sl, q, c, :], start=False, stop=True)], (q,)),
                                 R=[arT, STm, NK, Vst], W=[Q0])
                        yield A(lambda e: e.activation(out=R1[:], in_=v3(Q0[:, 0:256]), func=AF.Copy), R=[Q0], W=[R1])
                        yield PE(hl(lambda q, c, sl, j: [lambda e: e.matmul(Q0[sl, 256 + c * 64:256 + (c + 1) * 64], TT[sl, q, c, :], R1[sl, c, :], start=True, stop=True)], (q,)),
                                 R=[TT, R1], W=[Q0])
                        yield A(lambda e: e.activation(out=Ust[:], in_=v3(Q0[:, 256:512]), func=AF.Copy), R=[Q0], W=[Ust])
                        if n >= 1:
                            yield PE(hl(lambda q, c, sl, j: [lambda e: e.matmul(Q0[sl, c * 64:(c + 1) * 64], arT[sl, c, q, 1, :], STm[sl, c, :], start=True, stop=False),
                                                             lambda e: e.matmul(Q0[sl, c * 64:(c + 1) * 64], NB[sl, q, c, 1, :], Ust[sl, c, :], start=False, stop=False),
                                                             lambda e: e.matmul(Q0[sl, c * 64:(c + 1) * 64], NK[sl, q, c, 1, :], Vst[sl, q, c, :], start=False, stop=True)], (q,)),
                                     R=[arT, STm, NB, Ust, NK, Vst], W=[Q0])
                            yield V(lambda e, q=q: e.tensor_copy(Yst[:, q], v3(Q0[:, 0:256])), R=[Q0], W=[Yst])
                        yield PE(hl(lambda q, c, sl, j: [lambda e: e.matmul(Q0[sl, 256 + c * 64:256 + (c + 1) * 64], BKst[sl, q, 0, c, :], Ust[sl, c, :], start=True, stop=False),
                                                         lambda e: e.matmul(Q0[sl, 256 + c * 64:256 + (c + 1) * 64], BKst[sl, q, 1, c, :], Vst[sl, q, c, :], start=False, stop=True)], (q,)),
                                 R=[BKst, Ust, Vst], W=[Q0])
                        yield V(lambda e, q=q: e.tensor_tensor(out=STt[:], in0=ST32[:], in1=WC[:, :, q:q + 1].broadcast_to([128, 4, 64]), op=ALU.mult), R=[WC, ST32], W=[STt])
                        if MD != F32:
                            yield V(lambda e: e.tensor_tensor(out=STm[:], in0=STt[:], in1=v3(Q0[:, 256:512]), op=ALU.add), R=[Q0, STt], W=[STm])
                        yield V(lambda e: e.tensor_tensor(out=ST32[:], in0=STt[:], in1=v3(Q0[:, 256:512]), op=ALU.add), R=[Q0, STt], W=[ST32])
                    if n == 0:
                        return
                    Y8 = Yst[:].rearrange("p q c v -> p (q c) v")
                    yc8 = yc[:].rearrange("p q c v -> p (q c) v")
                    ysq8 = ysq[:].rearrange("p q c v -> p (q c) v")
                    V32_8 = Vst32[:].rearrange("p q c v -> p (q c) v")

                    def b8(ap):
                        return ap.unsqueeze(2).broadcast_to([128, 8, 64])
                    yield V(lambda e: e.tensor_reduce(out=gst[:, 0, :], in_=Y8, axis=AX.X, op=ALU.add), R=[Yst], W=[gst])
                    yield V(lambda e: e.tensor_scalar(out=gst[:, 1, :], in0=gst[:, 0, :], scalar1=-1.0 / 64, scalar2=None, op0=ALU.mult), R=[], W=[gst])
                    yield V(lambda e: e.tensor_tensor(out=yc8, in0=Y8, in1=b8(gst[:, 1, :]), op=ALU.add), R=[Yst], W=[yc, gst])
                    yield G(lambda e: e.tensor_tensor(out=ysq8, in0=yc8, in1=yc8, op=ALU.mult), R=[yc], W=[ysq])
                    yield V(lambda e: e.tensor_reduce(out=gst[:, 2, :], in_=ysq8, axis=AX.X, op=ALU.add), R=[ysq], W=[gst])
                    yield V(lambda e: e.tensor_scalar(out=gst[:, 3, :], in0=gst[:, 2, :], scalar1=1.0 / 64, scalar2=LN_EPS, op0=ALU.mult, op1=ALU.add), R=[], W=[gst])
                    yield G(lambda e: e.tensor_tensor(out=gst[:, 4, :], in0=gst[:, 3, :], in1=cst[:, CE + 4:CE + 12], op=ALU.pow), R=[cst], W=[gst])
                    yield V(lambda e: e.tensor_tensor(out=yc8, in0=yc8, in1=b8(gst[:, 4, :]), op=ALU.mult), R=[], W=[yc, gst])
                    yield G(lambda e: e.tensor_tensor(out=yc[:], in0=yc[:], in1=lnst[:, 0].unsqueeze(1).broadcast_to([128, 2, 4, 64]), op=ALU.mult), R=[lnst], W=[yc])
                    yield G(lambda e: e.tensor_tensor(out=yc[:], in0=yc[:], in1=lnst[:, 1].unsqueeze(1).broadcast_to([128, 2, 4, 64]), op=ALU.add), R=[lnst], W=[yc])
                    yield V(lambda e: e.tensor_tensor(out=ysq8, in0=V32_8, in1=b8(rk[:].rearrange("p q c -> p (q c)")), op=ALU.mult), R=[Vst32, rk], W=[ysq])
                    yield V(lambda e: e.tensor_tensor(out=yc8, in0=yc8, in1=ysq8, op=ALU.add), R=[ysq], W=[yc])
                    yield PE(hl(lambda q, c, sl, j: [lambda e: e.matmul(Q0[sl, q * 256 + c * 64:q * 256 + (c + 1) * 64], yc[sl, q, c, :], idsl(sl, j), start=True, stop=True)], Q2),
                             R=[yc, cst], W=[Q0])
                    yTn = yTr[n % 2]
                    yield V(lambda e: e.tensor_tensor(out=qc(yTn[:]), in0=v4(Q0[:]), in1=qc(B5[:]), op=ALU.mult), R=[Q0, B5], W=[yTn])
                    yield DMA("sp", ch_st[n % 2], lambda e: e.dma_start(out=yscr[n - 1][:, 512:1024], in_=yTn[:].rearrange("p c t -> p (c t)")), R=[yTn], W=[yscr_t[n - 1]])

                import os as _os
                W_PRE, W_ATT, W_SEQ, W_HEAD = [int(v) for v in _os.environ.get("KW", "3,3,1,1").split(",")]
                run(head(0))
                if nt > 1:
                    run(par([pre(0), attention(0), head(1)], [W_PRE, W_ATT, W_HEAD]))
                else:
                    run(par([pre(0), attention(0)], [W_PRE, W_ATT]))
                _skip = _os.environ.get("KSKIP", "")
                _ord = _os.environ.get("KORD", "att,pre,seq,head").split(",")
                for i in range(nt):
                    cand = {}
                    if "seq" not in _skip:
                        cand["seq"] = lambda i=i: seq(i)
                    if i + 1 < nt:
                        if "pre" not in _skip:
                            cand["pre"] = lambda i=i: pre(i + 1)
                        if "att" not in _skip:
                            cand["att"] = lambda i=i: attention(i + 1)
                    if i + 2 < nt:
                        cand["head"] = lambda i=i: head(i + 2)
                    run([cand[k]() for k in _ord if k in cand])
                S.barrier()

        es_bw = ExitStack()
        pre_w = {}
        if "B" in phases:
            sbw_ = mk_alloc(es_bw, "bw_")
            pre_w["wg"] = sbw_("wg", [128, 8, DFF], BF16)
            pre_w["wu"] = sbw_("wu", [128, 8, DFF], BF16)

        def load_bw():
            wg, wu = pre_w["wg"], pre_w["wu"]
            ngrp_ = (NFC + 3) // 4
            wg_b = []
            wu_b = []
            for g in range(ngrp_):
                c0, c1 = 512 * g, min(512 * (g + 1), DFF)
                wg_b.append(wload_blk(wg, wg[:, :, c0:c1], w_fg.rearrange("(c p) n -> p c n", p=128)[:, :, c0:c1]))
                wu_b.append(wload_blk(wu, wu[:, :, c0:c1], w_fu.rearrange("(c p) n -> p c n", p=128)[:, :, c0:c1]))
            pre_w["wg_b"] = wg_b
            pre_w["wu_b"] = wu_b

        if "A2" in phases:
            with ExitStack() as es:
                sb = mk_alloc(es, "a2_")
                CE = 128
                cst, pfm = load_consts(sb, 128)
                wgate = sb("wgate", [128, 8, 2048], BF16)
                wba = sb("wba", [128, 4, D], BF16)
                wbr = sb("wbr", [128, 4, D], BF16)
                wgate_b = [wload_blk(wgate, wgate[:, :, 512 * g:512 * (g + 1)], w_gate.rearrange("(c p) n -> p c n", p=128)[:, :, 512 * g:512 * (g + 1)]) for g in range(4)]
                wba_b = wload_blk(wba, wba[:], w_ba.rearrange("(c p) n -> p c n", p=128))
                wbr_b = wload_blk(wbr, wbr[:], w_br.rearrange("(c p) n -> p c n", p=128))
                S.finalize(ch_w, [cst, pfm])
                if "B" in phases:
                    load_bw()
                PS = [Tile(es.enter_context(nc.psum_tensor("psb%d" % i, [128, 512], F32)), "psb%d" % i, excl=True) for i in range(8)]
                xb = [sb("xb0", [128, D]), sb("xb1", [128, D])]
                yT = [sb("yT%d" % i, [128, 8, 128], BF16) for i in range(3)]
                xs = sb("xs", [128, D])
                st4 = sb("st4", [128, 4])
                uTs = [sb("uT0", [128, 8, 128], BF16), sb("uT1", [128, 8, 128], BF16)]
                sgs = [sb("sg0", [128, 16, 128]), sb("sg1", [128, 16, 128])]
                hbg = sb("hbg", [128, 16])
                V(lambda e: e.tensor_scalar(out=hbg[:], in0=pfm[:, P_BG:P_BG + 16], scalar1=0.5, scalar2=None, op0=ALU.mult), R=[pfm], W=[hbg])
                t1 = sb("t1", [128, 8, 128])
                t2 = sb("t2", [128, 8, 128])
                mT = [sb("mT0", [128, 8, 128], BF16), sb("mT1", [128, 8, 128], BF16)]

                def front2(n):
                    xt = xb[n % 2]
                    yTn = yT[n % 3]
                    yield DMA("sp", ch_x[n % 2], lambda e: e.dma_start(out=xt[:], in_=xe[n * 128:(n + 1) * 128, :]), W=[xt])
                    yield DMA("sp", ch_y[n % 2], lambda e: e.dma_start(out=yTn[:].rearrange("p c t -> p (c t)"), in_=yscr[n - 1]), R=[yscr_t[n - 1]], W=[yTn])
                    yield from norm_T(xt, xs, st4, cst, CE, pfm, P_GMIX, [PS[0], PS[1]], uTs[n % 2])

                def mid2(n):
                    uT = uTs[n % 2]
                    sg = sgs[n % 2]
                    for g in range(4):
                        bank = PS[2 + (g % 2)]
                        fns = []
                        for i in range(4):
                            col = (4 * g + i) * 128
                            for kc in range(8):
                                fns.append(lambda e, i=i, col=col, kc=kc, bank=bank: e.matmul(bank[:, i * 128:(i + 1) * 128], wgate[:, kc, col:col + 128], uT[:, kc, :],
                                                                                             start=(kc == 0), stop=(kc == 7)))
                        yield PE(fns, R=[uT, wgate_b[g]], W=[bank])
                        for i in range(4):
                            yield A(lambda e, g=g, i=i, bank=bank: e.activation(out=sg[:, 4 * g + i, :], in_=bank[:, i * 128:(i + 1) * 128], func=AF.Tanh,
                                                                                bias=hbg[:, 4 * g + i:4 * g + i + 1], scale=0.5), R=[bank, hbg], W=[sg])

                def tail2(n):
                    yTn = yT[n % 3]
                    mTn = mT[n % 2]
                    sg = sgs[n % 2]
                    for br, (wb, off, wb_b) in enumerate(((wba, 0, wba_b), (wbr, 4, wbr_b))):
                        for hh in range(2):
                            bank = PS[4 + 2 * br + hh]
                            fns = []
                            for i in range(4):
                                fc = 4 * hh + i
                                for kc in range(4):
                                    fns.append(lambda e, i=i, fc=fc, kc=kc, bank=bank, wb=wb, off=off: e.matmul(bank[:, i * 128:(i + 1) * 128], wb[:, kc, fc * 128:(fc + 1) * 128],
                                                                                                                yTn[:, off + kc, :], start=(kc == 0), stop=(kc == 3)))
                            yield PE(fns, R=[yTn, wb_b], W=[bank])
                    for hh in range(2):
                        yield V(lambda e, hh=hh: e.scalar_tensor_tensor(out=t1[:, 4 * hh:4 * hh + 4, :], in0=sg[:, 4 * hh:4 * hh + 4, :], scalar=1.0,
                                                                        in1=PS[4 + hh][:].rearrange("p (c t) -> p c t", c=4), op0=ALU.add, op1=ALU.mult),
                                R=[PS[4 + hh], sg], W=[t1])
                        yield V(lambda e, hh=hh: e.scalar_tensor_tensor(out=t2[:, 4 * hh:4 * hh + 4, :], in0=sg[:, 8 + 4 * hh:8 + 4 * hh + 4, :], scalar=1.0,
                                                                        in1=PS[6 + hh][:].rearrange("p (c t) -> p c t", c=4), op0=ALU.add, op1=ALU.mult),
                                R=[PS[6 + hh], sg], W=[t2])
                    yield G(lambda e: e.tensor_tensor(out=t1[:], in0=t1[:], in1=t2[:], op=ALU.add), R=[t2], W=[t1])
                    yield A(lambda e: e.activation(out=mTn[:], in_=t1[:], func=AF.Copy, scale=0.5), R=[t1], W=[mTn])
                    yield DMA("sp", ch_st[n % 2], lambda e: e.dma_start(out=mscr[n - 1], in_=mTn[:].rearrange("p c t -> p (c t)")), R=[mTn], W=[mscr_t[n - 1]])

                if nt > 1:
                    run(front2(1))
                if nt > 2:
                    run([mid2(1), front2(2)])
                elif nt > 1:
                    run(mid2(1))
                for n in range(1, nt):
                    streams = [tail2(n)]
                    if n + 1 < nt:
                        streams.append(mid2(n + 1))
                    if n + 2 < nt:
                        streams.append(front2(n + 2))
                    run(streams)
                S.barrier()

        if "B" in phases:
            with ExitStack() as es:
                sb = mk_alloc(es, "b_")
                CE = 128
                cst, pfm = load_consts(sb, 128)
                gfin = sb("gfin", [128, D])
                S.dma("sp", ch_w, lambda e: e.dma_start(out=gfin[:], in_=gfind.broadcast_to([128, D])), W=[gfin])
                wo = sb("wo", [128, 8, D], BF16)
                wg, wu = pre_w["wg"], pre_w["wu"]
                wd = sb("wd", [128, NFC, D], BF16)
                wo_b = wload_blk(wo, wo[:], w_o.rearrange("(c p) n -> p c n", p=128))
                if "wg_b" not in pre_w:
                    load_bw()
                wg_b, wu_b = pre_w["wg_b"], pre_w["wu_b"]
                wd_b = [wload_blk(wd, wd[:, 11 * hh:11 * hh + 11, :], w_fd.rearrange("(c p) n -> p c n", p=128)[:, 11 * hh:11 * hh + 11, :]) for hh in range(2)]
                S.finalize(ch_w, [cst, pfm, gfin])
                PS = [Tile(es.enter_context(nc.psum_tensor("psc%d" % i, [128, 512], F32)), "psc%d" % i, excl=True) for i in range(8)]
                xb = [sb("xb0", [128, D]), sb("xb1", [128, D])]
                mT = [sb("mT0", [128, 8, 128], BF16), sb("mT1", [128, 8, 128], BF16)]
                h1s = [sb("h1a", [128, D]), sb("h1b", [128, D]), sb("h1c", [128, D])]
                xsF = sb("xsF", [128, D])
                xsB = [sb("xsB0", [128, D]), sb("xsB1", [128, D])]
                st4 = sb("st4", [128, 4])
                st4b = sb("st4b", [128, 4])
                fTs = [sb("fT0", [128, 8, 128], BF16), sb("fT1", [128, 8, 128], BF16)]
                sl_ = sb("silu", [128, 4, 128])
                aTs = [sb("aT0", [128, NFC, 128], BF16), sb("aT1", [128, NFC, 128], BF16)]

                def front3(n):
                    xt = xb[n % 2]
                    mTn = mT[n % 2]
                    h1 = h1s[n % 3]
                    yield DMA("sp", ch_x[n % 2], lambda e: e.dma_start(out=xt[:], in_=xe[n * 128:(n + 1) * 128, :]), W=[xt])
                    yield DMA("sp", ch_y[n % 2], lambda e: e.dma_start(out=mTn[:].rearrange("p c t -> p (c t)"), in_=mscr[n - 1]), R=[mscr_t[n - 1]], W=[mTn])
                    for hh in range(2):
                        yield PE([lambda e, kc=kc, hh=hh: e.matmul(PS[0][:], mTn[:, kc, :], wo[:, kc, hh * 512:(hh + 1) * 512], start=(kc == 0), stop=(kc == 7)) for kc in range(8)],
                                 R=[mTn, wo_b], W=[PS[0]])
                        yield V(lambda e, hh=hh: e.tensor_tensor(out=h1[:, hh * 512:(hh + 1) * 512], in0=PS[0][:], in1=xt[:, hh * 512:(hh + 1) * 512], op=ALU.add),
                                R=[PS[0], xt], W=[h1])
                    yield from norm_T(h1, xsF, st4, cst, CE, pfm, P_GFFN, [PS[1], PS[1]], fTs[n % 2])

                def mid3(n):
                    fT = fTs[n % 2]
                    aT = aTs[n % 2]
                    ngrp = (NFC + 3) // 4
                    for g in range(ngrp):
                        nchunk = min(4, NFC - 4 * g)
                        bg = PS[2 + 2 * (g % 2)]
                        bu = PS[3 + 2 * (g % 2)]
                        for bank, wt, wt_b in ((bg, wg, wg_b[g]), (bu, wu, wu_b[g])):
                            fns = []
                            for i in range(nchunk):
                                fc = 4 * g + i
                                for kc in range(8):
                                    fns.append(lambda e, i=i, fc=fc, kc=kc, bank=bank, wt=wt: e.matmul(bank[:, i * 128:(i + 1) * 128], wt[:, kc, fc * 128:(fc + 1) * 128], fT[:, kc, :],
                                                                                                       start=(kc == 0), stop=(kc == 7)))
                            yield PE(fns, R=[fT, wt_b], W=[bank])
                        yield A(lambda e: e.activation(out=sl_[:, 0:nchunk, :], in_=bg[:, 0:nchunk * 128].rearrange("p (c t) -> p c t", c=nchunk), func=AF.Tanh, scale=0.5),
                                R=[bg], W=[sl_])
                        yield V(lambda e: e.scalar_tensor_tensor(out=sl_[:, 0:nchunk, :], in0=sl_[:, 0:nchunk, :], scalar=1.0,
                                                                 in1=bg[:, 0:nchunk * 128].rearrange("p (c t) -> p c t", c=nchunk), op0=ALU.add, op1=ALU.mult), R=[bg], W=[sl_])
                        yield V(lambda e: e.scalar_tensor_tensor(out=aT[:, 4 * g:4 * g + nchunk, :], in0=sl_[:, 0:nchunk, :], scalar=0.5,
                                                                 in1=bu[:, 0:nchunk * 128].rearrange("p (c t) -> p c t", c=nchunk), op0=ALU.mult, op1=ALU.mult), R=[bu, sl_], W=[aT])

                def tail3(n):
                    h1 = h1s[n % 3]
                    aT = aTs[n % 2]
                    o = xsB[n % 2]
                    for hh in range(2):
                        yield PE([lambda e, fc=fc, hh=hh: e.matmul(PS[6 + hh][:], aT[:, fc, :], wd[:, fc, hh * 512:(hh + 1) * 512], start=(fc == 0), stop=(fc == NFC - 1)) for fc in range(NFC)],
                                 R=[aT] + wd_b, W=[PS[6 + hh]])
                        yield V(lambda e, hh=hh: e.tensor_tensor(out=h1[:, hh * 512:(hh + 1) * 512], in0=PS[6 + hh][:], in1=h1[:, hh * 512:(hh + 1) * 512], op=ALU.add),
                                R=[PS[6 + hh]], W=[h1])
                    yield A(lambda e: e.activation(out=o[:], in_=h1[:], func=AF.Square, accum_out=st4b[:, 0:1]), R=[h1], W=[o, st4b])
                    yield V(lambda e: e.tensor_scalar(out=st4b[:, 1:2], in0=st4b[:, 0:1], scalar1=1.0 / D, scalar2=RMS_EPS, op0=ALU.mult, op1=ALU.add), R=[], W=[st4b])
                    yield G(lambda e: e.tensor_tensor(out=st4b[:, 2:3], in0=st4b[:, 1:2], in1=cst[:, CE + 4:CE + 5], op=ALU.pow), R=[cst], W=[st4b])
                    yield A(lambda e: e.activation(out=o[:], in_=h1[:], func=AF.Identity, scale=st4b[:, 2:3], bias=cst[:, CE + 2:CE + 3]), R=[h1, cst], W=[o, st4b])
                    yield G(lambda e: e.tensor_tensor(out=o[:], in0=o[:], in1=gfin[:], op=ALU.mult), R=[gfin], W=[o])
                    yield DMA("sp", ch_st[n % 2], lambda e: e.dma_start(out=outd[(n - 1) * 128:n * 128, :], in_=o[:]), R=[o], W=[])

                if nt > 1:
                    run(front3(1))
                if nt > 2:
                    run([mid3(1), front3(2)])
                elif nt > 1:
                    run(mid3(1))
                for n in range(1, nt):
                    streams = [tail3(n)]
                    if n + 1 < nt:
                        streams.append(mid3(n + 1))
                    if n + 2 < nt:
                        streams.append(front3(n + 2))
                    run(streams)
                S.barrier()
        else:
            S.barrier()
        es_bw.close()
    return nc


QPERM = [0, 4, 1, 5, 2, 6, 3, 7]


def make_consts():
    c = np.zeros((128, C_END), np.float32)
    c[:, C_ID:C_ID + 128] = np.eye(128, dtype=np.float32)
    s = np.arange(64)
    for j in range(2):
        rows = slice(64 * j, 64 * j + 64)
        c[rows, C_MB:C_MB + 64] = (s[None, :] > s[:, None])
        c[rows, C_MB + 64:C_MB + 128] = (s[None, :] >= s[:, None])
        c[rows, C_ML:C_ML + 64] = (s[None, :] < s[:, None])
        c[rows, C_I64:C_I64 + 64] = np.eye(64)
        c[rows, C_OBD + 64 * j:C_OBD + 64 * j + 64] = 1.0
        c[rows, C_TRI:C_TRI + 64] = CFAC * (s[:, None] <= s[None, :])
        c[rows, C_TRI + 64:C_TRI + 128] = CFAC * (s[:, None] < s[None, :])
    c[64:128, C_TRI0:C_TRI0 + 128] = c[64:128, C_TRI:C_TRI + 128]
    c[64:64 + 48, C_TRI0:C_TRI0 + 128] = 0.0
    i = np.arange(128)
    own = np.where(i[None, :] <= i[:, None], 0.0, NEG)
    prev = np.where(i[None, :] > i[:, None], 0.0, NEG)
    full = np.full((128, 128), NEG)
    for var, (a, b) in enumerate(((own, prev), (prev, own), (full, own))):
        base = C_AM + 272 * var
        c[:, base:base + 128] = a
        c[:, base + 128:base + 256] = b
        c[:, base + 256:base + 272] = 0.0
    half = 8
    inv_freq = np.power(np.float32(500000.0), -np.arange(half, dtype=np.float32) * np.float32(2.0 / 16)).astype(np.float32)
    for n in range(NTILES):
        pos = (n * 128 + np.arange(128) - 112).astype(np.float32)
        ang = (pos[:, None] * inv_freq[None, :]).astype(np.float32)
        c[:, C_ROPE + 16 * n:C_ROPE + 16 * n + 8] = np.cos(ang)
        c[:, C_ROPE + 16 * n + 8:C_ROPE + 16 * n + 16] = np.sin(ang)
    return c


def prep_shared(inp):
    f = np.float32
    w_in = np.asarray(inp["w_in"][0], f)
    b_in = np.asarray(inp["b_in"][0], f)
    qcols = np.concatenate([np.arange(h * 64, (h + 1) * 64) for h in QPERM])
    w_qkv = np.ascontiguousarray(np.concatenate([w_in[:, qcols], w_in[:, 512:768]], axis=1))
    b_qkv = np.concatenate([b_in[qcols], b_in[512:768]])
    R0 = 768
    w_fm = np.zeros((D, 2048), f)
    b_fm = np.zeros((2048,), f)
    mix = np.asarray(inp["rwkv_mix"][0], f)
    mix_fm = np.zeros((2048,), f)

    def put(dst0, src0, n):
        w_fm[:, dst0:dst0 + n] = w_in[:, R0 + src0:R0 + src0 + n]
        b_fm[dst0:dst0 + n] = b_in[R0 + src0:R0 + src0 + n]
        mix_fm[dst0:dst0 + n] = mix[src0:src0 + n]
    put(0, 0, 1536)
    put(1536, 1536, 64)
    put(1664, 1600, 64)
    put(1792, 1664, 128)
    put(1920, 1792, 32)
    G0 = 768 + 1824
    w_gate = np.ascontiguousarray(w_in[:, G0:G0 + 2048])
    b_gate = b_in[G0:G0 + 2048]
    rows_perm = qcols
    sh = {
        "w_qkv": w_qkv, "w_fm": w_fm, "w_gate": w_gate,
        "w_ba": np.ascontiguousarray(np.asarray(inp["w_br_attn"][0], f)[rows_perm, :]),
        "w_br": np.ascontiguousarray(np.asarray(inp["w_br_rwkv"][0], f)),
        "w_o": np.ascontiguousarray(np.asarray(inp["w_o"][0], f)),
        "w_fg": np.ascontiguousarray(np.asarray(inp["w_ffn_gate"][0], f)),
        "w_fu": np.ascontiguousarray(np.asarray(inp["w_ffn_up"][0], f)),
        "w_fd": np.ascontiguousarray(np.asarray(inp["w_ffn_down"][0], f)),
        "w2": np.ascontiguousarray(np.asarray(inp["rwkv_w2"][0], f)),
        "a2": np.ascontiguousarray(np.asarray(inp["rwkv_a2"][0], f)),
    }
    g2p = np.zeros((256, 512), f)
    g2p[0:160] = np.asarray(inp["rwkv_g2"][0], f)
    sh["g2p"] = g2p
    pfm = np.zeros((128, P_END), f)

    def fm(vec, ncol):
        return np.asarray(vec, f).reshape(ncol, 128).T
    pfm[:, P_GMIX:P_GMIX + 8] = fm(inp["norm_mix_g"][0], 8)
    pfm[:, P_GFFN:P_GFFN + 8] = fm(inp["norm_ffn_g"][0], 8)
    pfm[:, P_BFM:P_BFM + 16] = fm(b_fm, 16)
    pfm[:, P_BG:P_BG + 16] = fm(b_gate, 16)
    pfm[:, P_MIX:P_MIX + 16] = fm(mix_fm, 16)
    pfm[:, P_A0:P_A0 + 4] = fm(inp["rwkv_a0"][0], 4)
    pfm[:, P_KK:P_KK + 4] = fm(inp["rwkv_k_k"][0], 4)
    pfm[:, P_KA:P_KA + 4] = fm(inp["rwkv_k_a"][0], 4)
    pfm[:, P_RK:P_RK + 4] = fm(np.asarray(inp["rwkv_r_k"][0], f).reshape(-1), 4)
    sh["pfm"] = pfm
    rowsA = np.zeros((1, RA_END), f)
    rowsA[0, RA_BQ:RA_BQ + 768] = b_qkv
    rowsA[0, RA_W0:RA_W0 + 512] = np.asarray(inp["rwkv_w0"][0], f)
    rowsA[0, RA_SK:RA_SK + 8] = np.asarray(inp["attn_sinks"][0], f)[QPERM]
    sh["rowsA"] = rowsA
    sh["gfin"] = np.asarray(inp["norm_final_g"], f).reshape(1, D).copy()
    lnst = np.zeros((128, 2, 4, 64), f)
    for a, key in enumerate(("rwkv_ln_w", "rwkv_ln_b")):
        v = np.asarray(inp[key][0], f).reshape(4, 2, 64)
        for j in range(2):
            lnst[64 * j:64 * j + 64, a, :, :] = v[None, :, j, :]
    sh["lnst"] = lnst.reshape(128, -1)
    sh["cst"] = make_consts()
    return sh


def prep_xe(inp, b):
    xe = np.zeros((NTILES * 128, D), np.float32)
    xe[112:128] = np.asarray(inp["meta_tokens"], np.float32)
    xe[128:] = np.asarray(inp["x"][b], np.float32)
    return xe


_NC_CACHE = {}


def kernel(**inputs):
    n = 8
    sh = prep_shared(inputs)
    in_maps = []
    for b in range(n):
        m = dict(sh)
        m["xe"] = prep_xe(inputs, b)
        in_maps.append(m)
    if "nc" not in _NC_CACHE:
        _NC_CACHE["nc"] = build_program()
    res = run_bass_kernel_spmd(_NC_CACHE["nc"], in_maps, core_ids=list(range(n)))
    out = np.stack([np.asarray(r["out"], np.float32).reshape(4096, D) for r in res.results], axis=0)
    return out
```

```python
import numpy as np
import ml_dtypes
from contextlib import ExitStack
import concourse.bass as bass
import concourse.mybir as mybir
from concourse.bass_utils import run_bass_kernel_spmd

F32 = mybir.dt.float32
BF16 = mybir.dt.bfloat16
AF = mybir.ActivationFunctionType
ALU = mybir.AluOpType
AX = mybir.AxisListType

NTILES = 33
D = 1024
DFF = 2816
NFC = 22
RMS_EPS = 1e-6
LN_EPS = 64e-5
CFAC = -float(np.exp(-0.5))
NEG = -1e30
MD = BF16

C_ID = 0
C_MB = 128
C_ML = 256
C_I64 = 320
C_OBD = 384
C_TRI = 512
C_TRI0 = 640
C_AM = 768
C_ROPE = 768 + 816
C_END = C_ROPE + 33 * 16
P_GMIX, P_GFFN, P_BFM, P_BG, P_MIX, P_A0, P_KK, P_KA, P_RK, P_END = 0, 8, 16, 32, 48, 64, 68, 72, 76, 80
RA_BQ, RA_W0, RA_SK, RA_END = 0, 768, 1280, 1288


class Tile:
    def __init__(self, t, name, excl=False):
        self.t = t
        self.name = name
        self.w = None
        self.r = {}
        self.excl = excl
        self.tw = 0.0
        self.tr = 0.0
        self.weng = None

    def __getitem__(self, i):
        return self.t[i]


class Chan:
    def __init__(self, sem, key):
        self.sem = sem
        self.key = key
        self.count = 0


class Sched:
    def __init__(self, nc, es):
        self.nc = nc
        self.es = es
        self.E = {}
        for name, eng in (("pe", nc.tensor), ("act", nc.scalar), ("dve", nc.vector),
                          ("pool", nc.gpsimd), ("sp", nc.sync)):
            sem = es.enter_context(nc.semaphore("sem_" + name))
            self.E[name] = dict(eng=eng, sem=sem, count=0, seen={}, name=name)
        self.chans = []

    def chan(self, name):
        c = Chan(self.es.enter_context(self.nc.semaphore("ch_" + name)), "ch_" + name)
        self.chans.append(c)
        return c

    def _waits(self, E, R, W):
        deps = {}

        def add(d):
            key, val, sem = d
            if key not in deps or deps[key][0] < val:
                deps[key] = (val, sem)
        for t in R:
            if t.w is not None:
                add(t.w)
            if t.excl:
                for key, (val, sem) in t.r.items():
                    if key != E["name"]:
                        add((key, val, sem))
        for t in W:
            if t.w is not None:
                add(t.w)
            for key, (val, sem) in t.r.items():
                add((key, val, sem))
        for key, (val, sem) in deps.items():
            if key == "pe" and E["name"] == "pe":
                continue
            if E["seen"].get(key, 0) < val:
                E["eng"].wait_ge(sem, val)
                E["seen"][key] = val

    def op(self, ename, fns, R=(), W=()):
        E = self.E[ename]
        self._waits(E, R, W)
        if not isinstance(fns, (list, tuple)):
            fns = [fns]
        inst = None
        for f in fns:
            inst = f(E["eng"])
        E["count"] += 1
        inst.then_inc(E["sem"], 1)
        for t in W:
            t.w = (ename, E["count"], E["sem"])
            t.r = {}
        for t in R:
            if t not in W:
                t.r[ename] = (E["count"], E["sem"])

    def dma(self, qname, chan, fn, R=(), W=()):
        E = self.E[qname]
        self._waits(E, R, W)
        inst = fn(E["eng"])
        chan.count += 16
        inst.then_inc(chan.sem, 16)
        for t in W:
            t.w = (chan.key, chan.count, chan.sem)
            t.r = {}
        for t in R:
            t.r[chan.key] = (chan.count, chan.sem)

    def finalize(self, chan, tiles):
        for t in tiles:
            t.w = (chan.key, chan.count, chan.sem)

    def barrier(self):
        for name, E in self.E.items():
            for oname, O in self.E.items():
                if oname == name or O["count"] == 0:
                    continue
                if E["seen"].get(oname, 0) < O["count"]:
                    E["eng"].wait_ge(O["sem"], O["count"])
                    E["seen"][oname] = O["count"]
            for c in self.chans:
                if c.count and E["seen"].get(c.key, 0) < c.count:
                    E["eng"].wait_ge(c.sem, c.count)
                    E["seen"][c.key] = c.count


def build_program(nt=NTILES, phases=("A1", "A2", "B"), dbg=None, dbg_n=-1, md=None, scr_ext=False, stop=None):
    global MD
    if md is not None:
        MD = md
    nc = bass.Bass("TRN2", target_bir_lowering=False)

    def din(name, shape, dt=F32):
        return nc.dram_tensor(name, list(shape), dt, kind="ExternalInput").ap()

    xe = din("xe", [NTILES * 128, D])
    w_qkv = din("w_qkv", [D, 768])
    w_fm = din("w_fm", [D, 2048])
    w_gate = din("w_gate", [D, 2048])
    w_ba = din("w_ba", [512, D])
    w_br = din("w_br", [512, D])
    w_o = din("w_o", [D, D])
    w_fg = din("w_fg", [D, DFF])
    w_fu = din("w_fu", [D, DFF])
    w_fd = din("w_fd", [DFF, D])
    w2d = din("w2", [64, 512])
    a2d = din("a2", [64, 512])
    g2d = din("g2p", [256, 512])
    pfmd = din("pfm", [128, P_END])
    rowsAd = din("rowsA", [1, RA_END])
    gfind = din("gfin", [1, D])
    lnstd = din("lnst", [128, 2 * 4 * 64])
    cstd = din("cst", [128, C_END])
    outd = nc.dram_tensor("out", [(NTILES - 1) * 128, D], F32, kind="ExternalOutput").ap()
    skind = "ExternalOutput" if scr_ext else "Internal"
    yscr = nc.dram_tensor("yscr", [NTILES - 1, 128, 8 * 128], BF16, kind=skind).ap()
    mscr = nc.dram_tensor("mscr", [NTILES - 1, 128, 8 * 128], BF16, kind=skind).ap()
    dbg_out = {}
    if dbg:
        for name, shape in dbg.items():
            dbg_out[name] = nc.dram_tensor("dbg_" + name, list(shape), F32, kind="ExternalOutput").ap()

    with ExitStack() as es0:
        S = Sched(nc, es0)
        ch_w = S.chan("w")
        ch_x = [S.chan("x0"), S.chan("x1")]
        ch_y = [S.chan("y0"), S.chan("y1")]
        ch_st = [S.chan("s0"), S.chan("s1")]
        ch_dbg = S.chan("dbg")
        yscr_t = [Tile(None, "yscr%d" % i) for i in range(NTILES - 1)]
        mscr_t = [Tile(None, "mscr%d" % i) for i in range(NTILES - 1)]

        ST = {"defer": False, "small": False}
        eng_free = {"pe": 0.0, "act": 0.0, "dve": 0.0, "pool": 0.0, "sp": 0.0}
        import os as _os0
        import random as _random
        _cfg = _os0.environ.get("KCFG", "0,0.3,0.1,0.03,0.5,0.0").split(",")
        _rng = _random.Random(int(_cfg[0]))
        HOP = float(_cfg[1]); KPE = float(_cfg[2]); KPS = float(_cfg[3]); KVD = float(_cfg[4]); JIT = float(_cfg[5])

        class Op:
            __slots__ = ("ename", "fns", "R", "W", "dur", "chan")

            def __init__(self, ename, fns, R, W, dur, chan=None):
                self.ename = ename; self.fns = fns; self.R = R; self.W = W; self.dur = dur; self.chan = chan

        def est_start(op):
            t = eng_free[op.ename]
            for x in op.R:
                tw = getattr(x, "tw", 0.0)
                if x.weng != op.ename:
                    tw += HOP
                t = max(t, tw)
                if x.excl:
                    t = max(t, getattr(x, "tr", 0.0) + HOP)
            for x in op.W:
                t = max(t, getattr(x, "tw", 0.0) + (HOP if x.weng != op.ename else 0.0), getattr(x, "tr", 0.0) + HOP)
            return t

        def emit(op):
            t0 = est_start(op)
            t1 = t0 + op.dur
            if op.chan is None:
                S.op(op.ename, op.fns, op.R, op.W)
                eng_free[op.ename] = t1
            else:
                S.dma(op.ename, op.chan, op.fns, op.R, op.W)
                eng_free[op.ename] = t0 + 0.1
                t1 = t0 + 2.5
            for x in op.W:
                x.tw = t1; x.weng = op.ename; x.tr = 0.0
            for x in op.R:
                x.tr = max(getattr(x, "tr", 0.0), t1)

        def mkop(ename, fns, R, W, dur, chan=None):
            if JIT > 0:
                dur = dur * (1.0 + JIT * (_rng.random() - 0.5))
            op = Op(ename, fns, list(R), list(W), dur, chan)
            if ST["defer"]:
                return op
            emit(op)
            return None

        def V(fn, R=(), W=(), d=None):
            d = KVD if d is None else d
            return mkop("dve", fn, R, W, d)

        def A(fn, R=(), W=(), d=None):
            d = KVD if d is None else d
            return mkop("act", fn, R, W, d)

        def G(fn, R=(), W=(), d=1.2):
            return mkop("pool", fn, R, W, d)

        def PE(fns, R=(), W=(), d=None):
            n_ = len(fns) if isinstance(fns, (list, tuple)) else 1
            if d is None:
                d = n_ * (KPS if ST["small"] else KPE) + 0.1
            ST["small"] = False
            return mkop("pe", fns, R, W, d)

        def DMA(qname, chan, fn, R=(), W=()):
            return mkop(qname, fn, R, W, 2.5, chan)

        def dump(name, tile_ap, tiles):
            if name in dbg_out:
                S.dma("sp", ch_dbg, lambda e: e.dma_start(out=dbg_out[name], in_=tile_ap), R=tiles, W=[])

        def run(gens, bonus=None):
            if not isinstance(gens, (list, tuple)):
                gens = [gens]
            if bonus is None:
                bonus = [0.0] * len(gens)
            ST["defer"] = True
            heads = []
            for g in gens:
                heads.append(next(g, None))
            try:
                while True:
                    best = None
                    bt = None
                    for i, h in enumerate(heads):
                        if h is None:
                            continue
                        t = est_start(h) - bonus[i]
                        if bt is None or t < bt:
                            bt = t; best = i
                    if best is None:
                        break
                    ST["defer"] = False
                    emit(heads[best])
                    ST["defer"] = True
                    h = next(gens[best], None)
                    while h is None:
                        try:
                            h = next(gens[best])
                        except StopIteration:
                            h = None
                            break
                    heads[best] = h
            finally:
                ST["defer"] = False

        def par(gens, weights=None):
            return list(gens)

        def mk_alloc(es, pfx):
            def sb(name, shape, dt=F32):
                return Tile(es.enter_context(nc.sbuf_tensor(pfx + name, list(shape), dt)), pfx + name)
            return sb

        def norm_T(x, xs, st4, cst, ce, pfm, gcol, TR2, uT):
            yield A(lambda e: e.activation(out=xs[:], in_=x[:], func=AF.Square, accum_out=st4[:, 0:1]), R=[x], W=[xs, st4])
            yield V(lambda e: e.tensor_scalar(out=st4[:, 1:2], in0=st4[:, 0:1], scalar1=1.0 / D, scalar2=RMS_EPS, op0=ALU.mult, op1=ALU.add), R=[st4], W=[st4])
            yield G(lambda e: e.tensor_tensor(out=st4[:, 2:3], in0=st4[:, 1:2], in1=cst[:, ce + 4:ce + 5], op=ALU.pow), R=[st4, cst], W=[st4])
            yield A(lambda e: e.activation(out=xs[:], in_=x[:], func=AF.Identity, scale=st4[:, 2:3], bias=cst[:, ce + 2:ce + 3]),
                    R=[x, st4, cst], W=[xs])
            for h in range(2):
                yield PE([lambda e, c=c: e.transpose(TR2[h][:, (c % 4) * 128:(c % 4 + 1) * 128], xs[:, c * 128:(c + 1) * 128], cst[:, C_ID:C_ID + 128])
                          for c in range(4 * h, 4 * h + 4)], R=[xs, cst], W=[TR2[h]])
                yield V(lambda e, h=h: e.tensor_tensor(out=uT[:, 4 * h:4 * h + 4, :], in0=TR2[h][:].rearrange("p (c k) -> p c k", k=128),
                                                       in1=pfm[:, gcol + 4 * h:gcol + 4 * h + 4].unsqueeze(2).broadcast_to([128, 4, 128]), op=ALU.mult),
                        R=[TR2[h], pfm], W=[uT])

        def load_consts(sb, ncols):
            cst = sb("cst", [128, ncols + 12])
            pfm = sb("pfm", [128, P_END])
            G(lambda e: e.memset(cst[:, ncols:ncols + 1], RMS_EPS), W=[cst])
            G(lambda e: e.memset(cst[:, ncols + 1:ncols + 2], LN_EPS), W=[cst])
            G(lambda e: e.memset(cst[:, ncols + 2:ncols + 4], 0.0), W=[cst])
            G(lambda e: e.memset(cst[:, ncols + 4:ncols + 12], -0.5), W=[cst])
            S.dma("sp", ch_w, lambda e: e.dma_start(out=cst[:, 0:ncols], in_=cstd[:, 0:ncols]), W=[cst])
            S.dma("sp", ch_w, lambda e: e.dma_start(out=pfm[:], in_=pfmd), W=[pfm])
            return cst, pfm

        wl_n = [0]

        def wload_blk(tile_, out_ap, in_ap):
            wl_n[0] += 1
            ch = S.chan("wb%d" % wl_n[0])
            t = Tile(tile_.t, "%s_blk%d" % (tile_.name, wl_n[0]))
            S.dma("pool", ch, lambda e: e.dma_start(out=out_ap, in_=in_ap), W=[t])
            return t

        def wload(tile_, out_ap, in_ap):
            S.dma("pool", ch_w, lambda e: e.dma_start(out=out_ap, in_=in_ap), W=[tile_])

        if "A1" in phases:
            with ExitStack() as es:
                sb = mk_alloc(es, "a1_")
                CE = C_END
                cst, pfm = load_consts(sb, C_END)
                rowsA = sb("rowsA", [128, RA_END])
                S.dma("sp", ch_w, lambda e: e.dma_start(out=rowsA[:], in_=rowsAd.broadcast_to([128, RA_END])), W=[rowsA])
                lnst = sb("lnst", [128, 2, 4, 64])
                S.dma("sp", ch_w, lambda e: e.dma_start(out=lnst[:].rearrange("p a c v -> p (a c v)"), in_=lnstd), W=[lnst])
                wqkv = sb("wqkv", [128, 8, 768], BF16)
                wfm = sb("wfm", [128, 8, 2048], BF16)
                w2 = sb("w2", [64, 512])
                a2 = sb("a2", [64, 512])
                g2 = sb("g2", [128, 2, 512])
                S.dma("sp", ch_w, lambda e: e.dma_start(out=w2[:], in_=w2d), W=[w2])
                S.dma("sp", ch_w, lambda e: e.dma_start(out=a2[:], in_=a2d), W=[a2])
                S.dma("sp", ch_w, lambda e: e.dma_start(out=g2[:], in_=g2d.rearrange("(c p) n -> p c n", p=128)), W=[g2])
                wqkv_b = wload_blk(wqkv, wqkv[:], w_qkv.rearrange("(c p) n -> p c n", p=128))
                wfm_b = [wload_blk(wfm, wfm[:, :, 512 * g:512 * (g + 1)], w_fm.rearrange("(c p) n -> p c n", p=128)[:, :, 512 * g:512 * (g + 1)]) for g in range(4)]
                identb = sb("identb", [128, 128], BF16)
                ones2 = sb("ones2", [128, 2])
                G(lambda e: e.memset(ones2[:], 1.0), W=[ones2])
                S.finalize(ch_w, [cst, pfm, rowsA, lnst, w2, a2, g2])
                V(lambda e: e.tensor_copy(identb[:], cst[:, C_ID:C_ID + 128]), R=[cst], W=[identb])

                PS = [Tile(es.enter_context(nc.psum_tensor("ps%d" % i, [128, 512], F32)), "ps%d" % i, excl=True) for i in range(8)]
                H0, Q0, A0, A1_, A2_, R0, R1_, R2 = PS
                H1 = H0

                xb = [sb("xb0", [128, D]), sb("xb1", [128, D])]
                xs = sb("xs", [128, D])
                st4 = sb("st4", [128, 4])
                uT = sb("uT", [128, 8, 128], BF16)
                stg = sb("stg", [128, 16, 129])
                pfs = [sb("pf0", [128, 16, 128]), sb("pf1", [128, 16, 128])]
                qkvs = [sb("qkv0", [128, 768]), sb("qkv1", [128, 768])]
                rtmp = sb("rtmp", [128, 4, 10, 8])
                qT = sb("qT", [128, 4, 128], BF16)
                Kbuf = sb("Kbuf", [128, 272], BF16)
                Vbuf = sb("Vbuf", [128, 3, 128], BF16)
                Pb = [sb("Pb0", [128, 272], BF16), sb("Pb1", [128, 272], BF16)]
                PT = [sb("PT0", [128, 3, 128], BF16), sb("PT1", [128, 3, 128], BF16)]
                sm = sb("sm", [128, 5, 8])
                yat = sb("yat", [128, 8, 64])
                yTa = [sb("yTa0", [128, 4, 128], BF16), sb("yTa1", [128, 4, 128], BF16)]
                yTr = [sb("yTr0", [128, 4, 128], BF16), sb("yTr1", [128, 4, 128], BF16)]
                th = sb("th", [64, 128])
                sgd = sb("sgd", [128, 2, 128])
                B1 = sb("B1", [128, 4, 128]); B2 = sb("B2", [128, 4, 128]); B3 = sb("B3", [128, 4, 128])
                B4 = sb("B4", [128, 4, 128])
                arTs = [sb("arT%d" % i, [128, 4, 2, 2, 64], MD) for i in range(2)]
                BT = sb("BT", [128, 4, 128], MD); KT = sb("KT", [128, 4, 128], MD)
                BH = sb("BH", [128, 4, 128], MD); KH = sb("KH", [128, 4, 128], MD)
                cumC = sb("cumC", [128, 4, 2])
                WCs = [sb("WC%d" % i, [128, 4, 2]) for i in range(2)]
                rks = [sb("rk%d" % i, [128, 2, 4]) for i in range(2)]
                B5s = [sb("B5_%d" % i, [128, 4, 128]) for i in range(2)]
                TTs = [sb("TT%d" % i, [128, 2, 4, 64], MD) for i in range(2)]
                Ast = [sb("Ast0", [128, 2, 4, 64], MD), sb("Ast1", [128, 2, 4, 64], MD)]
                Nst = [sb("Nst0", [128, 2, 4, 64], MD), sb("Nst1", [128, 2, 4, 64], MD)]
                NBs = [sb("NB%d" % i, [128, 2, 4, 2, 64], MD) for i in range(2)]
                NKs = [sb("NK%d" % i, [128, 2, 4, 2, 64], MD) for i in range(2)]
                Pc = [sb("Pc0", [128, 2, 4, 64], MD), sb("Pc1", [128, 2, 4, 64], MD)]
                Vst32s = [sb("Vst32_%d" % i, [128, 2, 4, 64]) for i in range(2)]
                Vsts = [sb("Vst_%d" % i, [128, 2, 4, 64], MD) for i in range(2)] if MD != F32 else Vst32s
                BKsts = [sb("BKst%d" % i, [128, 2, 2, 4, 64], MD) for i in range(2)]
                R1 = sb("R1", [128, 4, 64], MD); Ust = sb("Ust", [128, 4, 64], MD)
                Yst = sb("Yst", [128, 2, 4, 64]); yc = sb("yc", [128, 2, 4, 64]); ysq = sb("ysq", [128, 2, 4, 64])
                ST32 = sb("ST32", [128, 4, 64])
                STt = sb("STt", [128, 4, 64])
                STm = sb("STm", [128, 4, 64], MD) if MD != F32 else ST32
                gst = sb("gst", [128, 6, 8])
                hb = sb("hb", [128, 4])
                V(lambda e: e.tensor_scalar(out=hb[:], in0=pfm[:, P_A0:P_A0 + 4], scalar1=0.5, scalar2=None, op0=ALU.mult), R=[pfm], W=[hb])
                G(lambda e: e.memset(ST32[:], 0.0), W=[ST32])
                if MD != F32:
                    G(lambda e: e.memset(STm[:], 0.0), W=[STm])
                G(lambda e: e.memset(stg[:], 0.0), W=[stg])
                G(lambda e: e.memset(Vbuf[:], 0.0), W=[Vbuf])
                G(lambda e: e.memset(Kbuf[:], 0.0), W=[Kbuf])

                def v3(ap2d, k=64):
                    return ap2d.rearrange("p (c k) -> p c k", k=k)

                def v4(ap2d):
                    return ap2d.rearrange("p (q c k) -> p q c k", q=2, c=4)

                def cq(t):
                    return t.rearrange("p c (q t) -> p c q t", q=2)

                def qc(t):
                    return t.rearrange("p c (q t) -> p q c t", q=2)

                ID0 = C_ID if MD == F32 else 0
                idm = cst if MD == F32 else identb

                def idsl(sl, j):
                    return cst[sl, C_ID + 64 * j:C_ID + 64 * j + 64]

                def idm_sl(sl, j):
                    return idm[sl, ID0 + 64 * j:ID0 + 64 * j + 64]

                def hl(fn, qs=(0,)):
                    ST["small"] = True
                    out = []
                    for q in qs:
                        for c in range(4):
                            for j in range(2):
                                out += fn(q, c, slice(64 * j, 64 * j + 64), j)
                    return out

                def head(n):
                    xt = xb[n % 2]
                    pf = pfs[n % 2]
                    qkv = qkvs[n % 2]
                    yield DMA("sp", ch_x[n % 2], lambda e: e.dma_start(out=xt[:], in_=xe[n * 128:(n + 1) * 128, :]), W=[xt])
                    yield from norm_T(xt, xs, st4, cst, CE, pfm, P_GMIX, [H0, H1], uT)
                    yield PE([lambda e, kc=kc: e.matmul(H0[:, 0:512], uT[:, kc, :], wqkv[:, kc, 0:512], start=(kc == 0), stop=(kc == 7)) for kc in range(8)],
                             R=[uT, wqkv_b], W=[H0])
                    yield V(lambda e: e.tensor_tensor(out=qkv[:, 0:512], in0=H0[:, 0:512], in1=rowsA[:, RA_BQ:RA_BQ + 512], op=ALU.add), R=[H0, rowsA], W=[qkv])
                    yield PE([lambda e, kc=kc: e.matmul(H1[:, 0:256], uT[:, kc, :], wqkv[:, kc, 512:768], start=(kc == 0), stop=(kc == 7)) for kc in range(8)],
                             R=[uT, wqkv_b], W=[H1])
                    yield V(lambda e: e.tensor_tensor(out=qkv[:, 512:768], in0=H1[:, 0:256], in1=rowsA[:, RA_BQ + 512:RA_BQ + 768], op=ALU.add), R=[H1, rowsA], W=[qkv])
                    for g in range(4):
                        bank = (H0, H1)[g % 2]
                        fns = []
                        for i in range(4):
                            col = (4 * g + i) * 128
                            for kc in range(8):
                                fns.append(lambda e, i=i, col=col, kc=kc, bank=bank: e.matmul(bank[:, i * 128:(i + 1) * 128], wfm[:, kc, col:col + 128], uT[:, kc, :],
                                                                                             start=(kc == 0), stop=(kc == 7)))
                        yield PE(fns, R=[uT, wfm_b[g]], W=[bank])
                        yield V(lambda e, g=g, bank=bank: e.tensor_tensor(out=stg[:, 4 * g:4 * g + 4, 1:129], in0=v3(bank[:], 128),
                                                                          in1=pfm[:, P_BFM + 4 * g:P_BFM + 4 * g + 4].unsqueeze(2).broadcast_to([128, 4, 128]), op=ALU.add),
                                R=[bank, pfm], W=[stg])
                    if n == 0:
                        yield G(lambda e: e.memset(stg[:, :, 1:113], 0.0), W=[stg])
                    yield G(lambda e: e.tensor_tensor(out=pf[:], in0=stg[:, :, 0:128], in1=stg[:, :, 1:129], op=ALU.subtract), R=[stg], W=[pf], d=3.6)
                    yield G(lambda e: e.tensor_tensor(out=pf[:], in0=pf[:], in1=pfm[:, P_MIX:P_MIX + 16].unsqueeze(2).broadcast_to([128, 16, 128]), op=ALU.mult),
                            R=[pfm], W=[pf], d=3.6)
                    yield G(lambda e: e.tensor_tensor(out=pf[:], in0=pf[:], in1=stg[:, :, 1:129], op=ALU.add), R=[stg], W=[pf], d=3.6)
                    yield G(lambda e: e.tensor_copy(stg[:, :, 0:1], stg[:, :, 128:129]), R=[], W=[stg])

                def attention(n):
                    qkv = qkvs[n % 2]
                    slot = n % 2
                    q10 = qkv[:, 0:640].rearrange("p (h d) -> p h d", d=64)
                    cosb = cst[:, C_ROPE + 16 * n:C_ROPE + 16 * n + 8].unsqueeze(1).broadcast_to([128, 10, 8])
                    sinb = cst[:, C_ROPE + 16 * n + 8:C_ROPE + 16 * n + 16].unsqueeze(1).broadcast_to([128, 10, 8])
                    yield G(lambda e: e.tensor_tensor(out=rtmp[:, 0], in0=q10[:, :, 0:8], in1=cosb, op=ALU.mult), R=[qkv, cst], W=[rtmp])
                    yield G(lambda e: e.tensor_tensor(out=rtmp[:, 1], in0=q10[:, :, 8:16], in1=sinb, op=ALU.mult), R=[qkv, cst], W=[rtmp])
                    yield G(lambda e: e.tensor_tensor(out=rtmp[:, 2], in0=q10[:, :, 8:16], in1=cosb, op=ALU.mult), R=[qkv, cst], W=[rtmp])
                    yield G(lambda e: e.tensor_tensor(out=rtmp[:, 3], in0=q10[:, :, 0:8], in1=sinb, op=ALU.mult), R=[qkv, cst], W=[rtmp])
                    yield G(lambda e: e.tensor_tensor(out=q10[:, :, 0:8], in0=rtmp[:, 0], in1=rtmp[:, 1], op=ALU.subtract), R=[rtmp], W=[qkv])
                    yield G(lambda e: e.tensor_tensor(out=q10[:, :, 8:16], in0=rtmp[:, 2], in1=rtmp[:, 3], op=ALU.add), R=[rtmp], W=[qkv])
                    yield PE([lambda e, c=c: e.transpose(A0[:, c * 128:(c + 1) * 128], qkv[:, c * 128:(c + 1) * 128], cst[:, C_ID:C_ID + 128]) for c in range(4)],
                             R=[qkv, cst], W=[A0])
                    yield PE(lambda e: e.transpose(A1_[:, 0:128], qkv[:, 512:640], cst[:, C_ID:C_ID + 128]), R=[qkv, cst], W=[A1_])
                    yield A(lambda e: e.activation(out=qT[:], in_=v3(A0[:], 128), func=AF.Copy, scale=0.125), R=[A0], W=[qT])
                    yield A(lambda e: e.activation(out=Kbuf[:, slot * 128:(slot + 1) * 128], in_=A1_[:, 0:128], func=AF.Copy), R=[A1_], W=[Kbuf])
                    yield V(lambda e: e.tensor_copy(Vbuf[:, slot, :], qkv[:, 640:768]), R=[qkv], W=[Vbuf])
                    if n == 0:
                        yield A(lambda e: e.activation(out=Kbuf[:, 256:272], in_=A1_[:, 112:128], func=AF.Copy), R=[A1_], W=[Kbuf])
                        yield PE(lambda e: e.matmul(A1_[0:16, 128:256], cst[:, C_ID + 112:C_ID + 128], qkv[:, 640:768], start=True, stop=True), R=[qkv, cst], W=[A1_])
                        yield V(lambda e: e.tensor_copy(Vbuf[0:16, 2, :], A1_[0:16, 128:256]), R=[A1_], W=[Vbuf])
                        return
                    yTn = yTa[n % 2]
                    mvar = 2 if n == 1 else (0 if n % 2 == 0 else 1)
                    mask = cst[:, C_AM + 272 * mvar:C_AM + 272 * (mvar + 1)]
                    for s in range(8):
                        c, j = s // 2, s % 2
                        sl = slice(64 * j, 64 * j + 64)
                        SC = A0
                        Pk = Pb[s % 2]
                        PTk = PT[s % 2]
                        yield PE(lambda e: e.matmul(SC[:, 0:272], qT[sl, c, :], Kbuf[sl, 0:272], start=True, stop=True), R=[qT, Kbuf], W=[SC])
                        yield V(lambda e: e.tensor_tensor(out=SC[:, 0:272], in0=SC[:, 0:272], in1=mask, op=ALU.add), R=[cst], W=[SC])
                        yield V(lambda e: e.tensor_reduce(out=sm[:, 0, s:s + 1], in_=SC[:, 0:272], axis=AX.X, op=ALU.max), R=[SC], W=[sm])
                        yield V(lambda e: e.tensor_scalar(out=sm[:, 1, s:s + 1], in0=sm[:, 0, s:s + 1], scalar1=rowsA[:, RA_SK + s:RA_SK + s + 1], scalar2=-1.0,
                                                          op0=ALU.max, op1=ALU.mult), R=[rowsA], W=[sm])
                        yield A(lambda e: e.activation(out=Pk[:], in_=SC[:, 0:272], func=AF.Exp, bias=sm[:, 1, s:s + 1], scale=1.0,
                                                       accum_out=sm[:, 2, s:s + 1]), R=[SC], W=[Pk, sm])
                        yield A(lambda e: e.activation(out=sm[:, 3, s:s + 1], in_=rowsA[:, RA_SK + s:RA_SK + s + 1], func=AF.Exp, bias=sm[:, 1, s:s + 1], scale=1.0),
                                R=[rowsA], W=[sm])
                        yield PE([lambda e, b=b, nk=nk: e.matmul(A1_[0:nk, b * 128:(b + 1) * 128], Pk[:, b * 128:b * 128 + nk], identb[:], start=True, stop=True)
                                  for b, nk in ((0, 128), (1, 128), (2, 16))], R=[Pk, identb], W=[A1_])
                        yield A(lambda e: e.activation(out=PTk[:, 0:2, :], in_=v3(A1_[:, 0:256], 128), func=AF.Copy), R=[A1_], W=[PTk])
                        yield A(lambda e: e.activation(out=PTk[0:16, 2, :], in_=A1_[0:16, 256:384], func=AF.Copy), R=[A1_], W=[PTk])
                        yield PE([lambda e: e.matmul(A2_[:, s * 64:(s + 1) * 64], PTk[:, 0, :], Vbuf[:, 0, sl], start=True, stop=False),
                                  lambda e: e.matmul(A2_[:, s * 64:(s + 1) * 64], PTk[:, 1, :], Vbuf[:, 1, sl], start=False, stop=False),
                                  lambda e: e.matmul(A2_[:, s * 64:(s + 1) * 64], PTk[0:16, 2, :], Vbuf[0:16, 2, sl], start=False, stop=True)],
                                 R=[PTk, Vbuf], W=[A2_])
                    yield V(lambda e: e.tensor_tensor(out=sm[:, 2, :], in0=sm[:, 2, :], in1=sm[:, 3, :], op=ALU.add), R=[], W=[sm])
                    yield V(lambda e: e.reciprocal(out=sm[:, 4, :], in_=sm[:, 2, :]), R=[], W=[sm])
                    yield V(lambda e: e.tensor_tensor(out=yat[:], in0=v3(A2_[:], 64), in1=sm[:, 4, :].unsqueeze(2).broadcast_to([128, 8, 64]), op=ALU.mult),
                            R=[A2_], W=[yat, sm])
                    yield PE([lambda e, c=c: e.transpose(A1_[:, c * 128:(c + 1) * 128], yat[:, 2 * c:2 * c + 2, :].rearrange("p a d -> p (a d)"), cst[:, C_ID:C_ID + 128])
                              for c in range(4)], R=[yat, cst], W=[A1_])
                    yield A(lambda e: e.activation(out=yTn[:], in_=v3(A1_[:], 128), func=AF.Copy), R=[A1_], W=[yTn])
                    if n == dbg_n:
                        dump("yat", yat[:].rearrange("p s d -> p (s d)"), [yat])
                    yield DMA("sp", ch_y[n % 2], lambda e: e.dma_start(out=yscr[n - 1][:, 0:512], in_=yTn[:].rearrange("p c t -> p (c t)")), R=[yTn], W=[yscr_t[n - 1]])

                def pre(n):
                    pf = pfs[n % 2]
                    pp_ = n % 2
                    arT, NB, NK, Vst, Vst32, BKst, WC, rk, B5 = arTs[pp_], NBs[pp_], NKs[pp_], Vsts[pp_], Vst32s[pp_], BKsts[pp_], WCs[pp_], rks[pp_], B5s[pp_]
                    tri0 = C_TRI0 if n == 0 else C_TRI
                    tri = cst[:, tri0:tri0 + 128]
                    yield A(lambda e: e.activation(out=th[:], in_=pf[0:64, 12, :], func=AF.Tanh), R=[pf], W=[th])
                    yield PE(lambda e: e.matmul(R0[:, 0:512], th[:], w2[:], start=True, stop=True), R=[th, w2], W=[R0])
                    B4f = B4[:].rearrange("p c t -> p (c t)")
                    yield V(lambda e: e.tensor_tensor(out=B4f, in0=R0[:, 0:512], in1=rowsA[:, RA_W0:RA_W0 + 512], op=ALU.add), R=[R0, rowsA], W=[B4])
                    yield A(lambda e: e.activation(out=B4f, in_=B4f, func=AF.Tanh, scale=0.5), R=[], W=[B4])
                    yield V(lambda e: e.tensor_scalar(out=B4f, in0=B4f, scalar1=0.5, scalar2=0.5, op0=ALU.mult, op1=ALU.add), R=[], W=[B4])
                    CB = (R1_, R2)
                    for q in range(2):
                        yield PE([lambda e, c=c, q=q: e.matmul(CB[q][:, c * 128:(c + 1) * 128], B4f[64 * q:64 * q + 64, c * 128:(c + 1) * 128],
                                                               tri[64 * q:64 * q + 64, :], start=True, stop=True) for c in range(4)], R=[B4, cst], W=[CB[q]])

                    def cums(a):
                        return [CB[q][:].rearrange("p (c a t) -> p c a t", c=4, a=2)[:, :, a, :] for q in range(2)]
                    yield PE([lambda e, c=c: e.matmul(R0[:, c * 128:(c + 1) * 128], a2[:, c * 128:(c + 1) * 128], pf[0:64, 13, :], start=True, stop=True) for c in range(4)],
                             R=[pf, a2], W=[R0])
                    for c in range(4):
                        yield A(lambda e, c=c: e.activation(out=B3[:, c, :], in_=R0[:, c * 128:(c + 1) * 128], func=AF.Tanh, bias=hb[:, c:c + 1], scale=0.5),
                                R=[R0, hb], W=[B3])
                    yield V(lambda e: e.tensor_scalar(out=B3[:], in0=B3[:], scalar1=0.5, scalar2=0.5, op0=ALU.mult, op1=ALU.add), R=[], W=[B3])
                    yield A(lambda e: e.activation(out=sgd[:], in_=pf[:, 14:16, :], func=AF.Tanh, scale=0.5), R=[pf], W=[sgd])
                    yield V(lambda e: e.tensor_scalar(out=sgd[:], in0=sgd[:], scalar1=0.5, scalar2=0.5, op0=ALU.mult, op1=ALU.add), R=[], W=[sgd])
                    fns = []
                    for c in range(4):
                        fns.append(lambda e, c=c: e.matmul(R0[:, c * 128:(c + 1) * 128], g2[:, 0, c * 128:(c + 1) * 128], sgd[:, 0, :], start=True, stop=False))
                        fns.append(lambda e, c=c: e.matmul(R0[:, c * 128:(c + 1) * 128], g2[0:32, 1, c * 128:(c + 1) * 128], sgd[0:32, 1, :], start=False, stop=True))
                    yield PE(fns, R=[sgd, g2], W=[R0])
                    yield A(lambda e: e.activation(out=B5[:].rearrange("p c t -> p (c t)"), in_=R0[:], func=AF.Copy), R=[R0], W=[B5])
                    kview = pf[:, 4:8, :]
                    rview = pf[:, 0:4, :]

                    def bc(col):
                        return pfm[:, col:col + 4].unsqueeze(2).broadcast_to([128, 4, 128])
                    yield V(lambda e: e.tensor_tensor(out=B1[:], in0=kview, in1=bc(P_KK), op=ALU.mult), R=[pf, pfm], W=[B1])
                    yield V(lambda e: e.tensor_tensor(out=B2[:], in0=B1[:], in1=B1[:], op=ALU.mult), R=[B1], W=[B2])
                    yield PE([lambda e, c=c: e.matmul(R0[:, c * 128:(c + 1) * 128], cst[:, C_OBD:C_OBD + 128], B2[:, c, :], start=True, stop=True) for c in range(4)],
                             R=[B2, cst], W=[R0])
                    yield A(lambda e: e.activation(out=B2[:].rearrange("p c t -> p (c t)"), in_=R0[:], func=AF.Sqrt), R=[R0], W=[B2])
                    yield V(lambda e: e.tensor_scalar(out=B2[:], in0=B2[:], scalar1=1e-12, scalar2=None, op0=ALU.max), R=[], W=[B2])
                    yield V(lambda e: e.reciprocal(out=B2[:], in_=B2[:]), R=[], W=[B2])
                    yield V(lambda e: e.tensor_tensor(out=B1[:], in0=B1[:], in1=B2[:], op=ALU.mult), R=[B2], W=[B1])
                    yield V(lambda e: e.scalar_tensor_tensor(out=B2[:], in0=B3[:], scalar=-1.0, in1=bc(P_KA), op0=ALU.add, op1=ALU.mult), R=[B3, pfm], W=[B2])
                    yield V(lambda e: e.scalar_tensor_tensor(out=kview, in0=B2[:], scalar=1.0, in1=kview, op0=ALU.add, op1=ALU.mult), R=[B2], W=[pf])
                    yield V(lambda e: e.tensor_tensor(out=B3[:], in0=B1[:], in1=B3[:], op=ALU.mult), R=[B1], W=[B3])
                    cex = cums(1)
                    cin = cums(0)
                    B4q = cq(B4[:])
                    for hh in range(2):
                        yield A(lambda e, hh=hh: e.activation(out=B4[:, :, 64 * hh:64 * hh + 64], in_=cex[hh], func=AF.Exp), R=[CB[hh]], W=[B4])
                    yield V(lambda e: e.scalar_tensor_tensor(out=arT[:, :, :, 0, :], in0=cq(B1[:]), scalar=-1.0, in1=B4q, op0=ALU.mult, op1=ALU.mult), R=[B1, B4], W=[arT])
                    for hh in range(2):
                        yield A(lambda e, hh=hh: e.activation(out=B4[:, :, 64 * hh:64 * hh + 64], in_=cin[hh], func=AF.Exp), R=[CB[hh]], W=[B4])
                    yield V(lambda e: e.tensor_tensor(out=arT[:, :, :, 1, :], in0=cq(rview), in1=B4q, op=ALU.mult), R=[pf, B4], W=[arT])
                    for hh in range(2):
                        yield A(lambda e, hh=hh: e.activation(out=B4[:, :, 64 * hh:64 * hh + 64], in_=cin[hh], func=AF.Exp, scale=-1.0), R=[CB[hh]], W=[B4])
                    yield V(lambda e: e.tensor_tensor(out=BT[:], in0=B3[:], in1=B4[:], op=ALU.mult), R=[B3, B4], W=[BT])
                    yield V(lambda e: e.tensor_tensor(out=KT[:], in0=kview, in1=B4[:], op=ALU.mult), R=[pf, B4], W=[KT])
                    for hh in range(2):
                        yield V(lambda e, hh=hh: e.tensor_copy(cumC[:, :, hh], cin[hh][:, :, 63]), R=[CB[hh]], W=[cumC])
                    for c in range(4):
                        for q in range(2):
                            yield A(lambda e, c=c, q=q: e.activation(out=B4[:, c, 64 * q:64 * q + 64], in_=cin[q][:, c, :], func=AF.Exp, scale=-1.0,
                                                                     bias=cumC[:, c, q:q + 1]), R=[CB[q], cumC], W=[B4])
                    yield V(lambda e: e.tensor_tensor(out=BH[:], in0=B3[:], in1=B4[:], op=ALU.mult), R=[B3, B4], W=[BH])
                    yield V(lambda e: e.tensor_tensor(out=KH[:], in0=kview, in1=B4[:], op=ALU.mult), R=[pf, B4], W=[KH])
                    yield A(lambda e: e.activation(out=WC[:], in_=cumC[:], func=AF.Exp), R=[cumC], W=[WC])

                    mb = cst[:, C_MB:C_MB + 128].rearrange("p (a t) -> p a t", a=2).unsqueeze(1).broadcast_to([128, 4, 2, 64])
                    ml8 = cst[:, C_ML:C_ML + 64].unsqueeze(1).broadcast_to([128, 8, 64])
                    i8 = cst[:, C_I64:C_I64 + 64].unsqueeze(1).broadcast_to([128, 8, 64])
                    Q2 = (0, 1)

                    def tq(q):
                        return slice(64 * q, 64 * q + 64)
                    yield PE(hl(lambda q, c, sl, j: [lambda e: e.matmul(R0[sl, q * 256 + c * 64:q * 256 + (c + 1) * 64], arT[sl, c, q, 0, :], BT[sl, c, tq(q)], start=True, stop=True)], Q2),
                             R=[arT, BT], W=[R0])
                    for q in Q2:
                        yield PE(hl(lambda q, c, sl, j: [lambda e: e.matmul(CB[q][sl, c * 128:(c + 1) * 128], BT[sl, c, tq(q)], arT[sl, c, q, :, :].rearrange("p a t -> p (a t)"),
                                                                            start=True, stop=True)], (q,)), R=[arT, BT], W=[CB[q]])
                    yield V(lambda e: e.tensor_tensor(out=Ast[0][:].rearrange("p q c t -> p (q c) t"), in0=v3(R0[:]), in1=ml8, op=ALU.mult), R=[R0, cst], W=[Ast[0]])
                    for q in Q2:
                        yield V(lambda e, q=q: e.tensor_tensor(out=NB[:, q], in0=CB[q][:].rearrange("p (c a t) -> p c a t", c=4, a=2), in1=mb, op=ALU.mult), R=[CB[q], cst], W=[NB])
                    for q in Q2:
                        yield PE(hl(lambda q, c, sl, j: [lambda e: e.matmul(CB[q][sl, c * 128:(c + 1) * 128], KT[sl, c, tq(q)], arT[sl, c, q, :, :].rearrange("p a t -> p (a t)"),
                                                                            start=True, stop=True)], (q,)), R=[arT, KT], W=[CB[q]])
                    for q in Q2:
                        yield V(lambda e, q=q: e.tensor_tensor(out=NK[:, q], in0=CB[q][:].rearrange("p (c a t) -> p c a t", c=4, a=2), in1=mb, op=ALU.mult), R=[CB[q], cst], W=[NK])
                    yield G(lambda e: e.tensor_copy(Nst[0][:], NB[:, :, :, 0, :]), R=[NB], W=[Nst[0]])
                    yield G(lambda e: e.tensor_tensor(out=Pc[0][:].rearrange("p q c t -> p (q c) t"), in0=Nst[0][:].rearrange("p q c t -> p (q c) t"), in1=i8, op=ALU.add),
                            R=[Nst[0], cst], W=[Pc[0]])
                    yield PE(hl(lambda q, c, sl, j: [lambda e: e.matmul(R0[sl, q * 256 + c * 64:q * 256 + (c + 1) * 64], pf[sl, 8 + c, tq(q)], idsl(sl, j), start=True, stop=True)], Q2),
                             R=[pf, cst], W=[R0])
                    yield A(lambda e: e.activation(out=Vst32[:], in_=v4(R0[:]), func=AF.Copy), R=[R0], W=[Vst32])
                    if MD != F32:
                        yield V(lambda e: e.tensor_copy(Vst[:], v4(R0[:])), R=[R0], W=[Vst])
                    for q in Q2:
                        yield PE(hl(lambda q, c, sl, j: [lambda e: e.matmul(CB[q][sl, c * 64:(c + 1) * 64], BH[sl, c, tq(q)], idm_sl(sl, j), start=True, stop=True),
                                                         lambda e: e.matmul(CB[q][sl, 256 + c * 64:256 + (c + 1) * 64], KH[sl, c, tq(q)], idm_sl(sl, j), start=True, stop=True)], (q,)),
                                 R=[BH, KH, idm], W=[CB[q]])
                    for q in Q2:
                        yield A(lambda e, q=q: e.activation(out=BKst[:, q], in_=CB[q][:].rearrange("p (a c t) -> p a c t", a=2, c=4), func=AF.Copy), R=[CB[q]], W=[BKst])
                    cur = 0

                    def sq_fns(cur, lvl):
                        f = hl(lambda q, c, sl, j: [lambda e: e.matmul(R0[sl, q * 256 + c * 64:q * 256 + (c + 1) * 64], Nst[cur][sl, q, c, :], Ast[cur][sl, q, c, :], start=True, stop=True)], Q2)
                        if lvl < 5:
                            f += hl(lambda q, c, sl, j: [lambda e: e.matmul(R1_[sl, q * 256 + c * 64:q * 256 + (c + 1) * 64], Ast[cur][sl, q, c, :], Nst[cur][sl, q, c, :], start=True, stop=True)], Q2)
                        return f

                    def pp_fns(a_t, pc):
                        return hl(lambda q, c, sl, j: [lambda e: e.matmul(R2[sl, q * 256 + c * 64:q * 256 + (c + 1) * 64], a_t[sl, q, c, :], pc[sl, q, c, :], start=True, stop=True)], Q2)

                    yield PE(sq_fns(0, 1), R=[Nst[0], Ast[0]], W=[R0, R1_])
                    for lvl in range(1, 6):
                        nxt = 1 - cur
                        yield A(lambda e, nxt=nxt: e.activation(out=Ast[nxt][:], in_=v4(R0[:]), func=AF.Copy), R=[R0], W=[Ast[nxt]])
                        if lvl < 5:
                            yield V(lambda e, nxt=nxt: e.tensor_copy(Nst[nxt][:], v4(R1_[:])), R=[R1_], W=[Nst[nxt]])
                        pc, pn = Pc[(lvl - 1) % 2], (Pc[lvl % 2] if lvl < 5 else TTs[pp_])
                        fns = pp_fns(Ast[nxt], pc)
                        Wl = [R2]
                        Rl = [Ast[nxt], pc]
                        if lvl < 5:
                            fns += sq_fns(nxt, lvl + 1)
                            Wl += [R0] + ([R1_] if lvl + 1 < 5 else [])
                            Rl += [Nst[nxt]]
                        yield PE(fns, R=Rl, W=Wl)
                        yield V(lambda e, pc=pc, pn=pn: e.tensor_tensor(out=pn[:], in0=v4(R2[:]), in1=pc[:], op=ALU.add), R=[R2, pc], W=[pn])
                        cur = nxt
                    yield G(lambda e: e.tensor_tensor(out=B2[:], in0=rview, in1=kview, op=ALU.mult), R=[pf], W=[B2])
                    yield G(lambda e: e.tensor_tensor(out=B2[:], in0=B2[:], in1=bc(P_RK), op=ALU.mult), R=[pfm], W=[B2])
                    fns = []
                    for q in range(2):
                        for c in range(4):
                            for j in range(2):
                                sl = slice(64 * j, 64 * j + 64)
                                fns.append(lambda e, q=q, c=c, sl=sl: e.matmul(R0[sl, (q * 4 + c) * 2:(q * 4 + c) * 2 + 2], B2[sl, c, 64 * q:64 * q + 64], ones2[sl, :],
                                                                              start=True, stop=True))
                    yield PE(fns, R=[B2, ones2], W=[R0])
                    yield A(lambda e: e.activation(out=rk[:].rearrange("p q c -> p (q c)"), in_=R0[:, 0:16].rearrange("p (x two) -> p x two", two=2)[:, :, 0], func=AF.Copy), R=[R0], W=[rk])
                    return

                def seq(n):
                    pp_ = n % 2
                    arT, NB, NK, Vst, Vst32, BKst, WC, rk, B5 = arTs[pp_], NBs[pp_], NKs[pp_], Vsts[pp_], Vst32s[pp_], BKsts[pp_], WCs[pp_], rks[pp_], B5s[pp_]
                    TT = TTs[pp_]
                    Q2 = (0, 1)
                    for q in Q2:
                        yield PE(hl(lambda q, c, sl, j: [lambda e: e.matmul(Q0[sl, c * 64:(c + 1) * 64], arT[sl, c, q, 0, :], STm[sl, c, :], start=True, stop=False),
                                                         lambda e: e.matmul(Q0[sl, c * 64:(c + 1) * 64], NK[sl, q, c, 0, :], Vst[sl, q, c, :], start=False, stop=True)], (q,)),
                                 R=[arT, STm, NK, Vst], W=[Q0])
                        yield A(lambda e: e.activation(out=R1[:], in_=v3(Q0[:, 0:256]), func=AF.Copy), R=[Q0], W=[R1])
                        yield PE(hl(lambda q, c, sl, j: [lambda e: e.matmul(Q0[sl, 256 + c * 64:256 + (c + 1) * 64], TT[sl, q, c, :], R1[sl, c, :], start=True, stop=True)], (q,)),
                                 R=[TT, R1], W=[Q0])
                        yield A(lambda e: e.activation(out=Ust[:], in_=v3(Q0[:, 256:512]), func=AF.Copy), R=[Q0], W=[Ust])
                        if n >= 1:
                            yield PE(hl(lambda q, c, sl, j: [lambda e: e.matmul(Q0[sl, c * 64:(c + 1) * 64], arT[sl, c, q, 1, :], STm[sl, c, :], start=True, stop=False),
                                                             lambda e: e.matmul(Q0[sl, c * 64:(c + 1) * 64], NB[sl, q, c, 1, :], Ust[sl, c, :], start=False, stop=False),
                                                             lambda e: e.matmul(Q0[sl, c * 64:(c + 1) * 64], NK[sl, q, c, 1, :], Vst[sl, q, c, :], start=False, stop=True)], (q,)),
                                     R=[arT, STm, NB, Ust, NK, Vst], W=[Q0])
                            yield V(lambda e, q=q: e.tensor_copy(Yst[:, q], v3(Q0[:, 0:256])), R=[Q0], W=[Yst])
                        yield PE(hl(lambda q, c, sl, j: [lambda e: e.matmul(Q0[sl, 256 + c * 64:256 + (c + 1) * 64], BKst[sl, q, 0, c, :], Ust[sl, c, :], start=True, stop=False),
                                                         lambda e: e.matmul(Q0[sl, 256 + c * 64:256 + (c + 1) * 64], BKst[sl, q, 1, c, :], Vst[sl, q, c, :], start=False, stop=True)], (q,)),
                                 R=[BKst, Ust, Vst], W=[Q0])
                        yield G(lambda e, q=q: e.tensor_tensor(out=STt[:], in0=ST32[:], in1=WC[:, :, q:q + 1].broadcast_to([128, 4, 64]), op=ALU.mult), R=[WC, ST32], W=[STt])
                        if MD != F32:
                            yield V(lambda e: e.tensor_tensor(out=STm[:], in0=STt[:], in1=v3(Q0[:, 256:512]), op=ALU.add), R=[Q0, STt], W=[STm])
                        yield V(lambda e: e.tensor_tensor(out=ST32[:], in0=STt[:], in1=v3(Q0[:, 256:512]), op=ALU.add), R=[Q0, STt], W=[ST32])
                    if n == 0:
                        return
                    Y8 = Yst[:].rearrange("p q c v -> p (q c) v")
                    yc8 = yc[:].rearrange("p q c v -> p (q c) v")
                    ysq8 = ysq[:].rearrange("p q c v -> p (q c) v")
                    V32_8 = Vst32[:].rearrange("p q c v -> p (q c) v")

                    def b8(ap):
                        return ap.unsqueeze(2).broadcast_to([128, 8, 64])
                    yield V(lambda e: e.tensor_reduce(out=gst[:, 0, :], in_=Y8, axis=AX.X, op=ALU.add), R=[Yst], W=[gst])
                    yield V(lambda e: e.tensor_scalar(out=gst[:, 1, :], in0=gst[:, 0, :], scalar1=-1.0 / 64, scalar2=None, op0=ALU.mult), R=[], W=[gst])
                    yield V(lambda e: e.tensor_tensor(out=yc8, in0=Y8, in1=b8(gst[:, 1, :]), op=ALU.add), R=[Yst], W=[yc, gst])
                    yield G(lambda e: e.tensor_tensor(out=ysq8, in0=yc8, in1=yc8, op=ALU.mult), R=[yc], W=[ysq])
                    yield V(lambda e: e.tensor_reduce(out=gst[:, 2, :], in_=ysq8, axis=AX.X, op=ALU.add), R=[ysq], W=[gst])
                    yield V(lambda e: e.tensor_scalar(out=gst[:, 3, :], in0=gst[:, 2, :], scalar1=1.0 / 64, scalar2=LN_EPS, op0=ALU.mult, op1=ALU.add), R=[], W=[gst])
                    yield G(lambda e: e.tensor_tensor(out=gst[:, 4, :], in0=gst[:, 3, :], in1=cst[:, CE + 4:CE + 12], op=ALU.pow), R=[cst], W=[gst])
                    yield V(lambda e: e.tensor_tensor(out=yc8, in0=yc8, in1=b8(gst[:, 4, :]), op=ALU.mult), R=[], W=[yc, gst])
                    yield G(lambda e: e.tensor_tensor(out=yc[:], in0=yc[:], in1=lnst[:, 0].unsqueeze(1).broadcast_to([128, 2, 4, 64]), op=ALU.mult), R=[lnst], W=[yc])
                    yield G(lambda e: e.tensor_tensor(out=yc[:], in0=yc[:], in1=lnst[:, 1].unsqueeze(1).broadcast_to([128, 2, 4, 64]), op=ALU.add), R=[lnst], W=[yc])
                    yield V(lambda e: e.tensor_tensor(out=ysq8, in0=V32_8, in1=b8(rk[:].rearrange("p q c -> p (q c)")), op=ALU.mult), R=[Vst32, rk], W=[ysq])
                    yield V(lambda e: e.tensor_tensor(out=yc8, in0=yc8, in1=ysq8, op=ALU.add), R=[ysq], W=[yc])
                    yield PE(hl(lambda q, c, sl, j: [lambda e: e.matmul(Q0[sl, q * 256 + c * 64:q * 256 + (c + 1) * 64], yc[sl, q, c, :], idsl(sl, j), start=True, stop=True)], Q2),
                             R=[yc, cst], W=[Q0])
                    yTn = yTr[n % 2]
                    yield V(lambda e: e.tensor_tensor(out=qc(yTn[:]), in0=v4(Q0[:]), in1=qc(B5[:]), op=ALU.mult), R=[Q0, B5], W=[yTn])
                    yield DMA("sp", ch_st[n % 2], lambda e: e.dma_start(out=yscr[n - 1][:, 512:1024], in_=yTn[:].rearrange("p c t -> p (c t)")), R=[yTn], W=[yscr_t[n - 1]])

                import os as _os
                W_PRE, W_ATT, W_SEQ, W_HEAD = [int(v) for v in _os.environ.get("KW", "3,3,1,1").split(",")]
                run(head(0))
                if nt > 1:
                    run(par([pre(0), attention(0), head(1)], [W_PRE, W_ATT, W_HEAD]))
                else:
                    run(par([pre(0), attention(0)], [W_PRE, W_ATT]))
                _skip = _os.environ.get("KSKIP", "")
                B_SEQ, B_PRE, B_ATT, B_HEAD = [float(v) for v in _os.environ.get("KB", "0,0,0,0").split(",")]
                for i in range(nt):
                    streams = [seq(i)] if "seq" not in _skip else []
                    bon = [B_SEQ] if "seq" not in _skip else []
                    if i + 1 < nt:
                        if "pre" not in _skip:
                            streams += [pre(i + 1)]
                            bon += [B_PRE]
                        if "att" not in _skip:
                            streams += [attention(i + 1)]
                            bon += [B_ATT]
                    if i + 2 < nt:
                        streams.append(head(i + 2))
                        bon.append(B_HEAD)
                    run(streams, bon)
                S.barrier()

        es_bw = ExitStack()
        pre_w = {}
        if "B" in phases:
            sbw_ = mk_alloc(es_bw, "bw_")
            pre_w["wg"] = sbw_("wg", [128, 8, DFF], BF16)
            pre_w["wu"] = sbw_("wu", [128, 8, DFF], BF16)
            pre_w["wo"] = sbw_("wo", [128, 8, D], BF16)

        def load_bw():
            wg, wu = pre_w["wg"], pre_w["wu"]
            wo_ = pre_w["wo"]
            pre_w["wo_b"] = wload_blk(wo_, wo_[:], w_o.rearrange("(c p) n -> p c n", p=128))
            ngrp_ = (NFC + 3) // 4
            wg_b = []
            wu_b = []
            for g in range(ngrp_):
                c0, c1 = 512 * g, min(512 * (g + 1), DFF)
                wg_b.append(wload_blk(wg, wg[:, :, c0:c1], w_fg.rearrange("(c p) n -> p c n", p=128)[:, :, c0:c1]))
                wu_b.append(wload_blk(wu, wu[:, :, c0:c1], w_fu.rearrange("(c p) n -> p c n", p=128)[:, :, c0:c1]))
            pre_w["wg_b"] = wg_b
            pre_w["wu_b"] = wu_b

        if "A2" in phases:
            with ExitStack() as es:
                sb = mk_alloc(es, "a2_")
                CE = 128
                cst, pfm = load_consts(sb, 128)
                wgate = sb("wgate", [128, 8, 2048], BF16)
                wba = sb("wba", [128, 4, D], BF16)
                wbr = sb("wbr", [128, 4, D], BF16)
                wgate_b = [wload_blk(wgate, wgate[:, :, 512 * g:512 * (g + 1)], w_gate.rearrange("(c p) n -> p c n", p=128)[:, :, 512 * g:512 * (g + 1)]) for g in range(4)]
                wba_b = wload_blk(wba, wba[:], w_ba.rearrange("(c p) n -> p c n", p=128))
                wbr_b = wload_blk(wbr, wbr[:], w_br.rearrange("(c p) n -> p c n", p=128))
                S.finalize(ch_w, [cst, pfm])
                if "B" in phases:
                    load_bw()
                PS = [Tile(es.enter_context(nc.psum_tensor("psb%d" % i, [128, 512], F32)), "psb%d" % i, excl=True) for i in range(8)]
                xb = [sb("xb0", [128, D]), sb("xb1", [128, D])]
                yT = [sb("yT%d" % i, [128, 8, 128], BF16) for i in range(3)]
                xs = sb("xs", [128, D])
                st4 = sb("st4", [128, 4])
                uTs = [sb("uT0", [128, 8, 128], BF16), sb("uT1", [128, 8, 128], BF16)]
                sgs = [sb("sg0", [128, 16, 128]), sb("sg1", [128, 16, 128])]
                hbg = sb("hbg", [128, 16])
                V(lambda e: e.tensor_scalar(out=hbg[:], in0=pfm[:, P_BG:P_BG + 16], scalar1=0.5, scalar2=None, op0=ALU.mult), R=[pfm], W=[hbg])
                t1 = sb("t1", [128, 8, 128])
                t2 = sb("t2", [128, 8, 128])
                mT = [sb("mT0", [128, 8, 128], BF16), sb("mT1", [128, 8, 128], BF16)]

                def front2(n):
                    xt = xb[n % 2]
                    yTn = yT[n % 3]
                    yield DMA("sp", ch_x[n % 2], lambda e: e.dma_start(out=xt[:], in_=xe[n * 128:(n + 1) * 128, :]), W=[xt])
                    yield DMA("sp", ch_y[n % 2], lambda e: e.dma_start(out=yTn[:].rearrange("p c t -> p (c t)"), in_=yscr[n - 1]), R=[yscr_t[n - 1]], W=[yTn])
                    yield from norm_T(xt, xs, st4, cst, CE, pfm, P_GMIX, [PS[0], PS[1]], uTs[n % 2])

                def mid2(n):
                    uT = uTs[n % 2]
                    sg = sgs[n % 2]
                    for g in range(4):
                        bank = PS[2 + (g % 2)]
                        fns = []
                        for i in range(4):
                            col = (4 * g + i) * 128
                            for kc in range(8):
                                fns.append(lambda e, i=i, col=col, kc=kc, bank=bank: e.matmul(bank[:, i * 128:(i + 1) * 128], wgate[:, kc, col:col + 128], uT[:, kc, :],
                                                                                             start=(kc == 0), stop=(kc == 7)))
                        yield PE(fns, R=[uT, wgate_b[g]], W=[bank])
                        for i in range(4):
                            yield A(lambda e, g=g, i=i, bank=bank: e.activation(out=sg[:, 4 * g + i, :], in_=bank[:, i * 128:(i + 1) * 128], func=AF.Tanh,
                                                                                bias=hbg[:, 4 * g + i:4 * g + i + 1], scale=0.5), R=[bank, hbg], W=[sg])

                def tail2(n):
                    yTn = yT[n % 3]
                    mTn = mT[n % 2]
                    sg = sgs[n % 2]
                    for br, (wb, off, wb_b) in enumerate(((wba, 0, wba_b), (wbr, 4, wbr_b))):
                        for hh in range(2):
                            bank = PS[4 + 2 * br + hh]
                            fns = []
                            for i in range(4):
                                fc = 4 * hh + i
                                for kc in range(4):
                                    fns.append(lambda e, i=i, fc=fc, kc=kc, bank=bank, wb=wb, off=off: e.matmul(bank[:, i * 128:(i + 1) * 128], wb[:, kc, fc * 128:(fc + 1) * 128],
                                                                                                                yTn[:, off + kc, :], start=(kc == 0), stop=(kc == 3)))
                            yield PE(fns, R=[yTn, wb_b], W=[bank])
                    for hh in range(2):
                        yield V(lambda e, hh=hh: e.scalar_tensor_tensor(out=t1[:, 4 * hh:4 * hh + 4, :], in0=sg[:, 4 * hh:4 * hh + 4, :], scalar=1.0,
                                                                        in1=PS[4 + hh][:].rearrange("p (c t) -> p c t", c=4), op0=ALU.add, op1=ALU.mult),
                                R=[PS[4 + hh], sg], W=[t1])
                        yield V(lambda e, hh=hh: e.scalar_tensor_tensor(out=t2[:, 4 * hh:4 * hh + 4, :], in0=sg[:, 8 + 4 * hh:8 + 4 * hh + 4, :], scalar=1.0,
                                                                        in1=PS[6 + hh][:].rearrange("p (c t) -> p c t", c=4), op0=ALU.add, op1=ALU.mult),
                                R=[PS[6 + hh], sg], W=[t2])
                    yield G(lambda e: e.tensor_tensor(out=t1[:], in0=t1[:], in1=t2[:], op=ALU.add), R=[t2], W=[t1])
                    yield A(lambda e: e.activation(out=mTn[:], in_=t1[:], func=AF.Copy, scale=0.5), R=[t1], W=[mTn])
                    yield DMA("sp", ch_st[n % 2], lambda e: e.dma_start(out=mscr[n - 1], in_=mTn[:].rearrange("p c t -> p (c t)")), R=[mTn], W=[mscr_t[n - 1]])

                if nt > 1:
                    run(front2(1))
                if nt > 2:
                    run([mid2(1), front2(2)])
                elif nt > 1:
                    run(mid2(1))
                for n in range(1, nt):
                    streams = [tail2(n)]
                    if n + 1 < nt:
                        streams.append(mid2(n + 1))
                    if n + 2 < nt:
                        streams.append(front2(n + 2))
                    run(streams)
                S.barrier()

        if "B" in phases:
            with ExitStack() as es:
                sb = mk_alloc(es, "b_")
                CE = 128
                cst, pfm = load_consts(sb, 128)
                gfin = sb("gfin", [128, D])
                S.dma("sp", ch_w, lambda e: e.dma_start(out=gfin[:], in_=gfind.broadcast_to([128, D])), W=[gfin])
                wo = pre_w["wo"]
                wg, wu = pre_w["wg"], pre_w["wu"]
                wd = sb("wd", [128, NFC, D], BF16)
                if "wg_b" not in pre_w:
                    load_bw()
                wo_b = pre_w["wo_b"]
                wg_b, wu_b = pre_w["wg_b"], pre_w["wu_b"]
                wd_b = [wload_blk(wd, wd[:, 11 * hh:11 * hh + 11, :], w_fd.rearrange("(c p) n -> p c n", p=128)[:, 11 * hh:11 * hh + 11, :]) for hh in range(2)]
                S.finalize(ch_w, [cst, pfm, gfin])
                PS = [Tile(es.enter_context(nc.psum_tensor("psc%d" % i, [128, 512], F32)), "psc%d" % i, excl=True) for i in range(8)]
                xb = [sb("xb0", [128, D]), sb("xb1", [128, D])]
                mT = [sb("mT0", [128, 8, 128], BF16), sb("mT1", [128, 8, 128], BF16)]
                h1s = [sb("h1a", [128, D]), sb("h1b", [128, D]), sb("h1c", [128, D])]
                xsF = sb("xsF", [128, D])
                xsB = [sb("xsB0", [128, D]), sb("xsB1", [128, D])]
                st4 = sb("st4", [128, 4])
                st4b = sb("st4b", [128, 4])
                fTs = [sb("fT0", [128, 8, 128], BF16), sb("fT1", [128, 8, 128], BF16)]
                sl_ = sb("silu", [128, 4, 128])
                aTs = [sb("aT0", [128, NFC, 128], BF16), sb("aT1", [128, NFC, 128], BF16)]

                def front3(n):
                    xt = xb[n % 2]
                    mTn = mT[n % 2]
                    h1 = h1s[n % 3]
                    yield DMA("sp", ch_x[n % 2], lambda e: e.dma_start(out=xt[:], in_=xe[n * 128:(n + 1) * 128, :]), W=[xt])
                    yield DMA("sp", ch_y[n % 2], lambda e: e.dma_start(out=mTn[:].rearrange("p c t -> p (c t)"), in_=mscr[n - 1]), R=[mscr_t[n - 1]], W=[mTn])
                    for hh in range(2):
                        yield PE([lambda e, kc=kc, hh=hh: e.matmul(PS[0][:], mTn[:, kc, :], wo[:, kc, hh * 512:(hh + 1) * 512], start=(kc == 0), stop=(kc == 7)) for kc in range(8)],
                                 R=[mTn, wo_b], W=[PS[0]])
                        yield V(lambda e, hh=hh: e.tensor_tensor(out=h1[:, hh * 512:(hh + 1) * 512], in0=PS[0][:], in1=xt[:, hh * 512:(hh + 1) * 512], op=ALU.add),
                                R=[PS[0], xt], W=[h1])
                    yield from norm_T(h1, xsF, st4, cst, CE, pfm, P_GFFN, [PS[1], PS[1]], fTs[n % 2])

                def mid3(n):
                    fT = fTs[n % 2]
                    aT = aTs[n % 2]
                    ngrp = (NFC + 3) // 4
                    for g in range(ngrp):
                        nchunk = min(4, NFC - 4 * g)
                        bg = PS[2 + 2 * (g % 2)]
                        bu = PS[3 + 2 * (g % 2)]
                        for bank, wt, wt_b in ((bg, wg, wg_b[g]), (bu, wu, wu_b[g])):
                            fns = []
                            for i in range(nchunk):
                                fc = 4 * g + i
                                for kc in range(8):
                                    fns.append(lambda e, i=i, fc=fc, kc=kc, bank=bank, wt=wt: e.matmul(bank[:, i * 128:(i + 1) * 128], wt[:, kc, fc * 128:(fc + 1) * 128], fT[:, kc, :],
                                                                                                       start=(kc == 0), stop=(kc == 7)))
                            yield PE(fns, R=[fT, wt_b], W=[bank])
                        yield A(lambda e: e.activation(out=sl_[:, 0:nchunk, :], in_=bg[:, 0:nchunk * 128].rearrange("p (c t) -> p c t", c=nchunk), func=AF.Tanh, scale=0.5),
                                R=[bg], W=[sl_])
                        yield V(lambda e: e.scalar_tensor_tensor(out=sl_[:, 0:nchunk, :], in0=sl_[:, 0:nchunk, :], scalar=1.0,
                                                                 in1=bg[:, 0:nchunk * 128].rearrange("p (c t) -> p c t", c=nchunk), op0=ALU.add, op1=ALU.mult), R=[bg], W=[sl_])
                        yield V(lambda e: e.scalar_tensor_tensor(out=aT[:, 4 * g:4 * g + nchunk, :], in0=sl_[:, 0:nchunk, :], scalar=0.5,
                                                                 in1=bu[:, 0:nchunk * 128].rearrange("p (c t) -> p c t", c=nchunk), op0=ALU.mult, op1=ALU.mult), R=[bu, sl_], W=[aT])

                def tail3(n):
                    h1 = h1s[n % 3]
                    aT = aTs[n % 2]
                    o = xsB[n % 2]
                    for hh in range(2):
                        yield PE([lambda e, fc=fc, hh=hh: e.matmul(PS[6 + hh][:], aT[:, fc, :], wd[:, fc, hh * 512:(hh + 1) * 512], start=(fc == 0), stop=(fc == NFC - 1)) for fc in range(NFC)],
                                 R=[aT] + wd_b, W=[PS[6 + hh]])
                        yield V(lambda e, hh=hh: e.tensor_tensor(out=h1[:, hh * 512:(hh + 1) * 512], in0=PS[6 + hh][:], in1=h1[:, hh * 512:(hh + 1) * 512], op=ALU.add),
                                R=[PS[6 + hh]], W=[h1])
                    yield A(lambda e: e.activation(out=o[:], in_=h1[:], func=AF.Square, accum_out=st4b[:, 0:1]), R=[h1], W=[o, st4b])
                    yield V(lambda e: e.tensor_scalar(out=st4b[:, 1:2], in0=st4b[:, 0:1], scalar1=1.0 / D, scalar2=RMS_EPS, op0=ALU.mult, op1=ALU.add), R=[], W=[st4b])
                    yield G(lambda e: e.tensor_tensor(out=st4b[:, 2:3], in0=st4b[:, 1:2], in1=cst[:, CE + 4:CE + 5], op=ALU.pow), R=[cst], W=[st4b])
                    yield A(lambda e: e.activation(out=o[:], in_=h1[:], func=AF.Identity, scale=st4b[:, 2:3], bias=cst[:, CE + 2:CE + 3]), R=[h1, cst], W=[o, st4b])
                    yield G(lambda e: e.tensor_tensor(out=o[:], in0=o[:], in1=gfin[:], op=ALU.mult), R=[gfin], W=[o])
                    yield DMA("sp", ch_st[n % 2], lambda e: e.dma_start(out=outd[(n - 1) * 128:n * 128, :], in_=o[:]), R=[o], W=[])

                if nt > 1:
                    run(front3(1))
                if nt > 2:
                    run([mid3(1), front3(2)])
                elif nt > 1:
                    run(mid3(1))
                for n in range(1, nt):
                    streams = [tail3(n)]
                    if n + 1 < nt:
                        streams.append(mid3(n + 1))
                    if n + 2 < nt:
                        streams.append(front3(n + 2))
                    run(streams)
                S.barrier()
        else:
            S.barrier()
        es_bw.close()
    return nc


QPERM = [0, 4, 1, 5, 2, 6, 3, 7]


def make_consts():
    c = np.zeros((128, C_END), np.float32)
    c[:, C_ID:C_ID + 128] = np.eye(128, dtype=np.float32)
    s = np.arange(64)
    for j in range(2):
        rows = slice(64 * j, 64 * j + 64)
        c[rows, C_MB:C_MB + 64] = (s[None, :] > s[:, None])
        c[rows, C_MB + 64:C_MB + 128] = (s[None, :] >= s[:, None])
        c[rows, C_ML:C_ML + 64] = (s[None, :] < s[:, None])
        c[rows, C_I64:C_I64 + 64] = np.eye(64)
        c[rows, C_OBD + 64 * j:C_OBD + 64 * j + 64] = 1.0
        c[rows, C_TRI:C_TRI + 64] = CFAC * (s[:, None] <= s[None, :])
        c[rows, C_TRI + 64:C_TRI + 128] = CFAC * (s[:, None] < s[None, :])
    c[64:128, C_TRI0:C_TRI0 + 128] = c[64:128, C_TRI:C_TRI + 128]
    c[64:64 + 48, C_TRI0:C_TRI0 + 128] = 0.0
    i = np.arange(128)
    own = np.where(i[None, :] <= i[:, None], 0.0, NEG)
    prev = np.where(i[None, :] > i[:, None], 0.0, NEG)
    full = np.full((128, 128), NEG)
    for var, (a, b) in enumerate(((own, prev), (prev, own), (full, own))):
        base = C_AM + 272 * var
        c[:, base:base + 128] = a
        c[:, base + 128:base + 256] = b
        c[:, base + 256:base + 272] = 0.0
    half = 8
    inv_freq = np.power(np.float32(500000.0), -np.arange(half, dtype=np.float32) * np.float32(2.0 / 16)).astype(np.float32)
    for n in range(NTILES):
        pos = (n * 128 + np.arange(128) - 112).astype(np.float32)
        ang = (pos[:, None] * inv_freq[None, :]).astype(np.float32)
        c[:, C_ROPE + 16 * n:C_ROPE + 16 * n + 8] = np.cos(ang)
        c[:, C_ROPE + 16 * n + 8:C_ROPE + 16 * n + 16] = np.sin(ang)
    return c


def prep_shared(inp):
    f = np.float32
    w_in = np.asarray(inp["w_in"][0], f)
    b_in = np.asarray(inp["b_in"][0], f)
    qcols = np.concatenate([np.arange(h * 64, (h + 1) * 64) for h in QPERM])
    w_qkv = np.ascontiguousarray(np.concatenate([w_in[:, qcols], w_in[:, 512:768]], axis=1))
    b_qkv = np.concatenate([b_in[qcols], b_in[512:768]])
    R0 = 768
    w_fm = np.zeros((D, 2048), f)
    b_fm = np.zeros((2048,), f)
    mix = np.asarray(inp["rwkv_mix"][0], f)
    mix_fm = np.zeros((2048,), f)

    def put(dst0, src0, n):
        w_fm[:, dst0:dst0 + n] = w_in[:, R0 + src0:R0 + src0 + n]
        b_fm[dst0:dst0 + n] = b_in[R0 + src0:R0 + src0 + n]
        mix_fm[dst0:dst0 + n] = mix[src0:src0 + n]
    put(0, 0, 1536)
    put(1536, 1536, 64)
    put(1664, 1600, 64)
    put(1792, 1664, 128)
    put(1920, 1792, 32)
    G0 = 768 + 1824
    w_gate = np.ascontiguousarray(w_in[:, G0:G0 + 2048])
    b_gate = b_in[G0:G0 + 2048]
    rows_perm = qcols
    sh = {
        "w_qkv": w_qkv, "w_fm": w_fm, "w_gate": w_gate,
        "w_ba": np.ascontiguousarray(np.asarray(inp["w_br_attn"][0], f)[rows_perm, :]),
        "w_br": np.ascontiguousarray(np.asarray(inp["w_br_rwkv"][0], f)),
        "w_o": np.ascontiguousarray(np.asarray(inp["w_o"][0], f)),
        "w_fg": np.ascontiguousarray(np.asarray(inp["w_ffn_gate"][0], f)),
        "w_fu": np.ascontiguousarray(np.asarray(inp["w_ffn_up"][0], f)),
        "w_fd": np.ascontiguousarray(np.asarray(inp["w_ffn_down"][0], f)),
        "w2": np.ascontiguousarray(np.asarray(inp["rwkv_w2"][0], f)),
        "a2": np.ascontiguousarray(np.asarray(inp["rwkv_a2"][0], f)),
    }
    g2p = np.zeros((256, 512), f)
    g2p[0:160] = np.asarray(inp["rwkv_g2"][0], f)
    sh["g2p"] = g2p
    pfm = np.zeros((128, P_END), f)

    def fm(vec, ncol):
        return np.asarray(vec, f).reshape(ncol, 128).T
    pfm[:, P_GMIX:P_GMIX + 8] = fm(inp["norm_mix_g"][0], 8)
    pfm[:, P_GFFN:P_GFFN + 8] = fm(inp["norm_ffn_g"][0], 8)
    pfm[:, P_BFM:P_BFM + 16] = fm(b_fm, 16)
    pfm[:, P_BG:P_BG + 16] = fm(b_gate, 16)
    pfm[:, P_MIX:P_MIX + 16] = fm(mix_fm, 16)
    pfm[:, P_A0:P_A0 + 4] = fm(inp["rwkv_a0"][0], 4)
    pfm[:, P_KK:P_KK + 4] = fm(inp["rwkv_k_k"][0], 4)
    pfm[:, P_KA:P_KA + 4] = fm(inp["rwkv_k_a"][0], 4)
    pfm[:, P_RK:P_RK + 4] = fm(np.asarray(inp["rwkv_r_k"][0], f).reshape(-1), 4)
    sh["pfm"] = pfm
    rowsA = np.zeros((1, RA_END), f)
    rowsA[0, RA_BQ:RA_BQ + 768] = b_qkv
    rowsA[0, RA_W0:RA_W0 + 512] = np.asarray(inp["rwkv_w0"][0], f)
    rowsA[0, RA_SK:RA_SK + 8] = np.asarray(inp["attn_sinks"][0], f)[QPERM]
    sh["rowsA"] = rowsA
    sh["gfin"] = np.asarray(inp["norm_final_g"], f).reshape(1, D).copy()
    lnst = np.zeros((128, 2, 4, 64), f)
    for a, key in enumerate(("rwkv_ln_w", "rwkv_ln_b")):
        v = np.asarray(inp[key][0], f).reshape(4, 2, 64)
        for j in range(2):
            lnst[64 * j:64 * j + 64, a, :, :] = v[None, :, j, :]
    sh["lnst"] = lnst.reshape(128, -1)
    sh["cst"] = make_consts()
    return sh


def prep_xe(inp, b):
    xe = np.zeros((NTILES * 128, D), np.float32)
    xe[112:128] = np.asarray(inp["meta_tokens"], np.float32)
    xe[128:] = np.asarray(inp["x"][b], np.float32)
    return xe


_NC_CACHE = {}


def kernel(**inputs):
    n = 8
    sh = prep_shared(inputs)
    in_maps = []
    for b in range(n):
        m = dict(sh)
        m["xe"] = prep_xe(inputs, b)
        in_maps.append(m)
    if "nc" not in _NC_CACHE:
        _NC_CACHE["nc"] = build_program()
    res = run_bass_kernel_spmd(_NC_CACHE["nc"], in_maps, core_ids=list(range(n)))
    out = np.stack([np.asarray(r["out"], np.float32).reshape(4096, D) for r in res.results], axis=0)
    return out
```

```python
import numpy as np
import ml_dtypes
from contextlib import ExitStack
import concourse.bass as bass
import concourse.mybir as mybir
from concourse.bass_utils import run_bass_kernel_spmd

F32 = mybir.dt.float32
BF16 = mybir.dt.bfloat16
AF = mybir.ActivationFunctionType
ALU = mybir.AluOpType
AX = mybir.AxisListType

NTILES = 33
D = 1024
DFF = 2816
NFC = 22
RMS_EPS = 1e-6
LN_EPS = 64e-5
CFAC = -float(np.exp(-0.5))
NEG = -1e30
MD = BF16

C_ID = 0
C_MB = 128
C_ML = 256
C_I64 = 320
C_OBD = 384
C_TRI = 512
C_TRI0 = 640
C_AM = 768
C_ROPE = 768 + 816
C_END = C_ROPE + 33 * 16
P_GMIX, P_GFFN, P_BFM, P_BG, P_MIX, P_A0, P_KK, P_KA, P_RK, P_END = 0, 8, 16, 32, 48, 64, 68, 72, 76, 80
RA_BQ, RA_W0, RA_SK, RA_END = 0, 768, 1280, 1288


class Tile:
    def __init__(self, t, name, excl=False):
        self.t = t
        self.name = name
        self.w = None
        self.r = {}
        self.excl = excl
        self.tw = 0.0
        self.tr = 0.0
        self.weng = None

    def __getitem__(self, i):
        return self.t[i]


class Chan:
    def __init__(self, sem, key):
        self.sem = sem
        self.key = key
        self.count = 0


class Sched:
    def __init__(self, nc, es):
        self.nc = nc
        self.es = es
        self.E = {}
        for name, eng in (("pe", nc.tensor), ("act", nc.scalar), ("dve", nc.vector),
                          ("pool", nc.gpsimd), ("sp", nc.sync)):
            sem = es.enter_context(nc.semaphore("sem_" + name))
            self.E[name] = dict(eng=eng, sem=sem, count=0, seen={}, name=name)
        self.chans = []

    def chan(self, name):
        c = Chan(self.es.enter_context(self.nc.semaphore("ch_" + name)), "ch_" + name)
        self.chans.append(c)
        return c

    def _waits(self, E, R, W):
        deps = {}

        def add(d):
            key, val, sem = d
            if key not in deps or deps[key][0] < val:
                deps[key] = (val, sem)
        for t in R:
            if t.w is not None:
                add(t.w)
            if t.excl:
                for key, (val, sem) in t.r.items():
                    if key != E["name"]:
                        add((key, val, sem))
        for t in W:
            if t.w is not None:
                add(t.w)
            for key, (val, sem) in t.r.items():
                add((key, val, sem))
        for key, (val, sem) in deps.items():
            if key == "pe" and E["name"] == "pe":
                continue
            if E["seen"].get(key, 0) < val:
                E["eng"].wait_ge(sem, val)
                E["seen"][key] = val

    def op(self, ename, fns, R=(), W=()):
        E = self.E[ename]
        self._waits(E, R, W)
        if not isinstance(fns, (list, tuple)):
            fns = [fns]
        inst = None
        for f in fns:
            inst = f(E["eng"])
        E["count"] += 1
        inst.then_inc(E["sem"], 1)
        for t in W:
            t.w = (ename, E["count"], E["sem"])
            t.r = {}
        for t in R:
            if t not in W:
                t.r[ename] = (E["count"], E["sem"])

    def dma(self, qname, chan, fn, R=(), W=()):
        E = self.E[qname]
        self._waits(E, R, W)
        inst = fn(E["eng"])
        chan.count += 16
        inst.then_inc(chan.sem, 16)
        for t in W:
            t.w = (chan.key, chan.count, chan.sem)
            t.r = {}
        for t in R:
            t.r[chan.key] = (chan.count, chan.sem)

    def finalize(self, chan, tiles):
        for t in tiles:
            t.w = (chan.key, chan.count, chan.sem)

    def barrier(self):
        for name, E in self.E.items():
            for oname, O in self.E.items():
                if oname == name or O["count"] == 0:
                    continue
                if E["seen"].get(oname, 0) < O["count"]:
                    E["eng"].wait_ge(O["sem"], O["count"])
                    E["seen"][oname] = O["count"]
            for c in self.chans:
                if c.count and E["seen"].get(c.key, 0) < c.count:
                    E["eng"].wait_ge(c.sem, c.count)
                    E["seen"][c.key] = c.count


def build_program(nt=NTILES, phases=("A1", "A2", "B"), dbg=None, dbg_n=-1, md=None, scr_ext=False, stop=None):
    global MD
    if md is not None:
        MD = md
    nc = bass.Bass("TRN2", target_bir_lowering=False)

    def din(name, shape, dt=F32):
        return nc.dram_tensor(name, list(shape), dt, kind="ExternalInput").ap()

    xe = din("xe", [NTILES * 128, D])
    w_qkv = din("w_qkv", [D, 768])
    w_fm = din("w_fm", [D, 2048])
    w_gate = din("w_gate", [D, 2048])
    w_ba = din("w_ba", [512, D])
    w_br = din("w_br", [512, D])
    w_o = din("w_o", [D, D])
    w_fg = din("w_fg", [D, DFF])
    w_fu = din("w_fu", [D, DFF])
    w_fd = din("w_fd", [DFF, D])
    w2d = din("w2", [64, 512])
    a2d = din("a2", [64, 512])
    g2d = din("g2p", [256, 512])
    pfmd = din("pfm", [128, P_END])
    rowsAd = din("rowsA", [1, RA_END])
    gfind = din("gfin", [1, D])
    lnstd = din("lnst", [128, 2 * 4 * 64])
    cstd = din("cst", [128, C_END])
    outd = nc.dram_tensor("out", [(NTILES - 1) * 128, D], F32, kind="ExternalOutput").ap()
    skind = "ExternalOutput" if scr_ext else "Internal"
    yscr = nc.dram_tensor("yscr", [NTILES - 1, 128, 8 * 128], BF16, kind=skind).ap()
    mscr = nc.dram_tensor("mscr", [NTILES - 1, 128, 8 * 128], BF16, kind=skind).ap()
    dbg_out = {}
    if dbg:
        for name, shape in dbg.items():
            dbg_out[name] = nc.dram_tensor("dbg_" + name, list(shape), F32, kind="ExternalOutput").ap()

    with ExitStack() as es0:
        S = Sched(nc, es0)
        ch_w = S.chan("w")
        ch_x = [S.chan("x0"), S.chan("x1")]
        ch_y = [S.chan("y0"), S.chan("y1")]
        ch_st = [S.chan("s0"), S.chan("s1")]
        ch_dbg = S.chan("dbg")
        yscr_t = [Tile(None, "yscr%d" % i) for i in range(NTILES - 1)]
        mscr_t = [Tile(None, "mscr%d" % i) for i in range(NTILES - 1)]

        ST = {"defer": False, "small": False}
        eng_free = {"pe": 0.0, "act": 0.0, "dve": 0.0, "pool": 0.0, "sp": 0.0}
        import os as _os0
        import random as _random
        _cfg = _os0.environ.get("KCFG", "0,0.3,0.1,0.03,0.5,0.0").split(",")
        _rng = _random.Random(int(_cfg[0]))
        HOP = float(_cfg[1]); KPE = float(_cfg[2]); KPS = float(_cfg[3]); KVD = float(_cfg[4]); JIT = float(_cfg[5])

        class Op:
            __slots__ = ("ename", "fns", "R", "W", "dur", "chan")

            def __init__(self, ename, fns, R, W, dur, chan=None):
                self.ename = ename; self.fns = fns; self.R = R; self.W = W; self.dur = dur; self.chan = chan

        def est_start(op):
            t = eng_free[op.ename]
            for x in op.R:
                tw = getattr(x, "tw", 0.0)
                if x.weng != op.ename:
                    tw += HOP
                t = max(t, tw)
                if x.excl:
                    t = max(t, getattr(x, "tr", 0.0) + HOP)
            for x in op.W:
                t = max(t, getattr(x, "tw", 0.0) + (HOP if x.weng != op.ename else 0.0), getattr(x, "tr", 0.0) + HOP)
            return t

        def emit(op):
            t0 = est_start(op)
            t1 = t0 + op.dur
            if op.chan is None:
                S.op(op.ename, op.fns, op.R, op.W)
                eng_free[op.ename] = t1
            else:
                S.dma(op.ename, op.chan, op.fns, op.R, op.W)
                eng_free[op.ename] = t0 + 0.1
                t1 = t0 + 2.5
            for x in op.W:
                x.tw = t1; x.weng = op.ename; x.tr = 0.0
            for x in op.R:
                x.tr = max(getattr(x, "tr", 0.0), t1)

        def mkop(ename, fns, R, W, dur, chan=None):
            if JIT > 0:
                dur = dur * (1.0 + JIT * (_rng.random() - 0.5))
            op = Op(ename, fns, list(R), list(W), dur, chan)
            if ST["defer"]:
                return op
            emit(op)
            return None

        def V(fn, R=(), W=(), d=None):
            d = KVD if d is None else d
            return mkop("dve", fn, R, W, d)

        def A(fn, R=(), W=(), d=None):
            d = KVD if d is None else d
            return mkop("act", fn, R, W, d)

        def G(fn, R=(), W=(), d=1.2):
            return mkop("pool", fn, R, W, d)

        def PE(fns, R=(), W=(), d=None):
            n_ = len(fns) if isinstance(fns, (list, tuple)) else 1
            if d is None:
                d = n_ * (KPS if ST["small"] else KPE) + 0.1
            ST["small"] = False
            return mkop("pe", fns, R, W, d)

        def DMA(qname, chan, fn, R=(), W=()):
            return mkop(qname, fn, R, W, 2.5, chan)

        def dump(name, tile_ap, tiles):
            if name in dbg_out:
                S.dma("sp", ch_dbg, lambda e: e.dma_start(out=dbg_out[name], in_=tile_ap), R=tiles, W=[])

        def run(gens, bonus=None):
            if not isinstance(gens, (list, tuple)):
                gens = [gens]
            if bonus is None:
                bonus = [0.0] * len(gens)
            ST["defer"] = True
            heads = []
            for g in gens:
                heads.append(next(g, None))
            try:
                while True:
                    best = None
                    bt = None
                    for i, h in enumerate(heads):
                        if h is None:
                            continue
                        t = est_start(h) - bonus[i]
                        if bt is None or t < bt:
                            bt = t; best = i
                    if best is None:
                        break
                    ST["defer"] = False
                    emit(heads[best])
                    ST["defer"] = True
                    h = next(gens[best], None)
                    while h is None:
                        try:
                            h = next(gens[best])
                        except StopIteration:
                            h = None
                            break
                    heads[best] = h
            finally:
                ST["defer"] = False

        def par(gens, weights=None):
            return list(gens)

        def mk_alloc(es, pfx):
            def sb(name, shape, dt=F32):
                return Tile(es.enter_context(nc.sbuf_tensor(pfx + name, list(shape), dt)), pfx + name)
            return sb

        def norm_T(x, xs, st4, cst, ce, pfm, gcol, TR2, uT):
            yield A(lambda e: e.activation(out=xs[:], in_=x[:], func=AF.Square, accum_out=st4[:, 0:1]), R=[x], W=[xs, st4])
            yield V(lambda e: e.tensor_scalar(out=st4[:, 1:2], in0=st4[:, 0:1], scalar1=1.0 / D, scalar2=RMS_EPS, op0=ALU.mult, op1=ALU.add), R=[st4], W=[st4])
            yield G(lambda e: e.tensor_tensor(out=st4[:, 2:3], in0=st4[:, 1:2], in1=cst[:, ce + 4:ce + 5], op=ALU.pow), R=[st4, cst], W=[st4])
            yield A(lambda e: e.activation(out=xs[:], in_=x[:], func=AF.Identity, scale=st4[:, 2:3], bias=cst[:, ce + 2:ce + 3]),
                    R=[x, st4, cst], W=[xs])
            for h in range(2):
                yield PE([lambda e, c=c: e.transpose(TR2[h][:, (c % 4) * 128:(c % 4 + 1) * 128], xs[:, c * 128:(c + 1) * 128], cst[:, C_ID:C_ID + 128])
                          for c in range(4 * h, 4 * h + 4)], R=[xs, cst], W=[TR2[h]])
                yield V(lambda e, h=h: e.tensor_tensor(out=uT[:, 4 * h:4 * h + 4, :], in0=TR2[h][:].rearrange("p (c k) -> p c k", k=128),
                                                       in1=pfm[:, gcol + 4 * h:gcol + 4 * h + 4].unsqueeze(2).broadcast_to([128, 4, 128]), op=ALU.mult),
                        R=[TR2[h], pfm], W=[uT])

        def load_consts(sb, ncols):
            cst = sb("cst", [128, ncols + 12])
            pfm = sb("pfm", [128, P_END])
            G(lambda e: e.memset(cst[:, ncols:ncols + 1], RMS_EPS), W=[cst])
            G(lambda e: e.memset(cst[:, ncols + 1:ncols + 2], LN_EPS), W=[cst])
            G(lambda e: e.memset(cst[:, ncols + 2:ncols + 4], 0.0), W=[cst])
            G(lambda e: e.memset(cst[:, ncols + 4:ncols + 12], -0.5), W=[cst])
            S.dma("sp", ch_w, lambda e: e.dma_start(out=cst[:, 0:ncols], in_=cstd[:, 0:ncols]), W=[cst])
            S.dma("sp", ch_w, lambda e: e.dma_start(out=pfm[:], in_=pfmd), W=[pfm])
            return cst, pfm

        wl_n = [0]

        def wload_blk(tile_, out_ap, in_ap):
            wl_n[0] += 1
            ch = S.chan("wb%d" % wl_n[0])
            t = Tile(tile_.t, "%s_blk%d" % (tile_.name, wl_n[0]))
            S.dma("pool", ch, lambda e: e.dma_start(out=out_ap, in_=in_ap), W=[t])
            return t

        def wload(tile_, out_ap, in_ap):
            S.dma("pool", ch_w, lambda e: e.dma_start(out=out_ap, in_=in_ap), W=[tile_])

        if "A1" in phases:
            with ExitStack() as es:
                sb = mk_alloc(es, "a1_")
                CE = C_END
                cst, pfm = load_consts(sb, C_END)
                rowsA = sb("rowsA", [128, RA_END])
                S.dma("sp", ch_w, lambda e: e.dma_start(out=rowsA[:], in_=rowsAd.broadcast_to([128, RA_END])), W=[rowsA])
                lnst = sb("lnst", [128, 2, 4, 64])
                S.dma("sp", ch_w, lambda e: e.dma_start(out=lnst[:].rearrange("p a c v -> p (a c v)"), in_=lnstd), W=[lnst])
                wqkv = sb("wqkv", [128, 8, 768], BF16)
                wfm = sb("wfm", [128, 8, 2048], BF16)
                w2 = sb("w2", [64, 512])
                a2 = sb("a2", [64, 512])
                g2 = sb("g2", [128, 2, 512])
                S.dma("sp", ch_w, lambda e: e.dma_start(out=w2[:], in_=w2d), W=[w2])
                S.dma("sp", ch_w, lambda e: e.dma_start(out=a2[:], in_=a2d), W=[a2])
                S.dma("sp", ch_w, lambda e: e.dma_start(out=g2[:], in_=g2d.rearrange("(c p) n -> p c n", p=128)), W=[g2])
                wqkv_b = wload_blk(wqkv, wqkv[:], w_qkv.rearrange("(c p) n -> p c n", p=128))
                wfm_b = [wload_blk(wfm, wfm[:, :, 512 * g:512 * (g + 1)], w_fm.rearrange("(c p) n -> p c n", p=128)[:, :, 512 * g:512 * (g + 1)]) for g in range(4)]
                identb = sb("identb", [128, 128], BF16)
                ones2 = sb("ones2", [128, 2])
                G(lambda e: e.memset(ones2[:], 1.0), W=[ones2])
                S.finalize(ch_w, [cst, pfm, rowsA, lnst, w2, a2, g2])
                V(lambda e: e.tensor_copy(identb[:], cst[:, C_ID:C_ID + 128]), R=[cst], W=[identb])

                PS = [Tile(es.enter_context(nc.psum_tensor("ps%d" % i, [128, 512], F32)), "ps%d" % i, excl=True) for i in range(8)]
                H0, Q0, A0, A1_, A2_, R0, R1_, R2 = PS
                H1 = H0

                xb = [sb("xb0", [128, D]), sb("xb1", [128, D])]
                xs = sb("xs", [128, D])
                st4 = sb("st4", [128, 4])
                uT = sb("uT", [128, 8, 128], BF16)
                stg = sb("stg", [128, 16, 129])
                pfs = [sb("pf0", [128, 16, 128]), sb("pf1", [128, 16, 128])]
                qkvs = [sb("qkv0", [128, 768]), sb("qkv1", [128, 768])]
                rtmp = sb("rtmp", [128, 4, 10, 8])
                qT = sb("qT", [128, 4, 128], BF16)
                Kbuf = sb("Kbuf", [128, 272], BF16)
                Vbuf = sb("Vbuf", [128, 3, 128], BF16)
                Pb = [sb("Pb0", [128, 272], BF16), sb("Pb1", [128, 272], BF16)]
                PT = [sb("PT0", [128, 3, 128], BF16), sb("PT1", [128, 3, 128], BF16)]
                sm = sb("sm", [128, 5, 8])
                yat = sb("yat", [128, 8, 64])
                yTa = [sb("yTa0", [128, 4, 128], BF16), sb("yTa1", [128, 4, 128], BF16)]
                yTr = [sb("yTr0", [128, 4, 128], BF16), sb("yTr1", [128, 4, 128], BF16)]
                th = sb("th", [64, 128])
                sgd = sb("sgd", [128, 2, 128])
                B1 = sb("B1", [128, 4, 128]); B2 = sb("B2", [128, 4, 128]); B3 = sb("B3", [128, 4, 128])
                B4 = sb("B4", [128, 4, 128])
                arTs = [sb("arT%d" % i, [128, 4, 2, 2, 64], MD) for i in range(2)]
                BT = sb("BT", [128, 4, 128], MD); KT = sb("KT", [128, 4, 128], MD)
                BH = sb("BH", [128, 4, 128], MD); KH = sb("KH", [128, 4, 128], MD)
                cumC = sb("cumC", [128, 4, 2])
                WCs = [sb("WC%d" % i, [128, 4, 2]) for i in range(2)]
                rks = [sb("rk%d" % i, [128, 2, 4]) for i in range(2)]
                B5s = [sb("B5_%d" % i, [128, 4, 128]) for i in range(2)]
                TTs = [sb("TT%d" % i, [128, 2, 4, 64], MD) for i in range(2)]
                Ast = [sb("Ast0", [128, 2, 4, 64], MD), sb("Ast1", [128, 2, 4, 64], MD)]
                Nst = [sb("Nst0", [128, 2, 4, 64], MD), sb("Nst1", [128, 2, 4, 64], MD)]
                NBs = [sb("NB%d" % i, [128, 2, 4, 2, 64], MD) for i in range(2)]
                NKs = [sb("NK%d" % i, [128, 2, 4, 2, 64], MD) for i in range(2)]
                Pc = [sb("Pc0", [128, 2, 4, 64], MD), sb("Pc1", [128, 2, 4, 64], MD)]
                Vst32s = [sb("Vst32_%d" % i, [128, 2, 4, 64]) for i in range(2)]
                Vsts = [sb("Vst_%d" % i, [128, 2, 4, 64], MD) for i in range(2)] if MD != F32 else Vst32s
                BKsts = [sb("BKst%d" % i, [128, 2, 2, 4, 64], MD) for i in range(2)]
                R1 = sb("R1", [128, 4, 64], MD); Ust = sb("Ust", [128, 4, 64], MD)
                Yst = sb("Yst", [128, 2, 4, 64]); yc = sb("yc", [128, 2, 4, 64]); ysq = sb("ysq", [128, 2, 4, 64])
                ST32 = sb("ST32", [128, 4, 64])
                STt = sb("STt", [128, 4, 64])
                STm = sb("STm", [128, 4, 64], MD) if MD != F32 else ST32
                gst = sb("gst", [128, 6, 8])
                hb = sb("hb", [128, 4])
                V(lambda e: e.tensor_scalar(out=hb[:], in0=pfm[:, P_A0:P_A0 + 4], scalar1=0.5, scalar2=None, op0=ALU.mult), R=[pfm], W=[hb])
                G(lambda e: e.memset(ST32[:], 0.0), W=[ST32])
                if MD != F32:
                    G(lambda e: e.memset(STm[:], 0.0), W=[STm])
                G(lambda e: e.memset(stg[:], 0.0), W=[stg])
                G(lambda e: e.memset(Vbuf[:], 0.0), W=[Vbuf])
                G(lambda e: e.memset(Kbuf[:], 0.0), W=[Kbuf])

                def v3(ap2d, k=64):
                    return ap2d.rearrange("p (c k) -> p c k", k=k)

                def v4(ap2d):
                    return ap2d.rearrange("p (q c k) -> p q c k", q=2, c=4)

                def cq(t):
                    return t.rearrange("p c (q t) -> p c q t", q=2)

                def qc(t):
                    return t.rearrange("p c (q t) -> p q c t", q=2)

                ID0 = C_ID if MD == F32 else 0
                idm = cst if MD == F32 else identb

                def idsl(sl, j):
                    return cst[sl, C_ID + 64 * j:C_ID + 64 * j + 64]

                def idm_sl(sl, j):
                    return idm[sl, ID0 + 64 * j:ID0 + 64 * j + 64]

                def hl(fn, qs=(0,)):
                    ST["small"] = True
                    out = []
                    for q in qs:
                        for c in range(4):
                            for j in range(2):
                                out += fn(q, c, slice(64 * j, 64 * j + 64), j)
                    return out

                def head(n):
                    xt = xb[n % 2]
                    pf = pfs[n % 2]
                    qkv = qkvs[n % 2]
                    yield DMA("sp", ch_x[n % 2], lambda e: e.dma_start(out=xt[:], in_=xe[n * 128:(n + 1) * 128, :]), W=[xt])
                    yield from norm_T(xt, xs, st4, cst, CE, pfm, P_GMIX, [H0, H1], uT)
                    yield PE([lambda e, kc=kc: e.matmul(H0[:, 0:512], uT[:, kc, :], wqkv[:, kc, 0:512], start=(kc == 0), stop=(kc == 7)) for kc in range(8)],
                             R=[uT, wqkv_b], W=[H0])
                    yield V(lambda e: e.tensor_tensor(out=qkv[:, 0:512], in0=H0[:, 0:512], in1=rowsA[:, RA_BQ:RA_BQ + 512], op=ALU.add), R=[H0, rowsA], W=[qkv])
                    yield PE([lambda e, kc=kc: e.matmul(H1[:, 0:256], uT[:, kc, :], wqkv[:, kc, 512:768], start=(kc == 0), stop=(kc == 7)) for kc in range(8)],
                             R=[uT, wqkv_b], W=[H1])
                    yield V(lambda e: e.tensor_tensor(out=qkv[:, 512:768], in0=H1[:, 0:256], in1=rowsA[:, RA_BQ + 512:RA_BQ + 768], op=ALU.add), R=[H1, rowsA], W=[qkv])
                    for g in range(4):
                        bank = (H0, H1)[g % 2]
                        fns = []
                        for i in range(4):
                            col = (4 * g + i) * 128
                            for kc in range(8):
                                fns.append(lambda e, i=i, col=col, kc=kc, bank=bank: e.matmul(bank[:, i * 128:(i + 1) * 128], wfm[:, kc, col:col + 128], uT[:, kc, :],
                                                                                             start=(kc == 0), stop=(kc == 7)))
                        yield PE(fns, R=[uT, wfm_b[g]], W=[bank])
                        yield V(lambda e, g=g, bank=bank: e.tensor_tensor(out=stg[:, 4 * g:4 * g + 4, 1:129], in0=v3(bank[:], 128),
                                                                          in1=pfm[:, P_BFM + 4 * g:P_BFM + 4 * g + 4].unsqueeze(2).broadcast_to([128, 4, 128]), op=ALU.add),
                                R=[bank, pfm], W=[stg])
                    if n == 0:
                        yield G(lambda e: e.memset(stg[:, :, 1:113], 0.0), W=[stg])
                    yield G(lambda e: e.tensor_tensor(out=pf[:], in0=stg[:, :, 0:128], in1=stg[:, :, 1:129], op=ALU.subtract), R=[stg], W=[pf], d=3.6)
                    yield G(lambda e: e.tensor_tensor(out=pf[:], in0=pf[:], in1=pfm[:, P_MIX:P_MIX + 16].unsqueeze(2).broadcast_to([128, 16, 128]), op=ALU.mult),
                            R=[pfm], W=[pf], d=3.6)
                    yield G(lambda e: e.tensor_tensor(out=pf[:], in0=pf[:], in1=stg[:, :, 1:129], op=ALU.add), R=[stg], W=[pf], d=3.6)
                    yield G(lambda e: e.tensor_copy(stg[:, :, 0:1], stg[:, :, 128:129]), R=[], W=[stg])

                def attention(n):
                    qkv = qkvs[n % 2]
                    slot = n % 2
                    q10 = qkv[:, 0:640].rearrange("p (h d) -> p h d", d=64)
                    cosb = cst[:, C_ROPE + 16 * n:C_ROPE + 16 * n + 8].unsqueeze(1).broadcast_to([128, 10, 8])
                    sinb = cst[:, C_ROPE + 16 * n + 8:C_ROPE + 16 * n + 16].unsqueeze(1).broadcast_to([128, 10, 8])
                    yield G(lambda e: e.tensor_tensor(out=rtmp[:, 0], in0=q10[:, :, 0:8], in1=cosb, op=ALU.mult), R=[qkv, cst], W=[rtmp])
                    yield G(lambda e: e.tensor_tensor(out=rtmp[:, 1], in0=q10[:, :, 8:16], in1=sinb, op=ALU.mult), R=[qkv, cst], W=[rtmp])
                    yield G(lambda e: e.tensor_tensor(out=rtmp[:, 2], in0=q10[:, :, 8:16], in1=cosb, op=ALU.mult), R=[qkv, cst], W=[rtmp])
                    yield G(lambda e: e.tensor_tensor(out=rtmp[:, 3], in0=q10[:, :, 0:8], in1=sinb, op=ALU.mult), R=[qkv, cst], W=[rtmp])
                    yield G(lambda e: e.tensor_tensor(out=q10[:, :, 0:8], in0=rtmp[:, 0], in1=rtmp[:, 1], op=ALU.subtract), R=[rtmp], W=[qkv])
                    yield G(lambda e: e.tensor_tensor(out=q10[:, :, 8:16], in0=rtmp[:, 2], in1=rtmp[:, 3], op=ALU.add), R=[rtmp], W=[qkv])
                    yield PE([lambda e, c=c: e.transpose(A0[:, c * 128:(c + 1) * 128], qkv[:, c * 128:(c + 1) * 128], cst[:, C_ID:C_ID + 128]) for c in range(4)],
                             R=[qkv, cst], W=[A0])
                    yield PE(lambda e: e.transpose(A1_[:, 0:128], qkv[:, 512:640], cst[:, C_ID:C_ID + 128]), R=[qkv, cst], W=[A1_])
                    yield A(lambda e: e.activation(out=qT[:], in_=v3(A0[:], 128), func=AF.Copy, scale=0.125), R=[A0], W=[qT])
                    yield A(lambda e: e.activation(out=Kbuf[:, slot * 128:(slot + 1) * 128], in_=A1_[:, 0:128], func=AF.Copy), R=[A1_], W=[Kbuf])
                    yield V(lambda e: e.tensor_copy(Vbuf[:, slot, :], qkv[:, 640:768]), R=[qkv], W=[Vbuf])
                    if n == 0:
                        yield A(lambda e: e.activation(out=Kbuf[:, 256:272], in_=A1_[:, 112:128], func=AF.Copy), R=[A1_], W=[Kbuf])
                        yield PE(lambda e: e.matmul(A1_[0:16, 128:256], cst[:, C_ID + 112:C_ID + 128], qkv[:, 640:768], start=True, stop=True), R=[qkv, cst], W=[A1_])
                        yield V(lambda e: e.tensor_copy(Vbuf[0:16, 2, :], A1_[0:16, 128:256]), R=[A1_], W=[Vbuf])
                        return
                    yTn = yTa[n % 2]
                    mvar = 2 if n == 1 else (0 if n % 2 == 0 else 1)
                    mask = cst[:, C_AM + 272 * mvar:C_AM + 272 * (mvar + 1)]
                    for s in range(8):
                        c, j = s // 2, s % 2
                        sl = slice(64 * j, 64 * j + 64)
                        SC = A0
                        Pk = Pb[s % 2]
                        PTk = PT[s % 2]
                        yield PE(lambda e: e.matmul(SC[:, 0:272], qT[sl, c, :], Kbuf[sl, 0:272], start=True, stop=True), R=[qT, Kbuf], W=[SC])
                        yield V(lambda e: e.tensor_tensor(out=SC[:, 0:272], in0=SC[:, 0:272], in1=mask, op=ALU.add), R=[cst], W=[SC])
                        yield V(lambda e: e.tensor_reduce(out=sm[:, 0, s:s + 1], in_=SC[:, 0:272], axis=AX.X, op=ALU.max), R=[SC], W=[sm])
                        yield V(lambda e: e.tensor_scalar(out=sm[:, 1, s:s + 1], in0=sm[:, 0, s:s + 1], scalar1=rowsA[:, RA_SK + s:RA_SK + s + 1], scalar2=-1.0,
                                                          op0=ALU.max, op1=ALU.mult), R=[rowsA], W=[sm])
                        yield A(lambda e: e.activation(out=Pk[:], in_=SC[:, 0:272], func=AF.Exp, bias=sm[:, 1, s:s + 1], scale=1.0,
                                                       accum_out=sm[:, 2, s:s + 1]), R=[SC], W=[Pk, sm])
                        yield PE([lambda e, b=b, nk=nk: e.matmul(A1_[0:nk, b * 128:(b + 1) * 128], Pk[:, b * 128:b * 128 + nk], identb[:], start=True, stop=True)
                                  for b, nk in ((0, 128), (1, 128), (2, 16))], R=[Pk, identb], W=[A1_])
                        yield A(lambda e: e.activation(out=PTk[:], in_=v3(A1_[:, 0:384], 128), func=AF.Copy), R=[A1_], W=[PTk])
                        yield PE([lambda e: e.matmul(A2_[:, s * 64:(s + 1) * 64], PTk[:, 0, :], Vbuf[:, 0, sl], start=True, stop=False),
                                  lambda e: e.matmul(A2_[:, s * 64:(s + 1) * 64], PTk[:, 1, :], Vbuf[:, 1, sl], start=False, stop=False),
                                  lambda e: e.matmul(A2_[:, s * 64:(s + 1) * 64], PTk[0:16, 2, :], Vbuf[0:16, 2, sl], start=False, stop=True)],
                                 R=[PTk, Vbuf], W=[A2_])
                    yield V(lambda e: e.tensor_tensor(out=sm[:, 3, :], in0=rowsA[:, RA_SK:RA_SK + 8], in1=sm[:, 1, :], op=ALU.add), R=[rowsA], W=[sm])
                    yield A(lambda e: e.activation(out=sm[:, 3, :], in_=sm[:, 3, :], func=AF.Exp), R=[], W=[sm])
                    yield V(lambda e: e.tensor_tensor(out=sm[:, 2, :], in0=sm[:, 2, :], in1=sm[:, 3, :], op=ALU.add), R=[], W=[sm])
                    yield V(lambda e: e.reciprocal(out=sm[:, 4, :], in_=sm[:, 2, :]), R=[], W=[sm])
                    yield V(lambda e: e.tensor_tensor(out=yat[:], in0=v3(A2_[:], 64), in1=sm[:, 4, :].unsqueeze(2).broadcast_to([128, 8, 64]), op=ALU.mult),
                            R=[A2_], W=[yat, sm])
                    yield PE([lambda e, c=c: e.transpose(A1_[:, c * 128:(c + 1) * 128], yat[:, 2 * c:2 * c + 2, :].rearrange("p a d -> p (a d)"), cst[:, C_ID:C_ID + 128])
                              for c in range(4)], R=[yat, cst], W=[A1_])
                    yield A(lambda e: e.activation(out=yTn[:], in_=v3(A1_[:], 128), func=AF.Copy), R=[A1_], W=[yTn])
                    if n == dbg_n:
                        dump("yat", yat[:].rearrange("p s d -> p (s d)"), [yat])
                    yield DMA("sp", ch_y[n % 2], lambda e: e.dma_start(out=yscr[n - 1][:, 0:512], in_=yTn[:].rearrange("p c t -> p (c t)")), R=[yTn], W=[yscr_t[n - 1]])

                def pre(n):
                    pf = pfs[n % 2]
                    pp_ = n % 2
                    arT, NB, NK, Vst, Vst32, BKst, WC, rk, B5 = arTs[pp_], NBs[pp_], NKs[pp_], Vsts[pp_], Vst32s[pp_], BKsts[pp_], WCs[pp_], rks[pp_], B5s[pp_]
                    tri0 = C_TRI0 if n == 0 else C_TRI
                    tri = cst[:, tri0:tri0 + 128]
                    yield A(lambda e: e.activation(out=th[:], in_=pf[0:64, 12, :], func=AF.Tanh), R=[pf], W=[th])
                    yield PE(lambda e: e.matmul(R0[:, 0:512], th[:], w2[:], start=True, stop=True), R=[th, w2], W=[R0])
                    B4f = B4[:].rearrange("p c t -> p (c t)")
                    yield V(lambda e: e.tensor_tensor(out=B4f, in0=R0[:, 0:512], in1=rowsA[:, RA_W0:RA_W0 + 512], op=ALU.add), R=[R0, rowsA], W=[B4])
                    yield A(lambda e: e.activation(out=B4f, in_=B4f, func=AF.Tanh, scale=0.5), R=[], W=[B4])
                    yield V(lambda e: e.tensor_scalar(out=B4f, in0=B4f, scalar1=0.5, scalar2=0.5, op0=ALU.mult, op1=ALU.add), R=[], W=[B4])
                    CB = (R1_, R2)
                    for q in range(2):
                        yield PE([lambda e, c=c, q=q: e.matmul(CB[q][:, c * 128:(c + 1) * 128], B4f[64 * q:64 * q + 64, c * 128:(c + 1) * 128],
                                                               tri[64 * q:64 * q + 64, :], start=True, stop=True) for c in range(4)], R=[B4, cst], W=[CB[q]])

                    def cums(a):
                        return [CB[q][:].rearrange("p (c a t) -> p c a t", c=4, a=2)[:, :, a, :] for q in range(2)]
                    yield PE([lambda e, c=c: e.matmul(R0[:, c * 128:(c + 1) * 128], a2[:, c * 128:(c + 1) * 128], pf[0:64, 13, :], start=True, stop=True) for c in range(4)],
                             R=[pf, a2], W=[R0])
                    for c in range(4):
                        yield A(lambda e, c=c: e.activation(out=B3[:, c, :], in_=R0[:, c * 128:(c + 1) * 128], func=AF.Tanh, bias=hb[:, c:c + 1], scale=0.5),
                                R=[R0, hb], W=[B3])
                    yield V(lambda e: e.tensor_scalar(out=B3[:], in0=B3[:], scalar1=0.5, scalar2=0.5, op0=ALU.mult, op1=ALU.add), R=[], W=[B3])
                    yield A(lambda e: e.activation(out=sgd[:], in_=pf[:, 14:16, :], func=AF.Tanh, scale=0.5), R=[pf], W=[sgd])
                    yield V(lambda e: e.tensor_scalar(out=sgd[:], in0=sgd[:], scalar1=0.5, scalar2=0.5, op0=ALU.mult, op1=ALU.add), R=[], W=[sgd])
                    fns = []
                    for c in range(4):
                        fns.append(lambda e, c=c: e.matmul(R0[:, c * 128:(c + 1) * 128], g2[:, 0, c * 128:(c + 1) * 128], sgd[:, 0, :], start=True, stop=False))
                        fns.append(lambda e, c=c: e.matmul(R0[:, c * 128:(c + 1) * 128], g2[0:32, 1, c * 128:(c + 1) * 128], sgd[0:32, 1, :], start=False, stop=True))
                    yield PE(fns, R=[sgd, g2], W=[R0])
                    yield A(lambda e: e.activation(out=B5[:].rearrange("p c t -> p (c t)"), in_=R0[:], func=AF.Copy), R=[R0], W=[B5])
                    kview = pf[:, 4:8, :]
                    rview = pf[:, 0:4, :]

                    def bc(col):
                        return pfm[:, col:col + 4].unsqueeze(2).broadcast_to([128, 4, 128])
                    yield V(lambda e: e.tensor_tensor(out=B1[:], in0=kview, in1=bc(P_KK), op=ALU.mult), R=[pf, pfm], W=[B1])
                    yield V(lambda e: e.tensor_tensor(out=B2[:], in0=B1[:], in1=B1[:], op=ALU.mult), R=[B1], W=[B2])
                    yield PE([lambda e, c=c: e.matmul(R0[:, c * 128:(c + 1) * 128], cst[:, C_OBD:C_OBD + 128], B2[:, c, :], start=True, stop=True) for c in range(4)],
                             R=[B2, cst], W=[R0])
                    yield A(lambda e: e.activation(out=B2[:].rearrange("p c t -> p (c t)"), in_=R0[:], func=AF.Sqrt), R=[R0], W=[B2])
                    yield V(lambda e: e.tensor_scalar(out=B2[:], in0=B2[:], scalar1=1e-12, scalar2=None, op0=ALU.max), R=[], W=[B2])
                    yield V(lambda e: e.reciprocal(out=B2[:], in_=B2[:]), R=[], W=[B2])
                    yield V(lambda e: e.tensor_tensor(out=B1[:], in0=B1[:], in1=B2[:], op=ALU.mult), R=[B2], W=[B1])
                    yield V(lambda e: e.scalar_tensor_tensor(out=B2[:], in0=B3[:], scalar=-1.0, in1=bc(P_KA), op0=ALU.add, op1=ALU.mult), R=[B3, pfm], W=[B2])
                    yield V(lambda e: e.scalar_tensor_tensor(out=kview, in0=B2[:], scalar=1.0, in1=kview, op0=ALU.add, op1=ALU.mult), R=[B2], W=[pf])
                    yield V(lambda e: e.tensor_tensor(out=B3[:], in0=B1[:], in1=B3[:], op=ALU.mult), R=[B1], W=[B3])
                    cex = cums(1)
                    cin = cums(0)
                    B4q = cq(B4[:])
                    for hh in range(2):
                        yield A(lambda e, hh=hh: e.activation(out=B4[:, :, 64 * hh:64 * hh + 64], in_=cex[hh], func=AF.Exp), R=[CB[hh]], W=[B4])
                    yield V(lambda e: e.scalar_tensor_tensor(out=arT[:, :, :, 0, :], in0=cq(B1[:]), scalar=-1.0, in1=B4q, op0=ALU.mult, op1=ALU.mult), R=[B1, B4], W=[arT])
                    for hh in range(2):
                        yield A(lambda e, hh=hh: e.activation(out=B4[:, :, 64 * hh:64 * hh + 64], in_=cin[hh], func=AF.Exp), R=[CB[hh]], W=[B4])
                    yield V(lambda e: e.tensor_tensor(out=arT[:, :, :, 1, :], in0=cq(rview), in1=B4q, op=ALU.mult), R=[pf, B4], W=[arT])
                    for hh in range(2):
                        yield A(lambda e, hh=hh: e.activation(out=B4[:, :, 64 * hh:64 * hh + 64], in_=cin[hh], func=AF.Exp, scale=-1.0), R=[CB[hh]], W=[B4])
                    yield V(lambda e: e.tensor_tensor(out=BT[:], in0=B3[:], in1=B4[:], op=ALU.mult), R=[B3, B4], W=[BT])
                    yield V(lambda e: e.tensor_tensor(out=KT[:], in0=kview, in1=B4[:], op=ALU.mult), R=[pf, B4], W=[KT])
                    for hh in range(2):
                        yield V(lambda e, hh=hh: e.tensor_copy(cumC[:, :, hh], cin[hh][:, :, 63]), R=[CB[hh]], W=[cumC])
                    for c in range(4):
                        for q in range(2):
                            yield A(lambda e, c=c, q=q: e.activation(out=B4[:, c, 64 * q:64 * q + 64], in_=cin[q][:, c, :], func=AF.Exp, scale=-1.0,
                                                                     bias=cumC[:, c, q:q + 1]), R=[CB[q], cumC], W=[B4])
                    yield V(lambda e: e.tensor_tensor(out=BH[:], in0=B3[:], in1=B4[:], op=ALU.mult), R=[B3, B4], W=[BH])
                    yield V(lambda e: e.tensor_tensor(out=KH[:], in0=kview, in1=B4[:], op=ALU.mult), R=[pf, B4], W=[KH])
                    yield A(lambda e: e.activation(out=WC[:], in_=cumC[:], func=AF.Exp), R=[cumC], W=[WC])

                    mb = cst[:, C_MB:C_MB + 128].rearrange("p (a t) -> p a t", a=2).unsqueeze(1).broadcast_to([128, 4, 2, 64])
                    ml8 = cst[:, C_ML:C_ML + 64].unsqueeze(1).broadcast_to([128, 8, 64])
                    i8 = cst[:, C_I64:C_I64 + 64].unsqueeze(1).broadcast_to([128, 8, 64])
                    Q2 = (0, 1)

                    def tq(q):
                        return slice(64 * q, 64 * q + 64)
                    yield PE(hl(lambda q, c, sl, j: [lambda e: e.matmul(R0[sl, q * 256 + c * 64:q * 256 + (c + 1) * 64], arT[sl, c, q, 0, :], BT[sl, c, tq(q)], start=True, stop=True)], Q2),
                             R=[arT, BT], W=[R0])
                    for q in Q2:
                        yield PE(hl(lambda q, c, sl, j: [lambda e: e.matmul(CB[q][sl, c * 128:(c + 1) * 128], BT[sl, c, tq(q)], arT[sl, c, q, :, :].rearrange("p a t -> p (a t)"),
                                                                            start=True, stop=True)], (q,)), R=[arT, BT], W=[CB[q]])
                    yield V(lambda e: e.tensor_tensor(out=Ast[0][:].rearrange("p q c t -> p (q c) t"), in0=v3(R0[:]), in1=ml8, op=ALU.mult), R=[R0, cst], W=[Ast[0]])
                    for q in Q2:
                        yield V(lambda e, q=q: e.tensor_tensor(out=NB[:, q], in0=CB[q][:].rearrange("p (c a t) -> p c a t", c=4, a=2), in1=mb, op=ALU.mult), R=[CB[q], cst], W=[NB])
                    for q in Q2:
                        yield PE(hl(lambda q, c, sl, j: [lambda e: e.matmul(CB[q][sl, c * 128:(c + 1) * 128], KT[sl, c, tq(q)], arT[sl, c, q, :, :].rearrange("p a t -> p (a t)"),
                                                                            start=True, stop=True)], (q,)), R=[arT, KT], W=[CB[q]])
                    for q in Q2:
                        yield V(lambda e, q=q: e.tensor_tensor(out=NK[:, q], in0=CB[q][:].rearrange("p (c a t) -> p c a t", c=4, a=2), in1=mb, op=ALU.mult), R=[CB[q], cst], W=[NK])
                    yield G(lambda e: e.tensor_copy(Nst[0][:], NB[:, :, :, 0, :]), R=[NB], W=[Nst[0]])
                    yield G(lambda e: e.tensor_tensor(out=Pc[0][:].rearrange("p q c t -> p (q c) t"), in0=Nst[0][:].rearrange("p q c t -> p (q c) t"), in1=i8, op=ALU.add),
                            R=[Nst[0], cst], W=[Pc[0]])
                    yield PE(hl(lambda q, c, sl, j: [lambda e: e.matmul(R0[sl, q * 256 + c * 64:q * 256 + (c + 1) * 64], pf[sl, 8 + c, tq(q)], idsl(sl, j), start=True, stop=True)], Q2),
                             R=[pf, cst], W=[R0])
                    yield A(lambda e: e.activation(out=Vst32[:], in_=v4(R0[:]), func=AF.Copy), R=[R0], W=[Vst32])
                    if MD != F32:
                        yield V(lambda e: e.tensor_copy(Vst[:], v4(R0[:])), R=[R0], W=[Vst])
                    for q in Q2:
                        yield PE(hl(lambda q, c, sl, j: [lambda e: e.matmul(CB[q][sl, c * 64:(c + 1) * 64], BH[sl, c, tq(q)], idm_sl(sl, j), start=True, stop=True),
                                                         lambda e: e.matmul(CB[q][sl, 256 + c * 64:256 + (c + 1) * 64], KH[sl, c, tq(q)], idm_sl(sl, j), start=True, stop=True)], (q,)),
                                 R=[BH, KH, idm], W=[CB[q]])
                    for q in Q2:
                        yield A(lambda e, q=q: e.activation(out=BKst[:, q], in_=CB[q][:].rearrange("p (a c t) -> p a c t", a=2, c=4), func=AF.Copy), R=[CB[q]], W=[BKst])
                    cur = 0

                    def sq_fns(cur, lvl):
                        f = hl(lambda q, c, sl, j: [lambda e: e.matmul(R0[sl, q * 256 + c * 64:q * 256 + (c + 1) * 64], Nst[cur][sl, q, c, :], Ast[cur][sl, q, c, :], start=True, stop=True)], Q2)
                        if lvl < 5:
                            f += hl(lambda q, c, sl, j: [lambda e: e.matmul(R1_[sl, q * 256 + c * 64:q * 256 + (c + 1) * 64], Ast[cur][sl, q, c, :], Nst[cur][sl, q, c, :], start=True, stop=True)], Q2)
                        return f

                    def pp_fns(a_t, pc):
                        return hl(lambda q, c, sl, j: [lambda e: e.matmul(R2[sl, q * 256 + c * 64:q * 256 + (c + 1) * 64], a_t[sl, q, c, :], pc[sl, q, c, :], start=True, stop=True)], Q2)

                    yield PE(sq_fns(0, 1), R=[Nst[0], Ast[0]], W=[R0, R1_])
                    for lvl in range(1, 6):
                        nxt = 1 - cur
                        yield A(lambda e, nxt=nxt: e.activation(out=Ast[nxt][:], in_=v4(R0[:]), func=AF.Copy), R=[R0], W=[Ast[nxt]])
                        if lvl < 5:
                            yield V(lambda e, nxt=nxt: e.tensor_copy(Nst[nxt][:], v4(R1_[:])), R=[R1_], W=[Nst[nxt]])
                        pc, pn = Pc[(lvl - 1) % 2], (Pc[lvl % 2] if lvl < 5 else TTs[pp_])
                        fns = pp_fns(Ast[nxt], pc)
                        Wl = [R2]
                        Rl = [Ast[nxt], pc]
                        if lvl < 5:
                            fns += sq_fns(nxt, lvl + 1)
                            Wl += [R0] + ([R1_] if lvl + 1 < 5 else [])
                            Rl += [Nst[nxt]]
                        yield PE(fns, R=Rl, W=Wl)
                        yield V(lambda e, pc=pc, pn=pn: e.tensor_tensor(out=pn[:], in0=v4(R2[:]), in1=pc[:], op=ALU.add), R=[R2, pc], W=[pn])
                        cur = nxt
                    yield G(lambda e: e.tensor_tensor(out=B2[:], in0=rview, in1=kview, op=ALU.mult), R=[pf], W=[B2])
                    yield G(lambda e: e.tensor_tensor(out=B2[:], in0=B2[:], in1=bc(P_RK), op=ALU.mult), R=[pfm], W=[B2])
                    fns = []
                    for q in range(2):
                        for c in range(4):
                            for j in range(2):
                                sl = slice(64 * j, 64 * j + 64)
                                fns.append(lambda e, q=q, c=c, sl=sl: e.matmul(R0[sl, (q * 4 + c) * 2:(q * 4 + c) * 2 + 2], B2[sl, c, 64 * q:64 * q + 64], ones2[sl, :],
                                                                              start=True, stop=True))
                    yield PE(fns, R=[B2, ones2], W=[R0])
                    yield A(lambda e: e.activation(out=rk[:].rearrange("p q c -> p (q c)"), in_=R0[:, 0:16].rearrange("p (x two) -> p x two", two=2)[:, :, 0], func=AF.Copy), R=[R0], W=[rk])
                    return

                def seq(n):
                    pp_ = n % 2
                    arT, NB, NK, Vst, Vst32, BKst, WC, rk, B5 = arTs[pp_], NBs[pp_], NKs[pp_], Vsts[pp_], Vst32s[pp_], BKsts[pp_], WCs[pp_], rks[pp_], B5s[pp_]
                    TT = TTs[pp_]
                    Q2 = (0, 1)
                    for q in Q2:
                        yield PE(hl(lambda q, c, sl, j: [lambda e: e.matmul(Q0[sl, c * 64:(c + 1) * 64], arT[sl, c, q, 0, :], STm[sl, c, :], start=True, stop=False),
                                                         lambda e: e.matmul(Q0[sl, c * 64:(c + 1) * 64], NK[sl, q, c, 0, :], Vst[sl, q, c, :], start=False, stop=True)], (q,)),
                                 R=[arT, STm, NK, Vst], W=[Q0])
                        yield A(lambda e: e.activation(out=R1[:], in_=v3(Q0[:, 0:256]), func=AF.Copy), R=[Q0], W=[R1])
                        yield PE(hl(lambda q, c, sl, j: [lambda e: e.matmul(Q0[sl, 256 + c * 64:256 + (c + 1) * 64], TT[sl, q, c, :], R1[sl, c, :], start=True, stop=True)], (q,)),
                                 R=[TT, R1], W=[Q0])
                        yield A(lambda e: e.activation(out=Ust[:], in_=v3(Q0[:, 256:512]), func=AF.Copy), R=[Q0], W=[Ust])
                        if n >= 1:
                            yield PE(hl(lambda q, c, sl, j: [lambda e: e.matmul(Q0[sl, c * 64:(c + 1) * 64], arT[sl, c, q, 1, :], STm[sl, c, :], start=True, stop=False),
                                                             lambda e: e.matmul(Q0[sl, c * 64:(c + 1) * 64], NB[sl, q, c, 1, :], Ust[sl, c, :], start=False, stop=False),
                                                             lambda e: e.matmul(Q0[sl, c * 64:(c + 1) * 64], NK[sl, q, c, 1, :], Vst[sl, q, c, :], start=False, stop=True)], (q,)),
                                     R=[arT, STm, NB, Ust, NK, Vst], W=[Q0])
                            yield V(lambda e, q=q: e.tensor_copy(Yst[:, q], v3(Q0[:, 0:256])), R=[Q0], W=[Yst])
                        yield PE(hl(lambda q, c, sl, j: [lambda e: e.matmul(Q0[sl, 256 + c * 64:256 + (c + 1) * 64], BKst[sl, q, 0, c, :], Ust[sl, c, :], start=True, stop=False),
                                                         lambda e: e.matmul(Q0[sl, 256 + c * 64:256 + (c + 1) * 64], BKst[sl, q, 1, c, :], Vst[sl, q, c, :], start=False, stop=True)], (q,)),
                                 R=[BKst, Ust, Vst], W=[Q0])
                        yield G(lambda e, q=q: e.tensor_tensor(out=STt[:], in0=ST32[:], in1=WC[:, :, q:q + 1].broadcast_to([128, 4, 64]), op=ALU.mult), R=[WC, ST32], W=[STt])
                        if MD != F32:
                            yield V(lambda e: e.tensor_tensor(out=STm[:], in0=STt[:], in1=v3(Q0[:, 256:512]), op=ALU.add), R=[Q0, STt], W=[STm])
                        yield V(lambda e: e.tensor_tensor(out=ST32[:], in0=STt[:], in1=v3(Q0[:, 256:512]), op=ALU.add), R=[Q0, STt], W=[ST32])
                    if n == 0:
                        return
                    Y8 = Yst[:].rearrange("p q c v -> p (q c) v")
                    yc8 = yc[:].rearrange("p q c v -> p (q c) v")
                    ysq8 = ysq[:].rearrange("p q c v -> p (q c) v")
                    V32_8 = Vst32[:].rearrange("p q c v -> p (q c) v")

                    def b8(ap):
                        return ap.unsqueeze(2).broadcast_to([128, 8, 64])
                    yield V(lambda e: e.tensor_reduce(out=gst[:, 0, :], in_=Y8, axis=AX.X, op=ALU.add), R=[Yst], W=[gst])
                    yield V(lambda e: e.tensor_scalar(out=gst[:, 1, :], in0=gst[:, 0, :], scalar1=-1.0 / 64, scalar2=None, op0=ALU.mult), R=[], W=[gst])
                    yield V(lambda e: e.tensor_tensor(out=yc8, in0=Y8, in1=b8(gst[:, 1, :]), op=ALU.add), R=[Yst], W=[yc, gst])
                    yield G(lambda e: e.tensor_tensor(out=ysq8, in0=yc8, in1=yc8, op=ALU.mult), R=[yc], W=[ysq])
                    yield V(lambda e: e.tensor_reduce(out=gst[:, 2, :], in_=ysq8, axis=AX.X, op=ALU.add), R=[ysq], W=[gst])
                    yield V(lambda e: e.tensor_scalar(out=gst[:, 3, :], in0=gst[:, 2, :], scalar1=1.0 / 64, scalar2=LN_EPS, op0=ALU.mult, op1=ALU.add), R=[], W=[gst])
                    yield G(lambda e: e.tensor_tensor(out=gst[:, 4, :], in0=gst[:, 3, :], in1=cst[:, CE + 4:CE + 12], op=ALU.pow), R=[cst], W=[gst])
                    yield V(lambda e: e.tensor_tensor(out=yc8, in0=yc8, in1=b8(gst[:, 4, :]), op=ALU.mult), R=[], W=[yc, gst])
                    yield G(lambda e: e.tensor_tensor(out=yc[:], in0=yc[:], in1=lnst[:, 0].unsqueeze(1).broadcast_to([128, 2, 4, 64]), op=ALU.mult), R=[lnst], W=[yc])
                    yield G(lambda e: e.tensor_tensor(out=yc[:], in0=yc[:], in1=lnst[:, 1].unsqueeze(1).broadcast_to([128, 2, 4, 64]), op=ALU.add), R=[lnst], W=[yc])
                    yield V(lambda e: e.tensor_tensor(out=ysq8, in0=V32_8, in1=b8(rk[:].rearrange("p q c -> p (q c)")), op=ALU.mult), R=[Vst32, rk], W=[ysq])
                    yield V(lambda e: e.tensor_tensor(out=yc8, in0=yc8, in1=ysq8, op=ALU.add), R=[ysq], W=[yc])
                    yield PE(hl(lambda q, c, sl, j: [lambda e: e.matmul(Q0[sl, q * 256 + c * 64:q * 256 + (c + 1) * 64], yc[sl, q, c, :], idsl(sl, j), start=True, stop=True)], Q2),
                             R=[yc, cst], W=[Q0])
                    yTn = yTr[n % 2]
                    yield V(lambda e: e.tensor_tensor(out=qc(yTn[:]), in0=v4(Q0[:]), in1=qc(B5[:]), op=ALU.mult), R=[Q0, B5], W=[yTn])
                    yield DMA("sp", ch_st[n % 2], lambda e: e.dma_start(out=yscr[n - 1][:, 512:1024], in_=yTn[:].rearrange("p c t -> p (c t)")), R=[yTn], W=[yscr_t[n - 1]])

                import os as _os
                W_PRE, W_ATT, W_SEQ, W_HEAD = [int(v) for v in _os.environ.get("KW", "3,3,1,1").split(",")]
                run(head(0))
                if nt > 1:
                    run(par([pre(0), attention(0), head(1)], [W_PRE, W_ATT, W_HEAD]))
                else:
                    run(par([pre(0), attention(0)], [W_PRE, W_ATT]))
                _skip = _os.environ.get("KSKIP", "")
                B_SEQ, B_PRE, B_ATT, B_HEAD = [float(v) for v in _os.environ.get("KB", "0,0,0,0").split(",")]
                for i in range(nt):
                    streams = [seq(i)] if "seq" not in _skip else []
                    bon = [B_SEQ] if "seq" not in _skip else []
                    if i + 1 < nt:
                        if "pre" not in _skip:
                            streams += [pre(i + 1)]
                            bon += [B_PRE]
                        if "att" not in _skip:
                            streams += [attention(i + 1)]
                            bon += [B_ATT]
                    if i + 2 < nt:
                        streams.append(head(i + 2))
                        bon.append(B_HEAD)
                    run(streams, bon)
                S.barrier()

        es_bw = ExitStack()
        pre_w = {}
        if "B" in phases:
            sbw_ = mk_alloc(es_bw, "bw_")
            pre_w["wg"] = sbw_("wg", [128, 8, DFF], BF16)
            pre_w["wu"] = sbw_("wu", [128, 8, DFF], BF16)

        def load_bw():
            wg, wu = pre_w["wg"], pre_w["wu"]
            ngrp_ = (NFC + 3) // 4
            wg_b = []
            wu_b = []
            for g in range(ngrp_):
                c0, c1 = 512 * g, min(512 * (g + 1), DFF)
                wg_b.append(wload_blk(wg, wg[:, :, c0:c1], w_fg.rearrange("(c p) n -> p c n", p=128)[:, :, c0:c1]))
                wu_b.append(wload_blk(wu, wu[:, :, c0:c1], w_fu.rearrange("(c p) n -> p c n", p=128)[:, :, c0:c1]))
            pre_w["wg_b"] = wg_b
            pre_w["wu_b"] = wu_b

        if "A2" in phases:
            with ExitStack() as es:
                sb = mk_alloc(es, "a2_")
                CE = 128
                cst, pfm = load_consts(sb, 128)
                wgate = sb("wgate", [128, 8, 2048], BF16)
                wba = sb("wba", [128, 4, D], BF16)
                wbr = sb("wbr", [128, 4, D], BF16)
                wgate_b = [wload_blk(wgate, wgate[:, :, 512 * g:512 * (g + 1)], w_gate.rearrange("(c p) n -> p c n", p=128)[:, :, 512 * g:512 * (g + 1)]) for g in range(4)]
                wba_b = wload_blk(wba, wba[:], w_ba.rearrange("(c p) n -> p c n", p=128))
                wbr_b = wload_blk(wbr, wbr[:], w_br.rearrange("(c p) n -> p c n", p=128))
                S.finalize(ch_w, [cst, pfm])
                if "B" in phases:
                    load_bw()
                PS = [Tile(es.enter_context(nc.psum_tensor("psb%d" % i, [128, 512], F32)), "psb%d" % i, excl=True) for i in range(8)]
                xb = [sb("xb0", [128, D]), sb("xb1", [128, D])]
                yT = [sb("yT%d" % i, [128, 8, 128], BF16) for i in range(3)]
                xs = sb("xs", [128, D])
                st4 = sb("st4", [128, 4])
                uTs = [sb("uT0", [128, 8, 128], BF16), sb("uT1", [128, 8, 128], BF16)]
                sgs = [sb("sg0", [128, 16, 128]), sb("sg1", [128, 16, 128])]
                hbg = sb("hbg", [128, 16])
                V(lambda e: e.tensor_scalar(out=hbg[:], in0=pfm[:, P_BG:P_BG + 16], scalar1=0.5, scalar2=None, op0=ALU.mult), R=[pfm], W=[hbg])
                t1 = sb("t1", [128, 8, 128])
                t2 = sb("t2", [128, 8, 128])
                mT = [sb("mT0", [128, 8, 128], BF16), sb("mT1", [128, 8, 128], BF16)]

                def front2(n):
                    xt = xb[n % 2]
                    yTn = yT[n % 3]
                    yield DMA("sp", ch_x[n % 2], lambda e: e.dma_start(out=xt[:], in_=xe[n * 128:(n + 1) * 128, :]), W=[xt])
                    yield DMA("sp", ch_y[n % 2], lambda e: e.dma_start(out=yTn[:].rearrange("p c t -> p (c t)"), in_=yscr[n - 1]), R=[yscr_t[n - 1]], W=[yTn])
                    yield from norm_T(xt, xs, st4, cst, CE, pfm, P_GMIX, [PS[0], PS[1]], uTs[n % 2])

                def mid2(n):
                    uT = uTs[n % 2]
                    sg = sgs[n % 2]
                    for g in range(4):
                        bank = PS[2 + (g % 2)]
                        fns = []
                        for i in range(4):
                            col = (4 * g + i) * 128
                            for kc in range(8):
                                fns.append(lambda e, i=i, col=col, kc=kc, bank=bank: e.matmul(bank[:, i * 128:(i + 1) * 128], wgate[:, kc, col:col + 128], uT[:, kc, :],
                                                                                             start=(kc == 0), stop=(kc == 7)))
                        yield PE(fns, R=[uT, wgate_b[g]], W=[bank])
                        for i in range(4):
                            yield A(lambda e, g=g, i=i, bank=bank: e.activation(out=sg[:, 4 * g + i, :], in_=bank[:, i * 128:(i + 1) * 128], func=AF.Tanh,
                                                                                bias=hbg[:, 4 * g + i:4 * g + i + 1], scale=0.5), R=[bank, hbg], W=[sg])

                def tail2(n):
                    yTn = yT[n % 3]
                    mTn = mT[n % 2]
                    sg = sgs[n % 2]
                    for br, (wb, off, wb_b) in enumerate(((wba, 0, wba_b), (wbr, 4, wbr_b))):
                        for hh in range(2):
                            bank = PS[4 + 2 * br + hh]
                            fns = []
                            for i in range(4):
                                fc = 4 * hh + i
                                for kc in range(4):
                                    fns.append(lambda e, i=i, fc=fc, kc=kc, bank=bank, wb=wb, off=off: e.matmul(bank[:, i * 128:(i + 1) * 128], wb[:, kc, fc * 128:(fc + 1) * 128],
                                                                                                                yTn[:, off + kc, :], start=(kc == 0), stop=(kc == 3)))
                            yield PE(fns, R=[yTn, wb_b], W=[bank])
                    for hh in range(2):
                        yield V(lambda e, hh=hh: e.scalar_tensor_tensor(out=t1[:, 4 * hh:4 * hh + 4, :], in0=sg[:, 4 * hh:4 * hh + 4, :], scalar=1.0,
                                                                        in1=PS[4 + hh][:].rearrange("p (c t) -> p c t", c=4), op0=ALU.add, op1=ALU.mult),
                                R=[PS[4 + hh], sg], W=[t1])
                        yield V(lambda e, hh=hh: e.scalar_tensor_tensor(out=t2[:, 4 * hh:4 * hh + 4, :], in0=sg[:, 8 + 4 * hh:8 + 4 * hh + 4, :], scalar=1.0,
                                                                        in1=PS[6 + hh][:].rearrange("p (c t) -> p c t", c=4), op0=ALU.add, op1=ALU.mult),
                                R=[PS[6 + hh], sg], W=[t2])
                    yield G(lambda e: e.tensor_tensor(out=t1[:], in0=t1[:], in1=t2[:], op=ALU.add), R=[t2], W=[t1])
                    yield A(lambda e: e.activation(out=mTn[:], in_=t1[:], func=AF.Copy, scale=0.5), R=[t1], W=[mTn])
                    yield DMA("sp", ch_st[n % 2], lambda e: e.dma_start(out=mscr[n - 1], in_=mTn[:].rearrange("p c t -> p (c t)")), R=[mTn], W=[mscr_t[n - 1]])

                if nt > 1:
                    run(front2(1))
                if nt > 2:
                    run([mid2(1), front2(2)])
                elif nt > 1:
                    run(mid2(1))
                for n in range(1, nt):
                    streams = [tail2(n)]
                    if n + 1 < nt:
                        streams.append(mid2(n + 1))
                    if n + 2 < nt:
                        streams.append(front2(n + 2))
                    run(streams)
                S.barrier()

        if "B" in phases:
            with ExitStack() as es:
                sb = mk_alloc(es, "b_")
                CE = 128
                cst, pfm = load_consts(sb, 128)
                gfin = sb("gfin", [128, D])
                S.dma("sp", ch_w, lambda e: e.dma_start(out=gfin[:], in_=gfind.broadcast_to([128, D])), W=[gfin])
                wo = sb("wo", [128, 8, D], BF16)
                wg, wu = pre_w["wg"], pre_w["wu"]
                wd = sb("wd", [128, NFC, D], BF16)
                wo_b = wload_blk(wo, wo[:], w_o.rearrange("(c p) n -> p c n", p=128))
                if "wg_b" not in pre_w:
                    load_bw()
                wg_b, wu_b = pre_w["wg_b"], pre_w["wu_b"]
                wd_b = [wload_blk(wd, wd[:, 11 * hh:11 * hh + 11, :], w_fd.rearrange("(c p) n -> p c n", p=128)[:, 11 * hh:11 * hh + 11, :]) for hh in range(2)]
                S.finalize(ch_w, [cst, pfm, gfin])
                PS = [Tile(es.enter_context(nc.psum_tensor("psc%d" % i, [128, 512], F32)), "psc%d" % i, excl=True) for i in range(8)]
                xb = [sb("xb0", [128, D]), sb("xb1", [128, D])]
                mT = [sb("mT0", [128, 8, 128], BF16), sb("mT1", [128, 8, 128], BF16)]
                h1s = [sb("h1a", [128, D]), sb("h1b", [128, D]), sb("h1c", [128, D])]
                xsF = sb("xsF", [128, D])
                xsB = [sb("xsB0", [128, D]), sb("xsB1", [128, D])]
                st4 = sb("st4", [128, 4])
                st4b = sb("st4b", [128, 4])
                fTs = [sb("fT0", [128, 8, 128], BF16), sb("fT1", [128, 8, 128], BF16)]
                sl_ = sb("silu", [128, 4, 128])
                aTs = [sb("aT0", [128, NFC, 128], BF16), sb("aT1", [128, NFC, 128], BF16)]

                def front3(n):
                    xt = xb[n % 2]
                    mTn = mT[n % 2]
                    h1 = h1s[n % 3]
                    yield DMA("sp", ch_x[n % 2], lambda e: e.dma_start(out=xt[:], in_=xe[n * 128:(n + 1) * 128, :]), W=[xt])
                    yield DMA("sp", ch_y[n % 2], lambda e: e.dma_start(out=mTn[:].rearrange("p c t -> p (c t)"), in_=mscr[n - 1]), R=[mscr_t[n - 1]], W=[mTn])
                    for hh in range(2):
                        yield PE([lambda e, kc=kc, hh=hh: e.matmul(PS[0][:], mTn[:, kc, :], wo[:, kc, hh * 512:(hh + 1) * 512], start=(kc == 0), stop=(kc == 7)) for kc in range(8)],
                                 R=[mTn, wo_b], W=[PS[0]])
                        yield V(lambda e, hh=hh: e.tensor_tensor(out=h1[:, hh * 512:(hh + 1) * 512], in0=PS[0][:], in1=xt[:, hh * 512:(hh + 1) * 512], op=ALU.add),
                                R=[PS[0], xt], W=[h1])
                    yield from norm_T(h1, xsF, st4, cst, CE, pfm, P_GFFN, [PS[1], PS[1]], fTs[n % 2])

                def mid3(n):
                    fT = fTs[n % 2]
                    aT = aTs[n % 2]
                    ngrp = (NFC + 3) // 4
                    for g in range(ngrp):
                        nchunk = min(4, NFC - 4 * g)
                        bg = PS[2 + 2 * (g % 2)]
                        bu = PS[3 + 2 * (g % 2)]
                        for bank, wt, wt_b in ((bg, wg, wg_b[g]), (bu, wu, wu_b[g])):
                            fns = []
                            for i in range(nchunk):
                                fc = 4 * g + i
                                for kc in range(8):
                                    fns.append(lambda e, i=i, fc=fc, kc=kc, bank=bank, wt=wt: e.matmul(bank[:, i * 128:(i + 1) * 128], wt[:, kc, fc * 128:(fc + 1) * 128], fT[:, kc, :],
                                                                                                       start=(kc == 0), stop=(kc == 7)))
                            yield PE(fns, R=[fT, wt_b], W=[bank])
                        yield A(lambda e: e.activation(out=sl_[:, 0:nchunk, :], in_=bg[:, 0:nchunk * 128].rearrange("p (c t) -> p c t", c=nchunk), func=AF.Tanh, scale=0.5),
                                R=[bg], W=[sl_])
                        yield V(lambda e: e.scalar_tensor_tensor(out=sl_[:, 0:nchunk, :], in0=sl_[:, 0:nchunk, :], scalar=1.0,
                                                                 in1=bg[:, 0:nchunk * 128].rearrange("p (c t) -> p c t", c=nchunk), op0=ALU.add, op1=ALU.mult), R=[bg], W=[sl_])
                        yield V(lambda e: e.scalar_tensor_tensor(out=aT[:, 4 * g:4 * g + nchunk, :], in0=sl_[:, 0:nchunk, :], scalar=0.5,
                                                                 in1=bu[:, 0:nchunk * 128].rearrange("p (c t) -> p c t", c=nchunk), op0=ALU.mult, op1=ALU.mult), R=[bu, sl_], W=[aT])

                def tail3(n):
                    h1 = h1s[n % 3]
                    aT = aTs[n % 2]
                    o = xsB[n % 2]
                    for hh in range(2):
                        yield PE([lambda e, fc=fc, hh=hh: e.matmul(PS[6 + hh][:], aT[:, fc, :], wd[:, fc, hh * 512:(hh + 1) * 512], start=(fc == 0), stop=(fc == NFC - 1)) for fc in range(NFC)],
                                 R=[aT] + wd_b, W=[PS[6 + hh]])
                        yield V(lambda e, hh=hh: e.tensor_tensor(out=h1[:, hh * 512:(hh + 1) * 512], in0=PS[6 + hh][:], in1=h1[:, hh * 512:(hh + 1) * 512], op=ALU.add),
                                R=[PS[6 + hh]], W=[h1])
                    yield A(lambda e: e.activation(out=o[:], in_=h1[:], func=AF.Square, accum_out=st4b[:, 0:1]), R=[h1], W=[o, st4b])
                    yield V(lambda e: e.tensor_scalar(out=st4b[:, 1:2], in0=st4b[:, 0:1], scalar1=1.0 / D, scalar2=RMS_EPS, op0=ALU.mult, op1=ALU.add), R=[], W=[st4b])
                    yield G(lambda e: e.tensor_tensor(out=st4b[:, 2:3], in0=st4b[:, 1:2], in1=cst[:, CE + 4:CE + 5], op=ALU.pow), R=[cst], W=[st4b])
                    yield A(lambda e: e.activation(out=o[:], in_=h1[:], func=AF.Identity, scale=st4b[:, 2:3], bias=cst[:, CE + 2:CE + 3]), R=[h1, cst], W=[o, st4b])
                    yield G(lambda e: e.tensor_tensor(out=o[:], in0=o[:], in1=gfin[:], op=ALU.mult), R=[gfin], W=[o])
                    yield DMA("sp", ch_st[n % 2], lambda e: e.dma_start(out=outd[(n - 1) * 128:n * 128, :], in_=o[:]), R=[o], W=[])

                if nt > 1:
                    run(front3(1))
                if nt > 2:
                    run([mid3(1), front3(2)])
                elif nt > 1:
                    run(mid3(1))
                for n in range(1, nt):
                    streams = [tail3(n)]
                    if n + 1 < nt:
                        streams.append(mid3(n + 1))
                    if n + 2 < nt:
                        streams.append(front3(n + 2))
                    run(streams)
                S.barrier()
        else:
            S.barrier()
        es_bw.close()
    return nc


QPERM = [0, 4, 1, 5, 2, 6, 3, 7]


def make_consts():
    c = np.zeros((128, C_END), np.float32)
    c[:, C_ID:C_ID + 128] = np.eye(128, dtype=np.float32)
    s = np.arange(64)
    for j in range(2):
        rows = slice(64 * j, 64 * j + 64)
        c[rows, C_MB:C_MB + 64] = (s[None, :] > s[:, None])
        c[rows, C_MB + 64:C_MB + 128] = (s[None, :] >= s[:, None])
        c[rows, C_ML:C_ML + 64] = (s[None, :] < s[:, None])
        c[rows, C_I64:C_I64 + 64] = np.eye(64)
        c[rows, C_OBD + 64 * j:C_OBD + 64 * j + 64] = 1.0
        c[rows, C_TRI:C_TRI + 64] = CFAC * (s[:, None] <= s[None, :])
        c[rows, C_TRI + 64:C_TRI + 128] = CFAC * (s[:, None] < s[None, :])
    c[64:128, C_TRI0:C_TRI0 + 128] = c[64:128, C_TRI:C_TRI + 128]
    c[64:64 + 48, C_TRI0:C_TRI0 + 128] = 0.0
    i = np.arange(128)
    own = np.where(i[None, :] <= i[:, None], 0.0, NEG)
    prev = np.where(i[None, :] > i[:, None], 0.0, NEG)
    full = np.full((128, 128), NEG)
    for var, (a, b) in enumerate(((own, prev), (prev, own), (full, own))):
        base = C_AM + 272 * var
        c[:, base:base + 128] = a
        c[:, base + 128:base + 256] = b
        c[:, base + 256:base + 272] = 0.0
    half = 8
    inv_freq = np.power(np.float32(500000.0), -np.arange(half, dtype=np.float32) * np.float32(2.0 / 16)).astype(np.float32)
    for n in range(NTILES):
        pos = (n * 128 + np.arange(128) - 112).astype(np.float32)
        ang = (pos[:, None] * inv_freq[None, :]).astype(np.float32)
        c[:, C_ROPE + 16 * n:C_ROPE + 16 * n + 8] = np.cos(ang)
        c[:, C_ROPE + 16 * n + 8:C_ROPE + 16 * n + 16] = np.sin(ang)
    return c


def prep_shared(inp):
    f = np.float32
    w_in = np.asarray(inp["w_in"][0], f)
    b_in = np.asarray(inp["b_in"][0], f)
    qcols = np.concatenate([np.arange(h * 64, (h + 1) * 64) for h in QPERM])
    w_qkv = np.ascontiguousarray(np.concatenate([w_in[:, qcols], w_in[:, 512:768]], axis=1))
    b_qkv = np.concatenate([b_in[qcols], b_in[512:768]])
    R0 = 768
    w_fm = np.zeros((D, 2048), f)
    b_fm = np.zeros((2048,), f)
    mix = np.asarray(inp["rwkv_mix"][0], f)
    mix_fm = np.zeros((2048,), f)

    def put(dst0, src0, n):
        w_fm[:, dst0:dst0 + n] = w_in[:, R0 + src0:R0 + src0 + n]
        b_fm[dst0:dst0 + n] = b_in[R0 + src0:R0 + src0 + n]
        mix_fm[dst0:dst0 + n] = mix[src0:src0 + n]
    put(0, 0, 1536)
    put(1536, 1536, 64)
    put(1664, 1600, 64)
    put(1792, 1664, 128)
    put(1920, 1792, 32)
    G0 = 768 + 1824
    w_gate = np.ascontiguousarray(w_in[:, G0:G0 + 2048])
    b_gate = b_in[G0:G0 + 2048]
    rows_perm = qcols
    sh = {
        "w_qkv": w_qkv, "w_fm": w_fm, "w_gate": w_gate,
        "w_ba": np.ascontiguousarray(np.asarray(inp["w_br_attn"][0], f)[rows_perm, :]),
        "w_br": np.ascontiguousarray(np.asarray(inp["w_br_rwkv"][0], f)),
        "w_o": np.ascontiguousarray(np.asarray(inp["w_o"][0], f)),
        "w_fg": np.ascontiguousarray(np.asarray(inp["w_ffn_gate"][0], f)),
        "w_fu": np.ascontiguousarray(np.asarray(inp["w_ffn_up"][0], f)),
        "w_fd": np.ascontiguousarray(np.asarray(inp["w_ffn_down"][0], f)),
        "w2": np.ascontiguousarray(np.asarray(inp["rwkv_w2"][0], f)),
        "a2": np.ascontiguousarray(np.asarray(inp["rwkv_a2"][0], f)),
    }
    g2p = np.zeros((256, 512), f)
    g2p[0:160] = np.asarray(inp["rwkv_g2"][0], f)
    sh["g2p"] = g2p
    pfm = np.zeros((128, P_END), f)

    def fm(vec, ncol):
        return np.asarray(vec, f).reshape(ncol, 128).T
    pfm[:, P_GMIX:P_GMIX + 8] = fm(inp["norm_mix_g"][0], 8)
    pfm[:, P_GFFN:P_GFFN + 8] = fm(inp["norm_ffn_g"][0], 8)
    pfm[:, P_BFM:P_BFM + 16] = fm(b_fm, 16)
    pfm[:, P_BG:P_BG + 16] = fm(b_gate, 16)
    pfm[:, P_MIX:P_MIX + 16] = fm(mix_fm, 16)
    pfm[:, P_A0:P_A0 + 4] = fm(inp["rwkv_a0"][0], 4)
    pfm[:, P_KK:P_KK + 4] = fm(inp["rwkv_k_k"][0], 4)
    pfm[:, P_KA:P_KA + 4] = fm(inp["rwkv_k_a"][0], 4)
    pfm[:, P_RK:P_RK + 4] = fm(np.asarray(inp["rwkv_r_k"][0], f).reshape(-1), 4)
    sh["pfm"] = pfm
    rowsA = np.zeros((1, RA_END), f)
    rowsA[0, RA_BQ:RA_BQ + 768] = b_qkv
    rowsA[0, RA_W0:RA_W0 + 512] = np.asarray(inp["rwkv_w0"][0], f)
    rowsA[0, RA_SK:RA_SK + 8] = np.asarray(inp["attn_sinks"][0], f)[QPERM]
    sh["rowsA"] = rowsA
    sh["gfin"] = np.asarray(inp["norm_final_g"], f).reshape(1, D).copy()
    lnst = np.zeros((128, 2, 4, 64), f)
    for a, key in enumerate(("rwkv_ln_w", "rwkv_ln_b")):
        v = np.asarray(inp[key][0], f).reshape(4, 2, 64)
        for j in range(2):
            lnst[64 * j:64 * j + 64, a, :, :] = v[None, :, j, :]
    sh["lnst"] = lnst.reshape(128, -1)
    sh["cst"] = make_consts()
    return sh


def prep_xe(inp, b):
    xe = np.zeros((NTILES * 128, D), np.float32)
    xe[112:128] = np.asarray(inp["meta_tokens"], np.float32)
    xe[128:] = np.asarray(inp["x"][b], np.float32)
    return xe


_NC_CACHE = {}


def kernel(**inputs):
    n = 8
    sh = prep_shared(inputs)
    in_maps = []
    for b in range(n):
        m = dict(sh)
        m["xe"] = prep_xe(inputs, b)
        in_maps.append(m)
    if "nc" not in _NC_CACHE:
        _NC_CACHE["nc"] = build_program()
    res = run_bass_kernel_spmd(_NC_CACHE["nc"], in_maps, core_ids=list(range(n)))
    out = np.stack([np.asarray(r["out"], np.float32).reshape(4096, D) for r in res.results], axis=0)
    return out
```

```python
import numpy as np
import ml_dtypes
from contextlib import ExitStack
import concourse.bass as bass
import concourse.mybir as mybir
from concourse.bass_utils import run_bass_kernel_spmd

F32 = mybir.dt.float32
BF16 = mybir.dt.bfloat16
AF = mybir.ActivationFunctionType
ALU = mybir.AluOpType
AX = mybir.AxisListType

NTILES = 33
D = 1024
DFF = 2816
NFC = 22
RMS_EPS = 1e-6
LN_EPS = 64e-5
CFAC = -float(np.exp(-0.5))
NEG = -1e30
MD = BF16

C_ID = 0
C_MB = 128
C_ML = 256
C_I64 = 320
C_OBD = 384
C_TRI = 512
C_TRI0 = 640
C_AM = 768
C_ROPE = 768 + 816
C_END = C_ROPE + 33 * 16
P_GMIX, P_GFFN, P_BFM, P_BG, P_MIX, P_A0, P_KK, P_KA, P_RK, P_END = 0, 8, 16, 32, 48, 64, 68, 72, 76, 80
RA_BQ, RA_W0, RA_SK, RA_END = 0, 768, 1280, 1288


class Tile:
    def __init__(self, t, name, excl=False):
        self.t = t
        self.name = name
        self.w = None
        self.r = {}
        self.excl = excl
        self.tw = 0.0
        self.tr = 0.0
        self.weng = None

    def __getitem__(self, i):
        return self.t[i]


class Chan:
    def __init__(self, sem, key):
        self.sem = sem
        self.key = key
        self.count = 0


class Sched:
    def __init__(self, nc, es):
        self.nc = nc
        self.es = es
        self.E = {}
        for name, eng in (("pe", nc.tensor), ("act", nc.scalar), ("dve", nc.vector),
                          ("pool", nc.gpsimd), ("sp", nc.sync)):
            sem = es.enter_context(nc.semaphore("sem_" + name))
            self.E[name] = dict(eng=eng, sem=sem, count=0, seen={}, name=name)
        self.chans = []

    def chan(self, name):
        c = Chan(self.es.enter_context(self.nc.semaphore("ch_" + name)), "ch_" + name)
        self.chans.append(c)
        return c

    def _waits(self, E, R, W):
        deps = {}

        def add(d):
            key, val, sem = d
            if key not in deps or deps[key][0] < val:
                deps[key] = (val, sem)
        for t in R:
            if t.w is not None:
                add(t.w)
            if t.excl:
                for key, (val, sem) in t.r.items():
                    if key != E["name"]:
                        add((key, val, sem))
        for t in W:
            if t.w is not None:
                add(t.w)
            for key, (val, sem) in t.r.items():
                add((key, val, sem))
        for key, (val, sem) in deps.items():
            if key == "pe" and E["name"] == "pe":
                continue
            if E["seen"].get(key, 0) < val:
                E["eng"].wait_ge(sem, val)
                E["seen"][key] = val

    def op(self, ename, fns, R=(), W=()):
        E = self.E[ename]
        self._waits(E, R, W)
        if not isinstance(fns, (list, tuple)):
            fns = [fns]
        inst = None
        for f in fns:
            inst = f(E["eng"])
        E["count"] += 1
        inst.then_inc(E["sem"], 1)
        for t in W:
            t.w = (ename, E["count"], E["sem"])
            t.r = {}
        for t in R:
            if t not in W:
                t.r[ename] = (E["count"], E["sem"])

    def dma(self, qname, chan, fn, R=(), W=()):
        E = self.E[qname]
        self._waits(E, R, W)
        inst = fn(E["eng"])
        chan.count += 16
        inst.then_inc(chan.sem, 16)
        for t in W:
            t.w = (chan.key, chan.count, chan.sem)
            t.r = {}
        for t in R:
            t.r[chan.key] = (chan.count, chan.sem)

    def finalize(self, chan, tiles):
        for t in tiles:
            t.w = (chan.key, chan.count, chan.sem)

    def barrier(self):
        for name, E in self.E.items():
            for oname, O in self.E.items():
                if oname == name or O["count"] == 0:
                    continue
                if E["seen"].get(oname, 0) < O["count"]:
                    E["eng"].wait_ge(O["sem"], O["count"])
                    E["seen"][oname] = O["count"]
            for c in self.chans:
                if c.count and E["seen"].get(c.key, 0) < c.count:
                    E["eng"].wait_ge(c.sem, c.count)
                    E["seen"][c.key] = c.count


def build_program(nt=NTILES, phases=("A1", "A2", "B"), dbg=None, dbg_n=-1, md=None, scr_ext=False, stop=None):
    global MD
    if md is not None:
        MD = md
    nc = bass.Bass("TRN2", target_bir_lowering=False)

    def din(name, shape, dt=F32):
        return nc.dram_tensor(name, list(shape), dt, kind="ExternalInput").ap()

    xe = din("xe", [NTILES * 128, D])
    w_qkv = din("w_qkv", [D, 768])
    w_fm = din("w_fm", [D, 2048])
    w_gate = din("w_gate", [D, 2048])
    w_ba = din("w_ba", [512, D])
    w_br = din("w_br", [512, D])
    w_o = din("w_o", [D, D])
    w_fg = din("w_fg", [D, DFF])
    w_fu = din("w_fu", [D, DFF])
    w_fd = din("w_fd", [DFF, D])
    w2d = din("w2", [64, 512])
    a2d = din("a2", [64, 512])
    g2d = din("g2p", [256, 512])
    pfmd = din("pfm", [128, P_END])
    rowsAd = din("rowsA", [1, RA_END])
    gfind = din("gfin", [1, D])
    lnstd = din("lnst", [128, 2 * 4 * 64])
    cstd = din("cst", [128, C_END])
    outd = nc.dram_tensor("out", [(NTILES - 1) * 128, D], F32, kind="ExternalOutput").ap()
    skind = "ExternalOutput" if scr_ext else "Internal"
    yscr = nc.dram_tensor("yscr", [NTILES - 1, 128, 8 * 128], BF16, kind=skind).ap()
    mscr = nc.dram_tensor("mscr", [NTILES - 1, 128, 8 * 128], BF16, kind=skind).ap()
    dbg_out = {}
    if dbg:
        for name, shape in dbg.items():
            dbg_out[name] = nc.dram_tensor("dbg_" + name, list(shape), F32, kind="ExternalOutput").ap()

    with ExitStack() as es0:
        S = Sched(nc, es0)
        ch_w = S.chan("w")
        ch_x = [S.chan("x0"), S.chan("x1")]
        ch_y = [S.chan("y0"), S.chan("y1")]
        ch_st = [S.chan("s0"), S.chan("s1")]
        ch_dbg = S.chan("dbg")
        yscr_t = [Tile(None, "yscr%d" % i) for i in range(NTILES - 1)]
        mscr_t = [Tile(None, "mscr%d" % i) for i in range(NTILES - 1)]

        ST = {"defer": False, "small": False}
        eng_free = {"pe": 0.0, "act": 0.0, "dve": 0.0, "pool": 0.0, "sp": 0.0}
        import os as _os0
        import random as _random
        _cfg = _os0.environ.get("KCFG", "0,0.3,0.1,0.03,0.5,0.0").split(",")
        _rng = _random.Random(int(_cfg[0]))
        HOP = float(_cfg[1]); KPE = float(_cfg[2]); KPS = float(_cfg[3]); KVD = float(_cfg[4]); JIT = float(_cfg[5])

        class Op:
            __slots__ = ("ename", "fns", "R", "W", "dur", "chan")

            def __init__(self, ename, fns, R, W, dur, chan=None):
                self.ename = ename; self.fns = fns; self.R = R; self.W = W; self.dur = dur; self.chan = chan

        def est_start(op):
            t = eng_free[op.ename]
            for x in op.R:
                tw = getattr(x, "tw", 0.0)
                if x.weng != op.ename:
                    tw += HOP
                t = max(t, tw)
                if x.excl:
                    t = max(t, getattr(x, "tr", 0.0) + HOP)
            for x in op.W:
                t = max(t, getattr(x, "tw", 0.0) + (HOP if x.weng != op.ename else 0.0), getattr(x, "tr", 0.0) + HOP)
            return t

        def emit(op):
            t0 = est_start(op)
            t1 = t0 + op.dur
            if op.chan is None:
                S.op(op.ename, op.fns, op.R, op.W)
                eng_free[op.ename] = t1
            else:
                S.dma(op.ename, op.chan, op.fns, op.R, op.W)
                eng_free[op.ename] = t0 + 0.1
                t1 = t0 + 2.5
            for x in op.W:
                x.tw = t1; x.weng = op.ename; x.tr = 0.0
            for x in op.R:
                x.tr = max(getattr(x, "tr", 0.0), t1)

        def mkop(ename, fns, R, W, dur, chan=None):
            if JIT > 0:
                dur = dur * (1.0 + JIT * (_rng.random() - 0.5))
            op = Op(ename, fns, list(R), list(W), dur, chan)
            if ST["defer"]:
                return op
            emit(op)
            return None

        def V(fn, R=(), W=(), d=None):
            d = KVD if d is None else d
            return mkop("dve", fn, R, W, d)

        def A(fn, R=(), W=(), d=None):
            d = KVD if d is None else d
            return mkop("act", fn, R, W, d)

        def G(fn, R=(), W=(), d=1.2):
            return mkop("pool", fn, R, W, d)

        def PE(fns, R=(), W=(), d=None):
            n_ = len(fns) if isinstance(fns, (list, tuple)) else 1
            if d is None:
                d = n_ * (KPS if ST["small"] else KPE) + 0.1
            ST["small"] = False
            return mkop("pe", fns, R, W, d)

        def DMA(qname, chan, fn, R=(), W=()):
            return mkop(qname, fn, R, W, 2.5, chan)

        def dump(name, tile_ap, tiles):
            if name in dbg_out:
                S.dma("sp", ch_dbg, lambda e: e.dma_start(out=dbg_out[name], in_=tile_ap), R=tiles, W=[])

        def run(gens, bonus=None):
            if not isinstance(gens, (list, tuple)):
                gens = [gens]
            if bonus is None:
                bonus = [0.0] * len(gens)
            ST["defer"] = True
            heads = []
            for g in gens:
                heads.append(next(g, None))
            try:
                while True:
                    best = None
                    bt = None
                    for i, h in enumerate(heads):
                        if h is None:
                            continue
                        t = est_start(h) - bonus[i]
                        if bt is None or t < bt:
                            bt = t; best = i
                    if best is None:
                        break
                    ST["defer"] = False
                    emit(heads[best])
                    ST["defer"] = True
                    h = next(gens[best], None)
                    while h is None:
                        try:
                            h = next(gens[best])
                        except StopIteration:
                            h = None
                            break
                    heads[best] = h
            finally:
                ST["defer"] = False

        def par(gens, weights=None):
            return list(gens)

        def mk_alloc(es, pfx):
            def sb(name, shape, dt=F32):
                return Tile(es.enter_context(nc.sbuf_tensor(pfx + name, list(shape), dt)), pfx + name)
            return sb

        def norm_T(x, xs, st4, cst, ce, pfm, gcol, TR2, uT):
            yield A(lambda e: e.activation(out=xs[:], in_=x[:], func=AF.Square, accum_out=st4[:, 0:1]), R=[x], W=[xs, st4])
            yield V(lambda e: e.tensor_scalar(out=st4[:, 1:2], in0=st4[:, 0:1], scalar1=1.0 / D, scalar2=RMS_EPS, op0=ALU.mult, op1=ALU.add), R=[st4], W=[st4])
            yield G(lambda e: e.tensor_tensor(out=st4[:, 2:3], in0=st4[:, 1:2], in1=cst[:, ce + 4:ce + 5], op=ALU.pow), R=[st4, cst], W=[st4])
            yield A(lambda e: e.activation(out=xs[:], in_=x[:], func=AF.Identity, scale=st4[:, 2:3], bias=cst[:, ce + 2:ce + 3]),
                    R=[x, st4, cst], W=[xs])
            for h in range(2):
                yield PE([lambda e, c=c: e.transpose(TR2[h][:, (c % 4) * 128:(c % 4 + 1) * 128], xs[:, c * 128:(c + 1) * 128], cst[:, C_ID:C_ID + 128])
                          for c in range(4 * h, 4 * h + 4)], R=[xs, cst], W=[TR2[h]])
                yield V(lambda e, h=h: e.tensor_tensor(out=uT[:, 4 * h:4 * h + 4, :], in0=TR2[h][:].rearrange("p (c k) -> p c k", k=128),
                                                       in1=pfm[:, gcol + 4 * h:gcol + 4 * h + 4].unsqueeze(2).broadcast_to([128, 4, 128]), op=ALU.mult),
                        R=[TR2[h], pfm], W=[uT])

        def load_consts(sb, ncols):
            cst = sb("cst", [128, ncols + 12])
            pfm = sb("pfm", [128, P_END])
            G(lambda e: e.memset(cst[:, ncols:ncols + 1], RMS_EPS), W=[cst])
            G(lambda e: e.memset(cst[:, ncols + 1:ncols + 2], LN_EPS), W=[cst])
            G(lambda e: e.memset(cst[:, ncols + 2:ncols + 4], 0.0), W=[cst])
            G(lambda e: e.memset(cst[:, ncols + 4:ncols + 12], -0.5), W=[cst])
            S.dma("sp", ch_w, lambda e: e.dma_start(out=cst[:, 0:ncols], in_=cstd[:, 0:ncols]), W=[cst])
            S.dma("sp", ch_w, lambda e: e.dma_start(out=pfm[:], in_=pfmd), W=[pfm])
            return cst, pfm

        wl_n = [0]

        def wload_blk(tile_, out_ap, in_ap):
            wl_n[0] += 1
            ch = S.chan("wb%d" % wl_n[0])
            t = Tile(tile_.t, "%s_blk%d" % (tile_.name, wl_n[0]))
            S.dma("pool", ch, lambda e: e.dma_start(out=out_ap, in_=in_ap), W=[t])
            return t

        def wload(tile_, out_ap, in_ap):
            S.dma("pool", ch_w, lambda e: e.dma_start(out=out_ap, in_=in_ap), W=[tile_])

        if "A1" in phases:
            with ExitStack() as es:
                sb = mk_alloc(es, "a1_")
                CE = C_END
                cst, pfm = load_consts(sb, C_END)
                rowsA = sb("rowsA", [128, RA_END])
                S.dma("sp", ch_w, lambda e: e.dma_start(out=rowsA[:], in_=rowsAd.broadcast_to([128, RA_END])), W=[rowsA])
                lnst = sb("lnst", [128, 2, 4, 64])
                S.dma("sp", ch_w, lambda e: e.dma_start(out=lnst[:].rearrange("p a c v -> p (a c v)"), in_=lnstd), W=[lnst])
                wqkv = sb("wqkv", [128, 8, 768], BF16)
                wfm = sb("wfm", [128, 8, 2048], BF16)
                w2 = sb("w2", [64, 512])
                a2 = sb("a2", [64, 512])
                g2 = sb("g2", [128, 2, 512])
                S.dma("sp", ch_w, lambda e: e.dma_start(out=w2[:], in_=w2d), W=[w2])
                S.dma("sp", ch_w, lambda e: e.dma_start(out=a2[:], in_=a2d), W=[a2])
                S.dma("sp", ch_w, lambda e: e.dma_start(out=g2[:], in_=g2d.rearrange("(c p) n -> p c n", p=128)), W=[g2])
                wqkv_b = wload_blk(wqkv, wqkv[:], w_qkv.rearrange("(c p) n -> p c n", p=128))
                wfm_b = [wload_blk(wfm, wfm[:, :, 512 * g:512 * (g + 1)], w_fm.rearrange("(c p) n -> p c n", p=128)[:, :, 512 * g:512 * (g + 1)]) for g in range(4)]
                identb = sb("identb", [128, 128], BF16)
                ones2 = sb("ones2", [128, 2])
                G(lambda e: e.memset(ones2[:], 1.0), W=[ones2])
                S.finalize(ch_w, [cst, pfm, rowsA, lnst, w2, a2, g2])
                V(lambda e: e.tensor_copy(identb[:], cst[:, C_ID:C_ID + 128]), R=[cst], W=[identb])

                PS = [Tile(es.enter_context(nc.psum_tensor("ps%d" % i, [128, 512], F32)), "ps%d" % i, excl=True) for i in range(8)]
                H0, Q0, A0, A1_, A2_, R0, R1_, R2 = PS
                H1 = H0

                xb = [sb("xb0", [128, D]), sb("xb1", [128, D])]
                xs = sb("xs", [128, D])
                st4 = sb("st4", [128, 4])
                uT = sb("uT", [128, 8, 128], BF16)
                stg = sb("stg", [128, 16, 129])
                pfs = [sb("pf0", [128, 16, 128]), sb("pf1", [128, 16, 128])]
                qkvs = [sb("qkv0", [128, 768]), sb("qkv1", [128, 768])]
                rtmp = sb("rtmp", [128, 4, 10, 8])
                qT = sb("qT", [128, 4, 128], BF16)
                Kbuf = sb("Kbuf", [128, 272], BF16)
                Vbuf = sb("Vbuf", [128, 3, 128], BF16)
                Pb = [sb("Pb0", [128, 272], BF16), sb("Pb1", [128, 272], BF16)]
                PT = [sb("PT0", [128, 3, 128], BF16), sb("PT1", [128, 3, 128], BF16)]
                sm = sb("sm", [128, 5, 8])
                yat = sb("yat", [128, 8, 64])
                yTa = [sb("yTa0", [128, 4, 128], BF16), sb("yTa1", [128, 4, 128], BF16)]
                yTr = [sb("yTr0", [128, 4, 128], BF16), sb("yTr1", [128, 4, 128], BF16)]
                th = sb("th", [64, 128])
                sgd = sb("sgd", [128, 2, 128])
                B1 = sb("B1", [128, 4, 128]); B2 = sb("B2", [128, 4, 128]); B3 = sb("B3", [128, 4, 128])
                B4 = sb("B4", [128, 4, 128])
                arTs = [sb("arT%d" % i, [128, 4, 2, 2, 64], MD) for i in range(2)]
                BT = sb("BT", [128, 4, 128], MD); KT = sb("KT", [128, 4, 128], MD)
                BH = sb("BH", [128, 4, 128], MD); KH = sb("KH", [128, 4, 128], MD)
                cumC = sb("cumC", [128, 4, 2])
                WCs = [sb("WC%d" % i, [128, 4, 2]) for i in range(2)]
                rks = [sb("rk%d" % i, [128, 2, 4]) for i in range(2)]
                B5s = [sb("B5_%d" % i, [128, 4, 128]) for i in range(2)]
                TTs = [sb("TT%d" % i, [128, 2, 4, 64], MD) for i in range(2)]
                Ast = [sb("Ast0", [128, 2, 4, 64], MD), sb("Ast1", [128, 2, 4, 64], MD)]
                Nst = [sb("Nst0", [128, 2, 4, 64], MD), sb("Nst1", [128, 2, 4, 64], MD)]
                NBs = [sb("NB%d" % i, [128, 2, 4, 2, 64], MD) for i in range(2)]
                NKs = [sb("NK%d" % i, [128, 2, 4, 2, 64], MD) for i in range(2)]
                Pc = [sb("Pc0", [128, 2, 4, 64], MD), sb("Pc1", [128, 2, 4, 64], MD)]
                Vst32s = [sb("Vst32_%d" % i, [128, 2, 4, 64]) for i in range(2)]
                Vsts = [sb("Vst_%d" % i, [128, 2, 4, 64], MD) for i in range(2)] if MD != F32 else Vst32s
                BKsts = [sb("BKst%d" % i, [128, 2, 2, 4, 64], MD) for i in range(2)]
                R1 = sb("R1", [128, 4, 64], MD); Ust = sb("Ust", [128, 4, 64], MD)
                Yst = sb("Yst", [128, 2, 4, 64]); yc = sb("yc", [128, 2, 4, 64]); ysq = sb("ysq", [128, 2, 4, 64])
                ST32 = sb("ST32", [128, 4, 64])
                STt = sb("STt", [128, 4, 64])
                STm = sb("STm", [128, 4, 64], MD) if MD != F32 else ST32
                gst = sb("gst", [128, 6, 8])
                hb = sb("hb", [128, 4])
                V(lambda e: e.tensor_scalar(out=hb[:], in0=pfm[:, P_A0:P_A0 + 4], scalar1=0.5, scalar2=None, op0=ALU.mult), R=[pfm], W=[hb])
                G(lambda e: e.memset(ST32[:], 0.0), W=[ST32])
                if MD != F32:
                    G(lambda e: e.memset(STm[:], 0.0), W=[STm])
                G(lambda e: e.memset(stg[:], 0.0), W=[stg])
                G(lambda e: e.memset(Vbuf[:], 0.0), W=[Vbuf])
                G(lambda e: e.memset(Kbuf[:], 0.0), W=[Kbuf])

                def v3(ap2d, k=64):
                    return ap2d.rearrange("p (c k) -> p c k", k=k)

                def v4(ap2d):
                    return ap2d.rearrange("p (q c k) -> p q c k", q=2, c=4)

                def cq(t):
                    return t.rearrange("p c (q t) -> p c q t", q=2)

                def qc(t):
                    return t.rearrange("p c (q t) -> p q c t", q=2)

                ID0 = C_ID if MD == F32 else 0
                idm = cst if MD == F32 else identb

                def idsl(sl, j):
                    return cst[sl, C_ID + 64 * j:C_ID + 64 * j + 64]

                def idm_sl(sl, j):
                    return idm[sl, ID0 + 64 * j:ID0 + 64 * j + 64]

                def hl(fn, qs=(0,)):
                    ST["small"] = True
                    out = []
                    for q in qs:
                        for c in range(4):
                            for j in range(2):
                                out += fn(q, c, slice(64 * j, 64 * j + 64), j)
                    return out

                def head(n):
                    xt = xb[n % 2]
                    pf = pfs[n % 2]
                    qkv = qkvs[n % 2]
                    yield DMA("sp", ch_x[n % 2], lambda e: e.dma_start(out=xt[:], in_=xe[n * 128:(n + 1) * 128, :]), W=[xt])
                    yield from norm_T(xt, xs, st4, cst, CE, pfm, P_GMIX, [H0, H1], uT)
                    yield PE([lambda e, kc=kc: e.matmul(H0[:, 0:512], uT[:, kc, :], wqkv[:, kc, 0:512], start=(kc == 0), stop=(kc == 7)) for kc in range(8)],
                             R=[uT, wqkv_b], W=[H0])
                    yield V(lambda e: e.tensor_tensor(out=qkv[:, 0:512], in0=H0[:, 0:512], in1=rowsA[:, RA_BQ:RA_BQ + 512], op=ALU.add), R=[H0, rowsA], W=[qkv])
                    yield PE([lambda e, kc=kc: e.matmul(H1[:, 0:256], uT[:, kc, :], wqkv[:, kc, 512:768], start=(kc == 0), stop=(kc == 7)) for kc in range(8)],
                             R=[uT, wqkv_b], W=[H1])
                    yield V(lambda e: e.tensor_tensor(out=qkv[:, 512:768], in0=H1[:, 0:256], in1=rowsA[:, RA_BQ + 512:RA_BQ + 768], op=ALU.add), R=[H1, rowsA], W=[qkv])
                    for g in range(4):
                        bank = (H0, H1)[g % 2]
                        fns = []
                        for i in range(4):
                            col = (4 * g + i) * 128
                            for kc in range(8):
                                fns.append(lambda e, i=i, col=col, kc=kc, bank=bank: e.matmul(bank[:, i * 128:(i + 1) * 128], wfm[:, kc, col:col + 128], uT[:, kc, :],
                                                                                             start=(kc == 0), stop=(kc == 7)))
                        yield PE(fns, R=[uT, wfm_b[g]], W=[bank])
                        yield V(lambda e, g=g, bank=bank: e.tensor_tensor(out=stg[:, 4 * g:4 * g + 4, 1:129], in0=v3(bank[:], 128),
                                                                          in1=pfm[:, P_BFM + 4 * g:P_BFM + 4 * g + 4].unsqueeze(2).broadcast_to([128, 4, 128]), op=ALU.add),
                                R=[bank, pfm], W=[stg])
                    if n == 0:
                        yield G(lambda e: e.memset(stg[:, :, 1:113], 0.0), W=[stg])
                    yield G(lambda e: e.tensor_tensor(out=pf[:], in0=stg[:, :, 0:128], in1=stg[:, :, 1:129], op=ALU.subtract), R=[stg], W=[pf], d=3.6)
                    yield G(lambda e: e.tensor_tensor(out=pf[:], in0=pf[:], in1=pfm[:, P_MIX:P_MIX + 16].unsqueeze(2).broadcast_to([128, 16, 128]), op=ALU.mult),
                            R=[pfm], W=[pf], d=3.6)
                    yield G(lambda e: e.tensor_tensor(out=pf[:], in0=pf[:], in1=stg[:, :, 1:129], op=ALU.add), R=[stg], W=[pf], d=3.6)
                    yield G(lambda e: e.tensor_copy(stg[:, :, 0:1], stg[:, :, 128:129]), R=[], W=[stg])

                def attention(n):
                    qkv = qkvs[n % 2]
                    slot = n % 2
                    q10 = qkv[:, 0:640].rearrange("p (h d) -> p h d", d=64)
                    cosb = cst[:, C_ROPE + 16 * n:C_ROPE + 16 * n + 8].unsqueeze(1).broadcast_to([128, 10, 8])
                    sinb = cst[:, C_ROPE + 16 * n + 8:C_ROPE + 16 * n + 16].unsqueeze(1).broadcast_to([128, 10, 8])
                    yield G(lambda e: e.tensor_tensor(out=rtmp[:, 0], in0=q10[:, :, 0:8], in1=cosb, op=ALU.mult), R=[qkv, cst], W=[rtmp])
                    yield G(lambda e: e.tensor_tensor(out=rtmp[:, 1], in0=q10[:, :, 8:16], in1=sinb, op=ALU.mult), R=[qkv, cst], W=[rtmp])
                    yield G(lambda e: e.tensor_tensor(out=rtmp[:, 2], in0=q10[:, :, 8:16], in1=cosb, op=ALU.mult), R=[qkv, cst], W=[rtmp])
                    yield G(lambda e: e.tensor_tensor(out=rtmp[:, 3], in0=q10[:, :, 0:8], in1=sinb, op=ALU.mult), R=[qkv, cst], W=[rtmp])
                    yield G(lambda e: e.tensor_tensor(out=q10[:, :, 0:8], in0=rtmp[:, 0], in1=rtmp[:, 1], op=ALU.subtract), R=[rtmp], W=[qkv])
                    yield G(lambda e: e.tensor_tensor(out=q10[:, :, 8:16], in0=rtmp[:, 2], in1=rtmp[:, 3], op=ALU.add), R=[rtmp], W=[qkv])
                    yield PE([lambda e, c=c: e.transpose(A0[:, c * 128:(c + 1) * 128], qkv[:, c * 128:(c + 1) * 128], cst[:, C_ID:C_ID + 128]) for c in range(4)],
                             R=[qkv, cst], W=[A0])
                    yield PE(lambda e: e.transpose(A1_[:, 0:128], qkv[:, 512:640], cst[:, C_ID:C_ID + 128]), R=[qkv, cst], W=[A1_])
                    yield A(lambda e: e.activation(out=qT[:], in_=v3(A0[:], 128), func=AF.Copy, scale=0.125), R=[A0], W=[qT])
                    yield A(lambda e: e.activation(out=Kbuf[:, slot * 128:(slot + 1) * 128], in_=A1_[:, 0:128], func=AF.Copy), R=[A1_], W=[Kbuf])
                    yield V(lambda e: e.tensor_copy(Vbuf[:, slot, :], qkv[:, 640:768]), R=[qkv], W=[Vbuf])
                    if n == 0:
                        yield A(lambda e: e.activation(out=Kbuf[:, 256:272], in_=A1_[:, 112:128], func=AF.Copy), R=[A1_], W=[Kbuf])
                        yield PE(lambda e: e.matmul(A1_[0:16, 128:256], cst[:, C_ID + 112:C_ID + 128], qkv[:, 640:768], start=True, stop=True), R=[qkv, cst], W=[A1_])
                        yield V(lambda e: e.tensor_copy(Vbuf[0:16, 2, :], A1_[0:16, 128:256]), R=[A1_], W=[Vbuf])
                        return
                    yTn = yTa[n % 2]
                    mvar = 2 if n == 1 else (0 if n % 2 == 0 else 1)
                    mask = cst[:, C_AM + 272 * mvar:C_AM + 272 * (mvar + 1)]
                    for s in range(8):
                        c, j = s // 2, s % 2
                        sl = slice(64 * j, 64 * j + 64)
                        SC = A0
                        Pk = Pb[s % 2]
                        PTk = PT[s % 2]
                        yield PE(lambda e: e.matmul(SC[:, 0:272], qT[sl, c, :], Kbuf[sl, 0:272], start=True, stop=True), R=[qT, Kbuf], W=[SC])
                        yield V(lambda e: e.tensor_tensor(out=SC[:, 0:272], in0=SC[:, 0:272], in1=mask, op=ALU.add), R=[cst], W=[SC])
                        yield V(lambda e: e.tensor_reduce(out=sm[:, 0, s:s + 1], in_=SC[:, 0:272], axis=AX.X, op=ALU.max), R=[SC], W=[sm])
                        yield V(lambda e: e.tensor_scalar(out=sm[:, 1, s:s + 1], in0=sm[:, 0, s:s + 1], scalar1=rowsA[:, RA_SK + s:RA_SK + s + 1], scalar2=-1.0,
                                                          op0=ALU.max, op1=ALU.mult), R=[rowsA], W=[sm])
                        yield A(lambda e: e.activation(out=Pk[:], in_=SC[:, 0:272], func=AF.Exp, bias=sm[:, 1, s:s + 1], scale=1.0,
                                                       accum_out=sm[:, 2, s:s + 1]), R=[SC], W=[Pk, sm])
                        yield PE([lambda e, b=b, nk=nk: e.matmul(A1_[0:nk, b * 128:(b + 1) * 128], Pk[:, b * 128:b * 128 + nk], identb[:], start=True, stop=True)
                                  for b, nk in ((0, 128), (1, 128), (2, 16))], R=[Pk, identb], W=[A1_])
                        yield A(lambda e: e.activation(out=PTk[:], in_=v3(A1_[:, 0:384], 128), func=AF.Copy), R=[A1_], W=[PTk])
                        yield PE([lambda e: e.matmul(A2_[:, s * 64:(s + 1) * 64], PTk[:, 0, :], Vbuf[:, 0, sl], start=True, stop=False),
                                  lambda e: e.matmul(A2_[:, s * 64:(s + 1) * 64], PTk[:, 1, :], Vbuf[:, 1, sl], start=False, stop=False),
                                  lambda e: e.matmul(A2_[:, s * 64:(s + 1) * 64], PTk[0:16, 2, :], Vbuf[0:16, 2, sl], start=False, stop=True)],
                                 R=[PTk, Vbuf], W=[A2_])
                    yield V(lambda e: e.tensor_tensor(out=sm[:, 3, :], in0=rowsA[:, RA_SK:RA_SK + 8], in1=sm[:, 1, :], op=ALU.add), R=[rowsA], W=[sm])
                    yield A(lambda e: e.activation(out=sm[:, 3, :], in_=sm[:, 3, :], func=AF.Exp), R=[], W=[sm])
                    yield V(lambda e: e.tensor_tensor(out=sm[:, 2, :], in0=sm[:, 2, :], in1=sm[:, 3, :], op=ALU.add), R=[], W=[sm])
                    yield V(lambda e: e.reciprocal(out=sm[:, 4, :], in_=sm[:, 2, :]), R=[], W=[sm])
                    yield V(lambda e: e.tensor_tensor(out=yat[:], in0=v3(A2_[:], 64), in1=sm[:, 4, :].unsqueeze(2).broadcast_to([128, 8, 64]), op=ALU.mult),
                            R=[A2_], W=[yat, sm])
                    yield PE([lambda e, c=c: e.transpose(A1_[:, c * 128:(c + 1) * 128], yat[:, 2 * c:2 * c + 2, :].rearrange("p a d -> p (a d)"), cst[:, C_ID:C_ID + 128])
                              for c in range(4)], R=[yat, cst], W=[A1_])
                    yield A(lambda e: e.activation(out=yTn[:], in_=v3(A1_[:], 128), func=AF.Copy), R=[A1_], W=[yTn])
                    if n == dbg_n:
                        dump("yat", yat[:].rearrange("p s d -> p (s d)"), [yat])
                    yield DMA("sp", ch_y[n % 2], lambda e: e.dma_start(out=yscr[n - 1][:, 0:512], in_=yTn[:].rearrange("p c t -> p (c t)")), R=[yTn], W=[yscr_t[n - 1]])

                def pre(n):
                    pf = pfs[n % 2]
                    pp_ = n % 2
                    arT, NB, NK, Vst, Vst32, BKst, WC, rk, B5 = arTs[pp_], NBs[pp_], NKs[pp_], Vsts[pp_], Vst32s[pp_], BKsts[pp_], WCs[pp_], rks[pp_], B5s[pp_]
                    tri0 = C_TRI0 if n == 0 else C_TRI
                    tri = cst[:, tri0:tri0 + 128]
                    yield A(lambda e: e.activation(out=th[:], in_=pf[0:64, 12, :], func=AF.Tanh), R=[pf], W=[th])
                    yield PE(lambda e: e.matmul(R0[:, 0:512], th[:], w2[:], start=True, stop=True), R=[th, w2], W=[R0])
                    B4f = B4[:].rearrange("p c t -> p (c t)")
                    yield V(lambda e: e.tensor_tensor(out=B4f, in0=R0[:, 0:512], in1=rowsA[:, RA_W0:RA_W0 + 512], op=ALU.add), R=[R0, rowsA], W=[B4])
                    yield A(lambda e: e.activation(out=B4f, in_=B4f, func=AF.Tanh, scale=0.5), R=[], W=[B4])
                    yield V(lambda e: e.tensor_scalar(out=B4f, in0=B4f, scalar1=0.5, scalar2=0.5, op0=ALU.mult, op1=ALU.add), R=[], W=[B4])
                    CB = (R1_, R2)
                    for q in range(2):
                        yield PE([lambda e, c=c, q=q: e.matmul(CB[q][:, c * 128:(c + 1) * 128], B4f[64 * q:64 * q + 64, c * 128:(c + 1) * 128],
                                                               tri[64 * q:64 * q + 64, :], start=True, stop=True) for c in range(4)], R=[B4, cst], W=[CB[q]])

                    def cums(a):
                        return [CB[q][:].rearrange("p (c a t) -> p c a t", c=4, a=2)[:, :, a, :] for q in range(2)]
                    yield PE([lambda e, c=c: e.matmul(R0[:, c * 128:(c + 1) * 128], a2[:, c * 128:(c + 1) * 128], pf[0:64, 13, :], start=True, stop=True) for c in range(4)],
                             R=[pf, a2], W=[R0])
                    for c in range(4):
                        yield A(lambda e, c=c: e.activation(out=B3[:, c, :], in_=R0[:, c * 128:(c + 1) * 128], func=AF.Tanh, bias=hb[:, c:c + 1], scale=0.5),
                                R=[R0, hb], W=[B3])
                    yield V(lambda e: e.tensor_scalar(out=B3[:], in0=B3[:], scalar1=0.5, scalar2=0.5, op0=ALU.mult, op1=ALU.add), R=[], W=[B3])
                    yield A(lambda e: e.activation(out=sgd[:], in_=pf[:, 14:16, :], func=AF.Tanh, scale=0.5), R=[pf], W=[sgd])
                    yield V(lambda e: e.tensor_scalar(out=sgd[:], in0=sgd[:], scalar1=0.5, scalar2=0.5, op0=ALU.mult, op1=ALU.add), R=[], W=[sgd])
                    fns = []
                    for c in range(4):
                        fns.append(lambda e, c=c: e.matmul(R0[:, c * 128:(c + 1) * 128], g2[:, 0, c * 128:(c + 1) * 128], sgd[:, 0, :], start=True, stop=False))
                        fns.append(lambda e, c=c: e.matmul(R0[:, c * 128:(c + 1) * 128], g2[0:32, 1, c * 128:(c + 1) * 128], sgd[0:32, 1, :], start=False, stop=True))
                    yield PE(fns, R=[sgd, g2], W=[R0])
                    yield A(lambda e: e.activation(out=B5[:].rearrange("p c t -> p (c t)"), in_=R0[:], func=AF.Copy), R=[R0], W=[B5])
                    kview = pf[:, 4:8, :]
                    rview = pf[:, 0:4, :]

                    def bc(col):
                        return pfm[:, col:col + 4].unsqueeze(2).broadcast_to([128, 4, 128])
                    yield V(lambda e: e.tensor_tensor(out=B1[:], in0=kview, in1=bc(P_KK), op=ALU.mult), R=[pf, pfm], W=[B1])
                    yield V(lambda e: e.tensor_tensor(out=B2[:], in0=B1[:], in1=B1[:], op=ALU.mult), R=[B1], W=[B2])
                    yield PE([lambda e, c=c: e.matmul(R0[:, c * 128:(c + 1) * 128], cst[:, C_OBD:C_OBD + 128], B2[:, c, :], start=True, stop=True) for c in range(4)],
                             R=[B2, cst], W=[R0])
                    yield A(lambda e: e.activation(out=B2[:].rearrange("p c t -> p (c t)"), in_=R0[:], func=AF.Sqrt), R=[R0], W=[B2])
                    yield V(lambda e: e.tensor_scalar(out=B2[:], in0=B2[:], scalar1=1e-12, scalar2=None, op0=ALU.max), R=[], W=[B2])
                    yield V(lambda e: e.reciprocal(out=B2[:], in_=B2[:]), R=[], W=[B2])
                    yield V(lambda e: e.tensor_tensor(out=B1[:], in0=B1[:], in1=B2[:], op=ALU.mult), R=[B2], W=[B1])
                    yield V(lambda e: e.scalar_tensor_tensor(out=B2[:], in0=B3[:], scalar=-1.0, in1=bc(P_KA), op0=ALU.add, op1=ALU.mult), R=[B3, pfm], W=[B2])
                    yield V(lambda e: e.scalar_tensor_tensor(out=kview, in0=B2[:], scalar=1.0, in1=kview, op0=ALU.add, op1=ALU.mult), R=[B2], W=[pf])
                    yield V(lambda e: e.tensor_tensor(out=B3[:], in0=B1[:], in1=B3[:], op=ALU.mult), R=[B1], W=[B3])
                    cex = cums(1)
                    cin = cums(0)
                    B4q = cq(B4[:])
                    for hh in range(2):
                        yield A(lambda e, hh=hh: e.activation(out=B4[:, :, 64 * hh:64 * hh + 64], in_=cex[hh], func=AF.Exp), R=[CB[hh]], W=[B4])
                    yield V(lambda e: e.scalar_tensor_tensor(out=arT[:, :, :, 0, :], in0=cq(B1[:]), scalar=-1.0, in1=B4q, op0=ALU.mult, op1=ALU.mult), R=[B1, B4], W=[arT])
                    for hh in range(2):
                        yield A(lambda e, hh=hh: e.activation(out=B4[:, :, 64 * hh:64 * hh + 64], in_=cin[hh], func=AF.Exp), R=[CB[hh]], W=[B4])
                    yield V(lambda e: e.tensor_tensor(out=arT[:, :, :, 1, :], in0=cq(rview), in1=B4q, op=ALU.mult), R=[pf, B4], W=[arT])
                    for hh in range(2):
                        yield A(lambda e, hh=hh: e.activation(out=B4[:, :, 64 * hh:64 * hh + 64], in_=cin[hh], func=AF.Exp, scale=-1.0), R=[CB[hh]], W=[B4])
                    yield V(lambda e: e.tensor_tensor(out=BT[:], in0=B3[:], in1=B4[:], op=ALU.mult), R=[B3, B4], W=[BT])
                    yield V(lambda e: e.tensor_tensor(out=KT[:], in0=kview, in1=B4[:], op=ALU.mult), R=[pf, B4], W=[KT])
                    for hh in range(2):
                        yield V(lambda e, hh=hh: e.tensor_copy(cumC[:, :, hh], cin[hh][:, :, 63]), R=[CB[hh]], W=[cumC])
                    for c in range(4):
                        for q in range(2):
                            yield A(lambda e, c=c, q=q: e.activation(out=B4[:, c, 64 * q:64 * q + 64], in_=cin[q][:, c, :], func=AF.Exp, scale=-1.0,
                                                                     bias=cumC[:, c, q:q + 1]), R=[CB[q], cumC], W=[B4])
                    yield V(lambda e: e.tensor_tensor(out=BH[:], in0=B3[:], in1=B4[:], op=ALU.mult), R=[B3, B4], W=[BH])
                    yield V(lambda e: e.tensor_tensor(out=KH[:], in0=kview, in1=B4[:], op=ALU.mult), R=[pf, B4], W=[KH])
                    yield A(lambda e: e.activation(out=WC[:], in_=cumC[:], func=AF.Exp), R=[cumC], W=[WC])

                    mb = cst[:, C_MB:C_MB + 128].rearrange("p (a t) -> p a t", a=2).unsqueeze(1).broadcast_to([128, 4, 2, 64])
                    ml8 = cst[:, C_ML:C_ML + 64].unsqueeze(1).broadcast_to([128, 8, 64])
                    i8 = cst[:, C_I64:C_I64 + 64].unsqueeze(1).broadcast_to([128, 8, 64])
                    Q2 = (0, 1)

                    def tq(q):
                        return slice(64 * q, 64 * q + 64)
                    yield PE(hl(lambda q, c, sl, j: [lambda e: e.matmul(R0[sl, q * 256 + c * 64:q * 256 + (c + 1) * 64], arT[sl, c, q, 0, :], BT[sl, c, tq(q)], start=True, stop=True)], Q2),
                             R=[arT, BT], W=[R0])
                    for q in Q2:
                        yield PE(hl(lambda q, c, sl, j: [lambda e: e.matmul(CB[q][sl, c * 128:(c + 1) * 128], BT[sl, c, tq(q)], arT[sl, c, q, :, :].rearrange("p a t -> p (a t)"),
                                                                            start=True, stop=True)], (q,)), R=[arT, BT], W=[CB[q]])
                    yield V(lambda e: e.tensor_tensor(out=Ast[0][:].rearrange("p q c t -> p (q c) t"), in0=v3(R0[:]), in1=ml8, op=ALU.mult), R=[R0, cst], W=[Ast[0]])
                    for q in Q2:
                        yield V(lambda e, q=q: e.tensor_tensor(out=NB[:, q], in0=CB[q][:].rearrange("p (c a t) -> p c a t", c=4, a=2), in1=mb, op=ALU.mult), R=[CB[q], cst], W=[NB])
                    for q in Q2:
                        yield PE(hl(lambda q, c, sl, j: [lambda e: e.matmul(CB[q][sl, c * 128:(c + 1) * 128], KT[sl, c, tq(q)], arT[sl, c, q, :, :].rearrange("p a t -> p (a t)"),
                                                                            start=True, stop=True)], (q,)), R=[arT, KT], W=[CB[q]])
                    for q in Q2:
                        yield V(lambda e, q=q: e.tensor_tensor(out=NK[:, q], in0=CB[q][:].rearrange("p (c a t) -> p c a t", c=4, a=2), in1=mb, op=ALU.mult), R=[CB[q], cst], W=[NK])
                    yield G(lambda e: e.tensor_copy(Nst[0][:], NB[:, :, :, 0, :]), R=[NB], W=[Nst[0]])
                    yield G(lambda e: e.tensor_tensor(out=Pc[0][:].rearrange("p q c t -> p (q c) t"), in0=Nst[0][:].rearrange("p q c t -> p (q c) t"), in1=i8, op=ALU.add),
                            R=[Nst[0], cst], W=[Pc[0]])
                    yield PE(hl(lambda q, c, sl, j: [lambda e: e.matmul(R0[sl, q * 256 + c * 64:q * 256 + (c + 1) * 64], pf[sl, 8 + c, tq(q)], idsl(sl, j), start=True, stop=True)], Q2),
                             R=[pf, cst], W=[R0])
                    yield A(lambda e: e.activation(out=Vst32[:], in_=v4(R0[:]), func=AF.Copy), R=[R0], W=[Vst32])
                    if MD != F32:
                        yield V(lambda e: e.tensor_copy(Vst[:], v4(R0[:])), R=[R0], W=[Vst])
                    for q in Q2:
                        yield PE(hl(lambda q, c, sl, j: [lambda e: e.matmul(CB[q][sl, c * 64:(c + 1) * 64], BH[sl, c, tq(q)], idm_sl(sl, j), start=True, stop=True),
                                                         lambda e: e.matmul(CB[q][sl, 256 + c * 64:256 + (c + 1) * 64], KH[sl, c, tq(q)], idm_sl(sl, j), start=True, stop=True)], (q,)),
                                 R=[BH, KH, idm], W=[CB[q]])
                    for q in Q2:
                        yield A(lambda e, q=q: e.activation(out=BKst[:, q], in_=CB[q][:].rearrange("p (a c t) -> p a c t", a=2, c=4), func=AF.Copy), R=[CB[q]], W=[BKst])
                    cur = 0

                    def sq_fns(cur, lvl):
                        f = hl(lambda q, c, sl, j: [lambda e: e.matmul(R0[sl, q * 256 + c * 64:q * 256 + (c + 1) * 64], Nst[cur][sl, q, c, :], Ast[cur][sl, q, c, :], start=True, stop=True)], Q2)
                        if lvl < 5:
                            f += hl(lambda q, c, sl, j: [lambda e: e.matmul(R1_[sl, q * 256 + c * 64:q * 256 + (c + 1) * 64], Ast[cur][sl, q, c, :], Nst[cur][sl, q, c, :], start=True, stop=True)], Q2)
                        return f

                    def pp_fns(a_t, pc):
                        return hl(lambda q, c, sl, j: [lambda e: e.matmul(R2[sl, q * 256 + c * 64:q * 256 + (c + 1) * 64], a_t[sl, q, c, :], pc[sl, q, c, :], start=True, stop=True)], Q2)

                    yield PE(sq_fns(0, 1), R=[Nst[0], Ast[0]], W=[R0, R1_])
                    for lvl in range(1, 6):
                        nxt = 1 - cur
                        yield A(lambda e, nxt=nxt: e.activation(out=Ast[nxt][:], in_=v4(R0[:]), func=AF.Copy), R=[R0], W=[Ast[nxt]])
                        if lvl < 5:
                            yield V(lambda e, nxt=nxt: e.tensor_copy(Nst[nxt][:], v4(R1_[:])), R=[R1_], W=[Nst[nxt]])
                        pc, pn = Pc[(lvl - 1) % 2], (Pc[lvl % 2] if lvl < 5 else TTs[pp_])
                        fns = pp_fns(Ast[nxt], pc)
                        Wl = [R2]
                        Rl = [Ast[nxt], pc]
                        if lvl < 5:
                            fns += sq_fns(nxt, lvl + 1)
                            Wl += [R0] + ([R1_] if lvl + 1 < 5 else [])
                            Rl += [Nst[nxt]]
                        yield PE(fns, R=Rl, W=Wl)
                        yield V(lambda e, pc=pc, pn=pn: e.tensor_tensor(out=pn[:], in0=v4(R2[:]), in1=pc[:], op=ALU.add), R=[R2, pc], W=[pn])
                        cur = nxt
                    yield G(lambda e: e.tensor_tensor(out=B2[:], in0=rview, in1=kview, op=ALU.mult), R=[pf], W=[B2])
                    yield G(lambda e: e.tensor_tensor(out=B2[:], in0=B2[:], in1=bc(P_RK), op=ALU.mult), R=[pfm], W=[B2])
                    fns = []
                    for q in range(2):
                        for c in range(4):
                            for j in range(2):
                                sl = slice(64 * j, 64 * j + 64)
                                fns.append(lambda e, q=q, c=c, sl=sl: e.matmul(R0[sl, (q * 4 + c) * 2:(q * 4 + c) * 2 + 2], B2[sl, c, 64 * q:64 * q + 64], ones2[sl, :],
                                                                              start=True, stop=True))
                    yield PE(fns, R=[B2, ones2], W=[R0])
                    yield A(lambda e: e.activation(out=rk[:].rearrange("p q c -> p (q c)"), in_=R0[:, 0:16].rearrange("p (x two) -> p x two", two=2)[:, :, 0], func=AF.Copy), R=[R0], W=[rk])
                    return

                def seq(n):
                    pp_ = n % 2
                    arT, NB, NK, Vst, Vst32, BKst, WC, rk, B5 = arTs[pp_], NBs[pp_], NKs[pp_], Vsts[pp_], Vst32s[pp_], BKsts[pp_], WCs[pp_], rks[pp_], B5s[pp_]
                    TT = TTs[pp_]
                    Q2 = (0, 1)
                    for q in Q2:
                        yield PE(hl(lambda q, c, sl, j: [lambda e: e.matmul(Q0[sl, c * 64:(c + 1) * 64], arT[sl, c, q, 0, :], STm[sl, c, :], start=True, stop=False),
                                                         lambda e: e.matmul(Q0[sl, c * 64:(c + 1) * 64], NK[sl, q, c, 0, :], Vst[sl, q, c, :], start=False, stop=True)], (q,)),
                                 R=[arT, STm, NK, Vst], W=[Q0])
                        yield A(lambda e: e.activation(out=R1[:], in_=v3(Q0[:, 0:256]), func=AF.Copy), R=[Q0], W=[R1])
                        yield PE(hl(lambda q, c, sl, j: [lambda e: e.matmul(Q0[sl, 256 + c * 64:256 + (c + 1) * 64], TT[sl, q, c, :], R1[sl, c, :], start=True, stop=True)], (q,)),
                                 R=[TT, R1], W=[Q0])
                        yield A(lambda e: e.activation(out=Ust[:], in_=v3(Q0[:, 256:512]), func=AF.Copy), R=[Q0], W=[Ust])
                        if n >= 1:
                            yield PE(hl(lambda q, c, sl, j: [lambda e: e.matmul(Q0[sl, c * 64:(c + 1) * 64], arT[sl, c, q, 1, :], STm[sl, c, :], start=True, stop=False),
                                                             lambda e: e.matmul(Q0[sl, c * 64:(c + 1) * 64], NB[sl, q, c, 1, :], Ust[sl, c, :], start=False, stop=False),
                                                             lambda e: e.matmul(Q0[sl, c * 64:(c + 1) * 64], NK[sl, q, c, 1, :], Vst[sl, q, c, :], start=False, stop=True)], (q,)),
                                     R=[arT, STm, NB, Ust, NK, Vst], W=[Q0])
                            yield V(lambda e, q=q: e.tensor_copy(Yst[:, q], v3(Q0[:, 0:256])), R=[Q0], W=[Yst])
                        yield PE(hl(lambda q, c, sl, j: [lambda e: e.matmul(Q0[sl, 256 + c * 64:256 + (c + 1) * 64], BKst[sl, q, 0, c, :], Ust[sl, c, :], start=True, stop=False),
                                                         lambda e: e.matmul(Q0[sl, 256 + c * 64:256 + (c + 1) * 64], BKst[sl, q, 1, c, :], Vst[sl, q, c, :], start=False, stop=True)], (q,)),
                                 R=[BKst, Ust, Vst], W=[Q0])
                        yield G(lambda e, q=q: e.tensor_tensor(out=STt[:], in0=ST32[:], in1=WC[:, :, q:q + 1].broadcast_to([128, 4, 64]), op=ALU.mult), R=[WC, ST32], W=[STt])
                        if MD != F32:
                            yield V(lambda e: e.tensor_tensor(out=STm[:], in0=STt[:], in1=v3(Q0[:, 256:512]), op=ALU.add), R=[Q0, STt], W=[STm])
                        yield V(lambda e: e.tensor_tensor(out=ST32[:], in0=STt[:], in1=v3(Q0[:, 256:512]), op=ALU.add), R=[Q0, STt], W=[ST32])
                    if n == 0:
                        return
                    Y8 = Yst[:].rearrange("p q c v -> p (q c) v")
                    yc8 = yc[:].rearrange("p q c v -> p (q c) v")
                    ysq8 = ysq[:].rearrange("p q c v -> p (q c) v")
                    V32_8 = Vst32[:].rearrange("p q c v -> p (q c) v")

                    def b8(ap):
                        return ap.unsqueeze(2).broadcast_to([128, 8, 64])
                    yield V(lambda e: e.tensor_reduce(out=gst[:, 1, :], in_=Y8, axis=AX.X, op=ALU.add, negate=True), R=[Yst], W=[gst])
                    yield V(lambda e: e.scalar_tensor_tensor(out=yc8, in0=b8(gst[:, 1, :]), scalar=1.0 / 64, in1=Y8, op0=ALU.mult, op1=ALU.add), R=[Yst], W=[yc, gst])
                    yield G(lambda e: e.tensor_tensor(out=ysq8, in0=yc8, in1=yc8, op=ALU.mult), R=[yc], W=[ysq])
                    yield V(lambda e: e.tensor_reduce(out=gst[:, 2, :], in_=ysq8, axis=AX.X, op=ALU.add), R=[ysq], W=[gst])
                    yield V(lambda e: e.tensor_scalar(out=gst[:, 3, :], in0=gst[:, 2, :], scalar1=1.0 / 64, scalar2=LN_EPS, op0=ALU.mult, op1=ALU.add), R=[], W=[gst])
                    yield G(lambda e: e.tensor_tensor(out=gst[:, 4, :], in0=gst[:, 3, :], in1=cst[:, CE + 4:CE + 12], op=ALU.pow), R=[cst], W=[gst])
                    yield V(lambda e: e.tensor_tensor(out=yc8, in0=yc8, in1=b8(gst[:, 4, :]), op=ALU.mult), R=[], W=[yc, gst])
                    yield G(lambda e: e.tensor_tensor(out=yc[:], in0=yc[:], in1=lnst[:, 0].unsqueeze(1).broadcast_to([128, 2, 4, 64]), op=ALU.mult), R=[lnst], W=[yc])
                    yield G(lambda e: e.tensor_tensor(out=yc[:], in0=yc[:], in1=lnst[:, 1].unsqueeze(1).broadcast_to([128, 2, 4, 64]), op=ALU.add), R=[lnst], W=[yc])
                    yield V(lambda e: e.tensor_tensor(out=ysq8, in0=V32_8, in1=b8(rk[:].rearrange("p q c -> p (q c)")), op=ALU.mult), R=[Vst32, rk], W=[ysq])
                    yield V(lambda e: e.tensor_tensor(out=yc8, in0=yc8, in1=ysq8, op=ALU.add), R=[ysq], W=[yc])
                    yield PE(hl(lambda q, c, sl, j: [lambda e: e.matmul(Q0[sl, q * 256 + c * 64:q * 256 + (c + 1) * 64], yc[sl, q, c, :], idsl(sl, j), start=True, stop=True)], Q2),
                             R=[yc, cst], W=[Q0])
                    yTn = yTr[n % 2]
                    yield V(lambda e: e.tensor_tensor(out=qc(yTn[:]), in0=v4(Q0[:]), in1=qc(B5[:]), op=ALU.mult), R=[Q0, B5], W=[yTn])
                    yield DMA("sp", ch_st[n % 2], lambda e: e.dma_start(out=yscr[n - 1][:, 512:1024], in_=yTn[:].rearrange("p c t -> p (c t)")), R=[yTn], W=[yscr_t[n - 1]])

                import os as _os
                W_PRE, W_ATT, W_SEQ, W_HEAD = [int(v) for v in _os.environ.get("KW", "3,3,1,1").split(",")]
                run(head(0))
                if nt > 1:
                    run(par([pre(0), attention(0), head(1)], [W_PRE, W_ATT, W_HEAD]))
                else:
                    run(par([pre(0), attention(0)], [W_PRE, W_ATT]))
                _skip = _os.environ.get("KSKIP", "")
                B_SEQ, B_PRE, B_ATT, B_HEAD = [float(v) for v in _os.environ.get("KB", "0,0,0,0").split(",")]
                for i in range(nt):
                    streams = [seq(i)] if "seq" not in _skip else []
                    bon = [B_SEQ] if "seq" not in _skip else []
                    if i + 1 < nt:
                        if "pre" not in _skip:
                            streams += [pre(i + 1)]
                            bon += [B_PRE]
                        if "att" not in _skip:
                            streams += [attention(i + 1)]
                            bon += [B_ATT]
                    if i + 2 < nt:
                        streams.append(head(i + 2))
                        bon.append(B_HEAD)
                    run(streams, bon)
                S.barrier()

        es_bw = ExitStack()
        pre_w = {}
        if "B" in phases:
            sbw_ = mk_alloc(es_bw, "bw_")
            pre_w["wg"] = sbw_("wg", [128, 8, DFF], BF16)
            pre_w["wu"] = sbw_("wu", [128, 8, DFF], BF16)

        def load_bw():
            wg, wu = pre_w["wg"], pre_w["wu"]
            ngrp_ = (NFC + 3) // 4
            wg_b = []
            wu_b = []
            for g in range(ngrp_):
                c0, c1 = 512 * g, min(512 * (g + 1), DFF)
                wg_b.append(wload_blk(wg, wg[:, :, c0:c1], w_fg.rearrange("(c p) n -> p c n", p=128)[:, :, c0:c1]))
                wu_b.append(wload_blk(wu, wu[:, :, c0:c1], w_fu.rearrange("(c p) n -> p c n", p=128)[:, :, c0:c1]))
            pre_w["wg_b"] = wg_b
            pre_w["wu_b"] = wu_b

        if "A2" in phases:
            with ExitStack() as es:
                sb = mk_alloc(es, "a2_")
                CE = 128
                cst, pfm = load_consts(sb, 128)
                wgate = sb("wgate", [128, 8, 2048], BF16)
                wba = sb("wba", [128, 4, D], BF16)
                wbr = sb("wbr", [128, 4, D], BF16)
                wgate_b = [wload_blk(wgate, wgate[:, :, 512 * g:512 * (g + 1)], w_gate.rearrange("(c p) n -> p c n", p=128)[:, :, 512 * g:512 * (g + 1)]) for g in range(4)]
                wba_b = wload_blk(wba, wba[:], w_ba.rearrange("(c p) n -> p c n", p=128))
                wbr_b = wload_blk(wbr, wbr[:], w_br.rearrange("(c p) n -> p c n", p=128))
                S.finalize(ch_w, [cst, pfm])
                if "B" in phases:
                    load_bw()
                PS = [Tile(es.enter_context(nc.psum_tensor("psb%d" % i, [128, 512], F32)), "psb%d" % i, excl=True) for i in range(8)]
                xb = [sb("xb0", [128, D]), sb("xb1", [128, D])]
                yT = [sb("yT%d" % i, [128, 8, 128], BF16) for i in range(3)]
                xs = sb("xs", [128, D])
                st4 = sb("st4", [128, 4])
                uTs = [sb("uT0", [128, 8, 128], BF16), sb("uT1", [128, 8, 128], BF16)]
                sgs = [sb("sg0", [128, 16, 128]), sb("sg1", [128, 16, 128])]
                hbg = sb("hbg", [128, 16])
                V(lambda e: e.tensor_scalar(out=hbg[:], in0=pfm[:, P_BG:P_BG + 16], scalar1=0.5, scalar2=None, op0=ALU.mult), R=[pfm], W=[hbg])
                t1 = sb("t1", [128, 8, 128])
                t2 = sb("t2", [128, 8, 128])
                mT = [sb("mT0", [128, 8, 128], BF16), sb("mT1", [128, 8, 128], BF16)]

                def front2(n):
                    xt = xb[n % 2]
                    yTn = yT[n % 3]
                    yield DMA("sp", ch_x[n % 2], lambda e: e.dma_start(out=xt[:], in_=xe[n * 128:(n + 1) * 128, :]), W=[xt])
                    yield DMA("sp", ch_y[n % 2], lambda e: e.dma_start(out=yTn[:].rearrange("p c t -> p (c t)"), in_=yscr[n - 1]), R=[yscr_t[n - 1]], W=[yTn])
                    yield from norm_T(xt, xs, st4, cst, CE, pfm, P_GMIX, [PS[0], PS[1]], uTs[n % 2])

                def mid2(n):
                    uT = uTs[n % 2]
                    sg = sgs[n % 2]
                    for g in range(4):
                        bank = PS[2 + (g % 2)]
                        fns = []
                        for i in range(4):
                            col = (4 * g + i) * 128
                            for kc in range(8):
                                fns.append(lambda e, i=i, col=col, kc=kc, bank=bank: e.matmul(bank[:, i * 128:(i + 1) * 128], wgate[:, kc, col:col + 128], uT[:, kc, :],
                                                                                             start=(kc == 0), stop=(kc == 7)))
                        yield PE(fns, R=[uT, wgate_b[g]], W=[bank])
                        for i in range(4):
                            yield A(lambda e, g=g, i=i, bank=bank: e.activation(out=sg[:, 4 * g + i, :], in_=bank[:, i * 128:(i + 1) * 128], func=AF.Tanh,
                                                                                bias=hbg[:, 4 * g + i:4 * g + i + 1], scale=0.5), R=[bank, hbg], W=[sg])

                def tail2(n):
                    yTn = yT[n % 3]
                    mTn = mT[n % 2]
                    sg = sgs[n % 2]
                    for br, (wb, off, wb_b) in enumerate(((wba, 0, wba_b), (wbr, 4, wbr_b))):
                        for hh in range(2):
                            bank = PS[4 + 2 * br + hh]
                            fns = []
                            for i in range(4):
                                fc = 4 * hh + i
                                for kc in range(4):
                                    fns.append(lambda e, i=i, fc=fc, kc=kc, bank=bank, wb=wb, off=off: e.matmul(bank[:, i * 128:(i + 1) * 128], wb[:, kc, fc * 128:(fc + 1) * 128],
                                                                                                                yTn[:, off + kc, :], start=(kc == 0), stop=(kc == 3)))
                            yield PE(fns, R=[yTn, wb_b], W=[bank])
                    for hh in range(2):
                        yield V(lambda e, hh=hh: e.scalar_tensor_tensor(out=t1[:, 4 * hh:4 * hh + 4, :], in0=sg[:, 4 * hh:4 * hh + 4, :], scalar=1.0,
                                                                        in1=PS[4 + hh][:].rearrange("p (c t) -> p c t", c=4), op0=ALU.add, op1=ALU.mult),
                                R=[PS[4 + hh], sg], W=[t1])
                        yield V(lambda e, hh=hh: e.scalar_tensor_tensor(out=t2[:, 4 * hh:4 * hh + 4, :], in0=sg[:, 8 + 4 * hh:8 + 4 * hh + 4, :], scalar=1.0,
                                                                        in1=PS[6 + hh][:].rearrange("p (c t) -> p c t", c=4), op0=ALU.add, op1=ALU.mult),
                                R=[PS[6 + hh], sg], W=[t2])
                    yield G(lambda e: e.tensor_tensor(out=t1[:], in0=t1[:], in1=t2[:], op=ALU.add), R=[t2], W=[t1])
                    yield A(lambda e: e.activation(out=mTn[:], in_=t1[:], func=AF.Copy, scale=0.5), R=[t1], W=[mTn])
                    yield DMA("sp", ch_st[n % 2], lambda e: e.dma_start(out=mscr[n - 1], in_=mTn[:].rearrange("p c t -> p (c t)")), R=[mTn], W=[mscr_t[n - 1]])

                if nt > 1:
                    run(front2(1))
                if nt > 2:
                    run([mid2(1), front2(2)])
                elif nt > 1:
                    run(mid2(1))
                for n in range(1, nt):
                    streams = [tail2(n)]
                    if n + 1 < nt:
                        streams.append(mid2(n + 1))
                    if n + 2 < nt:
                        streams.append(front2(n + 2))
                    run(streams)
                S.barrier()

        if "B" in phases:
            with ExitStack() as es:
                sb = mk_alloc(es, "b_")
                CE = 128
                cst, pfm = load_consts(sb, 128)
                gfin = sb("gfin", [128, D])
                S.dma("sp", ch_w, lambda e: e.dma_start(out=gfin[:], in_=gfind.broadcast_to([128, D])), W=[gfin])
                wo = sb("wo", [128, 8, D], BF16)
                wg, wu = pre_w["wg"], pre_w["wu"]
                wd = sb("wd", [128, NFC, D], BF16)
                wo_b = wload_blk(wo, wo[:], w_o.rearrange("(c p) n -> p c n", p=128))
                if "wg_b" not in pre_w:
                    load_bw()
                wg_b, wu_b = pre_w["wg_b"], pre_w["wu_b"]
                wd_b = [wload_blk(wd, wd[:, 11 * hh:11 * hh + 11, :], w_fd.rearrange("(c p) n -> p c n", p=128)[:, 11 * hh:11 * hh + 11, :]) for hh in range(2)]
                S.finalize(ch_w, [cst, pfm, gfin])
                PS = [Tile(es.enter_context(nc.psum_tensor("psc%d" % i, [128, 512], F32)), "psc%d" % i, excl=True) for i in range(8)]
                xb = [sb("xb0", [128, D]), sb("xb1", [128, D])]
                mT = [sb("mT0", [128, 8, 128], BF16), sb("mT1", [128, 8, 128], BF16)]
                h1s = [sb("h1a", [128, D]), sb("h1b", [128, D]), sb("h1c", [128, D])]
                xsF = sb("xsF", [128, D])
                xsB = [sb("xsB0", [128, D]), sb("xsB1", [128, D])]
                st4 = sb("st4", [128, 4])
                st4b = sb("st4b", [128, 4])
                fTs = [sb("fT0", [128, 8, 128], BF16), sb("fT1", [128, 8, 128], BF16)]
                sl_ = sb("silu", [128, 4, 128])
                aTs = [sb("aT0", [128, NFC, 128], BF16), sb("aT1", [128, NFC, 128], BF16)]

                def front3(n):
                    xt = xb[n % 2]
                    mTn = mT[n % 2]
                    h1 = h1s[n % 3]
                    yield DMA("sp", ch_x[n % 2], lambda e: e.dma_start(out=xt[:], in_=xe[n * 128:(n + 1) * 128, :]), W=[xt])
                    yield DMA("sp", ch_y[n % 2], lambda e: e.dma_start(out=mTn[:].rearrange("p c t -> p (c t)"), in_=mscr[n - 1]), R=[mscr_t[n - 1]], W=[mTn])
                    for hh in range(2):
                        yield PE([lambda e, kc=kc, hh=hh: e.matmul(PS[0][:], mTn[:, kc, :], wo[:, kc, hh * 512:(hh + 1) * 512], start=(kc == 0), stop=(kc == 7)) for kc in range(8)],
                                 R=[mTn, wo_b], W=[PS[0]])
                        yield V(lambda e, hh=hh: e.tensor_tensor(out=h1[:, hh * 512:(hh + 1) * 512], in0=PS[0][:], in1=xt[:, hh * 512:(hh + 1) * 512], op=ALU.add),
                                R=[PS[0], xt], W=[h1])
                    yield from norm_T(h1, xsF, st4, cst, CE, pfm, P_GFFN, [PS[1], PS[1]], fTs[n % 2])

                def mid3(n):
                    fT = fTs[n % 2]
                    aT = aTs[n % 2]
                    ngrp = (NFC + 3) // 4
                    for g in range(ngrp):
                        nchunk = min(4, NFC - 4 * g)
                        bg = PS[2 + 2 * (g % 2)]
                        bu = PS[3 + 2 * (g % 2)]
                        for bank, wt, wt_b in ((bg, wg, wg_b[g]), (bu, wu, wu_b[g])):
                            fns = []
                            for i in range(nchunk):
                                fc = 4 * g + i
                                for kc in range(8):
                                    fns.append(lambda e, i=i, fc=fc, kc=kc, bank=bank, wt=wt: e.matmul(bank[:, i * 128:(i + 1) * 128], wt[:, kc, fc * 128:(fc + 1) * 128], fT[:, kc, :],
                                                                                                       start=(kc == 0), stop=(kc == 7)))
                            yield PE(fns, R=[fT, wt_b], W=[bank])
                        yield A(lambda e: e.activation(out=sl_[:, 0:nchunk, :], in_=bg[:, 0:nchunk * 128].rearrange("p (c t) -> p c t", c=nchunk), func=AF.Tanh, scale=0.5),
                                R=[bg], W=[sl_])
                        yield V(lambda e: e.scalar_tensor_tensor(out=sl_[:, 0:nchunk, :], in0=sl_[:, 0:nchunk, :], scalar=1.0,
                                                                 in1=bg[:, 0:nchunk * 128].rearrange("p (c t) -> p c t", c=nchunk), op0=ALU.add, op1=ALU.mult), R=[bg], W=[sl_])
                        yield V(lambda e: e.scalar_tensor_tensor(out=aT[:, 4 * g:4 * g + nchunk, :], in0=sl_[:, 0:nchunk, :], scalar=0.5,
                                                                 in1=bu[:, 0:nchunk * 128].rearrange("p (c t) -> p c t", c=nchunk), op0=ALU.mult, op1=ALU.mult), R=[bu, sl_], W=[aT])

                def tail3(n):
                    h1 = h1s[n % 3]
                    aT = aTs[n % 2]
                    o = xsB[n % 2]
                    for hh in range(2):
                        yield PE([lambda e, fc=fc, hh=hh: e.matmul(PS[6 + hh][:], aT[:, fc, :], wd[:, fc, hh * 512:(hh + 1) * 512], start=(fc == 0), stop=(fc == NFC - 1)) for fc in range(NFC)],
                                 R=[aT] + wd_b, W=[PS[6 + hh]])
                        yield V(lambda e, hh=hh: e.tensor_tensor(out=h1[:, hh * 512:(hh + 1) * 512], in0=PS[6 + hh][:], in1=h1[:, hh * 512:(hh + 1) * 512], op=ALU.add),
                                R=[PS[6 + hh]], W=[h1])
                    yield A(lambda e: e.activation(out=o[:], in_=h1[:], func=AF.Square, accum_out=st4b[:, 0:1]), R=[h1], W=[o, st4b])
                    yield V(lambda e: e.tensor_scalar(out=st4b[:, 1:2], in0=st4b[:, 0:1], scalar1=1.0 / D, scalar2=RMS_EPS, op0=ALU.mult, op1=ALU.add), R=[], W=[st4b])
                    yield G(lambda e: e.tensor_tensor(out=st4b[:, 2:3], in0=st4b[:, 1:2], in1=cst[:, CE + 4:CE + 5], op=ALU.pow), R=[cst], W=[st4b])
                    yield A(lambda e: e.activation(out=o[:], in_=h1[:], func=AF.Identity, scale=st4b[:, 2:3], bias=cst[:, CE + 2:CE + 3]), R=[h1, cst], W=[o, st4b])
                    yield G(lambda e: e.tensor_tensor(out=o[:], in0=o[:], in1=gfin[:], op=ALU.mult), R=[gfin], W=[o])
                    yield DMA("sp", ch_st[n % 2], lambda e: e.dma_start(out=outd[(n - 1) * 128:n * 128, :], in_=o[:]), R=[o], W=[])

                if nt > 1:
                    run(front3(1))
                if nt > 2:
                    run([mid3(1), front3(2)])
                elif nt > 1:
                    run(mid3(1))
                for n in range(1, nt):
                    streams = [tail3(n)]
                    if n + 1 < nt:
                        streams.append(mid3(n + 1))
                    if n + 2 < nt:
                        streams.append(front3(n + 2))
                    run(streams)
                S.barrier()
        else:
            S.barrier()
        es_bw.close()
    return nc


QPERM = [0, 4, 1, 5, 2, 6, 3, 7]


def make_consts():
    c = np.zeros((128, C_END), np.float32)
    c[:, C_ID:C_ID + 128] = np.eye(128, dtype=np.float32)
    s = np.arange(64)
    for j in range(2):
        rows = slice(64 * j, 64 * j + 64)
        c[rows, C_MB:C_MB + 64] = (s[None, :] > s[:, None])
        c[rows, C_MB + 64:C_MB + 128] = (s[None, :] >= s[:, None])
        c[rows, C_ML:C_ML + 64] = (s[None, :] < s[:, None])
        c[rows, C_I64:C_I64 + 64] = np.eye(64)
        c[rows, C_OBD + 64 * j:C_OBD + 64 * j + 64] = 1.0
        c[rows, C_TRI:C_TRI + 64] = CFAC * (s[:, None] <= s[None, :])
        c[rows, C_TRI + 64:C_TRI + 128] = CFAC * (s[:, None] < s[None, :])
    c[64:128, C_TRI0:C_TRI0 + 128] = c[64:128, C_TRI:C_TRI + 128]
    c[64:64 + 48, C_TRI0:C_TRI0 + 128] = 0.0
    i = np.arange(128)
    own = np.where(i[None, :] <= i[:, None], 0.0, NEG)
    prev = np.where(i[None, :] > i[:, None], 0.0, NEG)
    full = np.full((128, 128), NEG)
    for var, (a, b) in enumerate(((own, prev), (prev, own), (full, own))):
        base = C_AM + 272 * var
        c[:, base:base + 128] = a
        c[:, base + 128:base + 256] = b
        c[:, base + 256:base + 272] = 0.0
    half = 8
    inv_freq = np.power(np.float32(500000.0), -np.arange(half, dtype=np.float32) * np.float32(2.0 / 16)).astype(np.float32)
    for n in range(NTILES):
        pos = (n * 128 + np.arange(128) - 112).astype(np.float32)
        ang = (pos[:, None] * inv_freq[None, :]).astype(np.float32)
        c[:, C_ROPE + 16 * n:C_ROPE + 16 * n + 8] = np.cos(ang)
        c[:, C_ROPE + 16 * n + 8:C_ROPE + 16 * n + 16] = np.sin(ang)
    return c


def prep_shared(inp):
    f = np.float32
    w_in = np.asarray(inp["w_in"][0], f)
    b_in = np.asarray(inp["b_in"][0], f)
    qcols = np.concatenate([np.arange(h * 64, (h + 1) * 64) for h in QPERM])
    w_qkv = np.ascontiguousarray(np.concatenate([w_in[:, qcols], w_in[:, 512:768]], axis=1))
    b_qkv = np.concatenate([b_in[qcols], b_in[512:768]])
    R0 = 768
    w_fm = np.zeros((D, 2048), f)
    b_fm = np.zeros((2048,), f)
    mix = np.asarray(inp["rwkv_mix"][0], f)
    mix_fm = np.zeros((2048,), f)

    def put(dst0, src0, n):
        w_fm[:, dst0:dst0 + n] = w_in[:, R0 + src0:R0 + src0 + n]
        b_fm[dst0:dst0 + n] = b_in[R0 + src0:R0 + src0 + n]
        mix_fm[dst0:dst0 + n] = mix[src0:src0 + n]
    put(0, 0, 1536)
    put(1536, 1536, 64)
    put(1664, 1600, 64)
    put(1792, 1664, 128)
    put(1920, 1792, 32)
    G0 = 768 + 1824
    w_gate = np.ascontiguousarray(w_in[:, G0:G0 + 2048])
    b_gate = b_in[G0:G0 + 2048]
    rows_perm = qcols
    sh = {
        "w_qkv": w_qkv, "w_fm": w_fm, "w_gate": w_gate,
        "w_ba": np.ascontiguousarray(np.asarray(inp["w_br_attn"][0], f)[rows_perm, :]),
        "w_br": np.ascontiguousarray(np.asarray(inp["w_br_rwkv"][0], f)),
        "w_o": np.ascontiguousarray(np.asarray(inp["w_o"][0], f)),
        "w_fg": np.ascontiguousarray(np.asarray(inp["w_ffn_gate"][0], f)),
        "w_fu": np.ascontiguousarray(np.asarray(inp["w_ffn_up"][0], f)),
        "w_fd": np.ascontiguousarray(np.asarray(inp["w_ffn_down"][0], f)),
        "w2": np.ascontiguousarray(np.asarray(inp["rwkv_w2"][0], f)),
        "a2": np.ascontiguousarray(np.asarray(inp["rwkv_a2"][0], f)),
    }
    g2p = np.zeros((256, 512), f)
    g2p[0:160] = np.asarray(inp["rwkv_g2"][0], f)
    sh["g2p"] = g2p
    pfm = np.zeros((128, P_END), f)

    def fm(vec, ncol):
        return np.asarray(vec, f).reshape(ncol, 128).T
    pfm[:, P_GMIX:P_GMIX + 8] = fm(inp["norm_mix_g"][0], 8)
    pfm[:, P_GFFN:P_GFFN + 8] = fm(inp["norm_ffn_g"][0], 8)
    pfm[:, P_BFM:P_BFM + 16] = fm(b_fm, 16)
    pfm[:, P_BG:P_BG + 16] = fm(b_gate, 16)
    pfm[:, P_MIX:P_MIX + 16] = fm(mix_fm, 16)
    pfm[:, P_A0:P_A0 + 4] = fm(inp["rwkv_a0"][0], 4)
    pfm[:, P_KK:P_KK + 4] = fm(inp["rwkv_k_k"][0], 4)
    pfm[:, P_KA:P_KA + 4] = fm(inp["rwkv_k_a"][0], 4)
    pfm[:, P_RK:P_RK + 4] = fm(np.asarray(inp["rwkv_r_k"][0], f).reshape(-1), 4)
    sh["pfm"] = pfm
    rowsA = np.zeros((1, RA_END), f)
    rowsA[0, RA_BQ:RA_BQ + 768] = b_qkv
    rowsA[0, RA_W0:RA_W0 + 512] = np.asarray(inp["rwkv_w0"][0], f)
    rowsA[0, RA_SK:RA_SK + 8] = np.asarray(inp["attn_sinks"][0], f)[QPERM]
    sh["rowsA"] = rowsA
    sh["gfin"] = np.asarray(inp["norm_final_g"], f).reshape(1, D).copy()
    lnst = np.zeros((128, 2, 4, 64), f)
    for a, key in enumerate(("rwkv_ln_w", "rwkv_ln_b")):
        v = np.asarray(inp[key][0], f).reshape(4, 2, 64)
        for j in range(2):
            lnst[64 * j:64 * j + 64, a, :, :] = v[None, :, j, :]
    sh["lnst"] = lnst.reshape(128, -1)
    sh["cst"] = make_consts()
    return sh


def prep_xe(inp, b):
    xe = np.zeros((NTILES * 128, D), np.float32)
    xe[112:128] = np.asarray(inp["meta_tokens"], np.float32)
    xe[128:] = np.asarray(inp["x"][b], np.float32)
    return xe


_NC_CACHE = {}


def kernel(**inputs):
    n = 8
    sh = prep_shared(inputs)
    in_maps = []
    for b in range(n):
        m = dict(sh)
        m["xe"] = prep_xe(inputs, b)
        in_maps.append(m)
    if "nc" not in _NC_CACHE:
        _NC_CACHE["nc"] = build_program()
    res = run_bass_kernel_spmd(_NC_CACHE["nc"], in_maps, core_ids=list(range(n)))
    out = np.stack([np.asarray(r["out"], np.float32).reshape(4096, D) for r in res.results], axis=0)
    return out
```
